# Optimizing a Trainium2 kernel written in Bass

```python
import math
import jax
import jax.numpy as jnp
from jax import lax
import numpy as np


D_MODEL = 1024
BATCH = 2
SEQ = 8192
DEPTH = 2

CTX_LEN = 256
GRID_W = 64
EPS = 1e-6
ROPE_BASE = 10000.0
CHUNK = 128
Q_BLOCK = 128
D_MIX = 2 * D_MODEL

ATT_QK_DIM = 64
ATT_V_DIM = 2 * ATT_QK_DIM
ATT_WIDTH = D_MODEL // 2
ATT_HEADS = ATT_WIDTH // ATT_V_DIM

SSD_HEAD_DIM = 64
SSD_WIDTH = D_MODEL
SSD_HEADS = SSD_WIDTH // SSD_HEAD_DIM
SSD_GROUPS = 2
SSD_HPG = SSD_HEADS // SSD_GROUPS
SSD_STATE = 128
SSD_CONV = 3
SSD_CONV_DIM = SSD_WIDTH + 2 * SSD_GROUPS * SSD_STATE

RET_WIDTH = D_MODEL // 2
RET_HEADS = 4
RET_V_DIM = RET_WIDTH // RET_HEADS
RET_QK_DIM = RET_V_DIM // 2
RET_DECAY_EXP_FWD = (5.0, 6.0, 7.0, 8.0)
RET_DECAY_EXP_BWD = (5.5, 6.5, 7.5, 8.5)

IN_SPLITS = (ATT_HEADS * 2 * ATT_QK_DIM, ATT_HEADS * 2 * ATT_QK_DIM, ATT_WIDTH, ATT_WIDTH,
             SSD_CONV_DIM, 2 * SSD_HEADS, SSD_WIDTH,
             RET_HEADS * RET_QK_DIM, RET_HEADS * RET_QK_DIM, RET_WIDTH, RET_WIDTH)
D_IN_PROJ = sum(IN_SPLITS)

kernel_name = 'hybrid_diffusion_parallel_heads'


def rms_norm(x, gain=None):
    xf = x.astype(jnp.float32)
    y = xf * lax.rsqrt(jnp.mean(xf * xf, axis=-1, keepdims=True) + EPS)
    if gain is not None:
        y = y * gain.astype(jnp.float32)
    return y.astype(x.dtype)


def split_columns(t, sizes):
    parts, off = [], 0
    for s in sizes:
        parts.append(t[..., off:off + s])
        off += s
    return parts


def axial_rope_table(n_rows, dim):
    row = jnp.repeat(jnp.arange(n_rows, dtype=jnp.float32), GRID_W)
    col = jnp.tile(jnp.arange(GRID_W, dtype=jnp.float32), n_rows)
    n_freq = dim // 4
    inv_freq = ROPE_BASE ** (-jnp.arange(n_freq, dtype=jnp.float32) / n_freq)
    ang = jnp.concatenate([row[:, None] * inv_freq, col[:, None] * inv_freq], axis=-1)
    return jnp.cos(ang), jnp.sin(ang)


def apply_rope(x, cos, sin):
    shape = (1, x.shape[1]) + (1,) * (x.ndim - 3) + (cos.shape[-1],)
    cs = cos.reshape(shape).astype(x.dtype)
    sn = sin.reshape(shape).astype(x.dtype)
    x1, x2 = jnp.split(x, 2, axis=-1)
    return jnp.concatenate([x1 * cs - x2 * sn, x1 * sn + x2 * cs], axis=-1)


def depthwise_centred_conv(u, w, b):
    pad = (w.shape[0] - 1) // 2
    y = lax.conv_general_dilated(u, w[:, None, :].astype(u.dtype), window_strides=(1,),
                                 padding=[(pad, pad)], dimension_numbers=('NWC', 'WIO', 'NWC'),
                                 feature_group_count=u.shape[-1])
    return jax.nn.silu(y + b.astype(u.dtype))


def diff_attention(q, k, v, lam):
    s = jnp.einsum('bqhmd,bkhmd->bhmqk', q, k).astype(jnp.float32) * (ATT_QK_DIM ** -0.5)
    p = jax.nn.softmax(s, axis=-1)
    a = p[:, :, 0] - lam * p[:, :, 1]
    return jnp.einsum('bhqk,bkhe->bqhe', a.astype(v.dtype), v)


def diff_attention_blocked(q, k, v, lam):
    b, L = q.shape[:2]
    qb = jnp.moveaxis(q.reshape((b, L // Q_BLOCK, Q_BLOCK) + q.shape[2:]), 1, 0)
    o = lax.map(lambda qq: diff_attention(qq, k, v, lam), qb)
    return jnp.moveaxis(o, 0, 1).reshape(b, L, ATT_HEADS, ATT_V_DIM)


def chunked_scan(q, k, v, log_a, s0, want_y):
    b, L, g, n = q.shape
    hg, p = v.shape[-2:]
    nc = L // CHUNK
    qc = q.reshape(b, nc, CHUNK, g, n)
    kc = k.reshape(b, nc, CHUNK, g, n)
    vc = v.reshape(b, nc, CHUNK, g, hg, p)
    cum = jnp.cumsum(log_a.astype(jnp.float32).reshape(b, nc, CHUNK, g, hg), axis=2)
    total = cum[:, :, -1]
    to_end = jnp.exp(total[:, :, None] - cum)
    chunk_states = jnp.einsum('bcjgn,bcjgh,bcjghp->bcghnp', kc, to_end, vc).astype(jnp.float32)

    def step(s, inp):
        st, tot = inp
        return jnp.exp(tot)[..., None, None] * s + st, s

    s_final, s_in = lax.scan(step, s0, (jnp.moveaxis(chunk_states, 1, 0), jnp.moveaxis(total, 1, 0)))
    if not want_y:
        return None, s_final
    s_in = jnp.moveaxis(s_in, 0, 1)
    lower = jnp.tril(jnp.ones((CHUNK, CHUNK), dtype=bool))[:, :, None, None]
    seg = cum[:, :, :, None] - cum[:, :, None]
    decay = jnp.exp(jnp.where(lower, seg, -jnp.inf))
    scores = jnp.einsum('bcign,bcjgn->bcijg', qc, kc)
    y = (jnp.einsum('bcijg,bcijgh,bcjghp->bcighp', scores, decay, vc)
         + jnp.einsum('bcign,bcigh,bcghnp->bcighp', qc, jnp.exp(cum), s_in))
    return y.reshape(b, L, g, hg, p), s_final


def bidirectional_scan(q, k, v_f, v_b, la_f, la_b, s0_f, s0_b, want_y):
    y_f, s_f = chunked_scan(q, k, v_f, la_f, s0_f, want_y)
    fl = lambda a: jnp.flip(a, axis=1)
    y_b, s_b = chunked_scan(fl(q), fl(k), fl(v_b), fl(la_b), s0_b, want_y)
    y = y_f + fl(y_b) if want_y else None
    return y, s_f, s_b


def retention_log_decay(exps):
    return jnp.log1p(-jnp.exp2(-jnp.asarray(exps, dtype=jnp.float32)))


def hybrid_layer(x, ctx, c, c_ctx, w_ada, b_ada, w_in, w_out, attn_q_norm, attn_k_norm,
                 lambda_q1, lambda_k1, lambda_q2, lambda_k2, attn_subln,
                 ssd_conv_w, ssd_conv_b, ssd_dt_bias, ssd_a_log, ssd_d, ssd_norm, ret_norm,
                 lam_init, cos, sin, need_ctx):
    bsz = x.shape[0]
    shift, scale, gate = jnp.split(jax.nn.silu(c) @ w_ada + b_ada, 3, axis=-1)
    shift_c, scale_c, gate_c = jnp.split(jax.nn.silu(c_ctx) @ w_ada + b_ada, 3, axis=-1)
    h = rms_norm(x) * (1 + scale[:, None]) + shift[:, None]
    hc = rms_norm(ctx) * (1 + scale_c) + shift_c
    aq, ak, av, ag, xbc, dtr, z, rq, rk, rv, rg = split_columns(h @ w_in, IN_SPLITS)
    aq_c, ak_c, av_c, ag_c, xbc_c, dtr_c, z_c, rq_c, rk_c, rv_c, rg_c = split_columns(hc @ w_in, IN_SPLITS)

    lam = (jnp.exp(jnp.sum(lambda_q1 * lambda_k1)) - jnp.exp(jnp.sum(lambda_q2 * lambda_k2))
           + lam_init).astype(jnp.float32)

    def attn_heads(t, gain):
        return rms_norm(t.reshape(bsz, t.shape[1], ATT_HEADS, 2, ATT_QK_DIM), gain)

    def attn_out(o, g):
        o = rms_norm(o, attn_subln) * (1.0 - lam_init)
        return o.reshape(bsz, o.shape[1], ATT_WIDTH) * jax.nn.silu(g)

    k_c = attn_heads(ak_c, attn_k_norm)
    v_c = av_c.reshape(bsz, -1, ATT_HEADS, ATT_V_DIM)
    q_l = apply_rope(attn_heads(aq, attn_q_norm), cos, sin)
    k_l = apply_rope(attn_heads(ak, attn_k_norm), cos, sin)
    v_l = av.reshape(bsz, -1, ATT_HEADS, ATT_V_DIM)
    attn_lat = attn_out(diff_attention_blocked(q_l, jnp.concatenate([k_l, k_c], axis=1),
                                               jnp.concatenate([v_l, v_c], axis=1), lam), ag)

    def ssd_prep(xbc_t, dtr_t):
        u = depthwise_centred_conv(xbc_t, ssd_conv_w, ssd_conv_b)
        L = u.shape[1]
        xs, bm, cm = split_columns(u, (SSD_WIDTH, SSD_GROUPS * SSD_STATE, SSD_GROUPS * SSD_STATE))
        xs = xs.reshape(bsz, L, SSD_GROUPS, SSD_HPG, SSD_HEAD_DIM)
        bm = bm.reshape(bsz, L, SSD_GROUPS, SSD_STATE)
        cm = cm.reshape(bsz, L, SSD_GROUPS, SSD_STATE)
        dt = jax.nn.softplus(dtr_t.astype(jnp.float32).reshape(bsz, L, 2, SSD_HEADS) + ssd_dt_bias)
        la = (dt * -jnp.exp(ssd_a_log)).reshape(bsz, L, 2, SSD_GROUPS, SSD_HPG)
        dt = dt.reshape(bsz, L, 2, SSD_GROUPS, SSD_HPG)
        return (xs, bm, cm, xs * dt[:, :, 0, ..., None], xs * dt[:, :, 1, ..., None],
                la[:, :, 0], la[:, :, 1])

    def ssd_out(y, xs, z_t):
        L = y.shape[1]
        y = y + xs * ssd_d.reshape(SSD_GROUPS, SSD_HPG, 1)
        y = y.reshape(bsz, L, SSD_WIDTH) * jax.nn.silu(z_t)
        y = rms_norm(y.reshape(bsz, L, SSD_GROUPS, SSD_WIDTH // SSD_GROUPS)).reshape(bsz, L, SSD_WIDTH)
        return y * ssd_norm

    zero_s = jnp.zeros((bsz, SSD_GROUPS, SSD_HPG, SSD_STATE, SSD_HEAD_DIM), jnp.float32)
    xs_c, bm_c, cm_c, vf_c, vb_c, laf_c, lab_c = ssd_prep(xbc_c, dtr_c)
    ys_c, ssd_sf, ssd_sb = bidirectional_scan(cm_c, bm_c, vf_c, vb_c, laf_c, lab_c, zero_s, zero_s, need_ctx)
    xs_l, bm_l, cm_l, vf_l, vb_l, laf_l, lab_l = ssd_prep(xbc, dtr)
    ys_l, _, _ = bidirectional_scan(cm_l, bm_l, vf_l, vb_l, laf_l, lab_l, ssd_sf, ssd_sb, True)
    ssd_lat = ssd_out(ys_l, xs_l, z)

    ret_la_f = retention_log_decay(RET_DECAY_EXP_FWD)[:, None]
    ret_la_b = retention_log_decay(RET_DECAY_EXP_BWD)[:, None]

    def ret_prep(q_t, k_t, v_t, rope):
        L = q_t.shape[1]
        q = q_t.reshape(bsz, L, RET_HEADS, RET_QK_DIM)
        k = k_t.reshape(bsz, L, RET_HEADS, RET_QK_DIM) * (RET_QK_DIM ** -0.5)
        if rope:
            q, k = apply_rope(q, cos, sin), apply_rope(k, cos, sin)
        v = v_t.reshape(bsz, L, RET_HEADS, 1, RET_V_DIM)
        shp = (bsz, L, RET_HEADS, 1)
        return q, k, v, jnp.broadcast_to(ret_la_f, shp), jnp.broadcast_to(ret_la_b, shp)

    def ret_out(y, g):
        L = y.shape[1]
        y = rms_norm(y.reshape(bsz, L, RET_HEADS, RET_V_DIM), ret_norm).reshape(bsz, L, RET_WIDTH)
        return y * jax.nn.silu(g)

    zero_r = jnp.zeros((bsz, RET_HEADS, 1, RET_QK_DIM, RET_V_DIM), jnp.float32)
    q_rc, k_rc, v_rc, lf_rc, lb_rc = ret_prep(rq_c, rk_c, rv_c, False)
    yr_c, ret_sf, ret_sb = bidirectional_scan(q_rc, k_rc, v_rc, v_rc, lf_rc, lb_rc, zero_r, zero_r, need_ctx)
    q_rl, k_rl, v_rl, lf_rl, lb_rl = ret_prep(rq, rk, rv, True)
    yr_l, _, _ = bidirectional_scan(q_rl, k_rl, v_rl, v_rl, lf_rl, lb_rl, ret_sf, ret_sb, True)
    ret_lat = ret_out(yr_l, rg)

    mix = jnp.concatenate([attn_lat, ssd_lat, ret_lat], axis=-1).astype(x.dtype)
    x_new = x + gate[:, None] * (mix @ w_out)
    if not need_ctx:
        return x_new, None
    attn_ctx = attn_out(diff_attention(attn_heads(aq_c, attn_q_norm), k_c, v_c, lam), ag_c)
    mix_c = jnp.concatenate([attn_ctx, ssd_out(ys_c, xs_c, z_c), ret_out(yr_c, rg_c)], axis=-1).astype(ctx.dtype)
    ctx_new = ctx + gate_c * (mix_c @ w_out)
    return x_new, ctx_new


def setup_inputs(seed: int = 0) -> dict:
    key = jax.random.key(seed)
    ks = jax.random.split(key, 24)
    f32 = jnp.float32

    def nrm(k, shape, s):
        return jax.random.normal(k, shape, f32) * s

    log_lo, log_hi = math.log(1e-3), math.log(1e-1)
    dt = jnp.exp(jax.random.uniform(ks[15], (DEPTH, 2, SSD_HEADS), f32) * (log_hi - log_lo) + log_lo)
    return {
        'x': nrm(ks[0], (BATCH, SEQ, D_MODEL), 1.0),
        'c': nrm(ks[1], (BATCH, D_MODEL), 1.0),
        'ctx': nrm(ks[2], (BATCH, CTX_LEN, D_MODEL), 1.0),
        'c_ctx': nrm(ks[3], (D_MODEL,), 1.0),
        'w_ada': nrm(ks[4], (DEPTH, D_MODEL, 3 * D_MODEL), 0.5 * D_MODEL ** -0.5),
        'b_ada': nrm(ks[5], (DEPTH, 3 * D_MODEL), 0.02),
        'w_in': nrm(ks[6], (DEPTH, D_MODEL, D_IN_PROJ), D_MODEL ** -0.5),
        'w_out': nrm(ks[7], (DEPTH, D_MIX, D_MODEL), D_MIX ** -0.5),
        'attn_q_norm': 1.0 + nrm(ks[8], (DEPTH, ATT_QK_DIM), 0.02),
        'attn_k_norm': 1.0 + nrm(ks[9], (DEPTH, ATT_QK_DIM), 0.02),
        'lambda_q1': nrm(ks[10], (DEPTH, ATT_QK_DIM), 0.1),
        'lambda_k1': nrm(ks[11], (DEPTH, ATT_QK_DIM), 0.1),
        'lambda_q2': nrm(ks[12], (DEPTH, ATT_QK_DIM), 0.1),
        'lambda_k2': nrm(ks[13], (DEPTH, ATT_QK_DIM), 0.1),
        'attn_subln': 1.0 + nrm(ks[14], (DEPTH, ATT_V_DIM), 0.02),
        'ssd_conv_w': nrm(ks[16], (DEPTH, SSD_CONV, SSD_CONV_DIM), SSD_CONV ** -0.5),
        'ssd_conv_b': nrm(ks[17], (DEPTH, SSD_CONV_DIM), 0.02),
        'ssd_dt_bias': dt + jnp.log(-jnp.expm1(-dt)),
        'ssd_a_log': jnp.log(jax.random.uniform(ks[18], (DEPTH, 2, SSD_HEADS), f32, 1.0, 16.0)),
        'ssd_d': 1.0 + nrm(ks[19], (DEPTH, SSD_HEADS), 0.1),
        'ssd_norm': 1.0 + nrm(ks[20], (DEPTH, SSD_WIDTH), 0.02),
        'ret_norm': 1.0 + nrm(ks[21], (DEPTH, RET_V_DIM), 0.02),
    }


def reference(x, c, ctx, c_ctx, w_ada, b_ada, w_in, w_out, attn_q_norm, attn_k_norm,
              lambda_q1, lambda_k1, lambda_q2, lambda_k2, attn_subln,
              ssd_conv_w, ssd_conv_b, ssd_dt_bias, ssd_a_log, ssd_d, ssd_norm, ret_norm):
    n_rows = x.shape[1] // GRID_W
    cos, sin = axial_rope_table(n_rows, ATT_QK_DIM)
    for layer in range(DEPTH):
        lam_init = 0.8 - 0.6 * math.exp(-0.3 * layer)
        x, ctx = hybrid_layer(x, ctx, c, c_ctx, w_ada[layer], b_ada[layer], w_in[layer], w_out[layer],
                              attn_q_norm[layer], attn_k_norm[layer], lambda_q1[layer], lambda_k1[layer],
                              lambda_q2[layer], lambda_k2[layer], attn_subln[layer],
                              ssd_conv_w[layer], ssd_conv_b[layer], ssd_dt_bias[layer], ssd_a_log[layer],
                              ssd_d[layer], ssd_norm[layer], ret_norm[layer],
                              lam_init, cos, sin, layer < DEPTH - 1)
    return x
```

```python
import numpy as np
import concourse.bass as bass
import concourse.mybir as mybir

F32 = mybir.dt.float32
BF16 = mybir.dt.bfloat16
AF = mybir.ActivationFunctionType
ALU = mybir.AluOpType
AX = mybir.AxisListType


class Res:
    __slots__ = ("name", "w", "r")

    def __init__(self, name):
        self.name = name
        self.w = None
        self.r = {}


class V:
    __slots__ = ("ap", "res")

    def __init__(self, ap, res):
        self.ap = ap
        self.res = res if isinstance(res, (list, tuple)) else [res]


class Buf:
    def __init__(self, fw, name, t, nparts=1):
        self.fw = fw
        self.name = name
        self.t = t
        self.parts = {}
        self.whole = Res(name)

    def __getitem__(self, idx):
        return V(self.t[idx], self.whole)

    def part(self, key):
        if key not in self.parts:
            self.parts[key] = Res(f"{self.name}.{key}")
        return _PartView(self, self.parts[key])

    def ap(self):
        return self.t.ap()


class _PartView:
    def __init__(self, buf, res):
        self.buf = buf
        self.res = res

    def __getitem__(self, idx):
        return V(self.buf.t[idx], self.res)


class EngState:
    def __init__(self, name, eng, sem):
        self.name = name
        self.eng = eng
        self.sem = sem
        self.count = 0
        self.pending = False
        self.seen = {}
        self.seen_dma = {}


class FW:
    def __init__(self, nc, n_dma_sems=24, same_engine_sync=True):
        self.nc = nc
        self.same_engine_sync = same_engine_sync
        self.engs = {}
        for name, eng in (("pe", nc.tensor), ("dve", nc.vector), ("act", nc.scalar),
                          ("pool", nc.gpsimd), ("sp", nc.sync)):
            self.engs[name] = EngState(name, eng, nc.alloc_semaphore(f"s_{name}"))
        self.dma_sems = [nc.alloc_semaphore(f"s_dma{i}") for i in range(n_dma_sems)]
        self.dma_vals = [0] * n_dma_sems
        self.dma_next = 0
        self.n_inst = 0
        self.out_tokens = []
        self.cc_sem = None
        self.cc_val = 0

    def sbuf(self, name, shape, dtype):
        return Buf(self, name, self.nc.alloc_sbuf_tensor("sb_" + name, list(shape), dtype))

    def psum(self, name, shape, dtype=F32):
        return Buf(self, name, self.nc.alloc_psum_tensor("ps_" + name, list(shape), dtype))

    def dram(self, name, shape, dtype, kind="Internal", **kw):
        return Buf(self, name, self.nc.dram_tensor(name, list(shape), dtype, kind=kind, **kw))

    def _need(self, E, tok):
        if tok is None:
            return
        if tok[0] == "eng":
            _, e, c = tok
            if e == E.name:
                if not self.same_engine_sync or e == "pe":
                    return
            if E.seen.get(e, 0) >= c:
                return
            P = self.engs[e]
            assert c <= P.count, f"{E.name} waits on pending (never-incremented) {e} count {c} > {P.count}"
            E.eng.wait_ge(P.sem, c)
            E.seen[e] = c
        elif tok[0] == "cc":
            val = tok[1]
            if E.seen_dma.get("cc", 0) >= val:
                return
            E.eng.wait_ge(self.cc_sem, val)
            E.seen_dma["cc"] = val
        else:
            _, si, val = tok
            if E.seen_dma.get(si, 0) >= val:
                return
            E.eng.wait_ge(self.dma_sems[si], val)
            E.seen_dma[si] = val

    def _pre(self, E, reads, writes):
        for v in reads:
            for r in v.res:
                self._need(E, r.w)
        for v in writes:
            for r in v.res:
                self._need(E, r.w)
                for tok in r.r.values():
                    self._need(E, tok)

    def _post(self, tok, key, reads, writes):
        for v in reads:
            for r in v.res:
                r.r[key] = tok
        for v in writes:
            for r in v.res:
                r.w = tok
                r.r = {}

    def op(self, engname, fn, reads, writes, inc=True):
        E = self.engs[engname]
        self._pre(E, reads, writes)
        ins = fn(E.eng)
        self.n_inst += 1
        if inc:
            E.count += 1
            ins.then_inc(E.sem, 1)
            tok = ("eng", engname, E.count)
        else:
            tok = ("eng", engname, E.count + 1)
        self._post(tok, engname, reads, writes)
        return ins

    def dma(self, qname, out, in_, **kw):
        E = self.engs[qname]
        self._pre(E, [in_], [out])
        si = self.dma_next
        self.dma_next = (self.dma_next + 1) % len(self.dma_sems)
        if self.dma_vals[si] > 0:
            self._need(E, ("dma", si, self.dma_vals[si]))
        self.dma_vals[si] += 16
        ins = E.eng.dma_start(out=out.ap, in_=in_.ap, **kw)
        ins.then_inc(self.dma_sems[si], 16)
        self.n_inst += 1
        tok = ("dma", si, self.dma_vals[si])
        self._post(tok, f"dma{si}", [in_], [out])
        return tok

    def wait_all(self, engname, views):
        E = self.engs[engname]
        for v in views:
            for r in v.res:
                self._need(E, r.w)

    def mm(self, out, lhsT, rhs, start, stop, last=None, **kw):
        if last is None:
            last = stop
        return self.op("pe", lambda e: e.matmul(out.ap, lhsT.ap, rhs.ap, start=start, stop=stop, **kw),
                       [lhsT, rhs], [out], inc=last)

    def transpose(self, out, in_, ident, last=True):
        return self.op("pe", lambda e: e.transpose(out.ap, in_.ap, ident.ap), [in_, ident], [out], inc=last)

    def act(self, out, in_, func, bias=None, scale=1.0, accum_out=None, eng="act"):
        reads = [in_]
        kw = {}
        if bias is not None:
            if isinstance(bias, V):
                reads.append(bias)
                kw["bias"] = bias.ap
            else:
                kw["bias"] = bias
        if isinstance(scale, V):
            reads.append(scale)
            kw["scale"] = scale.ap
        else:
            kw["scale"] = scale
        writes = [out]
        if accum_out is not None:
            writes.append(accum_out)
            kw["accum_out"] = accum_out.ap
        return self.op(eng, lambda e: e.activation(out.ap, in_.ap, func, **kw), reads, writes)

    def tt(self, out, in0, in1, op, eng="dve"):
        return self.op(eng, lambda e: e.tensor_tensor(out.ap, in0.ap, in1.ap, op), [in0, in1], [out])

    def ts(self, out, in0, s1, op0, s2=None, op1=None, eng="dve", accum_out=None):
        reads = [in0]
        a1 = s1
        if isinstance(s1, V):
            reads.append(s1)
            a1 = s1.ap
        a2 = s2
        if isinstance(s2, V):
            reads.append(s2)
            a2 = s2.ap
        kw = {}
        writes = [out]
        if op1 is not None:
            kw["op1"] = op1
        if accum_out is not None:
            kw["accum_out"] = accum_out.ap
            writes.append(accum_out)
        return self.op(eng, lambda e: e.tensor_scalar(out.ap, in0.ap, a1, a2, op0, **kw), reads, writes)

    def stt(self, out, in0, scalar, in1, op0, op1, eng="dve"):
        reads = [in0, in1]
        a = scalar
        if isinstance(scalar, V):
            reads.append(scalar)
            a = scalar.ap
        return self.op(eng, lambda e: e.scalar_tensor_tensor(out.ap, in0.ap, a, in1.ap, op0, op1), reads, [out])

    def copy(self, out, in_, eng="dve"):
        if eng == "act":
            return self.op("act", lambda e: e.copy(out.ap, in_.ap), [in_], [out])
        return self.op(eng, lambda e: e.tensor_copy(out.ap, in_.ap), [in_], [out])

    def memset(self, out, val, eng="dve"):
        return self.op(eng, lambda e: e.memset(out.ap, val), [], [out])

    def reduce(self, out, in_, op, axis=AX.X, eng="dve"):
        return self.op(eng, lambda e: e.tensor_reduce(out.ap, in_.ap, axis, op), [in_], [out])

    def recip(self, out, in_):
        return self.op("dve", lambda e: e.reciprocal(out.ap, in_.ap), [in_], [out])

    def collective(self, kind, op, groups, in_, out):
        E = self.engs["pool"]
        self._pre(E, [in_], [out])
        if self.cc_sem is None:
            self.cc_sem = self.nc.alloc_semaphore("s_cc")
        self.cc_val += 1
        ins = E.eng.collective_compute(kind, op, replica_groups=groups, ins=[in_.ap], outs=[out.ap])
        ins.then_inc(self.cc_sem)
        self.n_inst += 1
        tok = ("cc", self.cc_val)
        self._post(tok, "cc", [in_], [out])
        return tok

    def make_arena(self, kbytes):
        self.arena_t = self.nc.alloc_sbuf_tensor("sb_arena", [128, kbytes * 256], F32)
        self.arena_words = kbytes * 256
        self.arena_off = 0
        self.arena_gen = 0

    def carve(self, name, shape, dtype):
        esz = 2 if dtype == BF16 else 4
        n = 1
        for s in shape[1:]:
            n *= s
        words = (n * esz + 3) // 4
        words = (words + 7) // 8 * 8
        assert self.arena_off + words <= self.arena_words, f"arena overflow for {name}: {self.arena_off}+{words}>{self.arena_words}"
        raw = self.arena_t[0:shape[0], self.arena_off:self.arena_off + words]
        self.arena_off += words
        ap = raw.bitcast(dtype) if dtype != F32 else raw
        ap = ap[:, 0:n]
        if len(shape) > 2:
            names = " ".join(f"d{i}" for i in range(1, len(shape)))
            kw = {f"d{i}": shape[i] for i in range(1, len(shape))}
            ap = ap.rearrange(f"p ({names}) -> p {names}", **kw)
        return Buf(self, f"{name}@{self.arena_gen}", _APHandle(ap))

    def barrier(self):
        for E in self.engs.values():
            for P in self.engs.values():
                if P is not E and P.count > 0:
                    self._need(E, ("eng", P.name, P.count))
            for si, val in enumerate(self.dma_vals):
                if val > 0:
                    self._need(E, ("dma", si, val))
            if self.cc_val > 0:
                self._need(E, ("cc", self.cc_val))

    def phase_reset(self):
        self.barrier()
        self.arena_off = 0
        self.arena_gen += 1


class _APHandle:
    def __init__(self, ap):
        self._ap = ap

    def __getitem__(self, idx):
        return self._ap[idx]

    def ap(self):
        return self._ap


import math
KCUT = 9

D = 1024
NCH = 16
TOK = 2048
NTOK = TOK + 256
NLC = 18
DIN = 6176
EPS = 1e-6
GROUPS = [[0, 1, 2, 3], [4, 5, 6, 7]]
SW = 1568
NP = 3 * 1536 + 1536 + 32 + 32 + 16 + 128
RET_EXP_F = (5.0, 6.0, 7.0, 8.0)
RET_EXP_B = (5.5, 6.5, 7.5, 8.5)
BLOCKS = [("aq", 0, 512), ("ak", 512, 512), ("av", 1024, 512), ("ag", 1536, 512),
          ("xbc0", 2048, 512), ("xbc1", 2560, 512), ("xbc2", 3072, 512), ("dtr", 3584, 32),
          ("z0", 3616, 512), ("z1", 4128, 512), ("rqk", 4640, 512), ("rv", 5152, 512), ("rg", 5664, 512)]


def lam_init_of(layer):
    return 0.8 - 0.6 * math.exp(-0.3 * layer)


def build(depth=2, debug=None, stop_after=None):
    debug = debug or {}
    nc = bass.Bass("TRN2", target_bir_lowering=False)
    fw = FW(nc, same_engine_sync=True)
    I = lambda n, s, d=F32: fw.dram(n, s, d, kind="ExternalInput")
    x_d = I("x", [TOK, D])
    ctx_d = I("ctx", [256, D])
    cc_d = I("cc", [128, 8, 2])
    wada_d = I("w_ada", [2, D, 3 * D])
    bada_f_d = I("b_ada_f", [2, 128, 16])
    bada_d = I("b_ada", [2, 3 * D])
    win_d = I("w_in", [2, D, DIN])
    wout_d = I("w_out", [2, 2 * D, D])
    qkg_d = I("qkg", [2, 128, 2, 64])
    lamv_d = I("lamv", [2, 128, 4, 64])
    subln_d = I("subln", [2, 128, 1])
    rope_d = I("rope", [128, NCH, 2, 32])
    tri_d = I("tri", [128, 5, 128])
    rett_d = I("rett", [128, 4 * 128 + 24])
    ssdp_d = I("ssdp", [2, 128, NP])
    ssdn_d = I("ssdn", [2, 128, 8])
    cmask_d = I("cmask", [128, 8])
    out_d = fw.dram("out", [TOK, D], F32, kind="ExternalOutput")
    dbg = {k: fw.dram("dbg_" + k, shape, dt_, kind="ExternalOutput") for k, (shape, dt_) in debug.items()}

    proj_s = fw.dram("proj_s", [NTOK, DIN], F32)
    qT_s = fw.dram("qT_s", [4, 128, NTOK], BF16)
    agT_s = fw.dram("agT_s", [4, 128, NTOK], BF16)
    mixT_s = fw.dram("mixT_s", [16, 128, NTOK], BF16)
    kc_s = fw.dram("kc_s", [4, 128, 256], BF16)
    vc_s = fw.dram("vc_s", [256, 512], BF16)
    xbc_pad = fw.dram("xbc_pad", [TOK + 2, 1536], F32)
    xbc_cpad = fw.dram("xbc_cpad", [258, 1536], F32)
    hx_stage = fw.dram("hx_stage", [2, 1536], F32)
    hx_src = fw.dram("hx_src", [8, 1536], F32)
    hx_dst = fw.dram("hx_dst", [8, 1536], F32)
    hxL = fw.dram("hxL", [9, 1536], F32)
    hxR = fw.dram("hxR", [8, 1536], F32)
    cst_s = fw.dram("cst_s", [NLC, 2, 128, SW], F32)
    sb_s = fw.dram("sb_s", [NLC, 128, 1536], BF16)
    st_stage = [fw.dram(f"st_stage{d}", [128, SW], F32) for d in range(2)]
    st_src = [fw.dram(f"st_src{d}", [512, SW], F32) for d in range(2)]
    st_dst = [fw.dram(f"st_dst{d}", [512, SW], F32) for d in range(2)]
    kv_src = [fw.dram(f"kv_src{h}", [512, 4096], BF16) for h in range(4)]
    kv_dst = [fw.dram(f"kv_dst{h}", [512, 4096], BF16) for h in range(4)]
    kv_stage = [fw.dram(f"kv_stage{h}", [128, 4096], BF16) for h in range(4)]

    x_sb = fw.sbuf("x_sb", [128, NCH, D], F32)
    ctx_sb = fw.sbuf("ctx_sb", [128, 2, D], F32)
    gate = fw.sbuf("gate", [128, 2, D], F32)
    sc1 = fw.sbuf("sc1", [128, 2, 8], F32)
    sh = fw.sbuf("sh", [128, 2, 8], F32)
    ident = fw.sbuf("ident", [128, 128], F32)
    ident_bf = fw.sbuf("ident_bf", [128, 128], BF16)
    ones_bf = fw.sbuf("ones_bf", [128, 128], BF16)
    eps_t = fw.sbuf("eps_t", [128, 1], F32)
    cc = fw.sbuf("cc", [128, 8, 2], F32)
    rope = fw.sbuf("rope", [128, NCH, 2, 32], F32)
    qkg = fw.sbuf("qkg", [128, 2, 64], F32)
    lamv = fw.sbuf("lamv", [128, 4, 64], F32)
    neglam = fw.sbuf("neglam", [128, 1], F32)
    subln = fw.sbuf("subln", [128, 1], F32)
    small = fw.sbuf("small", [128, 64], F32)
    cmask = fw.sbuf("cmask", [128, 8], F32)
    fw.make_arena(119)
    pb_t = nc.alloc_psum_tensor("ps_banks", [128, 8, 512], F32)
    pb = [Buf(fw, f"pb{i}", _APHandle(pb_t[:, i, :])) for i in range(8)]
    pbh = [Buf(fw, f"pbh{i}", _APHandle(pb_t[:, i, :].bitcast(BF16))) for i in range(8)]
    for i in range(8):
        pbh[i].whole = pb[i].whole
    rank = nc.partition_id() % 4

    fw.memset(ident[:], 1.0, eng="pool")
    fw.op("pool", lambda e: e.affine_select(ident.t[:], ident.t[:], [[-1, 128]], ALU.is_equal, 0.0,
                                             base=0, channel_multiplier=1), [ident[:]], [ident[:]])
    fw.copy(ident_bf[:], ident[:])
    fw.memset(ones_bf[:], 1.0)
    fw.memset(eps_t[:], EPS)
    fw.dma("sp", V(x_sb.t[:], x_sb.whole), V(x_d.t.ap().rearrange("(c p) d -> p c d", p=128), x_d.whole))
    fw.dma("sp", V(ctx_sb.t[:], ctx_sb.whole), V(ctx_d.t.ap().rearrange("(c p) d -> p c d", p=128), ctx_d.whole))
    fw.dma("sp", cc[:], cc_d[:])
    fw.dma("sp", rope[:], rope_d[:])
    fw.dma("sp", cmask[:], cmask_d[:])
    fw.act(cc[:], cc[:], AF.Silu)
    zt = fw.carve("zt", [128, 4096], BF16)
    fw.memset(zt[:], 0.0)
    for h in range(4):
        fw.dma("sp", V(kv_src[h].t.ap().rearrange("(r p) c -> p r c", p=128), kv_src[h].whole),
               V(zt.t[:].unsqueeze(1).to_broadcast([128, 4, 4096]), zt.whole))
    zf = fw.carve("zf", [128, SW], F32)
    fw.memset(zf[:], 0.0)
    fw.dma("sp", xbc_cpad[0:1, :], zf[0:1, 0:1536])
    fw.dma("sp", xbc_cpad[257:258, :], zf[0:1, 0:1536])
    fw.dma("sp", hx_src[:, :], zf[0:8, 0:1536])
    fw.dma("sp", hxL[:, :], zf[0:9, 0:1536])
    fw.dma("sp", hxR[:, :], zf[0:8, 0:1536])
    for d in range(2):
        fw.dma("sp", V(st_src[d].t.ap().rearrange("(r p) c -> p r c", p=128), st_src[d].whole),
               V(zf.t[:].unsqueeze(1).to_broadcast([128, 4, SW]), zf.whole))
        fw.dma("sp", st_stage[d][:, :], zf[:, :])
    fw.phase_reset()

    def adaln(l):
        ccrep = fw.carve("ccrep", [128, 8, 2, 128], F32)
        badaf = fw.carve("badaf", [128, 16], F32)
        gbias = fw.carve("gbias", [128, D], F32)
        wada_sb = fw.carve("wada_sb", [128, 8, 512], F32)
        fw.copy(ccrep[:], V(cc.t[:].unsqueeze(3).to_broadcast([128, 8, 2, 128]), cc.whole))
        fw.dma("sp", badaf[:], bada_f_d[l])
        fw.dma("sp", gbias[:], V(bada_d.t[l:l + 1, 2 * D:3 * D].partition_broadcast(128), bada_d.whole))
        ps_s = V(pb[2].t[:, 0:32].rearrange("p (a b) -> p a b", b=2), pb[2].whole)
        for piece in range(6):
            fw.dma("sp", V(wada_sb.t[:], wada_sb.whole),
                   V(wada_d.t[l, :, piece * 512:(piece + 1) * 512].rearrange("(k p) c -> p k c", p=128), wada_d.whole))
            if piece < 4:
                for j in range(4):
                    blk = piece * 4 + j
                    for k in range(8):
                        fw.mm(V(ps_s.ap[:, blk, :], ps_s.res), wada_sb[:, k, j * 128:(j + 1) * 128], cc[:, k, :],
                              start=(k == 0), stop=(k == 7))
            else:
                half = piece - 4
                for v in range(2):
                    for k in range(8):
                        fw.mm(pb[3][:, :], ccrep[:, k, v, :], wada_sb[:, k, :], start=(k == 0), stop=(k == 7))
                    fw.tt(gate[:, v, half * 512:(half + 1) * 512], pb[3][:, :], gbias[:, half * 512:(half + 1) * 512], ALU.add)
        for v in range(2):
            fw.tt(sh[:, v, :], V(ps_s.ap[:, 0:8, v], ps_s.res), badaf[:, 0:8], ALU.add)
            fw.tt(sc1[:, v, :], V(ps_s.ap[:, 8:16, v], ps_s.res), badaf[:, 8:16], ALU.add)
        fw.ts(sc1[:], sc1[:], 1.0, ALU.add)
        fw.phase_reset()

    def layer_params(l):
        fw.dma("sp", qkg[:], qkg_d[l])
        fw.dma("sp", lamv[:], lamv_d[l])
        fw.dma("sp", subln[:], subln_d[l])
        fw.tt(small[:, 0:64], lamv[:, 0, :], lamv[:, 1, :], ALU.mult)
        s1 = fw.sbuf(f"lam_s1_{l}", [128, 1], F32)
        s2 = fw.sbuf(f"lam_s2_{l}", [128, 1], F32)
        fw.reduce(s1[:], small[:, 0:64], ALU.add)
        fw.tt(small[:, 0:64], lamv[:, 2, :], lamv[:, 3, :], ALU.mult)
        fw.reduce(s2[:], small[:, 0:64], ALU.add)
        fw.act(s1[:], s1[:], AF.Exp)
        fw.act(s2[:], s2[:], AF.Exp)
        fw.tt(neglam[:], s2[:], s1[:], ALU.subtract)
        fw.ts(neglam[:], neglam[:], -lam_init_of(l), ALU.add)
        fw.ts(subln[:], subln[:], 1.0 - lam_init_of(l), ALU.mult)

    def phase_proj(l, need_ctx_q):
        hT = fw.carve("hT", [128, 8, NTOK], BF16)
        xn = fw.carve("xn", [128, D], F32)
        junk = fw.carve("junk", [128, D], F32)
        ss = fw.carve("ss", [128, 1], F32)
        rs = fw.carve("rs", [128, 1], F32)
        wblk = [fw.carve(f"wblk{i}", [128, 8, 512], BF16) for i in range(2)]
        wst = fw.carve("wst", [128, 8, 512], F32)
        stage = [fw.carve(f"stage{i}", [128, 512], F32) for i in range(3)]
        sq = fw.carve("sq", [128, 8, 64], F32)
        qn = fw.carve("qn", [128, 8, 64], F32)
        t1 = fw.carve("t1", [128, 8, 32], F32)
        t2 = fw.carve("t2", [128, 8, 32], F32)
        ss8 = fw.carve("ss8", [128, 8], F32)
        qbf = [fw.carve(f"qbf{i}", [128, 8, 64], BF16) for i in range(2)]
        tb = [fw.carve(f"tb{i}", [128, 4, 128], BF16) for i in range(2)]
        vbf = [fw.carve(f"vbf{i}", [128, 512], BF16) for i in range(2)]

        for lc in range(NLC):
            src = x_sb[:, lc, :] if lc < NCH else ctx_sb[:, lc - NCH, :]
            v = 0 if lc < NCH else 1
            fw.act(junk[:], src, AF.Square, accum_out=ss[:])
            fw.act(rs[:], ss[:], AF.Sqrt, bias=eps_t[:], scale=1.0 / D)
            fw.recip(rs[:], rs[:])
            fw.ts(xn[:], src, rs[:], ALU.mult)
            pt = V(pb[lc % 2 * 2].t[:, :], [pb[lc % 2 * 2].whole, pb[lc % 2 * 2 + 1].whole])
            ptt = pb_t[:, lc % 2 * 2:lc % 2 * 2 + 2, :].rearrange("p a (k c) -> p (a k) c", c=128)
            for k in range(8):
                fw.transpose(V(ptt[:, k, :], pt.res), xn[:, k * 128:(k + 1) * 128], ident[:], last=(k == 7))
            for k in range(8):
                fw.ts(hT[:, k, lc * 128:(lc + 1) * 128], V(ptt[:, k, :], pt.res), sc1[:, v, k:k + 1], ALU.mult,
                      sh[:, v, k:k + 1], ALU.add)
        if "hT" in dbg:
            fw.dma("sp", V(dbg["hT"].t[:], dbg["hT"].whole), V(hT.t[:], hT.whole))

        it = 0
        for bi, (bname, col0, ncols) in enumerate(BLOCKS):
            wb = wblk[bi % 2]
            fw.dma("sp", V(wst.t[:, :, 0:ncols], wst.whole),
                   V(win_d.t[l, :, col0:col0 + ncols].rearrange("(k p) c -> p k c", p=128), win_d.whole))
            fw.copy(V(wb.t[:, :, 0:ncols], wb.whole), V(wst.t[:, :, 0:ncols], wst.whole), eng="pool")
            for lc in range(NLC):
                is_ctx = lc >= NCH
                bank = pb[4 + it % 2]
                it += 1
                for k in range(8):
                    fw.mm(bank[:, 0:ncols], hT[:, k, lc * 128:(lc + 1) * 128], wb[:, k, 0:ncols],
                          start=(k == 0), stop=(k == 7))
                rows = slice(lc * 128, (lc + 1) * 128)
                if bname in ("aq", "ak"):
                    if bname == "aq" and is_ctx and not need_ctx_q:
                        continue
                    gi = 0 if bname == "aq" else 1
                    psv = V(bank.t[:, :].rearrange("p (a b) -> p a b", b=64), bank.whole)
                    fw.act(sq[:], psv, AF.Square)
                    fw.reduce(ss8[:], sq[:], ALU.add)
                    fw.act(ss8[:], ss8[:], AF.Sqrt, bias=eps_t[:], scale=1.0 / 64)
                    fw.recip(ss8[:], ss8[:])
                    fw.tt(qn[:], psv, V(ss8.t[:].unsqueeze(2).to_broadcast([128, 8, 64]), ss8.whole), ALU.mult)
                    fw.tt(qn[:], qn[:], V(qkg.t[:, gi:gi + 1, :].to_broadcast([128, 8, 64]), qkg.whole), ALU.mult, eng="pool")
                    qo = qbf[it % 2]
                    if not is_ctx:
                        cosb = V(rope.t[:, lc, 0:1, :].to_broadcast([128, 8, 32]), rope.whole)
                        sinb = V(rope.t[:, lc, 1:2, :].to_broadcast([128, 8, 32]), rope.whole)
                        fw.tt(t1[:], qn[:, :, 0:32], cosb, ALU.mult)
                        fw.tt(t2[:], qn[:, :, 32:64], sinb, ALU.mult, eng="pool")
                        fw.tt(qo[:, :, 0:32], t1[:], t2[:], ALU.subtract)
                        fw.tt(t1[:], qn[:, :, 0:32], sinb, ALU.mult)
                        fw.tt(t2[:], qn[:, :, 32:64], cosb, ALU.mult, eng="pool")
                        fw.tt(qo[:, :, 32:64], t1[:], t2[:], ALU.add)
                    else:
                        fw.copy(qo[:], qn[:])
                    tbank = pbh[6 + lc % 2]
                    tbv = tbank.t[:, 0:512].rearrange("p (h c) -> p h c", c=128)
                    qof = qo.t[:].rearrange("p a b -> p (a b)")
                    for hd in range(4):
                        fw.transpose(V(tbv[:, hd, :], tbank.whole), V(qof[:, hd * 128:(hd + 1) * 128], qo.whole), ident_bf[:], last=(hd == 3))
                    tbs = tb[lc % 2]
                    fw.copy(tbs[:], V(tbv, tbank.whole), eng="act")
                    if bname == "aq":
                        fw.dma("act", V(qT_s.t[:, :, rows].rearrange("h p c -> p h c"), qT_s.part(lc).res), tbs[:])
                    elif is_ctx:
                        c0 = (lc - NCH) * 128
                        fw.dma("act", V(kc_s.t[:, :, c0:c0 + 128].rearrange("h p c -> p h c"), kc_s.part(lc).res), tbs[:])
                    else:
                        for hd in range(4):
                            fw.dma("act", V(kv_stage[hd].t[:, lc * 128:(lc + 1) * 128], kv_stage[hd].part(("k", lc)).res),
                                   tbs[:, hd, :])
                elif bname == "av":
                    vb = vbf[lc % 2]
                    fw.copy(vb[:], bank[:, :], eng="act")
                    if is_ctx:
                        c0 = (lc - NCH) * 128
                        fw.dma("act", V(vc_s.t[c0:c0 + 128, :], vc_s.part(lc).res), vb[:])
                    else:
                        for hd in range(4):
                            fw.dma("act", V(kv_stage[hd].t[:, 2048 + lc * 128:2048 + (lc + 1) * 128],
                                            kv_stage[hd].part(("v", lc)).res), vb[:, hd * 128:(hd + 1) * 128])
                elif bname == "ag":
                    st = stage[lc % 3]
                    fw.act(st[:], bank[:, :], AF.Silu)
                    tbank = pb[6 + lc % 2]
                    for hd in range(4):
                        fw.transpose(tbank[:, hd * 128:(hd + 1) * 128], st[:, hd * 128:(hd + 1) * 128], ident[:], last=(hd == 3))
                    tbs = tb[lc % 2]
                    fw.copy(V(tbs.t[:].rearrange("p h c -> p (h c)"), tbs.whole), tbank[:, :])
                    fw.dma("act", V(agT_s.t[:, :, rows].rearrange("h p c -> p h c"), agT_s.part(lc).res), tbs[:])
                else:
                    st = stage[lc % 3]
                    if lc % 2 == 0:
                        fw.copy(st[:, 0:ncols], bank[:, 0:ncols])
                    else:
                        fw.copy(st[:, 0:ncols], bank[:, 0:ncols], eng="act")
                    if bname.startswith("xbc"):
                        xc = (int(bname[3]) * 512)
                        if is_ctx:
                            r1 = 1 + (lc - NCH) * 128
                            fw.dma("sp", V(xbc_cpad.t[r1:r1 + 128, xc:xc + 512], xbc_cpad.part((bname, lc)).res), st[:, 0:ncols])
                        else:
                            r1 = 1 + lc * 128
                            fw.dma("sp", V(xbc_pad.t[r1:r1 + 128, xc:xc + 512], xbc_pad.part((bname, lc)).res), st[:, 0:ncols])
                    else:
                        fw.dma("sp", V(proj_s.t[rows, col0:col0 + ncols], proj_s.part((bname, lc)).res), st[:, 0:ncols])
            if bname == "av":
                for hd in range(4):
                    allres = [kv_stage[hd].part(("k", c)).res for c in range(NCH)] + [kv_stage[hd].part(("v", c)).res for c in range(NCH)]
                    fw.dma("sp", V(kv_src[hd].t[bass.ds(rank * 128, 128), :], kv_src[hd].whole), V(kv_stage[hd].t[:, :], allres))
                    fw.collective("AllReduce", ALU.add, GROUPS, kv_src[hd][:, :], kv_dst[hd][:, :])
        fw.phase_reset()

    def phase_attn(l, with_ctx_q):
        kT = fw.carve("kT", [128, 8448], BF16)
        vv = fw.carve("vv", [128, 66, 128], BF16)
        qh = fw.carve("qh", [128, NTOK], BF16)
        agh = fw.carve("agh", [128, NTOK], BF16)
        pT = [fw.carve(f"pT{i}", [128, 512], BF16) for i in range(3)]
        rec = fw.carve("rec", [128, 512], F32)
        om = [fw.carve(f"om{i}", [128, 512], F32) for i in range(2)]
        A = fw.carve("A", [128, 512], F32)
        sqb = fw.carve("sqb", [128, 512], BF16)
        rstd = fw.carve("rstd", [128, 512], F32)
        mixo = [fw.carve(f"mixo{i}", [128, 512], BF16) for i in range(2)]
        qblocks = [(q0, 512, 0, 66) for q0 in range(0, TOK, 512)]
        if with_ctx_q:
            qblocks.append((TOK, 256, 64, 66))
        for hd in range(4):
            fw.dma("sp", V(kT.t[:, 0:8192].rearrange("p (r c) -> p r c", r=4), kT.whole),
                   V(kv_dst[hd].t[:, 0:2048].rearrange("(r p) c -> p r c", p=128), kv_dst[hd].whole))
            fw.dma("sp", kT[:, 8192:8448], V(kc_s.t[hd], [kc_s.part(16).res, kc_s.part(17).res]))
            fw.dma("sp", V(vv.t[:, 0:64, :].rearrange("p (r c) e -> p r c e", r=4), vv.whole),
                   V(kv_dst[hd].t[:, 2048:4096].rearrange("(r p) (c e) -> p r c e", p=128, e=128), kv_dst[hd].whole))
            fw.dma("sp", vv[:, 64:66, :], V(vc_s.t[:, hd * 128:(hd + 1) * 128].rearrange("(c p) e -> p c e", p=128),
                                           [vc_s.part(16).res, vc_s.part(17).res]))
            qres = [qT_s.part(c).res for c in range(NLC if with_ctx_q else NCH)]
            nq = NTOK if with_ctx_q else TOK
            fw.dma("sp", qh[:, 0:nq], V(qT_s.t[hd, :, 0:nq], qres))
            fw.dma("sp", agh[:, 0:NTOK], V(agT_s.t[hd], [agT_s.part(c).res for c in range(NLC)]))
            for qi, (q0, nq_, kc0, kc1) in enumerate(qblocks):
                for m in range(2):
                    lo, hi = m * 64, (m + 1) * 64
                    ps_o, ps_n = pb[2 + m], pb[4 + m]
                    kcs = list(range(kc0, kc1))

                    def qk(i):
                        kc = kcs[i]
                        fw.mm(pb[i % 2][:, 0:nq_], kT[lo:hi, kc * 128:(kc + 1) * 128], qh[lo:hi, q0:q0 + nq_], True, True)

                    qk(0)
                    for i, kc in enumerate(kcs):
                        if i + 1 < len(kcs):
                            qk(i + 1)
                        p = pT[i % 3]
                        fw.act(p[:, 0:nq_], pb[i % 2][:, 0:nq_], AF.Exp, scale=0.125)
                        first, lastk = (i == 0), (i == len(kcs) - 1)
                        fw.mm(ps_o[:, 0:nq_], vv[:, kc, :], p[:, 0:nq_], start=first, stop=lastk)
                        fw.mm(ps_n[:, 0:nq_], ones_bf[:], p[:, 0:nq_], start=first, stop=lastk)
                    fw.recip(rec[:, 0:nq_], ps_n[:, 0:nq_])
                    fw.tt(om[m][:, 0:nq_], ps_o[:, 0:nq_], rec[:, 0:nq_], ALU.mult)
                fw.stt(A[:, 0:nq_], om[1][:, 0:nq_], neglam[:], om[0][:, 0:nq_], ALU.mult, ALU.add)
                fw.act(sqb[:, 0:nq_], A[:, 0:nq_], AF.Square)
                fw.mm(pb[6][:, 0:nq_], ones_bf[:], sqb[:, 0:nq_], True, True)
                fw.act(rstd[:, 0:nq_], pb[6][:, 0:nq_], AF.Sqrt, bias=eps_t[:], scale=1.0 / 128)
                fw.recip(rstd[:, 0:nq_], rstd[:, 0:nq_])
                fw.stt(A[:, 0:nq_], A[:, 0:nq_], subln[:], rstd[:, 0:nq_], ALU.mult, ALU.mult)
                mo = mixo[qi % 2]
                fw.tt(mo[:, 0:nq_], A[:, 0:nq_], agh[:, q0:q0 + nq_], ALU.mult, eng="pool")
                fw.dma("act", V(mixT_s.t[hd, :, q0:q0 + nq_], mixT_s.part((hd, qi)).res), mo[:, 0:nq_])
        fw.phase_reset()


    def phase_halo(l):
        xr = lambda c: [xbc_pad.part((f"xbc{i}", c)).res for i in range(3)]
        fw.dma("sp", hx_stage[0:1, :], V(xbc_pad.t[1:2, :], xr(0)))
        fw.dma("sp", hx_stage[1:2, :], V(xbc_pad.t[TOK:TOK + 1, :], xr(NCH - 1)))
        fw.dma("sp", V(hx_src.t[bass.ds(rank * 2, 2), :], hx_src.whole), hx_stage[:, :])
        fw.collective("AllReduce", ALU.add, GROUPS, hx_src[:, :], hx_dst[:, :])
        fw.dma("sp", hxL[1:9, :], hx_dst[:, :])
        fw.dma("sp", hxR[0:6, :], hx_dst[2:8, :])
        fw.dma("sp", V(xbc_pad.t[0:1, :], xbc_pad.part("hl").res), V(hxL.t[bass.ds(rank * 2, 1), :], hxL.whole))
        fw.dma("sp", V(xbc_pad.t[TOK + 1:TOK + 2, :], xbc_pad.part("hr").res), V(hxR.t[bass.ds(rank * 2, 1), :], hxR.whole))

    def rope_apply(out, x, lc, nh, t1, t2):
        cosb = V(rope.t[:, lc, 0:1, :].to_broadcast([128, nh, 32]), rope.whole)
        sinb = V(rope.t[:, lc, 1:2, :].to_broadcast([128, nh, 32]), rope.whole)
        fw.tt(t1[:, 0:nh, :], x[:, :, 0:32], cosb, ALU.mult)
        fw.tt(t2[:, 0:nh, :], x[:, :, 32:64], sinb, ALU.mult, eng="pool")
        fw.tt(out[:, :, 0:32], t1[:, 0:nh, :], t2[:, 0:nh, :], ALU.subtract)
        fw.tt(t1[:, 0:nh, :], x[:, :, 0:32], sinb, ALU.mult)
        fw.tt(t2[:, 0:nh, :], x[:, :, 32:64], cosb, ALU.mult, eng="pool")
        fw.tt(out[:, :, 32:64], t1[:, 0:nh, :], t2[:, 0:nh, :], ALU.add)

    def chain_combine(Sin, Sctx, d, col0, ncol, nparts, Aexp_of, tmp, slot):
        fw.copy(Sin[0:nparts, :], Sctx[0:nparts, :])
        order = range(4) if d == 0 else range(3, -1, -1)
        for sidx in order:
            fw.dma("sp", slot[0:nparts, 0:ncol], st_dst[d][sidx * 128:sidx * 128 + nparts, col0:col0 + ncol])
            Aexp_of(sidx, tmp)
            fw.tt(tmp[0:nparts, :], tmp[0:nparts, :], slot[0:nparts, 0:ncol], ALU.add)
            fw.tt(tmp[0:nparts, :], tmp[0:nparts, :], Sin[0:nparts, :], ALU.subtract)
            mcol = cmask[:, d * 4 + sidx:d * 4 + sidx + 1]
            fw.stt(Sin[0:nparts, :], tmp[0:nparts, :], V(mcol.ap[0:nparts], mcol.res), Sin[0:nparts, :], ALU.mult, ALU.add)

    def phase_ret(l, need_ctx):
        rett = fw.carve("rett", [128, 4 * 128 + 24], F32)
        rnorm = fw.carve("rnorm", [128, 128], F32)
        fw.dma("sp", rett[:], rett_d[:])
        fw.dma("sp", rnorm[:], ssdp_d[l, :, NP - 128:NP])
        Dret = V(rett.t[:, 0:512].rearrange("p (h i) -> p h i", i=128), rett.whole)
        tab = lambda k: V(rett.t[:, 512 + 4 * k:512 + 4 * k + 4], rett.whole)
        qk = fw.carve("qk", [128, 8, 64], F32)
        qkr = fw.carve("qkr", [128, 8, 64], F32)
        t1 = fw.carve("rt1", [128, 8, 32], F32)
        t2 = fw.carve("rt2", [128, 8, 32], F32)
        rv = fw.carve("rv", [128, 512], F32)
        rvbf = fw.carve("rvbf", [128, 512], BF16)
        kte = [fw.carve(f"kte{d}", [128, 4, 64], BF16) for d in range(2)]
        q3 = fw.carve("q3", [128, 3, 4, 64], BF16)
        kbf = fw.carve("kbf", [128, 4, 64], BF16)
        qT = fw.carve("qT", [64, 3, 4, 128], BF16)
        kT = fw.carve("kT", [64, 4, 128], BF16)
        Wt = fw.carve("Wt", [128, 4, 128], BF16)
        R = [fw.carve(f"R{d}", [64, 512], F32) for d in range(2)]
        Rbf = [fw.carve(f"Rbf{d}", [64, 512], BF16) for d in range(2)]
        Rctx = [fw.carve(f"Rctx{d}", [64, 512], F32) for d in range(2)]
        Pb = fw.carve("Pb", [128, 4], F32)
        cs_sb = [fw.carve(f"cs_sb{d}", [64, 512], F32) for d in range(2)]
        tmp = fw.carve("rtmp", [64, 512], F32)
        slot = fw.carve("rslot", [64, 512], F32)
        rg = fw.carve("rg", [128, 512], F32)
        ysq = fw.carve("ysq", [128, 4, 128], F32)
        yss = fw.carve("yss", [128, 4], F32)
        yn = fw.carve("yn", [128, 4, 128], F32)
        ybf = fw.carve("ybf", [128, 512], BF16)
        ytb = fw.carve("ytb", [128, 4, 128], BF16)
        bc4 = lambda v, n: V(v.ap.unsqueeze(2).to_broadcast([v.ap.shape[0], 4, n]), v.res)

        def prep(lc):
            is_ctx = lc >= NCH
            rows = slice(lc * 128, (lc + 1) * 128)
            pr = lambda n: proj_s.part((n, lc)).res
            fw.dma("sp", V(qk.t[:].rearrange("p a b -> p (a b)"), qk.whole), V(proj_s.t[rows, 4640:5152], pr("rqk")))
            fw.dma("sp", rv[:], V(proj_s.t[rows, 5152:5664], pr("rv")))
            if is_ctx:
                src = qk
            else:
                rope_apply(qkr, qk, lc, 8, t1, t2)
                src = qkr
            fw.copy(rvbf[:], rv[:], eng="pool")
            for d in range(2):
                fw.tt(kte[d][:], src[:, 4:8, :], bc4(tab(d), 64), ALU.mult)
            return src

        def chunk_states(start_banks=(0, 1)):
            for d in range(2):
                bank = pb[start_banks[d]]
                for h in range(4):
                    fw.mm(bank[0:64, h * 128:(h + 1) * 128], kte[d][:, h, :], rvbf[:, h * 128:(h + 1) * 128],
                          start=(h == 0), stop=(h == 3), skip_group_check=True)
            return [pb[start_banks[0]], pb[start_banks[1]]]

        A128 = [tab(4), tab(5)]

        def fold(acc, csb, lc, store):
            fw.tt(V(acc[0].t[:].rearrange("p (h n) -> p h n", n=128), acc[0].whole),
                  V(acc[0].t[:].rearrange("p (h n) -> p h n", n=128), acc[0].whole),
                  V(A128[0].ap[0:64].unsqueeze(2).to_broadcast([64, 4, 128]), A128[0].res), ALU.mult)
            fw.tt(acc[0][:], acc[0][:], csb[0][0:64, :], ALU.add)
            fw.tt(V(tmp.t[:].rearrange("p (h n) -> p h n", n=128), tmp.whole),
                  V(csb[1].t[0:64, :].rearrange("p (h n) -> p h n", n=128), csb[1].whole),
                  V(Pb.t[0:64, :].unsqueeze(2).to_broadcast([64, 4, 128]), Pb.whole), ALU.mult)
            fw.tt(acc[1][:], acc[1][:], tmp[:], ALU.add)
            fw.tt(Pb[:], Pb[:], A128[1], ALU.mult)
            if store:
                for d in range(2):
                    fw.copy(cs_sb[d][:], csb[d][0:64, :])
                    fw.dma("sp", V(cst_s.t[lc, d, 0:64, 1024:1536], cst_s.part(("r", lc, d)).res), cs_sb[d][:])

        for grp in ((16, 17), tuple(range(NCH))):
            for d in range(2):
                fw.memset(R[d][:], 0.0)
            fw.memset(Pb[:], 1.0)
            for lc in grp:
                prep(lc)
                if KCUT >= 2:
                    csb = chunk_states()
                if KCUT >= 3:
                    fold(R, csb, lc, KCUT >= 4)
            if grp[0] == 16:
                for d in range(2):
                    fw.copy(Rctx[d][:], R[d][:])
        if "ret_sf" in dbg:
            fw.dma("sp", dbg["ret_sf"][:, :], Rctx[0][:])
            fw.dma("sp", dbg["ret_sb"][:, :], Rctx[1][:])
        if stop_after == "ret_p1":
            fw.phase_reset(); return
        for d in range(2):
            fw.dma("sp", st_stage[d][0:64, 1024:1536], R[d][:])
            fw.dma("sp", V(st_src[d].t[bass.ds(rank * 128, 128), :], st_src[d].whole), st_stage[d][:, :])
            fw.collective("AllReduce", ALU.add, GROUPS, st_src[d][:, :], st_dst[d][:, :])
        if stop_after == "ret_x":
            fw.phase_reset(); return
        A2048 = [[(1.0 - 2.0 ** -e) ** 2048 for e in RET_EXP_F], [(1.0 - 2.0 ** -e) ** 2048 for e in RET_EXP_B]]
        Rin = [fw.carve(f"Rin{d}", [64, 512], F32) for d in range(2)]
        for d in range(2):
            def aexp(sidx, t_, d=d):
                for h in range(4):
                    fw.ts(t_[0:64, h * 128:(h + 1) * 128], Rin[d][0:64, h * 128:(h + 1) * 128], float(A2048[d][h]), ALU.mult)
            chain_combine(Rin[d], Rctx[d], d, 1024, 512, 64, aexp, tmp, slot)
        snap = fw.carve("rsnap", [64, 512], BF16)
        csl = fw.carve("rcsl", [64, 512], F32)
        fw.copy(R[1][:], Rin[1][:])
        for lc in range(NCH - 1, -1, -1):
            fw.copy(snap[:], R[1][:])
            fw.dma("sp", V(sb_s.t[lc, 0:64, 1024:1536], sb_s.part(("r", lc)).res), snap[:])
            fw.dma("sp", csl[:], V(cst_s.t[lc, 1, 0:64, 1024:1536], cst_s.part(("r", lc, 1)).res))
            fw.tt(V(R[1].t[:].rearrange("p (h n) -> p h n", n=128), R[1].whole),
                  V(R[1].t[:].rearrange("p (h n) -> p h n", n=128), R[1].whole),
                  V(A128[1].ap[0:64].unsqueeze(2).to_broadcast([64, 4, 128]), A128[1].res), ALU.mult)
            fw.tt(R[1][:], R[1][:], csl[:], ALU.add)
        if need_ctx:
            fw.dma("sp", csl[:], V(cst_s.t[17, 1, 0:64, 1024:1536], cst_s.part(("r", 17, 1)).res))
            fw.copy(snap[:], csl[:])
            fw.dma("sp", V(sb_s.t[16, 0:64, 1024:1536], sb_s.part(("r", 16)).res), snap[:])
            snap0 = fw.carve("rsnap0", [64, 512], BF16)
            fw.memset(snap0[:], 0.0)
            fw.dma("sp", V(sb_s.t[17, 0:64, 1024:1536], sb_s.part(("r", 17)).res), snap0[:])
        if stop_after == "ret_p2":
            fw.phase_reset(); return
        groups3 = [tuple(range(NCH))] + ([(16, 17)] if need_ctx else [])
        for grp in groups3:
            if grp[0] == 16:
                fw.memset(R[0][:], 0.0)
            else:
                fw.copy(R[0][:], Rin[0][:])
            for lc in grp:
                is_ctx = lc >= NCH
                rows = slice(lc * 128, (lc + 1) * 128)
                src = prep(lc)
                fw.dma("sp", rg[:], V(proj_s.t[rows, 5664:6176], proj_s.part(("rg", lc)).res))
                fw.dma("sp", Rbf[1][:], V(sb_s.t[lc, 0:64, 1024:1536], sb_s.part(("r", lc)).res))
                fw.copy(Rbf[0][:], R[0][:])
                fw.copy(q3[:, 0, :, :], src[:, 0:4, :])
                fw.tt(q3[:, 1, :, :], src[:, 0:4, :], bc4(tab(2), 64), ALU.mult)
                fw.tt(q3[:, 2, :, :], src[:, 0:4, :], bc4(tab(3), 64), ALU.mult, eng="pool")
                fw.copy(kbf[:], src[:, 4:8, :])
                tqa = pbh[2]
                tqav = tqa.t[0:64, 0:1024].rearrange("p (k h c) -> p k h c", k=2, h=4)
                tqb = pbh[3]
                tqbv = tqb.t[0:64, 0:512].rearrange("p (h c) -> p h c", h=4)
                for k3 in range(2):
                    for h in range(4):
                        fw.transpose(V(tqav[:, k3, h, :], tqa.whole), q3[:, k3, h, :], ident_bf[:], last=(k3 == 1 and h == 3))
                for h in range(4):
                    fw.transpose(V(tqbv[:, h, :], tqb.whole), q3[:, 2, h, :], ident_bf[:], last=(h == 3))
                fw.copy(qT[:, 0:2, :, :], V(tqav, tqa.whole))
                fw.copy(qT[:, 2, :, :], V(tqbv, tqb.whole))
                tk = pbh[4]
                tkv = tk.t[0:64, 0:512].rearrange("p (h c) -> p h c", h=4)
                for h in range(4):
                    fw.transpose(V(tkv[:, h, :], tk.whole), kbf[:, h, :], ident_bf[:], last=(h == 3))
                fw.copy(kT[:], V(tkv, tk.whole))
                sc = pb[5]
                for h in range(4):
                    fw.mm(sc[:, h * 128:(h + 1) * 128], kT[:, h, :], qT[:, 0, h, :], start=(h == 0), stop=(h == 3), skip_group_check=True)
                fw.tt(Wt[:], V(sc.t[:, :].rearrange("p (h i) -> p h i", i=128), sc.whole), Dret, ALU.mult)
                csb = chunk_states((0, 1))
                yb = pb[6]
                for h in range(4):
                    o = yb[:, h * 128:(h + 1) * 128]
                    fw.mm(o, Wt[:, h, :], rvbf[:, h * 128:(h + 1) * 128], start=(h == 0), stop=False, last=False, skip_group_check=True)
                    fw.mm(o, qT[:, 1, h, :], Rbf[0][:, h * 128:(h + 1) * 128], start=False, stop=False, last=False, skip_group_check=True)
                    fw.mm(o, qT[:, 2, h, :], Rbf[1][:, h * 128:(h + 1) * 128], start=False, stop=(h == 3), last=(h == 3), skip_group_check=True)
                fw.tt(V(R[0].t[:].rearrange("p (h n) -> p h n", n=128), R[0].whole),
                      V(R[0].t[:].rearrange("p (h n) -> p h n", n=128), R[0].whole),
                      V(A128[0].ap[0:64].unsqueeze(2).to_broadcast([64, 4, 128]), A128[0].res), ALU.mult)
                fw.tt(R[0][:], R[0][:], csb[0][0:64, :], ALU.add)
                ybv = V(yb.t[:, :].rearrange("p (h n) -> p h n", n=128), yb.whole)
                fw.act(ysq[:], ybv, AF.Square)
                fw.reduce(yss[:], ysq[:], ALU.add)
                fw.act(yss[:], yss[:], AF.Sqrt, bias=eps_t[:], scale=1.0 / 128)
                fw.recip(yss[:], yss[:])
                fw.tt(yn[:], ybv, V(yss.t[:].unsqueeze(2).to_broadcast([128, 4, 128]), yss.whole), ALU.mult)
                fw.tt(yn[:], yn[:], V(rnorm.t[:].unsqueeze(1).to_broadcast([128, 4, 128]), rnorm.whole), ALU.mult, eng="pool")
                fw.act(rg[:], rg[:], AF.Silu)
                fw.tt(ybf[:], V(yn.t[:].rearrange("p h n -> p (h n)"), yn.whole), rg[:], ALU.mult)
                to = pbh[7]
                tov = to.t[:, 0:512].rearrange("p (h c) -> p h c", h=4)
                for h in range(4):
                    fw.transpose(V(tov[:, h, :], to.whole), ybf[:, h * 128:(h + 1) * 128], ident_bf[:], last=(h == 3))
                fw.copy(ytb[:], V(tov, to.whole))
                fw.dma("sp", V(mixT_s.t[12:16, :, rows].rearrange("h p c -> p h c"), mixT_s.part(("ret", lc)).res), ytb[:])
        fw.phase_reset()

    def phase_ssd(l, need_ctx):
        OW, OB, ODT, OA, ODD = 0, 4608, 6144, 6176, 6208
        prm = fw.carve("prm", [128, 6224], F32)
        fw.dma("sp", prm[:], ssdp_d[l, :, 0:6224])
        tri = fw.carve("tri", [128, 5, 128], F32)
        fw.dma("sp", tri[:], tri_d[:])
        ssdn = fw.carve("ssdn", [128, 8], F32)
        fw.dma("sp", ssdn[:], ssdn_d[l])
        negA = fw.carve("negA", [128, 32], F32)
        fw.act(negA[:], prm[:, OA:OA + 32], AF.Exp)
        fw.ts(negA[:], negA[:], -1.0, ALU.mult)
        one_t = fw.carve("one_t", [128, 1], F32)
        fw.memset(one_t[:], 1.0)
        U = [fw.carve(f"U{i}", [128, 1536], F32) for i in range(3)]
        dtr = fw.carve("dtr", [128, 32], F32)
        la = fw.carve("la", [128, 32], F32)
        E = fw.carve("E", [128, 96], F32)
        praw = fw.carve("praw", [128, 96], F32)
        cumraw = Buf(fw, "cumraw", _APHandle(praw.t[:, 0:32]))
        cumraw.whole = praw.whole
        tots = fw.carve("tots", [128, 32], F32)
        v = [fw.carve(f"v{d}", [128, 1024], BF16) for d in range(2)]
        vte = [fw.carve(f"vte{d}", [128, 1024], BF16) for d in range(2)]
        BCbf = fw.carve("BCbf", [128, 512], BF16)
        BCT = fw.carve("BCT", [128, 4, 128], BF16)
        zt = fw.carve("zt", [128, 1024], F32)
        R1 = fw.carve("R1", [128, 16, 128], F32)
        seg = fw.carve("seg", [128, 16, 128], F32)
        Dm = fw.carve("Dm", [128, 16, 128], BF16)
        Sm = [fw.carve(f"Sm{d}", [128, 2, 128], F32) for d in range(2)]
        Wt = fw.carve("Wt", [128, 16, 128], BF16)
        S = [fw.carve(f"S{d}", [128, 1024], F32) for d in range(2)]
        Sx = [fw.carve(f"Sx{d}", [128, 1024], F32) for d in range(2)]
        Sbf = [fw.carve(f"Sbf{d}", [128, 1024], BF16) for d in range(2)]
        Pb = fw.carve("Pb", [128, 16], F32)
        yt = fw.carve("yt", [128, 1024], F32)
        y2 = fw.carve("y2", [128, 1024], F32)
        vtmp = Buf(fw, "vtmp", _APHandle(y2.t[:].rearrange("p (h n) -> p h n", n=64)))
        vtmp.whole = y2.whole
        gss = fw.carve("gss", [128, 2], F32)
        ybf = fw.carve("ybf", [128, 1024], BF16)
        ytb = fw.carve("ytb", [128, 8, 128], BF16)
        Aex = fw.carve("Aex", [128, 16], F32)
        h16 = lambda vv: V(vv.ap.unsqueeze(2).to_broadcast([128, 16, 64]), vv.res)
        as16 = lambda b_: V(b_.t[:].rearrange("p (h n) -> p h n", n=64), b_.whole)

        def prep(lc):
            is_ctx = lc >= NCH
            src_t, r0 = (xbc_cpad, (lc - NCH) * 128) if is_ctx else (xbc_pad, lc * 128)
            rr = [xbc_pad.part("hl").res, xbc_pad.part("hr").res]
            for k in range(3):
                fw.dma("sp", U[k][:], V(src_t.t[r0 + k:r0 + k + 128, :], rr))
            fw.dma("sp", dtr[:], V(proj_s.t[lc * 128:(lc + 1) * 128, 3584:3616], proj_s.part(("dtr", lc)).res))
            fw.tt(U[0][:], U[0][:], prm[:, OW:OW + 1536], ALU.mult, eng="pool")
            fw.tt(U[1][:], U[1][:], prm[:, OW + 1536:OW + 3072], ALU.mult)
            fw.tt(U[2][:], U[2][:], prm[:, OW + 3072:OW + 4608], ALU.mult, eng="pool")
            fw.tt(U[1][:], U[1][:], U[0][:], ALU.add)
            fw.tt(U[1][:], U[1][:], U[2][:], ALU.add)
            fw.tt(U[1][:], U[1][:], prm[:, OB:OB + 1536], ALU.add)
            fw.act(U[0][:], U[1][:], AF.Silu)
            fw.tt(dtr[:], dtr[:], prm[:, ODT:ODT + 32], ALU.add)
            fw.act(dtr[:], dtr[:], AF.Exp)
            fw.act(dtr[:], dtr[:], AF.Ln, bias=one_t[:])
            fw.tt(la[:], dtr[:], negA[:], ALU.mult)
            pe = pb[0]
            for i, (w, c0, c1) in enumerate(((0, 0, 16), (1, 16, 32), (2, 0, 16), (3, 16, 32), (4, 0, 32))):
                o0 = (0, 16, 32, 48, 64)[i]
                fw.mm(pe[:, o0:o0 + (c1 - c0)], tri[:, w, :], la[:, c0:c1], start=(i == 0), stop=(i == 4), skip_group_check=True)
            fw.copy(praw[:], pe[:, 0:96])
            fw.act(E[:], praw[:], AF.Exp)
            fw.tt(tots[:], tots[:], praw[:, 64:96], ALU.add)
            xs = V(U[0].t[:, 0:1024].rearrange("p (h n) -> p h n", n=64), U[0].whole)
            for d in range(2):
                fw.tt(vtmp[:], xs, h16(dtr[:, d * 16:(d + 1) * 16]), ALU.mult)
                fw.copy(as16(v[d]), vtmp[:], eng="pool")
                fw.tt(as16(vte[d]), vtmp[:], h16(E[:, 32 + d * 16:48 + d * 16]), ALU.mult)
            fw.copy(BCbf[:], U[0][:, 1024:1536], eng="pool")

        def chunk_state(d):
            banks = (pb[4], pb[5])
            for g in range(2):
                fw.mm(banks[g][:, :], BCbf[:, g * 128:(g + 1) * 128], vte[d][:, g * 512:(g + 1) * 512], True, True)
            return banks

        def mulA(dst, srcS, acol):
            fw.tt(as16(dst), as16(srcS), h16(acol), ALU.mult)

        for grp in ((16, 17), tuple(range(NCH))):
            for d in range(2):
                fw.memset(S[d][:], 0.0)
            fw.memset(Pb[:], 1.0)
            fw.memset(tots[:], 0.0)
            for lc in grp:
                prep(lc)
                for d in range(2):
                    banks = chunk_state(d)
                    csv = V(pb_t[:, 4:6, :], [pb[4].whole, pb[5].whole])
                    cs_sb = V(seg.t[:, d * 8:(d + 1) * 8, :].rearrange("p a (g c) -> p (a g) c", g=2)[:, 0:2, :] if False else seg.t[:, d * 8:(d + 1) * 8, :], seg.whole)
                    cs_flat = V(seg.t[:].rearrange("p h n -> p (h n)")[:, d * 1024:(d + 1) * 1024], seg.whole)
                    fw.copy(V(cs_flat.ap.rearrange("p (g c) -> p g c", g=2), seg.whole), csv)
                    fw.dma("sp", V(cst_s.t[lc, d, :, 0:1024], cst_s.part(("s", lc, d)).res), cs_flat)
                    if d == 0:
                        mulA(S[0], S[0], E[:, 64:80])
                        fw.tt(S[0][:], S[0][:], cs_flat, ALU.add)
                    else:
                        fw.tt(as16(y2), V(cs_flat.ap.rearrange("p (h n) -> p h n", n=64), seg.whole), h16(Pb[:, :]), ALU.mult)
                        fw.tt(S[1][:], S[1][:], y2[:], ALU.add)
                        fw.tt(Pb[:], Pb[:], E[:, 80:96], ALU.mult)
                fw.dma("sp", V(cst_s.t[lc, 0, :, 1536:1568], cst_s.part(("e", lc)).res), E[:, 64:96])
            if grp[0] == 16:
                for d in range(2):
                    fw.copy(Sx[d][:], S[d][:])
        if "ssd_sf" in dbg:
            fw.dma("sp", dbg["ssd_sf"][:, :], Sx[0][:])
            fw.dma("sp", dbg["ssd_sb"][:, :], Sx[1][:])
        for d in range(2):
            fw.dma("sp", st_stage[d][:, 0:1024], S[d][:])
            fw.dma("sp", st_stage[d][:, 1536:1552], tots[:, d * 16:(d + 1) * 16])
            fw.dma("sp", V(st_src[d].t[bass.ds(rank * 128, 128), :], st_src[d].whole), st_stage[d][:, :])
            fw.collective("AllReduce", ALU.add, GROUPS, st_src[d][:, :], st_dst[d][:, :])
        for d in range(2):
            order = range(4) if d == 0 else range(3, -1, -1)
            for sidx in order:
                fw.dma("sp", yt[:], st_dst[d][sidx * 128:(sidx + 1) * 128, 0:1024])
                fw.dma("sp", Aex[:], st_dst[d][sidx * 128:(sidx + 1) * 128, 1536:1552])
                fw.act(Aex[:], Aex[:], AF.Exp)
                mulA(y2, Sx[d], Aex[:, :])
                fw.tt(y2[:], y2[:], yt[:], ALU.add)
                fw.tt(y2[:], y2[:], Sx[d][:], ALU.subtract)
                fw.stt(Sx[d][:], y2[:], cmask[:, d * 4 + sidx:d * 4 + sidx + 1], Sx[d][:], ALU.mult, ALU.add)
        for lc in range(NCH - 1, -1, -1):
            fw.copy(Sbf[1][:], Sx[1][:])
            fw.dma("sp", V(sb_s.t[lc, :, 0:1024], sb_s.part(("s", lc)).res), Sbf[1][:])
            fw.dma("sp", yt[:], V(cst_s.t[lc, 1, :, 0:1024], cst_s.part(("s", lc, 1)).res))
            fw.dma("sp", Aex[:], V(cst_s.t[lc, 0, :, 1552:1568], cst_s.part(("e", lc)).res))
            mulA(Sx[1], Sx[1], Aex[:, :])
            fw.tt(Sx[1][:], Sx[1][:], yt[:], ALU.add)
        if need_ctx:
            fw.dma("sp", yt[:], V(cst_s.t[17, 1, :, 0:1024], cst_s.part(("s", 17, 1)).res))
            fw.copy(Sbf[1][:], yt[:])
            fw.dma("sp", V(sb_s.t[16, :, 0:1024], sb_s.part(("s", 16)).res), Sbf[1][:])
            fw.memset(Sbf[0][:], 0.0)
            fw.dma("sp", V(sb_s.t[17, :, 0:1024], sb_s.part(("s", 17)).res), Sbf[0][:])
        if stop_after == "ssd_p2":
            fw.phase_reset(); return
        groups3 = [tuple(range(NCH))] + ([(16, 17)] if need_ctx else [])
        for grp in groups3:
            if grp[0] == 16:
                fw.memset(Sx[0][:], 0.0)
            for lc in grp:
                rows = slice(lc * 128, (lc + 1) * 128)
                prep(lc)
                fw.dma("sp", zt[:, 0:512], V(proj_s.t[rows, 3616:4128], proj_s.part(("z0", lc)).res))
                fw.dma("sp", zt[:, 512:1024], V(proj_s.t[rows, 4128:4640], proj_s.part(("z1", lc)).res))
                fw.dma("sp", Sbf[1][:], V(sb_s.t[lc, :, 0:1024], sb_s.part(("s", lc)).res))
                fw.copy(Sbf[0][:], Sx[0][:], eng="pool")
                tb_ = pbh[1]
                tbv = tb_.t[:, 0:512].rearrange("p (a c) -> p a c", a=4)
                for a in range(4):
                    fw.transpose(V(tbv[:, a, :], tb_.whole), BCbf[:, a * 128:(a + 1) * 128], ident_bf[:], last=(a == 3))
                fw.copy(BCT[:], V(tbv, tb_.whole))
                sc = pb[1]
                scv = sc.t[:, 256:512].rearrange("p (g i) -> p g i", g=2)
                for g in range(2):
                    fw.mm(V(scv[:, g, :], sc.whole), BCT[:, g, :], BCT[:, 2 + g, :], start=False if False else (g == 0), stop=(g == 1), skip_group_check=True)
                for d in range(2):
                    fw.tt(Sm[d][:], V(scv, sc.whole), V(tri.t[:, d:d + 1, :].to_broadcast([128, 2, 128]), tri.whole), ALU.mult)
                yb = (pb[6], pb[7])
                for d in range(2):
                    fw.tt(R1[:], V(la.t[:, d * 16:(d + 1) * 16].unsqueeze(2).to_broadcast([128, 16, 128]), la.whole),
                          V(tri.t[:, d:d + 1, :].to_broadcast([128, 16, 128]), tri.whole), ALU.mult, eng="pool")
                    for q in range(4):
                        bank = pb[2 + q % 2]
                        fw.mm(bank[:, :], tri[:, 4, :], V(R1.t[:, 4 * q:4 * q + 4, :].rearrange("p h n -> p (h n)"), R1.whole), True, True)
                        for hh in range(4):
                            h = 4 * q + hh
                            fw.ts(seg[:, h, :], bank[:, hh * 128:(hh + 1) * 128], cumraw[:, d * 16 + h:d * 16 + h + 1], ALU.subtract, 0.0, ALU.min)
                    fw.act(Dm[:], seg[:], AF.Exp)
                    for g in range(2):
                        fw.tt(Wt[:, g * 8:(g + 1) * 8, :], Dm[:, g * 8:(g + 1) * 8, :],
                              V(Sm[d].t[:, g:g + 1, :].to_broadcast([128, 8, 128]), Sm[d].whole), ALU.mult)
                    for h in range(16):
                        fw.mm(yb[h // 8][:, (h % 8) * 64:(h % 8 + 1) * 64], Wt[:, h, :], v[d][:, h * 64:(h + 1) * 64],
                              start=(d == 0 and h % 8 == 0), stop=(d == 1 and h % 8 == 7), last=(d == 1 and h % 8 == 7), skip_group_check=True)
                for d in range(2):
                    for g in range(2):
                        fw.mm(pb[2 + g][:, :], BCT[:, 2 + g, :], Sbf[d][:, g * 512:(g + 1) * 512], True, True)
                    ysv = V(pb_t[:, 2:4, :].rearrange("p a (h n) -> p (a h) n", n=64), [pb[2].whole, pb[3].whole])
                    fw.tt(as16(yt if d == 0 else y2), ysv, h16(E[:, d * 16:(d + 1) * 16]), ALU.mult)
                fw.tt(yt[:], yt[:], y2[:], ALU.add)
                yv = V(pb_t[:, 6:8, :].rearrange("p a c -> p (a c)") if False else pb_t[:, 6:8, :], [pb[6].whole, pb[7].whole])
                fw.tt(V(yt.t[:].rearrange("p (a c) -> p a c", a=2), yt.whole), V(yt.t[:].rearrange("p (a c) -> p a c", a=2), yt.whole), yv, ALU.add)
                xs = V(U[0].t[:, 0:1024].rearrange("p (h n) -> p h n", n=64), U[0].whole)
                fw.tt(as16(y2), xs, h16(prm[:, ODD:ODD + 16]), ALU.mult, eng="pool")
                fw.tt(yt[:], yt[:], y2[:], ALU.add)
                fw.act(zt[:], zt[:], AF.Silu)
                fw.tt(yt[:], yt[:], zt[:], ALU.mult)
                banks = chunk_state(0)
                mulA(Sx[0], Sx[0], E[:, 64:80])
                fw.tt(V(Sx[0].t[:].rearrange("p (g c) -> p g c", g=2), Sx[0].whole), V(Sx[0].t[:].rearrange("p (g c) -> p g c", g=2), Sx[0].whole),
                      V(pb_t[:, 4:6, :], [pb[4].whole, pb[5].whole]), ALU.add)
                fw.act(y2[:], yt[:], AF.Square)
                fw.reduce(gss[:], V(y2.t[:].rearrange("p (g c) -> p g c", g=2), y2.whole), ALU.add)
                fw.act(gss[:], gss[:], AF.Sqrt, bias=eps_t[:], scale=1.0 / 512)
                fw.recip(gss[:], gss[:])
                fw.tt(V(ybf.t[:].rearrange("p (g c) -> p g c", g=2), ybf.whole), V(yt.t[:].rearrange("p (g c) -> p g c", g=2), yt.whole),
                      V(gss.t[:].unsqueeze(2).to_broadcast([128, 2, 512]), gss.whole), ALU.mult)
                to = pbh[1]
                tov = to.t[:, 0:1024].rearrange("p (a c) -> p a c", a=8)
                for a in range(8):
                    fw.transpose(V(tov[:, a, :], to.whole), ybf[:, a * 128:(a + 1) * 128], ident_bf[:], last=(a == 7))
                fw.tt(ytb[:], V(tov, to.whole), V(ssdn.t[:].unsqueeze(2).to_broadcast([128, 8, 128]), ssdn.whole), ALU.mult)
                fw.dma("sp", V(mixT_s.t[4:12, :, rows].rearrange("h p c -> p h c"), mixT_s.part(("ssd", lc)).res), ytb[:])
        fw.phase_reset()

    def phase_out(l, need_ctx):
        wout = fw.carve("wout", [128, 16, D], BF16)
        wst = fw.carve("wost", [128, 4, D], F32)
        mx = [fw.carve(f"mx{i}", [128, 16, 128], BF16) for i in range(2)]
        tmp = fw.carve("otmp", [128, 512], F32)
        for q in range(4):
            fw.dma("sp", V(wst.t[:], wst.whole),
                   V(wout_d.t[l, q * 512:(q + 1) * 512, :].rearrange("(f p) c -> p f c", p=128), wout_d.whole))
            fw.copy(wout[:, q * 4:(q + 1) * 4, :], wst[:], eng="pool")
        allmix = [r for r in mixT_s.parts.values()]
        for lc in (range(NLC) if need_ctx else range(NCH)):
            is_ctx = lc >= NCH
            m = mx[lc % 2]
            fw.dma("sp", m[:], V(mixT_s.t[:, :, lc * 128:(lc + 1) * 128].rearrange("f p c -> p f c"), allmix))
            for hh in range(2):
                bank = pb[(lc % 2) * 2 + hh]
                for fc in range(16):
                    fw.mm(bank[:, :], m[:, fc, :], wout[:, fc, hh * 512:(hh + 1) * 512], start=(fc == 0), stop=(fc == 15))
                cs = slice(hh * 512, (hh + 1) * 512)
                fw.tt(tmp[:], bank[:, :], gate[:, 1 if is_ctx else 0, cs], ALU.mult)
                dst = ctx_sb[:, lc - NCH, cs] if is_ctx else x_sb[:, lc, cs]
                fw.tt(dst, dst, tmp[:], ALU.add)
        fw.phase_reset()

    for l in range(depth):
        need_ctx = l < depth - 1
        adaln(l)
        layer_params(l)
        phase_proj(l, need_ctx)
        phase_attn(l, need_ctx)
        phase_halo(l)
        phase_ret(l, need_ctx)
        phase_ssd(l, need_ctx)
        phase_out(l, need_ctx)
        if l == 0 and "x0" in dbg:
            fw.dma("sp", V(dbg["x0"].t.ap().rearrange("(c p) d -> p c d", p=128), dbg["x0"].whole), V(x_sb.t[:], x_sb.whole))
            fw.dma("sp", V(dbg["ctx0"].t.ap().rearrange("(c p) d -> p c d", p=128), dbg["ctx0"].whole), V(ctx_sb.t[:], ctx_sb.whole))

    if "qT" in dbg:
        fw.dma("sp", dbg["qT"][:], V(qT_s.t[:], [qT_s.part(c).res for c in range(NLC)]))
    if "kv0" in dbg:
        fw.dma("sp", dbg["kv0"][:], kv_dst[0][:, :])
    if "proj" in dbg:
        fw.dma("sp", dbg["proj"][:], V(proj_s.t[:], [r for r in proj_s.parts.values()]))
    if "mixT" in dbg:
        fw.dma("sp", dbg["mixT"][:], V(mixT_s.t[0:4], [r for r in mixT_s.parts.values()]))
    if "mixS" in dbg:
        fw.dma("sp", dbg["mixS"][:], V(mixT_s.t[4:12], [r for r in mixT_s.parts.values()]))
    if "mixR" in dbg:
        fw.dma("sp", dbg["mixR"][:], V(mixT_s.t[12:16], [r for r in mixT_s.parts.values()]))
    fw.dma("sp", V(out_d.t.ap().rearrange("(c p) d -> p c d", p=128), out_d.whole), V(x_sb.t[:], x_sb.whole))
    fw.wait_all("sp", [out_d[:]] + [V(b.t[:], b.whole) for b in dbg.values()])
    return nc, fw


def rope_tables():
    n_freq = 16
    inv_freq = (10000.0 ** (-np.arange(n_freq, dtype=np.float32) / n_freq)).astype(np.float32)
    pos = np.arange(8192)
    row = (pos // 64).astype(np.float32)
    col = (pos % 64).astype(np.float32)
    ang = np.concatenate([row[:, None] * inv_freq, col[:, None] * inv_freq], axis=-1).astype(np.float32)
    return np.cos(ang).astype(np.float32), np.sin(ang).astype(np.float32)


def const_tables():
    j = np.arange(128)[:, None]; i = np.arange(128)[None, :]
    tri = np.stack([(j <= i), (j >= i), (j > i), (j < i), np.ones((128, 128), bool)], axis=1).astype(np.float32)
    gf = np.array([1.0 - 2.0 ** -e for e in RET_EXP_F], np.float64)
    gb = np.array([1.0 - 2.0 ** -e for e in RET_EXP_B], np.float64)
    dif = (i - j).astype(np.float64)
    Dret = np.zeros((128, 4, 128), np.float64)
    for h in range(4):
        Dret[:, h, :] = np.where(dif > 0, gf[h] ** np.abs(dif), 0.0) + np.where(dif < 0, gb[h] ** np.abs(dif), 0.0) + np.where(dif == 0, 2.0, 0.0)
    Dret *= 0.125
    pos = np.arange(128, dtype=np.float64)[:, None]
    te_f = gf[None, :] ** (127 - pos) * 0.125
    te_b = gb[None, :] ** pos * 0.125
    qsc_f = gf[None, :] ** (pos + 1)
    qsc_b = gb[None, :] ** (128 - pos)
    a_f = np.broadcast_to(gf[None, :] ** 128, (128, 4)); a_b = np.broadcast_to(gb[None, :] ** 128, (128, 4))
    rett = np.concatenate([Dret.reshape(128, 512), te_f, te_b, qsc_f, qsc_b, a_f, a_b], axis=1).astype(np.float32)
    return np.ascontiguousarray(tri), np.ascontiguousarray(rett)


def make_inputs(inp):
    cos, sin = rope_tables()
    tri, rett = const_tables()
    ssdp = np.concatenate([inp["ssd_conv_w"].reshape(2, -1), inp["ssd_conv_b"], inp["ssd_dt_bias"].reshape(2, -1),
                           inp["ssd_a_log"].reshape(2, -1), inp["ssd_d"], inp["ret_norm"]], axis=1).astype(np.float32)
    ssdp = np.ascontiguousarray(np.broadcast_to(ssdp[:, None, :], (2, 128, ssdp.shape[1])))
    ssdn = np.ascontiguousarray(inp["ssd_norm"].reshape(2, 8, 128).transpose(0, 2, 1))
    rep = lambda a: np.ascontiguousarray(np.broadcast_to(a[:, None], (a.shape[0], 128) + a.shape[1:]))
    qkg = rep(np.stack([inp["attn_q_norm"], inp["attn_k_norm"]], axis=1))
    lamv = rep(np.stack([inp["lambda_q1"], inp["lambda_k1"], inp["lambda_q2"], inp["lambda_k2"]], axis=1))
    subln = np.ascontiguousarray(inp["attn_subln"][:, :, None])
    maps = []
    for core in range(8):
        b, t = core // 4, core % 4
        lo = t * TOK
        cc = np.stack([inp["c"][b].reshape(8, 128).T, inp["c_ctx"].reshape(8, 128).T], axis=-1)
        rp = np.stack([cos[lo:lo + TOK], sin[lo:lo + TOK]], axis=1)
        rp = rp.reshape(NCH, 128, 2, 32).transpose(1, 0, 2, 3)
        m = {
            "x": np.ascontiguousarray(inp["x"][b, lo:lo + TOK]),
            "ctx": np.ascontiguousarray(inp["ctx"][b]),
            "cc": np.ascontiguousarray(cc.astype(np.float32)),
            "w_ada": inp["w_ada"],
            "b_ada_f": np.ascontiguousarray(inp["b_ada"][:, :2 * D].reshape(2, 16, 128).transpose(0, 2, 1)),
            "b_ada": inp["b_ada"],
            "w_in": inp["w_in"], "w_out": inp["w_out"],
            "qkg": qkg, "lamv": lamv, "subln": subln,
            "rope": np.ascontiguousarray(rp),
            "tri": tri, "rett": rett, "ssdp": ssdp, "ssdn": ssdn,
            "cmask": np.ascontiguousarray(np.broadcast_to(np.array([float(s_ < t) for s_ in range(4)] + [float(s_ > t) for s_ in range(4)], np.float32)[None], (128, 8))),
        }
        maps.append(m)
    return maps


from concourse.bass_utils import run_bass_kernel_spmd


def kernel(**inputs):
    inp = {k: np.asarray(v) for k, v in inputs.items()}
    nc, _ = build(depth=2)
    maps = make_inputs(inp)
    res = run_bass_kernel_spmd(nc, maps, core_ids=list(range(8)))
    outs = [np.asarray(res.results[c]["out"]) for c in range(8)]
    return np.stack([np.concatenate(outs[0:4], 0), np.concatenate(outs[4:8], 0)]).astype(np.float32)
```

```python
import numpy as np
import concourse.bass as bass
import concourse.mybir as mybir

F32 = mybir.dt.float32
BF16 = mybir.dt.bfloat16
AF = mybir.ActivationFunctionType
ALU = mybir.AluOpType
AX = mybir.AxisListType


class Res:
    __slots__ = ("name", "w", "r")

    def __init__(self, name):
        self.name = name
        self.w = None
        self.r = {}


class V:
    __slots__ = ("ap", "res")

    def __init__(self, ap, res):
        self.ap = ap
        self.res = res if isinstance(res, (list, tuple)) else [res]


class Buf:
    def __init__(self, fw, name, t, nparts=1):
        self.fw = fw
        self.name = name
        self.t = t
        self.parts = {}
        self.whole = Res(name)

    def __getitem__(self, idx):
        return V(self.t[idx], self.whole)

    def part(self, key):
        if key not in self.parts:
            self.parts[key] = Res(f"{self.name}.{key}")
        return _PartView(self, self.parts[key])

    def ap(self):
        return self.t.ap()


class _PartView:
    def __init__(self, buf, res):
        self.buf = buf
        self.res = res

    def __getitem__(self, idx):
        return V(self.buf.t[idx], self.res)


class EngState:
    def __init__(self, name, eng, sem):
        self.name = name
        self.eng = eng
        self.sem = sem
        self.count = 0
        self.pending = False
        self.seen = {}
        self.seen_dma = {}


class FW:
    def __init__(self, nc, n_dma_sems=24, same_engine_sync=True):
        self.nc = nc
        self.same_engine_sync = same_engine_sync
        self.engs = {}
        for name, eng in (("pe", nc.tensor), ("dve", nc.vector), ("act", nc.scalar),
                          ("pool", nc.gpsimd), ("sp", nc.sync)):
            self.engs[name] = EngState(name, eng, nc.alloc_semaphore(f"s_{name}"))
        self.dma_sems = [nc.alloc_semaphore(f"s_dma{i}") for i in range(n_dma_sems)]
        self.dma_vals = [0] * n_dma_sems
        self.dma_next = 0
        self.n_inst = 0
        self.out_tokens = []
        self.cc_sem = None
        self.cc_val = 0

    def sbuf(self, name, shape, dtype):
        return Buf(self, name, self.nc.alloc_sbuf_tensor("sb_" + name, list(shape), dtype))

    def psum(self, name, shape, dtype=F32):
        return Buf(self, name, self.nc.alloc_psum_tensor("ps_" + name, list(shape), dtype))

    def dram(self, name, shape, dtype, kind="Internal", **kw):
        return Buf(self, name, self.nc.dram_tensor(name, list(shape), dtype, kind=kind, **kw))

    def _need(self, E, tok):
        if tok is None:
            return
        if tok[0] == "eng":
            _, e, c = tok
            if e == E.name:
                if not self.same_engine_sync or e == "pe":
                    return
            if E.seen.get(e, 0) >= c:
                return
            P = self.engs[e]
            assert c <= P.count, f"{E.name} waits on pending (never-incremented) {e} count {c} > {P.count}"
            E.eng.wait_ge(P.sem, c)
            E.seen[e] = c
        elif tok[0] == "cc":
            val = tok[1]
            if E.seen_dma.get("cc", 0) >= val:
                return
            E.eng.wait_ge(self.cc_sem, val)
            E.seen_dma["cc"] = val
        else:
            _, si, val = tok
            if E.seen_dma.get(si, 0) >= val:
                return
            E.eng.wait_ge(self.dma_sems[si], val)
            E.seen_dma[si] = val

    def _pre(self, E, reads, writes):
        for v in reads:
            for r in v.res:
                self._need(E, r.w)
        for v in writes:
            for r in v.res:
                self._need(E, r.w)
                for tok in r.r.values():
                    self._need(E, tok)

    def _post(self, tok, key, reads, writes):
        for v in reads:
            for r in v.res:
                r.r[key] = tok
        for v in writes:
            for r in v.res:
                r.w = tok
                r.r = {}

    def op(self, engname, fn, reads, writes, inc=True):
        E = self.engs[engname]
        self._pre(E, reads, writes)
        ins = fn(E.eng)
        self.n_inst += 1
        if inc:
            E.count += 1
            ins.then_inc(E.sem, 1)
            tok = ("eng", engname, E.count)
        else:
            tok = ("eng", engname, E.count + 1)
        self._post(tok, engname, reads, writes)
        return ins

    def dma(self, qname, out, in_, **kw):
        E = self.engs[qname]
        self._pre(E, [in_], [out])
        si = self.dma_next
        self.dma_next = (self.dma_next + 1) % len(self.dma_sems)
        if self.dma_vals[si] > 0:
            self._need(E, ("dma", si, self.dma_vals[si]))
        self.dma_vals[si] += 16
        ins = E.eng.dma_start(out=out.ap, in_=in_.ap, **kw)
        ins.then_inc(self.dma_sems[si], 16)
        self.n_inst += 1
        tok = ("dma", si, self.dma_vals[si])
        self._post(tok, f"dma{si}", [in_], [out])
        return tok

    def wait_all(self, engname, views):
        E = self.engs[engname]
        for v in views:
            for r in v.res:
                self._need(E, r.w)

    def mm(self, out, lhsT, rhs, start, stop, last=None, **kw):
        if last is None:
            last = stop
        return self.op("pe", lambda e: e.matmul(out.ap, lhsT.ap, rhs.ap, start=start, stop=stop, **kw),
                       [lhsT, rhs], [out], inc=last)

    def transpose(self, out, in_, ident, last=True):
        return self.op("pe", lambda e: e.transpose(out.ap, in_.ap, ident.ap), [in_, ident], [out], inc=last)

    def act(self, out, in_, func, bias=None, scale=1.0, accum_out=None, eng="act"):
        reads = [in_]
        kw = {}
        if bias is not None:
            if isinstance(bias, V):
                reads.append(bias)
                kw["bias"] = bias.ap
            else:
                kw["bias"] = bias
        if isinstance(scale, V):
            reads.append(scale)
            kw["scale"] = scale.ap
        else:
            kw["scale"] = scale
        writes = [out]
        if accum_out is not None:
            writes.append(accum_out)
            kw["accum_out"] = accum_out.ap
        return self.op(eng, lambda e: e.activation(out.ap, in_.ap, func, **kw), reads, writes)

    def tt(self, out, in0, in1, op, eng="dve"):
        return self.op(eng, lambda e: e.tensor_tensor(out.ap, in0.ap, in1.ap, op), [in0, in1], [out])

    def ts(self, out, in0, s1, op0, s2=None, op1=None, eng="dve", accum_out=None):
        reads = [in0]
        a1 = s1
        if isinstance(s1, V):
            reads.append(s1)
            a1 = s1.ap
        a2 = s2
        if isinstance(s2, V):
            reads.append(s2)
            a2 = s2.ap
        kw = {}
        writes = [out]
        if op1 is not None:
            kw["op1"] = op1
        if accum_out is not None:
            kw["accum_out"] = accum_out.ap
            writes.append(accum_out)
        return self.op(eng, lambda e: e.tensor_scalar(out.ap, in0.ap, a1, a2, op0, **kw), reads, writes)

    def stt(self, out, in0, scalar, in1, op0, op1, eng="dve"):
        reads = [in0, in1]
        a = scalar
        if isinstance(scalar, V):
            reads.append(scalar)
            a = scalar.ap
        return self.op(eng, lambda e: e.scalar_tensor_tensor(out.ap, in0.ap, a, in1.ap, op0, op1), reads, [out])

    def copy(self, out, in_, eng="dve"):
        if eng == "act":
            return self.op("act", lambda e: e.copy(out.ap, in_.ap), [in_], [out])
        return self.op(eng, lambda e: e.tensor_copy(out.ap, in_.ap), [in_], [out])

    def memset(self, out, val, eng="dve"):
        return self.op(eng, lambda e: e.memset(out.ap, val), [], [out])

    def reduce(self, out, in_, op, axis=AX.X, eng="dve"):
        return self.op(eng, lambda e: e.tensor_reduce(out.ap, in_.ap, axis, op), [in_], [out])

    def recip(self, out, in_):
        return self.op("dve", lambda e: e.reciprocal(out.ap, in_.ap), [in_], [out])

    def collective(self, kind, op, groups, in_, out):
        E = self.engs["pool"]
        self._pre(E, [in_], [out])
        if self.cc_sem is None:
            self.cc_sem = self.nc.alloc_semaphore("s_cc")
        self.cc_val += 1
        ins = E.eng.collective_compute(kind, op, replica_groups=groups, ins=[in_.ap], outs=[out.ap])
        ins.then_inc(self.cc_sem)
        self.n_inst += 1
        tok = ("cc", self.cc_val)
        self._post(tok, "cc", [in_], [out])
        return tok

    def make_arena(self, kbytes):
        self.arena_t = self.nc.alloc_sbuf_tensor("sb_arena", [128, kbytes * 256], F32)
        self.arena_words = kbytes * 256
        self.arena_off = 0
        self.arena_gen = 0

    def carve(self, name, shape, dtype):
        esz = 2 if dtype == BF16 else 4
        n = 1
        for s in shape[1:]:
            n *= s
        words = (n * esz + 3) // 4
        words = (words + 7) // 8 * 8
        assert self.arena_off + words <= self.arena_words, f"arena overflow for {name}: {self.arena_off}+{words}>{self.arena_words}"
        raw = self.arena_t[0:shape[0], self.arena_off:self.arena_off + words]
        self.arena_off += words
        ap = raw.bitcast(dtype) if dtype != F32 else raw
        ap = ap[:, 0:n]
        if len(shape) > 2:
            names = " ".join(f"d{i}" for i in range(1, len(shape)))
            kw = {f"d{i}": shape[i] for i in range(1, len(shape))}
            ap = ap.rearrange(f"p ({names}) -> p {names}", **kw)
        return Buf(self, f"{name}@{self.arena_gen}", _APHandle(ap))

    def barrier(self):
        for E in self.engs.values():
            for P in self.engs.values():
                if P is not E and P.count > 0:
                    self._need(E, ("eng", P.name, P.count))
            for si, val in enumerate(self.dma_vals):
                if val > 0:
                    self._need(E, ("dma", si, val))
            if self.cc_val > 0:
                self._need(E, ("cc", self.cc_val))

    def phase_reset(self):
        self.barrier()
        self.arena_off = 0
        self.arena_gen += 1


class _APHandle:
    def __init__(self, ap):
        self._ap = ap

    def __getitem__(self, idx):
        return self._ap[idx]

    def ap(self):
        return self._ap


import math
KCUT = 9

D = 1024
NCH = 16
TOK = 2048
NTOK = TOK + 256
NLC = 18
DIN = 6176
EPS = 1e-6
GROUPS = [[0, 1, 2, 3], [4, 5, 6, 7]]
SW = 1568
NP = 3 * 1536 + 1536 + 32 + 32 + 16 + 128
RET_EXP_F = (5.0, 6.0, 7.0, 8.0)
RET_EXP_B = (5.5, 6.5, 7.5, 8.5)
BLOCKS = [("aq", 0, 512), ("ak", 512, 512), ("av", 1024, 512), ("ag", 1536, 512),
          ("xbc0", 2048, 512), ("xbc1", 2560, 512), ("xbc2", 3072, 512), ("dtr", 3584, 32),
          ("z0", 3616, 512), ("z1", 4128, 512), ("rqk", 4640, 512), ("rv", 5152, 512), ("rg", 5664, 512)]


def lam_init_of(layer):
    return 0.8 - 0.6 * math.exp(-0.3 * layer)


def build(depth=2, debug=None, stop_after=None):
    debug = debug or {}
    nc = bass.Bass("TRN2", target_bir_lowering=False)
    fw = FW(nc, same_engine_sync=True)
    I = lambda n, s, d=F32: fw.dram(n, s, d, kind="ExternalInput")
    x_d = I("x", [TOK, D])
    ctx_d = I("ctx", [256, D])
    cc_d = I("cc", [128, 8, 2])
    wada_d = I("w_ada", [2, D, 3 * D])
    bada_f_d = I("b_ada_f", [2, 128, 16])
    bada_d = I("b_ada", [2, 3 * D])
    win_d = I("w_in", [2, D, DIN])
    wout_d = I("w_out", [2, 2 * D, D])
    qkg_d = I("qkg", [2, 128, 2, 64])
    lamv_d = I("lamv", [2, 128, 4, 64])
    subln_d = I("subln", [2, 128, 1])
    rope_d = I("rope", [128, NCH, 2, 32])
    tri_d = I("tri", [128, 5, 128])
    rett_d = I("rett", [128, 4 * 128 + 24])
    ssdp_d = I("ssdp", [2, 128, NP])
    ssdn_d = I("ssdn", [2, 128, 8])
    cmask_d = I("cmask", [128, 8])
    out_d = fw.dram("out", [TOK, D], F32, kind="ExternalOutput")
    dbg = {k: fw.dram("dbg_" + k, shape, dt_, kind="ExternalOutput") for k, (shape, dt_) in debug.items()}

    proj_s = fw.dram("proj_s", [NTOK, DIN], F32)
    qT_s = fw.dram("qT_s", [4, 128, NTOK], BF16)
    agT_s = fw.dram("agT_s", [4, 128, NTOK], BF16)
    mixT_s = fw.dram("mixT_s", [16, 128, NTOK], BF16)
    kc_s = fw.dram("kc_s", [4, 128, 256], BF16)
    vc_s = fw.dram("vc_s", [256, 512], BF16)
    xbc_pad = fw.dram("xbc_pad", [TOK + 2, 1536], F32)
    xbc_cpad = fw.dram("xbc_cpad", [258, 1536], F32)
    hx_stage = fw.dram("hx_stage", [2, 1536], F32)
    hx_src = fw.dram("hx_src", [8, 1536], F32)
    hx_dst = fw.dram("hx_dst", [8, 1536], F32)
    hxL = fw.dram("hxL", [9, 1536], F32)
    hxR = fw.dram("hxR", [8, 1536], F32)
    cst_s = fw.dram("cst_s", [NLC, 2, 128, SW], F32)
    sb_s = fw.dram("sb_s", [NLC, 128, 1536], BF16)
    st_stage = [fw.dram(f"st_stage{d}", [128, SW], F32) for d in range(2)]
    st_src = [fw.dram(f"st_src{d}", [512, SW], F32) for d in range(2)]
    st_dst = [fw.dram(f"st_dst{d}", [512, SW], F32) for d in range(2)]
    kv_src = [fw.dram(f"kv_src{h}", [512, 4096], BF16) for h in range(4)]
    kv_dst = [fw.dram(f"kv_dst{h}", [512, 4096], BF16) for h in range(4)]
    kv_stage = [fw.dram(f"kv_stage{h}", [128, 4096], BF16) for h in range(4)]

    x_sb = fw.sbuf("x_sb", [128, NCH, D], F32)
    ctx_sb = fw.sbuf("ctx_sb", [128, 2, D], F32)
    gate = fw.sbuf("gate", [128, 2, D], F32)
    sc1 = fw.sbuf("sc1", [128, 2, 8], F32)
    sh = fw.sbuf("sh", [128, 2, 8], F32)
    ident = fw.sbuf("ident", [128, 128], F32)
    ident_bf = fw.sbuf("ident_bf", [128, 128], BF16)
    ones_bf = fw.sbuf("ones_bf", [128, 128], BF16)
    eps_t = fw.sbuf("eps_t", [128, 1], F32)
    cc = fw.sbuf("cc", [128, 8, 2], F32)
    rope = fw.sbuf("rope", [128, NCH, 2, 32], F32)
    qkg = fw.sbuf("qkg", [128, 2, 64], F32)
    lamv = fw.sbuf("lamv", [128, 4, 64], F32)
    neglam = fw.sbuf("neglam", [128, 1], F32)
    subln = fw.sbuf("subln", [128, 1], F32)
    small = fw.sbuf("small", [128, 64], F32)
    cmask = fw.sbuf("cmask", [128, 8], F32)
    fw.make_arena(119)
    pb_t = nc.alloc_psum_tensor("ps_banks", [128, 8, 512], F32)
    pb = [Buf(fw, f"pb{i}", _APHandle(pb_t[:, i, :])) for i in range(8)]
    pbh = [Buf(fw, f"pbh{i}", _APHandle(pb_t[:, i, :].bitcast(BF16))) for i in range(8)]
    for i in range(8):
        pbh[i].whole = pb[i].whole
    rank = nc.partition_id() % 4

    fw.memset(ident[:], 1.0, eng="pool")
    fw.op("pool", lambda e: e.affine_select(ident.t[:], ident.t[:], [[-1, 128]], ALU.is_equal, 0.0,
                                             base=0, channel_multiplier=1), [ident[:]], [ident[:]])
    fw.copy(ident_bf[:], ident[:])
    fw.memset(ones_bf[:], 1.0)
    fw.memset(eps_t[:], EPS)
    fw.dma("sp", V(x_sb.t[:], x_sb.whole), V(x_d.t.ap().rearrange("(c p) d -> p c d", p=128), x_d.whole))
    fw.dma("sp", V(ctx_sb.t[:], ctx_sb.whole), V(ctx_d.t.ap().rearrange("(c p) d -> p c d", p=128), ctx_d.whole))
    fw.dma("sp", cc[:], cc_d[:])
    fw.dma("sp", rope[:], rope_d[:])
    fw.dma("sp", cmask[:], cmask_d[:])
    fw.act(cc[:], cc[:], AF.Silu)
    zt = fw.carve("zt", [128, 4096], BF16)
    fw.memset(zt[:], 0.0)
    for h in range(4):
        fw.dma("sp", V(kv_src[h].t.ap().rearrange("(r p) c -> p r c", p=128), kv_src[h].whole),
               V(zt.t[:].unsqueeze(1).to_broadcast([128, 4, 4096]), zt.whole))
    zf = fw.carve("zf", [128, SW], F32)
    fw.memset(zf[:], 0.0)
    fw.dma("sp", xbc_cpad[0:1, :], zf[0:1, 0:1536])
    fw.dma("sp", xbc_cpad[257:258, :], zf[0:1, 0:1536])
    fw.dma("sp", hx_src[:, :], zf[0:8, 0:1536])
    fw.dma("sp", hxL[:, :], zf[0:9, 0:1536])
    fw.dma("sp", hxR[:, :], zf[0:8, 0:1536])
    for d in range(2):
        fw.dma("sp", V(st_src[d].t.ap().rearrange("(r p) c -> p r c", p=128), st_src[d].whole),
               V(zf.t[:].unsqueeze(1).to_broadcast([128, 4, SW]), zf.whole))
        fw.dma("sp", st_stage[d][:, :], zf[:, :])
    fw.phase_reset()

    def adaln(l):
        ccrep = fw.carve("ccrep", [128, 8, 2, 128], F32)
        badaf = fw.carve("badaf", [128, 16], F32)
        gbias = fw.carve("gbias", [128, D], F32)
        wada_sb = fw.carve("wada_sb", [128, 8, 512], F32)
        fw.copy(ccrep[:], V(cc.t[:].unsqueeze(3).to_broadcast([128, 8, 2, 128]), cc.whole))
        fw.dma("sp", badaf[:], bada_f_d[l])
        fw.dma("sp", gbias[:], V(bada_d.t[l:l + 1, 2 * D:3 * D].partition_broadcast(128), bada_d.whole))
        ps_s = V(pb[2].t[:, 0:32].rearrange("p (a b) -> p a b", b=2), pb[2].whole)
        for piece in range(6):
            fw.dma("sp", V(wada_sb.t[:], wada_sb.whole),
                   V(wada_d.t[l, :, piece * 512:(piece + 1) * 512].rearrange("(k p) c -> p k c", p=128), wada_d.whole))
            if piece < 4:
                for j in range(4):
                    blk = piece * 4 + j
                    for k in range(8):
                        fw.mm(V(ps_s.ap[:, blk, :], ps_s.res), wada_sb[:, k, j * 128:(j + 1) * 128], cc[:, k, :],
                              start=(k == 0), stop=(k == 7))
            else:
                half = piece - 4
                for v in range(2):
                    for k in range(8):
                        fw.mm(pb[3][:, :], ccrep[:, k, v, :], wada_sb[:, k, :], start=(k == 0), stop=(k == 7))
                    fw.tt(gate[:, v, half * 512:(half + 1) * 512], pb[3][:, :], gbias[:, half * 512:(half + 1) * 512], ALU.add)
        for v in range(2):
            fw.tt(sh[:, v, :], V(ps_s.ap[:, 0:8, v], ps_s.res), badaf[:, 0:8], ALU.add)
            fw.tt(sc1[:, v, :], V(ps_s.ap[:, 8:16, v], ps_s.res), badaf[:, 8:16], ALU.add)
        fw.ts(sc1[:], sc1[:], 1.0, ALU.add)
        fw.phase_reset()

    def layer_params(l):
        fw.dma("sp", qkg[:], qkg_d[l])
        fw.dma("sp", lamv[:], lamv_d[l])
        fw.dma("sp", subln[:], subln_d[l])
        fw.tt(small[:, 0:64], lamv[:, 0, :], lamv[:, 1, :], ALU.mult)
        s1 = fw.sbuf(f"lam_s1_{l}", [128, 1], F32)
        s2 = fw.sbuf(f"lam_s2_{l}", [128, 1], F32)
        fw.reduce(s1[:], small[:, 0:64], ALU.add)
        fw.tt(small[:, 0:64], lamv[:, 2, :], lamv[:, 3, :], ALU.mult)
        fw.reduce(s2[:], small[:, 0:64], ALU.add)
        fw.act(s1[:], s1[:], AF.Exp)
        fw.act(s2[:], s2[:], AF.Exp)
        fw.tt(neglam[:], s2[:], s1[:], ALU.subtract)
        fw.ts(neglam[:], neglam[:], -lam_init_of(l), ALU.add)
        fw.ts(subln[:], subln[:], 1.0 - lam_init_of(l), ALU.mult)

    def phase_proj(l, need_ctx_q):
        hT = fw.carve("hT", [128, 8, NTOK], BF16)
        xn = fw.carve("xn", [128, D], F32)
        junk = fw.carve("junk", [128, D], F32)
        ss = fw.carve("ss", [128, 1], F32)
        rs = fw.carve("rs", [128, 1], F32)
        wblk = [fw.carve(f"wblk{i}", [128, 8, 512], BF16) for i in range(2)]
        wst = fw.carve("wst", [128, 8, 512], F32)
        stage = [fw.carve(f"stage{i}", [128, 512], F32) for i in range(3)]
        sq = fw.carve("sq", [128, 8, 64], F32)
        qn = fw.carve("qn", [128, 8, 64], F32)
        t1 = fw.carve("t1", [128, 8, 32], F32)
        t2 = fw.carve("t2", [128, 8, 32], F32)
        ss8 = fw.carve("ss8", [128, 8], F32)
        qbf = [fw.carve(f"qbf{i}", [128, 8, 64], BF16) for i in range(2)]
        tb = [fw.carve(f"tb{i}", [128, 4, 128], BF16) for i in range(2)]
        vbf = [fw.carve(f"vbf{i}", [128, 512], BF16) for i in range(2)]

        for lc in range(NLC):
            src = x_sb[:, lc, :] if lc < NCH else ctx_sb[:, lc - NCH, :]
            v = 0 if lc < NCH else 1
            fw.act(junk[:], src, AF.Square, accum_out=ss[:])
            fw.act(rs[:], ss[:], AF.Sqrt, bias=eps_t[:], scale=1.0 / D)
            fw.recip(rs[:], rs[:])
            fw.ts(xn[:], src, rs[:], ALU.mult)
            pt = V(pb[lc % 2 * 2].t[:, :], [pb[lc % 2 * 2].whole, pb[lc % 2 * 2 + 1].whole])
            ptt = pb_t[:, lc % 2 * 2:lc % 2 * 2 + 2, :].rearrange("p a (k c) -> p (a k) c", c=128)
            for k in range(8):
                fw.transpose(V(ptt[:, k, :], pt.res), xn[:, k * 128:(k + 1) * 128], ident[:], last=(k == 7))
            for k in range(8):
                fw.ts(hT[:, k, lc * 128:(lc + 1) * 128], V(ptt[:, k, :], pt.res), sc1[:, v, k:k + 1], ALU.mult,
                      sh[:, v, k:k + 1], ALU.add)
        if "hT" in dbg:
            fw.dma("sp", V(dbg["hT"].t[:], dbg["hT"].whole), V(hT.t[:], hT.whole))

        it = 0
        for bi, (bname, col0, ncols) in enumerate(BLOCKS):
            wb = wblk[bi % 2]
            fw.dma("sp", V(wst.t[:, :, 0:ncols], wst.whole),
                   V(win_d.t[l, :, col0:col0 + ncols].rearrange("(k p) c -> p k c", p=128), win_d.whole))
            fw.copy(V(wb.t[:, :, 0:ncols], wb.whole), V(wst.t[:, :, 0:ncols], wst.whole), eng="pool")
            for lc in range(NLC):
                is_ctx = lc >= NCH
                bank = pb[4 + it % 2]
                it += 1
                for k in range(8):
                    fw.mm(bank[:, 0:ncols], hT[:, k, lc * 128:(lc + 1) * 128], wb[:, k, 0:ncols],
                          start=(k == 0), stop=(k == 7))
                rows = slice(lc * 128, (lc + 1) * 128)
                if bname in ("aq", "ak"):
                    if bname == "aq" and is_ctx and not need_ctx_q:
                        continue
                    gi = 0 if bname == "aq" else 1
                    psv = V(bank.t[:, :].rearrange("p (a b) -> p a b", b=64), bank.whole)
                    fw.act(sq[:], psv, AF.Square)
                    fw.reduce(ss8[:], sq[:], ALU.add)
                    fw.act(ss8[:], ss8[:], AF.Sqrt, bias=eps_t[:], scale=1.0 / 64)
                    fw.recip(ss8[:], ss8[:])
                    fw.tt(qn[:], psv, V(ss8.t[:].unsqueeze(2).to_broadcast([128, 8, 64]), ss8.whole), ALU.mult)
                    fw.tt(qn[:], qn[:], V(qkg.t[:, gi:gi + 1, :].to_broadcast([128, 8, 64]), qkg.whole), ALU.mult, eng="pool")
                    qo = qbf[it % 2]
                    if not is_ctx:
                        cosb = V(rope.t[:, lc, 0:1, :].to_broadcast([128, 8, 32]), rope.whole)
                        sinb = V(rope.t[:, lc, 1:2, :].to_broadcast([128, 8, 32]), rope.whole)
                        fw.tt(t1[:], qn[:, :, 0:32], cosb, ALU.mult)
                        fw.tt(t2[:], qn[:, :, 32:64], sinb, ALU.mult, eng="pool")
                        fw.tt(qo[:, :, 0:32], t1[:], t2[:], ALU.subtract)
                        fw.tt(t1[:], qn[:, :, 0:32], sinb, ALU.mult)
                        fw.tt(t2[:], qn[:, :, 32:64], cosb, ALU.mult, eng="pool")
                        fw.tt(qo[:, :, 32:64], t1[:], t2[:], ALU.add)
                    else:
                        fw.copy(qo[:], qn[:])
                    tbank = pbh[6 + lc % 2]
                    tbv = tbank.t[:, 0:512].rearrange("p (h c) -> p h c", c=128)
                    qof = qo.t[:].rearrange("p a b -> p (a b)")
                    for hd in range(4):
                        fw.transpose(V(tbv[:, hd, :], tbank.whole), V(qof[:, hd * 128:(hd + 1) * 128], qo.whole), ident_bf[:], last=(hd == 3))
                    tbs = tb[lc % 2]
                    fw.copy(tbs[:], V(tbv, tbank.whole), eng="act")
                    if bname == "aq":
                        fw.dma("act", V(qT_s.t[:, :, rows].rearrange("h p c -> p h c"), qT_s.part(lc).res), tbs[:])
                    elif is_ctx:
                        c0 = (lc - NCH) * 128
                        fw.dma("act", V(kc_s.t[:, :, c0:c0 + 128].rearrange("h p c -> p h c"), kc_s.part(lc).res), tbs[:])
                    else:
                        for hd in range(4):
                            fw.dma("act", V(kv_stage[hd].t[:, lc * 128:(lc + 1) * 128], kv_stage[hd].part(("k", lc)).res),
                                   tbs[:, hd, :])
                elif bname == "av":
                    vb = vbf[lc % 2]
                    fw.copy(vb[:], bank[:, :], eng="act")
                    if is_ctx:
                        c0 = (lc - NCH) * 128
                        fw.dma("act", V(vc_s.t[c0:c0 + 128, :], vc_s.part(lc).res), vb[:])
                    else:
                        for hd in range(4):
                            fw.dma("act", V(kv_stage[hd].t[:, 2048 + lc * 128:2048 + (lc + 1) * 128],
                                            kv_stage[hd].part(("v", lc)).res), vb[:, hd * 128:(hd + 1) * 128])
                elif bname == "ag":
                    st = stage[lc % 3]
                    fw.act(st[:], bank[:, :], AF.Silu)
                    tbank = pb[6 + lc % 2]
                    for hd in range(4):
                        fw.transpose(tbank[:, hd * 128:(hd + 1) * 128], st[:, hd * 128:(hd + 1) * 128], ident[:], last=(hd == 3))
                    tbs = tb[lc % 2]
                    fw.copy(V(tbs.t[:].rearrange("p h c -> p (h c)"), tbs.whole), tbank[:, :])
                    fw.dma("act", V(agT_s.t[:, :, rows].rearrange("h p c -> p h c"), agT_s.part(lc).res), tbs[:])
                else:
                    st = stage[lc % 3]
                    if lc % 2 == 0:
                        fw.copy(st[:, 0:ncols], bank[:, 0:ncols])
                    else:
                        fw.copy(st[:, 0:ncols], bank[:, 0:ncols], eng="act")
                    if bname.startswith("xbc"):
                        xc = (int(bname[3]) * 512)
                        if is_ctx:
                            r1 = 1 + (lc - NCH) * 128
                            fw.dma("sp", V(xbc_cpad.t[r1:r1 + 128, xc:xc + 512], xbc_cpad.part((bname, lc)).res), st[:, 0:ncols])
                        else:
                            r1 = 1 + lc * 128
                            fw.dma("sp", V(xbc_pad.t[r1:r1 + 128, xc:xc + 512], xbc_pad.part((bname, lc)).res), st[:, 0:ncols])
                    else:
                        fw.dma("sp", V(proj_s.t[rows, col0:col0 + ncols], proj_s.part((bname, lc)).res), st[:, 0:ncols])
            if bname == "av":
                for hd in range(4):
                    allres = [kv_stage[hd].part(("k", c)).res for c in range(NCH)] + [kv_stage[hd].part(("v", c)).res for c in range(NCH)]
                    fw.dma("sp", V(kv_src[hd].t[bass.ds(rank * 128, 128), :], kv_src[hd].whole), V(kv_stage[hd].t[:, :], allres))
                    fw.collective("AllReduce", ALU.add, GROUPS, kv_src[hd][:, :], kv_dst[hd][:, :])
        fw.phase_reset()

    def phase_attn(l, with_ctx_q):
        kTb = [fw.carve(f"kT{i}", [128, 8448], BF16) for i in range(2)]
        vvb = [fw.carve(f"vv{i}", [128, 66, 128], BF16) for i in range(2)]
        qhb = [fw.carve(f"qh{i}", [128, NTOK], BF16) for i in range(2)]
        aghb = [fw.carve(f"agh{i}", [128, NTOK], BF16) for i in range(2)]
        pT = [fw.carve(f"pT{i}", [128, 512], BF16) for i in range(4)]
        rec = fw.carve("rec", [128, 512], F32)
        om = [fw.carve(f"om{i}", [128, 512], F32) for i in range(2)]
        A = fw.carve("A", [128, 512], F32)
        sqb = fw.carve("sqb", [128, 512], BF16)
        rstd = fw.carve("rstd", [128, 512], F32)
        mixo = [fw.carve(f"mixo{i}", [128, 512], BF16) for i in range(2)]
        accs = [fw.carve(f"acc{i}", [128, 512], F32) for i in range(2)]
        ones_f = fw.carve("ones_f", [128, 128], F32)
        fw.memset(ones_f[:], 1.0)
        qblocks = [(q0, 512, 0, 66) for q0 in range(0, TOK, 512)]
        if with_ctx_q:
            qblocks.append((TOK, 256, 64, 66))
        sbank = (pb[0], pb[1], pb[7])
        nq_all = NTOK if with_ctx_q else TOK

        def load_head(hd):
            kT, vv, qh, agh = kTb[hd % 2], vvb[hd % 2], qhb[hd % 2], aghb[hd % 2]
            fw.dma("sp", V(kT.t[:, 0:8192].rearrange("p (r c) -> p r c", r=4), kT.whole),
                   V(kv_dst[hd].t[:, 0:2048].rearrange("(r p) c -> p r c", p=128), kv_dst[hd].whole))
            fw.dma("sp", kT[:, 8192:8448], V(kc_s.t[hd], [kc_s.part(16).res, kc_s.part(17).res]))
            fw.dma("sp", V(vv.t[:, 0:64, :].rearrange("p (r c) e -> p r c e", r=4), vv.whole),
                   V(kv_dst[hd].t[:, 2048:4096].rearrange("(r p) (c e) -> p r c e", p=128, e=128), kv_dst[hd].whole))
            fw.dma("sp", vv[:, 64:66, :], V(vc_s.t[:, hd * 128:(hd + 1) * 128].rearrange("(c p) e -> p c e", p=128),
                                           [vc_s.part(16).res, vc_s.part(17).res]))
            qres = [qT_s.part(c).res for c in range(NLC if with_ctx_q else NCH)]
            fw.dma("sp", qh[:, 0:nq_all], V(qT_s.t[hd, :, 0:nq_all], qres))
            fw.dma("sp", agh[:, 0:NTOK], V(agT_s.t[hd], [agT_s.part(c).res for c in range(NLC)]))

        load_head(0)
        for hd in range(4):
            if hd + 1 < 4:
                load_head(hd + 1)
            kT, vv, qh, agh = kTb[hd % 2], vvb[hd % 2], qhb[hd % 2], aghb[hd % 2]
            for qi, (q0, nq_, kc0, kc1) in enumerate(qblocks):
                for m in range(2):
                    lo, hi = m * 64, (m + 1) * 64
                    ps_o, ps_n = pb[2 + m], pb[4 + m]
                    kcs = list(range(kc0, kc1))
                    n = len(kcs)

                    def qk(i):
                        kc = kcs[i]
                        fw.mm(sbank[i % 3][:, 0:nq_], kT[lo:hi, kc * 128:(kc + 1) * 128], qh[lo:hi, q0:q0 + nq_], True, True)

                    qk(0)
                    if n > 1:
                        qk(1)
                    for i, kc in enumerate(kcs):
                        if i + 2 < n:
                            qk(i + 2)
                        p = pT[i % 4]
                        fw.act(p[:, 0:nq_], sbank[i % 3][:, 0:nq_], AF.Exp, scale=0.125)
                        first, lastk = (i == 0), (i == n - 1)
                        fw.mm(ps_o[:, 0:nq_], vv[:, kc, :], p[:, 0:nq_], start=first, stop=lastk)
                        eng_ = "dve" if i % 2 == 0 else "pool"
                        acc_ = accs[i % 2]
                        if i < 2:
                            fw.copy(acc_[:, 0:nq_], p[:, 0:nq_], eng=eng_)
                        else:
                            fw.tt(acc_[:, 0:nq_], acc_[:, 0:nq_], p[:, 0:nq_], ALU.add, eng=eng_)
                    if n > 1:
                        fw.tt(accs[0][:, 0:nq_], accs[0][:, 0:nq_], accs[1][:, 0:nq_], ALU.add)
                    fw.mm(ps_n[:, 0:nq_], ones_f[:], accs[0][:, 0:nq_], True, True)
                    fw.recip(rec[:, 0:nq_], ps_n[:, 0:nq_])
                    fw.tt(om[m][:, 0:nq_], ps_o[:, 0:nq_], rec[:, 0:nq_], ALU.mult)
                fw.stt(A[:, 0:nq_], om[1][:, 0:nq_], neglam[:], om[0][:, 0:nq_], ALU.mult, ALU.add)
                fw.act(sqb[:, 0:nq_], A[:, 0:nq_], AF.Square)
                fw.mm(pb[6][:, 0:nq_], ones_bf[:], sqb[:, 0:nq_], True, True)
                fw.act(rstd[:, 0:nq_], pb[6][:, 0:nq_], AF.Sqrt, bias=eps_t[:], scale=1.0 / 128)
                fw.recip(rstd[:, 0:nq_], rstd[:, 0:nq_])
                fw.stt(A[:, 0:nq_], A[:, 0:nq_], subln[:], rstd[:, 0:nq_], ALU.mult, ALU.mult)
                mo = mixo[qi % 2]
                fw.tt(mo[:, 0:nq_], A[:, 0:nq_], agh[:, q0:q0 + nq_], ALU.mult, eng="pool")
                fw.dma("act", V(mixT_s.t[hd, :, q0:q0 + nq_], mixT_s.part((hd, qi)).res), mo[:, 0:nq_])
        fw.phase_reset()

    def phase_halo(l):
        xr = lambda c: [xbc_pad.part((f"xbc{i}", c)).res for i in range(3)]
        fw.dma("sp", hx_stage[0:1, :], V(xbc_pad.t[1:2, :], xr(0)))
        fw.dma("sp", hx_stage[1:2, :], V(xbc_pad.t[TOK:TOK + 1, :], xr(NCH - 1)))
        fw.dma("sp", V(hx_src.t[bass.ds(rank * 2, 2), :], hx_src.whole), hx_stage[:, :])
        fw.collective("AllReduce", ALU.add, GROUPS, hx_src[:, :], hx_dst[:, :])
        fw.dma("sp", hxL[1:9, :], hx_dst[:, :])
        fw.dma("sp", hxR[0:6, :], hx_dst[2:8, :])
        fw.dma("sp", V(xbc_pad.t[0:1, :], xbc_pad.part("hl").res), V(hxL.t[bass.ds(rank * 2, 1), :], hxL.whole))
        fw.dma("sp", V(xbc_pad.t[TOK + 1:TOK + 2, :], xbc_pad.part("hr").res), V(hxR.t[bass.ds(rank * 2, 1), :], hxR.whole))

    def rope_apply(out, x, lc, nh, t1, t2):
        cosb = V(rope.t[:, lc, 0:1, :].to_broadcast([128, nh, 32]), rope.whole)
        sinb = V(rope.t[:, lc, 1:2, :].to_broadcast([128, nh, 32]), rope.whole)
        fw.tt(t1[:, 0:nh, :], x[:, :, 0:32], cosb, ALU.mult)
        fw.tt(t2[:, 0:nh, :], x[:, :, 32:64], sinb, ALU.mult, eng="pool")
        fw.tt(out[:, :, 0:32], t1[:, 0:nh, :], t2[:, 0:nh, :], ALU.subtract)
        fw.tt(t1[:, 0:nh, :], x[:, :, 0:32], sinb, ALU.mult)
        fw.tt(t2[:, 0:nh, :], x[:, :, 32:64], cosb, ALU.mult, eng="pool")
        fw.tt(out[:, :, 32:64], t1[:, 0:nh, :], t2[:, 0:nh, :], ALU.add)

    def chain_combine(Sin, Sctx, d, col0, ncol, nparts, Aexp_of, tmp, slot):
        fw.copy(Sin[0:nparts, :], Sctx[0:nparts, :])
        order = range(4) if d == 0 else range(3, -1, -1)
        for sidx in order:
            fw.dma("sp", slot[0:nparts, 0:ncol], st_dst[d][sidx * 128:sidx * 128 + nparts, col0:col0 + ncol])
            Aexp_of(sidx, tmp)
            fw.tt(tmp[0:nparts, :], tmp[0:nparts, :], slot[0:nparts, 0:ncol], ALU.add)
            fw.tt(tmp[0:nparts, :], tmp[0:nparts, :], Sin[0:nparts, :], ALU.subtract)
            mcol = cmask[:, d * 4 + sidx:d * 4 + sidx + 1]
            fw.stt(Sin[0:nparts, :], tmp[0:nparts, :], V(mcol.ap[0:nparts], mcol.res), Sin[0:nparts, :], ALU.mult, ALU.add)

    def phase_ret(l, need_ctx):
        rett = fw.carve("rett", [128, 4 * 128 + 24], F32)
        rnorm = fw.carve("rnorm", [128, 128], F32)
        fw.dma("sp", rett[:], rett_d[:])
        fw.dma("sp", rnorm[:], ssdp_d[l, :, NP - 128:NP])
        Dret = V(rett.t[:, 0:512].rearrange("p (h i) -> p h i", i=128), rett.whole)
        tab = lambda k: V(rett.t[:, 512 + 4 * k:512 + 4 * k + 4], rett.whole)
        qk = fw.carve("qk", [128, 8, 64], F32)
        qkr = fw.carve("qkr", [128, 8, 64], F32)
        t1 = fw.carve("rt1", [128, 8, 32], F32)
        t2 = fw.carve("rt2", [128, 8, 32], F32)
        rv = fw.carve("rv", [128, 512], F32)
        rvbf = fw.carve("rvbf", [128, 512], BF16)
        kte = [fw.carve(f"kte{d}", [128, 4, 64], BF16) for d in range(2)]
        q3 = fw.carve("q3", [128, 3, 4, 64], BF16)
        kbf = fw.carve("kbf", [128, 4, 64], BF16)
        qT = fw.carve("qT", [64, 3, 4, 128], BF16)
        kT = fw.carve("kT", [64, 4, 128], BF16)
        Wt = fw.carve("Wt", [128, 4, 128], BF16)
        R = [fw.carve(f"R{d}", [64, 512], F32) for d in range(2)]
        Rbf = [fw.carve(f"Rbf{d}", [64, 512], BF16) for d in range(2)]
        Rctx = [fw.carve(f"Rctx{d}", [64, 512], F32) for d in range(2)]
        Pb = fw.carve("Pb", [128, 4], F32)
        cs_sb = [fw.carve(f"cs_sb{d}", [64, 512], F32) for d in range(2)]
        tmp = fw.carve("rtmp", [64, 512], F32)
        slot = fw.carve("rslot", [64, 512], F32)
        rg = fw.carve("rg", [128, 512], F32)
        ysq = fw.carve("ysq", [128, 4, 128], F32)
        yss = fw.carve("yss", [128, 4], F32)
        yn = fw.carve("yn", [128, 4, 128], F32)
        ybf = fw.carve("ybf", [128, 512], BF16)
        ytb = fw.carve("ytb", [128, 4, 128], BF16)
        bc4 = lambda v, n: V(v.ap.unsqueeze(2).to_broadcast([v.ap.shape[0], 4, n]), v.res)

        def prep(lc):
            is_ctx = lc >= NCH
            rows = slice(lc * 128, (lc + 1) * 128)
            pr = lambda n: proj_s.part((n, lc)).res
            fw.dma("sp", V(qk.t[:].rearrange("p a b -> p (a b)"), qk.whole), V(proj_s.t[rows, 4640:5152], pr("rqk")))
            fw.dma("sp", rv[:], V(proj_s.t[rows, 5152:5664], pr("rv")))
            if is_ctx:
                src = qk
            else:
                rope_apply(qkr, qk, lc, 8, t1, t2)
                src = qkr
            fw.copy(rvbf[:], rv[:], eng="pool")
            for d in range(2):
                fw.tt(kte[d][:], src[:, 4:8, :], bc4(tab(d), 64), ALU.mult)
            return src

        def chunk_states(start_banks=(0, 1)):
            for d in range(2):
                bank = pb[start_banks[d]]
                for h in range(4):
                    fw.mm(bank[0:64, h * 128:(h + 1) * 128], kte[d][:, h, :], rvbf[:, h * 128:(h + 1) * 128],
                          start=(h == 0), stop=(h == 3), skip_group_check=True)
            return [pb[start_banks[0]], pb[start_banks[1]]]

        A128 = [tab(4), tab(5)]

        def fold(acc, csb, lc, store):
            fw.tt(V(acc[0].t[:].rearrange("p (h n) -> p h n", n=128), acc[0].whole),
                  V(acc[0].t[:].rearrange("p (h n) -> p h n", n=128), acc[0].whole),
                  V(A128[0].ap[0:64].unsqueeze(2).to_broadcast([64, 4, 128]), A128[0].res), ALU.mult)
            fw.tt(acc[0][:], acc[0][:], csb[0][0:64, :], ALU.add)
            fw.tt(V(tmp.t[:].rearrange("p (h n) -> p h n", n=128), tmp.whole),
                  V(csb[1].t[0:64, :].rearrange("p (h n) -> p h n", n=128), csb[1].whole),
                  V(Pb.t[0:64, :].unsqueeze(2).to_broadcast([64, 4, 128]), Pb.whole), ALU.mult)
            fw.tt(acc[1][:], acc[1][:], tmp[:], ALU.add)
            fw.tt(Pb[:], Pb[:], A128[1], ALU.mult)
            if store:
                for d in range(2):
                    fw.copy(cs_sb[d][:], csb[d][0:64, :])
                    fw.dma("sp", V(cst_s.t[lc, d, 0:64, 1024:1536], cst_s.part(("r", lc, d)).res), cs_sb[d][:])

        for grp in ((16, 17), tuple(range(NCH))):
            for d in range(2):
                fw.memset(R[d][:], 0.0)
            fw.memset(Pb[:], 1.0)
            for lc in grp:
                prep(lc)
                if KCUT >= 2:
                    csb = chunk_states()
                if KCUT >= 3:
                    fold(R, csb, lc, KCUT >= 4)
            if grp[0] == 16:
                for d in range(2):
                    fw.copy(Rctx[d][:], R[d][:])
        if "ret_sf" in dbg:
            fw.dma("sp", dbg["ret_sf"][:, :], Rctx[0][:])
            fw.dma("sp", dbg["ret_sb"][:, :], Rctx[1][:])
        if stop_after == "ret_p1":
            fw.phase_reset(); return
        for d in range(2):
            fw.dma("sp", st_stage[d][0:64, 1024:1536], R[d][:])
            fw.dma("sp", V(st_src[d].t[bass.ds(rank * 128, 128), :], st_src[d].whole), st_stage[d][:, :])
            fw.collective("AllReduce", ALU.add, GROUPS, st_src[d][:, :], st_dst[d][:, :])
        if stop_after == "ret_x":
            fw.phase_reset(); return
        A2048 = [[(1.0 - 2.0 ** -e) ** 2048 for e in RET_EXP_F], [(1.0 - 2.0 ** -e) ** 2048 for e in RET_EXP_B]]
        Rin = [fw.carve(f"Rin{d}", [64, 512], F32) for d in range(2)]
        for d in range(2):
            def aexp(sidx, t_, d=d):
                for h in range(4):
                    fw.ts(t_[0:64, h * 128:(h + 1) * 128], Rin[d][0:64, h * 128:(h + 1) * 128], float(A2048[d][h]), ALU.mult)
            chain_combine(Rin[d], Rctx[d], d, 1024, 512, 64, aexp, tmp, slot)
        snap = fw.carve("rsnap", [64, 512], BF16)
        csl = fw.carve("rcsl", [64, 512], F32)
        fw.copy(R[1][:], Rin[1][:])
        for lc in range(NCH - 1, -1, -1):
            fw.copy(snap[:], R[1][:])
            fw.dma("sp", V(sb_s.t[lc, 0:64, 1024:1536], sb_s.part(("r", lc)).res), snap[:])
            fw.dma("sp", csl[:], V(cst_s.t[lc, 1, 0:64, 1024:1536], cst_s.part(("r", lc, 1)).res))
            fw.tt(V(R[1].t[:].rearrange("p (h n) -> p h n", n=128), R[1].whole),
                  V(R[1].t[:].rearrange("p (h n) -> p h n", n=128), R[1].whole),
                  V(A128[1].ap[0:64].unsqueeze(2).to_broadcast([64, 4, 128]), A128[1].res), ALU.mult)
            fw.tt(R[1][:], R[1][:], csl[:], ALU.add)
        if need_ctx:
            fw.dma("sp", csl[:], V(cst_s.t[17, 1, 0:64, 1024:1536], cst_s.part(("r", 17, 1)).res))
            fw.copy(snap[:], csl[:])
            fw.dma("sp", V(sb_s.t[16, 0:64, 1024:1536], sb_s.part(("r", 16)).res), snap[:])
            snap0 = fw.carve("rsnap0", [64, 512], BF16)
            fw.memset(snap0[:], 0.0)
            fw.dma("sp", V(sb_s.t[17, 0:64, 1024:1536], sb_s.part(("r", 17)).res), snap0[:])
        if stop_after == "ret_p2":
            fw.phase_reset(); return
        groups3 = [tuple(range(NCH))] + ([(16, 17)] if need_ctx else [])
        for grp in groups3:
            if grp[0] == 16:
                fw.memset(R[0][:], 0.0)
            else:
                fw.copy(R[0][:], Rin[0][:])
            for lc in grp:
                is_ctx = lc >= NCH
                rows = slice(lc * 128, (lc + 1) * 128)
                src = prep(lc)
                fw.dma("sp", rg[:], V(proj_s.t[rows, 5664:6176], proj_s.part(("rg", lc)).res))
                fw.dma("sp", Rbf[1][:], V(sb_s.t[lc, 0:64, 1024:1536], sb_s.part(("r", lc)).res))
                fw.copy(Rbf[0][:], R[0][:])
                fw.copy(q3[:, 0, :, :], src[:, 0:4, :])
                fw.tt(q3[:, 1, :, :], src[:, 0:4, :], bc4(tab(2), 64), ALU.mult)
                fw.tt(q3[:, 2, :, :], src[:, 0:4, :], bc4(tab(3), 64), ALU.mult, eng="pool")
                fw.copy(kbf[:], src[:, 4:8, :])
                tqa = pbh[2]
                tqav = tqa.t[0:64, 0:1024].rearrange("p (k h c) -> p k h c", k=2, h=4)
                tqb = pbh[3]
                tqbv = tqb.t[0:64, 0:512].rearrange("p (h c) -> p h c", h=4)
                for k3 in range(2):
                    for h in range(4):
                        fw.transpose(V(tqav[:, k3, h, :], tqa.whole), q3[:, k3, h, :], ident_bf[:], last=(k3 == 1 and h == 3))
                for h in range(4):
                    fw.transpose(V(tqbv[:, h, :], tqb.whole), q3[:, 2, h, :], ident_bf[:], last=(h == 3))
                fw.copy(qT[:, 0:2, :, :], V(tqav, tqa.whole))
                fw.copy(qT[:, 2, :, :], V(tqbv, tqb.whole))
                tk = pbh[4]
                tkv = tk.t[0:64, 0:512].rearrange("p (h c) -> p h c", h=4)
                for h in range(4):
                    fw.transpose(V(tkv[:, h, :], tk.whole), kbf[:, h, :], ident_bf[:], last=(h == 3))
                fw.copy(kT[:], V(tkv, tk.whole))
                sc = pb[5]
                for h in range(4):
                    fw.mm(sc[:, h * 128:(h + 1) * 128], kT[:, h, :], qT[:, 0, h, :], start=(h == 0), stop=(h == 3), skip_group_check=True)
                fw.tt(Wt[:], V(sc.t[:, :].rearrange("p (h i) -> p h i", i=128), sc.whole), Dret, ALU.mult)
                csb = chunk_states((0, 1))
                yb = pb[6]
                for h in range(4):
                    o = yb[:, h * 128:(h + 1) * 128]
                    fw.mm(o, Wt[:, h, :], rvbf[:, h * 128:(h + 1) * 128], start=(h == 0), stop=False, last=False, skip_group_check=True)
                    fw.mm(o, qT[:, 1, h, :], Rbf[0][:, h * 128:(h + 1) * 128], start=False, stop=False, last=False, skip_group_check=True)
                    fw.mm(o, qT[:, 2, h, :], Rbf[1][:, h * 128:(h + 1) * 128], start=False, stop=(h == 3), last=(h == 3), skip_group_check=True)
                fw.tt(V(R[0].t[:].rearrange("p (h n) -> p h n", n=128), R[0].whole),
                      V(R[0].t[:].rearrange("p (h n) -> p h n", n=128), R[0].whole),
                      V(A128[0].ap[0:64].unsqueeze(2).to_broadcast([64, 4, 128]), A128[0].res), ALU.mult)
                fw.tt(R[0][:], R[0][:], csb[0][0:64, :], ALU.add)
                ybv = V(yb.t[:, :].rearrange("p (h n) -> p h n", n=128), yb.whole)
                fw.act(ysq[:], ybv, AF.Square)
                fw.reduce(yss[:], ysq[:], ALU.add)
                fw.act(yss[:], yss[:], AF.Sqrt, bias=eps_t[:], scale=1.0 / 128)
                fw.recip(yss[:], yss[:])
                fw.tt(yn[:], ybv, V(yss.t[:].unsqueeze(2).to_broadcast([128, 4, 128]), yss.whole), ALU.mult)
                fw.tt(yn[:], yn[:], V(rnorm.t[:].unsqueeze(1).to_broadcast([128, 4, 128]), rnorm.whole), ALU.mult, eng="pool")
                fw.act(rg[:], rg[:], AF.Silu)
                fw.tt(ybf[:], V(yn.t[:].rearrange("p h n -> p (h n)"), yn.whole), rg[:], ALU.mult)
                to = pbh[7]
                tov = to.t[:, 0:512].rearrange("p (h c) -> p h c", h=4)
                for h in range(4):
                    fw.transpose(V(tov[:, h, :], to.whole), ybf[:, h * 128:(h + 1) * 128], ident_bf[:], last=(h == 3))
                fw.copy(ytb[:], V(tov, to.whole))
                fw.dma("sp", V(mixT_s.t[12:16, :, rows].rearrange("h p c -> p h c"), mixT_s.part(("ret", lc)).res), ytb[:])
        fw.phase_reset()

    def phase_ssd(l, need_ctx):
        OW, OB, ODT, OA, ODD = 0, 4608, 6144, 6176, 6208
        prm = fw.carve("prm", [128, 6224], F32)
        fw.dma("sp", prm[:], ssdp_d[l, :, 0:6224])
        tri = fw.carve("tri", [128, 5, 128], F32)
        fw.dma("sp", tri[:], tri_d[:])
        ssdn = fw.carve("ssdn", [128, 8], F32)
        fw.dma("sp", ssdn[:], ssdn_d[l])
        negA = fw.carve("negA", [128, 32], F32)
        fw.act(negA[:], prm[:, OA:OA + 32], AF.Exp)
        fw.ts(negA[:], negA[:], -1.0, ALU.mult)
        one_t = fw.carve("one_t", [128, 1], F32)
        fw.memset(one_t[:], 1.0)
        U = [fw.carve(f"U{i}", [128, 1536], F32) for i in range(3)]
        dtr = fw.carve("dtr", [128, 32], F32)
        la = fw.carve("la", [128, 32], F32)
        E = fw.carve("E", [128, 96], F32)
        praw = fw.carve("praw", [128, 96], F32)
        cumraw = Buf(fw, "cumraw", _APHandle(praw.t[:, 0:32]))
        cumraw.whole = praw.whole
        tots = fw.carve("tots", [128, 32], F32)
        v = [fw.carve(f"v{d}", [128, 1024], BF16) for d in range(2)]
        vte = [fw.carve(f"vte{d}", [128, 1024], BF16) for d in range(2)]
        BCbf = fw.carve("BCbf", [128, 512], BF16)
        BCT = fw.carve("BCT", [128, 4, 128], BF16)
        zt = fw.carve("zt", [128, 1024], F32)
        R1 = fw.carve("R1", [128, 16, 128], F32)
        seg = fw.carve("seg", [128, 16, 128], F32)
        Dm = fw.carve("Dm", [128, 16, 128], BF16)
        Sm = [fw.carve(f"Sm{d}", [128, 2, 128], F32) for d in range(2)]
        Wt = fw.carve("Wt", [128, 16, 128], BF16)
        S = [fw.carve(f"S{d}", [128, 1024], F32) for d in range(2)]
        Sx = [fw.carve(f"Sx{d}", [128, 1024], F32) for d in range(2)]
        Sbf = [fw.carve(f"Sbf{d}", [128, 1024], BF16) for d in range(2)]
        Pb = fw.carve("Pb", [128, 16], F32)
        yt = fw.carve("yt", [128, 1024], F32)
        y2 = fw.carve("y2", [128, 1024], F32)
        vtmp = Buf(fw, "vtmp", _APHandle(y2.t[:].rearrange("p (h n) -> p h n", n=64)))
        vtmp.whole = y2.whole
        gss = fw.carve("gss", [128, 2], F32)
        ybf = fw.carve("ybf", [128, 1024], BF16)
        ytb = fw.carve("ytb", [128, 8, 128], BF16)
        Aex = fw.carve("Aex", [128, 16], F32)
        h16 = lambda vv: V(vv.ap.unsqueeze(2).to_broadcast([128, 16, 64]), vv.res)
        as16 = lambda b_: V(b_.t[:].rearrange("p (h n) -> p h n", n=64), b_.whole)

        def prep(lc):
            is_ctx = lc >= NCH
            src_t, r0 = (xbc_cpad, (lc - NCH) * 128) if is_ctx else (xbc_pad, lc * 128)
            rr = [xbc_pad.part("hl").res, xbc_pad.part("hr").res]
            for k in range(3):
                fw.dma("sp", U[k][:], V(src_t.t[r0 + k:r0 + k + 128, :], rr))
            fw.dma("sp", dtr[:], V(proj_s.t[lc * 128:(lc + 1) * 128, 3584:3616], proj_s.part(("dtr", lc)).res))
            fw.tt(U[0][:], U[0][:], prm[:, OW:OW + 1536], ALU.mult, eng="pool")
            fw.tt(U[1][:], U[1][:], prm[:, OW + 1536:OW + 3072], ALU.mult)
            fw.tt(U[2][:], U[2][:], prm[:, OW + 3072:OW + 4608], ALU.mult, eng="pool")
            fw.tt(U[1][:], U[1][:], U[0][:], ALU.add)
            fw.tt(U[1][:], U[1][:], U[2][:], ALU.add)
            fw.tt(U[1][:], U[1][:], prm[:, OB:OB + 1536], ALU.add)
            fw.act(U[0][:], U[1][:], AF.Silu)
            fw.tt(dtr[:], dtr[:], prm[:, ODT:ODT + 32], ALU.add)
            fw.act(dtr[:], dtr[:], AF.Exp)
            fw.act(dtr[:], dtr[:], AF.Ln, bias=one_t[:])
            fw.tt(la[:], dtr[:], negA[:], ALU.mult)
            pe = pb[0]
            for i, (w, c0, c1) in enumerate(((0, 0, 16), (1, 16, 32), (2, 0, 16), (3, 16, 32), (4, 0, 32))):
                o0 = (0, 16, 32, 48, 64)[i]
                fw.mm(pe[:, o0:o0 + (c1 - c0)], tri[:, w, :], la[:, c0:c1], start=(i == 0), stop=(i == 4), skip_group_check=True)
            fw.copy(praw[:], pe[:, 0:96])
            fw.act(E[:], praw[:], AF.Exp)
            fw.tt(tots[:], tots[:], praw[:, 64:96], ALU.add)
            xs = V(U[0].t[:, 0:1024].rearrange("p (h n) -> p h n", n=64), U[0].whole)
            for d in range(2):
                fw.tt(vtmp[:], xs, h16(dtr[:, d * 16:(d + 1) * 16]), ALU.mult)
                fw.copy(as16(v[d]), vtmp[:], eng="pool")
                fw.tt(as16(vte[d]), vtmp[:], h16(E[:, 32 + d * 16:48 + d * 16]), ALU.mult)
            fw.copy(BCbf[:], U[0][:, 1024:1536], eng="pool")

        def chunk_state(d):
            banks = (pb[4], pb[5])
            for g in range(2):
                fw.mm(banks[g][:, :], BCbf[:, g * 128:(g + 1) * 128], vte[d][:, g * 512:(g + 1) * 512], True, True)
            return banks

        def mulA(dst, srcS, acol):
            fw.tt(as16(dst), as16(srcS), h16(acol), ALU.mult)

        for grp in ((16, 17), tuple(range(NCH))):
            for d in range(2):
                fw.memset(S[d][:], 0.0)
            fw.memset(Pb[:], 1.0)
            fw.memset(tots[:], 0.0)
            for lc in grp:
                prep(lc)
                for d in range(2):
                    banks = chunk_state(d)
                    csv = V(pb_t[:, 4:6, :], [pb[4].whole, pb[5].whole])
                    cs_sb = V(seg.t[:, d * 8:(d + 1) * 8, :].rearrange("p a (g c) -> p (a g) c", g=2)[:, 0:2, :] if False else seg.t[:, d * 8:(d + 1) * 8, :], seg.whole)
                    cs_flat = V(seg.t[:].rearrange("p h n -> p (h n)")[:, d * 1024:(d + 1) * 1024], seg.whole)
                    fw.copy(V(cs_flat.ap.rearrange("p (g c) -> p g c", g=2), seg.whole), csv)
                    fw.dma("sp", V(cst_s.t[lc, d, :, 0:1024], cst_s.part(("s", lc, d)).res), cs_flat)
                    if d == 0:
                        mulA(S[0], S[0], E[:, 64:80])
                        fw.tt(S[0][:], S[0][:], cs_flat, ALU.add)
                    else:
                        fw.tt(as16(y2), V(cs_flat.ap.rearrange("p (h n) -> p h n", n=64), seg.whole), h16(Pb[:, :]), ALU.mult)
                        fw.tt(S[1][:], S[1][:], y2[:], ALU.add)
                        fw.tt(Pb[:], Pb[:], E[:, 80:96], ALU.mult)
                fw.dma("sp", V(cst_s.t[lc, 0, :, 1536:1568], cst_s.part(("e", lc)).res), E[:, 64:96])
            if grp[0] == 16:
                for d in range(2):
                    fw.copy(Sx[d][:], S[d][:])
        if "ssd_sf" in dbg:
            fw.dma("sp", dbg["ssd_sf"][:, :], Sx[0][:])
            fw.dma("sp", dbg["ssd_sb"][:, :], Sx[1][:])
        for d in range(2):
            fw.dma("sp", st_stage[d][:, 0:1024], S[d][:])
            fw.dma("sp", st_stage[d][:, 1536:1552], tots[:, d * 16:(d + 1) * 16])
            fw.dma("sp", V(st_src[d].t[bass.ds(rank * 128, 128), :], st_src[d].whole), st_stage[d][:, :])
            fw.collective("AllReduce", ALU.add, GROUPS, st_src[d][:, :], st_dst[d][:, :])
        for d in range(2):
            order = range(4) if d == 0 else range(3, -1, -1)
            for sidx in order:
                fw.dma("sp", yt[:], st_dst[d][sidx * 128:(sidx + 1) * 128, 0:1024])
                fw.dma("sp", Aex[:], st_dst[d][sidx * 128:(sidx + 1) * 128, 1536:1552])
                fw.act(Aex[:], Aex[:], AF.Exp)
                mulA(y2, Sx[d], Aex[:, :])
                fw.tt(y2[:], y2[:], yt[:], ALU.add)
                fw.tt(y2[:], y2[:], Sx[d][:], ALU.subtract)
                fw.stt(Sx[d][:], y2[:], cmask[:, d * 4 + sidx:d * 4 + sidx + 1], Sx[d][:], ALU.mult, ALU.add)
        for lc in range(NCH - 1, -1, -1):
            fw.copy(Sbf[1][:], Sx[1][:])
            fw.dma("sp", V(sb_s.t[lc, :, 0:1024], sb_s.part(("s", lc)).res), Sbf[1][:])
            fw.dma("sp", yt[:], V(cst_s.t[lc, 1, :, 0:1024], cst_s.part(("s", lc, 1)).res))
            fw.dma("sp", Aex[:], V(cst_s.t[lc, 0, :, 1552:1568], cst_s.part(("e", lc)).res))
            mulA(Sx[1], Sx[1], Aex[:, :])
            fw.tt(Sx[1][:], Sx[1][:], yt[:], ALU.add)
        if need_ctx:
            fw.dma("sp", yt[:], V(cst_s.t[17, 1, :, 0:1024], cst_s.part(("s", 17, 1)).res))
            fw.copy(Sbf[1][:], yt[:])
            fw.dma("sp", V(sb_s.t[16, :, 0:1024], sb_s.part(("s", 16)).res), Sbf[1][:])
            fw.memset(Sbf[0][:], 0.0)
            fw.dma("sp", V(sb_s.t[17, :, 0:1024], sb_s.part(("s", 17)).res), Sbf[0][:])
        if stop_after == "ssd_p2":
            fw.phase_reset(); return
        groups3 = [tuple(range(NCH))] + ([(16, 17)] if need_ctx else [])
        for grp in groups3:
            if grp[0] == 16:
                fw.memset(Sx[0][:], 0.0)
            for lc in grp:
                rows = slice(lc * 128, (lc + 1) * 128)
                prep(lc)
                fw.dma("sp", zt[:, 0:512], V(proj_s.t[rows, 3616:4128], proj_s.part(("z0", lc)).res))
                fw.dma("sp", zt[:, 512:1024], V(proj_s.t[rows, 4128:4640], proj_s.part(("z1", lc)).res))
                fw.dma("sp", Sbf[1][:], V(sb_s.t[lc, :, 0:1024], sb_s.part(("s", lc)).res))
                fw.copy(Sbf[0][:], Sx[0][:], eng="pool")
                tb_ = pbh[1]
                tbv = tb_.t[:, 0:512].rearrange("p (a c) -> p a c", a=4)
                for a in range(4):
                    fw.transpose(V(tbv[:, a, :], tb_.whole), BCbf[:, a * 128:(a + 1) * 128], ident_bf[:], last=(a == 3))
                fw.copy(BCT[:], V(tbv, tb_.whole))
                sc = pb[1]
                scv = sc.t[:, 256:512].rearrange("p (g i) -> p g i", g=2)
                for g in range(2):
                    fw.mm(V(scv[:, g, :], sc.whole), BCT[:, g, :], BCT[:, 2 + g, :], start=False if False else (g == 0), stop=(g == 1), skip_group_check=True)
                for d in range(2):
                    fw.tt(Sm[d][:], V(scv, sc.whole), V(tri.t[:, d:d + 1, :].to_broadcast([128, 2, 128]), tri.whole), ALU.mult)
                yb = (pb[6], pb[7])
                for d in range(2):
                    fw.tt(R1[:], V(la.t[:, d * 16:(d + 1) * 16].unsqueeze(2).to_broadcast([128, 16, 128]), la.whole),
                          V(tri.t[:, d:d + 1, :].to_broadcast([128, 16, 128]), tri.whole), ALU.mult, eng="pool")
                    for q in range(4):
                        bank = pb[2 + q % 2]
                        fw.mm(bank[:, :], tri[:, 4, :], V(R1.t[:, 4 * q:4 * q + 4, :].rearrange("p h n -> p (h n)"), R1.whole), True, True)
                        for hh in range(4):
                            h = 4 * q + hh
                            fw.ts(seg[:, h, :], bank[:, hh * 128:(hh + 1) * 128], cumraw[:, d * 16 + h:d * 16 + h + 1], ALU.subtract, 0.0, ALU.min)
                    fw.act(Dm[:], seg[:], AF.Exp)
                    for g in range(2):
                        fw.tt(Wt[:, g * 8:(g + 1) * 8, :], Dm[:, g * 8:(g + 1) * 8, :],
                              V(Sm[d].t[:, g:g + 1, :].to_broadcast([128, 8, 128]), Sm[d].whole), ALU.mult)
                    for h in range(16):
                        fw.mm(yb[h // 8][:, (h % 8) * 64:(h % 8 + 1) * 64], Wt[:, h, :], v[d][:, h * 64:(h + 1) * 64],
                              start=(d == 0 and h % 8 == 0), stop=(d == 1 and h % 8 == 7), last=(d == 1 and h % 8 == 7), skip_group_check=True)
                for d in range(2):
                    for g in range(2):
                        fw.mm(pb[2 + g][:, :], BCT[:, 2 + g, :], Sbf[d][:, g * 512:(g + 1) * 512], True, True)
                    ysv = V(pb_t[:, 2:4, :].rearrange("p a (h n) -> p (a h) n", n=64), [pb[2].whole, pb[3].whole])
                    fw.tt(as16(yt if d == 0 else y2), ysv, h16(E[:, d * 16:(d + 1) * 16]), ALU.mult)
                fw.tt(yt[:], yt[:], y2[:], ALU.add)
                yv = V(pb_t[:, 6:8, :].rearrange("p a c -> p (a c)") if False else pb_t[:, 6:8, :], [pb[6].whole, pb[7].whole])
                fw.tt(V(yt.t[:].rearrange("p (a c) -> p a c", a=2), yt.whole), V(yt.t[:].rearrange("p (a c) -> p a c", a=2), yt.whole), yv, ALU.add)
                xs = V(U[0].t[:, 0:1024].rearrange("p (h n) -> p h n", n=64), U[0].whole)
                fw.tt(as16(y2), xs, h16(prm[:, ODD:ODD + 16]), ALU.mult, eng="pool")
                fw.tt(yt[:], yt[:], y2[:], ALU.add)
                fw.act(zt[:], zt[:], AF.Silu)
                fw.tt(yt[:], yt[:], zt[:], ALU.mult)
                banks = chunk_state(0)
                mulA(Sx[0], Sx[0], E[:, 64:80])
                fw.tt(V(Sx[0].t[:].rearrange("p (g c) -> p g c", g=2), Sx[0].whole), V(Sx[0].t[:].rearrange("p (g c) -> p g c", g=2), Sx[0].whole),
                      V(pb_t[:, 4:6, :], [pb[4].whole, pb[5].whole]), ALU.add)
                fw.act(y2[:], yt[:], AF.Square)
                fw.reduce(gss[:], V(y2.t[:].rearrange("p (g c) -> p g c", g=2), y2.whole), ALU.add)
                fw.act(gss[:], gss[:], AF.Sqrt, bias=eps_t[:], scale=1.0 / 512)
                fw.recip(gss[:], gss[:])
                fw.tt(V(ybf.t[:].rearrange("p (g c) -> p g c", g=2), ybf.whole), V(yt.t[:].rearrange("p (g c) -> p g c", g=2), yt.whole),
                      V(gss.t[:].unsqueeze(2).to_broadcast([128, 2, 512]), gss.whole), ALU.mult)
                to = pbh[1]
                tov = to.t[:, 0:1024].rearrange("p (a c) -> p a c", a=8)
                for a in range(8):
                    fw.transpose(V(tov[:, a, :], to.whole), ybf[:, a * 128:(a + 1) * 128], ident_bf[:], last=(a == 7))
                fw.tt(ytb[:], V(tov, to.whole), V(ssdn.t[:].unsqueeze(2).to_broadcast([128, 8, 128]), ssdn.whole), ALU.mult)
                fw.dma("sp", V(mixT_s.t[4:12, :, rows].rearrange("h p c -> p h c"), mixT_s.part(("ssd", lc)).res), ytb[:])
        fw.phase_reset()

    def phase_out(l, need_ctx):
        wout = fw.carve("wout", [128, 16, D], BF16)
        wst = fw.carve("wost", [128, 4, D], F32)
        mx = [fw.carve(f"mx{i}", [128, 16, 128], BF16) for i in range(2)]
        tmp = fw.carve("otmp", [128, 512], F32)
        for q in range(4):
            fw.dma("sp", V(wst.t[:], wst.whole),
                   V(wout_d.t[l, q * 512:(q + 1) * 512, :].rearrange("(f p) c -> p f c", p=128), wout_d.whole))
            fw.copy(wout[:, q * 4:(q + 1) * 4, :], wst[:], eng="pool")
        allmix = [r for r in mixT_s.parts.values()]
        for lc in (range(NLC) if need_ctx else range(NCH)):
            is_ctx = lc >= NCH
            m = mx[lc % 2]
            fw.dma("sp", m[:], V(mixT_s.t[:, :, lc * 128:(lc + 1) * 128].rearrange("f p c -> p f c"), allmix))
            for hh in range(2):
                bank = pb[(lc % 2) * 2 + hh]
                for fc in range(16):
                    fw.mm(bank[:, :], m[:, fc, :], wout[:, fc, hh * 512:(hh + 1) * 512], start=(fc == 0), stop=(fc == 15))
                cs = slice(hh * 512, (hh + 1) * 512)
                fw.tt(tmp[:], bank[:, :], gate[:, 1 if is_ctx else 0, cs], ALU.mult)
                dst = ctx_sb[:, lc - NCH, cs] if is_ctx else x_sb[:, lc, cs]
                fw.tt(dst, dst, tmp[:], ALU.add)
        fw.phase_reset()

    for l in range(depth):
        need_ctx = l < depth - 1
        adaln(l)
        layer_params(l)
        phase_proj(l, need_ctx)
        if stop_after == "t_proj": break
        phase_attn(l, need_ctx)
        if stop_after == "t_attn": break
        phase_halo(l)
        phase_ret(l, need_ctx)
        if stop_after == "t_ret": break
        phase_ssd(l, need_ctx)
        if stop_after == "t_ssd": break
        phase_out(l, need_ctx)
        if l == 0 and "x0" in dbg:
            fw.dma("sp", V(dbg["x0"].t.ap().rearrange("(c p) d -> p c d", p=128), dbg["x0"].whole), V(x_sb.t[:], x_sb.whole))
            fw.dma("sp", V(dbg["ctx0"].t.ap().rearrange("(c p) d -> p c d", p=128), dbg["ctx0"].whole), V(ctx_sb.t[:], ctx_sb.whole))

    if "qT" in dbg:
        fw.dma("sp", dbg["qT"][:], V(qT_s.t[:], [qT_s.part(c).res for c in range(NLC)]))
    if "kv0" in dbg:
        fw.dma("sp", dbg["kv0"][:], kv_dst[0][:, :])
    if "proj" in dbg:
        fw.dma("sp", dbg["proj"][:], V(proj_s.t[:], [r for r in proj_s.parts.values()]))
    if "mixT" in dbg:
        fw.dma("sp", dbg["mixT"][:], V(mixT_s.t[0:4], [r for r in mixT_s.parts.values()]))
    if "mixS" in dbg:
        fw.dma("sp", dbg["mixS"][:], V(mixT_s.t[4:12], [r for r in mixT_s.parts.values()]))
    if "mixR" in dbg:
        fw.dma("sp", dbg["mixR"][:], V(mixT_s.t[12:16], [r for r in mixT_s.parts.values()]))
    fw.dma("sp", V(out_d.t.ap().rearrange("(c p) d -> p c d", p=128), out_d.whole), V(x_sb.t[:], x_sb.whole))
    fw.wait_all("sp", [out_d[:]] + [V(b.t[:], b.whole) for b in dbg.values()])
    return nc, fw


def rope_tables():
    n_freq = 16
    inv_freq = (10000.0 ** (-np.arange(n_freq, dtype=np.float32) / n_freq)).astype(np.float32)
    pos = np.arange(8192)
    row = (pos // 64).astype(np.float32)
    col = (pos % 64).astype(np.float32)
    ang = np.concatenate([row[:, None] * inv_freq, col[:, None] * inv_freq], axis=-1).astype(np.float32)
    return np.cos(ang).astype(np.float32), np.sin(ang).astype(np.float32)


def const_tables():
    j = np.arange(128)[:, None]; i = np.arange(128)[None, :]
    tri = np.stack([(j <= i), (j >= i), (j > i), (j < i), np.ones((128, 128), bool)], axis=1).astype(np.float32)
    gf = np.array([1.0 - 2.0 ** -e for e in RET_EXP_F], np.float64)
    gb = np.array([1.0 - 2.0 ** -e for e in RET_EXP_B], np.float64)
    dif = (i - j).astype(np.float64)
    Dret = np.zeros((128, 4, 128), np.float64)
    for h in range(4):
        Dret[:, h, :] = np.where(dif > 0, gf[h] ** np.abs(dif), 0.0) + np.where(dif < 0, gb[h] ** np.abs(dif), 0.0) + np.where(dif == 0, 2.0, 0.0)
    Dret *= 0.125
    pos = np.arange(128, dtype=np.float64)[:, None]
    te_f = gf[None, :] ** (127 - pos) * 0.125
    te_b = gb[None, :] ** pos * 0.125
    qsc_f = gf[None, :] ** (pos + 1)
    qsc_b = gb[None, :] ** (128 - pos)
    a_f = np.broadcast_to(gf[None, :] ** 128, (128, 4)); a_b = np.broadcast_to(gb[None, :] ** 128, (128, 4))
    rett = np.concatenate([Dret.reshape(128, 512), te_f, te_b, qsc_f, qsc_b, a_f, a_b], axis=1).astype(np.float32)
    return np.ascontiguousarray(tri), np.ascontiguousarray(rett)


def make_inputs(inp):
    cos, sin = rope_tables()
    tri, rett = const_tables()
    ssdp = np.concatenate([inp["ssd_conv_w"].reshape(2, -1), inp["ssd_conv_b"], inp["ssd_dt_bias"].reshape(2, -1),
                           inp["ssd_a_log"].reshape(2, -1), inp["ssd_d"], inp["ret_norm"]], axis=1).astype(np.float32)
    ssdp = np.ascontiguousarray(np.broadcast_to(ssdp[:, None, :], (2, 128, ssdp.shape[1])))
    ssdn = np.ascontiguousarray(inp["ssd_norm"].reshape(2, 8, 128).transpose(0, 2, 1))
    rep = lambda a: np.ascontiguousarray(np.broadcast_to(a[:, None], (a.shape[0], 128) + a.shape[1:]))
    qkg = rep(np.stack([inp["attn_q_norm"], inp["attn_k_norm"]], axis=1))
    lamv = rep(np.stack([inp["lambda_q1"], inp["lambda_k1"], inp["lambda_q2"], inp["lambda_k2"]], axis=1))
    subln = np.ascontiguousarray(inp["attn_subln"][:, :, None])
    maps = []
    for core in range(8):
        b, t = core // 4, core % 4
        lo = t * TOK
        cc = np.stack([inp["c"][b].reshape(8, 128).T, inp["c_ctx"].reshape(8, 128).T], axis=-1)
        rp = np.stack([cos[lo:lo + TOK], sin[lo:lo + TOK]], axis=1)
        rp = rp.reshape(NCH, 128, 2, 32).transpose(1, 0, 2, 3)
        m = {
            "x": np.ascontiguousarray(inp["x"][b, lo:lo + TOK]),
            "ctx": np.ascontiguousarray(inp["ctx"][b]),
            "cc": np.ascontiguousarray(cc.astype(np.float32)),
            "w_ada": inp["w_ada"],
            "b_ada_f": np.ascontiguousarray(inp["b_ada"][:, :2 * D].reshape(2, 16, 128).transpose(0, 2, 1)),
            "b_ada": inp["b_ada"],
            "w_in": inp["w_in"], "w_out": inp["w_out"],
            "qkg": qkg, "lamv": lamv, "subln": subln,
            "rope": np.ascontiguousarray(rp),
            "tri": tri, "rett": rett, "ssdp": ssdp, "ssdn": ssdn,
            "cmask": np.ascontiguousarray(np.broadcast_to(np.array([float(s_ < t) for s_ in range(4)] + [float(s_ > t) for s_ in range(4)], np.float32)[None], (128, 8))),
        }
        maps.append(m)
    return maps


from concourse.bass_utils import run_bass_kernel_spmd


def kernel(**inputs):
    inp = {k: np.asarray(v) for k, v in inputs.items()}
    nc, _ = build(depth=2)
    maps = make_inputs(inp)
    res = run_bass_kernel_spmd(nc, maps, core_ids=list(range(8)))
    outs = [np.asarray(res.results[c]["out"]) for c in range(8)]
    return np.stack([np.concatenate(outs[0:4], 0), np.concatenate(outs[4:8], 0)]).astype(np.float32)
```

```python
import numpy as np
import concourse.bass as bass
import concourse.mybir as mybir

F32 = mybir.dt.float32
BF16 = mybir.dt.bfloat16
AF = mybir.ActivationFunctionType
ALU = mybir.AluOpType
AX = mybir.AxisListType


class Res:
    __slots__ = ("name", "w", "r")

    def __init__(self, name):
        self.name = name
        self.w = None
        self.r = {}


class V:
    __slots__ = ("ap", "res")

    def __init__(self, ap, res):
        self.ap = ap
        self.res = res if isinstance(res, (list, tuple)) else [res]


class Buf:
    def __init__(self, fw, name, t, nparts=1):
        self.fw = fw
        self.name = name
        self.t = t
        self.parts = {}
        self.whole = Res(name)

    def __getitem__(self, idx):
        return V(self.t[idx], self.whole)

    def part(self, key):
        if key not in self.parts:
            self.parts[key] = Res(f"{self.name}.{key}")
        return _PartView(self, self.parts[key])

    def ap(self):
        return self.t.ap()


class _PartView:
    def __init__(self, buf, res):
        self.buf = buf
        self.res = res

    def __getitem__(self, idx):
        return V(self.buf.t[idx], self.res)


class EngState:
    def __init__(self, name, eng, sem):
        self.name = name
        self.eng = eng
        self.sem = sem
        self.count = 0
        self.pending = False
        self.seen = {}
        self.seen_dma = {}


class FW:
    def __init__(self, nc, n_dma_sems=24, same_engine_sync=True):
        self.nc = nc
        self.same_engine_sync = same_engine_sync
        self.engs = {}
        for name, eng in (("pe", nc.tensor), ("dve", nc.vector), ("act", nc.scalar),
                          ("pool", nc.gpsimd), ("sp", nc.sync)):
            self.engs[name] = EngState(name, eng, nc.alloc_semaphore(f"s_{name}"))
        self.dma_sems = [nc.alloc_semaphore(f"s_dma{i}") for i in range(n_dma_sems)]
        self.dma_vals = [0] * n_dma_sems
        self.dma_next = 0
        self.n_inst = 0
        self.out_tokens = []
        self.cc_sem = None
        self.cc_val = 0

    def sbuf(self, name, shape, dtype):
        return Buf(self, name, self.nc.alloc_sbuf_tensor("sb_" + name, list(shape), dtype))

    def psum(self, name, shape, dtype=F32):
        return Buf(self, name, self.nc.alloc_psum_tensor("ps_" + name, list(shape), dtype))

    def dram(self, name, shape, dtype, kind="Internal", **kw):
        return Buf(self, name, self.nc.dram_tensor(name, list(shape), dtype, kind=kind, **kw))

    def _need(self, E, tok):
        if tok is None:
            return
        if tok[0] == "eng":
            _, e, c = tok
            if e == E.name:
                if not self.same_engine_sync or e == "pe":
                    return
            if E.seen.get(e, 0) >= c:
                return
            P = self.engs[e]
            assert c <= P.count, f"{E.name} waits on pending (never-incremented) {e} count {c} > {P.count}"
            E.eng.wait_ge(P.sem, c)
            E.seen[e] = c
        elif tok[0] == "cc":
            val = tok[1]
            if E.seen_dma.get("cc", 0) >= val:
                return
            E.eng.wait_ge(self.cc_sem, val)
            E.seen_dma["cc"] = val
        else:
            _, si, val = tok
            if E.seen_dma.get(si, 0) >= val:
                return
            E.eng.wait_ge(self.dma_sems[si], val)
            E.seen_dma[si] = val

    def _pre(self, E, reads, writes):
        for v in reads:
            for r in v.res:
                self._need(E, r.w)
        for v in writes:
            for r in v.res:
                self._need(E, r.w)
                for tok in r.r.values():
                    self._need(E, tok)

    def _post(self, tok, key, reads, writes):
        for v in reads:
            for r in v.res:
                r.r[key] = tok
        for v in writes:
            for r in v.res:
                r.w = tok
                r.r = {}

    def op(self, engname, fn, reads, writes, inc=True):
        E = self.engs[engname]
        self._pre(E, reads, writes)
        ins = fn(E.eng)
        self.n_inst += 1
        if inc:
            E.count += 1
            ins.then_inc(E.sem, 1)
            tok = ("eng", engname, E.count)
        else:
            tok = ("eng", engname, E.count + 1)
        self._post(tok, engname, reads, writes)
        return ins

    def dma(self, qname, out, in_, **kw):
        E = self.engs[qname]
        self._pre(E, [in_], [out])
        si = self.dma_next
        self.dma_next = (self.dma_next + 1) % len(self.dma_sems)
        if self.dma_vals[si] > 0:
            self._need(E, ("dma", si, self.dma_vals[si]))
        self.dma_vals[si] += 16
        ins = E.eng.dma_start(out=out.ap, in_=in_.ap, **kw)
        ins.then_inc(self.dma_sems[si], 16)
        self.n_inst += 1
        tok = ("dma", si, self.dma_vals[si])
        self._post(tok, f"dma{si}", [in_], [out])
        return tok

    def wait_all(self, engname, views):
        E = self.engs[engname]
        for v in views:
            for r in v.res:
                self._need(E, r.w)

    def mm(self, out, lhsT, rhs, start, stop, last=None, **kw):
        if last is None:
            last = stop
        return self.op("pe", lambda e: e.matmul(out.ap, lhsT.ap, rhs.ap, start=start, stop=stop, **kw),
                       [lhsT, rhs], [out], inc=last)

    def transpose(self, out, in_, ident, last=True):
        return self.op("pe", lambda e: e.transpose(out.ap, in_.ap, ident.ap), [in_, ident], [out], inc=last)

    def act(self, out, in_, func, bias=None, scale=1.0, accum_out=None, eng="act"):
        reads = [in_]
        kw = {}
        if bias is not None:
            if isinstance(bias, V):
                reads.append(bias)
                kw["bias"] = bias.ap
            else:
                kw["bias"] = bias
        if isinstance(scale, V):
            reads.append(scale)
            kw["scale"] = scale.ap
        else:
            kw["scale"] = scale
        writes = [out]
        if accum_out is not None:
            writes.append(accum_out)
            kw["accum_out"] = accum_out.ap
        return self.op(eng, lambda e: e.activation(out.ap, in_.ap, func, **kw), reads, writes)

    def tt(self, out, in0, in1, op, eng="dve"):
        return self.op(eng, lambda e: e.tensor_tensor(out.ap, in0.ap, in1.ap, op), [in0, in1], [out])

    def ts(self, out, in0, s1, op0, s2=None, op1=None, eng="dve", accum_out=None):
        reads = [in0]
        a1 = s1
        if isinstance(s1, V):
            reads.append(s1)
            a1 = s1.ap
        a2 = s2
        if isinstance(s2, V):
            reads.append(s2)
            a2 = s2.ap
        kw = {}
        writes = [out]
        if op1 is not None:
            kw["op1"] = op1
        if accum_out is not None:
            kw["accum_out"] = accum_out.ap
            writes.append(accum_out)
        return self.op(eng, lambda e: e.tensor_scalar(out.ap, in0.ap, a1, a2, op0, **kw), reads, writes)

    def stt(self, out, in0, scalar, in1, op0, op1, eng="dve"):
        reads = [in0, in1]
        a = scalar
        if isinstance(scalar, V):
            reads.append(scalar)
            a = scalar.ap
        return self.op(eng, lambda e: e.scalar_tensor_tensor(out.ap, in0.ap, a, in1.ap, op0, op1), reads, [out])

    def copy(self, out, in_, eng="dve"):
        if eng == "act":
            return self.op("act", lambda e: e.copy(out.ap, in_.ap), [in_], [out])
        return self.op(eng, lambda e: e.tensor_copy(out.ap, in_.ap), [in_], [out])

    def memset(self, out, val, eng="dve"):
        return self.op(eng, lambda e: e.memset(out.ap, val), [], [out])

    def reduce(self, out, in_, op, axis=AX.X, eng="dve"):
        return self.op(eng, lambda e: e.tensor_reduce(out.ap, in_.ap, axis, op), [in_], [out])

    def recip(self, out, in_):
        return self.op("dve", lambda e: e.reciprocal(out.ap, in_.ap), [in_], [out])

    def collective(self, kind, op, groups, in_, out):
        E = self.engs["pool"]
        self._pre(E, [in_], [out])
        if self.cc_sem is None:
            self.cc_sem = self.nc.alloc_semaphore("s_cc")
        self.cc_val += 1
        ins = E.eng.collective_compute(kind, op, replica_groups=groups, ins=[in_.ap], outs=[out.ap])
        ins.then_inc(self.cc_sem)
        self.n_inst += 1
        tok = ("cc", self.cc_val)
        self._post(tok, "cc", [in_], [out])
        return tok

    def make_arena(self, kbytes):
        self.arena_t = self.nc.alloc_sbuf_tensor("sb_arena", [128, kbytes * 256], F32)
        self.arena_words = kbytes * 256
        self.arena_off = 0
        self.arena_gen = 0

    def carve(self, name, shape, dtype):
        esz = 2 if dtype == BF16 else 4
        n = 1
        for s in shape[1:]:
            n *= s
        words = (n * esz + 3) // 4
        words = (words + 7) // 8 * 8
        assert self.arena_off + words <= self.arena_words, f"arena overflow for {name}: {self.arena_off}+{words}>{self.arena_words}"
        raw = self.arena_t[0:shape[0], self.arena_off:self.arena_off + words]
        self.arena_off += words
        ap = raw.bitcast(dtype) if dtype != F32 else raw
        ap = ap[:, 0:n]
        if len(shape) > 2:
            names = " ".join(f"d{i}" for i in range(1, len(shape)))
            kw = {f"d{i}": shape[i] for i in range(1, len(shape))}
            ap = ap.rearrange(f"p ({names}) -> p {names}", **kw)
        return Buf(self, f"{name}@{self.arena_gen}", _APHandle(ap))

    def barrier(self):
        for E in self.engs.values():
            for P in self.engs.values():
                if P is not E and P.count > 0:
                    self._need(E, ("eng", P.name, P.count))
            for si, val in enumerate(self.dma_vals):
                if val > 0:
                    self._need(E, ("dma", si, val))
            if self.cc_val > 0:
                self._need(E, ("cc", self.cc_val))

    def phase_reset(self):
        self.barrier()
        self.arena_off = 0
        self.arena_gen += 1


class _APHandle:
    def __init__(self, ap):
        self._ap = ap

    def __getitem__(self, idx):
        return self._ap[idx]

    def ap(self):
        return self._ap


import math
KCUT = 9

D = 1024
NCH = 16
TOK = 2048
NTOK = TOK + 256
NLC = 18
DIN = 6176
EPS = 1e-6
GROUPS = [[0, 1, 2, 3], [4, 5, 6, 7]]
SW = 1568
NP = 3 * 1536 + 1536 + 32 + 32 + 16 + 128
RET_EXP_F = (5.0, 6.0, 7.0, 8.0)
RET_EXP_B = (5.5, 6.5, 7.5, 8.5)
BLOCKS = [("aq", 0, 512), ("ak", 512, 512), ("av", 1024, 512), ("ag", 1536, 512),
          ("xbc0", 2048, 512), ("xbc1", 2560, 512), ("xbc2", 3072, 512), ("dtr", 3584, 32),
          ("z0", 3616, 512), ("z1", 4128, 512), ("rqk", 4640, 512), ("rv", 5152, 512), ("rg", 5664, 512)]


class Alt:
    def __init__(self, bufs, par):
        self.bufs, self.par = bufs, par

    @property
    def cur(self):
        return self.bufs[self.par[0] % len(self.bufs)]

    def __getitem__(self, idx):
        return self.cur[idx]

    @property
    def t(self):
        return self.cur.t

    @property
    def whole(self):
        return self.cur.whole


def lam_init_of(layer):
    return 0.8 - 0.6 * math.exp(-0.3 * layer)


def build(depth=2, debug=None, stop_after=None):
    debug = debug or {}
    nc = bass.Bass("TRN2", target_bir_lowering=False)
    fw = FW(nc, same_engine_sync=True)
    I = lambda n, s, d=F32: fw.dram(n, s, d, kind="ExternalInput")
    x_d = I("x", [TOK, D])
    ctx_d = I("ctx", [256, D])
    cc_d = I("cc", [128, 8, 2])
    wada_d = I("w_ada", [2, D, 3 * D])
    bada_f_d = I("b_ada_f", [2, 128, 16])
    bada_d = I("b_ada", [2, 3 * D])
    win_d = I("w_in", [2, D, DIN])
    wout_d = I("w_out", [2, 2 * D, D])
    qkg_d = I("qkg", [2, 128, 2, 64])
    lamv_d = I("lamv", [2, 128, 4, 64])
    subln_d = I("subln", [2, 128, 1])
    rope_d = I("rope", [128, NCH, 2, 32])
    tri_d = I("tri", [128, 5, 128])
    rett_d = I("rett", [128, 4 * 128 + 24])
    ssdp_d = I("ssdp", [2, 128, NP])
    ssdn_d = I("ssdn", [2, 128, 8])
    cmask_d = I("cmask", [128, 8])
    out_d = fw.dram("out", [TOK, D], F32, kind="ExternalOutput")
    dbg = {k: fw.dram("dbg_" + k, shape, dt_, kind="ExternalOutput") for k, (shape, dt_) in debug.items()}

    proj_s = fw.dram("proj_s", [NTOK, DIN], F32)
    qT_s = fw.dram("qT_s", [4, 128, NTOK], BF16)
    agT_s = fw.dram("agT_s", [4, 128, NTOK], BF16)
    mixT_s = fw.dram("mixT_s", [16, 128, NTOK], BF16)
    kc_s = fw.dram("kc_s", [4, 128, 256], BF16)
    vc_s = fw.dram("vc_s", [256, 512], BF16)
    xbc_pad = fw.dram("xbc_pad", [TOK + 2, 1536], F32)
    xbc_cpad = fw.dram("xbc_cpad", [258, 1536], F32)
    hx_stage = fw.dram("hx_stage", [2, 1536], F32)
    hx_src = fw.dram("hx_src", [8, 1536], F32)
    hx_dst = fw.dram("hx_dst", [8, 1536], F32)
    hxL = fw.dram("hxL", [9, 1536], F32)
    hxR = fw.dram("hxR", [8, 1536], F32)
    cst_s = fw.dram("cst_s", [NLC, 2, 128, SW], F32)
    sb_s = fw.dram("sb_s", [NLC, 128, 1536], BF16)
    st_stage = [fw.dram(f"st_stage{d}", [128, SW], F32) for d in range(2)]
    st_src = [fw.dram(f"st_src{d}", [512, SW], F32) for d in range(2)]
    st_dst = [fw.dram(f"st_dst{d}", [512, SW], F32) for d in range(2)]
    kv_src = [fw.dram(f"kv_src{h}", [512, 4096], BF16) for h in range(4)]
    kv_dst = [fw.dram(f"kv_dst{h}", [512, 4096], BF16) for h in range(4)]
    kv_stage = [fw.dram(f"kv_stage{h}", [128, 4096], BF16) for h in range(4)]

    x_sb = fw.sbuf("x_sb", [128, NCH, D], F32)
    ctx_sb = fw.sbuf("ctx_sb", [128, 2, D], F32)
    gate = fw.sbuf("gate", [128, 2, D], F32)
    sc1 = fw.sbuf("sc1", [128, 2, 8], F32)
    sh = fw.sbuf("sh", [128, 2, 8], F32)
    ident = fw.sbuf("ident", [128, 128], F32)
    ident_bf = fw.sbuf("ident_bf", [128, 128], BF16)
    ones_bf = fw.sbuf("ones_bf", [128, 128], BF16)
    eps_t = fw.sbuf("eps_t", [128, 1], F32)
    cc = fw.sbuf("cc", [128, 8, 2], F32)
    rope = fw.sbuf("rope", [128, NCH, 2, 32], F32)
    qkg = fw.sbuf("qkg", [128, 2, 64], F32)
    lamv = fw.sbuf("lamv", [128, 4, 64], F32)
    neglam = fw.sbuf("neglam", [128, 1], F32)
    subln = fw.sbuf("subln", [128, 1], F32)
    small = fw.sbuf("small", [128, 64], F32)
    cmask = fw.sbuf("cmask", [128, 8], F32)
    fw.make_arena(119)
    pb_t = nc.alloc_psum_tensor("ps_banks", [128, 8, 512], F32)
    pb = [Buf(fw, f"pb{i}", _APHandle(pb_t[:, i, :])) for i in range(8)]
    pbh = [Buf(fw, f"pbh{i}", _APHandle(pb_t[:, i, :].bitcast(BF16))) for i in range(8)]
    for i in range(8):
        pbh[i].whole = pb[i].whole
    rank = nc.partition_id() % 4

    fw.memset(ident[:], 1.0, eng="pool")
    fw.op("pool", lambda e: e.affine_select(ident.t[:], ident.t[:], [[-1, 128]], ALU.is_equal, 0.0,
                                             base=0, channel_multiplier=1), [ident[:]], [ident[:]])
    fw.copy(ident_bf[:], ident[:])
    fw.memset(ones_bf[:], 1.0)
    fw.memset(eps_t[:], EPS)
    fw.dma("sp", V(x_sb.t[:], x_sb.whole), V(x_d.t.ap().rearrange("(c p) d -> p c d", p=128), x_d.whole))
    fw.dma("sp", V(ctx_sb.t[:], ctx_sb.whole), V(ctx_d.t.ap().rearrange("(c p) d -> p c d", p=128), ctx_d.whole))
    fw.dma("sp", cc[:], cc_d[:])
    fw.dma("sp", rope[:], rope_d[:])
    fw.dma("sp", cmask[:], cmask_d[:])
    fw.act(cc[:], cc[:], AF.Silu)
    zt = fw.carve("zt", [128, 4096], BF16)
    fw.memset(zt[:], 0.0)
    for h in range(4):
        fw.dma("sp", V(kv_src[h].t.ap().rearrange("(r p) c -> p r c", p=128), kv_src[h].whole),
               V(zt.t[:].unsqueeze(1).to_broadcast([128, 4, 4096]), zt.whole))
    zf = fw.carve("zf", [128, SW], F32)
    fw.memset(zf[:], 0.0)
    fw.dma("sp", xbc_cpad[0:1, :], zf[0:1, 0:1536])
    fw.dma("sp", xbc_cpad[257:258, :], zf[0:1, 0:1536])
    fw.dma("sp", hx_src[:, :], zf[0:8, 0:1536])
    fw.dma("sp", hxL[:, :], zf[0:9, 0:1536])
    fw.dma("sp", hxR[:, :], zf[0:8, 0:1536])
    for d in range(2):
        fw.dma("sp", V(st_src[d].t.ap().rearrange("(r p) c -> p r c", p=128), st_src[d].whole),
               V(zf.t[:].unsqueeze(1).to_broadcast([128, 4, SW]), zf.whole))
        fw.dma("sp", st_stage[d][:, :], zf[:, :])
    fw.phase_reset()

    def adaln(l):
        ccrep = fw.carve("ccrep", [128, 8, 2, 128], F32)
        badaf = fw.carve("badaf", [128, 16], F32)
        gbias = fw.carve("gbias", [128, D], F32)
        wada_sb = fw.carve("wada_sb", [128, 8, 512], F32)
        fw.copy(ccrep[:], V(cc.t[:].unsqueeze(3).to_broadcast([128, 8, 2, 128]), cc.whole))
        fw.dma("sp", badaf[:], bada_f_d[l])
        fw.dma("sp", gbias[:], V(bada_d.t[l:l + 1, 2 * D:3 * D].partition_broadcast(128), bada_d.whole))
        ps_s = V(pb[2].t[:, 0:32].rearrange("p (a b) -> p a b", b=2), pb[2].whole)
        for piece in range(6):
            fw.dma("sp", V(wada_sb.t[:], wada_sb.whole),
                   V(wada_d.t[l, :, piece * 512:(piece + 1) * 512].rearrange("(k p) c -> p k c", p=128), wada_d.whole))
            if piece < 4:
                for j in range(4):
                    blk = piece * 4 + j
                    for k in range(8):
                        fw.mm(V(ps_s.ap[:, blk, :], ps_s.res), wada_sb[:, k, j * 128:(j + 1) * 128], cc[:, k, :],
                              start=(k == 0), stop=(k == 7))
            else:
                half = piece - 4
                for v in range(2):
                    for k in range(8):
                        fw.mm(pb[3][:, :], ccrep[:, k, v, :], wada_sb[:, k, :], start=(k == 0), stop=(k == 7))
                    fw.tt(gate[:, v, half * 512:(half + 1) * 512], pb[3][:, :], gbias[:, half * 512:(half + 1) * 512], ALU.add)
        for v in range(2):
            fw.tt(sh[:, v, :], V(ps_s.ap[:, 0:8, v], ps_s.res), badaf[:, 0:8], ALU.add)
            fw.tt(sc1[:, v, :], V(ps_s.ap[:, 8:16, v], ps_s.res), badaf[:, 8:16], ALU.add)
        fw.ts(sc1[:], sc1[:], 1.0, ALU.add)
        fw.phase_reset()

    def layer_params(l):
        fw.dma("sp", qkg[:], qkg_d[l])
        fw.dma("sp", lamv[:], lamv_d[l])
        fw.dma("sp", subln[:], subln_d[l])
        fw.tt(small[:, 0:64], lamv[:, 0, :], lamv[:, 1, :], ALU.mult)
        s1 = fw.sbuf(f"lam_s1_{l}", [128, 1], F32)
        s2 = fw.sbuf(f"lam_s2_{l}", [128, 1], F32)
        fw.reduce(s1[:], small[:, 0:64], ALU.add)
        fw.tt(small[:, 0:64], lamv[:, 2, :], lamv[:, 3, :], ALU.mult)
        fw.reduce(s2[:], small[:, 0:64], ALU.add)
        fw.act(s1[:], s1[:], AF.Exp)
        fw.act(s2[:], s2[:], AF.Exp)
        fw.tt(neglam[:], s2[:], s1[:], ALU.subtract)
        fw.ts(neglam[:], neglam[:], -lam_init_of(l), ALU.add)
        fw.ts(subln[:], subln[:], 1.0 - lam_init_of(l), ALU.mult)

    def phase_proj(l, need_ctx_q):
        hT = fw.carve("hT", [128, 8, NTOK], BF16)
        par = [0]
        alt = lambda n, sh, dt_: Alt([fw.carve(f"{n}_{i}", sh, dt_) for i in range(2)], par)
        xn = alt("xn", [128, D], F32)
        junk = alt("junk", [128, D], F32)
        ss = alt("ss", [128, 1], F32)
        rs = alt("rs", [128, 1], F32)
        wblk = [fw.carve(f"wblk{i}", [128, 8, 512], BF16) for i in range(2)]
        wst = fw.carve("wst", [128, 8, 512], F32)
        stage = [fw.carve(f"stage{i}", [128, 512], F32) for i in range(3)]
        sq = alt("sq", [128, 8, 64], F32)
        qn = alt("qn", [128, 8, 64], F32)
        t1 = alt("t1", [128, 8, 32], F32)
        t2 = alt("t2", [128, 8, 32], F32)
        ss8 = alt("ss8", [128, 8], F32)
        qbf = [fw.carve(f"qbf{i}", [128, 8, 64], BF16) for i in range(2)]
        tb = [fw.carve(f"tb{i}", [128, 4, 128], BF16) for i in range(2)]
        vbf = [fw.carve(f"vbf{i}", [128, 512], BF16) for i in range(2)]

        for lc in range(NLC):
            par[0] = lc
            src = x_sb[:, lc, :] if lc < NCH else ctx_sb[:, lc - NCH, :]
            v = 0 if lc < NCH else 1
            fw.act(junk[:], src, AF.Square, accum_out=ss[:])
            fw.act(rs[:], ss[:], AF.Sqrt, bias=eps_t[:], scale=1.0 / D)
            fw.recip(rs[:], rs[:])
            fw.ts(xn[:], src, rs[:], ALU.mult)
            pt = V(pb[lc % 2 * 2].t[:, :], [pb[lc % 2 * 2].whole, pb[lc % 2 * 2 + 1].whole])
            ptt = pb_t[:, lc % 2 * 2:lc % 2 * 2 + 2, :].rearrange("p a (k c) -> p (a k) c", c=128)
            for k in range(8):
                fw.transpose(V(ptt[:, k, :], pt.res), xn[:, k * 128:(k + 1) * 128], ident[:], last=(k == 7))
            for k in range(8):
                fw.ts(hT[:, k, lc * 128:(lc + 1) * 128], V(ptt[:, k, :], pt.res), sc1[:, v, k:k + 1], ALU.mult,
                      sh[:, v, k:k + 1], ALU.add)
        if "hT" in dbg:
            fw.dma("sp", V(dbg["hT"].t[:], dbg["hT"].whole), V(hT.t[:], hT.whole))

        it = 0
        for bi, (bname, col0, ncols) in enumerate(BLOCKS):
            wb = wblk[bi % 2]
            fw.dma("sp", V(wst.t[:, :, 0:ncols], wst.whole),
                   V(win_d.t[l, :, col0:col0 + ncols].rearrange("(k p) c -> p k c", p=128), win_d.whole))
            fw.copy(V(wb.t[:, :, 0:ncols], wb.whole), V(wst.t[:, :, 0:ncols], wst.whole), eng="pool")
            for lc in range(NLC):
                is_ctx = lc >= NCH
                bank = pb[4 + it % 2]
                it += 1
                par[0] = it
                for k in range(8):
                    fw.mm(bank[:, 0:ncols], hT[:, k, lc * 128:(lc + 1) * 128], wb[:, k, 0:ncols],
                          start=(k == 0), stop=(k == 7))
                rows = slice(lc * 128, (lc + 1) * 128)
                if bname in ("aq", "ak"):
                    if bname == "aq" and is_ctx and not need_ctx_q:
                        continue
                    gi = 0 if bname == "aq" else 1
                    psv = V(bank.t[:, :].rearrange("p (a b) -> p a b", b=64), bank.whole)
                    fw.act(sq[:], psv, AF.Square)
                    fw.reduce(ss8[:], sq[:], ALU.add)
                    fw.act(ss8[:], ss8[:], AF.Sqrt, bias=eps_t[:], scale=1.0 / 64)
                    fw.recip(ss8[:], ss8[:])
                    fw.tt(qn[:], psv, V(ss8.t[:].unsqueeze(2).to_broadcast([128, 8, 64]), ss8.whole), ALU.mult)
                    fw.tt(qn[:], qn[:], V(qkg.t[:, gi:gi + 1, :].to_broadcast([128, 8, 64]), qkg.whole), ALU.mult, eng="pool")
                    qo = qbf[it % 2]
                    if not is_ctx:
                        cosb = V(rope.t[:, lc, 0:1, :].to_broadcast([128, 8, 32]), rope.whole)
                        sinb = V(rope.t[:, lc, 1:2, :].to_broadcast([128, 8, 32]), rope.whole)
                        fw.tt(t1[:], qn[:, :, 0:32], cosb, ALU.mult)
                        fw.tt(t2[:], qn[:, :, 32:64], sinb, ALU.mult, eng="pool")
                        fw.tt(qo[:, :, 0:32], t1[:], t2[:], ALU.subtract)
                        fw.tt(t1[:], qn[:, :, 0:32], sinb, ALU.mult)
                        fw.tt(t2[:], qn[:, :, 32:64], cosb, ALU.mult, eng="pool")
                        fw.tt(qo[:, :, 32:64], t1[:], t2[:], ALU.add)
                    else:
                        fw.copy(qo[:], qn[:])
                    tbank = pbh[6 + lc % 2]
                    tbv = tbank.t[:, 0:512].rearrange("p (h c) -> p h c", c=128)
                    qof = qo.t[:].rearrange("p a b -> p (a b)")
                    for hd in range(4):
                        fw.transpose(V(tbv[:, hd, :], tbank.whole), V(qof[:, hd * 128:(hd + 1) * 128], qo.whole), ident_bf[:], last=(hd == 3))
                    tbs = tb[lc % 2]
                    fw.copy(tbs[:], V(tbv, tbank.whole), eng="act")
                    if bname == "aq":
                        fw.dma("act", V(qT_s.t[:, :, rows].rearrange("h p c -> p h c"), qT_s.part(lc).res), tbs[:])
                    elif is_ctx:
                        c0 = (lc - NCH) * 128
                        fw.dma("act", V(kc_s.t[:, :, c0:c0 + 128].rearrange("h p c -> p h c"), kc_s.part(lc).res), tbs[:])
                    else:
                        for hd in range(4):
                            fw.dma("act", V(kv_stage[hd].t[:, lc * 128:(lc + 1) * 128], kv_stage[hd].part(("k", lc)).res),
                                   tbs[:, hd, :])
                elif bname == "av":
                    vb = vbf[lc % 2]
                    fw.copy(vb[:], bank[:, :], eng="act")
                    if is_ctx:
                        c0 = (lc - NCH) * 128
                        fw.dma("act", V(vc_s.t[c0:c0 + 128, :], vc_s.part(lc).res), vb[:])
                    else:
                        for hd in range(4):
                            fw.dma("act", V(kv_stage[hd].t[:, 2048 + lc * 128:2048 + (lc + 1) * 128],
                                            kv_stage[hd].part(("v", lc)).res), vb[:, hd * 128:(hd + 1) * 128])
                elif bname == "ag":
                    st = stage[lc % 3]
                    fw.act(st[:], bank[:, :], AF.Silu)
                    tbank = pb[6 + lc % 2]
                    for hd in range(4):
                        fw.transpose(tbank[:, hd * 128:(hd + 1) * 128], st[:, hd * 128:(hd + 1) * 128], ident[:], last=(hd == 3))
                    tbs = tb[lc % 2]
                    fw.copy(V(tbs.t[:].rearrange("p h c -> p (h c)"), tbs.whole), tbank[:, :])
                    fw.dma("act", V(agT_s.t[:, :, rows].rearrange("h p c -> p h c"), agT_s.part(lc).res), tbs[:])
                else:
                    st = stage[lc % 3]
                    if lc % 2 == 0:
                        fw.copy(st[:, 0:ncols], bank[:, 0:ncols])
                    else:
                        fw.copy(st[:, 0:ncols], bank[:, 0:ncols], eng="act")
                    if bname.startswith("xbc"):
                        xc = (int(bname[3]) * 512)
                        if is_ctx:
                            r1 = 1 + (lc - NCH) * 128
                            fw.dma("sp", V(xbc_cpad.t[r1:r1 + 128, xc:xc + 512], xbc_cpad.part((bname, lc)).res), st[:, 0:ncols])
                        else:
                            r1 = 1 + lc * 128
                            fw.dma("sp", V(xbc_pad.t[r1:r1 + 128, xc:xc + 512], xbc_pad.part((bname, lc)).res), st[:, 0:ncols])
                    else:
                        fw.dma("sp", V(proj_s.t[rows, col0:col0 + ncols], proj_s.part((bname, lc)).res), st[:, 0:ncols])
            if bname == "av":
                for hd in range(4):
                    allres = [kv_stage[hd].part(("k", c)).res for c in range(NCH)] + [kv_stage[hd].part(("v", c)).res for c in range(NCH)]
                    fw.dma("sp", V(kv_src[hd].t[bass.ds(rank * 128, 128), :], kv_src[hd].whole), V(kv_stage[hd].t[:, :], allres))
                    fw.collective("AllReduce", ALU.add, GROUPS, kv_src[hd][:, :], kv_dst[hd][:, :])
        fw.phase_reset()

    def phase_attn(l, with_ctx_q):
        kTb = [fw.carve(f"kT{i}", [128, 8448], BF16) for i in range(2)]
        vvb = [fw.carve(f"vv{i}", [128, 66, 128], BF16) for i in range(2)]
        qhb = [fw.carve(f"qh{i}", [128, NTOK], BF16) for i in range(2)]
        aghb = [fw.carve(f"agh{i}", [128, NTOK], BF16) for i in range(2)]
        rec = fw.carve("rec", [128, 512], F32)
        om = [fw.carve(f"om{i}", [128, 512], F32) for i in range(2)]
        A = fw.carve("A", [128, 512], F32)
        sqb = fw.carve("sqb", [128, 512], BF16)
        rstd = fw.carve("rstd", [128, 512], F32)
        mixo = [fw.carve(f"mixo{i}", [128, 512], BF16) for i in range(2)]
        ones_f = fw.carve("ones_f", [128, 128], F32)
        fw.memset(ones_f[:], 1.0)
        qblocks = [(q0, 512, 0, 66) for q0 in range(0, TOK, 512)]
        if with_ctx_q:
            qblocks.append((TOK, 256, 64, 66))
        sbank = (pb[0], pb[1], pb[7])
        nq_all = NTOK if with_ctx_q else TOK

        def load_head(hd):
            kT, vv, qh, agh = kTb[hd % 2], vvb[hd % 2], qhb[hd % 2], aghb[hd % 2]
            fw.dma("sp", V(kT.t[:, 0:8192].rearrange("p (r c) -> p r c", r=4), kT.whole),
                   V(kv_dst[hd].t[:, 0:2048].rearrange("(r p) c -> p r c", p=128), kv_dst[hd].whole))
            fw.dma("sp", kT[:, 8192:8448], V(kc_s.t[hd], [kc_s.part(16).res, kc_s.part(17).res]))
            fw.dma("sp", V(vv.t[:, 0:64, :].rearrange("p (r c) e -> p r c e", r=4), vv.whole),
                   V(kv_dst[hd].t[:, 2048:4096].rearrange("(r p) (c e) -> p r c e", p=128, e=128), kv_dst[hd].whole))
            fw.dma("sp", vv[:, 64:66, :], V(vc_s.t[:, hd * 128:(hd + 1) * 128].rearrange("(c p) e -> p c e", p=128),
                                           [vc_s.part(16).res, vc_s.part(17).res]))
            qres = [qT_s.part(c).res for c in range(NLC if with_ctx_q else NCH)]
            fw.dma("sp", qh[:, 0:nq_all], V(qT_s.t[hd, :, 0:nq_all], qres))
            fw.dma("sp", agh[:, 0:NTOK], V(agT_s.t[hd], [agT_s.part(c).res for c in range(NLC)]))

        load_head(0)
        pT2 = [fw.carve(f"pTT{i}", [128, 2, 512], BF16) for i in range(3)]
        acc2 = [fw.carve(f"accT{i}", [128, 2, 512], F32) for i in range(2)]
        stage_banks = ((0, 1), (4, 5))
        for hd in range(4):
            if hd + 1 < 4:
                load_head(hd + 1)
            kT, vv, qh, agh = kTb[hd % 2], vvb[hd % 2], qhb[hd % 2], aghb[hd % 2]
            for qi, (q0, nq_, kc0, kc1) in enumerate(qblocks):
                kcs = list(range(kc0, kc1))
                n = len(kcs)

                def qk(i):
                    kc = kcs[i]
                    b0, b1 = stage_banks[i % 2]
                    for m, bk in ((0, b0), (1, b1)):
                        fw.mm(pb[bk][:, 0:nq_], kT[m * 64:(m + 1) * 64, kc * 128:(kc + 1) * 128], qh[m * 64:(m + 1) * 64, q0:q0 + nq_], True, True)

                qk(0)
                for i, kc in enumerate(kcs):
                    if i + 1 < n:
                        qk(i + 1)
                    b0, b1 = stage_banks[i % 2]
                    p = pT2[i % 3]
                    sc2 = V(pb_t[:, b0:b0 + 2, 0:nq_], [pb[b0].whole, pb[b1].whole])
                    fw.act(p[:, :, 0:nq_], sc2, AF.Exp, scale=0.125)
                    first, lastk = (i == 0), (i == n - 1)
                    for m in range(2):
                        fw.mm(pb[2 + m][:, 0:nq_], vv[:, kc, :], p[:, m, 0:nq_], start=first, stop=lastk)
                    eng_ = "dve" if i % 2 == 0 else "pool"
                    acc_ = acc2[i % 2]
                    if i < 2:
                        fw.copy(acc_[:, :, 0:nq_], p[:, :, 0:nq_], eng=eng_)
                    else:
                        fw.tt(acc_[:, :, 0:nq_], acc_[:, :, 0:nq_], p[:, :, 0:nq_], ALU.add, eng=eng_)
                if n > 1:
                    fw.tt(acc2[0][:, :, 0:nq_], acc2[0][:, :, 0:nq_], acc2[1][:, :, 0:nq_], ALU.add)
                for m in range(2):
                    fw.mm(pb[7][:, 0:nq_], ones_f[:], acc2[0][:, m, 0:nq_], True, True)
                    fw.recip(rec[:, 0:nq_], pb[7][:, 0:nq_])
                    fw.tt(om[m][:, 0:nq_], pb[2 + m][:, 0:nq_], rec[:, 0:nq_], ALU.mult)
                fw.stt(A[:, 0:nq_], om[1][:, 0:nq_], neglam[:], om[0][:, 0:nq_], ALU.mult, ALU.add)
                fw.act(sqb[:, 0:nq_], A[:, 0:nq_], AF.Square)
                fw.mm(pb[6][:, 0:nq_], ones_bf[:], sqb[:, 0:nq_], True, True)
                fw.act(rstd[:, 0:nq_], pb[6][:, 0:nq_], AF.Sqrt, bias=eps_t[:], scale=1.0 / 128)
                fw.recip(rstd[:, 0:nq_], rstd[:, 0:nq_])
                fw.stt(A[:, 0:nq_], A[:, 0:nq_], subln[:], rstd[:, 0:nq_], ALU.mult, ALU.mult)
                mo = mixo[qi % 2]
                fw.tt(mo[:, 0:nq_], A[:, 0:nq_], agh[:, q0:q0 + nq_], ALU.mult, eng="pool")
                fw.dma("act", V(mixT_s.t[hd, :, q0:q0 + nq_], mixT_s.part((hd, qi)).res), mo[:, 0:nq_])
        fw.phase_reset()

    def phase_halo(l):
        xr = lambda c: [xbc_pad.part((f"xbc{i}", c)).res for i in range(3)]
        fw.dma("sp", hx_stage[0:1, :], V(xbc_pad.t[1:2, :], xr(0)))
        fw.dma("sp", hx_stage[1:2, :], V(xbc_pad.t[TOK:TOK + 1, :], xr(NCH - 1)))
        fw.dma("sp", V(hx_src.t[bass.ds(rank * 2, 2), :], hx_src.whole), hx_stage[:, :])
        fw.collective("AllReduce", ALU.add, GROUPS, hx_src[:, :], hx_dst[:, :])
        fw.dma("sp", hxL[1:9, :], hx_dst[:, :])
        fw.dma("sp", hxR[0:6, :], hx_dst[2:8, :])
        fw.dma("sp", V(xbc_pad.t[0:1, :], xbc_pad.part("hl").res), V(hxL.t[bass.ds(rank * 2, 1), :], hxL.whole))
        fw.dma("sp", V(xbc_pad.t[TOK + 1:TOK + 2, :], xbc_pad.part("hr").res), V(hxR.t[bass.ds(rank * 2, 1), :], hxR.whole))

    def rope_apply(out, x, lc, nh, t1, t2):
        cosb = V(rope.t[:, lc, 0:1, :].to_broadcast([128, nh, 32]), rope.whole)
        sinb = V(rope.t[:, lc, 1:2, :].to_broadcast([128, nh, 32]), rope.whole)
        fw.tt(t1[:, 0:nh, :], x[:, :, 0:32], cosb, ALU.mult)
        fw.tt(t2[:, 0:nh, :], x[:, :, 32:64], sinb, ALU.mult, eng="pool")
        fw.tt(out[:, :, 0:32], t1[:, 0:nh, :], t2[:, 0:nh, :], ALU.subtract)
        fw.tt(t1[:, 0:nh, :], x[:, :, 0:32], sinb, ALU.mult)
        fw.tt(t2[:, 0:nh, :], x[:, :, 32:64], cosb, ALU.mult, eng="pool")
        fw.tt(out[:, :, 32:64], t1[:, 0:nh, :], t2[:, 0:nh, :], ALU.add)

    def chain_combine(Sin, Sctx, d, col0, ncol, nparts, Aexp_of, tmp, slot):
        fw.copy(Sin[0:nparts, :], Sctx[0:nparts, :])
        order = range(4) if d == 0 else range(3, -1, -1)
        for sidx in order:
            fw.dma("sp", slot[0:nparts, 0:ncol], st_dst[d][sidx * 128:sidx * 128 + nparts, col0:col0 + ncol])
            Aexp_of(sidx, tmp)
            fw.tt(tmp[0:nparts, :], tmp[0:nparts, :], slot[0:nparts, 0:ncol], ALU.add)
            fw.tt(tmp[0:nparts, :], tmp[0:nparts, :], Sin[0:nparts, :], ALU.subtract)
            mcol = cmask[:, d * 4 + sidx:d * 4 + sidx + 1]
            fw.stt(Sin[0:nparts, :], tmp[0:nparts, :], V(mcol.ap[0:nparts], mcol.res), Sin[0:nparts, :], ALU.mult, ALU.add)

    def phase_ret(l, need_ctx):
        rett = fw.carve("rett", [128, 4 * 128 + 24], F32)
        rnorm = fw.carve("rnorm", [128, 128], F32)
        fw.dma("sp", rett[:], rett_d[:])
        fw.dma("sp", rnorm[:], ssdp_d[l, :, NP - 128:NP])
        Dret = V(rett.t[:, 0:512].rearrange("p (h i) -> p h i", i=128), rett.whole)
        tab = lambda k: V(rett.t[:, 512 + 4 * k:512 + 4 * k + 4], rett.whole)
        par = [0]
        alt = lambda n, sh, dt_: Alt([fw.carve(f"{n}_{i}", sh, dt_) for i in range(2)], par)
        qk = alt("qk", [128, 8, 64], F32)
        qkr = alt("qkr", [128, 8, 64], F32)
        t1 = alt("rt1", [128, 8, 32], F32)
        t2 = alt("rt2", [128, 8, 32], F32)
        rv = alt("rv", [128, 512], F32)
        rvbf = alt("rvbf", [128, 512], BF16)
        kte = [alt(f"kte{d}", [128, 4, 64], BF16) for d in range(2)]
        q3 = alt("q3", [128, 3, 4, 64], BF16)
        kbf = alt("kbf", [128, 4, 64], BF16)
        qT = alt("qT", [64, 3, 4, 128], BF16)
        kT = alt("kT", [64, 4, 128], BF16)
        Wt = alt("Wt", [128, 4, 128], BF16)
        R = [fw.carve(f"R{d}", [64, 512], F32) for d in range(2)]
        Rbf = [fw.carve(f"Rbf{d}", [64, 512], BF16) for d in range(2)]
        Rctx = [fw.carve(f"Rctx{d}", [64, 512], F32) for d in range(2)]
        Pb = fw.carve("Pb", [128, 4], F32)
        cs_sb = [alt(f"cs_sb{d}", [64, 512], F32) for d in range(2)]
        tmp = fw.carve("rtmp", [64, 512], F32)
        slot = fw.carve("rslot", [64, 512], F32)
        rg = alt("rg", [128, 512], F32)
        ysq = alt("ysq", [128, 4, 128], F32)
        yss = alt("yss", [128, 4], F32)
        yn = alt("yn", [128, 4, 128], F32)
        ybf = alt("ybf", [128, 512], BF16)
        ytb = alt("ytb", [128, 4, 128], BF16)
        bc4 = lambda v, n: V(v.ap.unsqueeze(2).to_broadcast([v.ap.shape[0], 4, n]), v.res)

        def prep(lc):
            par[0] = lc
            is_ctx = lc >= NCH
            rows = slice(lc * 128, (lc + 1) * 128)
            pr = lambda n: proj_s.part((n, lc)).res
            fw.dma("sp", V(qk.t[:].rearrange("p a b -> p (a b)"), qk.whole), V(proj_s.t[rows, 4640:5152], pr("rqk")))
            fw.dma("sp", rv[:], V(proj_s.t[rows, 5152:5664], pr("rv")))
            if is_ctx:
                src = qk
            else:
                rope_apply(qkr, qk, lc, 8, t1, t2)
                src = qkr
            fw.copy(rvbf[:], rv[:], eng="pool")
            for d in range(2):
                fw.tt(kte[d][:], src[:, 4:8, :], bc4(tab(d), 64), ALU.mult)
            return src

        def chunk_states(start_banks=(0, 1)):
            for d in range(2):
                bank = pb[start_banks[d]]
                for h in range(4):
                    fw.mm(bank[0:64, h * 128:(h + 1) * 128], kte[d][:, h, :], rvbf[:, h * 128:(h + 1) * 128],
                          start=(h == 0), stop=(h == 3), skip_group_check=True)
            return [pb[start_banks[0]], pb[start_banks[1]]]

        A128 = [tab(4), tab(5)]

        def fold(acc, csb, lc, store):
            fw.tt(V(acc[0].t[:].rearrange("p (h n) -> p h n", n=128), acc[0].whole),
                  V(acc[0].t[:].rearrange("p (h n) -> p h n", n=128), acc[0].whole),
                  V(A128[0].ap[0:64].unsqueeze(2).to_broadcast([64, 4, 128]), A128[0].res), ALU.mult)
            fw.tt(acc[0][:], acc[0][:], csb[0][0:64, :], ALU.add)
            fw.tt(V(tmp.t[:].rearrange("p (h n) -> p h n", n=128), tmp.whole),
                  V(csb[1].t[0:64, :].rearrange("p (h n) -> p h n", n=128), csb[1].whole),
                  V(Pb.t[0:64, :].unsqueeze(2).to_broadcast([64, 4, 128]), Pb.whole), ALU.mult)
            fw.tt(acc[1][:], acc[1][:], tmp[:], ALU.add)
            fw.tt(Pb[:], Pb[:], A128[1], ALU.mult)
            if store:
                for d in range(2):
                    fw.copy(cs_sb[d][:], csb[d][0:64, :])
                    fw.dma("act", V(cst_s.t[lc, d, 0:64, 1024:1536], cst_s.part(("r", lc, d)).res), cs_sb[d][:])

        for grp in ((16, 17), tuple(range(NCH))):
            for d in range(2):
                fw.memset(R[d][:], 0.0)
            fw.memset(Pb[:], 1.0)
            for lc in grp:
                prep(lc)
                if KCUT >= 2:
                    csb = chunk_states()
                if KCUT >= 3:
                    fold(R, csb, lc, KCUT >= 4)
            if grp[0] == 16:
                for d in range(2):
                    fw.copy(Rctx[d][:], R[d][:])
        if "ret_sf" in dbg:
            fw.dma("sp", dbg["ret_sf"][:, :], Rctx[0][:])
            fw.dma("sp", dbg["ret_sb"][:, :], Rctx[1][:])
        if stop_after == "ret_p1":
            fw.phase_reset(); return
        for d in range(2):
            fw.dma("sp", st_stage[d][0:64, 1024:1536], R[d][:])
            fw.dma("sp", V(st_src[d].t[bass.ds(rank * 128, 128), :], st_src[d].whole), st_stage[d][:, :])
            fw.collective("AllReduce", ALU.add, GROUPS, st_src[d][:, :], st_dst[d][:, :])
        if stop_after == "ret_x":
            fw.phase_reset(); return
        A2048 = [[(1.0 - 2.0 ** -e) ** 2048 for e in RET_EXP_F], [(1.0 - 2.0 ** -e) ** 2048 for e in RET_EXP_B]]
        Rin = [fw.carve(f"Rin{d}", [64, 512], F32) for d in range(2)]
        for d in range(2):
            def aexp(sidx, t_, d=d):
                for h in range(4):
                    fw.ts(t_[0:64, h * 128:(h + 1) * 128], Rin[d][0:64, h * 128:(h + 1) * 128], float(A2048[d][h]), ALU.mult)
            chain_combine(Rin[d], Rctx[d], d, 1024, 512, 64, aexp, tmp, slot)
        snap = fw.carve("rsnap", [64, 512], BF16)
        csl = fw.carve("rcsl", [64, 512], F32)
        fw.copy(R[1][:], Rin[1][:])
        for lc in range(NCH - 1, -1, -1):
            fw.copy(snap[:], R[1][:])
            fw.dma("act", V(sb_s.t[lc, 0:64, 1024:1536], sb_s.part(("r", lc)).res), snap[:])
            fw.dma("sp", csl[:], V(cst_s.t[lc, 1, 0:64, 1024:1536], cst_s.part(("r", lc, 1)).res))
            fw.tt(V(R[1].t[:].rearrange("p (h n) -> p h n", n=128), R[1].whole),
                  V(R[1].t[:].rearrange("p (h n) -> p h n", n=128), R[1].whole),
                  V(A128[1].ap[0:64].unsqueeze(2).to_broadcast([64, 4, 128]), A128[1].res), ALU.mult)
            fw.tt(R[1][:], R[1][:], csl[:], ALU.add)
        if need_ctx:
            fw.dma("sp", csl[:], V(cst_s.t[17, 1, 0:64, 1024:1536], cst_s.part(("r", 17, 1)).res))
            fw.copy(snap[:], csl[:])
            fw.dma("sp", V(sb_s.t[16, 0:64, 1024:1536], sb_s.part(("r", 16)).res), snap[:])
            snap0 = fw.carve("rsnap0", [64, 512], BF16)
            fw.memset(snap0[:], 0.0)
            fw.dma("sp", V(sb_s.t[17, 0:64, 1024:1536], sb_s.part(("r", 17)).res), snap0[:])
        if stop_after == "ret_p2":
            fw.phase_reset(); return
        groups3 = [tuple(range(NCH))] + ([(16, 17)] if need_ctx else [])
        for grp in groups3:
            if grp[0] == 16:
                fw.memset(R[0][:], 0.0)
            else:
                fw.copy(R[0][:], Rin[0][:])
            for lc in grp:
                is_ctx = lc >= NCH
                rows = slice(lc * 128, (lc + 1) * 128)
                src = prep(lc)
                fw.dma("sp", rg[:], V(proj_s.t[rows, 5664:6176], proj_s.part(("rg", lc)).res))
                fw.dma("sp", Rbf[1][:], V(sb_s.t[lc, 0:64, 1024:1536], sb_s.part(("r", lc)).res))
                fw.copy(Rbf[0][:], R[0][:])
                fw.copy(q3[:, 0, :, :], src[:, 0:4, :])
                fw.tt(q3[:, 1, :, :], src[:, 0:4, :], bc4(tab(2), 64), ALU.mult)
                fw.tt(q3[:, 2, :, :], src[:, 0:4, :], bc4(tab(3), 64), ALU.mult, eng="pool")
                fw.copy(kbf[:], src[:, 4:8, :])
                tqa = pbh[2]
                tqav = tqa.t[0:64, 0:1024].rearrange("p (k h c) -> p k h c", k=2, h=4)
                tqb = pbh[3]
                tqbv = tqb.t[0:64, 0:512].rearrange("p (h c) -> p h c", h=4)
                for k3 in range(2):
                    for h in range(4):
                        fw.transpose(V(tqav[:, k3, h, :], tqa.whole), q3[:, k3, h, :], ident_bf[:], last=(k3 == 1 and h == 3))
                for h in range(4):
                    fw.transpose(V(tqbv[:, h, :], tqb.whole), q3[:, 2, h, :], ident_bf[:], last=(h == 3))
                fw.copy(qT[:, 0:2, :, :], V(tqav, tqa.whole))
                fw.copy(qT[:, 2, :, :], V(tqbv, tqb.whole))
                tk = pbh[4]
                tkv = tk.t[0:64, 0:512].rearrange("p (h c) -> p h c", h=4)
                for h in range(4):
                    fw.transpose(V(tkv[:, h, :], tk.whole), kbf[:, h, :], ident_bf[:], last=(h == 3))
                fw.copy(kT[:], V(tkv, tk.whole))
                sc = pb[5]
                for h in range(4):
                    fw.mm(sc[:, h * 128:(h + 1) * 128], kT[:, h, :], qT[:, 0, h, :], start=(h == 0), stop=(h == 3), skip_group_check=True)
                fw.tt(Wt[:], V(sc.t[:, :].rearrange("p (h i) -> p h i", i=128), sc.whole), Dret, ALU.mult)
                csb = chunk_states((0, 1))
                yb = pb[6]
                for h in range(4):
                    o = yb[:, h * 128:(h + 1) * 128]
                    fw.mm(o, Wt[:, h, :], rvbf[:, h * 128:(h + 1) * 128], start=(h == 0), stop=False, last=False, skip_group_check=True)
                    fw.mm(o, qT[:, 1, h, :], Rbf[0][:, h * 128:(h + 1) * 128], start=False, stop=False, last=False, skip_group_check=True)
                    fw.mm(o, qT[:, 2, h, :], Rbf[1][:, h * 128:(h + 1) * 128], start=False, stop=(h == 3), last=(h == 3), skip_group_check=True)
                fw.tt(V(R[0].t[:].rearrange("p (h n) -> p h n", n=128), R[0].whole),
                      V(R[0].t[:].rearrange("p (h n) -> p h n", n=128), R[0].whole),
                      V(A128[0].ap[0:64].unsqueeze(2).to_broadcast([64, 4, 128]), A128[0].res), ALU.mult)
                fw.tt(R[0][:], R[0][:], csb[0][0:64, :], ALU.add)
                ybv = V(yb.t[:, :].rearrange("p (h n) -> p h n", n=128), yb.whole)
                fw.act(ysq[:], ybv, AF.Square)
                fw.reduce(yss[:], ysq[:], ALU.add)
                fw.act(yss[:], yss[:], AF.Sqrt, bias=eps_t[:], scale=1.0 / 128)
                fw.recip(yss[:], yss[:])
                fw.tt(yn[:], ybv, V(yss.t[:].unsqueeze(2).to_broadcast([128, 4, 128]), yss.whole), ALU.mult)
                fw.tt(yn[:], yn[:], V(rnorm.t[:].unsqueeze(1).to_broadcast([128, 4, 128]), rnorm.whole), ALU.mult, eng="pool")
                fw.act(rg[:], rg[:], AF.Silu)
                fw.tt(ybf[:], V(yn.t[:].rearrange("p h n -> p (h n)"), yn.whole), rg[:], ALU.mult)
                to = pbh[7]
                tov = to.t[:, 0:512].rearrange("p (h c) -> p h c", h=4)
                for h in range(4):
                    fw.transpose(V(tov[:, h, :], to.whole), ybf[:, h * 128:(h + 1) * 128], ident_bf[:], last=(h == 3))
                fw.copy(ytb[:], V(tov, to.whole))
                fw.dma("act", V(mixT_s.t[12:16, :, rows].rearrange("h p c -> p h c"), mixT_s.part(("ret", lc)).res), ytb[:])
        fw.phase_reset()

    def phase_ssd(l, need_ctx):
        OW, OB, ODT, OA, ODD = 0, 4608, 6144, 6176, 6208
        prm = fw.carve("prm", [128, 6224], F32)
        fw.dma("sp", prm[:], ssdp_d[l, :, 0:6224])
        tri = fw.carve("tri", [128, 5, 128], F32)
        fw.dma("sp", tri[:], tri_d[:])
        ssdn = fw.carve("ssdn", [128, 8], F32)
        fw.dma("sp", ssdn[:], ssdn_d[l])
        negA = fw.carve("negA", [128, 32], F32)
        fw.act(negA[:], prm[:, OA:OA + 32], AF.Exp)
        fw.ts(negA[:], negA[:], -1.0, ALU.mult)
        one_t = fw.carve("one_t", [128, 1], F32)
        fw.memset(one_t[:], 1.0)
        U = [fw.carve(f"U{i}", [128, 1536], F32) for i in range(3)]
        dtr = fw.carve("dtr", [128, 32], F32)
        la = fw.carve("la", [128, 32], F32)
        E = fw.carve("E", [128, 96], F32)
        praw = fw.carve("praw", [128, 96], F32)
        cumraw = Buf(fw, "cumraw", _APHandle(praw.t[:, 0:32]))
        cumraw.whole = praw.whole
        tots = fw.carve("tots", [128, 32], F32)
        v = [fw.carve(f"v{d}", [128, 1024], BF16) for d in range(2)]
        vte = [fw.carve(f"vte{d}", [128, 1024], BF16) for d in range(2)]
        BCbf = fw.carve("BCbf", [128, 512], BF16)
        BCT = fw.carve("BCT", [128, 4, 128], BF16)
        zt = fw.carve("zt", [128, 1024], F32)
        R1 = fw.carve("R1", [128, 16, 128], F32)
        seg = fw.carve("seg", [128, 16, 128], F32)
        Dm = fw.carve("Dm", [128, 16, 128], BF16)
        Sm = [fw.carve(f"Sm{d}", [128, 2, 128], F32) for d in range(2)]
        Wt = fw.carve("Wt", [128, 16, 128], BF16)
        S = [fw.carve(f"S{d}", [128, 1024], F32) for d in range(2)]
        Sx = [fw.carve(f"Sx{d}", [128, 1024], F32) for d in range(2)]
        Sbf = [fw.carve(f"Sbf{d}", [128, 1024], BF16) for d in range(2)]
        Pb = fw.carve("Pb", [128, 16], F32)
        yt = fw.carve("yt", [128, 1024], F32)
        y2 = fw.carve("y2", [128, 1024], F32)
        vtmp = Buf(fw, "vtmp", _APHandle(y2.t[:].rearrange("p (h n) -> p h n", n=64)))
        vtmp.whole = y2.whole
        gss = fw.carve("gss", [128, 2], F32)
        ybf = fw.carve("ybf", [128, 1024], BF16)
        ytb = fw.carve("ytb", [128, 8, 128], BF16)
        Aex = fw.carve("Aex", [128, 16], F32)
        h16 = lambda vv: V(vv.ap.unsqueeze(2).to_broadcast([128, 16, 64]), vv.res)
        as16 = lambda b_: V(b_.t[:].rearrange("p (h n) -> p h n", n=64), b_.whole)

        def prep(lc):
            is_ctx = lc >= NCH
            src_t, r0 = (xbc_cpad, (lc - NCH) * 128) if is_ctx else (xbc_pad, lc * 128)
            rr = [xbc_pad.part("hl").res, xbc_pad.part("hr").res]
            for k in range(3):
                fw.dma("sp", U[k][:], V(src_t.t[r0 + k:r0 + k + 128, :], rr))
            fw.dma("sp", dtr[:], V(proj_s.t[lc * 128:(lc + 1) * 128, 3584:3616], proj_s.part(("dtr", lc)).res))
            fw.tt(U[0][:], U[0][:], prm[:, OW:OW + 1536], ALU.mult, eng="pool")
            fw.tt(U[1][:], U[1][:], prm[:, OW + 1536:OW + 3072], ALU.mult)
            fw.tt(U[2][:], U[2][:], prm[:, OW + 3072:OW + 4608], ALU.mult, eng="pool")
            fw.tt(U[1][:], U[1][:], U[0][:], ALU.add)
            fw.tt(U[1][:], U[1][:], U[2][:], ALU.add)
            fw.tt(U[1][:], U[1][:], prm[:, OB:OB + 1536], ALU.add)
            fw.act(U[0][:], U[1][:], AF.Silu)
            fw.tt(dtr[:], dtr[:], prm[:, ODT:ODT + 32], ALU.add)
            fw.act(dtr[:], dtr[:], AF.Exp)
            fw.act(dtr[:], dtr[:], AF.Ln, bias=one_t[:])
            fw.tt(la[:], dtr[:], negA[:], ALU.mult)
            pe = pb[0]
            for i, (w, c0, c1) in enumerate(((0, 0, 16), (1, 16, 32), (2, 0, 16), (3, 16, 32), (4, 0, 32))):
                o0 = (0, 16, 32, 48, 64)[i]
                fw.mm(pe[:, o0:o0 + (c1 - c0)], tri[:, w, :], la[:, c0:c1], start=(i == 0), stop=(i == 4), skip_group_check=True)
            fw.copy(praw[:], pe[:, 0:96])
            fw.act(E[:], praw[:], AF.Exp)
            fw.tt(tots[:], tots[:], praw[:, 64:96], ALU.add)
            xs = V(U[0].t[:, 0:1024].rearrange("p (h n) -> p h n", n=64), U[0].whole)
            for d in range(2):
                fw.tt(vtmp[:], xs, h16(dtr[:, d * 16:(d + 1) * 16]), ALU.mult)
                fw.copy(as16(v[d]), vtmp[:], eng="pool")
                fw.tt(as16(vte[d]), vtmp[:], h16(E[:, 32 + d * 16:48 + d * 16]), ALU.mult)
            fw.copy(BCbf[:], U[0][:, 1024:1536], eng="pool")

        def chunk_state(d):
            banks = (pb[4], pb[5])
            for g in range(2):
                fw.mm(banks[g][:, :], BCbf[:, g * 128:(g + 1) * 128], vte[d][:, g * 512:(g + 1) * 512], True, True)
            return banks

        def mulA(dst, srcS, acol):
            fw.tt(as16(dst), as16(srcS), h16(acol), ALU.mult)

        for grp in ((16, 17), tuple(range(NCH))):
            for d in range(2):
                fw.memset(S[d][:], 0.0)
            fw.memset(Pb[:], 1.0)
            fw.memset(tots[:], 0.0)
            for lc in grp:
                prep(lc)
                for d in range(2):
                    banks = chunk_state(d)
                    csv = V(pb_t[:, 4:6, :], [pb[4].whole, pb[5].whole])
                    cs_sb = V(seg.t[:, d * 8:(d + 1) * 8, :].rearrange("p a (g c) -> p (a g) c", g=2)[:, 0:2, :] if False else seg.t[:, d * 8:(d + 1) * 8, :], seg.whole)
                    cs_flat = V(seg.t[:].rearrange("p h n -> p (h n)")[:, d * 1024:(d + 1) * 1024], seg.whole)
                    fw.copy(V(cs_flat.ap.rearrange("p (g c) -> p g c", g=2), seg.whole), csv)
                    fw.dma("sp", V(cst_s.t[lc, d, :, 0:1024], cst_s.part(("s", lc, d)).res), cs_flat)
                    if d == 0:
                        mulA(S[0], S[0], E[:, 64:80])
                        fw.tt(S[0][:], S[0][:], cs_flat, ALU.add)
                    else:
                        fw.tt(as16(y2), V(cs_flat.ap.rearrange("p (h n) -> p h n", n=64), seg.whole), h16(Pb[:, :]), ALU.mult)
                        fw.tt(S[1][:], S[1][:], y2[:], ALU.add)
                        fw.tt(Pb[:], Pb[:], E[:, 80:96], ALU.mult)
                fw.dma("sp", V(cst_s.t[lc, 0, :, 1536:1568], cst_s.part(("e", lc)).res), E[:, 64:96])
            if grp[0] == 16:
                for d in range(2):
                    fw.copy(Sx[d][:], S[d][:])
        if "ssd_sf" in dbg:
            fw.dma("sp", dbg["ssd_sf"][:, :], Sx[0][:])
            fw.dma("sp", dbg["ssd_sb"][:, :], Sx[1][:])
        for d in range(2):
            fw.dma("sp", st_stage[d][:, 0:1024], S[d][:])
            fw.dma("sp", st_stage[d][:, 1536:1552], tots[:, d * 16:(d + 1) * 16])
            fw.dma("sp", V(st_src[d].t[bass.ds(rank * 128, 128), :], st_src[d].whole), st_stage[d][:, :])
            fw.collective("AllReduce", ALU.add, GROUPS, st_src[d][:, :], st_dst[d][:, :])
        for d in range(2):
            order = range(4) if d == 0 else range(3, -1, -1)
            for sidx in order:
                fw.dma("sp", yt[:], st_dst[d][sidx * 128:(sidx + 1) * 128, 0:1024])
                fw.dma("sp", Aex[:], st_dst[d][sidx * 128:(sidx + 1) * 128, 1536:1552])
                fw.act(Aex[:], Aex[:], AF.Exp)
                mulA(y2, Sx[d], Aex[:, :])
                fw.tt(y2[:], y2[:], yt[:], ALU.add)
                fw.tt(y2[:], y2[:], Sx[d][:], ALU.subtract)
                fw.stt(Sx[d][:], y2[:], cmask[:, d * 4 + sidx:d * 4 + sidx + 1], Sx[d][:], ALU.mult, ALU.add)
        for lc in range(NCH - 1, -1, -1):
            fw.copy(Sbf[1][:], Sx[1][:])
            fw.dma("sp", V(sb_s.t[lc, :, 0:1024], sb_s.part(("s", lc)).res), Sbf[1][:])
            fw.dma("sp", yt[:], V(cst_s.t[lc, 1, :, 0:1024], cst_s.part(("s", lc, 1)).res))
            fw.dma("sp", Aex[:], V(cst_s.t[lc, 0, :, 1552:1568], cst_s.part(("e", lc)).res))
            mulA(Sx[1], Sx[1], Aex[:, :])
            fw.tt(Sx[1][:], Sx[1][:], yt[:], ALU.add)
        if need_ctx:
            fw.dma("sp", yt[:], V(cst_s.t[17, 1, :, 0:1024], cst_s.part(("s", 17, 1)).res))
            fw.copy(Sbf[1][:], yt[:])
            fw.dma("sp", V(sb_s.t[16, :, 0:1024], sb_s.part(("s", 16)).res), Sbf[1][:])
            fw.memset(Sbf[0][:], 0.0)
            fw.dma("sp", V(sb_s.t[17, :, 0:1024], sb_s.part(("s", 17)).res), Sbf[0][:])
        if stop_after == "ssd_p2":
            fw.phase_reset(); return
        groups3 = [tuple(range(NCH))] + ([(16, 17)] if need_ctx else [])
        for grp in groups3:
            if grp[0] == 16:
                fw.memset(Sx[0][:], 0.0)
            for lc in grp:
                rows = slice(lc * 128, (lc + 1) * 128)
                prep(lc)
                fw.dma("sp", zt[:, 0:512], V(proj_s.t[rows, 3616:4128], proj_s.part(("z0", lc)).res))
                fw.dma("sp", zt[:, 512:1024], V(proj_s.t[rows, 4128:4640], proj_s.part(("z1", lc)).res))
                fw.dma("sp", Sbf[1][:], V(sb_s.t[lc, :, 0:1024], sb_s.part(("s", lc)).res))
                fw.copy(Sbf[0][:], Sx[0][:], eng="pool")
                tb_ = pbh[1]
                tbv = tb_.t[:, 0:512].rearrange("p (a c) -> p a c", a=4)
                for a in range(4):
                    fw.transpose(V(tbv[:, a, :], tb_.whole), BCbf[:, a * 128:(a + 1) * 128], ident_bf[:], last=(a == 3))
                fw.copy(BCT[:], V(tbv, tb_.whole))
                sc = pb[1]
                scv = sc.t[:, 256:512].rearrange("p (g i) -> p g i", g=2)
                for g in range(2):
                    fw.mm(V(scv[:, g, :], sc.whole), BCT[:, g, :], BCT[:, 2 + g, :], start=False if False else (g == 0), stop=(g == 1), skip_group_check=True)
                for d in range(2):
                    fw.tt(Sm[d][:], V(scv, sc.whole), V(tri.t[:, d:d + 1, :].to_broadcast([128, 2, 128]), tri.whole), ALU.mult)
                yb = (pb[6], pb[7])
                for d in range(2):
                    fw.tt(R1[:], V(la.t[:, d * 16:(d + 1) * 16].unsqueeze(2).to_broadcast([128, 16, 128]), la.whole),
                          V(tri.t[:, d:d + 1, :].to_broadcast([128, 16, 128]), tri.whole), ALU.mult, eng="pool")
                    for q in range(4):
                        bank = pb[2 + q % 2]
                        fw.mm(bank[:, :], tri[:, 4, :], V(R1.t[:, 4 * q:4 * q + 4, :].rearrange("p h n -> p (h n)"), R1.whole), True, True)
                        for hh in range(4):
                            h = 4 * q + hh
                            fw.ts(seg[:, h, :], bank[:, hh * 128:(hh + 1) * 128], cumraw[:, d * 16 + h:d * 16 + h + 1], ALU.subtract, 0.0, ALU.min)
                    fw.act(Dm[:], seg[:], AF.Exp)
                    for g in range(2):
                        fw.tt(Wt[:, g * 8:(g + 1) * 8, :], Dm[:, g * 8:(g + 1) * 8, :],
                              V(Sm[d].t[:, g:g + 1, :].to_broadcast([128, 8, 128]), Sm[d].whole), ALU.mult)
                    for h in range(16):
                        fw.mm(yb[h // 8][:, (h % 8) * 64:(h % 8 + 1) * 64], Wt[:, h, :], v[d][:, h * 64:(h + 1) * 64],
                              start=(d == 0 and h % 8 == 0), stop=(d == 1 and h % 8 == 7), last=(d == 1 and h % 8 == 7), skip_group_check=True)
                for d in range(2):
                    for g in range(2):
                        fw.mm(pb[2 + g][:, :], BCT[:, 2 + g, :], Sbf[d][:, g * 512:(g + 1) * 512], True, True)
                    ysv = V(pb_t[:, 2:4, :].rearrange("p a (h n) -> p (a h) n", n=64), [pb[2].whole, pb[3].whole])
                    fw.tt(as16(yt if d == 0 else y2), ysv, h16(E[:, d * 16:(d + 1) * 16]), ALU.mult)
                fw.tt(yt[:], yt[:], y2[:], ALU.add)
                yv = V(pb_t[:, 6:8, :].rearrange("p a c -> p (a c)") if False else pb_t[:, 6:8, :], [pb[6].whole, pb[7].whole])
                fw.tt(V(yt.t[:].rearrange("p (a c) -> p a c", a=2), yt.whole), V(yt.t[:].rearrange("p (a c) -> p a c", a=2), yt.whole), yv, ALU.add)
                xs = V(U[0].t[:, 0:1024].rearrange("p (h n) -> p h n", n=64), U[0].whole)
                fw.tt(as16(y2), xs, h16(prm[:, ODD:ODD + 16]), ALU.mult, eng="pool")
                fw.tt(yt[:], yt[:], y2[:], ALU.add)
                fw.act(zt[:], zt[:], AF.Silu)
                fw.tt(yt[:], yt[:], zt[:], ALU.mult)
                banks = chunk_state(0)
                mulA(Sx[0], Sx[0], E[:, 64:80])
                fw.tt(V(Sx[0].t[:].rearrange("p (g c) -> p g c", g=2), Sx[0].whole), V(Sx[0].t[:].rearrange("p (g c) -> p g c", g=2), Sx[0].whole),
                      V(pb_t[:, 4:6, :], [pb[4].whole, pb[5].whole]), ALU.add)
                fw.act(y2[:], yt[:], AF.Square)
                fw.reduce(gss[:], V(y2.t[:].rearrange("p (g c) -> p g c", g=2), y2.whole), ALU.add)
                fw.act(gss[:], gss[:], AF.Sqrt, bias=eps_t[:], scale=1.0 / 512)
                fw.recip(gss[:], gss[:])
                fw.tt(V(ybf.t[:].rearrange("p (g c) -> p g c", g=2), ybf.whole), V(yt.t[:].rearrange("p (g c) -> p g c", g=2), yt.whole),
                      V(gss.t[:].unsqueeze(2).to_broadcast([128, 2, 512]), gss.whole), ALU.mult)
                to = pbh[1]
                tov = to.t[:, 0:1024].rearrange("p (a c) -> p a c", a=8)
                for a in range(8):
                    fw.transpose(V(tov[:, a, :], to.whole), ybf[:, a * 128:(a + 1) * 128], ident_bf[:], last=(a == 7))
                fw.tt(ytb[:], V(tov, to.whole), V(ssdn.t[:].unsqueeze(2).to_broadcast([128, 8, 128]), ssdn.whole), ALU.mult)
                fw.dma("sp", V(mixT_s.t[4:12, :, rows].rearrange("h p c -> p h c"), mixT_s.part(("ssd", lc)).res), ytb[:])
        fw.phase_reset()

    def phase_out(l, need_ctx):
        wout = fw.carve("wout", [128, 16, D], BF16)
        wst = fw.carve("wost", [128, 4, D], F32)
        mx = [fw.carve(f"mx{i}", [128, 16, 128], BF16) for i in range(2)]
        tmp = fw.carve("otmp", [128, 512], F32)
        for q in range(4):
            fw.dma("sp", V(wst.t[:], wst.whole),
                   V(wout_d.t[l, q * 512:(q + 1) * 512, :].rearrange("(f p) c -> p f c", p=128), wout_d.whole))
            fw.copy(wout[:, q * 4:(q + 1) * 4, :], wst[:], eng="pool")
        allmix = [r for r in mixT_s.parts.values()]
        for lc in (range(NLC) if need_ctx else range(NCH)):
            is_ctx = lc >= NCH
            m = mx[lc % 2]
            fw.dma("sp", m[:], V(mixT_s.t[:, :, lc * 128:(lc + 1) * 128].rearrange("f p c -> p f c"), allmix))
            for hh in range(2):
                bank = pb[(lc % 2) * 2 + hh]
                for fc in range(16):
                    fw.mm(bank[:, :], m[:, fc, :], wout[:, fc, hh * 512:(hh + 1) * 512], start=(fc == 0), stop=(fc == 15))
                cs = slice(hh * 512, (hh + 1) * 512)
                fw.tt(tmp[:], bank[:, :], gate[:, 1 if is_ctx else 0, cs], ALU.mult)
                dst = ctx_sb[:, lc - NCH, cs] if is_ctx else x_sb[:, lc, cs]
                fw.tt(dst, dst, tmp[:], ALU.add)
        fw.phase_reset()

    for l in range(depth):
        need_ctx = l < depth - 1
        adaln(l)
        layer_params(l)
        phase_proj(l, need_ctx)
        if stop_after == "t_proj": break
        phase_attn(l, need_ctx)
        if stop_after == "t_attn": break
        phase_halo(l)
        phase_ret(l, need_ctx)
        if stop_after == "t_ret": break
        phase_ssd(l, need_ctx)
        if stop_after == "t_ssd": break
        phase_out(l, need_ctx)
        if l == 0 and "x0" in dbg:
            fw.dma("sp", V(dbg["x0"].t.ap().rearrange("(c p) d -> p c d", p=128), dbg["x0"].whole), V(x_sb.t[:], x_sb.whole))
            fw.dma("sp", V(dbg["ctx0"].t.ap().rearrange("(c p) d -> p c d", p=128), dbg["ctx0"].whole), V(ctx_sb.t[:], ctx_sb.whole))

    if "qT" in dbg:
        fw.dma("sp", dbg["qT"][:], V(qT_s.t[:], [qT_s.part(c).res for c in range(NLC)]))
    if "kv0" in dbg:
        fw.dma("sp", dbg["kv0"][:], kv_dst[0][:, :])
    if "proj" in dbg:
        fw.dma("sp", dbg["proj"][:], V(proj_s.t[:], [r for r in proj_s.parts.values()]))
    if "mixT" in dbg:
        fw.dma("sp", dbg["mixT"][:], V(mixT_s.t[0:4], [r for r in mixT_s.parts.values()]))
    if "mixS" in dbg:
        fw.dma("sp", dbg["mixS"][:], V(mixT_s.t[4:12], [r for r in mixT_s.parts.values()]))
    if "mixR" in dbg:
        fw.dma("sp", dbg["mixR"][:], V(mixT_s.t[12:16], [r for r in mixT_s.parts.values()]))
    fw.dma("sp", V(out_d.t.ap().rearrange("(c p) d -> p c d", p=128), out_d.whole), V(x_sb.t[:], x_sb.whole))
    fw.wait_all("sp", [out_d[:]] + [V(b.t[:], b.whole) for b in dbg.values()])
    return nc, fw


def rope_tables():
    n_freq = 16
    inv_freq = (10000.0 ** (-np.arange(n_freq, dtype=np.float32) / n_freq)).astype(np.float32)
    pos = np.arange(8192)
    row = (pos // 64).astype(np.float32)
    col = (pos % 64).astype(np.float32)
    ang = np.concatenate([row[:, None] * inv_freq, col[:, None] * inv_freq], axis=-1).astype(np.float32)
    return np.cos(ang).astype(np.float32), np.sin(ang).astype(np.float32)


def const_tables():
    j = np.arange(128)[:, None]; i = np.arange(128)[None, :]
    tri = np.stack([(j <= i), (j >= i), (j > i), (j < i), np.ones((128, 128), bool)], axis=1).astype(np.float32)
    gf = np.array([1.0 - 2.0 ** -e for e in RET_EXP_F], np.float64)
    gb = np.array([1.0 - 2.0 ** -e for e in RET_EXP_B], np.float64)
    dif = (i - j).astype(np.float64)
    Dret = np.zeros((128, 4, 128), np.float64)
    for h in range(4):
        Dret[:, h, :] = np.where(dif > 0, gf[h] ** np.abs(dif), 0.0) + np.where(dif < 0, gb[h] ** np.abs(dif), 0.0) + np.where(dif == 0, 2.0, 0.0)
    Dret *= 0.125
    pos = np.arange(128, dtype=np.float64)[:, None]
    te_f = gf[None, :] ** (127 - pos) * 0.125
    te_b = gb[None, :] ** pos * 0.125
    qsc_f = gf[None, :] ** (pos + 1)
    qsc_b = gb[None, :] ** (128 - pos)
    a_f = np.broadcast_to(gf[None, :] ** 128, (128, 4)); a_b = np.broadcast_to(gb[None, :] ** 128, (128, 4))
    rett = np.concatenate([Dret.reshape(128, 512), te_f, te_b, qsc_f, qsc_b, a_f, a_b], axis=1).astype(np.float32)
    return np.ascontiguousarray(tri), np.ascontiguousarray(rett)


def make_inputs(inp):
    cos, sin = rope_tables()
    tri, rett = const_tables()
    ssdp = np.concatenate([inp["ssd_conv_w"].reshape(2, -1), inp["ssd_conv_b"], inp["ssd_dt_bias"].reshape(2, -1),
                           inp["ssd_a_log"].reshape(2, -1), inp["ssd_d"], inp["ret_norm"]], axis=1).astype(np.float32)
    ssdp = np.ascontiguousarray(np.broadcast_to(ssdp[:, None, :], (2, 128, ssdp.shape[1])))
    ssdn = np.ascontiguousarray(inp["ssd_norm"].reshape(2, 8, 128).transpose(0, 2, 1))
    rep = lambda a: np.ascontiguousarray(np.broadcast_to(a[:, None], (a.shape[0], 128) + a.shape[1:]))
    qkg = rep(np.stack([inp["attn_q_norm"], inp["attn_k_norm"]], axis=1))
    lamv = rep(np.stack([inp["lambda_q1"], inp["lambda_k1"], inp["lambda_q2"], inp["lambda_k2"]], axis=1))
    subln = np.ascontiguousarray(inp["attn_subln"][:, :, None])
    maps = []
    for core in range(8):
        b, t = core // 4, core % 4
        lo = t * TOK
        cc = np.stack([inp["c"][b].reshape(8, 128).T, inp["c_ctx"].reshape(8, 128).T], axis=-1)
        rp = np.stack([cos[lo:lo + TOK], sin[lo:lo + TOK]], axis=1)
        rp = rp.reshape(NCH, 128, 2, 32).transpose(1, 0, 2, 3)
        m = {
            "x": np.ascontiguousarray(inp["x"][b, lo:lo + TOK]),
            "ctx": np.ascontiguousarray(inp["ctx"][b]),
            "cc": np.ascontiguousarray(cc.astype(np.float32)),
            "w_ada": inp["w_ada"],
            "b_ada_f": np.ascontiguousarray(inp["b_ada"][:, :2 * D].reshape(2, 16, 128).transpose(0, 2, 1)),
            "b_ada": inp["b_ada"],
            "w_in": inp["w_in"], "w_out": inp["w_out"],
            "qkg": qkg, "lamv": lamv, "subln": subln,
            "rope": np.ascontiguousarray(rp),
            "tri": tri, "rett": rett, "ssdp": ssdp, "ssdn": ssdn,
            "cmask": np.ascontiguousarray(np.broadcast_to(np.array([float(s_ < t) for s_ in range(4)] + [float(s_ > t) for s_ in range(4)], np.float32)[None], (128, 8))),
        }
        maps.append(m)
    return maps


from concourse.bass_utils import run_bass_kernel_spmd


def kernel(**inputs):
    inp = {k: np.asarray(v) for k, v in inputs.items()}
    nc, _ = build(depth=2)
    maps = make_inputs(inp)
    res = run_bass_kernel_spmd(nc, maps, core_ids=list(range(8)))
    outs = [np.asarray(res.results[c]["out"]) for c in range(8)]
    return np.stack([np.concatenate(outs[0:4], 0), np.concatenate(outs[4:8], 0)]).astype(np.float32)
```

```python
import numpy as np
import concourse.bass as bass
import concourse.mybir as mybir

F32 = mybir.dt.float32
BF16 = mybir.dt.bfloat16
AF = mybir.ActivationFunctionType
ALU = mybir.AluOpType
AX = mybir.AxisListType


class Res:
    __slots__ = ("name", "w", "r")

    def __init__(self, name):
        self.name = name
        self.w = None
        self.r = {}


class V:
    __slots__ = ("ap", "res")

    def __init__(self, ap, res):
        self.ap = ap
        self.res = res if isinstance(res, (list, tuple)) else [res]


class Buf:
    def __init__(self, fw, name, t, nparts=1):
        self.fw = fw
        self.name = name
        self.t = t
        self.parts = {}
        self.whole = Res(name)

    def __getitem__(self, idx):
        return V(self.t[idx], self.whole)

    def part(self, key):
        if key not in self.parts:
            self.parts[key] = Res(f"{self.name}.{key}")
        return _PartView(self, self.parts[key])

    def ap(self):
        return self.t.ap()


class _PartView:
    def __init__(self, buf, res):
        self.buf = buf
        self.res = res

    def __getitem__(self, idx):
        return V(self.buf.t[idx], self.res)


class EngState:
    def __init__(self, name, eng, sem):
        self.name = name
        self.eng = eng
        self.sem = sem
        self.count = 0
        self.pending = False
        self.seen = {}
        self.seen_dma = {}


class FW:
    def __init__(self, nc, n_dma_sems=24, same_engine_sync=True):
        self.nc = nc
        self.same_engine_sync = same_engine_sync
        self.engs = {}
        for name, eng in (("pe", nc.tensor), ("dve", nc.vector), ("act", nc.scalar),
                          ("pool", nc.gpsimd), ("sp", nc.sync)):
            self.engs[name] = EngState(name, eng, nc.alloc_semaphore(f"s_{name}"))
        self.dma_sems = [nc.alloc_semaphore(f"s_dma{i}") for i in range(n_dma_sems)]
        self.dma_vals = [0] * n_dma_sems
        self.dma_next = 0
        self.n_inst = 0
        self.out_tokens = []
        self.cc_sem = None
        self.cc_val = 0

    def sbuf(self, name, shape, dtype):
        return Buf(self, name, self.nc.alloc_sbuf_tensor("sb_" + name, list(shape), dtype))

    def psum(self, name, shape, dtype=F32):
        return Buf(self, name, self.nc.alloc_psum_tensor("ps_" + name, list(shape), dtype))

    def dram(self, name, shape, dtype, kind="Internal", **kw):
        return Buf(self, name, self.nc.dram_tensor(name, list(shape), dtype, kind=kind, **kw))

    def _need(self, E, tok):
        if tok is None:
            return
        if tok[0] == "eng":
            _, e, c = tok
            if e == E.name:
                if not self.same_engine_sync or e == "pe":
                    return
            if E.seen.get(e, 0) >= c:
                return
            P = self.engs[e]
            assert c <= P.count, f"{E.name} waits on pending (never-incremented) {e} count {c} > {P.count}"
            E.eng.wait_ge(P.sem, c)
            E.seen[e] = c
        elif tok[0] == "cc":
            val = tok[1]
            if E.seen_dma.get("cc", 0) >= val:
                return
            E.eng.wait_ge(self.cc_sem, val)
            E.seen_dma["cc"] = val
        else:
            _, si, val = tok
            if E.seen_dma.get(si, 0) >= val:
                return
            E.eng.wait_ge(self.dma_sems[si], val)
            E.seen_dma[si] = val

    def _pre(self, E, reads, writes):
        for v in reads:
            for r in v.res:
                self._need(E, r.w)
        for v in writes:
            for r in v.res:
                self._need(E, r.w)
                for tok in r.r.values():
                    self._need(E, tok)

    def _post(self, tok, key, reads, writes):
        for v in reads:
            for r in v.res:
                r.r[key] = tok
        for v in writes:
            for r in v.res:
                r.w = tok
                r.r = {}

    def op(self, engname, fn, reads, writes, inc=True):
        E = self.engs[engname]
        self._pre(E, reads, writes)
        ins = fn(E.eng)
        self.n_inst += 1
        if inc:
            E.count += 1
            ins.then_inc(E.sem, 1)
            tok = ("eng", engname, E.count)
        else:
            tok = ("eng", engname, E.count + 1)
        self._post(tok, engname, reads, writes)
        return ins

    def dma(self, qname, out, in_, **kw):
        E = self.engs[qname]
        self._pre(E, [in_], [out])
        si = self.dma_next
        self.dma_next = (self.dma_next + 1) % len(self.dma_sems)
        if self.dma_vals[si] > 0:
            self._need(E, ("dma", si, self.dma_vals[si]))
        self.dma_vals[si] += 16
        ins = E.eng.dma_start(out=out.ap, in_=in_.ap, **kw)
        ins.then_inc(self.dma_sems[si], 16)
        self.n_inst += 1
        tok = ("dma", si, self.dma_vals[si])
        self._post(tok, f"dma{si}", [in_], [out])
        return tok

    def wait_all(self, engname, views):
        E = self.engs[engname]
        for v in views:
            for r in v.res:
                self._need(E, r.w)

    def mm(self, out, lhsT, rhs, start, stop, last=None, **kw):
        if last is None:
            last = stop
        return self.op("pe", lambda e: e.matmul(out.ap, lhsT.ap, rhs.ap, start=start, stop=stop, **kw),
                       [lhsT, rhs], [out], inc=last)

    def transpose(self, out, in_, ident, last=True):
        return self.op("pe", lambda e: e.transpose(out.ap, in_.ap, ident.ap), [in_, ident], [out], inc=last)

    def act(self, out, in_, func, bias=None, scale=1.0, accum_out=None, eng="act"):
        reads = [in_]
        kw = {}
        if bias is not None:
            if isinstance(bias, V):
                reads.append(bias)
                kw["bias"] = bias.ap
            else:
                kw["bias"] = bias
        if isinstance(scale, V):
            reads.append(scale)
            kw["scale"] = scale.ap
        else:
            kw["scale"] = scale
        writes = [out]
        if accum_out is not None:
            writes.append(accum_out)
            kw["accum_out"] = accum_out.ap
        return self.op(eng, lambda e: e.activation(out.ap, in_.ap, func, **kw), reads, writes)

    def tt(self, out, in0, in1, op, eng="dve"):
        return self.op(eng, lambda e: e.tensor_tensor(out.ap, in0.ap, in1.ap, op), [in0, in1], [out])

    def ts(self, out, in0, s1, op0, s2=None, op1=None, eng="dve", accum_out=None):
        reads = [in0]
        a1 = s1
        if isinstance(s1, V):
            reads.append(s1)
            a1 = s1.ap
        a2 = s2
        if isinstance(s2, V):
            reads.append(s2)
            a2 = s2.ap
        kw = {}
        writes = [out]
        if op1 is not None:
            kw["op1"] = op1
        if accum_out is not None:
            kw["accum_out"] = accum_out.ap
            writes.append(accum_out)
        return self.op(eng, lambda e: e.tensor_scalar(out.ap, in0.ap, a1, a2, op0, **kw), reads, writes)

    def stt(self, out, in0, scalar, in1, op0, op1, eng="dve"):
        reads = [in0, in1]
        a = scalar
        if isinstance(scalar, V):
            reads.append(scalar)
            a = scalar.ap
        return self.op(eng, lambda e: e.scalar_tensor_tensor(out.ap, in0.ap, a, in1.ap, op0, op1), reads, [out])

    def copy(self, out, in_, eng="dve"):
        if eng == "act":
            return self.op("act", lambda e: e.copy(out.ap, in_.ap), [in_], [out])
        return self.op(eng, lambda e: e.tensor_copy(out.ap, in_.ap), [in_], [out])

    def memset(self, out, val, eng="dve"):
        return self.op(eng, lambda e: e.memset(out.ap, val), [], [out])

    def reduce(self, out, in_, op, axis=AX.X, eng="dve"):
        return self.op(eng, lambda e: e.tensor_reduce(out.ap, in_.ap, axis, op), [in_], [out])

    def recip(self, out, in_):
        return self.op("dve", lambda e: e.reciprocal(out.ap, in_.ap), [in_], [out])

    def collective(self, kind, op, groups, in_, out):
        E = self.engs["pool"]
        self._pre(E, [in_], [out])
        if self.cc_sem is None:
            self.cc_sem = self.nc.alloc_semaphore("s_cc")
        self.cc_val += 1
        ins = E.eng.collective_compute(kind, op, replica_groups=groups, ins=[in_.ap], outs=[out.ap])
        ins.then_inc(self.cc_sem)
        self.n_inst += 1
        tok = ("cc", self.cc_val)
        self._post(tok, "cc", [in_], [out])
        return tok

    def make_arena(self, kbytes):
        self.arena_t = self.nc.alloc_sbuf_tensor("sb_arena", [128, kbytes * 256], F32)
        self.arena_words = kbytes * 256
        self.arena_off = 0
        self.arena_gen = 0

    def carve(self, name, shape, dtype):
        esz = 2 if dtype == BF16 else 4
        n = 1
        for s in shape[1:]:
            n *= s
        words = (n * esz + 3) // 4
        words = (words + 7) // 8 * 8
        assert self.arena_off + words <= self.arena_words, f"arena overflow for {name}: {self.arena_off}+{words}>{self.arena_words}"
        raw = self.arena_t[0:shape[0], self.arena_off:self.arena_off + words]
        self.arena_off += words
        ap = raw.bitcast(dtype) if dtype != F32 else raw
        ap = ap[:, 0:n]
        if len(shape) > 2:
            names = " ".join(f"d{i}" for i in range(1, len(shape)))
            kw = {f"d{i}": shape[i] for i in range(1, len(shape))}
            ap = ap.rearrange(f"p ({names}) -> p {names}", **kw)
        return Buf(self, f"{name}@{self.arena_gen}", _APHandle(ap))

    def barrier(self):
        for E in self.engs.values():
            for P in self.engs.values():
                if P is not E and P.count > 0:
                    self._need(E, ("eng", P.name, P.count))
            for si, val in enumerate(self.dma_vals):
                if val > 0:
                    self._need(E, ("dma", si, val))
            if self.cc_val > 0:
                self._need(E, ("cc", self.cc_val))

    def phase_reset(self):
        self.barrier()
        self.arena_off = 0
        self.arena_gen += 1


class _APHandle:
    def __init__(self, ap):
        self._ap = ap

    def __getitem__(self, idx):
        return self._ap[idx]

    def ap(self):
        return self._ap


import math
KCUT = 9

D = 1024
NCH = 16
TOK = 2048
NTOK = TOK + 256
NLC = 18
DIN = 6176
EPS = 1e-6
GROUPS = [[0, 1, 2, 3], [4, 5, 6, 7]]
SW = 1568
NP = 3 * 1536 + 1536 + 32 + 32 + 16 + 128
RET_EXP_F = (5.0, 6.0, 7.0, 8.0)
RET_EXP_B = (5.5, 6.5, 7.5, 8.5)
BLOCKS = [("aq", 0, 512), ("ak", 512, 512), ("av", 1024, 512), ("ag", 1536, 512),
          ("xbc0", 2048, 512), ("xbc1", 2560, 512), ("xbc2", 3072, 512), ("dtr", 3584, 32),
          ("z0", 3616, 512), ("z1", 4128, 512), ("rqk", 4640, 512), ("rv", 5152, 512), ("rg", 5664, 512)]


class Alt:
    def __init__(self, bufs, par):
        self.bufs, self.par = bufs, par

    @property
    def cur(self):
        return self.bufs[self.par[0] % len(self.bufs)]

    def __getitem__(self, idx):
        return self.cur[idx]

    @property
    def t(self):
        return self.cur.t

    @property
    def whole(self):
        return self.cur.whole


def lam_init_of(layer):
    return 0.8 - 0.6 * math.exp(-0.3 * layer)


def build(depth=2, debug=None, stop_after=None):
    debug = debug or {}
    nc = bass.Bass("TRN2", target_bir_lowering=False)
    fw = FW(nc, same_engine_sync=True)
    I = lambda n, s, d=F32: fw.dram(n, s, d, kind="ExternalInput")
    x_d = I("x", [TOK, D])
    ctx_d = I("ctx", [256, D])
    cc_d = I("cc", [128, 8, 2])
    wada_d = I("w_ada", [2, D, 3 * D])
    bada_f_d = I("b_ada_f", [2, 128, 16])
    bada_d = I("b_ada", [2, 3 * D])
    win_d = I("w_in", [2, D, DIN])
    wout_d = I("w_out", [2, 2 * D, D])
    qkg_d = I("qkg", [2, 128, 2, 64])
    lamv_d = I("lamv", [2, 128, 4, 64])
    subln_d = I("subln", [2, 128, 1])
    rope_d = I("rope", [128, NCH, 2, 32])
    tri_d = I("tri", [128, 5, 128])
    rett_d = I("rett", [128, 4 * 128 + 24])
    ssdp_d = I("ssdp", [2, 128, NP])
    ssdn_d = I("ssdn", [2, 128, 8])
    cmask_d = I("cmask", [128, 8])
    out_d = fw.dram("out", [TOK, D], F32, kind="ExternalOutput")
    dbg = {k: fw.dram("dbg_" + k, shape, dt_, kind="ExternalOutput") for k, (shape, dt_) in debug.items()}

    proj_s = fw.dram("proj_s", [NTOK, DIN], F32)
    qT_s = fw.dram("qT_s", [4, 128, NTOK], BF16)
    agT_s = fw.dram("agT_s", [4, 128, NTOK], BF16)
    mixT_s = fw.dram("mixT_s", [16, 128, NTOK], BF16)
    kc_s = fw.dram("kc_s", [4, 128, 256], BF16)
    vc_s = fw.dram("vc_s", [256, 512], BF16)
    xbc_pad = fw.dram("xbc_pad", [TOK + 2, 1536], F32)
    xbc_cpad = fw.dram("xbc_cpad", [258, 1536], F32)
    hx_stage = fw.dram("hx_stage", [2, 1536], F32)
    hx_src = fw.dram("hx_src", [8, 1536], F32)
    hx_dst = fw.dram("hx_dst", [8, 1536], F32)
    hxL = fw.dram("hxL", [9, 1536], F32)
    hxR = fw.dram("hxR", [8, 1536], F32)
    cst_s = fw.dram("cst_s", [NLC, 2, 128, SW], F32)
    sb_s = fw.dram("sb_s", [NLC, 128, 1536], BF16)
    pc_h = fw.dram("pc_h", [NLC, 128, 3584], BF16)
    pc_f = fw.dram("pc_f", [NLC, 128, 1152], F32)
    st_stage = [fw.dram(f"st_stage{d}", [128, SW], F32) for d in range(2)]
    st_src = [fw.dram(f"st_src{d}", [512, SW], F32) for d in range(2)]
    st_dst = [fw.dram(f"st_dst{d}", [512, SW], F32) for d in range(2)]
    kv_src = [fw.dram(f"kv_src{h}", [512, 4096], BF16) for h in range(4)]
    kv_dst = [fw.dram(f"kv_dst{h}", [512, 4096], BF16) for h in range(4)]
    kv_stage = [fw.dram(f"kv_stage{h}", [128, 4096], BF16) for h in range(4)]

    x_sb = fw.sbuf("x_sb", [128, NCH, D], F32)
    ctx_sb = fw.sbuf("ctx_sb", [128, 2, D], F32)
    gate = fw.sbuf("gate", [128, 2, D], F32)
    sc1 = fw.sbuf("sc1", [128, 2, 8], F32)
    sh = fw.sbuf("sh", [128, 2, 8], F32)
    ident = fw.sbuf("ident", [128, 128], F32)
    ident_bf = fw.sbuf("ident_bf", [128, 128], BF16)
    ones_bf = fw.sbuf("ones_bf", [128, 128], BF16)
    eps_t = fw.sbuf("eps_t", [128, 1], F32)
    cc = fw.sbuf("cc", [128, 8, 2], F32)
    rope = fw.sbuf("rope", [128, NCH, 2, 32], F32)
    qkg = fw.sbuf("qkg", [128, 2, 64], F32)
    lamv = fw.sbuf("lamv", [128, 4, 64], F32)
    neglam = fw.sbuf("neglam", [128, 1], F32)
    subln = fw.sbuf("subln", [128, 1], F32)
    small = fw.sbuf("small", [128, 64], F32)
    cmask = fw.sbuf("cmask", [128, 8], F32)
    fw.make_arena(119)
    pb_t = nc.alloc_psum_tensor("ps_banks", [128, 8, 512], F32)
    pb = [Buf(fw, f"pb{i}", _APHandle(pb_t[:, i, :])) for i in range(8)]
    pbh = [Buf(fw, f"pbh{i}", _APHandle(pb_t[:, i, :].bitcast(BF16))) for i in range(8)]
    for i in range(8):
        pbh[i].whole = pb[i].whole
    rank = nc.partition_id() % 4

    fw.memset(ident[:], 1.0, eng="pool")
    fw.op("pool", lambda e: e.affine_select(ident.t[:], ident.t[:], [[-1, 128]], ALU.is_equal, 0.0,
                                             base=0, channel_multiplier=1), [ident[:]], [ident[:]])
    fw.copy(ident_bf[:], ident[:])
    fw.memset(ones_bf[:], 1.0)
    fw.memset(eps_t[:], EPS)
    fw.dma("sp", V(x_sb.t[:], x_sb.whole), V(x_d.t.ap().rearrange("(c p) d -> p c d", p=128), x_d.whole))
    fw.dma("sp", V(ctx_sb.t[:], ctx_sb.whole), V(ctx_d.t.ap().rearrange("(c p) d -> p c d", p=128), ctx_d.whole))
    fw.dma("sp", cc[:], cc_d[:])
    fw.dma("sp", rope[:], rope_d[:])
    fw.dma("sp", cmask[:], cmask_d[:])
    fw.act(cc[:], cc[:], AF.Silu)
    zt = fw.carve("zt", [128, 4096], BF16)
    fw.memset(zt[:], 0.0)
    for h in range(4):
        fw.dma("sp", V(kv_src[h].t.ap().rearrange("(r p) c -> p r c", p=128), kv_src[h].whole),
               V(zt.t[:].unsqueeze(1).to_broadcast([128, 4, 4096]), zt.whole))
    zf = fw.carve("zf", [128, SW], F32)
    fw.memset(zf[:], 0.0)
    fw.dma("sp", xbc_cpad[0:1, :], zf[0:1, 0:1536])
    fw.dma("sp", xbc_cpad[257:258, :], zf[0:1, 0:1536])
    fw.dma("sp", hx_src[:, :], zf[0:8, 0:1536])
    fw.dma("sp", hxL[:, :], zf[0:9, 0:1536])
    fw.dma("sp", hxR[:, :], zf[0:8, 0:1536])
    for d in range(2):
        fw.dma("sp", V(st_src[d].t.ap().rearrange("(r p) c -> p r c", p=128), st_src[d].whole),
               V(zf.t[:].unsqueeze(1).to_broadcast([128, 4, SW]), zf.whole))
        fw.dma("sp", st_stage[d][:, :], zf[:, :])
    fw.phase_reset()

    def adaln(l):
        ccrep = fw.carve("ccrep", [128, 8, 2, 128], F32)
        badaf = fw.carve("badaf", [128, 16], F32)
        gbias = fw.carve("gbias", [128, D], F32)
        wada_sb = fw.carve("wada_sb", [128, 8, 512], F32)
        fw.copy(ccrep[:], V(cc.t[:].unsqueeze(3).to_broadcast([128, 8, 2, 128]), cc.whole))
        fw.dma("sp", badaf[:], bada_f_d[l])
        fw.dma("sp", gbias[:], V(bada_d.t[l:l + 1, 2 * D:3 * D].partition_broadcast(128), bada_d.whole))
        ps_s = V(pb[2].t[:, 0:32].rearrange("p (a b) -> p a b", b=2), pb[2].whole)
        for piece in range(6):
            fw.dma("sp", V(wada_sb.t[:], wada_sb.whole),
                   V(wada_d.t[l, :, piece * 512:(piece + 1) * 512].rearrange("(k p) c -> p k c", p=128), wada_d.whole))
            if piece < 4:
                for j in range(4):
                    blk = piece * 4 + j
                    for k in range(8):
                        fw.mm(V(ps_s.ap[:, blk, :], ps_s.res), wada_sb[:, k, j * 128:(j + 1) * 128], cc[:, k, :],
                              start=(k == 0), stop=(k == 7))
            else:
                half = piece - 4
                for v in range(2):
                    for k in range(8):
                        fw.mm(pb[3][:, :], ccrep[:, k, v, :], wada_sb[:, k, :], start=(k == 0), stop=(k == 7))
                    fw.tt(gate[:, v, half * 512:(half + 1) * 512], pb[3][:, :], gbias[:, half * 512:(half + 1) * 512], ALU.add)
        for v in range(2):
            fw.tt(sh[:, v, :], V(ps_s.ap[:, 0:8, v], ps_s.res), badaf[:, 0:8], ALU.add)
            fw.tt(sc1[:, v, :], V(ps_s.ap[:, 8:16, v], ps_s.res), badaf[:, 8:16], ALU.add)
        fw.ts(sc1[:], sc1[:], 1.0, ALU.add)
        fw.phase_reset()

    def layer_params(l):
        fw.dma("sp", qkg[:], qkg_d[l])
        fw.dma("sp", lamv[:], lamv_d[l])
        fw.dma("sp", subln[:], subln_d[l])
        fw.tt(small[:, 0:64], lamv[:, 0, :], lamv[:, 1, :], ALU.mult)
        s1 = fw.sbuf(f"lam_s1_{l}", [128, 1], F32)
        s2 = fw.sbuf(f"lam_s2_{l}", [128, 1], F32)
        fw.reduce(s1[:], small[:, 0:64], ALU.add)
        fw.tt(small[:, 0:64], lamv[:, 2, :], lamv[:, 3, :], ALU.mult)
        fw.reduce(s2[:], small[:, 0:64], ALU.add)
        fw.act(s1[:], s1[:], AF.Exp)
        fw.act(s2[:], s2[:], AF.Exp)
        fw.tt(neglam[:], s2[:], s1[:], ALU.subtract)
        fw.ts(neglam[:], neglam[:], -lam_init_of(l), ALU.add)
        fw.ts(subln[:], subln[:], 1.0 - lam_init_of(l), ALU.mult)

    def phase_proj(l, need_ctx_q):
        hT = fw.carve("hT", [128, 8, NTOK], BF16)
        par = [0]
        alt = lambda n, sh, dt_: Alt([fw.carve(f"{n}_{i}", sh, dt_) for i in range(2)], par)
        xn = alt("xn", [128, D], F32)
        junk = alt("junk", [128, D], F32)
        ss = alt("ss", [128, 1], F32)
        rs = alt("rs", [128, 1], F32)
        wblk = [fw.carve(f"wblk{i}", [128, 8, 512], BF16) for i in range(2)]
        wst = fw.carve("wst", [128, 8, 512], F32)
        stage = [fw.carve(f"stage{i}", [128, 512], F32) for i in range(3)]
        sq = alt("sq", [128, 8, 64], F32)
        qn = alt("qn", [128, 8, 64], F32)
        t1 = alt("t1", [128, 8, 32], F32)
        t2 = alt("t2", [128, 8, 32], F32)
        ss8 = alt("ss8", [128, 8], F32)
        qbf = [fw.carve(f"qbf{i}", [128, 8, 64], BF16) for i in range(2)]
        tb = [fw.carve(f"tb{i}", [128, 4, 128], BF16) for i in range(2)]
        vbf = [fw.carve(f"vbf{i}", [128, 512], BF16) for i in range(2)]

        for lc in range(NLC):
            par[0] = lc
            src = x_sb[:, lc, :] if lc < NCH else ctx_sb[:, lc - NCH, :]
            v = 0 if lc < NCH else 1
            fw.act(junk[:], src, AF.Square, accum_out=ss[:])
            fw.act(rs[:], ss[:], AF.Sqrt, bias=eps_t[:], scale=1.0 / D)
            fw.recip(rs[:], rs[:])
            fw.ts(xn[:], src, rs[:], ALU.mult)
            pt = V(pb[lc % 2 * 2].t[:, :], [pb[lc % 2 * 2].whole, pb[lc % 2 * 2 + 1].whole])
            ptt = pb_t[:, lc % 2 * 2:lc % 2 * 2 + 2, :].rearrange("p a (k c) -> p (a k) c", c=128)
            for k in range(8):
                fw.transpose(V(ptt[:, k, :], pt.res), xn[:, k * 128:(k + 1) * 128], ident[:], last=(k == 7))
            for k in range(8):
                fw.ts(hT[:, k, lc * 128:(lc + 1) * 128], V(ptt[:, k, :], pt.res), sc1[:, v, k:k + 1], ALU.mult,
                      sh[:, v, k:k + 1], ALU.add)
        if "hT" in dbg:
            fw.dma("sp", V(dbg["hT"].t[:], dbg["hT"].whole), V(hT.t[:], hT.whole))

        it = 0
        for bi, (bname, col0, ncols) in enumerate(BLOCKS):
            wb = wblk[bi % 2]
            fw.dma("sp", V(wst.t[:, :, 0:ncols], wst.whole),
                   V(win_d.t[l, :, col0:col0 + ncols].rearrange("(k p) c -> p k c", p=128), win_d.whole))
            fw.copy(V(wb.t[:, :, 0:ncols], wb.whole), V(wst.t[:, :, 0:ncols], wst.whole), eng="pool")
            for lc in range(NLC):
                is_ctx = lc >= NCH
                bank = pb[4 + it % 2]
                it += 1
                par[0] = it
                for k in range(8):
                    fw.mm(bank[:, 0:ncols], hT[:, k, lc * 128:(lc + 1) * 128], wb[:, k, 0:ncols],
                          start=(k == 0), stop=(k == 7))
                rows = slice(lc * 128, (lc + 1) * 128)
                if bname in ("aq", "ak"):
                    if bname == "aq" and is_ctx and not need_ctx_q:
                        continue
                    gi = 0 if bname == "aq" else 1
                    psv = V(bank.t[:, :].rearrange("p (a b) -> p a b", b=64), bank.whole)
                    fw.act(sq[:], psv, AF.Square)
                    fw.reduce(ss8[:], sq[:], ALU.add)
                    fw.act(ss8[:], ss8[:], AF.Sqrt, bias=eps_t[:], scale=1.0 / 64)
                    fw.recip(ss8[:], ss8[:])
                    fw.tt(qn[:], psv, V(ss8.t[:].unsqueeze(2).to_broadcast([128, 8, 64]), ss8.whole), ALU.mult)
                    fw.tt(qn[:], qn[:], V(qkg.t[:, gi:gi + 1, :].to_broadcast([128, 8, 64]), qkg.whole), ALU.mult, eng="pool")
                    qo = qbf[it % 2]
                    if not is_ctx:
                        cosb = V(rope.t[:, lc, 0:1, :].to_broadcast([128, 8, 32]), rope.whole)
                        sinb = V(rope.t[:, lc, 1:2, :].to_broadcast([128, 8, 32]), rope.whole)
                        fw.tt(t1[:], qn[:, :, 0:32], cosb, ALU.mult)
                        fw.tt(t2[:], qn[:, :, 32:64], sinb, ALU.mult, eng="pool")
                        fw.tt(qo[:, :, 0:32], t1[:], t2[:], ALU.subtract)
                        fw.tt(t1[:], qn[:, :, 0:32], sinb, ALU.mult)
                        fw.tt(t2[:], qn[:, :, 32:64], cosb, ALU.mult, eng="pool")
                        fw.tt(qo[:, :, 32:64], t1[:], t2[:], ALU.add)
                    else:
                        fw.copy(qo[:], qn[:])
                    tbank = pbh[6 + lc % 2]
                    tbv = tbank.t[:, 0:512].rearrange("p (h c) -> p h c", c=128)
                    qof = qo.t[:].rearrange("p a b -> p (a b)")
                    for hd in range(4):
                        fw.transpose(V(tbv[:, hd, :], tbank.whole), V(qof[:, hd * 128:(hd + 1) * 128], qo.whole), ident_bf[:], last=(hd == 3))
                    tbs = tb[lc % 2]
                    fw.copy(tbs[:], V(tbv, tbank.whole), eng="act")
                    if bname == "aq":
                        fw.dma("act", V(qT_s.t[:, :, rows].rearrange("h p c -> p h c"), qT_s.part(lc).res), tbs[:])
                    elif is_ctx:
                        c0 = (lc - NCH) * 128
                        fw.dma("act", V(kc_s.t[:, :, c0:c0 + 128].rearrange("h p c -> p h c"), kc_s.part(lc).res), tbs[:])
                    else:
                        for hd in range(4):
                            fw.dma("act", V(kv_stage[hd].t[:, lc * 128:(lc + 1) * 128], kv_stage[hd].part(("k", lc)).res),
                                   tbs[:, hd, :])
                elif bname == "av":
                    vb = vbf[lc % 2]
                    fw.copy(vb[:], bank[:, :], eng="act")
                    if is_ctx:
                        c0 = (lc - NCH) * 128
                        fw.dma("act", V(vc_s.t[c0:c0 + 128, :], vc_s.part(lc).res), vb[:])
                    else:
                        for hd in range(4):
                            fw.dma("act", V(kv_stage[hd].t[:, 2048 + lc * 128:2048 + (lc + 1) * 128],
                                            kv_stage[hd].part(("v", lc)).res), vb[:, hd * 128:(hd + 1) * 128])
                elif bname == "ag":
                    st = stage[lc % 3]
                    fw.act(st[:], bank[:, :], AF.Silu)
                    tbank = pb[6 + lc % 2]
                    for hd in range(4):
                        fw.transpose(tbank[:, hd * 128:(hd + 1) * 128], st[:, hd * 128:(hd + 1) * 128], ident[:], last=(hd == 3))
                    tbs = tb[lc % 2]
                    fw.copy(V(tbs.t[:].rearrange("p h c -> p (h c)"), tbs.whole), tbank[:, :])
                    fw.dma("act", V(agT_s.t[:, :, rows].rearrange("h p c -> p h c"), agT_s.part(lc).res), tbs[:])
                else:
                    st = stage[lc % 3]
                    if lc % 2 == 0:
                        fw.copy(st[:, 0:ncols], bank[:, 0:ncols])
                    else:
                        fw.copy(st[:, 0:ncols], bank[:, 0:ncols], eng="act")
                    if bname.startswith("xbc"):
                        xc = (int(bname[3]) * 512)
                        if is_ctx:
                            r1 = 1 + (lc - NCH) * 128
                            fw.dma("sp", V(xbc_cpad.t[r1:r1 + 128, xc:xc + 512], xbc_cpad.part((bname, lc)).res), st[:, 0:ncols])
                        else:
                            r1 = 1 + lc * 128
                            fw.dma("sp", V(xbc_pad.t[r1:r1 + 128, xc:xc + 512], xbc_pad.part((bname, lc)).res), st[:, 0:ncols])
                    else:
                        fw.dma("sp", V(proj_s.t[rows, col0:col0 + ncols], proj_s.part((bname, lc)).res), st[:, 0:ncols])
            if bname == "av":
                for hd in range(4):
                    allres = [kv_stage[hd].part(("k", c)).res for c in range(NCH)] + [kv_stage[hd].part(("v", c)).res for c in range(NCH)]
                    fw.dma("sp", V(kv_src[hd].t[bass.ds(rank * 128, 128), :], kv_src[hd].whole), V(kv_stage[hd].t[:, :], allres))
                    fw.collective("AllReduce", ALU.add, GROUPS, kv_src[hd][:, :], kv_dst[hd][:, :])
        fw.phase_reset()

    def phase_attn(l, with_ctx_q):
        kTb = [fw.carve(f"kT{i}", [128, 8448], BF16) for i in range(2)]
        vvb = [fw.carve(f"vv{i}", [128, 66, 128], BF16) for i in range(2)]
        qhb = [fw.carve(f"qh{i}", [128, NTOK], BF16) for i in range(2)]
        aghb = [fw.carve(f"agh{i}", [128, NTOK], BF16) for i in range(2)]
        rec = fw.carve("rec", [128, 512], F32)
        om = [fw.carve(f"om{i}", [128, 512], F32) for i in range(2)]
        A = fw.carve("A", [128, 512], F32)
        sqb = fw.carve("sqb", [128, 512], BF16)
        rstd = fw.carve("rstd", [128, 512], F32)
        mixo = [fw.carve(f"mixo{i}", [128, 512], BF16) for i in range(2)]
        ones_f = fw.carve("ones_f", [128, 128], F32)
        fw.memset(ones_f[:], 1.0)
        qblocks = [(q0, 512, 0, 66) for q0 in range(0, TOK, 512)]
        if with_ctx_q:
            qblocks.append((TOK, 256, 64, 66))
        sbank = (pb[0], pb[1], pb[7])
        nq_all = NTOK if with_ctx_q else TOK

        def load_head(hd):
            kT, vv, qh, agh = kTb[hd % 2], vvb[hd % 2], qhb[hd % 2], aghb[hd % 2]
            fw.dma("sp", V(kT.t[:, 0:8192].rearrange("p (r c) -> p r c", r=4), kT.whole),
                   V(kv_dst[hd].t[:, 0:2048].rearrange("(r p) c -> p r c", p=128), kv_dst[hd].whole))
            fw.dma("sp", kT[:, 8192:8448], V(kc_s.t[hd], [kc_s.part(16).res, kc_s.part(17).res]))
            fw.dma("sp", V(vv.t[:, 0:64, :].rearrange("p (r c) e -> p r c e", r=4), vv.whole),
                   V(kv_dst[hd].t[:, 2048:4096].rearrange("(r p) (c e) -> p r c e", p=128, e=128), kv_dst[hd].whole))
            fw.dma("sp", vv[:, 64:66, :], V(vc_s.t[:, hd * 128:(hd + 1) * 128].rearrange("(c p) e -> p c e", p=128),
                                           [vc_s.part(16).res, vc_s.part(17).res]))
            qres = [qT_s.part(c).res for c in range(NLC if with_ctx_q else NCH)]
            fw.dma("sp", qh[:, 0:nq_all], V(qT_s.t[hd, :, 0:nq_all], qres))
            fw.dma("sp", agh[:, 0:NTOK], V(agT_s.t[hd], [agT_s.part(c).res for c in range(NLC)]))

        load_head(0)
        pT2 = [fw.carve(f"pTT{i}", [128, 2, 512], BF16) for i in range(3)]
        acc2 = [fw.carve(f"accT{i}", [128, 2, 512], F32) for i in range(2)]
        stage_banks = ((0, 1), (4, 5))
        for hd in range(4):
            if hd + 1 < 4:
                load_head(hd + 1)
            kT, vv, qh, agh = kTb[hd % 2], vvb[hd % 2], qhb[hd % 2], aghb[hd % 2]
            for qi, (q0, nq_, kc0, kc1) in enumerate(qblocks):
                kcs = list(range(kc0, kc1))
                n = len(kcs)

                def qk(i):
                    kc = kcs[i]
                    b0, b1 = stage_banks[i % 2]
                    for m, bk in ((0, b0), (1, b1)):
                        fw.mm(pb[bk][:, 0:nq_], kT[m * 64:(m + 1) * 64, kc * 128:(kc + 1) * 128], qh[m * 64:(m + 1) * 64, q0:q0 + nq_], True, True)

                qk(0)
                for i, kc in enumerate(kcs):
                    if i + 1 < n:
                        qk(i + 1)
                    b0, b1 = stage_banks[i % 2]
                    p = pT2[i % 3]
                    sc2 = V(pb_t[:, b0:b0 + 2, 0:nq_], [pb[b0].whole, pb[b1].whole])
                    fw.act(p[:, :, 0:nq_], sc2, AF.Exp, scale=0.125)
                    first, lastk = (i == 0), (i == n - 1)
                    for m in range(2):
                        fw.mm(pb[2 + m][:, 0:nq_], vv[:, kc, :], p[:, m, 0:nq_], start=first, stop=lastk)
                    eng_ = "dve" if i % 2 == 0 else "pool"
                    acc_ = acc2[i % 2]
                    if i < 2:
                        fw.copy(acc_[:, :, 0:nq_], p[:, :, 0:nq_], eng=eng_)
                    else:
                        fw.tt(acc_[:, :, 0:nq_], acc_[:, :, 0:nq_], p[:, :, 0:nq_], ALU.add, eng=eng_)
                if n > 1:
                    fw.tt(acc2[0][:, :, 0:nq_], acc2[0][:, :, 0:nq_], acc2[1][:, :, 0:nq_], ALU.add)
                for m in range(2):
                    fw.mm(pb[7][:, 0:nq_], ones_f[:], acc2[0][:, m, 0:nq_], True, True)
                    fw.recip(rec[:, 0:nq_], pb[7][:, 0:nq_])
                    fw.tt(om[m][:, 0:nq_], pb[2 + m][:, 0:nq_], rec[:, 0:nq_], ALU.mult)
                fw.stt(A[:, 0:nq_], om[1][:, 0:nq_], neglam[:], om[0][:, 0:nq_], ALU.mult, ALU.add)
                fw.act(sqb[:, 0:nq_], A[:, 0:nq_], AF.Square)
                fw.mm(pb[6][:, 0:nq_], ones_bf[:], sqb[:, 0:nq_], True, True)
                fw.act(rstd[:, 0:nq_], pb[6][:, 0:nq_], AF.Sqrt, bias=eps_t[:], scale=1.0 / 128)
                fw.recip(rstd[:, 0:nq_], rstd[:, 0:nq_])
                fw.stt(A[:, 0:nq_], A[:, 0:nq_], subln[:], rstd[:, 0:nq_], ALU.mult, ALU.mult)
                mo = mixo[qi % 2]
                fw.tt(mo[:, 0:nq_], A[:, 0:nq_], agh[:, q0:q0 + nq_], ALU.mult, eng="pool")
                fw.dma("act", V(mixT_s.t[hd, :, q0:q0 + nq_], mixT_s.part((hd, qi)).res), mo[:, 0:nq_])
        fw.phase_reset()

    def phase_halo(l):
        xr = lambda c: [xbc_pad.part((f"xbc{i}", c)).res for i in range(3)]
        fw.dma("sp", hx_stage[0:1, :], V(xbc_pad.t[1:2, :], xr(0)))
        fw.dma("sp", hx_stage[1:2, :], V(xbc_pad.t[TOK:TOK + 1, :], xr(NCH - 1)))
        fw.dma("sp", V(hx_src.t[bass.ds(rank * 2, 2), :], hx_src.whole), hx_stage[:, :])
        fw.collective("AllReduce", ALU.add, GROUPS, hx_src[:, :], hx_dst[:, :])
        fw.dma("sp", hxL[1:9, :], hx_dst[:, :])
        fw.dma("sp", hxR[0:6, :], hx_dst[2:8, :])
        fw.dma("sp", V(xbc_pad.t[0:1, :], xbc_pad.part("hl").res), V(hxL.t[bass.ds(rank * 2, 1), :], hxL.whole))
        fw.dma("sp", V(xbc_pad.t[TOK + 1:TOK + 2, :], xbc_pad.part("hr").res), V(hxR.t[bass.ds(rank * 2, 1), :], hxR.whole))

    def rope_apply(out, x, lc, nh, t1, t2):
        cosb = V(rope.t[:, lc, 0:1, :].to_broadcast([128, nh, 32]), rope.whole)
        sinb = V(rope.t[:, lc, 1:2, :].to_broadcast([128, nh, 32]), rope.whole)
        fw.tt(t1[:, 0:nh, :], x[:, :, 0:32], cosb, ALU.mult)
        fw.tt(t2[:, 0:nh, :], x[:, :, 32:64], sinb, ALU.mult, eng="pool")
        fw.tt(out[:, :, 0:32], t1[:, 0:nh, :], t2[:, 0:nh, :], ALU.subtract)
        fw.tt(t1[:, 0:nh, :], x[:, :, 0:32], sinb, ALU.mult)
        fw.tt(t2[:, 0:nh, :], x[:, :, 32:64], cosb, ALU.mult, eng="pool")
        fw.tt(out[:, :, 32:64], t1[:, 0:nh, :], t2[:, 0:nh, :], ALU.add)

    def chain_combine(Sin, Sctx, d, col0, ncol, nparts, Aexp_of, tmp, slot):
        fw.copy(Sin[0:nparts, :], Sctx[0:nparts, :])
        order = range(4) if d == 0 else range(3, -1, -1)
        for sidx in order:
            fw.dma("sp", slot[0:nparts, 0:ncol], st_dst[d][sidx * 128:sidx * 128 + nparts, col0:col0 + ncol])
            Aexp_of(sidx, tmp)
            fw.tt(tmp[0:nparts, :], tmp[0:nparts, :], slot[0:nparts, 0:ncol], ALU.add)
            fw.tt(tmp[0:nparts, :], tmp[0:nparts, :], Sin[0:nparts, :], ALU.subtract)
            mcol = cmask[:, d * 4 + sidx:d * 4 + sidx + 1]
            fw.stt(Sin[0:nparts, :], tmp[0:nparts, :], V(mcol.ap[0:nparts], mcol.res), Sin[0:nparts, :], ALU.mult, ALU.add)

    def phase_ret(l, need_ctx):
        rett = fw.carve("rett", [128, 4 * 128 + 24], F32)
        rnorm = fw.carve("rnorm", [128, 128], F32)
        fw.dma("sp", rett[:], rett_d[:])
        fw.dma("sp", rnorm[:], ssdp_d[l, :, NP - 128:NP])
        Dret = V(rett.t[:, 0:512].rearrange("p (h i) -> p h i", i=128), rett.whole)
        tab = lambda k: V(rett.t[:, 512 + 4 * k:512 + 4 * k + 4], rett.whole)
        par = [0]
        alt = lambda n, sh, dt_: Alt([fw.carve(f"{n}_{i}", sh, dt_) for i in range(2)], par)
        qk = alt("qk", [128, 8, 64], F32)
        qkr = alt("qkr", [128, 8, 64], F32)
        t1 = alt("rt1", [128, 8, 32], F32)
        t2 = alt("rt2", [128, 8, 32], F32)
        rv = alt("rv", [128, 512], F32)
        rvbf = alt("rvbf", [128, 512], BF16)
        kte = [alt(f"kte{d}", [128, 4, 64], BF16) for d in range(2)]
        q3 = alt("q3", [128, 3, 4, 64], BF16)
        kbf = alt("kbf", [128, 4, 64], BF16)
        qT = alt("qT", [64, 3, 4, 128], BF16)
        kT = alt("kT", [64, 4, 128], BF16)
        Wt = alt("Wt", [128, 4, 128], BF16)
        R = [fw.carve(f"R{d}", [64, 512], F32) for d in range(2)]
        Rbf = [fw.carve(f"Rbf{d}", [64, 512], BF16) for d in range(2)]
        Rctx = [fw.carve(f"Rctx{d}", [64, 512], F32) for d in range(2)]
        Pb = fw.carve("Pb", [128, 4], F32)
        cs_sb = [alt(f"cs_sb{d}", [64, 512], F32) for d in range(2)]
        tmp = fw.carve("rtmp", [64, 512], F32)
        slot = fw.carve("rslot", [64, 512], F32)
        rg = alt("rg", [128, 512], F32)
        ysq = alt("ysq", [128, 4, 128], F32)
        yss = alt("yss", [128, 4], F32)
        yn = alt("yn", [128, 4, 128], F32)
        ybf = alt("ybf", [128, 512], BF16)
        ytb = alt("ytb", [128, 4, 128], BF16)
        bc4 = lambda v, n: V(v.ap.unsqueeze(2).to_broadcast([v.ap.shape[0], 4, n]), v.res)

        def prep(lc):
            par[0] = lc
            is_ctx = lc >= NCH
            rows = slice(lc * 128, (lc + 1) * 128)
            pr = lambda n: proj_s.part((n, lc)).res
            fw.dma("sp", V(qk.t[:].rearrange("p a b -> p (a b)"), qk.whole), V(proj_s.t[rows, 4640:5152], pr("rqk")))
            fw.dma("sp", rv[:], V(proj_s.t[rows, 5152:5664], pr("rv")))
            if is_ctx:
                src = qk
            else:
                rope_apply(qkr, qk, lc, 8, t1, t2)
                src = qkr
            fw.copy(rvbf[:], rv[:], eng="pool")
            for d in range(2):
                fw.tt(kte[d][:], src[:, 4:8, :], bc4(tab(d), 64), ALU.mult)
            return src

        def chunk_states(start_banks=(0, 1)):
            for d in range(2):
                bank = pb[start_banks[d]]
                for h in range(4):
                    fw.mm(bank[0:64, h * 128:(h + 1) * 128], kte[d][:, h, :], rvbf[:, h * 128:(h + 1) * 128],
                          start=(h == 0), stop=(h == 3), skip_group_check=True)
            return [pb[start_banks[0]], pb[start_banks[1]]]

        A128 = [tab(4), tab(5)]

        def fold(acc, csb, lc, store):
            fw.tt(V(acc[0].t[:].rearrange("p (h n) -> p h n", n=128), acc[0].whole),
                  V(acc[0].t[:].rearrange("p (h n) -> p h n", n=128), acc[0].whole),
                  V(A128[0].ap[0:64].unsqueeze(2).to_broadcast([64, 4, 128]), A128[0].res), ALU.mult)
            fw.tt(acc[0][:], acc[0][:], csb[0][0:64, :], ALU.add)
            fw.tt(V(tmp.t[:].rearrange("p (h n) -> p h n", n=128), tmp.whole),
                  V(csb[1].t[0:64, :].rearrange("p (h n) -> p h n", n=128), csb[1].whole),
                  V(Pb.t[0:64, :].unsqueeze(2).to_broadcast([64, 4, 128]), Pb.whole), ALU.mult)
            fw.tt(acc[1][:], acc[1][:], tmp[:], ALU.add)
            fw.tt(Pb[:], Pb[:], A128[1], ALU.mult)
            if store:
                for d in range(2):
                    fw.copy(cs_sb[d][:], csb[d][0:64, :])
                    fw.dma("act", V(cst_s.t[lc, d, 0:64, 1024:1536], cst_s.part(("r", lc, d)).res), cs_sb[d][:])

        for grp in ((16, 17), tuple(range(NCH))):
            for d in range(2):
                fw.memset(R[d][:], 0.0)
            fw.memset(Pb[:], 1.0)
            for lc in grp:
                prep(lc)
                if KCUT >= 2:
                    csb = chunk_states()
                if KCUT >= 3:
                    fold(R, csb, lc, KCUT >= 4)
            if grp[0] == 16:
                for d in range(2):
                    fw.copy(Rctx[d][:], R[d][:])
        if "ret_sf" in dbg:
            fw.dma("sp", dbg["ret_sf"][:, :], Rctx[0][:])
            fw.dma("sp", dbg["ret_sb"][:, :], Rctx[1][:])
        if stop_after == "ret_p1":
            fw.phase_reset(); return
        for d in range(2):
            fw.dma("sp", st_stage[d][0:64, 1024:1536], R[d][:])
            fw.dma("sp", V(st_src[d].t[bass.ds(rank * 128, 128), :], st_src[d].whole), st_stage[d][:, :])
            fw.collective("AllReduce", ALU.add, GROUPS, st_src[d][:, :], st_dst[d][:, :])
        if stop_after == "ret_x":
            fw.phase_reset(); return
        A2048 = [[(1.0 - 2.0 ** -e) ** 2048 for e in RET_EXP_F], [(1.0 - 2.0 ** -e) ** 2048 for e in RET_EXP_B]]
        Rin = [fw.carve(f"Rin{d}", [64, 512], F32) for d in range(2)]
        for d in range(2):
            def aexp(sidx, t_, d=d):
                for h in range(4):
                    fw.ts(t_[0:64, h * 128:(h + 1) * 128], Rin[d][0:64, h * 128:(h + 1) * 128], float(A2048[d][h]), ALU.mult)
            chain_combine(Rin[d], Rctx[d], d, 1024, 512, 64, aexp, tmp, slot)
        snap = fw.carve("rsnap", [64, 512], BF16)
        csl = fw.carve("rcsl", [64, 512], F32)
        fw.copy(R[1][:], Rin[1][:])
        for lc in range(NCH - 1, -1, -1):
            fw.copy(snap[:], R[1][:])
            fw.dma("act", V(sb_s.t[lc, 0:64, 1024:1536], sb_s.part(("r", lc)).res), snap[:])
            fw.dma("sp", csl[:], V(cst_s.t[lc, 1, 0:64, 1024:1536], cst_s.part(("r", lc, 1)).res))
            fw.tt(V(R[1].t[:].rearrange("p (h n) -> p h n", n=128), R[1].whole),
                  V(R[1].t[:].rearrange("p (h n) -> p h n", n=128), R[1].whole),
                  V(A128[1].ap[0:64].unsqueeze(2).to_broadcast([64, 4, 128]), A128[1].res), ALU.mult)
            fw.tt(R[1][:], R[1][:], csl[:], ALU.add)
        if need_ctx:
            fw.dma("sp", csl[:], V(cst_s.t[17, 1, 0:64, 1024:1536], cst_s.part(("r", 17, 1)).res))
            fw.copy(snap[:], csl[:])
            fw.dma("sp", V(sb_s.t[16, 0:64, 1024:1536], sb_s.part(("r", 16)).res), snap[:])
            snap0 = fw.carve("rsnap0", [64, 512], BF16)
            fw.memset(snap0[:], 0.0)
            fw.dma("sp", V(sb_s.t[17, 0:64, 1024:1536], sb_s.part(("r", 17)).res), snap0[:])
        if stop_after == "ret_p2":
            fw.phase_reset(); return
        groups3 = [tuple(range(NCH))] + ([(16, 17)] if need_ctx else [])
        for grp in groups3:
            if grp[0] == 16:
                fw.memset(R[0][:], 0.0)
            else:
                fw.copy(R[0][:], Rin[0][:])
            for lc in grp:
                is_ctx = lc >= NCH
                rows = slice(lc * 128, (lc + 1) * 128)
                src = prep(lc)
                fw.dma("sp", rg[:], V(proj_s.t[rows, 5664:6176], proj_s.part(("rg", lc)).res))
                fw.dma("sp", Rbf[1][:], V(sb_s.t[lc, 0:64, 1024:1536], sb_s.part(("r", lc)).res))
                fw.copy(Rbf[0][:], R[0][:])
                fw.copy(q3[:, 0, :, :], src[:, 0:4, :])
                fw.tt(q3[:, 1, :, :], src[:, 0:4, :], bc4(tab(2), 64), ALU.mult)
                fw.tt(q3[:, 2, :, :], src[:, 0:4, :], bc4(tab(3), 64), ALU.mult, eng="pool")
                fw.copy(kbf[:], src[:, 4:8, :])
                tqa = pbh[2]
                tqav = tqa.t[0:64, 0:1024].rearrange("p (k h c) -> p k h c", k=2, h=4)
                tqb = pbh[3]
                tqbv = tqb.t[0:64, 0:512].rearrange("p (h c) -> p h c", h=4)
                for k3 in range(2):
                    for h in range(4):
                        fw.transpose(V(tqav[:, k3, h, :], tqa.whole), q3[:, k3, h, :], ident_bf[:], last=(k3 == 1 and h == 3))
                for h in range(4):
                    fw.transpose(V(tqbv[:, h, :], tqb.whole), q3[:, 2, h, :], ident_bf[:], last=(h == 3))
                fw.copy(qT[:, 0:2, :, :], V(tqav, tqa.whole))
                fw.copy(qT[:, 2, :, :], V(tqbv, tqb.whole))
                tk = pbh[4]
                tkv = tk.t[0:64, 0:512].rearrange("p (h c) -> p h c", h=4)
                for h in range(4):
                    fw.transpose(V(tkv[:, h, :], tk.whole), kbf[:, h, :], ident_bf[:], last=(h == 3))
                fw.copy(kT[:], V(tkv, tk.whole))
                sc = pb[5]
                for h in range(4):
                    fw.mm(sc[:, h * 128:(h + 1) * 128], kT[:, h, :], qT[:, 0, h, :], start=(h == 0), stop=(h == 3), skip_group_check=True)
                fw.tt(Wt[:], V(sc.t[:, :].rearrange("p (h i) -> p h i", i=128), sc.whole), Dret, ALU.mult)
                csb = chunk_states((0, 1))
                yb = pb[6]
                for h in range(4):
                    o = yb[:, h * 128:(h + 1) * 128]
                    fw.mm(o, Wt[:, h, :], rvbf[:, h * 128:(h + 1) * 128], start=(h == 0), stop=False, last=False, skip_group_check=True)
                    fw.mm(o, qT[:, 1, h, :], Rbf[0][:, h * 128:(h + 1) * 128], start=False, stop=False, last=False, skip_group_check=True)
                    fw.mm(o, qT[:, 2, h, :], Rbf[1][:, h * 128:(h + 1) * 128], start=False, stop=(h == 3), last=(h == 3), skip_group_check=True)
                fw.tt(V(R[0].t[:].rearrange("p (h n) -> p h n", n=128), R[0].whole),
                      V(R[0].t[:].rearrange("p (h n) -> p h n", n=128), R[0].whole),
                      V(A128[0].ap[0:64].unsqueeze(2).to_broadcast([64, 4, 128]), A128[0].res), ALU.mult)
                fw.tt(R[0][:], R[0][:], csb[0][0:64, :], ALU.add)
                ybv = V(yb.t[:, :].rearrange("p (h n) -> p h n", n=128), yb.whole)
                fw.act(ysq[:], ybv, AF.Square)
                fw.reduce(yss[:], ysq[:], ALU.add)
                fw.act(yss[:], yss[:], AF.Sqrt, bias=eps_t[:], scale=1.0 / 128)
                fw.recip(yss[:], yss[:])
                fw.tt(yn[:], ybv, V(yss.t[:].unsqueeze(2).to_broadcast([128, 4, 128]), yss.whole), ALU.mult)
                fw.tt(yn[:], yn[:], V(rnorm.t[:].unsqueeze(1).to_broadcast([128, 4, 128]), rnorm.whole), ALU.mult, eng="pool")
                fw.act(rg[:], rg[:], AF.Silu)
                fw.tt(ybf[:], V(yn.t[:].rearrange("p h n -> p (h n)"), yn.whole), rg[:], ALU.mult)
                to = pbh[7]
                tov = to.t[:, 0:512].rearrange("p (h c) -> p h c", h=4)
                for h in range(4):
                    fw.transpose(V(tov[:, h, :], to.whole), ybf[:, h * 128:(h + 1) * 128], ident_bf[:], last=(h == 3))
                fw.copy(ytb[:], V(tov, to.whole))
                fw.dma("act", V(mixT_s.t[12:16, :, rows].rearrange("h p c -> p h c"), mixT_s.part(("ret", lc)).res), ytb[:])
        fw.phase_reset()

    def phase_ssd(l, need_ctx):
        OW, OB, ODT, OA, ODD = 0, 4608, 6144, 6176, 6208
        prm = fw.carve("prm", [128, 6224], F32)
        fw.dma("sp", prm[:], ssdp_d[l, :, 0:6224])
        tri = fw.carve("tri", [128, 5, 128], F32)
        fw.dma("sp", tri[:], tri_d[:])
        ssdn = fw.carve("ssdn", [128, 8], F32)
        fw.dma("sp", ssdn[:], ssdn_d[l])
        negA = fw.carve("negA", [128, 32], F32)
        fw.act(negA[:], prm[:, OA:OA + 32], AF.Exp)
        fw.ts(negA[:], negA[:], -1.0, ALU.mult)
        one_t = fw.carve("one_t", [128, 1], F32)
        fw.memset(one_t[:], 1.0)
        U = [fw.carve(f"U{i}", [128, 1536], F32) for i in range(3)]
        dtr = fw.carve("dtr", [128, 32], F32)
        la = fw.carve("la", [128, 32], F32)
        E = fw.carve("E", [128, 96], F32)
        praw = fw.carve("praw", [128, 96], F32)
        cumraw = Buf(fw, "cumraw", _APHandle(praw.t[:, 0:32]))
        cumraw.whole = praw.whole
        tots = fw.carve("tots", [128, 32], F32)
        v = [fw.carve(f"v{d}", [128, 1024], BF16) for d in range(2)]
        vte = [fw.carve(f"vte{d}", [128, 1024], BF16) for d in range(2)]
        BCbf = fw.carve("BCbf", [128, 512], BF16)
        BCT = fw.carve("BCT", [128, 4, 128], BF16)
        zt = fw.carve("zt", [128, 1024], F32)
        R1 = fw.carve("R1", [128, 16, 128], F32)
        seg = fw.carve("seg", [128, 16, 128], F32)
        Dm = fw.carve("Dm", [128, 16, 128], BF16)
        Sm = [fw.carve(f"Sm{d}", [128, 2, 128], F32) for d in range(2)]
        Wt = fw.carve("Wt", [128, 16, 128], BF16)
        S = [fw.carve(f"S{d}", [128, 1024], F32) for d in range(2)]
        Sx = [fw.carve(f"Sx{d}", [128, 1024], F32) for d in range(2)]
        Sbf = [fw.carve(f"Sbf{d}", [128, 1024], BF16) for d in range(2)]
        Pb = fw.carve("Pb", [128, 16], F32)
        yt = fw.carve("yt", [128, 1024], F32)
        y2 = fw.carve("y2", [128, 1024], F32)
        vtmp = Buf(fw, "vtmp", _APHandle(y2.t[:].rearrange("p (h n) -> p h n", n=64)))
        vtmp.whole = y2.whole
        gss = fw.carve("gss", [128, 2], F32)
        ybf = fw.carve("ybf", [128, 1024], BF16)
        ytb = fw.carve("ytb", [128, 8, 128], BF16)
        Aex = fw.carve("Aex", [128, 16], F32)
        h16 = lambda vv: V(vv.ap.unsqueeze(2).to_broadcast([128, 16, 64]), vv.res)
        as16 = lambda b_: V(b_.t[:].rearrange("p (h n) -> p h n", n=64), b_.whole)

        def prep(lc):
            is_ctx = lc >= NCH
            src_t, r0 = (xbc_cpad, (lc - NCH) * 128) if is_ctx else (xbc_pad, lc * 128)
            rr = [xbc_pad.part("hl").res, xbc_pad.part("hr").res]
            for k in range(3):
                fw.dma("sp", U[k][:], V(src_t.t[r0 + k:r0 + k + 128, :], rr))
            fw.dma("sp", dtr[:], V(proj_s.t[lc * 128:(lc + 1) * 128, 3584:3616], proj_s.part(("dtr", lc)).res))
            fw.tt(U[0][:], U[0][:], prm[:, OW:OW + 1536], ALU.mult, eng="pool")
            fw.tt(U[1][:], U[1][:], prm[:, OW + 1536:OW + 3072], ALU.mult)
            fw.tt(U[2][:], U[2][:], prm[:, OW + 3072:OW + 4608], ALU.mult, eng="pool")
            fw.tt(U[1][:], U[1][:], U[0][:], ALU.add)
            fw.tt(U[1][:], U[1][:], U[2][:], ALU.add)
            fw.tt(U[1][:], U[1][:], prm[:, OB:OB + 1536], ALU.add)
            fw.act(U[0][:], U[1][:], AF.Silu)
            fw.tt(dtr[:], dtr[:], prm[:, ODT:ODT + 32], ALU.add)
            fw.act(dtr[:], dtr[:], AF.Exp)
            fw.act(dtr[:], dtr[:], AF.Ln, bias=one_t[:])
            fw.tt(la[:], dtr[:], negA[:], ALU.mult)
            pe = pb[0]
            for i, (w, c0, c1) in enumerate(((0, 0, 16), (1, 16, 32), (2, 0, 16), (3, 16, 32), (4, 0, 32))):
                o0 = (0, 16, 32, 48, 64)[i]
                fw.mm(pe[:, o0:o0 + (c1 - c0)], tri[:, w, :], la[:, c0:c1], start=(i == 0), stop=(i == 4), skip_group_check=True)
            fw.copy(praw[:], pe[:, 0:96])
            fw.act(E[:], praw[:], AF.Exp)
            fw.tt(tots[:], tots[:], praw[:, 64:96], ALU.add)
            xs = V(U[0].t[:, 0:1024].rearrange("p (h n) -> p h n", n=64), U[0].whole)
            for d in range(2):
                fw.tt(vtmp[:], xs, h16(dtr[:, d * 16:(d + 1) * 16]), ALU.mult)
                fw.copy(as16(v[d]), vtmp[:], eng="pool")
                fw.tt(as16(vte[d]), vtmp[:], h16(E[:, 32 + d * 16:48 + d * 16]), ALU.mult)
            fw.copy(BCbf[:], U[0][:, 1024:1536], eng="pool")

        def chunk_state(d):
            banks = (pb[4], pb[5])
            for g in range(2):
                fw.mm(banks[g][:, :], BCbf[:, g * 128:(g + 1) * 128], vte[d][:, g * 512:(g + 1) * 512], True, True)
            return banks

        def mulA(dst, srcS, acol):
            fw.tt(as16(dst), as16(srcS), h16(acol), ALU.mult)

        for grp in ((16, 17), tuple(range(NCH))):
            for d in range(2):
                fw.memset(S[d][:], 0.0)
            fw.memset(Pb[:], 1.0)
            fw.memset(tots[:], 0.0)
            for lc in grp:
                prep(lc)
                if lc < NCH or need_ctx:
                    pr_ = pc_h.part(lc).res
                    fw.dma("sp", V(pc_h.t[lc, :, 0:1024], pr_), v[0][:])
                    fw.dma("sp", V(pc_h.t[lc, :, 1024:2048], pr_), v[1][:])
                    fw.dma("sp", V(pc_h.t[lc, :, 2048:3072], pr_), vte[0][:])
                    fw.dma("sp", V(pc_h.t[lc, :, 3072:3584], pr_), BCbf[:])
                    pf_ = pc_f.part(lc).res
                    fw.dma("sp", V(pc_f.t[lc, :, 0:1024], pf_), U[0][:, 0:1024])
                    fw.dma("sp", V(pc_f.t[lc, :, 1024:1120], pf_), praw[:])
                    fw.dma("sp", V(pc_f.t[lc, :, 1120:1152], pf_), la[:])
                for d in range(2):
                    banks = chunk_state(d)
                    csv = V(pb_t[:, 4:6, :], [pb[4].whole, pb[5].whole])
                    cs_sb = V(seg.t[:, d * 8:(d + 1) * 8, :].rearrange("p a (g c) -> p (a g) c", g=2)[:, 0:2, :] if False else seg.t[:, d * 8:(d + 1) * 8, :], seg.whole)
                    cs_flat = V(seg.t[:].rearrange("p h n -> p (h n)")[:, d * 1024:(d + 1) * 1024], seg.whole)
                    fw.copy(V(cs_flat.ap.rearrange("p (g c) -> p g c", g=2), seg.whole), csv)
                    fw.dma("sp", V(cst_s.t[lc, d, :, 0:1024], cst_s.part(("s", lc, d)).res), cs_flat)
                    if d == 0:
                        mulA(S[0], S[0], E[:, 64:80])
                        fw.tt(S[0][:], S[0][:], cs_flat, ALU.add)
                    else:
                        fw.tt(as16(y2), V(cs_flat.ap.rearrange("p (h n) -> p h n", n=64), seg.whole), h16(Pb[:, :]), ALU.mult)
                        fw.tt(S[1][:], S[1][:], y2[:], ALU.add)
                        fw.tt(Pb[:], Pb[:], E[:, 80:96], ALU.mult)
                fw.dma("sp", V(cst_s.t[lc, 0, :, 1536:1568], cst_s.part(("e", lc)).res), E[:, 64:96])
            if grp[0] == 16:
                for d in range(2):
                    fw.copy(Sx[d][:], S[d][:])
        if "ssd_sf" in dbg:
            fw.dma("sp", dbg["ssd_sf"][:, :], Sx[0][:])
            fw.dma("sp", dbg["ssd_sb"][:, :], Sx[1][:])
        for d in range(2):
            fw.dma("sp", st_stage[d][:, 0:1024], S[d][:])
            fw.dma("sp", st_stage[d][:, 1536:1552], tots[:, d * 16:(d + 1) * 16])
            fw.dma("sp", V(st_src[d].t[bass.ds(rank * 128, 128), :], st_src[d].whole), st_stage[d][:, :])
            fw.collective("AllReduce", ALU.add, GROUPS, st_src[d][:, :], st_dst[d][:, :])
        for d in range(2):
            order = range(4) if d == 0 else range(3, -1, -1)
            for sidx in order:
                fw.dma("sp", yt[:], st_dst[d][sidx * 128:(sidx + 1) * 128, 0:1024])
                fw.dma("sp", Aex[:], st_dst[d][sidx * 128:(sidx + 1) * 128, 1536:1552])
                fw.act(Aex[:], Aex[:], AF.Exp)
                mulA(y2, Sx[d], Aex[:, :])
                fw.tt(y2[:], y2[:], yt[:], ALU.add)
                fw.tt(y2[:], y2[:], Sx[d][:], ALU.subtract)
                fw.stt(Sx[d][:], y2[:], cmask[:, d * 4 + sidx:d * 4 + sidx + 1], Sx[d][:], ALU.mult, ALU.add)
        for lc in range(NCH - 1, -1, -1):
            fw.copy(Sbf[1][:], Sx[1][:])
            fw.dma("sp", V(sb_s.t[lc, :, 0:1024], sb_s.part(("s", lc)).res), Sbf[1][:])
            fw.dma("sp", yt[:], V(cst_s.t[lc, 1, :, 0:1024], cst_s.part(("s", lc, 1)).res))
            fw.dma("sp", Aex[:], V(cst_s.t[lc, 0, :, 1552:1568], cst_s.part(("e", lc)).res))
            mulA(Sx[1], Sx[1], Aex[:, :])
            fw.tt(Sx[1][:], Sx[1][:], yt[:], ALU.add)
        if need_ctx:
            fw.dma("sp", yt[:], V(cst_s.t[17, 1, :, 0:1024], cst_s.part(("s", 17, 1)).res))
            fw.copy(Sbf[1][:], yt[:])
            fw.dma("sp", V(sb_s.t[16, :, 0:1024], sb_s.part(("s", 16)).res), Sbf[1][:])
            fw.memset(Sbf[0][:], 0.0)
            fw.dma("sp", V(sb_s.t[17, :, 0:1024], sb_s.part(("s", 17)).res), Sbf[0][:])
        if stop_after == "ssd_p2":
            fw.phase_reset(); return
        groups3 = [tuple(range(NCH))] + ([(16, 17)] if need_ctx else [])
        for grp in groups3:
            if grp[0] == 16:
                fw.memset(Sx[0][:], 0.0)
            for lc in grp:
                rows = slice(lc * 128, (lc + 1) * 128)
                pr_ = pc_h.part(lc).res
                pf_ = pc_f.part(lc).res
                fw.dma("sp", v[0][:], V(pc_h.t[lc, :, 0:1024], pr_))
                fw.dma("sp", v[1][:], V(pc_h.t[lc, :, 1024:2048], pr_))
                fw.dma("sp", vte[0][:], V(pc_h.t[lc, :, 2048:3072], pr_))
                fw.dma("sp", BCbf[:], V(pc_h.t[lc, :, 3072:3584], pr_))
                fw.dma("sp", U[0][:, 0:1024], V(pc_f.t[lc, :, 0:1024], pf_))
                fw.dma("sp", praw[:], V(pc_f.t[lc, :, 1024:1120], pf_))
                fw.dma("sp", la[:], V(pc_f.t[lc, :, 1120:1152], pf_))
                fw.act(E[:], praw[:], AF.Exp)
                fw.dma("sp", zt[:, 0:512], V(proj_s.t[rows, 3616:4128], proj_s.part(("z0", lc)).res))
                fw.dma("sp", zt[:, 512:1024], V(proj_s.t[rows, 4128:4640], proj_s.part(("z1", lc)).res))
                fw.dma("sp", Sbf[1][:], V(sb_s.t[lc, :, 0:1024], sb_s.part(("s", lc)).res))
                fw.copy(Sbf[0][:], Sx[0][:], eng="pool")
                tb_ = pbh[1]
                tbv = tb_.t[:, 0:512].rearrange("p (a c) -> p a c", a=4)
                for a in range(4):
                    fw.transpose(V(tbv[:, a, :], tb_.whole), BCbf[:, a * 128:(a + 1) * 128], ident_bf[:], last=(a == 3))
                fw.copy(BCT[:], V(tbv, tb_.whole))
                sc = pb[1]
                scv = sc.t[:, 256:512].rearrange("p (g i) -> p g i", g=2)
                for g in range(2):
                    fw.mm(V(scv[:, g, :], sc.whole), BCT[:, g, :], BCT[:, 2 + g, :], start=False if False else (g == 0), stop=(g == 1), skip_group_check=True)
                for d in range(2):
                    fw.tt(Sm[d][:], V(scv, sc.whole), V(tri.t[:, d:d + 1, :].to_broadcast([128, 2, 128]), tri.whole), ALU.mult)
                yb = (pb[6], pb[7])
                for d in range(2):
                    fw.tt(R1[:], V(la.t[:, d * 16:(d + 1) * 16].unsqueeze(2).to_broadcast([128, 16, 128]), la.whole),
                          V(tri.t[:, d:d + 1, :].to_broadcast([128, 16, 128]), tri.whole), ALU.mult, eng="pool")
                    for q in range(4):
                        bank = pb[2 + q % 2]
                        fw.mm(bank[:, :], tri[:, 4, :], V(R1.t[:, 4 * q:4 * q + 4, :].rearrange("p h n -> p (h n)"), R1.whole), True, True)
                        for hh in range(4):
                            h = 4 * q + hh
                            fw.ts(seg[:, h, :], bank[:, hh * 128:(hh + 1) * 128], cumraw[:, d * 16 + h:d * 16 + h + 1], ALU.subtract, 0.0, ALU.min)
                    fw.act(Dm[:], seg[:], AF.Exp)
                    for g in range(2):
                        fw.tt(Wt[:, g * 8:(g + 1) * 8, :], Dm[:, g * 8:(g + 1) * 8, :],
                              V(Sm[d].t[:, g:g + 1, :].to_broadcast([128, 8, 128]), Sm[d].whole), ALU.mult)
                    for h in range(16):
                        fw.mm(yb[h // 8][:, (h % 8) * 64:(h % 8 + 1) * 64], Wt[:, h, :], v[d][:, h * 64:(h + 1) * 64],
                              start=(d == 0 and h % 8 == 0), stop=(d == 1 and h % 8 == 7), last=(d == 1 and h % 8 == 7), skip_group_check=True)
                for d in range(2):
                    for g in range(2):
                        fw.mm(pb[2 + g][:, :], BCT[:, 2 + g, :], Sbf[d][:, g * 512:(g + 1) * 512], True, True)
                    ysv = V(pb_t[:, 2:4, :].rearrange("p a (h n) -> p (a h) n", n=64), [pb[2].whole, pb[3].whole])
                    fw.tt(as16(yt if d == 0 else y2), ysv, h16(E[:, d * 16:(d + 1) * 16]), ALU.mult)
                fw.tt(yt[:], yt[:], y2[:], ALU.add)
                yv = V(pb_t[:, 6:8, :].rearrange("p a c -> p (a c)") if False else pb_t[:, 6:8, :], [pb[6].whole, pb[7].whole])
                fw.tt(V(yt.t[:].rearrange("p (a c) -> p a c", a=2), yt.whole), V(yt.t[:].rearrange("p (a c) -> p a c", a=2), yt.whole), yv, ALU.add)
                xs = V(U[0].t[:, 0:1024].rearrange("p (h n) -> p h n", n=64), U[0].whole)
                fw.tt(as16(y2), xs, h16(prm[:, ODD:ODD + 16]), ALU.mult, eng="pool")
                fw.tt(yt[:], yt[:], y2[:], ALU.add)
                fw.act(zt[:], zt[:], AF.Silu)
                fw.tt(yt[:], yt[:], zt[:], ALU.mult)
                banks = chunk_state(0)
                mulA(Sx[0], Sx[0], E[:, 64:80])
                fw.tt(V(Sx[0].t[:].rearrange("p (g c) -> p g c", g=2), Sx[0].whole), V(Sx[0].t[:].rearrange("p (g c) -> p g c", g=2), Sx[0].whole),
                      V(pb_t[:, 4:6, :], [pb[4].whole, pb[5].whole]), ALU.add)
                fw.act(y2[:], yt[:], AF.Square)
                fw.reduce(gss[:], V(y2.t[:].rearrange("p (g c) -> p g c", g=2), y2.whole), ALU.add)
                fw.act(gss[:], gss[:], AF.Sqrt, bias=eps_t[:], scale=1.0 / 512)
                fw.recip(gss[:], gss[:])
                fw.tt(V(ybf.t[:].rearrange("p (g c) -> p g c", g=2), ybf.whole), V(yt.t[:].rearrange("p (g c) -> p g c", g=2), yt.whole),
                      V(gss.t[:].unsqueeze(2).to_broadcast([128, 2, 512]), gss.whole), ALU.mult)
                to = pbh[1]
                tov = to.t[:, 0:1024].rearrange("p (a c) -> p a c", a=8)
                for a in range(8):
                    fw.transpose(V(tov[:, a, :], to.whole), ybf[:, a * 128:(a + 1) * 128], ident_bf[:], last=(a == 7))
                fw.tt(ytb[:], V(tov, to.whole), V(ssdn.t[:].unsqueeze(2).to_broadcast([128, 8, 128]), ssdn.whole), ALU.mult)
                fw.dma("sp", V(mixT_s.t[4:12, :, rows].rearrange("h p c -> p h c"), mixT_s.part(("ssd", lc)).res), ytb[:])
        fw.phase_reset()

    def phase_out(l, need_ctx):
        wout = fw.carve("wout", [128, 16, D], BF16)
        wst = fw.carve("wost", [128, 4, D], F32)
        mx = [fw.carve(f"mx{i}", [128, 16, 128], BF16) for i in range(2)]
        tmp = fw.carve("otmp", [128, 512], F32)
        for q in range(4):
            fw.dma("sp", V(wst.t[:], wst.whole),
                   V(wout_d.t[l, q * 512:(q + 1) * 512, :].rearrange("(f p) c -> p f c", p=128), wout_d.whole))
            fw.copy(wout[:, q * 4:(q + 1) * 4, :], wst[:], eng="pool")
        allmix = [r for r in mixT_s.parts.values()]
        for lc in (range(NLC) if need_ctx else range(NCH)):
            is_ctx = lc >= NCH
            m = mx[lc % 2]
            fw.dma("sp", m[:], V(mixT_s.t[:, :, lc * 128:(lc + 1) * 128].rearrange("f p c -> p f c"), allmix))
            for hh in range(2):
                bank = pb[(lc % 2) * 2 + hh]
                for fc in range(16):
                    fw.mm(bank[:, :], m[:, fc, :], wout[:, fc, hh * 512:(hh + 1) * 512], start=(fc == 0), stop=(fc == 15))
                cs = slice(hh * 512, (hh + 1) * 512)
                fw.tt(tmp[:], bank[:, :], gate[:, 1 if is_ctx else 0, cs], ALU.mult)
                dst = ctx_sb[:, lc - NCH, cs] if is_ctx else x_sb[:, lc, cs]
                fw.tt(dst, dst, tmp[:], ALU.add)
        fw.phase_reset()

    for l in range(depth):
        need_ctx = l < depth - 1
        adaln(l)
        layer_params(l)
        phase_proj(l, need_ctx)
        if stop_after == "t_proj": break
        phase_attn(l, need_ctx)
        if stop_after == "t_attn": break
        phase_halo(l)
        phase_ret(l, need_ctx)
        if stop_after == "t_ret": break
        phase_ssd(l, need_ctx)
        if stop_after == "t_ssd": break
        phase_out(l, need_ctx)
        if l == 0 and "x0" in dbg:
            fw.dma("sp", V(dbg["x0"].t.ap().rearrange("(c p) d -> p c d", p=128), dbg["x0"].whole), V(x_sb.t[:], x_sb.whole))
            fw.dma("sp", V(dbg["ctx0"].t.ap().rearrange("(c p) d -> p c d", p=128), dbg["ctx0"].whole), V(ctx_sb.t[:], ctx_sb.whole))

    if "qT" in dbg:
        fw.dma("sp", dbg["qT"][:], V(qT_s.t[:], [qT_s.part(c).res for c in range(NLC)]))
    if "kv0" in dbg:
        fw.dma("sp", dbg["kv0"][:], kv_dst[0][:, :])
    if "proj" in dbg:
        fw.dma("sp", dbg["proj"][:], V(proj_s.t[:], [r for r in proj_s.parts.values()]))
    if "mixT" in dbg:
        fw.dma("sp", dbg["mixT"][:], V(mixT_s.t[0:4], [r for r in mixT_s.parts.values()]))
    if "mixS" in dbg:
        fw.dma("sp", dbg["mixS"][:], V(mixT_s.t[4:12], [r for r in mixT_s.parts.values()]))
    if "mixR" in dbg:
        fw.dma("sp", dbg["mixR"][:], V(mixT_s.t[12:16], [r for r in mixT_s.parts.values()]))
    fw.dma("sp", V(out_d.t.ap().rearrange("(c p) d -> p c d", p=128), out_d.whole), V(x_sb.t[:], x_sb.whole))
    fw.wait_all("sp", [out_d[:]] + [V(b.t[:], b.whole) for b in dbg.values()])
    return nc, fw


def rope_tables():
    n_freq = 16
    inv_freq = (10000.0 ** (-np.arange(n_freq, dtype=np.float32) / n_freq)).astype(np.float32)
    pos = np.arange(8192)
    row = (pos // 64).astype(np.float32)
    col = (pos % 64).astype(np.float32)
    ang = np.concatenate([row[:, None] * inv_freq, col[:, None] * inv_freq], axis=-1).astype(np.float32)
    return np.cos(ang).astype(np.float32), np.sin(ang).astype(np.float32)


def const_tables():
    j = np.arange(128)[:, None]; i = np.arange(128)[None, :]
    tri = np.stack([(j <= i), (j >= i), (j > i), (j < i), np.ones((128, 128), bool)], axis=1).astype(np.float32)
    gf = np.array([1.0 - 2.0 ** -e for e in RET_EXP_F], np.float64)
    gb = np.array([1.0 - 2.0 ** -e for e in RET_EXP_B], np.float64)
    dif = (i - j).astype(np.float64)
    Dret = np.zeros((128, 4, 128), np.float64)
    for h in range(4):
        Dret[:, h, :] = np.where(dif > 0, gf[h] ** np.abs(dif), 0.0) + np.where(dif < 0, gb[h] ** np.abs(dif), 0.0) + np.where(dif == 0, 2.0, 0.0)
    Dret *= 0.125
    pos = np.arange(128, dtype=np.float64)[:, None]
    te_f = gf[None, :] ** (127 - pos) * 0.125
    te_b = gb[None, :] ** pos * 0.125
    qsc_f = gf[None, :] ** (pos + 1)
    qsc_b = gb[None, :] ** (128 - pos)
    a_f = np.broadcast_to(gf[None, :] ** 128, (128, 4)); a_b = np.broadcast_to(gb[None, :] ** 128, (128, 4))
    rett = np.concatenate([Dret.reshape(128, 512), te_f, te_b, qsc_f, qsc_b, a_f, a_b], axis=1).astype(np.float32)
    return np.ascontiguousarray(tri), np.ascontiguousarray(rett)


def make_inputs(inp):
    cos, sin = rope_tables()
    tri, rett = const_tables()
    ssdp = np.concatenate([inp["ssd_conv_w"].reshape(2, -1), inp["ssd_conv_b"], inp["ssd_dt_bias"].reshape(2, -1),
                           inp["ssd_a_log"].reshape(2, -1), inp["ssd_d"], inp["ret_norm"]], axis=1).astype(np.float32)
    ssdp = np.ascontiguousarray(np.broadcast_to(ssdp[:, None, :], (2, 128, ssdp.shape[1])))
    ssdn = np.ascontiguousarray(inp["ssd_norm"].reshape(2, 8, 128).transpose(0, 2, 1))
    rep = lambda a: np.ascontiguousarray(np.broadcast_to(a[:, None], (a.shape[0], 128) + a.shape[1:]))
    qkg = rep(np.stack([inp["attn_q_norm"], inp["attn_k_norm"]], axis=1))
    lamv = rep(np.stack([inp["lambda_q1"], inp["lambda_k1"], inp["lambda_q2"], inp["lambda_k2"]], axis=1))
    subln = np.ascontiguousarray(inp["attn_subln"][:, :, None])
    maps = []
    for core in range(8):
        b, t = core // 4, core % 4
        lo = t * TOK
        cc = np.stack([inp["c"][b].reshape(8, 128).T, inp["c_ctx"].reshape(8, 128).T], axis=-1)
        rp = np.stack([cos[lo:lo + TOK], sin[lo:lo + TOK]], axis=1)
        rp = rp.reshape(NCH, 128, 2, 32).transpose(1, 0, 2, 3)
        m = {
            "x": np.ascontiguousarray(inp["x"][b, lo:lo + TOK]),
            "ctx": np.ascontiguousarray(inp["ctx"][b]),
            "cc": np.ascontiguousarray(cc.astype(np.float32)),
            "w_ada": inp["w_ada"],
            "b_ada_f": np.ascontiguousarray(inp["b_ada"][:, :2 * D].reshape(2, 16, 128).transpose(0, 2, 1)),
            "b_ada": inp["b_ada"],
            "w_in": inp["w_in"], "w_out": inp["w_out"],
            "qkg": qkg, "lamv": lamv, "subln": subln,
            "rope": np.ascontiguousarray(rp),
            "tri": tri, "rett": rett, "ssdp": ssdp, "ssdn": ssdn,
            "cmask": np.ascontiguousarray(np.broadcast_to(np.array([float(s_ < t) for s_ in range(4)] + [float(s_ > t) for s_ in range(4)], np.float32)[None], (128, 8))),
        }
        maps.append(m)
    return maps


from concourse.bass_utils import run_bass_kernel_spmd


def kernel(**inputs):
    inp = {k: np.asarray(v) for k, v in inputs.items()}
    nc, _ = build(depth=2)
    maps = make_inputs(inp)
    res = run_bass_kernel_spmd(nc, maps, core_ids=list(range(8)))
    outs = [np.asarray(res.results[c]["out"]) for c in range(8)]
    return np.stack([np.concatenate(outs[0:4], 0), np.concatenate(outs[4:8], 0)]).astype(np.float32)
```

```python
import numpy as np
import concourse.bass as bass
import concourse.mybir as mybir

F32 = mybir.dt.float32
BF16 = mybir.dt.bfloat16
AF = mybir.ActivationFunctionType
ALU = mybir.AluOpType
AX = mybir.AxisListType


class Res:
    __slots__ = ("name", "w", "r")

    def __init__(self, name):
        self.name = name
        self.w = None
        self.r = {}


class V:
    __slots__ = ("ap", "res")

    def __init__(self, ap, res):
        self.ap = ap
        self.res = res if isinstance(res, (list, tuple)) else [res]


class Buf:
    def __init__(self, fw, name, t, nparts=1):
        self.fw = fw
        self.name = name
        self.t = t
        self.parts = {}
        self.whole = Res(name)

    def __getitem__(self, idx):
        return V(self.t[idx], self.whole)

    def part(self, key):
        if key not in self.parts:
            self.parts[key] = Res(f"{self.name}.{key}")
        return _PartView(self, self.parts[key])

    def ap(self):
        return self.t.ap()


class _PartView:
    def __init__(self, buf, res):
        self.buf = buf
        self.res = res

    def __getitem__(self, idx):
        return V(self.buf.t[idx], self.res)


class EngState:
    def __init__(self, name, eng, sem):
        self.name = name
        self.eng = eng
        self.sem = sem
        self.count = 0
        self.pending = False
        self.seen = {}
        self.seen_dma = {}


class FW:
    def __init__(self, nc, n_dma_sems=24, same_engine_sync=True):
        self.nc = nc
        self.same_engine_sync = same_engine_sync
        self.engs = {}
        for name, eng in (("pe", nc.tensor), ("dve", nc.vector), ("act", nc.scalar),
                          ("pool", nc.gpsimd), ("sp", nc.sync)):
            self.engs[name] = EngState(name, eng, nc.alloc_semaphore(f"s_{name}"))
        self.dma_sems = [nc.alloc_semaphore(f"s_dma{i}") for i in range(n_dma_sems)]
        self.dma_vals = [0] * n_dma_sems
        self.dma_next = 0
        self.n_inst = 0
        self.out_tokens = []
        self.cc_sem = None
        self.cc_val = 0

    def sbuf(self, name, shape, dtype):
        return Buf(self, name, self.nc.alloc_sbuf_tensor("sb_" + name, list(shape), dtype))

    def psum(self, name, shape, dtype=F32):
        return Buf(self, name, self.nc.alloc_psum_tensor("ps_" + name, list(shape), dtype))

    def dram(self, name, shape, dtype, kind="Internal", **kw):
        return Buf(self, name, self.nc.dram_tensor(name, list(shape), dtype, kind=kind, **kw))

    def _need(self, E, tok):
        if tok is None:
            return
        if tok[0] == "eng":
            _, e, c = tok
            if e == E.name:
                if not self.same_engine_sync or e == "pe":
                    return
            if E.seen.get(e, 0) >= c:
                return
            P = self.engs[e]
            assert c <= P.count, f"{E.name} waits on pending (never-incremented) {e} count {c} > {P.count}"
            E.eng.wait_ge(P.sem, c)
            E.seen[e] = c
        elif tok[0] == "cc":
            val = tok[1]
            if E.seen_dma.get("cc", 0) >= val:
                return
            E.eng.wait_ge(self.cc_sem, val)
            E.seen_dma["cc"] = val
        else:
            _, si, val = tok
            if E.seen_dma.get(si, 0) >= val:
                return
            E.eng.wait_ge(self.dma_sems[si], val)
            E.seen_dma[si] = val

    def _pre(self, E, reads, writes):
        for v in reads:
            for r in v.res:
                self._need(E, r.w)
        for v in writes:
            for r in v.res:
                self._need(E, r.w)
                for tok in r.r.values():
                    self._need(E, tok)

    def _post(self, tok, key, reads, writes):
        for v in reads:
            for r in v.res:
                r.r[key] = tok
        for v in writes:
            for r in v.res:
                r.w = tok
                r.r = {}

    def op(self, engname, fn, reads, writes, inc=True):
        E = self.engs[engname]
        self._pre(E, reads, writes)
        ins = fn(E.eng)
        self.n_inst += 1
        if inc:
            E.count += 1
            ins.then_inc(E.sem, 1)
            tok = ("eng", engname, E.count)
        else:
            tok = ("eng", engname, E.count + 1)
        self._post(tok, engname, reads, writes)
        return ins

    def dma(self, qname, out, in_, **kw):
        E = self.engs[qname]
        self._pre(E, [in_], [out])
        si = self.dma_next
        self.dma_next = (self.dma_next + 1) % len(self.dma_sems)
        if self.dma_vals[si] > 0:
            self._need(E, ("dma", si, self.dma_vals[si]))
        self.dma_vals[si] += 16
        ins = E.eng.dma_start(out=out.ap, in_=in_.ap, **kw)
        ins.then_inc(self.dma_sems[si], 16)
        self.n_inst += 1
        tok = ("dma", si, self.dma_vals[si])
        self._post(tok, f"dma{si}", [in_], [out])
        return tok

    def wait_all(self, engname, views):
        E = self.engs[engname]
        for v in views:
            for r in v.res:
                self._need(E, r.w)

    def mm(self, out, lhsT, rhs, start, stop, last=None, **kw):
        if last is None:
            last = stop
        return self.op("pe", lambda e: e.matmul(out.ap, lhsT.ap, rhs.ap, start=start, stop=stop, **kw),
                       [lhsT, rhs], [out], inc=last)

    def transpose(self, out, in_, ident, last=True):
        return self.op("pe", lambda e: e.transpose(out.ap, in_.ap, ident.ap), [in_, ident], [out], inc=last)

    def act(self, out, in_, func, bias=None, scale=1.0, accum_out=None, eng="act"):
        reads = [in_]
        kw = {}
        if bias is not None:
            if isinstance(bias, V):
                reads.append(bias)
                kw["bias"] = bias.ap
            else:
                kw["bias"] = bias
        if isinstance(scale, V):
            reads.append(scale)
            kw["scale"] = scale.ap
        else:
            kw["scale"] = scale
        writes = [out]
        if accum_out is not None:
            writes.append(accum_out)
            kw["accum_out"] = accum_out.ap
        return self.op(eng, lambda e: e.activation(out.ap, in_.ap, func, **kw), reads, writes)

    def tt(self, out, in0, in1, op, eng="dve"):
        return self.op(eng, lambda e: e.tensor_tensor(out.ap, in0.ap, in1.ap, op), [in0, in1], [out])

    def ts(self, out, in0, s1, op0, s2=None, op1=None, eng="dve", accum_out=None):
        reads = [in0]
        a1 = s1
        if isinstance(s1, V):
            reads.append(s1)
            a1 = s1.ap
        a2 = s2
        if isinstance(s2, V):
            reads.append(s2)
            a2 = s2.ap
        kw = {}
        writes = [out]
        if op1 is not None:
            kw["op1"] = op1
        if accum_out is not None:
            kw["accum_out"] = accum_out.ap
            writes.append(accum_out)
        return self.op(eng, lambda e: e.tensor_scalar(out.ap, in0.ap, a1, a2, op0, **kw), reads, writes)

    def stt(self, out, in0, scalar, in1, op0, op1, eng="dve"):
        reads = [in0, in1]
        a = scalar
        if isinstance(scalar, V):
            reads.append(scalar)
            a = scalar.ap
        return self.op(eng, lambda e: e.scalar_tensor_tensor(out.ap, in0.ap, a, in1.ap, op0, op1), reads, [out])

    def copy(self, out, in_, eng="dve"):
        if eng == "act":
            return self.op("act", lambda e: e.copy(out.ap, in_.ap), [in_], [out])
        return self.op(eng, lambda e: e.tensor_copy(out.ap, in_.ap), [in_], [out])

    def memset(self, out, val, eng="dve"):
        return self.op(eng, lambda e: e.memset(out.ap, val), [], [out])

    def reduce(self, out, in_, op, axis=AX.X, eng="dve"):
        return self.op(eng, lambda e: e.tensor_reduce(out.ap, in_.ap, axis, op), [in_], [out])

    def recip(self, out, in_):
        return self.op("dve", lambda e: e.reciprocal(out.ap, in_.ap), [in_], [out])

    def collective(self, kind, op, groups, in_, out):
        E = self.engs["pool"]
        self._pre(E, [in_], [out])
        if self.cc_sem is None:
            self.cc_sem = self.nc.alloc_semaphore("s_cc")
        self.cc_val += 1
        ins = E.eng.collective_compute(kind, op, replica_groups=groups, ins=[in_.ap], outs=[out.ap])
        ins.then_inc(self.cc_sem)
        self.n_inst += 1
        tok = ("cc", self.cc_val)
        self._post(tok, "cc", [in_], [out])
        return tok

    def make_arena(self, kbytes):
        self.arena_t = self.nc.alloc_sbuf_tensor("sb_arena", [128, kbytes * 256], F32)
        self.arena_words = kbytes * 256
        self.arena_off = 0
        self.arena_gen = 0

    def carve(self, name, shape, dtype):
        esz = 2 if dtype == BF16 else 4
        n = 1
        for s in shape[1:]:
            n *= s
        words = (n * esz + 3) // 4
        words = (words + 7) // 8 * 8
        assert self.arena_off + words <= self.arena_words, f"arena overflow for {name}: {self.arena_off}+{words}>{self.arena_words}"
        raw = self.arena_t[0:shape[0], self.arena_off:self.arena_off + words]
        self.arena_off += words
        ap = raw.bitcast(dtype) if dtype != F32 else raw
        ap = ap[:, 0:n]
        if len(shape) > 2:
            names = " ".join(f"d{i}" for i in range(1, len(shape)))
            kw = {f"d{i}": shape[i] for i in range(1, len(shape))}
            ap = ap.rearrange(f"p ({names}) -> p {names}", **kw)
        return Buf(self, f"{name}@{self.arena_gen}", _APHandle(ap))

    def barrier(self):
        for E in self.engs.values():
            for P in self.engs.values():
                if P is not E and P.count > 0:
                    self._need(E, ("eng", P.name, P.count))
            for si, val in enumerate(self.dma_vals):
                if val > 0:
                    self._need(E, ("dma", si, val))
            if self.cc_val > 0:
                self._need(E, ("cc", self.cc_val))

    def phase_reset(self):
        self.barrier()
        self.arena_off = 0
        self.arena_gen += 1


class _APHandle:
    def __init__(self, ap):
        self._ap = ap

    def __getitem__(self, idx):
        return self._ap[idx]

    def ap(self):
        return self._ap


import math
KCUT = 9

D = 1024
NCH = 16
TOK = 2048
NTOK = TOK + 256
NLC = 18
DIN = 6176
EPS = 1e-6
GROUPS = [[0, 1, 2, 3], [4, 5, 6, 7]]
SW = 1568
NP = 3 * 1536 + 1536 + 32 + 32 + 16 + 128
RET_EXP_F = (5.0, 6.0, 7.0, 8.0)
RET_EXP_B = (5.5, 6.5, 7.5, 8.5)
BLOCKS = [("aq", 0, 512), ("ak", 512, 512), ("av", 1024, 512), ("ag", 1536, 512),
          ("xbc0", 2048, 512), ("xbc1", 2560, 512), ("xbc2", 3072, 512), ("dtr", 3584, 32),
          ("z0", 3616, 512), ("z1", 4128, 512), ("rqk", 4640, 512), ("rv", 5152, 512), ("rg", 5664, 512)]


class Alt:
    def __init__(self, bufs, par):
        self.bufs, self.par = bufs, par

    @property
    def cur(self):
        return self.bufs[self.par[0] % len(self.bufs)]

    def __getitem__(self, idx):
        return self.cur[idx]

    @property
    def t(self):
        return self.cur.t

    @property
    def whole(self):
        return self.cur.whole


def lam_init_of(layer):
    return 0.8 - 0.6 * math.exp(-0.3 * layer)


def build(depth=2, debug=None, stop_after=None):
    debug = debug or {}
    nc = bass.Bass("TRN2", target_bir_lowering=False)
    fw = FW(nc, same_engine_sync=True)
    I = lambda n, s, d=F32: fw.dram(n, s, d, kind="ExternalInput")
    x_d = I("x", [TOK, D])
    ctx_d = I("ctx", [256, D])
    cc_d = I("cc", [128, 8, 2])
    wada_d = I("w_ada", [2, D, 3 * D])
    bada_f_d = I("b_ada_f", [2, 128, 16])
    bada_d = I("b_ada", [2, 3 * D])
    win_d = I("w_in", [2, D, DIN])
    wout_d = I("w_out", [2, 2 * D, D])
    qkg_d = I("qkg", [2, 128, 2, 64])
    lamv_d = I("lamv", [2, 128, 4, 64])
    subln_d = I("subln", [2, 128, 1])
    rope_d = I("rope", [128, NCH, 2, 32])
    tri_d = I("tri", [128, 5, 128])
    rett_d = I("rett", [128, 4 * 128 + 24])
    ssdp_d = I("ssdp", [2, 128, NP])
    ssdn_d = I("ssdn", [2, 128, 8])
    cmask_d = I("cmask", [128, 8])
    out_d = fw.dram("out", [TOK, D], F32, kind="ExternalOutput")
    dbg = {k: fw.dram("dbg_" + k, shape, dt_, kind="ExternalOutput") for k, (shape, dt_) in debug.items()}

    proj_s = fw.dram("proj_s", [NTOK, DIN], F32)
    qT_s = fw.dram("qT_s", [4, 128, NTOK], BF16)
    agT_s = fw.dram("agT_s", [4, 128, NTOK], BF16)
    mixT_s = fw.dram("mixT_s", [16, 128, NTOK], BF16)
    kc_s = fw.dram("kc_s", [4, 128, 256], BF16)
    vc_s = fw.dram("vc_s", [256, 512], BF16)
    xbc_pad = fw.dram("xbc_pad", [TOK + 2, 1536], F32)
    xbc_cpad = fw.dram("xbc_cpad", [258, 1536], F32)
    hx_stage = fw.dram("hx_stage", [2, 1536], F32)
    hx_src = fw.dram("hx_src", [8, 1536], F32)
    hx_dst = fw.dram("hx_dst", [8, 1536], F32)
    hxL = fw.dram("hxL", [9, 1536], F32)
    hxR = fw.dram("hxR", [8, 1536], F32)
    cst_s = fw.dram("cst_s", [NLC, 2, 128, SW], F32)
    sb_s = fw.dram("sb_s", [NLC, 128, 1536], BF16)
    pc_h = fw.dram("pc_h", [NLC, 128, 3584], BF16)
    pc_f = fw.dram("pc_f", [NLC, 128, 1152], F32)
    rctx_s = fw.dram("rctx_s", [2, 64, 512], F32)
    st_stage = [fw.dram(f"st_stage{d}", [128, SW], F32) for d in range(2)]
    st_src = [fw.dram(f"st_src{d}", [512, SW], F32) for d in range(2)]
    st_dst = [fw.dram(f"st_dst{d}", [512, SW], F32) for d in range(2)]
    kv_src = [fw.dram(f"kv_src{h}", [512, 4096], BF16) for h in range(4)]
    kv_dst = [fw.dram(f"kv_dst{h}", [512, 4096], BF16) for h in range(4)]
    kv_stage = [fw.dram(f"kv_stage{h}", [128, 4096], BF16) for h in range(4)]

    x_sb = fw.sbuf("x_sb", [128, NCH, D], F32)
    ctx_sb = fw.sbuf("ctx_sb", [128, 2, D], F32)
    gate = fw.sbuf("gate", [128, 2, D], F32)
    sc1 = fw.sbuf("sc1", [128, 2, 8], F32)
    sh = fw.sbuf("sh", [128, 2, 8], F32)
    ident = fw.sbuf("ident", [128, 128], F32)
    ident_bf = fw.sbuf("ident_bf", [128, 128], BF16)
    ones_bf = fw.sbuf("ones_bf", [128, 128], BF16)
    eps_t = fw.sbuf("eps_t", [128, 1], F32)
    cc = fw.sbuf("cc", [128, 8, 2], F32)
    rope = fw.sbuf("rope", [128, NCH, 2, 32], F32)
    qkg = fw.sbuf("qkg", [128, 2, 64], F32)
    lamv = fw.sbuf("lamv", [128, 4, 64], F32)
    neglam = fw.sbuf("neglam", [128, 1], F32)
    subln = fw.sbuf("subln", [128, 1], F32)
    small = fw.sbuf("small", [128, 64], F32)
    cmask = fw.sbuf("cmask", [128, 8], F32)
    fw.make_arena(119)
    pb_t = nc.alloc_psum_tensor("ps_banks", [128, 8, 512], F32)
    pb = [Buf(fw, f"pb{i}", _APHandle(pb_t[:, i, :])) for i in range(8)]
    pbh = [Buf(fw, f"pbh{i}", _APHandle(pb_t[:, i, :].bitcast(BF16))) for i in range(8)]
    for i in range(8):
        pbh[i].whole = pb[i].whole
    rank = nc.partition_id() % 4

    fw.memset(ident[:], 1.0, eng="pool")
    fw.op("pool", lambda e: e.affine_select(ident.t[:], ident.t[:], [[-1, 128]], ALU.is_equal, 0.0,
                                             base=0, channel_multiplier=1), [ident[:]], [ident[:]])
    fw.copy(ident_bf[:], ident[:])
    fw.memset(ones_bf[:], 1.0)
    fw.memset(eps_t[:], EPS)
    fw.dma("sp", V(x_sb.t[:], x_sb.whole), V(x_d.t.ap().rearrange("(c p) d -> p c d", p=128), x_d.whole))
    fw.dma("sp", V(ctx_sb.t[:], ctx_sb.whole), V(ctx_d.t.ap().rearrange("(c p) d -> p c d", p=128), ctx_d.whole))
    fw.dma("sp", cc[:], cc_d[:])
    fw.dma("sp", rope[:], rope_d[:])
    fw.dma("sp", cmask[:], cmask_d[:])
    fw.act(cc[:], cc[:], AF.Silu)
    zt = fw.carve("zt", [128, 4096], BF16)
    fw.memset(zt[:], 0.0)
    for h in range(4):
        fw.dma("sp", V(kv_src[h].t.ap().rearrange("(r p) c -> p r c", p=128), kv_src[h].whole),
               V(zt.t[:].unsqueeze(1).to_broadcast([128, 4, 4096]), zt.whole))
    zf = fw.carve("zf", [128, SW], F32)
    fw.memset(zf[:], 0.0)
    fw.dma("sp", xbc_cpad[0:1, :], zf[0:1, 0:1536])
    fw.dma("sp", xbc_cpad[257:258, :], zf[0:1, 0:1536])
    fw.dma("sp", hx_src[:, :], zf[0:8, 0:1536])
    fw.dma("sp", hxL[:, :], zf[0:9, 0:1536])
    fw.dma("sp", hxR[:, :], zf[0:8, 0:1536])
    for d in range(2):
        fw.dma("sp", V(st_src[d].t.ap().rearrange("(r p) c -> p r c", p=128), st_src[d].whole),
               V(zf.t[:].unsqueeze(1).to_broadcast([128, 4, SW]), zf.whole))
        fw.dma("sp", st_stage[d][:, :], zf[:, :])
    fw.phase_reset()

    def adaln(l):
        ccrep = fw.carve("ccrep", [128, 8, 2, 128], F32)
        badaf = fw.carve("badaf", [128, 16], F32)
        gbias = fw.carve("gbias", [128, D], F32)
        wada_sb = fw.carve("wada_sb", [128, 8, 512], F32)
        fw.copy(ccrep[:], V(cc.t[:].unsqueeze(3).to_broadcast([128, 8, 2, 128]), cc.whole))
        fw.dma("sp", badaf[:], bada_f_d[l])
        fw.dma("sp", gbias[:], V(bada_d.t[l:l + 1, 2 * D:3 * D].partition_broadcast(128), bada_d.whole))
        ps_s = V(pb[2].t[:, 0:32].rearrange("p (a b) -> p a b", b=2), pb[2].whole)
        for piece in range(6):
            fw.dma("sp", V(wada_sb.t[:], wada_sb.whole),
                   V(wada_d.t[l, :, piece * 512:(piece + 1) * 512].rearrange("(k p) c -> p k c", p=128), wada_d.whole))
            if piece < 4:
                for j in range(4):
                    blk = piece * 4 + j
                    for k in range(8):
                        fw.mm(V(ps_s.ap[:, blk, :], ps_s.res), wada_sb[:, k, j * 128:(j + 1) * 128], cc[:, k, :],
                              start=(k == 0), stop=(k == 7))
            else:
                half = piece - 4
                for v in range(2):
                    for k in range(8):
                        fw.mm(pb[3][:, :], ccrep[:, k, v, :], wada_sb[:, k, :], start=(k == 0), stop=(k == 7))
                    fw.tt(gate[:, v, half * 512:(half + 1) * 512], pb[3][:, :], gbias[:, half * 512:(half + 1) * 512], ALU.add)
        for v in range(2):
            fw.tt(sh[:, v, :], V(ps_s.ap[:, 0:8, v], ps_s.res), badaf[:, 0:8], ALU.add)
            fw.tt(sc1[:, v, :], V(ps_s.ap[:, 8:16, v], ps_s.res), badaf[:, 8:16], ALU.add)
        fw.ts(sc1[:], sc1[:], 1.0, ALU.add)
        fw.phase_reset()

    def layer_params(l):
        fw.dma("sp", qkg[:], qkg_d[l])
        fw.dma("sp", lamv[:], lamv_d[l])
        fw.dma("sp", subln[:], subln_d[l])
        fw.tt(small[:, 0:64], lamv[:, 0, :], lamv[:, 1, :], ALU.mult)
        s1 = fw.sbuf(f"lam_s1_{l}", [128, 1], F32)
        s2 = fw.sbuf(f"lam_s2_{l}", [128, 1], F32)
        fw.reduce(s1[:], small[:, 0:64], ALU.add)
        fw.tt(small[:, 0:64], lamv[:, 2, :], lamv[:, 3, :], ALU.mult)
        fw.reduce(s2[:], small[:, 0:64], ALU.add)
        fw.act(s1[:], s1[:], AF.Exp)
        fw.act(s2[:], s2[:], AF.Exp)
        fw.tt(neglam[:], s2[:], s1[:], ALU.subtract)
        fw.ts(neglam[:], neglam[:], -lam_init_of(l), ALU.add)
        fw.ts(subln[:], subln[:], 1.0 - lam_init_of(l), ALU.mult)

    def phase_proj(l, need_ctx_q):
        hT = fw.carve("hT", [128, 8, NTOK], BF16)
        par = [0]
        alt = lambda n, sh, dt_: Alt([fw.carve(f"{n}_{i}", sh, dt_) for i in range(2)], par)
        xn = alt("xn", [128, D], F32)
        junk = alt("junk", [128, D], F32)
        ss = alt("ss", [128, 1], F32)
        rs = alt("rs", [128, 1], F32)
        wblk = [fw.carve(f"wblk{i}", [128, 8, 512], BF16) for i in range(2)]
        wst = fw.carve("wst", [128, 8, 512], F32)
        stage = [fw.carve(f"stage{i}", [128, 512], F32) for i in range(3)]
        sq = alt("sq", [128, 8, 64], F32)
        qn = alt("qn", [128, 8, 64], F32)
        t1 = alt("t1", [128, 8, 32], F32)
        t2 = alt("t2", [128, 8, 32], F32)
        ss8 = alt("ss8", [128, 8], F32)
        qbf = [fw.carve(f"qbf{i}", [128, 8, 64], BF16) for i in range(2)]
        tb = [fw.carve(f"tb{i}", [128, 4, 128], BF16) for i in range(2)]
        vbf = [fw.carve(f"vbf{i}", [128, 512], BF16) for i in range(2)]

        for lc in range(NLC):
            par[0] = lc
            src = x_sb[:, lc, :] if lc < NCH else ctx_sb[:, lc - NCH, :]
            v = 0 if lc < NCH else 1
            fw.act(junk[:], src, AF.Square, accum_out=ss[:])
            fw.act(rs[:], ss[:], AF.Sqrt, bias=eps_t[:], scale=1.0 / D)
            fw.recip(rs[:], rs[:])
            fw.ts(xn[:], src, rs[:], ALU.mult)
            pt = V(pb[lc % 2 * 2].t[:, :], [pb[lc % 2 * 2].whole, pb[lc % 2 * 2 + 1].whole])
            ptt = pb_t[:, lc % 2 * 2:lc % 2 * 2 + 2, :].rearrange("p a (k c) -> p (a k) c", c=128)
            for k in range(8):
                fw.transpose(V(ptt[:, k, :], pt.res), xn[:, k * 128:(k + 1) * 128], ident[:], last=(k == 7))
            for k in range(8):
                fw.ts(hT[:, k, lc * 128:(lc + 1) * 128], V(ptt[:, k, :], pt.res), sc1[:, v, k:k + 1], ALU.mult,
                      sh[:, v, k:k + 1], ALU.add)
        if "hT" in dbg:
            fw.dma("sp", V(dbg["hT"].t[:], dbg["hT"].whole), V(hT.t[:], hT.whole))

        it = 0
        for bi, (bname, col0, ncols) in enumerate(BLOCKS):
            wb = wblk[bi % 2]
            fw.dma("sp", V(wst.t[:, :, 0:ncols], wst.whole),
                   V(win_d.t[l, :, col0:col0 + ncols].rearrange("(k p) c -> p k c", p=128), win_d.whole))
            fw.copy(V(wb.t[:, :, 0:ncols], wb.whole), V(wst.t[:, :, 0:ncols], wst.whole), eng="pool")
            for lc in range(NLC):
                is_ctx = lc >= NCH
                bank = pb[4 + it % 2]
                it += 1
                par[0] = it
                for k in range(8):
                    fw.mm(bank[:, 0:ncols], hT[:, k, lc * 128:(lc + 1) * 128], wb[:, k, 0:ncols],
                          start=(k == 0), stop=(k == 7))
                rows = slice(lc * 128, (lc + 1) * 128)
                if bname in ("aq", "ak"):
                    if bname == "aq" and is_ctx and not need_ctx_q:
                        continue
                    gi = 0 if bname == "aq" else 1
                    psv = V(bank.t[:, :].rearrange("p (a b) -> p a b", b=64), bank.whole)
                    fw.act(sq[:], psv, AF.Square)
                    fw.reduce(ss8[:], sq[:], ALU.add)
                    fw.act(ss8[:], ss8[:], AF.Sqrt, bias=eps_t[:], scale=1.0 / 64)
                    fw.recip(ss8[:], ss8[:])
                    fw.tt(qn[:], psv, V(ss8.t[:].unsqueeze(2).to_broadcast([128, 8, 64]), ss8.whole), ALU.mult)
                    fw.tt(qn[:], qn[:], V(qkg.t[:, gi:gi + 1, :].to_broadcast([128, 8, 64]), qkg.whole), ALU.mult, eng="pool")
                    qo = qbf[it % 2]
                    if not is_ctx:
                        cosb = V(rope.t[:, lc, 0:1, :].to_broadcast([128, 8, 32]), rope.whole)
                        sinb = V(rope.t[:, lc, 1:2, :].to_broadcast([128, 8, 32]), rope.whole)
                        fw.tt(t1[:], qn[:, :, 0:32], cosb, ALU.mult)
                        fw.tt(t2[:], qn[:, :, 32:64], sinb, ALU.mult, eng="pool")
                        fw.tt(qo[:, :, 0:32], t1[:], t2[:], ALU.subtract)
                        fw.tt(t1[:], qn[:, :, 0:32], sinb, ALU.mult)
                        fw.tt(t2[:], qn[:, :, 32:64], cosb, ALU.mult, eng="pool")
                        fw.tt(qo[:, :, 32:64], t1[:], t2[:], ALU.add)
                    else:
                        fw.copy(qo[:], qn[:])
                    tbank = pbh[6 + lc % 2]
                    tbv = tbank.t[:, 0:512].rearrange("p (h c) -> p h c", c=128)
                    qof = qo.t[:].rearrange("p a b -> p (a b)")
                    for hd in range(4):
                        fw.transpose(V(tbv[:, hd, :], tbank.whole), V(qof[:, hd * 128:(hd + 1) * 128], qo.whole), ident_bf[:], last=(hd == 3))
                    tbs = tb[lc % 2]
                    fw.copy(tbs[:], V(tbv, tbank.whole), eng="act")
                    if bname == "aq":
                        fw.dma("act", V(qT_s.t[:, :, rows].rearrange("h p c -> p h c"), qT_s.part(lc).res), tbs[:])
                    elif is_ctx:
                        c0 = (lc - NCH) * 128
                        fw.dma("act", V(kc_s.t[:, :, c0:c0 + 128].rearrange("h p c -> p h c"), kc_s.part(lc).res), tbs[:])
                    else:
                        for hd in range(4):
                            fw.dma("act", V(kv_stage[hd].t[:, lc * 128:(lc + 1) * 128], kv_stage[hd].part(("k", lc)).res),
                                   tbs[:, hd, :])
                elif bname == "av":
                    vb = vbf[lc % 2]
                    fw.copy(vb[:], bank[:, :], eng="act")
                    if is_ctx:
                        c0 = (lc - NCH) * 128
                        fw.dma("act", V(vc_s.t[c0:c0 + 128, :], vc_s.part(lc).res), vb[:])
                    else:
                        for hd in range(4):
                            fw.dma("act", V(kv_stage[hd].t[:, 2048 + lc * 128:2048 + (lc + 1) * 128],
                                            kv_stage[hd].part(("v", lc)).res), vb[:, hd * 128:(hd + 1) * 128])
                elif bname == "ag":
                    st = stage[lc % 3]
                    fw.act(st[:], bank[:, :], AF.Silu)
                    tbank = pb[6 + lc % 2]
                    for hd in range(4):
                        fw.transpose(tbank[:, hd * 128:(hd + 1) * 128], st[:, hd * 128:(hd + 1) * 128], ident[:], last=(hd == 3))
                    tbs = tb[lc % 2]
                    fw.copy(V(tbs.t[:].rearrange("p h c -> p (h c)"), tbs.whole), tbank[:, :])
                    fw.dma("act", V(agT_s.t[:, :, rows].rearrange("h p c -> p h c"), agT_s.part(lc).res), tbs[:])
                else:
                    st = stage[lc % 3]
                    if lc % 2 == 0:
                        fw.copy(st[:, 0:ncols], bank[:, 0:ncols])
                    else:
                        fw.copy(st[:, 0:ncols], bank[:, 0:ncols], eng="act")
                    if bname.startswith("xbc"):
                        xc = (int(bname[3]) * 512)
                        if is_ctx:
                            r1 = 1 + (lc - NCH) * 128
                            fw.dma("sp", V(xbc_cpad.t[r1:r1 + 128, xc:xc + 512], xbc_cpad.part((bname, lc)).res), st[:, 0:ncols])
                        else:
                            r1 = 1 + lc * 128
                            fw.dma("sp", V(xbc_pad.t[r1:r1 + 128, xc:xc + 512], xbc_pad.part((bname, lc)).res), st[:, 0:ncols])
                    else:
                        fw.dma("sp", V(proj_s.t[rows, col0:col0 + ncols], proj_s.part((bname, lc)).res), st[:, 0:ncols])
            if bname == "av":
                for hd in range(4):
                    allres = [kv_stage[hd].part(("k", c)).res for c in range(NCH)] + [kv_stage[hd].part(("v", c)).res for c in range(NCH)]
                    fw.dma("sp", V(kv_src[hd].t[bass.ds(rank * 128, 128), :], kv_src[hd].whole), V(kv_stage[hd].t[:, :], allres))
                    fw.collective("AllReduce", ALU.add, GROUPS, kv_src[hd][:, :], kv_dst[hd][:, :])
        fw.phase_reset()

    def phase_attn(l, with_ctx_q):
        kTb = [fw.carve(f"kT{i}", [128, 8448], BF16) for i in range(2)]
        vvb = [fw.carve(f"vv{i}", [128, 66, 128], BF16) for i in range(2)]
        qhb = [fw.carve(f"qh{i}", [128, NTOK], BF16) for i in range(2)]
        aghb = [fw.carve(f"agh{i}", [128, NTOK], BF16) for i in range(2)]
        rec = fw.carve("rec", [128, 512], F32)
        om = [fw.carve(f"om{i}", [128, 512], F32) for i in range(2)]
        A = fw.carve("A", [128, 512], F32)
        sqb = fw.carve("sqb", [128, 512], BF16)
        rstd = fw.carve("rstd", [128, 512], F32)
        mixo = [fw.carve(f"mixo{i}", [128, 512], BF16) for i in range(2)]
        ones_f = fw.carve("ones_f", [128, 128], F32)
        fw.memset(ones_f[:], 1.0)
        qblocks = [(q0, 512, 0, 66) for q0 in range(0, TOK, 512)]
        if with_ctx_q:
            qblocks.append((TOK, 256, 64, 66))
        sbank = (pb[0], pb[1], pb[7])
        nq_all = NTOK if with_ctx_q else TOK

        def load_head(hd):
            kT, vv, qh, agh = kTb[hd % 2], vvb[hd % 2], qhb[hd % 2], aghb[hd % 2]
            fw.dma("sp", V(kT.t[:, 0:8192].rearrange("p (r c) -> p r c", r=4), kT.whole),
                   V(kv_dst[hd].t[:, 0:2048].rearrange("(r p) c -> p r c", p=128), kv_dst[hd].whole))
            fw.dma("sp", kT[:, 8192:8448], V(kc_s.t[hd], [kc_s.part(16).res, kc_s.part(17).res]))
            fw.dma("sp", V(vv.t[:, 0:64, :].rearrange("p (r c) e -> p r c e", r=4), vv.whole),
                   V(kv_dst[hd].t[:, 2048:4096].rearrange("(r p) (c e) -> p r c e", p=128, e=128), kv_dst[hd].whole))
            fw.dma("sp", vv[:, 64:66, :], V(vc_s.t[:, hd * 128:(hd + 1) * 128].rearrange("(c p) e -> p c e", p=128),
                                           [vc_s.part(16).res, vc_s.part(17).res]))
            qres = [qT_s.part(c).res for c in range(NLC if with_ctx_q else NCH)]
            fw.dma("sp", qh[:, 0:nq_all], V(qT_s.t[hd, :, 0:nq_all], qres))
            fw.dma("sp", agh[:, 0:NTOK], V(agT_s.t[hd], [agT_s.part(c).res for c in range(NLC)]))

        load_head(0)
        pT2 = [fw.carve(f"pTT{i}", [128, 2, 512], BF16) for i in range(3)]
        acc2 = [fw.carve(f"accT{i}", [128, 2, 512], F32) for i in range(2)]
        stage_banks = ((0, 1), (4, 5))
        for hd in range(4):
            if hd + 1 < 4:
                load_head(hd + 1)
            kT, vv, qh, agh = kTb[hd % 2], vvb[hd % 2], qhb[hd % 2], aghb[hd % 2]
            for qi, (q0, nq_, kc0, kc1) in enumerate(qblocks):
                kcs = list(range(kc0, kc1))
                n = len(kcs)

                def qk(i):
                    kc = kcs[i]
                    b0, b1 = stage_banks[i % 2]
                    for m, bk in ((0, b0), (1, b1)):
                        fw.mm(pb[bk][:, 0:nq_], kT[m * 64:(m + 1) * 64, kc * 128:(kc + 1) * 128], qh[m * 64:(m + 1) * 64, q0:q0 + nq_], True, True)

                qk(0)
                for i, kc in enumerate(kcs):
                    if i + 1 < n:
                        qk(i + 1)
                    b0, b1 = stage_banks[i % 2]
                    p = pT2[i % 3]
                    sc2 = V(pb_t[:, b0:b0 + 2, 0:nq_], [pb[b0].whole, pb[b1].whole])
                    fw.act(p[:, :, 0:nq_], sc2, AF.Exp, scale=0.125)
                    first, lastk = (i == 0), (i == n - 1)
                    for m in range(2):
                        fw.mm(pb[2 + m][:, 0:nq_], vv[:, kc, :], p[:, m, 0:nq_], start=first, stop=lastk)
                    eng_ = "dve" if i % 2 == 0 else "pool"
                    acc_ = acc2[i % 2]
                    if i < 2:
                        fw.copy(acc_[:, :, 0:nq_], p[:, :, 0:nq_], eng=eng_)
                    else:
                        fw.tt(acc_[:, :, 0:nq_], acc_[:, :, 0:nq_], p[:, :, 0:nq_], ALU.add, eng=eng_)
                if n > 1:
                    fw.tt(acc2[0][:, :, 0:nq_], acc2[0][:, :, 0:nq_], acc2[1][:, :, 0:nq_], ALU.add)
                for m in range(2):
                    fw.mm(pb[7][:, 0:nq_], ones_f[:], acc2[0][:, m, 0:nq_], True, True)
                    fw.recip(rec[:, 0:nq_], pb[7][:, 0:nq_])
                    fw.tt(om[m][:, 0:nq_], pb[2 + m][:, 0:nq_], rec[:, 0:nq_], ALU.mult)
                fw.stt(A[:, 0:nq_], om[1][:, 0:nq_], neglam[:], om[0][:, 0:nq_], ALU.mult, ALU.add)
                fw.act(sqb[:, 0:nq_], A[:, 0:nq_], AF.Square)
                fw.mm(pb[6][:, 0:nq_], ones_bf[:], sqb[:, 0:nq_], True, True)
                fw.act(rstd[:, 0:nq_], pb[6][:, 0:nq_], AF.Sqrt, bias=eps_t[:], scale=1.0 / 128)
                fw.recip(rstd[:, 0:nq_], rstd[:, 0:nq_])
                fw.stt(A[:, 0:nq_], A[:, 0:nq_], subln[:], rstd[:, 0:nq_], ALU.mult, ALU.mult)
                mo = mixo[qi % 2]
                fw.tt(mo[:, 0:nq_], A[:, 0:nq_], agh[:, q0:q0 + nq_], ALU.mult, eng="pool")
                fw.dma("act", V(mixT_s.t[hd, :, q0:q0 + nq_], mixT_s.part((hd, qi)).res), mo[:, 0:nq_])
        fw.phase_reset()

    def phase_halo(l):
        xr = lambda c: [xbc_pad.part((f"xbc{i}", c)).res for i in range(3)]
        fw.dma("sp", hx_stage[0:1, :], V(xbc_pad.t[1:2, :], xr(0)))
        fw.dma("sp", hx_stage[1:2, :], V(xbc_pad.t[TOK:TOK + 1, :], xr(NCH - 1)))
        fw.dma("sp", V(hx_src.t[bass.ds(rank * 2, 2), :], hx_src.whole), hx_stage[:, :])
        fw.collective("AllReduce", ALU.add, GROUPS, hx_src[:, :], hx_dst[:, :])
        fw.dma("sp", hxL[1:9, :], hx_dst[:, :])
        fw.dma("sp", hxR[0:6, :], hx_dst[2:8, :])
        fw.dma("sp", V(xbc_pad.t[0:1, :], xbc_pad.part("hl").res), V(hxL.t[bass.ds(rank * 2, 1), :], hxL.whole))
        fw.dma("sp", V(xbc_pad.t[TOK + 1:TOK + 2, :], xbc_pad.part("hr").res), V(hxR.t[bass.ds(rank * 2, 1), :], hxR.whole))

    def rope_apply(out, x, lc, nh, t1, t2):
        cosb = V(rope.t[:, lc, 0:1, :].to_broadcast([128, nh, 32]), rope.whole)
        sinb = V(rope.t[:, lc, 1:2, :].to_broadcast([128, nh, 32]), rope.whole)
        fw.tt(t1[:, 0:nh, :], x[:, :, 0:32], cosb, ALU.mult)
        fw.tt(t2[:, 0:nh, :], x[:, :, 32:64], sinb, ALU.mult, eng="pool")
        fw.tt(out[:, :, 0:32], t1[:, 0:nh, :], t2[:, 0:nh, :], ALU.subtract)
        fw.tt(t1[:, 0:nh, :], x[:, :, 0:32], sinb, ALU.mult)
        fw.tt(t2[:, 0:nh, :], x[:, :, 32:64], cosb, ALU.mult, eng="pool")
        fw.tt(out[:, :, 32:64], t1[:, 0:nh, :], t2[:, 0:nh, :], ALU.add)

    def chain_combine(Sin, Sctx, d, col0, ncol, nparts, Aexp_of, tmp, slot):
        fw.copy(Sin[0:nparts, :], Sctx[0:nparts, :])
        order = range(4) if d == 0 else range(3, -1, -1)
        for sidx in order:
            fw.dma("sp", slot[0:nparts, 0:ncol], st_dst[d][sidx * 128:sidx * 128 + nparts, col0:col0 + ncol])
            Aexp_of(sidx, tmp)
            fw.tt(tmp[0:nparts, :], tmp[0:nparts, :], slot[0:nparts, 0:ncol], ALU.add)
            fw.tt(tmp[0:nparts, :], tmp[0:nparts, :], Sin[0:nparts, :], ALU.subtract)
            mcol = cmask[:, d * 4 + sidx:d * 4 + sidx + 1]
            fw.stt(Sin[0:nparts, :], tmp[0:nparts, :], V(mcol.ap[0:nparts], mcol.res), Sin[0:nparts, :], ALU.mult, ALU.add)

    def phase_ret(l, need_ctx, part):
        rett = fw.carve("rett", [128, 4 * 128 + 24], F32)
        rnorm = fw.carve("rnorm", [128, 128], F32)
        fw.dma("sp", rett[:], rett_d[:])
        fw.dma("sp", rnorm[:], ssdp_d[l, :, NP - 128:NP])
        Dret = V(rett.t[:, 0:512].rearrange("p (h i) -> p h i", i=128), rett.whole)
        tab = lambda k: V(rett.t[:, 512 + 4 * k:512 + 4 * k + 4], rett.whole)
        par = [0]
        alt = lambda n, sh, dt_: Alt([fw.carve(f"{n}_{i}", sh, dt_) for i in range(2)], par)
        qk = alt("qk", [128, 8, 64], F32)
        qkr = alt("qkr", [128, 8, 64], F32)
        t1 = alt("rt1", [128, 8, 32], F32)
        t2 = alt("rt2", [128, 8, 32], F32)
        rv = alt("rv", [128, 512], F32)
        rvbf = alt("rvbf", [128, 512], BF16)
        kte = [alt(f"kte{d}", [128, 4, 64], BF16) for d in range(2)]
        q3 = alt("q3", [128, 3, 4, 64], BF16)
        kbf = alt("kbf", [128, 4, 64], BF16)
        qT = alt("qT", [64, 3, 4, 128], BF16)
        kT = alt("kT", [64, 4, 128], BF16)
        Wt = alt("Wt", [128, 4, 128], BF16)
        R = [fw.carve(f"R{d}", [64, 512], F32) for d in range(2)]
        Rbf = [fw.carve(f"Rbf{d}", [64, 512], BF16) for d in range(2)]
        Rctx = [fw.carve(f"Rctx{d}", [64, 512], F32) for d in range(2)]
        Pb = fw.carve("Pb", [128, 4], F32)
        cs_sb = [alt(f"cs_sb{d}", [64, 512], F32) for d in range(2)]
        tmp = fw.carve("rtmp", [64, 512], F32)
        slot = fw.carve("rslot", [64, 512], F32)
        rg = alt("rg", [128, 512], F32)
        ysq = alt("ysq", [128, 4, 128], F32)
        yss = alt("yss", [128, 4], F32)
        yn = alt("yn", [128, 4, 128], F32)
        ybf = alt("ybf", [128, 512], BF16)
        ytb = alt("ytb", [128, 4, 128], BF16)
        bc4 = lambda v, n: V(v.ap.unsqueeze(2).to_broadcast([v.ap.shape[0], 4, n]), v.res)

        def prep(lc):
            par[0] = lc
            is_ctx = lc >= NCH
            rows = slice(lc * 128, (lc + 1) * 128)
            pr = lambda n: proj_s.part((n, lc)).res
            fw.dma("sp", V(qk.t[:].rearrange("p a b -> p (a b)"), qk.whole), V(proj_s.t[rows, 4640:5152], pr("rqk")))
            fw.dma("sp", rv[:], V(proj_s.t[rows, 5152:5664], pr("rv")))
            if is_ctx:
                src = qk
            else:
                rope_apply(qkr, qk, lc, 8, t1, t2)
                src = qkr
            fw.copy(rvbf[:], rv[:], eng="pool")
            for d in range(2):
                fw.tt(kte[d][:], src[:, 4:8, :], bc4(tab(d), 64), ALU.mult)
            return src

        def chunk_states(start_banks=(0, 1)):
            for d in range(2):
                bank = pb[start_banks[d]]
                for h in range(4):
                    fw.mm(bank[0:64, h * 128:(h + 1) * 128], kte[d][:, h, :], rvbf[:, h * 128:(h + 1) * 128],
                          start=(h == 0), stop=(h == 3), skip_group_check=True)
            return [pb[start_banks[0]], pb[start_banks[1]]]

        A128 = [tab(4), tab(5)]

        def fold(acc, csb, lc, store):
            fw.tt(V(acc[0].t[:].rearrange("p (h n) -> p h n", n=128), acc[0].whole),
                  V(acc[0].t[:].rearrange("p (h n) -> p h n", n=128), acc[0].whole),
                  V(A128[0].ap[0:64].unsqueeze(2).to_broadcast([64, 4, 128]), A128[0].res), ALU.mult)
            fw.tt(acc[0][:], acc[0][:], csb[0][0:64, :], ALU.add)
            fw.tt(V(tmp.t[:].rearrange("p (h n) -> p h n", n=128), tmp.whole),
                  V(csb[1].t[0:64, :].rearrange("p (h n) -> p h n", n=128), csb[1].whole),
                  V(Pb.t[0:64, :].unsqueeze(2).to_broadcast([64, 4, 128]), Pb.whole), ALU.mult)
            fw.tt(acc[1][:], acc[1][:], tmp[:], ALU.add)
            fw.tt(Pb[:], Pb[:], A128[1], ALU.mult)
            if store:
                for d in range(2):
                    fw.copy(cs_sb[d][:], csb[d][0:64, :])
                    fw.dma("act", V(cst_s.t[lc, d, 0:64, 1024:1536], cst_s.part(("r", lc, d)).res), cs_sb[d][:])

        for grp in (((16, 17), tuple(range(NCH))) if part == "a" else ()):
            for d in range(2):
                fw.memset(R[d][:], 0.0)
            fw.memset(Pb[:], 1.0)
            for lc in grp:
                prep(lc)
                if KCUT >= 2:
                    csb = chunk_states()
                if KCUT >= 3:
                    fold(R, csb, lc, KCUT >= 4)
            if grp[0] == 16:
                for d in range(2):
                    fw.copy(Rctx[d][:], R[d][:])
        if "ret_sf" in dbg:
            fw.dma("sp", dbg["ret_sf"][:, :], Rctx[0][:])
            fw.dma("sp", dbg["ret_sb"][:, :], Rctx[1][:])
        if stop_after == "ret_p1":
            fw.phase_reset(); return
        if part == "a":
            for d in range(2):
                fw.dma("sp", st_stage[d][0:64, 1024:1536], R[d][:])
                fw.dma("sp", V(rctx_s.t[d], rctx_s.whole), Rctx[d][:])
            fw.phase_reset()
            return
        for d in range(2):
            fw.dma("sp", Rctx[d][:], V(rctx_s.t[d], rctx_s.whole))
        A2048 = [[(1.0 - 2.0 ** -e) ** 2048 for e in RET_EXP_F], [(1.0 - 2.0 ** -e) ** 2048 for e in RET_EXP_B]]
        Rin = [fw.carve(f"Rin{d}", [64, 512], F32) for d in range(2)]
        for d in range(2):
            def aexp(sidx, t_, d=d):
                for h in range(4):
                    fw.ts(t_[0:64, h * 128:(h + 1) * 128], Rin[d][0:64, h * 128:(h + 1) * 128], float(A2048[d][h]), ALU.mult)
            chain_combine(Rin[d], Rctx[d], d, 1024, 512, 64, aexp, tmp, slot)
        snap = fw.carve("rsnap", [64, 512], BF16)
        csl = fw.carve("rcsl", [64, 512], F32)
        fw.copy(R[1][:], Rin[1][:])
        for lc in range(NCH - 1, -1, -1):
            fw.copy(snap[:], R[1][:])
            fw.dma("act", V(sb_s.t[lc, 0:64, 1024:1536], sb_s.part(("r", lc)).res), snap[:])
            fw.dma("sp", csl[:], V(cst_s.t[lc, 1, 0:64, 1024:1536], cst_s.part(("r", lc, 1)).res))
            fw.tt(V(R[1].t[:].rearrange("p (h n) -> p h n", n=128), R[1].whole),
                  V(R[1].t[:].rearrange("p (h n) -> p h n", n=128), R[1].whole),
                  V(A128[1].ap[0:64].unsqueeze(2).to_broadcast([64, 4, 128]), A128[1].res), ALU.mult)
            fw.tt(R[1][:], R[1][:], csl[:], ALU.add)
        if need_ctx:
            fw.dma("sp", csl[:], V(cst_s.t[17, 1, 0:64, 1024:1536], cst_s.part(("r", 17, 1)).res))
            fw.copy(snap[:], csl[:])
            fw.dma("sp", V(sb_s.t[16, 0:64, 1024:1536], sb_s.part(("r", 16)).res), snap[:])
            snap0 = fw.carve("rsnap0", [64, 512], BF16)
            fw.memset(snap0[:], 0.0)
            fw.dma("sp", V(sb_s.t[17, 0:64, 1024:1536], sb_s.part(("r", 17)).res), snap0[:])
        if stop_after == "ret_p2":
            fw.phase_reset(); return
        groups3 = [tuple(range(NCH))] + ([(16, 17)] if need_ctx else [])
        for grp in groups3:
            if grp[0] == 16:
                fw.memset(R[0][:], 0.0)
            else:
                fw.copy(R[0][:], Rin[0][:])
            for lc in grp:
                is_ctx = lc >= NCH
                rows = slice(lc * 128, (lc + 1) * 128)
                src = prep(lc)
                fw.dma("sp", rg[:], V(proj_s.t[rows, 5664:6176], proj_s.part(("rg", lc)).res))
                fw.dma("sp", Rbf[1][:], V(sb_s.t[lc, 0:64, 1024:1536], sb_s.part(("r", lc)).res))
                fw.copy(Rbf[0][:], R[0][:])
                fw.copy(q3[:, 0, :, :], src[:, 0:4, :])
                fw.tt(q3[:, 1, :, :], src[:, 0:4, :], bc4(tab(2), 64), ALU.mult)
                fw.tt(q3[:, 2, :, :], src[:, 0:4, :], bc4(tab(3), 64), ALU.mult, eng="pool")
                fw.copy(kbf[:], src[:, 4:8, :])
                tqa = pbh[2]
                tqav = tqa.t[0:64, 0:1024].rearrange("p (k h c) -> p k h c", k=2, h=4)
                tqb = pbh[3]
                tqbv = tqb.t[0:64, 0:512].rearrange("p (h c) -> p h c", h=4)
                for k3 in range(2):
                    for h in range(4):
                        fw.transpose(V(tqav[:, k3, h, :], tqa.whole), q3[:, k3, h, :], ident_bf[:], last=(k3 == 1 and h == 3))
                for h in range(4):
                    fw.transpose(V(tqbv[:, h, :], tqb.whole), q3[:, 2, h, :], ident_bf[:], last=(h == 3))
                fw.copy(qT[:, 0:2, :, :], V(tqav, tqa.whole))
                fw.copy(qT[:, 2, :, :], V(tqbv, tqb.whole))
                tk = pbh[4]
                tkv = tk.t[0:64, 0:512].rearrange("p (h c) -> p h c", h=4)
                for h in range(4):
                    fw.transpose(V(tkv[:, h, :], tk.whole), kbf[:, h, :], ident_bf[:], last=(h == 3))
                fw.copy(kT[:], V(tkv, tk.whole))
                sc = pb[5]
                for h in range(4):
                    fw.mm(sc[:, h * 128:(h + 1) * 128], kT[:, h, :], qT[:, 0, h, :], start=(h == 0), stop=(h == 3), skip_group_check=True)
                fw.tt(Wt[:], V(sc.t[:, :].rearrange("p (h i) -> p h i", i=128), sc.whole), Dret, ALU.mult)
                csb = chunk_states((0, 1))
                yb = pb[6]
                for h in range(4):
                    o = yb[:, h * 128:(h + 1) * 128]
                    fw.mm(o, Wt[:, h, :], rvbf[:, h * 128:(h + 1) * 128], start=(h == 0), stop=False, last=False, skip_group_check=True)
                    fw.mm(o, qT[:, 1, h, :], Rbf[0][:, h * 128:(h + 1) * 128], start=False, stop=False, last=False, skip_group_check=True)
                    fw.mm(o, qT[:, 2, h, :], Rbf[1][:, h * 128:(h + 1) * 128], start=False, stop=(h == 3), last=(h == 3), skip_group_check=True)
                fw.tt(V(R[0].t[:].rearrange("p (h n) -> p h n", n=128), R[0].whole),
                      V(R[0].t[:].rearrange("p (h n) -> p h n", n=128), R[0].whole),
                      V(A128[0].ap[0:64].unsqueeze(2).to_broadcast([64, 4, 128]), A128[0].res), ALU.mult)
                fw.tt(R[0][:], R[0][:], csb[0][0:64, :], ALU.add)
                ybv = V(yb.t[:, :].rearrange("p (h n) -> p h n", n=128), yb.whole)
                fw.act(ysq[:], ybv, AF.Square)
                fw.reduce(yss[:], ysq[:], ALU.add)
                fw.act(yss[:], yss[:], AF.Sqrt, bias=eps_t[:], scale=1.0 / 128)
                fw.recip(yss[:], yss[:])
                fw.tt(yn[:], ybv, V(yss.t[:].unsqueeze(2).to_broadcast([128, 4, 128]), yss.whole), ALU.mult)
                fw.tt(yn[:], yn[:], V(rnorm.t[:].unsqueeze(1).to_broadcast([128, 4, 128]), rnorm.whole), ALU.mult, eng="pool")
                fw.act(rg[:], rg[:], AF.Silu)
                fw.tt(ybf[:], V(yn.t[:].rearrange("p h n -> p (h n)"), yn.whole), rg[:], ALU.mult)
                to = pbh[7]
                tov = to.t[:, 0:512].rearrange("p (h c) -> p h c", h=4)
                for h in range(4):
                    fw.transpose(V(tov[:, h, :], to.whole), ybf[:, h * 128:(h + 1) * 128], ident_bf[:], last=(h == 3))
                fw.copy(ytb[:], V(tov, to.whole))
                fw.dma("act", V(mixT_s.t[12:16, :, rows].rearrange("h p c -> p h c"), mixT_s.part(("ret", lc)).res), ytb[:])
        fw.phase_reset()

    def phase_ssd(l, need_ctx):
        OW, OB, ODT, OA, ODD = 0, 4608, 6144, 6176, 6208
        prm = fw.carve("prm", [128, 6224], F32)
        fw.dma("sp", prm[:], ssdp_d[l, :, 0:6224])
        tri = fw.carve("tri", [128, 5, 128], F32)
        fw.dma("sp", tri[:], tri_d[:])
        ssdn = fw.carve("ssdn", [128, 8], F32)
        fw.dma("sp", ssdn[:], ssdn_d[l])
        negA = fw.carve("negA", [128, 32], F32)
        fw.act(negA[:], prm[:, OA:OA + 32], AF.Exp)
        fw.ts(negA[:], negA[:], -1.0, ALU.mult)
        one_t = fw.carve("one_t", [128, 1], F32)
        fw.memset(one_t[:], 1.0)
        U = [fw.carve(f"U{i}", [128, 1536], F32) for i in range(3)]
        dtr = fw.carve("dtr", [128, 32], F32)
        la = fw.carve("la", [128, 32], F32)
        E = fw.carve("E", [128, 96], F32)
        praw = fw.carve("praw", [128, 96], F32)
        cumraw = Buf(fw, "cumraw", _APHandle(praw.t[:, 0:32]))
        cumraw.whole = praw.whole
        tots = fw.carve("tots", [128, 32], F32)
        v = [fw.carve(f"v{d}", [128, 1024], BF16) for d in range(2)]
        vte = [fw.carve(f"vte{d}", [128, 1024], BF16) for d in range(2)]
        BCbf = fw.carve("BCbf", [128, 512], BF16)
        BCT = fw.carve("BCT", [128, 4, 128], BF16)
        zt = fw.carve("zt", [128, 1024], F32)
        R1 = fw.carve("R1", [128, 16, 128], F32)
        seg = fw.carve("seg", [128, 16, 128], F32)
        Dm = fw.carve("Dm", [128, 16, 128], BF16)
        Sm = [fw.carve(f"Sm{d}", [128, 2, 128], F32) for d in range(2)]
        Wt = fw.carve("Wt", [128, 16, 128], BF16)
        S = [fw.carve(f"S{d}", [128, 1024], F32) for d in range(2)]
        Sx = [fw.carve(f"Sx{d}", [128, 1024], F32) for d in range(2)]
        Sbf = [fw.carve(f"Sbf{d}", [128, 1024], BF16) for d in range(2)]
        Pb = fw.carve("Pb", [128, 16], F32)
        yt = fw.carve("yt", [128, 1024], F32)
        y2 = fw.carve("y2", [128, 1024], F32)
        vtmp = Buf(fw, "vtmp", _APHandle(y2.t[:].rearrange("p (h n) -> p h n", n=64)))
        vtmp.whole = y2.whole
        gss = fw.carve("gss", [128, 2], F32)
        ybf = fw.carve("ybf", [128, 1024], BF16)
        ytb = fw.carve("ytb", [128, 8, 128], BF16)
        Aex = fw.carve("Aex", [128, 16], F32)
        h16 = lambda vv: V(vv.ap.unsqueeze(2).to_broadcast([128, 16, 64]), vv.res)
        as16 = lambda b_: V(b_.t[:].rearrange("p (h n) -> p h n", n=64), b_.whole)

        def prep(lc):
            is_ctx = lc >= NCH
            src_t, r0 = (xbc_cpad, (lc - NCH) * 128) if is_ctx else (xbc_pad, lc * 128)
            rr = [xbc_pad.part("hl").res, xbc_pad.part("hr").res]
            for k in range(3):
                fw.dma("sp", U[k][:], V(src_t.t[r0 + k:r0 + k + 128, :], rr))
            fw.dma("sp", dtr[:], V(proj_s.t[lc * 128:(lc + 1) * 128, 3584:3616], proj_s.part(("dtr", lc)).res))
            fw.tt(U[0][:], U[0][:], prm[:, OW:OW + 1536], ALU.mult, eng="pool")
            fw.tt(U[1][:], U[1][:], prm[:, OW + 1536:OW + 3072], ALU.mult)
            fw.tt(U[2][:], U[2][:], prm[:, OW + 3072:OW + 4608], ALU.mult, eng="pool")
            fw.tt(U[1][:], U[1][:], U[0][:], ALU.add)
            fw.tt(U[1][:], U[1][:], U[2][:], ALU.add)
            fw.tt(U[1][:], U[1][:], prm[:, OB:OB + 1536], ALU.add)
            fw.act(U[0][:], U[1][:], AF.Silu)
            fw.tt(dtr[:], dtr[:], prm[:, ODT:ODT + 32], ALU.add)
            fw.act(dtr[:], dtr[:], AF.Exp)
            fw.act(dtr[:], dtr[:], AF.Ln, bias=one_t[:])
            fw.tt(la[:], dtr[:], negA[:], ALU.mult)
            pe = pb[0]
            for i, (w, c0, c1) in enumerate(((0, 0, 16), (1, 16, 32), (2, 0, 16), (3, 16, 32), (4, 0, 32))):
                o0 = (0, 16, 32, 48, 64)[i]
                fw.mm(pe[:, o0:o0 + (c1 - c0)], tri[:, w, :], la[:, c0:c1], start=(i == 0), stop=(i == 4), skip_group_check=True)
            fw.copy(praw[:], pe[:, 0:96])
            fw.act(E[:], praw[:], AF.Exp)
            fw.tt(tots[:], tots[:], praw[:, 64:96], ALU.add)
            xs = V(U[0].t[:, 0:1024].rearrange("p (h n) -> p h n", n=64), U[0].whole)
            for d in range(2):
                fw.tt(vtmp[:], xs, h16(dtr[:, d * 16:(d + 1) * 16]), ALU.mult)
                fw.copy(as16(v[d]), vtmp[:], eng="pool")
                fw.tt(as16(vte[d]), vtmp[:], h16(E[:, 32 + d * 16:48 + d * 16]), ALU.mult)
            fw.copy(BCbf[:], U[0][:, 1024:1536], eng="pool")

        def chunk_state(d):
            banks = (pb[4], pb[5])
            for g in range(2):
                fw.mm(banks[g][:, :], BCbf[:, g * 128:(g + 1) * 128], vte[d][:, g * 512:(g + 1) * 512], True, True)
            return banks

        def mulA(dst, srcS, acol):
            fw.tt(as16(dst), as16(srcS), h16(acol), ALU.mult)

        for grp in ((16, 17), tuple(range(NCH))):
            for d in range(2):
                fw.memset(S[d][:], 0.0)
            fw.memset(Pb[:], 1.0)
            fw.memset(tots[:], 0.0)
            for lc in grp:
                prep(lc)
                if lc < NCH or need_ctx:
                    pr_ = pc_h.part(lc).res
                    fw.dma("sp", V(pc_h.t[lc, :, 0:1024], pr_), v[0][:])
                    fw.dma("sp", V(pc_h.t[lc, :, 1024:2048], pr_), v[1][:])
                    fw.dma("sp", V(pc_h.t[lc, :, 2048:3072], pr_), vte[0][:])
                    fw.dma("sp", V(pc_h.t[lc, :, 3072:3584], pr_), BCbf[:])
                    pf_ = pc_f.part(lc).res
                    fw.dma("sp", V(pc_f.t[lc, :, 0:1024], pf_), U[0][:, 0:1024])
                    fw.dma("sp", V(pc_f.t[lc, :, 1024:1120], pf_), praw[:])
                    fw.dma("sp", V(pc_f.t[lc, :, 1120:1152], pf_), la[:])
                for d in range(2):
                    banks = chunk_state(d)
                    csv = V(pb_t[:, 4:6, :], [pb[4].whole, pb[5].whole])
                    cs_sb = V(seg.t[:, d * 8:(d + 1) * 8, :].rearrange("p a (g c) -> p (a g) c", g=2)[:, 0:2, :] if False else seg.t[:, d * 8:(d + 1) * 8, :], seg.whole)
                    cs_flat = V(seg.t[:].rearrange("p h n -> p (h n)")[:, d * 1024:(d + 1) * 1024], seg.whole)
                    fw.copy(V(cs_flat.ap.rearrange("p (g c) -> p g c", g=2), seg.whole), csv)
                    fw.dma("sp", V(cst_s.t[lc, d, :, 0:1024], cst_s.part(("s", lc, d)).res), cs_flat)
                    if d == 0:
                        mulA(S[0], S[0], E[:, 64:80])
                        fw.tt(S[0][:], S[0][:], cs_flat, ALU.add)
                    else:
                        fw.tt(as16(y2), V(cs_flat.ap.rearrange("p (h n) -> p h n", n=64), seg.whole), h16(Pb[:, :]), ALU.mult)
                        fw.tt(S[1][:], S[1][:], y2[:], ALU.add)
                        fw.tt(Pb[:], Pb[:], E[:, 80:96], ALU.mult)
                fw.dma("sp", V(cst_s.t[lc, 0, :, 1536:1568], cst_s.part(("e", lc)).res), E[:, 64:96])
            if grp[0] == 16:
                for d in range(2):
                    fw.copy(Sx[d][:], S[d][:])
        if "ssd_sf" in dbg:
            fw.dma("sp", dbg["ssd_sf"][:, :], Sx[0][:])
            fw.dma("sp", dbg["ssd_sb"][:, :], Sx[1][:])
        for d in range(2):
            fw.dma("sp", st_stage[d][:, 0:1024], S[d][:])
            fw.dma("sp", st_stage[d][:, 1536:1552], tots[:, d * 16:(d + 1) * 16])
            fw.dma("sp", V(st_src[d].t[bass.ds(rank * 128, 128), :], st_src[d].whole), st_stage[d][:, :])
            fw.collective("AllReduce", ALU.add, GROUPS, st_src[d][:, :], st_dst[d][:, :])
        for d in range(2):
            order = range(4) if d == 0 else range(3, -1, -1)
            for sidx in order:
                fw.dma("sp", yt[:], st_dst[d][sidx * 128:(sidx + 1) * 128, 0:1024])
                fw.dma("sp", Aex[:], st_dst[d][sidx * 128:(sidx + 1) * 128, 1536:1552])
                fw.act(Aex[:], Aex[:], AF.Exp)
                mulA(y2, Sx[d], Aex[:, :])
                fw.tt(y2[:], y2[:], yt[:], ALU.add)
                fw.tt(y2[:], y2[:], Sx[d][:], ALU.subtract)
                fw.stt(Sx[d][:], y2[:], cmask[:, d * 4 + sidx:d * 4 + sidx + 1], Sx[d][:], ALU.mult, ALU.add)
        for lc in range(NCH - 1, -1, -1):
            fw.copy(Sbf[1][:], Sx[1][:])
            fw.dma("sp", V(sb_s.t[lc, :, 0:1024], sb_s.part(("s", lc)).res), Sbf[1][:])
            fw.dma("sp", yt[:], V(cst_s.t[lc, 1, :, 0:1024], cst_s.part(("s", lc, 1)).res))
            fw.dma("sp", Aex[:], V(cst_s.t[lc, 0, :, 1552:1568], cst_s.part(("e", lc)).res))
            mulA(Sx[1], Sx[1], Aex[:, :])
            fw.tt(Sx[1][:], Sx[1][:], yt[:], ALU.add)
        if need_ctx:
            fw.dma("sp", yt[:], V(cst_s.t[17, 1, :, 0:1024], cst_s.part(("s", 17, 1)).res))
            fw.copy(Sbf[1][:], yt[:])
            fw.dma("sp", V(sb_s.t[16, :, 0:1024], sb_s.part(("s", 16)).res), Sbf[1][:])
            fw.memset(Sbf[0][:], 0.0)
            fw.dma("sp", V(sb_s.t[17, :, 0:1024], sb_s.part(("s", 17)).res), Sbf[0][:])
        if stop_after == "ssd_p2":
            fw.phase_reset(); return
        groups3 = [tuple(range(NCH))] + ([(16, 17)] if need_ctx else [])
        for grp in groups3:
            if grp[0] == 16:
                fw.memset(Sx[0][:], 0.0)
            for lc in grp:
                rows = slice(lc * 128, (lc + 1) * 128)
                pr_ = pc_h.part(lc).res
                pf_ = pc_f.part(lc).res
                fw.dma("sp", v[0][:], V(pc_h.t[lc, :, 0:1024], pr_))
                fw.dma("sp", v[1][:], V(pc_h.t[lc, :, 1024:2048], pr_))
                fw.dma("sp", vte[0][:], V(pc_h.t[lc, :, 2048:3072], pr_))
                fw.dma("sp", BCbf[:], V(pc_h.t[lc, :, 3072:3584], pr_))
                fw.dma("sp", U[0][:, 0:1024], V(pc_f.t[lc, :, 0:1024], pf_))
                fw.dma("sp", praw[:], V(pc_f.t[lc, :, 1024:1120], pf_))
                fw.dma("sp", la[:], V(pc_f.t[lc, :, 1120:1152], pf_))
                fw.act(E[:], praw[:], AF.Exp)
                fw.dma("sp", zt[:, 0:512], V(proj_s.t[rows, 3616:4128], proj_s.part(("z0", lc)).res))
                fw.dma("sp", zt[:, 512:1024], V(proj_s.t[rows, 4128:4640], proj_s.part(("z1", lc)).res))
                fw.dma("sp", Sbf[1][:], V(sb_s.t[lc, :, 0:1024], sb_s.part(("s", lc)).res))
                fw.copy(Sbf[0][:], Sx[0][:], eng="pool")
                tb_ = pbh[1]
                tbv = tb_.t[:, 0:512].rearrange("p (a c) -> p a c", a=4)
                for a in range(4):
                    fw.transpose(V(tbv[:, a, :], tb_.whole), BCbf[:, a * 128:(a + 1) * 128], ident_bf[:], last=(a == 3))
                fw.copy(BCT[:], V(tbv, tb_.whole))
                sc = pb[1]
                scv = sc.t[:, 256:512].rearrange("p (g i) -> p g i", g=2)
                for g in range(2):
                    fw.mm(V(scv[:, g, :], sc.whole), BCT[:, g, :], BCT[:, 2 + g, :], start=False if False else (g == 0), stop=(g == 1), skip_group_check=True)
                for d in range(2):
                    fw.tt(Sm[d][:], V(scv, sc.whole), V(tri.t[:, d:d + 1, :].to_broadcast([128, 2, 128]), tri.whole), ALU.mult)
                yb = (pb[6], pb[7])
                for d in range(2):
                    fw.tt(R1[:], V(la.t[:, d * 16:(d + 1) * 16].unsqueeze(2).to_broadcast([128, 16, 128]), la.whole),
                          V(tri.t[:, d:d + 1, :].to_broadcast([128, 16, 128]), tri.whole), ALU.mult, eng="pool")
                    for q in range(4):
                        bank = pb[2 + q % 2]
                        fw.mm(bank[:, :], tri[:, 4, :], V(R1.t[:, 4 * q:4 * q + 4, :].rearrange("p h n -> p (h n)"), R1.whole), True, True)
                        for hh in range(4):
                            h = 4 * q + hh
                            fw.ts(seg[:, h, :], bank[:, hh * 128:(hh + 1) * 128], cumraw[:, d * 16 + h:d * 16 + h + 1], ALU.subtract, 0.0, ALU.min)
                    fw.act(Dm[:], seg[:], AF.Exp)
                    for g in range(2):
                        fw.tt(Wt[:, g * 8:(g + 1) * 8, :], Dm[:, g * 8:(g + 1) * 8, :],
                              V(Sm[d].t[:, g:g + 1, :].to_broadcast([128, 8, 128]), Sm[d].whole), ALU.mult)
                    for h in range(16):
                        fw.mm(yb[h // 8][:, (h % 8) * 64:(h % 8 + 1) * 64], Wt[:, h, :], v[d][:, h * 64:(h + 1) * 64],
                              start=(d == 0 and h % 8 == 0), stop=(d == 1 and h % 8 == 7), last=(d == 1 and h % 8 == 7), skip_group_check=True)
                for d in range(2):
                    for g in range(2):
                        fw.mm(pb[2 + g][:, :], BCT[:, 2 + g, :], Sbf[d][:, g * 512:(g + 1) * 512], True, True)
                    ysv = V(pb_t[:, 2:4, :].rearrange("p a (h n) -> p (a h) n", n=64), [pb[2].whole, pb[3].whole])
                    fw.tt(as16(yt if d == 0 else y2), ysv, h16(E[:, d * 16:(d + 1) * 16]), ALU.mult)
                fw.tt(yt[:], yt[:], y2[:], ALU.add)
                yv = V(pb_t[:, 6:8, :].rearrange("p a c -> p (a c)") if False else pb_t[:, 6:8, :], [pb[6].whole, pb[7].whole])
                fw.tt(V(yt.t[:].rearrange("p (a c) -> p a c", a=2), yt.whole), V(yt.t[:].rearrange("p (a c) -> p a c", a=2), yt.whole), yv, ALU.add)
                xs = V(U[0].t[:, 0:1024].rearrange("p (h n) -> p h n", n=64), U[0].whole)
                fw.tt(as16(y2), xs, h16(prm[:, ODD:ODD + 16]), ALU.mult, eng="pool")
                fw.tt(yt[:], yt[:], y2[:], ALU.add)
                fw.act(zt[:], zt[:], AF.Silu)
                fw.tt(yt[:], yt[:], zt[:], ALU.mult)
                banks = chunk_state(0)
                mulA(Sx[0], Sx[0], E[:, 64:80])
                fw.tt(V(Sx[0].t[:].rearrange("p (g c) -> p g c", g=2), Sx[0].whole), V(Sx[0].t[:].rearrange("p (g c) -> p g c", g=2), Sx[0].whole),
                      V(pb_t[:, 4:6, :], [pb[4].whole, pb[5].whole]), ALU.add)
                fw.act(y2[:], yt[:], AF.Square)
                fw.reduce(gss[:], V(y2.t[:].rearrange("p (g c) -> p g c", g=2), y2.whole), ALU.add)
                fw.act(gss[:], gss[:], AF.Sqrt, bias=eps_t[:], scale=1.0 / 512)
                fw.recip(gss[:], gss[:])
                fw.tt(V(ybf.t[:].rearrange("p (g c) -> p g c", g=2), ybf.whole), V(yt.t[:].rearrange("p (g c) -> p g c", g=2), yt.whole),
                      V(gss.t[:].unsqueeze(2).to_broadcast([128, 2, 512]), gss.whole), ALU.mult)
                to = pbh[1]
                tov = to.t[:, 0:1024].rearrange("p (a c) -> p a c", a=8)
                for a in range(8):
                    fw.transpose(V(tov[:, a, :], to.whole), ybf[:, a * 128:(a + 1) * 128], ident_bf[:], last=(a == 7))
                fw.tt(ytb[:], V(tov, to.whole), V(ssdn.t[:].unsqueeze(2).to_broadcast([128, 8, 128]), ssdn.whole), ALU.mult)
                fw.dma("sp", V(mixT_s.t[4:12, :, rows].rearrange("h p c -> p h c"), mixT_s.part(("ssd", lc)).res), ytb[:])
        fw.phase_reset()

    def phase_out(l, need_ctx):
        wout = fw.carve("wout", [128, 16, D], BF16)
        wst = fw.carve("wost", [128, 4, D], F32)
        mx = [fw.carve(f"mx{i}", [128, 16, 128], BF16) for i in range(2)]
        tmp = fw.carve("otmp", [128, 512], F32)
        for q in range(4):
            fw.dma("sp", V(wst.t[:], wst.whole),
                   V(wout_d.t[l, q * 512:(q + 1) * 512, :].rearrange("(f p) c -> p f c", p=128), wout_d.whole))
            fw.copy(wout[:, q * 4:(q + 1) * 4, :], wst[:], eng="pool")
        allmix = [r for r in mixT_s.parts.values()]
        for lc in (range(NLC) if need_ctx else range(NCH)):
            is_ctx = lc >= NCH
            m = mx[lc % 2]
            fw.dma("sp", m[:], V(mixT_s.t[:, :, lc * 128:(lc + 1) * 128].rearrange("f p c -> p f c"), allmix))
            for hh in range(2):
                bank = pb[(lc % 2) * 2 + hh]
                for fc in range(16):
                    fw.mm(bank[:, :], m[:, fc, :], wout[:, fc, hh * 512:(hh + 1) * 512], start=(fc == 0), stop=(fc == 15))
                cs = slice(hh * 512, (hh + 1) * 512)
                fw.tt(tmp[:], bank[:, :], gate[:, 1 if is_ctx else 0, cs], ALU.mult)
                dst = ctx_sb[:, lc - NCH, cs] if is_ctx else x_sb[:, lc, cs]
                fw.tt(dst, dst, tmp[:], ALU.add)
        fw.phase_reset()

    for l in range(depth):
        need_ctx = l < depth - 1
        adaln(l)
        layer_params(l)
        phase_proj(l, need_ctx)
        if stop_after == "t_proj": break
        phase_attn(l, need_ctx)
        if stop_after == "t_attn": break
        phase_halo(l)
        phase_ret(l, need_ctx, "a")
        phase_ssd(l, need_ctx)
        phase_ret(l, need_ctx, "b")
        if stop_after == "t_ssd": break
        phase_out(l, need_ctx)
        if l == 0 and "x0" in dbg:
            fw.dma("sp", V(dbg["x0"].t.ap().rearrange("(c p) d -> p c d", p=128), dbg["x0"].whole), V(x_sb.t[:], x_sb.whole))
            fw.dma("sp", V(dbg["ctx0"].t.ap().rearrange("(c p) d -> p c d", p=128), dbg["ctx0"].whole), V(ctx_sb.t[:], ctx_sb.whole))

    if "qT" in dbg:
        fw.dma("sp", dbg["qT"][:], V(qT_s.t[:], [qT_s.part(c).res for c in range(NLC)]))
    if "kv0" in dbg:
        fw.dma("sp", dbg["kv0"][:], kv_dst[0][:, :])
    if "proj" in dbg:
        fw.dma("sp", dbg["proj"][:], V(proj_s.t[:], [r for r in proj_s.parts.values()]))
    if "mixT" in dbg:
        fw.dma("sp", dbg["mixT"][:], V(mixT_s.t[0:4], [r for r in mixT_s.parts.values()]))
    if "mixS" in dbg:
        fw.dma("sp", dbg["mixS"][:], V(mixT_s.t[4:12], [r for r in mixT_s.parts.values()]))
    if "mixR" in dbg:
        fw.dma("sp", dbg["mixR"][:], V(mixT_s.t[12:16], [r for r in mixT_s.parts.values()]))
    fw.dma("sp", V(out_d.t.ap().rearrange("(c p) d -> p c d", p=128), out_d.whole), V(x_sb.t[:], x_sb.whole))
    fw.wait_all("sp", [out_d[:]] + [V(b.t[:], b.whole) for b in dbg.values()])
    return nc, fw


def rope_tables():
    n_freq = 16
    inv_freq = (10000.0 ** (-np.arange(n_freq, dtype=np.float32) / n_freq)).astype(np.float32)
    pos = np.arange(8192)
    row = (pos // 64).astype(np.float32)
    col = (pos % 64).astype(np.float32)
    ang = np.concatenate([row[:, None] * inv_freq, col[:, None] * inv_freq], axis=-1).astype(np.float32)
    return np.cos(ang).astype(np.float32), np.sin(ang).astype(np.float32)


def const_tables():
    j = np.arange(128)[:, None]; i = np.arange(128)[None, :]
    tri = np.stack([(j <= i), (j >= i), (j > i), (j < i), np.ones((128, 128), bool)], axis=1).astype(np.float32)
    gf = np.array([1.0 - 2.0 ** -e for e in RET_EXP_F], np.float64)
    gb = np.array([1.0 - 2.0 ** -e for e in RET_EXP_B], np.float64)
    dif = (i - j).astype(np.float64)
    Dret = np.zeros((128, 4, 128), np.float64)
    for h in range(4):
        Dret[:, h, :] = np.where(dif > 0, gf[h] ** np.abs(dif), 0.0) + np.where(dif < 0, gb[h] ** np.abs(dif), 0.0) + np.where(dif == 0, 2.0, 0.0)
    Dret *= 0.125
    pos = np.arange(128, dtype=np.float64)[:, None]
    te_f = gf[None, :] ** (127 - pos) * 0.125
    te_b = gb[None, :] ** pos * 0.125
    qsc_f = gf[None, :] ** (pos + 1)
    qsc_b = gb[None, :] ** (128 - pos)
    a_f = np.broadcast_to(gf[None, :] ** 128, (128, 4)); a_b = np.broadcast_to(gb[None, :] ** 128, (128, 4))
    rett = np.concatenate([Dret.reshape(128, 512), te_f, te_b, qsc_f, qsc_b, a_f, a_b], axis=1).astype(np.float32)
    return np.ascontiguousarray(tri), np.ascontiguousarray(rett)


def make_inputs(inp):
    cos, sin = rope_tables()
    tri, rett = const_tables()
    ssdp = np.concatenate([inp["ssd_conv_w"].reshape(2, -1), inp["ssd_conv_b"], inp["ssd_dt_bias"].reshape(2, -1),
                           inp["ssd_a_log"].reshape(2, -1), inp["ssd_d"], inp["ret_norm"]], axis=1).astype(np.float32)
    ssdp = np.ascontiguousarray(np.broadcast_to(ssdp[:, None, :], (2, 128, ssdp.shape[1])))
    ssdn = np.ascontiguousarray(inp["ssd_norm"].reshape(2, 8, 128).transpose(0, 2, 1))
    rep = lambda a: np.ascontiguousarray(np.broadcast_to(a[:, None], (a.shape[0], 128) + a.shape[1:]))
    qkg = rep(np.stack([inp["attn_q_norm"], inp["attn_k_norm"]], axis=1))
    lamv = rep(np.stack([inp["lambda_q1"], inp["lambda_k1"], inp["lambda_q2"], inp["lambda_k2"]], axis=1))
    subln = np.ascontiguousarray(inp["attn_subln"][:, :, None])
    maps = []
    for core in range(8):
        b, t = core // 4, core % 4
        lo = t * TOK
        cc = np.stack([inp["c"][b].reshape(8, 128).T, inp["c_ctx"].reshape(8, 128).T], axis=-1)
        rp = np.stack([cos[lo:lo + TOK], sin[lo:lo + TOK]], axis=1)
        rp = rp.reshape(NCH, 128, 2, 32).transpose(1, 0, 2, 3)
        m = {
            "x": np.ascontiguousarray(inp["x"][b, lo:lo + TOK]),
            "ctx": np.ascontiguousarray(inp["ctx"][b]),
            "cc": np.ascontiguousarray(cc.astype(np.float32)),
            "w_ada": inp["w_ada"],
            "b_ada_f": np.ascontiguousarray(inp["b_ada"][:, :2 * D].reshape(2, 16, 128).transpose(0, 2, 1)),
            "b_ada": inp["b_ada"],
            "w_in": inp["w_in"], "w_out": inp["w_out"],
            "qkg": qkg, "lamv": lamv, "subln": subln,
            "rope": np.ascontiguousarray(rp),
            "tri": tri, "rett": rett, "ssdp": ssdp, "ssdn": ssdn,
            "cmask": np.ascontiguousarray(np.broadcast_to(np.array([float(s_ < t) for s_ in range(4)] + [float(s_ > t) for s_ in range(4)], np.float32)[None], (128, 8))),
        }
        maps.append(m)
    return maps


from concourse.bass_utils import run_bass_kernel_spmd


def kernel(**inputs):
    inp = {k: np.asarray(v) for k, v in inputs.items()}
    nc, _ = build(depth=2)
    maps = make_inputs(inp)
    res = run_bass_kernel_spmd(nc, maps, core_ids=list(range(8)))
    outs = [np.asarray(res.results[c]["out"]) for c in range(8)]
    return np.stack([np.concatenate(outs[0:4], 0), np.concatenate(outs[4:8], 0)]).astype(np.float32)
```

```python
import numpy as np
import concourse.bass as bass
import concourse.mybir as mybir

F32 = mybir.dt.float32
BF16 = mybir.dt.bfloat16
AF = mybir.ActivationFunctionType
ALU = mybir.AluOpType
AX = mybir.AxisListType


class Res:
    __slots__ = ("name", "w", "r")

    def __init__(self, name):
        self.name = name
        self.w = None
        self.r = {}


class V:
    __slots__ = ("ap", "res")

    def __init__(self, ap, res):
        self.ap = ap
        self.res = res if isinstance(res, (list, tuple)) else [res]


class Buf:
    def __init__(self, fw, name, t, nparts=1):
        self.fw = fw
        self.name = name
        self.t = t
        self.parts = {}
        self.whole = Res(name)

    def __getitem__(self, idx):
        return V(self.t[idx], self.whole)

    def part(self, key):
        if key not in self.parts:
            self.parts[key] = Res(f"{self.name}.{key}")
        return _PartView(self, self.parts[key])

    def ap(self):
        return self.t.ap()


class _PartView:
    def __init__(self, buf, res):
        self.buf = buf
        self.res = res

    def __getitem__(self, idx):
        return V(self.buf.t[idx], self.res)


class EngState:
    def __init__(self, name, eng, sem):
        self.name = name
        self.eng = eng
        self.sem = sem
        self.count = 0
        self.pending = False
        self.seen = {}
        self.seen_dma = {}


class FW:
    def __init__(self, nc, n_dma_sems=24, same_engine_sync=True):
        self.nc = nc
        self.same_engine_sync = same_engine_sync
        self.engs = {}
        for name, eng in (("pe", nc.tensor), ("dve", nc.vector), ("act", nc.scalar),
                          ("pool", nc.gpsimd), ("sp", nc.sync)):
            self.engs[name] = EngState(name, eng, nc.alloc_semaphore(f"s_{name}"))
        self.dma_sems = [nc.alloc_semaphore(f"s_dma{i}") for i in range(n_dma_sems)]
        self.dma_vals = [0] * n_dma_sems
        self.dma_next = 0
        self.n_inst = 0
        self.out_tokens = []
        self.cc_sem = None
        self.cc_val = 0

    def sbuf(self, name, shape, dtype):
        return Buf(self, name, self.nc.alloc_sbuf_tensor("sb_" + name, list(shape), dtype))

    def psum(self, name, shape, dtype=F32):
        return Buf(self, name, self.nc.alloc_psum_tensor("ps_" + name, list(shape), dtype))

    def dram(self, name, shape, dtype, kind="Internal", **kw):
        return Buf(self, name, self.nc.dram_tensor(name, list(shape), dtype, kind=kind, **kw))

    def _need(self, E, tok):
        if tok is None:
            return
        if tok[0] == "eng":
            _, e, c = tok
            if e == E.name:
                if not self.same_engine_sync or e == "pe":
                    return
            if E.seen.get(e, 0) >= c:
                return
            P = self.engs[e]
            assert c <= P.count, f"{E.name} waits on pending (never-incremented) {e} count {c} > {P.count}"
            E.eng.wait_ge(P.sem, c)
            E.seen[e] = c
        elif tok[0] == "cc":
            val = tok[1]
            if E.seen_dma.get("cc", 0) >= val:
                return
            E.eng.wait_ge(self.cc_sem, val)
            E.seen_dma["cc"] = val
        else:
            _, si, val = tok
            if E.seen_dma.get(si, 0) >= val:
                return
            E.eng.wait_ge(self.dma_sems[si], val)
            E.seen_dma[si] = val

    def _pre(self, E, reads, writes):
        for v in reads:
            for r in v.res:
                self._need(E, r.w)
        for v in writes:
            for r in v.res:
                self._need(E, r.w)
                for tok in r.r.values():
                    self._need(E, tok)

    def _post(self, tok, key, reads, writes):
        for v in reads:
            for r in v.res:
                r.r[key] = tok
        for v in writes:
            for r in v.res:
                r.w = tok
                r.r = {}

    def op(self, engname, fn, reads, writes, inc=True):
        E = self.engs[engname]
        self._pre(E, reads, writes)
        ins = fn(E.eng)
        self.n_inst += 1
        if inc:
            E.count += 1
            ins.then_inc(E.sem, 1)
            tok = ("eng", engname, E.count)
        else:
            tok = ("eng", engname, E.count + 1)
        self._post(tok, engname, reads, writes)
        return ins

    def dma(self, qname, out, in_, **kw):
        E = self.engs[qname]
        self._pre(E, [in_], [out])
        si = self.dma_next
        self.dma_next = (self.dma_next + 1) % len(self.dma_sems)
        if self.dma_vals[si] > 0:
            self._need(E, ("dma", si, self.dma_vals[si]))
        self.dma_vals[si] += 16
        ins = E.eng.dma_start(out=out.ap, in_=in_.ap, **kw)
        ins.then_inc(self.dma_sems[si], 16)
        self.n_inst += 1
        tok = ("dma", si, self.dma_vals[si])
        self._post(tok, f"dma{si}", [in_], [out])
        return tok

    def wait_all(self, engname, views):
        E = self.engs[engname]
        for v in views:
            for r in v.res:
                self._need(E, r.w)

    def mm(self, out, lhsT, rhs, start, stop, last=None, **kw):
        if last is None:
            last = stop
        return self.op("pe", lambda e: e.matmul(out.ap, lhsT.ap, rhs.ap, start=start, stop=stop, **kw),
                       [lhsT, rhs], [out], inc=last)

    def transpose(self, out, in_, ident, last=True):
        return self.op("pe", lambda e: e.transpose(out.ap, in_.ap, ident.ap), [in_, ident], [out], inc=last)

    def act(self, out, in_, func, bias=None, scale=1.0, accum_out=None, eng="act"):
        reads = [in_]
        kw = {}
        if bias is not None:
            if isinstance(bias, V):
                reads.append(bias)
                kw["bias"] = bias.ap
            else:
                kw["bias"] = bias
        if isinstance(scale, V):
            reads.append(scale)
            kw["scale"] = scale.ap
        else:
            kw["scale"] = scale
        writes = [out]
        if accum_out is not None:
            writes.append(accum_out)
            kw["accum_out"] = accum_out.ap
        return self.op(eng, lambda e: e.activation(out.ap, in_.ap, func, **kw), reads, writes)

    def tt(self, out, in0, in1, op, eng="dve"):
        return self.op(eng, lambda e: e.tensor_tensor(out.ap, in0.ap, in1.ap, op), [in0, in1], [out])

    def ts(self, out, in0, s1, op0, s2=None, op1=None, eng="dve", accum_out=None):
        reads = [in0]
        a1 = s1
        if isinstance(s1, V):
            reads.append(s1)
            a1 = s1.ap
        a2 = s2
        if isinstance(s2, V):
            reads.append(s2)
            a2 = s2.ap
        kw = {}
        writes = [out]
        if op1 is not None:
            kw["op1"] = op1
        if accum_out is not None:
            kw["accum_out"] = accum_out.ap
            writes.append(accum_out)
        return self.op(eng, lambda e: e.tensor_scalar(out.ap, in0.ap, a1, a2, op0, **kw), reads, writes)

    def stt(self, out, in0, scalar, in1, op0, op1, eng="dve"):
        reads = [in0, in1]
        a = scalar
        if isinstance(scalar, V):
            reads.append(scalar)
            a = scalar.ap
        return self.op(eng, lambda e: e.scalar_tensor_tensor(out.ap, in0.ap, a, in1.ap, op0, op1), reads, [out])

    def copy(self, out, in_, eng="dve"):
        if eng == "act":
            return self.op("act", lambda e: e.copy(out.ap, in_.ap), [in_], [out])
        return self.op(eng, lambda e: e.tensor_copy(out.ap, in_.ap), [in_], [out])

    def memset(self, out, val, eng="dve"):
        return self.op(eng, lambda e: e.memset(out.ap, val), [], [out])

    def reduce(self, out, in_, op, axis=AX.X, eng="dve"):
        return self.op(eng, lambda e: e.tensor_reduce(out.ap, in_.ap, axis, op), [in_], [out])

    def recip(self, out, in_):
        return self.op("dve", lambda e: e.reciprocal(out.ap, in_.ap), [in_], [out])

    def collective(self, kind, op, groups, in_, out):
        E = self.engs["pool"]
        self._pre(E, [in_], [out])
        if self.cc_sem is None:
            self.cc_sem = self.nc.alloc_semaphore("s_cc")
        self.cc_val += 1
        ins = E.eng.collective_compute(kind, op, replica_groups=groups, ins=[in_.ap], outs=[out.ap])
        ins.then_inc(self.cc_sem)
        self.n_inst += 1
        tok = ("cc", self.cc_val)
        self._post(tok, "cc", [in_], [out])
        return tok

    def make_arena(self, kbytes):
        self.arena_t = self.nc.alloc_sbuf_tensor("sb_arena", [128, kbytes * 256], F32)
        self.arena_words = kbytes * 256
        self.arena_off = 0
        self.arena_gen = 0

    def carve(self, name, shape, dtype):
        esz = 2 if dtype == BF16 else 4
        n = 1
        for s in shape[1:]:
            n *= s
        words = (n * esz + 3) // 4
        words = (words + 7) // 8 * 8
        assert self.arena_off + words <= self.arena_words, f"arena overflow for {name}: {self.arena_off}+{words}>{self.arena_words}"
        raw = self.arena_t[0:shape[0], self.arena_off:self.arena_off + words]
        self.arena_off += words
        ap = raw.bitcast(dtype) if dtype != F32 else raw
        ap = ap[:, 0:n]
        if len(shape) > 2:
            names = " ".join(f"d{i}" for i in range(1, len(shape)))
            kw = {f"d{i}": shape[i] for i in range(1, len(shape))}
            ap = ap.rearrange(f"p ({names}) -> p {names}", **kw)
        return Buf(self, f"{name}@{self.arena_gen}", _APHandle(ap))

    def barrier(self):
        for E in self.engs.values():
            for P in self.engs.values():
                if P is not E and P.count > 0:
                    self._need(E, ("eng", P.name, P.count))
            for si, val in enumerate(self.dma_vals):
                if val > 0:
                    self._need(E, ("dma", si, val))
            if self.cc_val > 0:
                self._need(E, ("cc", self.cc_val))

    def phase_reset(self):
        self.barrier()
        self.arena_off = 0
        self.arena_gen += 1


class _APHandle:
    def __init__(self, ap):
        self._ap = ap

    def __getitem__(self, idx):
        return self._ap[idx]

    def ap(self):
        return self._ap


import math
KCUT = 9

D = 1024
NCH = 16
TOK = 2048
NTOK = TOK + 256
NLC = 18
DIN = 6176
EPS = 1e-6
GROUPS = [[0, 1, 2, 3], [4, 5, 6, 7]]
SW = 1568
NP = 3 * 1536 + 1536 + 32 + 32 + 16 + 128
RET_EXP_F = (5.0, 6.0, 7.0, 8.0)
RET_EXP_B = (5.5, 6.5, 7.5, 8.5)
BLOCKS = [("aq", 0, 512), ("ak", 512, 512), ("av", 1024, 512), ("ag", 1536, 512),
          ("xbc0", 2048, 512), ("xbc1", 2560, 512), ("xbc2", 3072, 512), ("dtr", 3584, 32),
          ("z0", 3616, 512), ("z1", 4128, 512), ("rqk", 4640, 512), ("rv", 5152, 512), ("rg", 5664, 512)]


class Alt:
    def __init__(self, bufs, par):
        self.bufs, self.par = bufs, par

    @property
    def cur(self):
        return self.bufs[self.par[0] % len(self.bufs)]

    def __getitem__(self, idx):
        return self.cur[idx]

    @property
    def t(self):
        return self.cur.t

    @property
    def whole(self):
        return self.cur.whole


def lam_init_of(layer):
    return 0.8 - 0.6 * math.exp(-0.3 * layer)


def build(depth=2, debug=None, stop_after=None):
    debug = debug or {}
    nc = bass.Bass("TRN2", target_bir_lowering=False)
    fw = FW(nc, same_engine_sync=True)
    I = lambda n, s, d=F32: fw.dram(n, s, d, kind="ExternalInput")
    x_d = I("x", [TOK, D])
    ctx_d = I("ctx", [256, D])
    cc_d = I("cc", [128, 8, 2])
    wada_d = I("w_ada", [2, D, 3 * D])
    bada_f_d = I("b_ada_f", [2, 128, 16])
    bada_d = I("b_ada", [2, 3 * D])
    win_d = I("w_in", [2, D, DIN])
    wout_d = I("w_out", [2, 2 * D, D])
    qkg_d = I("qkg", [2, 128, 2, 64])
    lamv_d = I("lamv", [2, 128, 4, 64])
    subln_d = I("subln", [2, 128, 1])
    rope_d = I("rope", [128, NCH, 2, 32])
    tri_d = I("tri", [128, 5, 128])
    rett_d = I("rett", [128, 4 * 128 + 24])
    ssdp_d = I("ssdp", [2, 128, NP])
    ssdn_d = I("ssdn", [2, 128, 8])
    cmask_d = I("cmask", [128, 8])
    out_d = fw.dram("out", [TOK, D], F32, kind="ExternalOutput")
    dbg = {k: fw.dram("dbg_" + k, shape, dt_, kind="ExternalOutput") for k, (shape, dt_) in debug.items()}

    proj_s = fw.dram("proj_s", [NTOK, DIN], F32)
    qT_s = fw.dram("qT_s", [4, 128, NTOK], BF16)
    agT_s = fw.dram("agT_s", [4, 128, NTOK], BF16)
    mixT_s = fw.dram("mixT_s", [16, 128, NTOK], BF16)
    kc_s = fw.dram("kc_s", [4, 128, 256], BF16)
    vc_s = fw.dram("vc_s", [256, 512], BF16)
    xbc_pad = fw.dram("xbc_pad", [TOK + 2, 1536], F32)
    xbc_cpad = fw.dram("xbc_cpad", [258, 1536], F32)
    hx_stage = fw.dram("hx_stage", [2, 1536], F32)
    hx_src = fw.dram("hx_src", [8, 1536], F32)
    hx_dst = fw.dram("hx_dst", [8, 1536], F32)
    hxL = fw.dram("hxL", [9, 1536], F32)
    hxR = fw.dram("hxR", [8, 1536], F32)
    cst_s = fw.dram("cst_s", [NLC, 2, 128, SW], F32)
    sb_s = fw.dram("sb_s", [NLC, 128, 1536], BF16)
    pc_h = fw.dram("pc_h", [NLC, 128, 3584], BF16)
    pc_f = fw.dram("pc_f", [NLC, 128, 1152], F32)
    rc_f = fw.dram("rc_f", [NLC, 128, 512], F32)
    rc_h = fw.dram("rc_h", [NLC, 128, 1024], BF16)
    rctx_s = fw.dram("rctx_s", [2, 64, 512], F32)
    st_stage = [fw.dram(f"st_stage{d}", [128, SW], F32) for d in range(2)]
    st_src = [fw.dram(f"st_src{d}", [512, SW], F32) for d in range(2)]
    st_dst = [fw.dram(f"st_dst{d}", [512, SW], F32) for d in range(2)]
    kv_src = [fw.dram(f"kv_src{h}", [512, 4096], BF16) for h in range(4)]
    kv_dst = [fw.dram(f"kv_dst{h}", [512, 4096], BF16) for h in range(4)]
    kv_stage = [fw.dram(f"kv_stage{h}", [128, 4096], BF16) for h in range(4)]

    x_sb = fw.sbuf("x_sb", [128, NCH, D], F32)
    ctx_sb = fw.sbuf("ctx_sb", [128, 2, D], F32)
    gate = fw.sbuf("gate", [128, 2, D], F32)
    sc1 = fw.sbuf("sc1", [128, 2, 8], F32)
    sh = fw.sbuf("sh", [128, 2, 8], F32)
    ident = fw.sbuf("ident", [128, 128], F32)
    ident_bf = fw.sbuf("ident_bf", [128, 128], BF16)
    ones_bf = fw.sbuf("ones_bf", [128, 128], BF16)
    eps_t = fw.sbuf("eps_t", [128, 1], F32)
    cc = fw.sbuf("cc", [128, 8, 2], F32)
    rope = fw.sbuf("rope", [128, NCH, 2, 32], F32)
    qkg = fw.sbuf("qkg", [128, 2, 64], F32)
    lamv = fw.sbuf("lamv", [128, 4, 64], F32)
    neglam = fw.sbuf("neglam", [128, 1], F32)
    subln = fw.sbuf("subln", [128, 1], F32)
    small = fw.sbuf("small", [128, 64], F32)
    cmask = fw.sbuf("cmask", [128, 8], F32)
    fw.make_arena(119)
    pb_t = nc.alloc_psum_tensor("ps_banks", [128, 8, 512], F32)
    pb = [Buf(fw, f"pb{i}", _APHandle(pb_t[:, i, :])) for i in range(8)]
    pbh = [Buf(fw, f"pbh{i}", _APHandle(pb_t[:, i, :].bitcast(BF16))) for i in range(8)]
    for i in range(8):
        pbh[i].whole = pb[i].whole
    rank = nc.partition_id() % 4

    fw.memset(ident[:], 1.0, eng="pool")
    fw.op("pool", lambda e: e.affine_select(ident.t[:], ident.t[:], [[-1, 128]], ALU.is_equal, 0.0,
                                             base=0, channel_multiplier=1), [ident[:]], [ident[:]])
    fw.copy(ident_bf[:], ident[:])
    fw.memset(ones_bf[:], 1.0)
    fw.memset(eps_t[:], EPS)
    fw.dma("sp", V(x_sb.t[:], x_sb.whole), V(x_d.t.ap().rearrange("(c p) d -> p c d", p=128), x_d.whole))
    fw.dma("sp", V(ctx_sb.t[:], ctx_sb.whole), V(ctx_d.t.ap().rearrange("(c p) d -> p c d", p=128), ctx_d.whole))
    fw.dma("sp", cc[:], cc_d[:])
    fw.dma("sp", rope[:], rope_d[:])
    fw.dma("sp", cmask[:], cmask_d[:])
    fw.act(cc[:], cc[:], AF.Silu)
    zt = fw.carve("zt", [128, 4096], BF16)
    fw.memset(zt[:], 0.0)
    for h in range(4):
        fw.dma("sp", V(kv_src[h].t.ap().rearrange("(r p) c -> p r c", p=128), kv_src[h].whole),
               V(zt.t[:].unsqueeze(1).to_broadcast([128, 4, 4096]), zt.whole))
    zf = fw.carve("zf", [128, SW], F32)
    fw.memset(zf[:], 0.0)
    fw.dma("sp", xbc_cpad[0:1, :], zf[0:1, 0:1536])
    fw.dma("sp", xbc_cpad[257:258, :], zf[0:1, 0:1536])
    fw.dma("sp", hx_src[:, :], zf[0:8, 0:1536])
    fw.dma("sp", hxL[:, :], zf[0:9, 0:1536])
    fw.dma("sp", hxR[:, :], zf[0:8, 0:1536])
    for d in range(2):
        fw.dma("sp", V(st_src[d].t.ap().rearrange("(r p) c -> p r c", p=128), st_src[d].whole),
               V(zf.t[:].unsqueeze(1).to_broadcast([128, 4, SW]), zf.whole))
        fw.dma("sp", st_stage[d][:, :], zf[:, :])
    fw.phase_reset()

    def adaln(l):
        ccrep = fw.carve("ccrep", [128, 8, 2, 128], F32)
        badaf = fw.carve("badaf", [128, 16], F32)
        gbias = fw.carve("gbias", [128, D], F32)
        wada_sb = fw.carve("wada_sb", [128, 8, 512], F32)
        fw.copy(ccrep[:], V(cc.t[:].unsqueeze(3).to_broadcast([128, 8, 2, 128]), cc.whole))
        fw.dma("sp", badaf[:], bada_f_d[l])
        fw.dma("sp", gbias[:], V(bada_d.t[l:l + 1, 2 * D:3 * D].partition_broadcast(128), bada_d.whole))
        ps_s = V(pb[2].t[:, 0:32].rearrange("p (a b) -> p a b", b=2), pb[2].whole)
        for piece in range(6):
            fw.dma("sp", V(wada_sb.t[:], wada_sb.whole),
                   V(wada_d.t[l, :, piece * 512:(piece + 1) * 512].rearrange("(k p) c -> p k c", p=128), wada_d.whole))
            if piece < 4:
                for j in range(4):
                    blk = piece * 4 + j
                    for k in range(8):
                        fw.mm(V(ps_s.ap[:, blk, :], ps_s.res), wada_sb[:, k, j * 128:(j + 1) * 128], cc[:, k, :],
                              start=(k == 0), stop=(k == 7))
            else:
                half = piece - 4
                for v in range(2):
                    for k in range(8):
                        fw.mm(pb[3][:, :], ccrep[:, k, v, :], wada_sb[:, k, :], start=(k == 0), stop=(k == 7))
                    fw.tt(gate[:, v, half * 512:(half + 1) * 512], pb[3][:, :], gbias[:, half * 512:(half + 1) * 512], ALU.add)
        for v in range(2):
            fw.tt(sh[:, v, :], V(ps_s.ap[:, 0:8, v], ps_s.res), badaf[:, 0:8], ALU.add)
            fw.tt(sc1[:, v, :], V(ps_s.ap[:, 8:16, v], ps_s.res), badaf[:, 8:16], ALU.add)
        fw.ts(sc1[:], sc1[:], 1.0, ALU.add)
        fw.phase_reset()

    def layer_params(l):
        fw.dma("sp", qkg[:], qkg_d[l])
        fw.dma("sp", lamv[:], lamv_d[l])
        fw.dma("sp", subln[:], subln_d[l])
        fw.tt(small[:, 0:64], lamv[:, 0, :], lamv[:, 1, :], ALU.mult)
        s1 = fw.sbuf(f"lam_s1_{l}", [128, 1], F32)
        s2 = fw.sbuf(f"lam_s2_{l}", [128, 1], F32)
        fw.reduce(s1[:], small[:, 0:64], ALU.add)
        fw.tt(small[:, 0:64], lamv[:, 2, :], lamv[:, 3, :], ALU.mult)
        fw.reduce(s2[:], small[:, 0:64], ALU.add)
        fw.act(s1[:], s1[:], AF.Exp)
        fw.act(s2[:], s2[:], AF.Exp)
        fw.tt(neglam[:], s2[:], s1[:], ALU.subtract)
        fw.ts(neglam[:], neglam[:], -lam_init_of(l), ALU.add)
        fw.ts(subln[:], subln[:], 1.0 - lam_init_of(l), ALU.mult)

    def phase_proj(l, need_ctx_q):
        hT = fw.carve("hT", [128, 8, NTOK], BF16)
        par = [0]
        alt = lambda n, sh, dt_: Alt([fw.carve(f"{n}_{i}", sh, dt_) for i in range(2)], par)
        xn = alt("xn", [128, D], F32)
        junk = alt("junk", [128, D], F32)
        ss = alt("ss", [128, 1], F32)
        rs = alt("rs", [128, 1], F32)
        wblk = [fw.carve(f"wblk{i}", [128, 8, 512], BF16) for i in range(2)]
        wst = fw.carve("wst", [128, 8, 512], F32)
        stage = [fw.carve(f"stage{i}", [128, 512], F32) for i in range(3)]
        sq = alt("sq", [128, 8, 64], F32)
        qn = alt("qn", [128, 8, 64], F32)
        t1 = alt("t1", [128, 8, 32], F32)
        t2 = alt("t2", [128, 8, 32], F32)
        ss8 = alt("ss8", [128, 8], F32)
        qbf = [fw.carve(f"qbf{i}", [128, 8, 64], BF16) for i in range(2)]
        tb = [fw.carve(f"tb{i}", [128, 4, 128], BF16) for i in range(2)]
        vbf = [fw.carve(f"vbf{i}", [128, 512], BF16) for i in range(2)]

        for lc in range(NLC):
            par[0] = lc
            src = x_sb[:, lc, :] if lc < NCH else ctx_sb[:, lc - NCH, :]
            v = 0 if lc < NCH else 1
            fw.act(junk[:], src, AF.Square, accum_out=ss[:])
            fw.act(rs[:], ss[:], AF.Sqrt, bias=eps_t[:], scale=1.0 / D)
            fw.recip(rs[:], rs[:])
            fw.ts(xn[:], src, rs[:], ALU.mult)
            pt = V(pb[lc % 2 * 2].t[:, :], [pb[lc % 2 * 2].whole, pb[lc % 2 * 2 + 1].whole])
            ptt = pb_t[:, lc % 2 * 2:lc % 2 * 2 + 2, :].rearrange("p a (k c) -> p (a k) c", c=128)
            for k in range(8):
                fw.transpose(V(ptt[:, k, :], pt.res), xn[:, k * 128:(k + 1) * 128], ident[:], last=(k == 7))
            for k in range(8):
                fw.ts(hT[:, k, lc * 128:(lc + 1) * 128], V(ptt[:, k, :], pt.res), sc1[:, v, k:k + 1], ALU.mult,
                      sh[:, v, k:k + 1], ALU.add)
        if "hT" in dbg:
            fw.dma("sp", V(dbg["hT"].t[:], dbg["hT"].whole), V(hT.t[:], hT.whole))

        it = 0
        for bi, (bname, col0, ncols) in enumerate(BLOCKS):
            wb = wblk[bi % 2]
            fw.dma("sp", V(wst.t[:, :, 0:ncols], wst.whole),
                   V(win_d.t[l, :, col0:col0 + ncols].rearrange("(k p) c -> p k c", p=128), win_d.whole))
            fw.copy(V(wb.t[:, :, 0:ncols], wb.whole), V(wst.t[:, :, 0:ncols], wst.whole), eng="pool")
            for lc in range(NLC):
                is_ctx = lc >= NCH
                bank = pb[4 + it % 2]
                it += 1
                par[0] = it
                for k in range(8):
                    fw.mm(bank[:, 0:ncols], hT[:, k, lc * 128:(lc + 1) * 128], wb[:, k, 0:ncols],
                          start=(k == 0), stop=(k == 7))
                rows = slice(lc * 128, (lc + 1) * 128)
                if bname in ("aq", "ak"):
                    if bname == "aq" and is_ctx and not need_ctx_q:
                        continue
                    gi = 0 if bname == "aq" else 1
                    psv = V(bank.t[:, :].rearrange("p (a b) -> p a b", b=64), bank.whole)
                    fw.act(sq[:], psv, AF.Square)
                    fw.reduce(ss8[:], sq[:], ALU.add)
                    fw.act(ss8[:], ss8[:], AF.Sqrt, bias=eps_t[:], scale=1.0 / 64)
                    fw.recip(ss8[:], ss8[:])
                    fw.tt(qn[:], psv, V(ss8.t[:].unsqueeze(2).to_broadcast([128, 8, 64]), ss8.whole), ALU.mult)
                    fw.tt(qn[:], qn[:], V(qkg.t[:, gi:gi + 1, :].to_broadcast([128, 8, 64]), qkg.whole), ALU.mult, eng="pool")
                    qo = qbf[it % 2]
                    if not is_ctx:
                        cosb = V(rope.t[:, lc, 0:1, :].to_broadcast([128, 8, 32]), rope.whole)
                        sinb = V(rope.t[:, lc, 1:2, :].to_broadcast([128, 8, 32]), rope.whole)
                        fw.tt(t1[:], qn[:, :, 0:32], cosb, ALU.mult)
                        fw.tt(t2[:], qn[:, :, 32:64], sinb, ALU.mult, eng="pool")
                        fw.tt(qo[:, :, 0:32], t1[:], t2[:], ALU.subtract)
                        fw.tt(t1[:], qn[:, :, 0:32], sinb, ALU.mult)
                        fw.tt(t2[:], qn[:, :, 32:64], cosb, ALU.mult, eng="pool")
                        fw.tt(qo[:, :, 32:64], t1[:], t2[:], ALU.add)
                    else:
                        fw.copy(qo[:], qn[:])
                    tbank = pbh[6 + lc % 2]
                    tbv = tbank.t[:, 0:512].rearrange("p (h c) -> p h c", c=128)
                    qof = qo.t[:].rearrange("p a b -> p (a b)")
                    for hd in range(4):
                        fw.transpose(V(tbv[:, hd, :], tbank.whole), V(qof[:, hd * 128:(hd + 1) * 128], qo.whole), ident_bf[:], last=(hd == 3))
                    tbs = tb[lc % 2]
                    fw.copy(tbs[:], V(tbv, tbank.whole), eng="act")
                    if bname == "aq":
                        fw.dma("act", V(qT_s.t[:, :, rows].rearrange("h p c -> p h c"), qT_s.part(lc).res), tbs[:])
                    elif is_ctx:
                        c0 = (lc - NCH) * 128
                        fw.dma("act", V(kc_s.t[:, :, c0:c0 + 128].rearrange("h p c -> p h c"), kc_s.part(lc).res), tbs[:])
                    else:
                        for hd in range(4):
                            fw.dma("act", V(kv_stage[hd].t[:, lc * 128:(lc + 1) * 128], kv_stage[hd].part(("k", lc)).res),
                                   tbs[:, hd, :])
                elif bname == "av":
                    vb = vbf[lc % 2]
                    fw.copy(vb[:], bank[:, :], eng="act")
                    if is_ctx:
                        c0 = (lc - NCH) * 128
                        fw.dma("act", V(vc_s.t[c0:c0 + 128, :], vc_s.part(lc).res), vb[:])
                    else:
                        for hd in range(4):
                            fw.dma("act", V(kv_stage[hd].t[:, 2048 + lc * 128:2048 + (lc + 1) * 128],
                                            kv_stage[hd].part(("v", lc)).res), vb[:, hd * 128:(hd + 1) * 128])
                elif bname == "ag":
                    st = stage[lc % 3]
                    fw.act(st[:], bank[:, :], AF.Silu)
                    tbank = pb[6 + lc % 2]
                    for hd in range(4):
                        fw.transpose(tbank[:, hd * 128:(hd + 1) * 128], st[:, hd * 128:(hd + 1) * 128], ident[:], last=(hd == 3))
                    tbs = tb[lc % 2]
                    fw.copy(V(tbs.t[:].rearrange("p h c -> p (h c)"), tbs.whole), tbank[:, :])
                    fw.dma("act", V(agT_s.t[:, :, rows].rearrange("h p c -> p h c"), agT_s.part(lc).res), tbs[:])
                else:
                    st = stage[lc % 3]
                    if lc % 2 == 0:
                        fw.copy(st[:, 0:ncols], bank[:, 0:ncols])
                    else:
                        fw.copy(st[:, 0:ncols], bank[:, 0:ncols], eng="act")
                    if bname.startswith("xbc"):
                        xc = (int(bname[3]) * 512)
                        if is_ctx:
                            r1 = 1 + (lc - NCH) * 128
                            fw.dma("sp", V(xbc_cpad.t[r1:r1 + 128, xc:xc + 512], xbc_cpad.part((bname, lc)).res), st[:, 0:ncols])
                        else:
                            r1 = 1 + lc * 128
                            fw.dma("sp", V(xbc_pad.t[r1:r1 + 128, xc:xc + 512], xbc_pad.part((bname, lc)).res), st[:, 0:ncols])
                    else:
                        fw.dma("sp", V(proj_s.t[rows, col0:col0 + ncols], proj_s.part((bname, lc)).res), st[:, 0:ncols])
            if bname == "av":
                for hd in range(4):
                    allres = [kv_stage[hd].part(("k", c)).res for c in range(NCH)] + [kv_stage[hd].part(("v", c)).res for c in range(NCH)]
                    fw.dma("sp", V(kv_src[hd].t[bass.ds(rank * 128, 128), :], kv_src[hd].whole), V(kv_stage[hd].t[:, :], allres))
                    fw.collective("AllReduce", ALU.add, GROUPS, kv_src[hd][:, :], kv_dst[hd][:, :])
        fw.phase_reset()

    def phase_attn(l, with_ctx_q):
        kTb = [fw.carve(f"kT{i}", [128, 8448], BF16) for i in range(2)]
        vvb = [fw.carve(f"vv{i}", [128, 66, 128], BF16) for i in range(2)]
        qhb = [fw.carve(f"qh{i}", [128, NTOK], BF16) for i in range(2)]
        aghb = [fw.carve(f"agh{i}", [128, NTOK], BF16) for i in range(2)]
        rec = fw.carve("rec", [128, 512], F32)
        om = [fw.carve(f"om{i}", [128, 512], F32) for i in range(2)]
        A = fw.carve("A", [128, 512], F32)
        sqb = fw.carve("sqb", [128, 512], BF16)
        rstd = fw.carve("rstd", [128, 512], F32)
        mixo = [fw.carve(f"mixo{i}", [128, 512], BF16) for i in range(2)]
        ones_f = fw.carve("ones_f", [128, 128], F32)
        fw.memset(ones_f[:], 1.0)
        qblocks = [(q0, 512, 0, 66) for q0 in range(0, TOK, 512)]
        if with_ctx_q:
            qblocks.append((TOK, 256, 64, 66))
        sbank = (pb[0], pb[1], pb[7])
        nq_all = NTOK if with_ctx_q else TOK

        def load_head(hd):
            kT, vv, qh, agh = kTb[hd % 2], vvb[hd % 2], qhb[hd % 2], aghb[hd % 2]
            fw.dma("sp", V(kT.t[:, 0:8192].rearrange("p (r c) -> p r c", r=4), kT.whole),
                   V(kv_dst[hd].t[:, 0:2048].rearrange("(r p) c -> p r c", p=128), kv_dst[hd].whole))
            fw.dma("sp", kT[:, 8192:8448], V(kc_s.t[hd], [kc_s.part(16).res, kc_s.part(17).res]))
            fw.dma("sp", V(vv.t[:, 0:64, :].rearrange("p (r c) e -> p r c e", r=4), vv.whole),
                   V(kv_dst[hd].t[:, 2048:4096].rearrange("(r p) (c e) -> p r c e", p=128, e=128), kv_dst[hd].whole))
            fw.dma("sp", vv[:, 64:66, :], V(vc_s.t[:, hd * 128:(hd + 1) * 128].rearrange("(c p) e -> p c e", p=128),
                                           [vc_s.part(16).res, vc_s.part(17).res]))
            qres = [qT_s.part(c).res for c in range(NLC if with_ctx_q else NCH)]
            fw.dma("sp", qh[:, 0:nq_all], V(qT_s.t[hd, :, 0:nq_all], qres))
            fw.dma("sp", agh[:, 0:NTOK], V(agT_s.t[hd], [agT_s.part(c).res for c in range(NLC)]))

        load_head(0)
        pT2 = [fw.carve(f"pTT{i}", [128, 2, 512], BF16) for i in range(3)]
        acc2 = [fw.carve(f"accT{i}", [128, 2, 512], F32) for i in range(2)]
        stage_banks = ((0, 1), (4, 5))
        for hd in range(4):
            if hd + 1 < 4:
                load_head(hd + 1)
            kT, vv, qh, agh = kTb[hd % 2], vvb[hd % 2], qhb[hd % 2], aghb[hd % 2]
            for qi, (q0, nq_, kc0, kc1) in enumerate(qblocks):
                kcs = list(range(kc0, kc1))
                n = len(kcs)

                def qk(i):
                    kc = kcs[i]
                    b0, b1 = stage_banks[i % 2]
                    for m, bk in ((0, b0), (1, b1)):
                        fw.mm(pb[bk][:, 0:nq_], kT[m * 64:(m + 1) * 64, kc * 128:(kc + 1) * 128], qh[m * 64:(m + 1) * 64, q0:q0 + nq_], True, True)

                qk(0)
                for i, kc in enumerate(kcs):
                    if i + 1 < n:
                        qk(i + 1)
                    b0, b1 = stage_banks[i % 2]
                    p = pT2[i % 3]
                    sc2 = V(pb_t[:, b0:b0 + 2, 0:nq_], [pb[b0].whole, pb[b1].whole])
                    fw.act(p[:, :, 0:nq_], sc2, AF.Exp, scale=0.125)
                    first, lastk = (i == 0), (i == n - 1)
                    for m in range(2):
                        fw.mm(pb[2 + m][:, 0:nq_], vv[:, kc, :], p[:, m, 0:nq_], start=first, stop=lastk)
                    eng_ = "dve" if i % 2 == 0 else "pool"
                    acc_ = acc2[i % 2]
                    if i < 2:
                        fw.copy(acc_[:, :, 0:nq_], p[:, :, 0:nq_], eng=eng_)
                    else:
                        fw.tt(acc_[:, :, 0:nq_], acc_[:, :, 0:nq_], p[:, :, 0:nq_], ALU.add, eng=eng_)
                if n > 1:
                    fw.tt(acc2[0][:, :, 0:nq_], acc2[0][:, :, 0:nq_], acc2[1][:, :, 0:nq_], ALU.add)
                for m in range(2):
                    fw.mm(pb[7][:, 0:nq_], ones_f[:], acc2[0][:, m, 0:nq_], True, True)
                    fw.recip(rec[:, 0:nq_], pb[7][:, 0:nq_])
                    fw.tt(om[m][:, 0:nq_], pb[2 + m][:, 0:nq_], rec[:, 0:nq_], ALU.mult)
                fw.stt(A[:, 0:nq_], om[1][:, 0:nq_], neglam[:], om[0][:, 0:nq_], ALU.mult, ALU.add)
                fw.act(sqb[:, 0:nq_], A[:, 0:nq_], AF.Square)
                fw.mm(pb[6][:, 0:nq_], ones_bf[:], sqb[:, 0:nq_], True, True)
                fw.act(rstd[:, 0:nq_], pb[6][:, 0:nq_], AF.Sqrt, bias=eps_t[:], scale=1.0 / 128)
                fw.recip(rstd[:, 0:nq_], rstd[:, 0:nq_])
                fw.stt(A[:, 0:nq_], A[:, 0:nq_], subln[:], rstd[:, 0:nq_], ALU.mult, ALU.mult)
                mo = mixo[qi % 2]
                fw.tt(mo[:, 0:nq_], A[:, 0:nq_], agh[:, q0:q0 + nq_], ALU.mult, eng="pool")
                fw.dma("act", V(mixT_s.t[hd, :, q0:q0 + nq_], mixT_s.part((hd, qi)).res), mo[:, 0:nq_])
        fw.phase_reset()

    def phase_halo(l):
        xr = lambda c: [xbc_pad.part((f"xbc{i}", c)).res for i in range(3)]
        fw.dma("sp", hx_stage[0:1, :], V(xbc_pad.t[1:2, :], xr(0)))
        fw.dma("sp", hx_stage[1:2, :], V(xbc_pad.t[TOK:TOK + 1, :], xr(NCH - 1)))
        fw.dma("sp", V(hx_src.t[bass.ds(rank * 2, 2), :], hx_src.whole), hx_stage[:, :])
        fw.collective("AllReduce", ALU.add, GROUPS, hx_src[:, :], hx_dst[:, :])
        fw.dma("sp", hxL[1:9, :], hx_dst[:, :])
        fw.dma("sp", hxR[0:6, :], hx_dst[2:8, :])
        fw.dma("sp", V(xbc_pad.t[0:1, :], xbc_pad.part("hl").res), V(hxL.t[bass.ds(rank * 2, 1), :], hxL.whole))
        fw.dma("sp", V(xbc_pad.t[TOK + 1:TOK + 2, :], xbc_pad.part("hr").res), V(hxR.t[bass.ds(rank * 2, 1), :], hxR.whole))

    def rope_apply(out, x, lc, nh, t1, t2):
        cosb = V(rope.t[:, lc, 0:1, :].to_broadcast([128, nh, 32]), rope.whole)
        sinb = V(rope.t[:, lc, 1:2, :].to_broadcast([128, nh, 32]), rope.whole)
        fw.tt(t1[:, 0:nh, :], x[:, :, 0:32], cosb, ALU.mult)
        fw.tt(t2[:, 0:nh, :], x[:, :, 32:64], sinb, ALU.mult, eng="pool")
        fw.tt(out[:, :, 0:32], t1[:, 0:nh, :], t2[:, 0:nh, :], ALU.subtract)
        fw.tt(t1[:, 0:nh, :], x[:, :, 0:32], sinb, ALU.mult)
        fw.tt(t2[:, 0:nh, :], x[:, :, 32:64], cosb, ALU.mult, eng="pool")
        fw.tt(out[:, :, 32:64], t1[:, 0:nh, :], t2[:, 0:nh, :], ALU.add)

    def chain_combine(Sin, Sctx, d, col0, ncol, nparts, Aexp_of, tmp, slot):
        fw.copy(Sin[0:nparts, :], Sctx[0:nparts, :])
        order = range(4) if d == 0 else range(3, -1, -1)
        for sidx in order:
            fw.dma("sp", slot[0:nparts, 0:ncol], st_dst[d][sidx * 128:sidx * 128 + nparts, col0:col0 + ncol])
            Aexp_of(sidx, tmp)
            fw.tt(tmp[0:nparts, :], tmp[0:nparts, :], slot[0:nparts, 0:ncol], ALU.add)
            fw.tt(tmp[0:nparts, :], tmp[0:nparts, :], Sin[0:nparts, :], ALU.subtract)
            mcol = cmask[:, d * 4 + sidx:d * 4 + sidx + 1]
            fw.stt(Sin[0:nparts, :], tmp[0:nparts, :], V(mcol.ap[0:nparts], mcol.res), Sin[0:nparts, :], ALU.mult, ALU.add)

    def phase_ret(l, need_ctx, part):
        rett = fw.carve("rett", [128, 4 * 128 + 24], F32)
        rnorm = fw.carve("rnorm", [128, 128], F32)
        fw.dma("sp", rett[:], rett_d[:])
        fw.dma("sp", rnorm[:], ssdp_d[l, :, NP - 128:NP])
        Dret = V(rett.t[:, 0:512].rearrange("p (h i) -> p h i", i=128), rett.whole)
        tab = lambda k: V(rett.t[:, 512 + 4 * k:512 + 4 * k + 4], rett.whole)
        par = [0]
        alt = lambda n, sh, dt_: Alt([fw.carve(f"{n}_{i}", sh, dt_) for i in range(2)], par)
        qk = alt("qk", [128, 8, 64], F32)
        qkr = alt("qkr", [128, 8, 64], F32)
        t1 = alt("rt1", [128, 8, 32], F32)
        t2 = alt("rt2", [128, 8, 32], F32)
        rv = alt("rv", [128, 512], F32)
        rvbf = alt("rvbf", [128, 512], BF16)
        kte = [alt(f"kte{d}", [128, 4, 64], BF16) for d in range(2)]
        q3 = alt("q3", [128, 3, 4, 64], BF16)
        kbf = alt("kbf", [128, 4, 64], BF16)
        qT = alt("qT", [64, 3, 4, 128], BF16)
        kT = alt("kT", [64, 4, 128], BF16)
        Wt = alt("Wt", [128, 4, 128], BF16)
        R = [fw.carve(f"R{d}", [64, 512], F32) for d in range(2)]
        Rbf = [fw.carve(f"Rbf{d}", [64, 512], BF16) for d in range(2)]
        Rctx = [fw.carve(f"Rctx{d}", [64, 512], F32) for d in range(2)]
        Pb = fw.carve("Pb", [128, 4], F32)
        cs_sb = [alt(f"cs_sb{d}", [64, 512], F32) for d in range(2)]
        tmp = fw.carve("rtmp", [64, 512], F32)
        slot = fw.carve("rslot", [64, 512], F32)
        rg = alt("rg", [128, 512], F32)
        ysq = alt("ysq", [128, 4, 128], F32)
        yss = alt("yss", [128, 4], F32)
        yn = alt("yn", [128, 4, 128], F32)
        ybf = alt("ybf", [128, 512], BF16)
        ytb = alt("ytb", [128, 4, 128], BF16)
        bc4 = lambda v, n: V(v.ap.unsqueeze(2).to_broadcast([v.ap.shape[0], 4, n]), v.res)

        def prep(lc):
            par[0] = lc
            is_ctx = lc >= NCH
            rows = slice(lc * 128, (lc + 1) * 128)
            pr = lambda n: proj_s.part((n, lc)).res
            fw.dma("sp", V(qk.t[:].rearrange("p a b -> p (a b)"), qk.whole), V(proj_s.t[rows, 4640:5152], pr("rqk")))
            fw.dma("sp", rv[:], V(proj_s.t[rows, 5152:5664], pr("rv")))
            if is_ctx:
                src = qk
            else:
                rope_apply(qkr, qk, lc, 8, t1, t2)
                src = qkr
            fw.copy(rvbf[:], rv[:], eng="pool")
            for d in range(2):
                fw.tt(kte[d][:], src[:, 4:8, :], bc4(tab(d), 64), ALU.mult)
            return src

        def chunk_states(start_banks=(0, 1)):
            for d in range(2):
                bank = pb[start_banks[d]]
                for h in range(4):
                    fw.mm(bank[0:64, h * 128:(h + 1) * 128], kte[d][:, h, :], rvbf[:, h * 128:(h + 1) * 128],
                          start=(h == 0), stop=(h == 3), skip_group_check=True)
            return [pb[start_banks[0]], pb[start_banks[1]]]

        A128 = [tab(4), tab(5)]

        def fold(acc, csb, lc, store):
            fw.tt(V(acc[0].t[:].rearrange("p (h n) -> p h n", n=128), acc[0].whole),
                  V(acc[0].t[:].rearrange("p (h n) -> p h n", n=128), acc[0].whole),
                  V(A128[0].ap[0:64].unsqueeze(2).to_broadcast([64, 4, 128]), A128[0].res), ALU.mult)
            fw.tt(acc[0][:], acc[0][:], csb[0][0:64, :], ALU.add)
            fw.tt(V(tmp.t[:].rearrange("p (h n) -> p h n", n=128), tmp.whole),
                  V(csb[1].t[0:64, :].rearrange("p (h n) -> p h n", n=128), csb[1].whole),
                  V(Pb.t[0:64, :].unsqueeze(2).to_broadcast([64, 4, 128]), Pb.whole), ALU.mult)
            fw.tt(acc[1][:], acc[1][:], tmp[:], ALU.add)
            fw.tt(Pb[:], Pb[:], A128[1], ALU.mult)
            if store:
                for d in range(2):
                    fw.copy(cs_sb[d][:], csb[d][0:64, :])
                    fw.dma("act", V(cst_s.t[lc, d, 0:64, 1024:1536], cst_s.part(("r", lc, d)).res), cs_sb[d][:])

        for grp in (((16, 17), tuple(range(NCH))) if part == "a" else ()):
            for d in range(2):
                fw.memset(R[d][:], 0.0)
            fw.memset(Pb[:], 1.0)
            for lc in grp:
                src_ = prep(lc)
                if lc < NCH or need_ctx:
                    fw.dma("sp", V(rc_f.t[lc], rc_f.part(lc).res), V(src_.t[:].rearrange("p a b -> p (a b)"), src_.whole))
                    fw.dma("sp", V(rc_h.t[lc, :, 0:512], rc_h.part(lc).res), rvbf[:])
                    for d in range(2):
                        fw.dma("sp", V(rc_h.t[lc, :, 512 + d * 256:768 + d * 256], rc_h.part(lc).res),
                               V(kte[d].t[:].rearrange("p a b -> p (a b)"), kte[d].whole))
                if KCUT >= 2:
                    csb = chunk_states()
                if KCUT >= 3:
                    fold(R, csb, lc, KCUT >= 4)
            if grp[0] == 16:
                for d in range(2):
                    fw.copy(Rctx[d][:], R[d][:])
        if "ret_sf" in dbg:
            fw.dma("sp", dbg["ret_sf"][:, :], Rctx[0][:])
            fw.dma("sp", dbg["ret_sb"][:, :], Rctx[1][:])
        if stop_after == "ret_p1":
            fw.phase_reset(); return
        if part == "a":
            for d in range(2):
                fw.dma("sp", st_stage[d][0:64, 1024:1536], R[d][:])
                fw.dma("sp", V(rctx_s.t[d], rctx_s.whole), Rctx[d][:])
            fw.phase_reset()
            return
        for d in range(2):
            fw.dma("sp", Rctx[d][:], V(rctx_s.t[d], rctx_s.whole))
        A2048 = [[(1.0 - 2.0 ** -e) ** 2048 for e in RET_EXP_F], [(1.0 - 2.0 ** -e) ** 2048 for e in RET_EXP_B]]
        Rin = [fw.carve(f"Rin{d}", [64, 512], F32) for d in range(2)]
        for d in range(2):
            def aexp(sidx, t_, d=d):
                for h in range(4):
                    fw.ts(t_[0:64, h * 128:(h + 1) * 128], Rin[d][0:64, h * 128:(h + 1) * 128], float(A2048[d][h]), ALU.mult)
            chain_combine(Rin[d], Rctx[d], d, 1024, 512, 64, aexp, tmp, slot)
        snap = fw.carve("rsnap", [64, 512], BF16)
        csl = fw.carve("rcsl", [64, 512], F32)
        fw.copy(R[1][:], Rin[1][:])
        for lc in range(NCH - 1, -1, -1):
            fw.copy(snap[:], R[1][:])
            fw.dma("act", V(sb_s.t[lc, 0:64, 1024:1536], sb_s.part(("r", lc)).res), snap[:])
            fw.dma("sp", csl[:], V(cst_s.t[lc, 1, 0:64, 1024:1536], cst_s.part(("r", lc, 1)).res))
            fw.tt(V(R[1].t[:].rearrange("p (h n) -> p h n", n=128), R[1].whole),
                  V(R[1].t[:].rearrange("p (h n) -> p h n", n=128), R[1].whole),
                  V(A128[1].ap[0:64].unsqueeze(2).to_broadcast([64, 4, 128]), A128[1].res), ALU.mult)
            fw.tt(R[1][:], R[1][:], csl[:], ALU.add)
        if need_ctx:
            fw.dma("sp", csl[:], V(cst_s.t[17, 1, 0:64, 1024:1536], cst_s.part(("r", 17, 1)).res))
            fw.copy(snap[:], csl[:])
            fw.dma("sp", V(sb_s.t[16, 0:64, 1024:1536], sb_s.part(("r", 16)).res), snap[:])
            snap0 = fw.carve("rsnap0", [64, 512], BF16)
            fw.memset(snap0[:], 0.0)
            fw.dma("sp", V(sb_s.t[17, 0:64, 1024:1536], sb_s.part(("r", 17)).res), snap0[:])
        if stop_after == "ret_p2":
            fw.phase_reset(); return
        groups3 = [tuple(range(NCH))] + ([(16, 17)] if need_ctx else [])
        for grp in groups3:
            if grp[0] == 16:
                fw.memset(R[0][:], 0.0)
            else:
                fw.copy(R[0][:], Rin[0][:])
            for lc in grp:
                is_ctx = lc >= NCH
                rows = slice(lc * 128, (lc + 1) * 128)
                par[0] = lc
                src = qkr
                fw.dma("sp", V(qkr.t[:].rearrange("p a b -> p (a b)"), qkr.whole), V(rc_f.t[lc], rc_f.part(lc).res))
                fw.dma("sp", rvbf[:], V(rc_h.t[lc, :, 0:512], rc_h.part(lc).res))
                for d in range(2):
                    fw.dma("sp", V(kte[d].t[:].rearrange("p a b -> p (a b)"), kte[d].whole),
                           V(rc_h.t[lc, :, 512 + d * 256:768 + d * 256], rc_h.part(lc).res))
                fw.dma("sp", rg[:], V(proj_s.t[rows, 5664:6176], proj_s.part(("rg", lc)).res))
                fw.dma("sp", Rbf[1][:], V(sb_s.t[lc, 0:64, 1024:1536], sb_s.part(("r", lc)).res))
                fw.copy(Rbf[0][:], R[0][:])
                fw.copy(q3[:, 0, :, :], src[:, 0:4, :])
                fw.tt(q3[:, 1, :, :], src[:, 0:4, :], bc4(tab(2), 64), ALU.mult)
                fw.tt(q3[:, 2, :, :], src[:, 0:4, :], bc4(tab(3), 64), ALU.mult, eng="pool")
                fw.copy(kbf[:], src[:, 4:8, :])
                tqa = pbh[2]
                tqav = tqa.t[0:64, 0:1024].rearrange("p (k h c) -> p k h c", k=2, h=4)
                tqb = pbh[3]
                tqbv = tqb.t[0:64, 0:512].rearrange("p (h c) -> p h c", h=4)
                for k3 in range(2):
                    for h in range(4):
                        fw.transpose(V(tqav[:, k3, h, :], tqa.whole), q3[:, k3, h, :], ident_bf[:], last=(k3 == 1 and h == 3))
                for h in range(4):
                    fw.transpose(V(tqbv[:, h, :], tqb.whole), q3[:, 2, h, :], ident_bf[:], last=(h == 3))
                fw.copy(qT[:, 0:2, :, :], V(tqav, tqa.whole))
                fw.copy(qT[:, 2, :, :], V(tqbv, tqb.whole))
                tk = pbh[4]
                tkv = tk.t[0:64, 0:512].rearrange("p (h c) -> p h c", h=4)
                for h in range(4):
                    fw.transpose(V(tkv[:, h, :], tk.whole), kbf[:, h, :], ident_bf[:], last=(h == 3))
                fw.copy(kT[:], V(tkv, tk.whole))
                sc = pb[5]
                for h in range(4):
                    fw.mm(sc[:, h * 128:(h + 1) * 128], kT[:, h, :], qT[:, 0, h, :], start=(h == 0), stop=(h == 3), skip_group_check=True)
                fw.tt(Wt[:], V(sc.t[:, :].rearrange("p (h i) -> p h i", i=128), sc.whole), Dret, ALU.mult)
                csb = chunk_states((0, 1))
                yb = pb[6]
                for h in range(4):
                    o = yb[:, h * 128:(h + 1) * 128]
                    fw.mm(o, Wt[:, h, :], rvbf[:, h * 128:(h + 1) * 128], start=(h == 0), stop=False, last=False, skip_group_check=True)
                    fw.mm(o, qT[:, 1, h, :], Rbf[0][:, h * 128:(h + 1) * 128], start=False, stop=False, last=False, skip_group_check=True)
                    fw.mm(o, qT[:, 2, h, :], Rbf[1][:, h * 128:(h + 1) * 128], start=False, stop=(h == 3), last=(h == 3), skip_group_check=True)
                fw.tt(V(R[0].t[:].rearrange("p (h n) -> p h n", n=128), R[0].whole),
                      V(R[0].t[:].rearrange("p (h n) -> p h n", n=128), R[0].whole),
                      V(A128[0].ap[0:64].unsqueeze(2).to_broadcast([64, 4, 128]), A128[0].res), ALU.mult)
                fw.tt(R[0][:], R[0][:], csb[0][0:64, :], ALU.add)
                ybv = V(yb.t[:, :].rearrange("p (h n) -> p h n", n=128), yb.whole)
                fw.act(ysq[:], ybv, AF.Square)
                fw.reduce(yss[:], ysq[:], ALU.add)
                fw.act(yss[:], yss[:], AF.Sqrt, bias=eps_t[:], scale=1.0 / 128)
                fw.recip(yss[:], yss[:])
                fw.tt(yn[:], ybv, V(yss.t[:].unsqueeze(2).to_broadcast([128, 4, 128]), yss.whole), ALU.mult)
                fw.tt(yn[:], yn[:], V(rnorm.t[:].unsqueeze(1).to_broadcast([128, 4, 128]), rnorm.whole), ALU.mult, eng="pool")
                fw.act(rg[:], rg[:], AF.Silu)
                fw.tt(ybf[:], V(yn.t[:].rearrange("p h n -> p (h n)"), yn.whole), rg[:], ALU.mult)
                to = pbh[7]
                tov = to.t[:, 0:512].rearrange("p (h c) -> p h c", h=4)
                for h in range(4):
                    fw.transpose(V(tov[:, h, :], to.whole), ybf[:, h * 128:(h + 1) * 128], ident_bf[:], last=(h == 3))
                fw.copy(ytb[:], V(tov, to.whole))
                fw.dma("act", V(mixT_s.t[12:16, :, rows].rearrange("h p c -> p h c"), mixT_s.part(("ret", lc)).res), ytb[:])
        fw.phase_reset()

    def phase_ssd(l, need_ctx):
        OW, OB, ODT, OA, ODD = 0, 4608, 6144, 6176, 6208
        prm = fw.carve("prm", [128, 6224], F32)
        fw.dma("sp", prm[:], ssdp_d[l, :, 0:6224])
        tri = fw.carve("tri", [128, 5, 128], F32)
        fw.dma("sp", tri[:], tri_d[:])
        ssdn = fw.carve("ssdn", [128, 8], F32)
        fw.dma("sp", ssdn[:], ssdn_d[l])
        negA = fw.carve("negA", [128, 32], F32)
        fw.act(negA[:], prm[:, OA:OA + 32], AF.Exp)
        fw.ts(negA[:], negA[:], -1.0, ALU.mult)
        one_t = fw.carve("one_t", [128, 1], F32)
        fw.memset(one_t[:], 1.0)
        U = [fw.carve(f"U{i}", [128, 1536], F32) for i in range(3)]
        dtr = fw.carve("dtr", [128, 32], F32)
        la = fw.carve("la", [128, 32], F32)
        E = fw.carve("E", [128, 96], F32)
        praw = fw.carve("praw", [128, 96], F32)
        cumraw = Buf(fw, "cumraw", _APHandle(praw.t[:, 0:32]))
        cumraw.whole = praw.whole
        tots = fw.carve("tots", [128, 32], F32)
        v = [fw.carve(f"v{d}", [128, 1024], BF16) for d in range(2)]
        vte = [fw.carve(f"vte{d}", [128, 1024], BF16) for d in range(2)]
        BCbf = fw.carve("BCbf", [128, 512], BF16)
        BCT = fw.carve("BCT", [128, 4, 128], BF16)
        zt = fw.carve("zt", [128, 1024], F32)
        R1 = fw.carve("R1", [128, 16, 128], F32)
        seg = fw.carve("seg", [128, 16, 128], F32)
        Dm = fw.carve("Dm", [128, 16, 128], BF16)
        Sm = [fw.carve(f"Sm{d}", [128, 2, 128], F32) for d in range(2)]
        Wt = fw.carve("Wt", [128, 16, 128], BF16)
        S = [fw.carve(f"S{d}", [128, 1024], F32) for d in range(2)]
        Sx = [fw.carve(f"Sx{d}", [128, 1024], F32) for d in range(2)]
        Sbf = [fw.carve(f"Sbf{d}", [128, 1024], BF16) for d in range(2)]
        Pb = fw.carve("Pb", [128, 16], F32)
        yt = fw.carve("yt", [128, 1024], F32)
        y2 = fw.carve("y2", [128, 1024], F32)
        vtmp = Buf(fw, "vtmp", _APHandle(y2.t[:].rearrange("p (h n) -> p h n", n=64)))
        vtmp.whole = y2.whole
        gss = fw.carve("gss", [128, 2], F32)
        ybf = fw.carve("ybf", [128, 1024], BF16)
        ytb = fw.carve("ytb", [128, 8, 128], BF16)
        Aex = fw.carve("Aex", [128, 16], F32)
        h16 = lambda vv: V(vv.ap.unsqueeze(2).to_broadcast([128, 16, 64]), vv.res)
        as16 = lambda b_: V(b_.t[:].rearrange("p (h n) -> p h n", n=64), b_.whole)

        def prep(lc):
            is_ctx = lc >= NCH
            src_t, r0 = (xbc_cpad, (lc - NCH) * 128) if is_ctx else (xbc_pad, lc * 128)
            rr = [xbc_pad.part("hl").res, xbc_pad.part("hr").res]
            for k in range(3):
                fw.dma("sp", U[k][:], V(src_t.t[r0 + k:r0 + k + 128, :], rr))
            fw.dma("sp", dtr[:], V(proj_s.t[lc * 128:(lc + 1) * 128, 3584:3616], proj_s.part(("dtr", lc)).res))
            fw.tt(U[0][:], U[0][:], prm[:, OW:OW + 1536], ALU.mult, eng="pool")
            fw.tt(U[1][:], U[1][:], prm[:, OW + 1536:OW + 3072], ALU.mult)
            fw.tt(U[2][:], U[2][:], prm[:, OW + 3072:OW + 4608], ALU.mult, eng="pool")
            fw.tt(U[1][:], U[1][:], U[0][:], ALU.add)
            fw.tt(U[1][:], U[1][:], U[2][:], ALU.add)
            fw.tt(U[1][:], U[1][:], prm[:, OB:OB + 1536], ALU.add)
            fw.act(U[0][:], U[1][:], AF.Silu)
            fw.tt(dtr[:], dtr[:], prm[:, ODT:ODT + 32], ALU.add)
            fw.act(dtr[:], dtr[:], AF.Exp)
            fw.act(dtr[:], dtr[:], AF.Ln, bias=one_t[:])
            fw.tt(la[:], dtr[:], negA[:], ALU.mult)
            pe = pb[0]
            for i, (w, c0, c1) in enumerate(((0, 0, 16), (1, 16, 32), (2, 0, 16), (3, 16, 32), (4, 0, 32))):
                o0 = (0, 16, 32, 48, 64)[i]
                fw.mm(pe[:, o0:o0 + (c1 - c0)], tri[:, w, :], la[:, c0:c1], start=(i == 0), stop=(i == 4), skip_group_check=True)
            fw.copy(praw[:], pe[:, 0:96])
            fw.act(E[:], praw[:], AF.Exp)
            fw.tt(tots[:], tots[:], praw[:, 64:96], ALU.add)
            xs = V(U[0].t[:, 0:1024].rearrange("p (h n) -> p h n", n=64), U[0].whole)
            for d in range(2):
                fw.tt(vtmp[:], xs, h16(dtr[:, d * 16:(d + 1) * 16]), ALU.mult)
                fw.copy(as16(v[d]), vtmp[:], eng="pool")
                fw.tt(as16(vte[d]), vtmp[:], h16(E[:, 32 + d * 16:48 + d * 16]), ALU.mult)
            fw.copy(BCbf[:], U[0][:, 1024:1536], eng="pool")

        def chunk_state(d):
            banks = (pb[4], pb[5])
            for g in range(2):
                fw.mm(banks[g][:, :], BCbf[:, g * 128:(g + 1) * 128], vte[d][:, g * 512:(g + 1) * 512], True, True)
            return banks

        def mulA(dst, srcS, acol):
            fw.tt(as16(dst), as16(srcS), h16(acol), ALU.mult)

        for grp in ((16, 17), tuple(range(NCH))):
            for d in range(2):
                fw.memset(S[d][:], 0.0)
            fw.memset(Pb[:], 1.0)
            fw.memset(tots[:], 0.0)
            for lc in grp:
                prep(lc)
                if lc < NCH or need_ctx:
                    pr_ = pc_h.part(lc).res
                    fw.dma("sp", V(pc_h.t[lc, :, 0:1024], pr_), v[0][:])
                    fw.dma("sp", V(pc_h.t[lc, :, 1024:2048], pr_), v[1][:])
                    fw.dma("sp", V(pc_h.t[lc, :, 2048:3072], pr_), vte[0][:])
                    fw.dma("sp", V(pc_h.t[lc, :, 3072:3584], pr_), BCbf[:])
                    pf_ = pc_f.part(lc).res
                    fw.dma("sp", V(pc_f.t[lc, :, 0:1024], pf_), U[0][:, 0:1024])
                    fw.dma("sp", V(pc_f.t[lc, :, 1024:1120], pf_), praw[:])
                    fw.dma("sp", V(pc_f.t[lc, :, 1120:1152], pf_), la[:])
                for d in range(2):
                    banks = chunk_state(d)
                    csv = V(pb_t[:, 4:6, :], [pb[4].whole, pb[5].whole])
                    cs_sb = V(seg.t[:, d * 8:(d + 1) * 8, :].rearrange("p a (g c) -> p (a g) c", g=2)[:, 0:2, :] if False else seg.t[:, d * 8:(d + 1) * 8, :], seg.whole)
                    cs_flat = V(seg.t[:].rearrange("p h n -> p (h n)")[:, d * 1024:(d + 1) * 1024], seg.whole)
                    fw.copy(V(cs_flat.ap.rearrange("p (g c) -> p g c", g=2), seg.whole), csv)
                    fw.dma("sp", V(cst_s.t[lc, d, :, 0:1024], cst_s.part(("s", lc, d)).res), cs_flat)
                    if d == 0:
                        mulA(S[0], S[0], E[:, 64:80])
                        fw.tt(S[0][:], S[0][:], cs_flat, ALU.add)
                    else:
                        fw.tt(as16(y2), V(cs_flat.ap.rearrange("p (h n) -> p h n", n=64), seg.whole), h16(Pb[:, :]), ALU.mult)
                        fw.tt(S[1][:], S[1][:], y2[:], ALU.add)
                        fw.tt(Pb[:], Pb[:], E[:, 80:96], ALU.mult)
                fw.dma("sp", V(cst_s.t[lc, 0, :, 1536:1568], cst_s.part(("e", lc)).res), E[:, 64:96])
            if grp[0] == 16:
                for d in range(2):
                    fw.copy(Sx[d][:], S[d][:])
        if "ssd_sf" in dbg:
            fw.dma("sp", dbg["ssd_sf"][:, :], Sx[0][:])
            fw.dma("sp", dbg["ssd_sb"][:, :], Sx[1][:])
        for d in range(2):
            fw.dma("sp", st_stage[d][:, 0:1024], S[d][:])
            fw.dma("sp", st_stage[d][:, 1536:1552], tots[:, d * 16:(d + 1) * 16])
            fw.dma("sp", V(st_src[d].t[bass.ds(rank * 128, 128), :], st_src[d].whole), st_stage[d][:, :])
            fw.collective("AllReduce", ALU.add, GROUPS, st_src[d][:, :], st_dst[d][:, :])
        for d in range(2):
            order = range(4) if d == 0 else range(3, -1, -1)
            for sidx in order:
                fw.dma("sp", yt[:], st_dst[d][sidx * 128:(sidx + 1) * 128, 0:1024])
                fw.dma("sp", Aex[:], st_dst[d][sidx * 128:(sidx + 1) * 128, 1536:1552])
                fw.act(Aex[:], Aex[:], AF.Exp)
                mulA(y2, Sx[d], Aex[:, :])
                fw.tt(y2[:], y2[:], yt[:], ALU.add)
                fw.tt(y2[:], y2[:], Sx[d][:], ALU.subtract)
                fw.stt(Sx[d][:], y2[:], cmask[:, d * 4 + sidx:d * 4 + sidx + 1], Sx[d][:], ALU.mult, ALU.add)
        for lc in range(NCH - 1, -1, -1):
            fw.copy(Sbf[1][:], Sx[1][:])
            fw.dma("sp", V(sb_s.t[lc, :, 0:1024], sb_s.part(("s", lc)).res), Sbf[1][:])
            ld = yt if lc % 2 == 0 else y2
            fw.dma("sp", ld[:], V(cst_s.t[lc, 1, :, 0:1024], cst_s.part(("s", lc, 1)).res))
            fw.dma("sp", Aex[:], V(cst_s.t[lc, 0, :, 1552:1568], cst_s.part(("e", lc)).res))
            mulA(Sx[1], Sx[1], Aex[:, :])
            fw.tt(Sx[1][:], Sx[1][:], ld[:], ALU.add)
        if need_ctx:
            fw.dma("sp", yt[:], V(cst_s.t[17, 1, :, 0:1024], cst_s.part(("s", 17, 1)).res))
            fw.copy(Sbf[1][:], yt[:])
            fw.dma("sp", V(sb_s.t[16, :, 0:1024], sb_s.part(("s", 16)).res), Sbf[1][:])
            fw.memset(Sbf[0][:], 0.0)
            fw.dma("sp", V(sb_s.t[17, :, 0:1024], sb_s.part(("s", 17)).res), Sbf[0][:])
        if stop_after == "ssd_p2":
            fw.phase_reset(); return
        groups3 = [tuple(range(NCH))] + ([(16, 17)] if need_ctx else [])
        for grp in groups3:
            if grp[0] == 16:
                fw.memset(Sx[0][:], 0.0)
            for lc in grp:
                rows = slice(lc * 128, (lc + 1) * 128)
                pr_ = pc_h.part(lc).res
                pf_ = pc_f.part(lc).res
                fw.dma("sp", v[0][:], V(pc_h.t[lc, :, 0:1024], pr_))
                fw.dma("sp", v[1][:], V(pc_h.t[lc, :, 1024:2048], pr_))
                fw.dma("sp", vte[0][:], V(pc_h.t[lc, :, 2048:3072], pr_))
                fw.dma("sp", BCbf[:], V(pc_h.t[lc, :, 3072:3584], pr_))
                fw.dma("sp", U[0][:, 0:1024], V(pc_f.t[lc, :, 0:1024], pf_))
                fw.dma("sp", praw[:], V(pc_f.t[lc, :, 1024:1120], pf_))
                fw.dma("sp", la[:], V(pc_f.t[lc, :, 1120:1152], pf_))
                fw.act(E[:], praw[:], AF.Exp)
                fw.dma("sp", zt[:, 0:512], V(proj_s.t[rows, 3616:4128], proj_s.part(("z0", lc)).res))
                fw.dma("sp", zt[:, 512:1024], V(proj_s.t[rows, 4128:4640], proj_s.part(("z1", lc)).res))
                fw.dma("sp", Sbf[1][:], V(sb_s.t[lc, :, 0:1024], sb_s.part(("s", lc)).res))
                fw.copy(Sbf[0][:], Sx[0][:], eng="pool")
                tb_ = pbh[1]
                tbv = tb_.t[:, 0:512].rearrange("p (a c) -> p a c", a=4)
                for a in range(4):
                    fw.transpose(V(tbv[:, a, :], tb_.whole), BCbf[:, a * 128:(a + 1) * 128], ident_bf[:], last=(a == 3))
                fw.copy(BCT[:], V(tbv, tb_.whole))
                sc = pb[1]
                scv = sc.t[:, 256:512].rearrange("p (g i) -> p g i", g=2)
                for g in range(2):
                    fw.mm(V(scv[:, g, :], sc.whole), BCT[:, g, :], BCT[:, 2 + g, :], start=False if False else (g == 0), stop=(g == 1), skip_group_check=True)
                for d in range(2):
                    fw.tt(Sm[d][:], V(scv, sc.whole), V(tri.t[:, d:d + 1, :].to_broadcast([128, 2, 128]), tri.whole), ALU.mult)
                yb = (pb[6], pb[7])
                for d in range(2):
                    fw.tt(R1[:], V(la.t[:, d * 16:(d + 1) * 16].unsqueeze(2).to_broadcast([128, 16, 128]), la.whole),
                          V(tri.t[:, d:d + 1, :].to_broadcast([128, 16, 128]), tri.whole), ALU.mult, eng="pool")
                    for q in range(4):
                        bank = pb[2 + q % 2]
                        fw.mm(bank[:, :], tri[:, 4, :], V(R1.t[:, 4 * q:4 * q + 4, :].rearrange("p h n -> p (h n)"), R1.whole), True, True)
                        for hh in range(4):
                            h = 4 * q + hh
                            fw.ts(seg[:, h, :], bank[:, hh * 128:(hh + 1) * 128], cumraw[:, d * 16 + h:d * 16 + h + 1], ALU.subtract, 0.0, ALU.min)
                    fw.act(Dm[:], seg[:], AF.Exp)
                    for g in range(2):
                        fw.tt(Wt[:, g * 8:(g + 1) * 8, :], Dm[:, g * 8:(g + 1) * 8, :],
                              V(Sm[d].t[:, g:g + 1, :].to_broadcast([128, 8, 128]), Sm[d].whole), ALU.mult)
                    for h in range(16):
                        fw.mm(yb[h // 8][:, (h % 8) * 64:(h % 8 + 1) * 64], Wt[:, h, :], v[d][:, h * 64:(h + 1) * 64],
                              start=(d == 0 and h % 8 == 0), stop=(d == 1 and h % 8 == 7), last=(d == 1 and h % 8 == 7), skip_group_check=True)
                for d in range(2):
                    for g in range(2):
                        fw.mm(pb[2 + g][:, :], BCT[:, 2 + g, :], Sbf[d][:, g * 512:(g + 1) * 512], True, True)
                    ysv = V(pb_t[:, 2:4, :].rearrange("p a (h n) -> p (a h) n", n=64), [pb[2].whole, pb[3].whole])
                    fw.tt(as16(yt if d == 0 else y2), ysv, h16(E[:, d * 16:(d + 1) * 16]), ALU.mult)
                fw.tt(yt[:], yt[:], y2[:], ALU.add)
                yv = V(pb_t[:, 6:8, :].rearrange("p a c -> p (a c)") if False else pb_t[:, 6:8, :], [pb[6].whole, pb[7].whole])
                fw.tt(V(yt.t[:].rearrange("p (a c) -> p a c", a=2), yt.whole), V(yt.t[:].rearrange("p (a c) -> p a c", a=2), yt.whole), yv, ALU.add)
                xs = V(U[0].t[:, 0:1024].rearrange("p (h n) -> p h n", n=64), U[0].whole)
                fw.tt(as16(y2), xs, h16(prm[:, ODD:ODD + 16]), ALU.mult, eng="pool")
                fw.tt(yt[:], yt[:], y2[:], ALU.add)
                fw.act(zt[:], zt[:], AF.Silu)
                fw.tt(yt[:], yt[:], zt[:], ALU.mult)
                banks = chunk_state(0)
                mulA(Sx[0], Sx[0], E[:, 64:80])
                fw.tt(V(Sx[0].t[:].rearrange("p (g c) -> p g c", g=2), Sx[0].whole), V(Sx[0].t[:].rearrange("p (g c) -> p g c", g=2), Sx[0].whole),
                      V(pb_t[:, 4:6, :], [pb[4].whole, pb[5].whole]), ALU.add)
                fw.act(y2[:], yt[:], AF.Square)
                fw.reduce(gss[:], V(y2.t[:].rearrange("p (g c) -> p g c", g=2), y2.whole), ALU.add)
                fw.act(gss[:], gss[:], AF.Sqrt, bias=eps_t[:], scale=1.0 / 512)
                fw.recip(gss[:], gss[:])
                fw.tt(V(ybf.t[:].rearrange("p (g c) -> p g c", g=2), ybf.whole), V(yt.t[:].rearrange("p (g c) -> p g c", g=2), yt.whole),
                      V(gss.t[:].unsqueeze(2).to_broadcast([128, 2, 512]), gss.whole), ALU.mult)
                to = pbh[1]
                tov = to.t[:, 0:1024].rearrange("p (a c) -> p a c", a=8)
                for a in range(8):
                    fw.transpose(V(tov[:, a, :], to.whole), ybf[:, a * 128:(a + 1) * 128], ident_bf[:], last=(a == 7))
                fw.tt(ytb[:], V(tov, to.whole), V(ssdn.t[:].unsqueeze(2).to_broadcast([128, 8, 128]), ssdn.whole), ALU.mult)
                fw.dma("sp", V(mixT_s.t[4:12, :, rows].rearrange("h p c -> p h c"), mixT_s.part(("ssd", lc)).res), ytb[:])
        fw.phase_reset()

    def phase_out(l, need_ctx):
        wout = fw.carve("wout", [128, 16, D], BF16)
        wst = fw.carve("wost", [128, 4, D], F32)
        mx = [fw.carve(f"mx{i}", [128, 16, 128], BF16) for i in range(2)]
        tmp = fw.carve("otmp", [128, 512], F32)
        for q in range(4):
            fw.dma("sp", V(wst.t[:], wst.whole),
                   V(wout_d.t[l, q * 512:(q + 1) * 512, :].rearrange("(f p) c -> p f c", p=128), wout_d.whole))
            fw.copy(wout[:, q * 4:(q + 1) * 4, :], wst[:], eng="pool")
        allmix = [r for r in mixT_s.parts.values()]
        for lc in (range(NLC) if need_ctx else range(NCH)):
            is_ctx = lc >= NCH
            m = mx[lc % 2]
            fw.dma("sp", m[:], V(mixT_s.t[:, :, lc * 128:(lc + 1) * 128].rearrange("f p c -> p f c"), allmix))
            for hh in range(2):
                bank = pb[(lc % 2) * 2 + hh]
                for fc in range(16):
                    fw.mm(bank[:, :], m[:, fc, :], wout[:, fc, hh * 512:(hh + 1) * 512], start=(fc == 0), stop=(fc == 15))
                cs = slice(hh * 512, (hh + 1) * 512)
                fw.tt(tmp[:], bank[:, :], gate[:, 1 if is_ctx else 0, cs], ALU.mult)
                dst = ctx_sb[:, lc - NCH, cs] if is_ctx else x_sb[:, lc, cs]
                fw.tt(dst, dst, tmp[:], ALU.add)
        fw.phase_reset()

    for l in range(depth):
        need_ctx = l < depth - 1
        adaln(l)
        layer_params(l)
        phase_proj(l, need_ctx)
        if stop_after == "t_proj": break
        phase_attn(l, need_ctx)
        if stop_after == "t_attn": break
        phase_halo(l)
        phase_ret(l, need_ctx, "a")
        phase_ssd(l, need_ctx)
        phase_ret(l, need_ctx, "b")
        if stop_after == "t_ssd": break
        phase_out(l, need_ctx)
        if l == 0 and "x0" in dbg:
            fw.dma("sp", V(dbg["x0"].t.ap().rearrange("(c p) d -> p c d", p=128), dbg["x0"].whole), V(x_sb.t[:], x_sb.whole))
            fw.dma("sp", V(dbg["ctx0"].t.ap().rearrange("(c p) d -> p c d", p=128), dbg["ctx0"].whole), V(ctx_sb.t[:], ctx_sb.whole))

    if "qT" in dbg:
        fw.dma("sp", dbg["qT"][:], V(qT_s.t[:], [qT_s.part(c).res for c in range(NLC)]))
    if "kv0" in dbg:
        fw.dma("sp", dbg["kv0"][:], kv_dst[0][:, :])
    if "proj" in dbg:
        fw.dma("sp", dbg["proj"][:], V(proj_s.t[:], [r for r in proj_s.parts.values()]))
    if "mixT" in dbg:
        fw.dma("sp", dbg["mixT"][:], V(mixT_s.t[0:4], [r for r in mixT_s.parts.values()]))
    if "mixS" in dbg:
        fw.dma("sp", dbg["mixS"][:], V(mixT_s.t[4:12], [r for r in mixT_s.parts.values()]))
    if "mixR" in dbg:
        fw.dma("sp", dbg["mixR"][:], V(mixT_s.t[12:16], [r for r in mixT_s.parts.values()]))
    fw.dma("sp", V(out_d.t.ap().rearrange("(c p) d -> p c d", p=128), out_d.whole), V(x_sb.t[:], x_sb.whole))
    fw.wait_all("sp", [out_d[:]] + [V(b.t[:], b.whole) for b in dbg.values()])
    return nc, fw


def rope_tables():
    n_freq = 16
    inv_freq = (10000.0 ** (-np.arange(n_freq, dtype=np.float32) / n_freq)).astype(np.float32)
    pos = np.arange(8192)
    row = (pos // 64).astype(np.float32)
    col = (pos % 64).astype(np.float32)
    ang = np.concatenate([row[:, None] * inv_freq, col[:, None] * inv_freq], axis=-1).astype(np.float32)
    return np.cos(ang).astype(np.float32), np.sin(ang).astype(np.float32)


def const_tables():
    j = np.arange(128)[:, None]; i = np.arange(128)[None, :]
    tri = np.stack([(j <= i), (j >= i), (j > i), (j < i), np.ones((128, 128), bool)], axis=1).astype(np.float32)
    gf = np.array([1.0 - 2.0 ** -e for e in RET_EXP_F], np.float64)
    gb = np.array([1.0 - 2.0 ** -e for e in RET_EXP_B], np.float64)
    dif = (i - j).astype(np.float64)
    Dret = np.zeros((128, 4, 128), np.float64)
    for h in range(4):
        Dret[:, h, :] = np.where(dif > 0, gf[h] ** np.abs(dif), 0.0) + np.where(dif < 0, gb[h] ** np.abs(dif), 0.0) + np.where(dif == 0, 2.0, 0.0)
    Dret *= 0.125
    pos = np.arange(128, dtype=np.float64)[:, None]
    te_f = gf[None, :] ** (127 - pos) * 0.125
    te_b = gb[None, :] ** pos * 0.125
    qsc_f = gf[None, :] ** (pos + 1)
    qsc_b = gb[None, :] ** (128 - pos)
    a_f = np.broadcast_to(gf[None, :] ** 128, (128, 4)); a_b = np.broadcast_to(gb[None, :] ** 128, (128, 4))
    rett = np.concatenate([Dret.reshape(128, 512), te_f, te_b, qsc_f, qsc_b, a_f, a_b], axis=1).astype(np.float32)
    return np.ascontiguousarray(tri), np.ascontiguousarray(rett)


def make_inputs(inp):
    cos, sin = rope_tables()
    tri, rett = const_tables()
    ssdp = np.concatenate([inp["ssd_conv_w"].reshape(2, -1), inp["ssd_conv_b"], inp["ssd_dt_bias"].reshape(2, -1),
                           inp["ssd_a_log"].reshape(2, -1), inp["ssd_d"], inp["ret_norm"]], axis=1).astype(np.float32)
    ssdp = np.ascontiguousarray(np.broadcast_to(ssdp[:, None, :], (2, 128, ssdp.shape[1])))
    ssdn = np.ascontiguousarray(inp["ssd_norm"].reshape(2, 8, 128).transpose(0, 2, 1))
    rep = lambda a: np.ascontiguousarray(np.broadcast_to(a[:, None], (a.shape[0], 128) + a.shape[1:]))
    qkg = rep(np.stack([inp["attn_q_norm"], inp["attn_k_norm"]], axis=1))
    lamv = rep(np.stack([inp["lambda_q1"], inp["lambda_k1"], inp["lambda_q2"], inp["lambda_k2"]], axis=1))
    subln = np.ascontiguousarray(inp["attn_subln"][:, :, None])
    maps = []
    for core in range(8):
        b, t = core // 4, core % 4
        lo = t * TOK
        cc = np.stack([inp["c"][b].reshape(8, 128).T, inp["c_ctx"].reshape(8, 128).T], axis=-1)
        rp = np.stack([cos[lo:lo + TOK], sin[lo:lo + TOK]], axis=1)
        rp = rp.reshape(NCH, 128, 2, 32).transpose(1, 0, 2, 3)
        m = {
            "x": np.ascontiguousarray(inp["x"][b, lo:lo + TOK]),
            "ctx": np.ascontiguousarray(inp["ctx"][b]),
            "cc": np.ascontiguousarray(cc.astype(np.float32)),
            "w_ada": inp["w_ada"],
            "b_ada_f": np.ascontiguousarray(inp["b_ada"][:, :2 * D].reshape(2, 16, 128).transpose(0, 2, 1)),
            "b_ada": inp["b_ada"],
            "w_in": inp["w_in"], "w_out": inp["w_out"],
            "qkg": qkg, "lamv": lamv, "subln": subln,
            "rope": np.ascontiguousarray(rp),
            "tri": tri, "rett": rett, "ssdp": ssdp, "ssdn": ssdn,
            "cmask": np.ascontiguousarray(np.broadcast_to(np.array([float(s_ < t) for s_ in range(4)] + [float(s_ > t) for s_ in range(4)], np.float32)[None], (128, 8))),
        }
        maps.append(m)
    return maps


from concourse.bass_utils import run_bass_kernel_spmd


def kernel(**inputs):
    inp = {k: np.asarray(v) for k, v in inputs.items()}
    nc, _ = build(depth=2)
    maps = make_inputs(inp)
    res = run_bass_kernel_spmd(nc, maps, core_ids=list(range(8)))
    outs = [np.asarray(res.results[c]["out"]) for c in range(8)]
    return np.stack([np.concatenate(outs[0:4], 0), np.concatenate(outs[4:8], 0)]).astype(np.float32)
```

```python
import numpy as np
import concourse.bass as bass
import concourse.mybir as mybir

F32 = mybir.dt.float32
BF16 = mybir.dt.bfloat16
AF = mybir.ActivationFunctionType
ALU = mybir.AluOpType
AX = mybir.AxisListType


class Res:
    __slots__ = ("name", "w", "r")

    def __init__(self, name):
        self.name = name
        self.w = None
        self.r = {}


class V:
    __slots__ = ("ap", "res")

    def __init__(self, ap, res):
        self.ap = ap
        self.res = res if isinstance(res, (list, tuple)) else [res]


class Buf:
    def __init__(self, fw, name, t, nparts=1):
        self.fw = fw
        self.name = name
        self.t = t
        self.parts = {}
        self.whole = Res(name)

    def __getitem__(self, idx):
        return V(self.t[idx], self.whole)

    def part(self, key):
        if key not in self.parts:
            self.parts[key] = Res(f"{self.name}.{key}")
        return _PartView(self, self.parts[key])

    def ap(self):
        return self.t.ap()


class _PartView:
    def __init__(self, buf, res):
        self.buf = buf
        self.res = res

    def __getitem__(self, idx):
        return V(self.buf.t[idx], self.res)


class EngState:
    def __init__(self, name, eng, sem):
        self.name = name
        self.eng = eng
        self.sem = sem
        self.count = 0
        self.pending = False
        self.seen = {}
        self.seen_dma = {}


class FW:
    def __init__(self, nc, n_dma_sems=24, same_engine_sync=True):
        self.nc = nc
        self.same_engine_sync = same_engine_sync
        self.engs = {}
        for name, eng in (("pe", nc.tensor), ("dve", nc.vector), ("act", nc.scalar),
                          ("pool", nc.gpsimd), ("sp", nc.sync)):
            self.engs[name] = EngState(name, eng, nc.alloc_semaphore(f"s_{name}"))
        self.dma_sems = [nc.alloc_semaphore(f"s_dma{i}") for i in range(n_dma_sems)]
        self.dma_vals = [0] * n_dma_sems
        self.dma_next = 0
        self.n_inst = 0
        self.out_tokens = []
        self.cc_sem = None
        self.cc_val = 0

    def sbuf(self, name, shape, dtype):
        return Buf(self, name, self.nc.alloc_sbuf_tensor("sb_" + name, list(shape), dtype))

    def psum(self, name, shape, dtype=F32):
        return Buf(self, name, self.nc.alloc_psum_tensor("ps_" + name, list(shape), dtype))

    def dram(self, name, shape, dtype, kind="Internal", **kw):
        return Buf(self, name, self.nc.dram_tensor(name, list(shape), dtype, kind=kind, **kw))

    def _need(self, E, tok):
        if tok is None:
            return
        if tok[0] == "eng":
            _, e, c = tok
            if e == E.name:
                if not self.same_engine_sync or e == "pe":
                    return
            if E.seen.get(e, 0) >= c:
                return
            P = self.engs[e]
            assert c <= P.count, f"{E.name} waits on pending (never-incremented) {e} count {c} > {P.count}"
            E.eng.wait_ge(P.sem, c)
            E.seen[e] = c
        elif tok[0] == "cc":
            val = tok[1]
            if E.seen_dma.get("cc", 0) >= val:
                return
            E.eng.wait_ge(self.cc_sem, val)
            E.seen_dma["cc"] = val
        else:
            _, si, val = tok
            if E.seen_dma.get(si, 0) >= val:
                return
            E.eng.wait_ge(self.dma_sems[si], val)
            E.seen_dma[si] = val

    def _pre(self, E, reads, writes):
        for v in reads:
            for r in v.res:
                self._need(E, r.w)
        for v in writes:
            for r in v.res:
                self._need(E, r.w)
                for tok in r.r.values():
                    self._need(E, tok)

    def _post(self, tok, key, reads, writes):
        for v in reads:
            for r in v.res:
                r.r[key] = tok
        for v in writes:
            for r in v.res:
                r.w = tok
                r.r = {}

    def op(self, engname, fn, reads, writes, inc=True):
        E = self.engs[engname]
        self._pre(E, reads, writes)
        ins = fn(E.eng)
        self.n_inst += 1
        if inc:
            E.count += 1
            ins.then_inc(E.sem, 1)
            tok = ("eng", engname, E.count)
        else:
            tok = ("eng", engname, E.count + 1)
        self._post(tok, engname, reads, writes)
        return ins

    def dma(self, qname, out, in_, **kw):
        E = self.engs[qname]
        self._pre(E, [in_], [out])
        si = self.dma_next
        self.dma_next = (self.dma_next + 1) % len(self.dma_sems)
        if self.dma_vals[si] > 0:
            self._need(E, ("dma", si, self.dma_vals[si]))
        self.dma_vals[si] += 16
        ins = E.eng.dma_start(out=out.ap, in_=in_.ap, **kw)
        ins.then_inc(self.dma_sems[si], 16)
        self.n_inst += 1
        tok = ("dma", si, self.dma_vals[si])
        self._post(tok, f"dma{si}", [in_], [out])
        return tok

    def wait_all(self, engname, views):
        E = self.engs[engname]
        for v in views:
            for r in v.res:
                self._need(E, r.w)

    def mm(self, out, lhsT, rhs, start, stop, last=None, **kw):
        if last is None:
            last = stop
        return self.op("pe", lambda e: e.matmul(out.ap, lhsT.ap, rhs.ap, start=start, stop=stop, **kw),
                       [lhsT, rhs], [out], inc=last)

    def transpose(self, out, in_, ident, last=True):
        return self.op("pe", lambda e: e.transpose(out.ap, in_.ap, ident.ap), [in_, ident], [out], inc=last)

    def act(self, out, in_, func, bias=None, scale=1.0, accum_out=None, eng="act"):
        reads = [in_]
        kw = {}
        if bias is not None:
            if isinstance(bias, V):
                reads.append(bias)
                kw["bias"] = bias.ap
            else:
                kw["bias"] = bias
        if isinstance(scale, V):
            reads.append(scale)
            kw["scale"] = scale.ap
        else:
            kw["scale"] = scale
        writes = [out]
        if accum_out is not None:
            writes.append(accum_out)
            kw["accum_out"] = accum_out.ap
        return self.op(eng, lambda e: e.activation(out.ap, in_.ap, func, **kw), reads, writes)

    def tt(self, out, in0, in1, op, eng="dve"):
        return self.op(eng, lambda e: e.tensor_tensor(out.ap, in0.ap, in1.ap, op), [in0, in1], [out])

    def ts(self, out, in0, s1, op0, s2=None, op1=None, eng="dve", accum_out=None):
        reads = [in0]
        a1 = s1
        if isinstance(s1, V):
            reads.append(s1)
            a1 = s1.ap
        a2 = s2
        if isinstance(s2, V):
            reads.append(s2)
            a2 = s2.ap
        kw = {}
        writes = [out]
        if op1 is not None:
            kw["op1"] = op1
        if accum_out is not None:
            kw["accum_out"] = accum_out.ap
            writes.append(accum_out)
        return self.op(eng, lambda e: e.tensor_scalar(out.ap, in0.ap, a1, a2, op0, **kw), reads, writes)

    def stt(self, out, in0, scalar, in1, op0, op1, eng="dve"):
        reads = [in0, in1]
        a = scalar
        if isinstance(scalar, V):
            reads.append(scalar)
            a = scalar.ap
        return self.op(eng, lambda e: e.scalar_tensor_tensor(out.ap, in0.ap, a, in1.ap, op0, op1), reads, [out])

    def copy(self, out, in_, eng="dve"):
        if eng == "act":
            return self.op("act", lambda e: e.copy(out.ap, in_.ap), [in_], [out])
        return self.op(eng, lambda e: e.tensor_copy(out.ap, in_.ap), [in_], [out])

    def memset(self, out, val, eng="dve"):
        return self.op(eng, lambda e: e.memset(out.ap, val), [], [out])

    def reduce(self, out, in_, op, axis=AX.X, eng="dve"):
        return self.op(eng, lambda e: e.tensor_reduce(out.ap, in_.ap, axis, op), [in_], [out])

    def recip(self, out, in_):
        return self.op("dve", lambda e: e.reciprocal(out.ap, in_.ap), [in_], [out])

    def collective(self, kind, op, groups, in_, out):
        E = self.engs["pool"]
        self._pre(E, [in_], [out])
        if self.cc_sem is None:
            self.cc_sem = self.nc.alloc_semaphore("s_cc")
        self.cc_val += 1
        ins = E.eng.collective_compute(kind, op, replica_groups=groups, ins=[in_.ap], outs=[out.ap])
        ins.then_inc(self.cc_sem)
        self.n_inst += 1
        tok = ("cc", self.cc_val)
        self._post(tok, "cc", [in_], [out])
        return tok

    def make_arena(self, kbytes):
        self.arena_t = self.nc.alloc_sbuf_tensor("sb_arena", [128, kbytes * 256], F32)
        self.arena_words = kbytes * 256
        self.arena_off = 0
        self.arena_gen = 0

    def carve(self, name, shape, dtype):
        esz = 2 if dtype == BF16 else 4
        n = 1
        for s in shape[1:]:
            n *= s
        words = (n * esz + 3) // 4
        words = (words + 7) // 8 * 8
        assert self.arena_off + words <= self.arena_words, f"arena overflow for {name}: {self.arena_off}+{words}>{self.arena_words}"
        raw = self.arena_t[0:shape[0], self.arena_off:self.arena_off + words]
        self.arena_off += words
        ap = raw.bitcast(dtype) if dtype != F32 else raw
        ap = ap[:, 0:n]
        if len(shape) > 2:
            names = " ".join(f"d{i}" for i in range(1, len(shape)))
            kw = {f"d{i}": shape[i] for i in range(1, len(shape))}
            ap = ap.rearrange(f"p ({names}) -> p {names}", **kw)
        return Buf(self, f"{name}@{self.arena_gen}", _APHandle(ap))

    def barrier(self):
        for E in self.engs.values():
            for P in self.engs.values():
                if P is not E and P.count > 0:
                    self._need(E, ("eng", P.name, P.count))
            for si, val in enumerate(self.dma_vals):
                if val > 0:
                    self._need(E, ("dma", si, val))
            if self.cc_val > 0:
                self._need(E, ("cc", self.cc_val))

    def phase_reset(self):
        self.barrier()
        self.arena_off = 0
        self.arena_gen += 1


class _APHandle:
    def __init__(self, ap):
        self._ap = ap

    def __getitem__(self, idx):
        return self._ap[idx]

    def ap(self):
        return self._ap


import math
KCUT = 9

D = 1024
NCH = 16
TOK = 2048
NTOK = TOK + 256
NLC = 18
DIN = 6176
EPS = 1e-6
GROUPS = [[0, 1, 2, 3], [4, 5, 6, 7]]
SW = 1568
NP = 3 * 1536 + 1536 + 32 + 32 + 16 + 128
RET_EXP_F = (5.0, 6.0, 7.0, 8.0)
RET_EXP_B = (5.5, 6.5, 7.5, 8.5)
BLOCKS = [("aq", 0, 512), ("ak", 512, 512), ("av", 1024, 512), ("ag", 1536, 512),
          ("xbc0", 2048, 512), ("xbc1", 2560, 512), ("xbc2", 3072, 512), ("dtr", 3584, 32),
          ("z0", 3616, 512), ("z1", 4128, 512), ("rqk", 4640, 512), ("rv", 5152, 512), ("rg", 5664, 512)]


class Alt:
    def __init__(self, bufs, par):
        self.bufs, self.par = bufs, par

    @property
    def cur(self):
        return self.bufs[self.par[0] % len(self.bufs)]

    def __getitem__(self, idx):
        return self.cur[idx]

    @property
    def t(self):
        return self.cur.t

    @property
    def whole(self):
        return self.cur.whole


def lam_init_of(layer):
    return 0.8 - 0.6 * math.exp(-0.3 * layer)


def build(depth=2, debug=None, stop_after=None):
    debug = debug or {}
    nc = bass.Bass("TRN2", target_bir_lowering=False)
    fw = FW(nc, same_engine_sync=True)
    I = lambda n, s, d=F32: fw.dram(n, s, d, kind="ExternalInput")
    x_d = I("x", [TOK, D])
    ctx_d = I("ctx", [256, D])
    cc_d = I("cc", [128, 8, 2])
    wada_d = I("w_ada", [2, D, 3 * D])
    bada_f_d = I("b_ada_f", [2, 128, 16])
    bada_d = I("b_ada", [2, 3 * D])
    win_d = I("w_in", [2, D, DIN])
    wout_d = I("w_out", [2, 2 * D, D])
    qkg_d = I("qkg", [2, 128, 2, 64])
    lamv_d = I("lamv", [2, 128, 4, 64])
    subln_d = I("subln", [2, 128, 1])
    rope_d = I("rope", [128, NCH, 2, 32])
    tri_d = I("tri", [128, 5, 128])
    rett_d = I("rett", [128, 4 * 128 + 24])
    ssdp_d = I("ssdp", [2, 128, NP])
    ssdn_d = I("ssdn", [2, 128, 8])
    cmask_d = I("cmask", [128, 8])
    out_d = fw.dram("out", [TOK, D], F32, kind="ExternalOutput")
    dbg = {k: fw.dram("dbg_" + k, shape, dt_, kind="ExternalOutput") for k, (shape, dt_) in debug.items()}

    proj_s = fw.dram("proj_s", [NTOK, DIN], F32)
    qT_s = fw.dram("qT_s", [4, 128, NTOK], BF16)
    agT_s = fw.dram("agT_s", [4, 128, NTOK], BF16)
    mixT_s = fw.dram("mixT_s", [16, 128, NTOK], BF16)
    kc_s = fw.dram("kc_s", [4, 128, 256], BF16)
    vc_s = fw.dram("vc_s", [256, 512], BF16)
    xbc_pad = fw.dram("xbc_pad", [TOK + 2, 1536], F32)
    xbc_cpad = fw.dram("xbc_cpad", [258, 1536], F32)
    hx_stage = fw.dram("hx_stage", [2, 1536], F32)
    hx_src = fw.dram("hx_src", [8, 1536], F32)
    hx_dst = fw.dram("hx_dst", [8, 1536], F32)
    hxL = fw.dram("hxL", [9, 1536], F32)
    hxR = fw.dram("hxR", [8, 1536], F32)
    cst_s = fw.dram("cst_s", [NLC, 2, 128, SW], F32)
    sb_s = fw.dram("sb_s", [NLC, 128, 1536], BF16)
    pc_h = fw.dram("pc_h", [NLC, 128, 3584], BF16)
    pc_f = fw.dram("pc_f", [NLC, 128, 1152], F32)
    rc_f = fw.dram("rc_f", [NLC, 128, 512], F32)
    rc_h = fw.dram("rc_h", [NLC, 128, 1024], BF16)
    rctx_s = fw.dram("rctx_s", [2, 64, 512], F32)
    st_stage = [fw.dram(f"st_stage{d}", [128, SW], F32) for d in range(2)]
    st_src = [fw.dram(f"st_src{d}", [512, SW], F32) for d in range(2)]
    st_dst = [fw.dram(f"st_dst{d}", [512, SW], F32) for d in range(2)]
    kv_src = [fw.dram(f"kv_src{h}", [512, 4096], BF16) for h in range(4)]
    kv_dst = [fw.dram(f"kv_dst{h}", [512, 4096], BF16) for h in range(4)]
    kv_stage = [fw.dram(f"kv_stage{h}", [128, 4096], BF16) for h in range(4)]

    x_sb = fw.sbuf("x_sb", [128, NCH, D], F32)
    ctx_sb = fw.sbuf("ctx_sb", [128, 2, D], F32)
    gate = fw.sbuf("gate", [128, 2, D], F32)
    sc1 = fw.sbuf("sc1", [128, 2, 8], F32)
    sh = fw.sbuf("sh", [128, 2, 8], F32)
    ident = fw.sbuf("ident", [128, 128], F32)
    ident_bf = fw.sbuf("ident_bf", [128, 128], BF16)
    ones_bf = fw.sbuf("ones_bf", [128, 128], BF16)
    eps_t = fw.sbuf("eps_t", [128, 1], F32)
    cc = fw.sbuf("cc", [128, 8, 2], F32)
    rope = fw.sbuf("rope", [128, NCH, 2, 32], F32)
    qkg = fw.sbuf("qkg", [128, 2, 64], F32)
    lamv = fw.sbuf("lamv", [128, 4, 64], F32)
    neglam = fw.sbuf("neglam", [128, 1], F32)
    subln = fw.sbuf("subln", [128, 1], F32)
    small = fw.sbuf("small", [128, 64], F32)
    cmask = fw.sbuf("cmask", [128, 8], F32)
    fw.make_arena(119)
    pb_t = nc.alloc_psum_tensor("ps_banks", [128, 8, 512], F32)
    pb = [Buf(fw, f"pb{i}", _APHandle(pb_t[:, i, :])) for i in range(8)]
    pbh = [Buf(fw, f"pbh{i}", _APHandle(pb_t[:, i, :].bitcast(BF16))) for i in range(8)]
    for i in range(8):
        pbh[i].whole = pb[i].whole
    rank = nc.partition_id() % 4

    fw.memset(ident[:], 1.0, eng="pool")
    fw.op("pool", lambda e: e.affine_select(ident.t[:], ident.t[:], [[-1, 128]], ALU.is_equal, 0.0,
                                             base=0, channel_multiplier=1), [ident[:]], [ident[:]])
    fw.copy(ident_bf[:], ident[:])
    fw.memset(ones_bf[:], 1.0)
    fw.memset(eps_t[:], EPS)
    fw.dma("sp", V(x_sb.t[:], x_sb.whole), V(x_d.t.ap().rearrange("(c p) d -> p c d", p=128), x_d.whole))
    fw.dma("sp", V(ctx_sb.t[:], ctx_sb.whole), V(ctx_d.t.ap().rearrange("(c p) d -> p c d", p=128), ctx_d.whole))
    fw.dma("sp", cc[:], cc_d[:])
    fw.dma("sp", rope[:], rope_d[:])
    fw.dma("sp", cmask[:], cmask_d[:])
    fw.act(cc[:], cc[:], AF.Silu)
    zt = fw.carve("zt", [128, 4096], BF16)
    fw.memset(zt[:], 0.0)
    for h in range(4):
        fw.dma("sp", V(kv_src[h].t.ap().rearrange("(r p) c -> p r c", p=128), kv_src[h].whole),
               V(zt.t[:].unsqueeze(1).to_broadcast([128, 4, 4096]), zt.whole))
    zf = fw.carve("zf", [128, SW], F32)
    fw.memset(zf[:], 0.0)
    fw.dma("sp", xbc_cpad[0:1, :], zf[0:1, 0:1536])
    fw.dma("sp", xbc_cpad[257:258, :], zf[0:1, 0:1536])
    fw.dma("sp", hx_src[:, :], zf[0:8, 0:1536])
    fw.dma("sp", hxL[:, :], zf[0:9, 0:1536])
    fw.dma("sp", hxR[:, :], zf[0:8, 0:1536])
    for d in range(2):
        fw.dma("sp", V(st_src[d].t.ap().rearrange("(r p) c -> p r c", p=128), st_src[d].whole),
               V(zf.t[:].unsqueeze(1).to_broadcast([128, 4, SW]), zf.whole))
        fw.dma("sp", st_stage[d][:, :], zf[:, :])
    fw.phase_reset()

    def adaln(l):
        ccrep = fw.carve("ccrep", [128, 8, 2, 128], F32)
        badaf = fw.carve("badaf", [128, 16], F32)
        gbias = fw.carve("gbias", [128, D], F32)
        wada_sb = fw.carve("wada_sb", [128, 8, 512], F32)
        fw.copy(ccrep[:], V(cc.t[:].unsqueeze(3).to_broadcast([128, 8, 2, 128]), cc.whole))
        fw.dma("sp", badaf[:], bada_f_d[l])
        fw.dma("sp", gbias[:], V(bada_d.t[l:l + 1, 2 * D:3 * D].partition_broadcast(128), bada_d.whole))
        ps_s = V(pb[2].t[:, 0:32].rearrange("p (a b) -> p a b", b=2), pb[2].whole)
        for piece in range(6):
            fw.dma("sp", V(wada_sb.t[:], wada_sb.whole),
                   V(wada_d.t[l, :, piece * 512:(piece + 1) * 512].rearrange("(k p) c -> p k c", p=128), wada_d.whole))
            if piece < 4:
                for j in range(4):
                    blk = piece * 4 + j
                    for k in range(8):
                        fw.mm(V(ps_s.ap[:, blk, :], ps_s.res), wada_sb[:, k, j * 128:(j + 1) * 128], cc[:, k, :],
                              start=(k == 0), stop=(k == 7))
            else:
                half = piece - 4
                for v in range(2):
                    for k in range(8):
                        fw.mm(pb[3][:, :], ccrep[:, k, v, :], wada_sb[:, k, :], start=(k == 0), stop=(k == 7))
                    fw.tt(gate[:, v, half * 512:(half + 1) * 512], pb[3][:, :], gbias[:, half * 512:(half + 1) * 512], ALU.add)
        for v in range(2):
            fw.tt(sh[:, v, :], V(ps_s.ap[:, 0:8, v], ps_s.res), badaf[:, 0:8], ALU.add)
            fw.tt(sc1[:, v, :], V(ps_s.ap[:, 8:16, v], ps_s.res), badaf[:, 8:16], ALU.add)
        fw.ts(sc1[:], sc1[:], 1.0, ALU.add)
        fw.phase_reset()

    def layer_params(l):
        fw.dma("sp", qkg[:], qkg_d[l])
        fw.dma("sp", lamv[:], lamv_d[l])
        fw.dma("sp", subln[:], subln_d[l])
        fw.tt(small[:, 0:64], lamv[:, 0, :], lamv[:, 1, :], ALU.mult)
        s1 = fw.sbuf(f"lam_s1_{l}", [128, 1], F32)
        s2 = fw.sbuf(f"lam_s2_{l}", [128, 1], F32)
        fw.reduce(s1[:], small[:, 0:64], ALU.add)
        fw.tt(small[:, 0:64], lamv[:, 2, :], lamv[:, 3, :], ALU.mult)
        fw.reduce(s2[:], small[:, 0:64], ALU.add)
        fw.act(s1[:], s1[:], AF.Exp)
        fw.act(s2[:], s2[:], AF.Exp)
        fw.tt(neglam[:], s2[:], s1[:], ALU.subtract)
        fw.ts(neglam[:], neglam[:], -lam_init_of(l), ALU.add)
        fw.ts(subln[:], subln[:], 1.0 - lam_init_of(l), ALU.mult)

    def phase_proj(l, need_ctx_q):
        hT = fw.carve("hT", [128, 8, NTOK], BF16)
        par = [0]
        alt = lambda n, sh, dt_: Alt([fw.carve(f"{n}_{i}", sh, dt_) for i in range(2)], par)
        xn = alt("xn", [128, D], F32)
        junk = alt("junk", [128, D], F32)
        ss = alt("ss", [128, 1], F32)
        rs = alt("rs", [128, 1], F32)
        wblk = [fw.carve(f"wblk{i}", [128, 8, 512], BF16) for i in range(2)]
        wst = fw.carve("wst", [128, 8, 512], F32)
        stage = [fw.carve(f"stage{i}", [128, 512], F32) for i in range(3)]
        sq = alt("sq", [128, 8, 64], F32)
        qn = alt("qn", [128, 8, 64], F32)
        t1 = alt("t1", [128, 8, 32], F32)
        t2 = alt("t2", [128, 8, 32], F32)
        ss8 = alt("ss8", [128, 8], F32)
        qbf = [fw.carve(f"qbf{i}", [128, 8, 64], BF16) for i in range(2)]
        tb = [fw.carve(f"tb{i}", [128, 4, 128], BF16) for i in range(2)]
        vbf = [fw.carve(f"vbf{i}", [128, 512], BF16) for i in range(2)]

        for lc in range(NLC):
            par[0] = lc
            src = x_sb[:, lc, :] if lc < NCH else ctx_sb[:, lc - NCH, :]
            v = 0 if lc < NCH else 1
            fw.act(junk[:], src, AF.Square, accum_out=ss[:])
            fw.act(rs[:], ss[:], AF.Sqrt, bias=eps_t[:], scale=1.0 / D)
            fw.recip(rs[:], rs[:])
            fw.ts(xn[:], src, rs[:], ALU.mult)
            pt = V(pb[lc % 2 * 2].t[:, :], [pb[lc % 2 * 2].whole, pb[lc % 2 * 2 + 1].whole])
            ptt = pb_t[:, lc % 2 * 2:lc % 2 * 2 + 2, :].rearrange("p a (k c) -> p (a k) c", c=128)
            for k in range(8):
                fw.transpose(V(ptt[:, k, :], pt.res), xn[:, k * 128:(k + 1) * 128], ident[:], last=(k == 7))
            for k in range(8):
                fw.ts(hT[:, k, lc * 128:(lc + 1) * 128], V(ptt[:, k, :], pt.res), sc1[:, v, k:k + 1], ALU.mult,
                      sh[:, v, k:k + 1], ALU.add)
        if "hT" in dbg:
            fw.dma("sp", V(dbg["hT"].t[:], dbg["hT"].whole), V(hT.t[:], hT.whole))

        it = 0
        for bi, (bname, col0, ncols) in enumerate(BLOCKS):
            wb = wblk[bi % 2]
            fw.dma("sp", V(wst.t[:, :, 0:ncols], wst.whole),
                   V(win_d.t[l, :, col0:col0 + ncols].rearrange("(k p) c -> p k c", p=128), win_d.whole))
            fw.copy(V(wb.t[:, :, 0:ncols], wb.whole), V(wst.t[:, :, 0:ncols], wst.whole), eng="pool")
            for lc in range(NLC):
                is_ctx = lc >= NCH
                bank = pb[4 + it % 2]
                it += 1
                par[0] = it
                for k in range(8):
                    fw.mm(bank[:, 0:ncols], hT[:, k, lc * 128:(lc + 1) * 128], wb[:, k, 0:ncols],
                          start=(k == 0), stop=(k == 7))
                rows = slice(lc * 128, (lc + 1) * 128)
                if bname in ("aq", "ak"):
                    if bname == "aq" and is_ctx and not need_ctx_q:
                        continue
                    gi = 0 if bname == "aq" else 1
                    psv = V(bank.t[:, :].rearrange("p (a b) -> p a b", b=64), bank.whole)
                    fw.act(sq[:], psv, AF.Square)
                    fw.reduce(ss8[:], sq[:], ALU.add)
                    fw.act(ss8[:], ss8[:], AF.Sqrt, bias=eps_t[:], scale=1.0 / 64)
                    fw.recip(ss8[:], ss8[:])
                    fw.tt(qn[:], psv, V(ss8.t[:].unsqueeze(2).to_broadcast([128, 8, 64]), ss8.whole), ALU.mult)
                    fw.tt(qn[:], qn[:], V(qkg.t[:, gi:gi + 1, :].to_broadcast([128, 8, 64]), qkg.whole), ALU.mult, eng="pool")
                    qo = qbf[it % 2]
                    if not is_ctx:
                        cosb = V(rope.t[:, lc, 0:1, :].to_broadcast([128, 8, 32]), rope.whole)
                        sinb = V(rope.t[:, lc, 1:2, :].to_broadcast([128, 8, 32]), rope.whole)
                        fw.tt(t1[:], qn[:, :, 0:32], cosb, ALU.mult)
                        fw.tt(t2[:], qn[:, :, 32:64], sinb, ALU.mult, eng="pool")
                        fw.tt(qo[:, :, 0:32], t1[:], t2[:], ALU.subtract)
                        fw.tt(t1[:], qn[:, :, 0:32], sinb, ALU.mult)
                        fw.tt(t2[:], qn[:, :, 32:64], cosb, ALU.mult, eng="pool")
                        fw.tt(qo[:, :, 32:64], t1[:], t2[:], ALU.add)
                    else:
                        fw.copy(qo[:], qn[:])
                    tbank = pbh[6 + lc % 2]
                    tbv = tbank.t[:, 0:512].rearrange("p (h c) -> p h c", c=128)
                    qof = qo.t[:].rearrange("p a b -> p (a b)")
                    for hd in range(4):
                        fw.transpose(V(tbv[:, hd, :], tbank.whole), V(qof[:, hd * 128:(hd + 1) * 128], qo.whole), ident_bf[:], last=(hd == 3))
                    tbs = tb[lc % 2]
                    fw.copy(tbs[:], V(tbv, tbank.whole), eng="act")
                    if bname == "aq":
                        fw.dma("act", V(qT_s.t[:, :, rows].rearrange("h p c -> p h c"), qT_s.part(lc).res), tbs[:])
                    elif is_ctx:
                        c0 = (lc - NCH) * 128
                        fw.dma("act", V(kc_s.t[:, :, c0:c0 + 128].rearrange("h p c -> p h c"), kc_s.part(lc).res), tbs[:])
                    else:
                        for hd in range(4):
                            fw.dma("act", V(kv_stage[hd].t[:, lc * 128:(lc + 1) * 128], kv_stage[hd].part(("k", lc)).res),
                                   tbs[:, hd, :])
                elif bname == "av":
                    vb = vbf[lc % 2]
                    fw.copy(vb[:], bank[:, :], eng="act")
                    if is_ctx:
                        c0 = (lc - NCH) * 128
                        fw.dma("act", V(vc_s.t[c0:c0 + 128, :], vc_s.part(lc).res), vb[:])
                    else:
                        for hd in range(4):
                            fw.dma("act", V(kv_stage[hd].t[:, 2048 + lc * 128:2048 + (lc + 1) * 128],
                                            kv_stage[hd].part(("v", lc)).res), vb[:, hd * 128:(hd + 1) * 128])
                elif bname == "ag":
                    st = stage[lc % 3]
                    fw.act(st[:], bank[:, :], AF.Silu)
                    tbank = pb[6 + lc % 2]
                    for hd in range(4):
                        fw.transpose(tbank[:, hd * 128:(hd + 1) * 128], st[:, hd * 128:(hd + 1) * 128], ident[:], last=(hd == 3))
                    tbs = tb[lc % 2]
                    fw.copy(V(tbs.t[:].rearrange("p h c -> p (h c)"), tbs.whole), tbank[:, :])
                    fw.dma("act", V(agT_s.t[:, :, rows].rearrange("h p c -> p h c"), agT_s.part(lc).res), tbs[:])
                else:
                    st = stage[lc % 3]
                    if lc % 2 == 0:
                        fw.copy(st[:, 0:ncols], bank[:, 0:ncols])
                    else:
                        fw.copy(st[:, 0:ncols], bank[:, 0:ncols], eng="act")
                    if bname.startswith("xbc"):
                        xc = (int(bname[3]) * 512)
                        if is_ctx:
                            r1 = 1 + (lc - NCH) * 128
                            fw.dma("sp", V(xbc_cpad.t[r1:r1 + 128, xc:xc + 512], xbc_cpad.part((bname, lc)).res), st[:, 0:ncols])
                        else:
                            r1 = 1 + lc * 128
                            fw.dma("sp", V(xbc_pad.t[r1:r1 + 128, xc:xc + 512], xbc_pad.part((bname, lc)).res), st[:, 0:ncols])
                    else:
                        fw.dma("sp", V(proj_s.t[rows, col0:col0 + ncols], proj_s.part((bname, lc)).res), st[:, 0:ncols])
            if bname == "xbc2":
                halo_issue(l)
            if bname == "av":
                for hd in range(4):
                    allres = [kv_stage[hd].part(("k", c)).res for c in range(NCH)] + [kv_stage[hd].part(("v", c)).res for c in range(NCH)]
                    fw.dma("sp", V(kv_src[hd].t[bass.ds(rank * 128, 128), :], kv_src[hd].whole), V(kv_stage[hd].t[:, :], allres))
                    fw.collective("AllReduce", ALU.add, GROUPS, kv_src[hd][:, :], kv_dst[hd][:, :])
        fw.phase_reset()

    def phase_attn(l, with_ctx_q):
        kTb = [fw.carve(f"kT{i}", [128, 8448], BF16) for i in range(2)]
        vvb = [fw.carve(f"vv{i}", [128, 66, 128], BF16) for i in range(2)]
        qhb = [fw.carve(f"qh{i}", [128, NTOK], BF16) for i in range(2)]
        aghb = [fw.carve(f"agh{i}", [128, NTOK], BF16) for i in range(2)]
        rec = fw.carve("rec", [128, 512], F32)
        om = [fw.carve(f"om{i}", [128, 512], F32) for i in range(2)]
        A = fw.carve("A", [128, 512], F32)
        sqb = fw.carve("sqb", [128, 512], BF16)
        rstd = fw.carve("rstd", [128, 512], F32)
        mixo = [fw.carve(f"mixo{i}", [128, 512], BF16) for i in range(2)]
        ones_f = fw.carve("ones_f", [128, 128], F32)
        fw.memset(ones_f[:], 1.0)
        qblocks = [(q0, 512, 0, 66) for q0 in range(0, TOK, 512)]
        if with_ctx_q:
            qblocks.append((TOK, 256, 64, 66))
        sbank = (pb[0], pb[1], pb[7])
        nq_all = NTOK if with_ctx_q else TOK

        def load_head(hd):
            kT, vv, qh, agh = kTb[hd % 2], vvb[hd % 2], qhb[hd % 2], aghb[hd % 2]
            fw.dma("sp", V(kT.t[:, 0:8192].rearrange("p (r c) -> p r c", r=4), kT.whole),
                   V(kv_dst[hd].t[:, 0:2048].rearrange("(r p) c -> p r c", p=128), kv_dst[hd].whole))
            fw.dma("sp", kT[:, 8192:8448], V(kc_s.t[hd], [kc_s.part(16).res, kc_s.part(17).res]))
            fw.dma("sp", V(vv.t[:, 0:64, :].rearrange("p (r c) e -> p r c e", r=4), vv.whole),
                   V(kv_dst[hd].t[:, 2048:4096].rearrange("(r p) (c e) -> p r c e", p=128, e=128), kv_dst[hd].whole))
            fw.dma("sp", vv[:, 64:66, :], V(vc_s.t[:, hd * 128:(hd + 1) * 128].rearrange("(c p) e -> p c e", p=128),
                                           [vc_s.part(16).res, vc_s.part(17).res]))
            qres = [qT_s.part(c).res for c in range(NLC if with_ctx_q else NCH)]
            fw.dma("sp", qh[:, 0:nq_all], V(qT_s.t[hd, :, 0:nq_all], qres))
            fw.dma("sp", agh[:, 0:NTOK], V(agT_s.t[hd], [agT_s.part(c).res for c in range(NLC)]))

        load_head(0)
        pT2 = [fw.carve(f"pTT{i}", [128, 2, 512], BF16) for i in range(3)]
        acc2 = [fw.carve(f"accT{i}", [128, 2, 512], F32) for i in range(2)]
        stage_banks = ((0, 1), (4, 5))
        for hd in range(4):
            if hd + 1 < 4:
                load_head(hd + 1)
            kT, vv, qh, agh = kTb[hd % 2], vvb[hd % 2], qhb[hd % 2], aghb[hd % 2]
            for qi, (q0, nq_, kc0, kc1) in enumerate(qblocks):
                kcs = list(range(kc0, kc1))
                n = len(kcs)

                def qk(i):
                    kc = kcs[i]
                    b0, b1 = stage_banks[i % 2]
                    for m, bk in ((0, b0), (1, b1)):
                        fw.mm(pb[bk][:, 0:nq_], kT[m * 64:(m + 1) * 64, kc * 128:(kc + 1) * 128], qh[m * 64:(m + 1) * 64, q0:q0 + nq_], True, True)

                qk(0)
                for i, kc in enumerate(kcs):
                    if i + 1 < n:
                        qk(i + 1)
                    b0, b1 = stage_banks[i % 2]
                    p = pT2[i % 3]
                    sc2 = V(pb_t[:, b0:b0 + 2, 0:nq_], [pb[b0].whole, pb[b1].whole])
                    fw.act(p[:, :, 0:nq_], sc2, AF.Exp, scale=0.125)
                    first, lastk = (i == 0), (i == n - 1)
                    for m in range(2):
                        fw.mm(pb[2 + m][:, 0:nq_], vv[:, kc, :], p[:, m, 0:nq_], start=first, stop=lastk)
                    eng_ = "dve" if i % 2 == 0 else "pool"
                    acc_ = acc2[i % 2]
                    if i < 2:
                        fw.copy(acc_[:, :, 0:nq_], p[:, :, 0:nq_], eng=eng_)
                    else:
                        fw.tt(acc_[:, :, 0:nq_], acc_[:, :, 0:nq_], p[:, :, 0:nq_], ALU.add, eng=eng_)
                if n > 1:
                    fw.tt(acc2[0][:, :, 0:nq_], acc2[0][:, :, 0:nq_], acc2[1][:, :, 0:nq_], ALU.add)
                for m in range(2):
                    fw.mm(pb[7][:, 0:nq_], ones_f[:], acc2[0][:, m, 0:nq_], True, True)
                    fw.recip(rec[:, 0:nq_], pb[7][:, 0:nq_])
                    fw.tt(om[m][:, 0:nq_], pb[2 + m][:, 0:nq_], rec[:, 0:nq_], ALU.mult)
                fw.stt(A[:, 0:nq_], om[1][:, 0:nq_], neglam[:], om[0][:, 0:nq_], ALU.mult, ALU.add)
                fw.act(sqb[:, 0:nq_], A[:, 0:nq_], AF.Square)
                fw.mm(pb[6][:, 0:nq_], ones_bf[:], sqb[:, 0:nq_], True, True)
                fw.act(rstd[:, 0:nq_], pb[6][:, 0:nq_], AF.Sqrt, bias=eps_t[:], scale=1.0 / 128)
                fw.recip(rstd[:, 0:nq_], rstd[:, 0:nq_])
                fw.stt(A[:, 0:nq_], A[:, 0:nq_], subln[:], rstd[:, 0:nq_], ALU.mult, ALU.mult)
                mo = mixo[qi % 2]
                fw.tt(mo[:, 0:nq_], A[:, 0:nq_], agh[:, q0:q0 + nq_], ALU.mult, eng="pool")
                fw.dma("act", V(mixT_s.t[hd, :, q0:q0 + nq_], mixT_s.part((hd, qi)).res), mo[:, 0:nq_])
        fw.phase_reset()

    def halo_issue(l):
        xr = lambda c: [xbc_pad.part((f"xbc{i}", c)).res for i in range(3)]
        fw.dma("sp", hx_stage[0:1, :], V(xbc_pad.t[1:2, :], xr(0)))
        fw.dma("sp", hx_stage[1:2, :], V(xbc_pad.t[TOK:TOK + 1, :], xr(NCH - 1)))
        fw.dma("sp", V(hx_src.t[bass.ds(rank * 2, 2), :], hx_src.whole), hx_stage[:, :])
        fw.collective("AllReduce", ALU.add, GROUPS, hx_src[:, :], hx_dst[:, :])

    def phase_halo(l):
        fw.dma("sp", hxL[1:9, :], hx_dst[:, :])
        fw.dma("sp", hxR[0:6, :], hx_dst[2:8, :])
        fw.dma("sp", V(xbc_pad.t[0:1, :], xbc_pad.part("hl").res), V(hxL.t[bass.ds(rank * 2, 1), :], hxL.whole))
        fw.dma("sp", V(xbc_pad.t[TOK + 1:TOK + 2, :], xbc_pad.part("hr").res), V(hxR.t[bass.ds(rank * 2, 1), :], hxR.whole))

    def rope_apply(out, x, lc, nh, t1, t2):
        cosb = V(rope.t[:, lc, 0:1, :].to_broadcast([128, nh, 32]), rope.whole)
        sinb = V(rope.t[:, lc, 1:2, :].to_broadcast([128, nh, 32]), rope.whole)
        fw.tt(t1[:, 0:nh, :], x[:, :, 0:32], cosb, ALU.mult)
        fw.tt(t2[:, 0:nh, :], x[:, :, 32:64], sinb, ALU.mult, eng="pool")
        fw.tt(out[:, :, 0:32], t1[:, 0:nh, :], t2[:, 0:nh, :], ALU.subtract)
        fw.tt(t1[:, 0:nh, :], x[:, :, 0:32], sinb, ALU.mult)
        fw.tt(t2[:, 0:nh, :], x[:, :, 32:64], cosb, ALU.mult, eng="pool")
        fw.tt(out[:, :, 32:64], t1[:, 0:nh, :], t2[:, 0:nh, :], ALU.add)

    def chain_combine(Sin, Sctx, d, col0, ncol, nparts, Aexp_of, tmp, slot):
        fw.copy(Sin[0:nparts, :], Sctx[0:nparts, :])
        order = range(4) if d == 0 else range(3, -1, -1)
        for sidx in order:
            fw.dma("sp", slot[0:nparts, 0:ncol], st_dst[d][sidx * 128:sidx * 128 + nparts, col0:col0 + ncol])
            Aexp_of(sidx, tmp)
            fw.tt(tmp[0:nparts, :], tmp[0:nparts, :], slot[0:nparts, 0:ncol], ALU.add)
            fw.tt(tmp[0:nparts, :], tmp[0:nparts, :], Sin[0:nparts, :], ALU.subtract)
            mcol = cmask[:, d * 4 + sidx:d * 4 + sidx + 1]
            fw.stt(Sin[0:nparts, :], tmp[0:nparts, :], V(mcol.ap[0:nparts], mcol.res), Sin[0:nparts, :], ALU.mult, ALU.add)

    def phase_ret(l, need_ctx, part):
        rett = fw.carve("rett", [128, 4 * 128 + 24], F32)
        rnorm = fw.carve("rnorm", [128, 128], F32)
        fw.dma("sp", rett[:], rett_d[:])
        fw.dma("sp", rnorm[:], ssdp_d[l, :, NP - 128:NP])
        Dret = V(rett.t[:, 0:512].rearrange("p (h i) -> p h i", i=128), rett.whole)
        tab = lambda k: V(rett.t[:, 512 + 4 * k:512 + 4 * k + 4], rett.whole)
        par = [0]
        alt = lambda n, sh, dt_: Alt([fw.carve(f"{n}_{i}", sh, dt_) for i in range(2)], par)
        qk = alt("qk", [128, 8, 64], F32)
        qkr = alt("qkr", [128, 8, 64], F32)
        t1 = alt("rt1", [128, 8, 32], F32)
        t2 = alt("rt2", [128, 8, 32], F32)
        rv = alt("rv", [128, 512], F32)
        rvbf = alt("rvbf", [128, 512], BF16)
        kte = [alt(f"kte{d}", [128, 4, 64], BF16) for d in range(2)]
        q3 = alt("q3", [128, 3, 4, 64], BF16)
        kbf = alt("kbf", [128, 4, 64], BF16)
        qT = alt("qT", [64, 3, 4, 128], BF16)
        kT = alt("kT", [64, 4, 128], BF16)
        Wt = alt("Wt", [128, 4, 128], BF16)
        R = [fw.carve(f"R{d}", [64, 512], F32) for d in range(2)]
        Rbf = [fw.carve(f"Rbf{d}", [64, 512], BF16) for d in range(2)]
        Rctx = [fw.carve(f"Rctx{d}", [64, 512], F32) for d in range(2)]
        Pb = fw.carve("Pb", [128, 4], F32)
        cs_sb = [alt(f"cs_sb{d}", [64, 512], F32) for d in range(2)]
        tmp = fw.carve("rtmp", [64, 512], F32)
        slot = fw.carve("rslot", [64, 512], F32)
        rg = alt("rg", [128, 512], F32)
        ysq = alt("ysq", [128, 4, 128], F32)
        yss = alt("yss", [128, 4], F32)
        yn = alt("yn", [128, 4, 128], F32)
        ybf = alt("ybf", [128, 512], BF16)
        ytb = alt("ytb", [128, 4, 128], BF16)
        bc4 = lambda v, n: V(v.ap.unsqueeze(2).to_broadcast([v.ap.shape[0], 4, n]), v.res)

        def prep(lc):
            par[0] = lc
            is_ctx = lc >= NCH
            rows = slice(lc * 128, (lc + 1) * 128)
            pr = lambda n: proj_s.part((n, lc)).res
            fw.dma("sp", V(qk.t[:].rearrange("p a b -> p (a b)"), qk.whole), V(proj_s.t[rows, 4640:5152], pr("rqk")))
            fw.dma("sp", rv[:], V(proj_s.t[rows, 5152:5664], pr("rv")))
            if is_ctx:
                src = qk
            else:
                rope_apply(qkr, qk, lc, 8, t1, t2)
                src = qkr
            fw.copy(rvbf[:], rv[:], eng="pool")
            for d in range(2):
                fw.tt(kte[d][:], src[:, 4:8, :], bc4(tab(d), 64), ALU.mult)
            return src

        def chunk_states(start_banks=(0, 1)):
            for d in range(2):
                bank = pb[start_banks[d]]
                for h in range(4):
                    fw.mm(bank[0:64, h * 128:(h + 1) * 128], kte[d][:, h, :], rvbf[:, h * 128:(h + 1) * 128],
                          start=(h == 0), stop=(h == 3), skip_group_check=True)
            return [pb[start_banks[0]], pb[start_banks[1]]]

        A128 = [tab(4), tab(5)]

        def fold(acc, csb, lc, store):
            fw.tt(V(acc[0].t[:].rearrange("p (h n) -> p h n", n=128), acc[0].whole),
                  V(acc[0].t[:].rearrange("p (h n) -> p h n", n=128), acc[0].whole),
                  V(A128[0].ap[0:64].unsqueeze(2).to_broadcast([64, 4, 128]), A128[0].res), ALU.mult)
            fw.tt(acc[0][:], acc[0][:], csb[0][0:64, :], ALU.add)
            fw.tt(V(tmp.t[:].rearrange("p (h n) -> p h n", n=128), tmp.whole),
                  V(csb[1].t[0:64, :].rearrange("p (h n) -> p h n", n=128), csb[1].whole),
                  V(Pb.t[0:64, :].unsqueeze(2).to_broadcast([64, 4, 128]), Pb.whole), ALU.mult)
            fw.tt(acc[1][:], acc[1][:], tmp[:], ALU.add)
            fw.tt(Pb[:], Pb[:], A128[1], ALU.mult)
            if store:
                for d in range(2):
                    fw.copy(cs_sb[d][:], csb[d][0:64, :])
                    fw.dma("act", V(cst_s.t[lc, d, 0:64, 1024:1536], cst_s.part(("r", lc, d)).res), cs_sb[d][:])

        for grp in (((16, 17), tuple(range(NCH))) if part == "a" else ()):
            for d in range(2):
                fw.memset(R[d][:], 0.0)
            fw.memset(Pb[:], 1.0)
            for lc in grp:
                src_ = prep(lc)
                if lc < NCH or need_ctx:
                    fw.dma("sp", V(rc_f.t[lc], rc_f.part(lc).res), V(src_.t[:].rearrange("p a b -> p (a b)"), src_.whole))
                    fw.dma("sp", V(rc_h.t[lc, :, 0:512], rc_h.part(lc).res), rvbf[:])
                    for d in range(2):
                        fw.dma("sp", V(rc_h.t[lc, :, 512 + d * 256:768 + d * 256], rc_h.part(lc).res),
                               V(kte[d].t[:].rearrange("p a b -> p (a b)"), kte[d].whole))
                if KCUT >= 2:
                    csb = chunk_states()
                if KCUT >= 3:
                    fold(R, csb, lc, KCUT >= 4)
            if grp[0] == 16:
                for d in range(2):
                    fw.copy(Rctx[d][:], R[d][:])
        if "ret_sf" in dbg:
            fw.dma("sp", dbg["ret_sf"][:, :], Rctx[0][:])
            fw.dma("sp", dbg["ret_sb"][:, :], Rctx[1][:])
        if stop_after == "ret_p1":
            fw.phase_reset(); return
        if part == "a":
            for d in range(2):
                fw.dma("sp", st_stage[d][0:64, 1024:1536], R[d][:])
                fw.dma("sp", V(rctx_s.t[d], rctx_s.whole), Rctx[d][:])
            fw.phase_reset()
            return
        for d in range(2):
            fw.dma("sp", Rctx[d][:], V(rctx_s.t[d], rctx_s.whole))
        A2048 = [[(1.0 - 2.0 ** -e) ** 2048 for e in RET_EXP_F], [(1.0 - 2.0 ** -e) ** 2048 for e in RET_EXP_B]]
        Rin = [fw.carve(f"Rin{d}", [64, 512], F32) for d in range(2)]
        for d in range(2):
            def aexp(sidx, t_, d=d):
                for h in range(4):
                    fw.ts(t_[0:64, h * 128:(h + 1) * 128], Rin[d][0:64, h * 128:(h + 1) * 128], float(A2048[d][h]), ALU.mult)
            chain_combine(Rin[d], Rctx[d], d, 1024, 512, 64, aexp, tmp, slot)
        snap = fw.carve("rsnap", [64, 512], BF16)
        csl = fw.carve("rcsl", [64, 512], F32)
        fw.copy(R[1][:], Rin[1][:])
        for lc in range(NCH - 1, -1, -1):
            fw.copy(snap[:], R[1][:])
            fw.dma("act", V(sb_s.t[lc, 0:64, 1024:1536], sb_s.part(("r", lc)).res), snap[:])
            fw.dma("sp", csl[:], V(cst_s.t[lc, 1, 0:64, 1024:1536], cst_s.part(("r", lc, 1)).res))
            fw.tt(V(R[1].t[:].rearrange("p (h n) -> p h n", n=128), R[1].whole),
                  V(R[1].t[:].rearrange("p (h n) -> p h n", n=128), R[1].whole),
                  V(A128[1].ap[0:64].unsqueeze(2).to_broadcast([64, 4, 128]), A128[1].res), ALU.mult)
            fw.tt(R[1][:], R[1][:], csl[:], ALU.add)
        if need_ctx:
            fw.dma("sp", csl[:], V(cst_s.t[17, 1, 0:64, 1024:1536], cst_s.part(("r", 17, 1)).res))
            fw.copy(snap[:], csl[:])
            fw.dma("sp", V(sb_s.t[16, 0:64, 1024:1536], sb_s.part(("r", 16)).res), snap[:])
            snap0 = fw.carve("rsnap0", [64, 512], BF16)
            fw.memset(snap0[:], 0.0)
            fw.dma("sp", V(sb_s.t[17, 0:64, 1024:1536], sb_s.part(("r", 17)).res), snap0[:])
        if stop_after == "ret_p2":
            fw.phase_reset(); return
        groups3 = [tuple(range(NCH))] + ([(16, 17)] if need_ctx else [])
        for grp in groups3:
            if grp[0] == 16:
                fw.memset(R[0][:], 0.0)
            else:
                fw.copy(R[0][:], Rin[0][:])
            for lc in grp:
                is_ctx = lc >= NCH
                rows = slice(lc * 128, (lc + 1) * 128)
                par[0] = lc
                src = qkr
                fw.dma("sp", V(qkr.t[:].rearrange("p a b -> p (a b)"), qkr.whole), V(rc_f.t[lc], rc_f.part(lc).res))
                fw.dma("sp", rvbf[:], V(rc_h.t[lc, :, 0:512], rc_h.part(lc).res))
                for d in range(2):
                    fw.dma("sp", V(kte[d].t[:].rearrange("p a b -> p (a b)"), kte[d].whole),
                           V(rc_h.t[lc, :, 512 + d * 256:768 + d * 256], rc_h.part(lc).res))
                fw.dma("sp", rg[:], V(proj_s.t[rows, 5664:6176], proj_s.part(("rg", lc)).res))
                fw.dma("sp", Rbf[1][:], V(sb_s.t[lc, 0:64, 1024:1536], sb_s.part(("r", lc)).res))
                fw.copy(Rbf[0][:], R[0][:])
                fw.copy(q3[:, 0, :, :], src[:, 0:4, :])
                fw.tt(q3[:, 1, :, :], src[:, 0:4, :], bc4(tab(2), 64), ALU.mult)
                fw.tt(q3[:, 2, :, :], src[:, 0:4, :], bc4(tab(3), 64), ALU.mult, eng="pool")
                fw.copy(kbf[:], src[:, 4:8, :])
                tqa = pbh[2]
                tqav = tqa.t[0:64, 0:1024].rearrange("p (k h c) -> p k h c", k=2, h=4)
                tqb = pbh[3]
                tqbv = tqb.t[0:64, 0:512].rearrange("p (h c) -> p h c", h=4)
                for k3 in range(2):
                    for h in range(4):
                        fw.transpose(V(tqav[:, k3, h, :], tqa.whole), q3[:, k3, h, :], ident_bf[:], last=(k3 == 1 and h == 3))
                for h in range(4):
                    fw.transpose(V(tqbv[:, h, :], tqb.whole), q3[:, 2, h, :], ident_bf[:], last=(h == 3))
                fw.copy(qT[:, 0:2, :, :], V(tqav, tqa.whole))
                fw.copy(qT[:, 2, :, :], V(tqbv, tqb.whole))
                tk = pbh[4]
                tkv = tk.t[0:64, 0:512].rearrange("p (h c) -> p h c", h=4)
                for h in range(4):
                    fw.transpose(V(tkv[:, h, :], tk.whole), kbf[:, h, :], ident_bf[:], last=(h == 3))
                fw.copy(kT[:], V(tkv, tk.whole))
                sc = pb[5]
                for h in range(4):
                    fw.mm(sc[:, h * 128:(h + 1) * 128], kT[:, h, :], qT[:, 0, h, :], start=(h == 0), stop=(h == 3), skip_group_check=True)
                fw.tt(Wt[:], V(sc.t[:, :].rearrange("p (h i) -> p h i", i=128), sc.whole), Dret, ALU.mult)
                csb = chunk_states((0, 1))
                yb = pb[6]
                for h in range(4):
                    o = yb[:, h * 128:(h + 1) * 128]
                    fw.mm(o, Wt[:, h, :], rvbf[:, h * 128:(h + 1) * 128], start=(h == 0), stop=False, last=False, skip_group_check=True)
                    fw.mm(o, qT[:, 1, h, :], Rbf[0][:, h * 128:(h + 1) * 128], start=False, stop=False, last=False, skip_group_check=True)
                    fw.mm(o, qT[:, 2, h, :], Rbf[1][:, h * 128:(h + 1) * 128], start=False, stop=(h == 3), last=(h == 3), skip_group_check=True)
                fw.tt(V(R[0].t[:].rearrange("p (h n) -> p h n", n=128), R[0].whole),
                      V(R[0].t[:].rearrange("p (h n) -> p h n", n=128), R[0].whole),
                      V(A128[0].ap[0:64].unsqueeze(2).to_broadcast([64, 4, 128]), A128[0].res), ALU.mult)
                fw.tt(R[0][:], R[0][:], csb[0][0:64, :], ALU.add)
                ybv = V(yb.t[:, :].rearrange("p (h n) -> p h n", n=128), yb.whole)
                fw.act(ysq[:], ybv, AF.Square)
                fw.reduce(yss[:], ysq[:], ALU.add)
                fw.act(yss[:], yss[:], AF.Sqrt, bias=eps_t[:], scale=1.0 / 128)
                fw.recip(yss[:], yss[:])
                fw.tt(yn[:], ybv, V(yss.t[:].unsqueeze(2).to_broadcast([128, 4, 128]), yss.whole), ALU.mult)
                fw.tt(yn[:], yn[:], V(rnorm.t[:].unsqueeze(1).to_broadcast([128, 4, 128]), rnorm.whole), ALU.mult, eng="pool")
                fw.act(rg[:], rg[:], AF.Silu)
                fw.tt(ybf[:], V(yn.t[:].rearrange("p h n -> p (h n)"), yn.whole), rg[:], ALU.mult)
                to = pbh[7]
                tov = to.t[:, 0:512].rearrange("p (h c) -> p h c", h=4)
                for h in range(4):
                    fw.transpose(V(tov[:, h, :], to.whole), ybf[:, h * 128:(h + 1) * 128], ident_bf[:], last=(h == 3))
                fw.copy(ytb[:], V(tov, to.whole))
                fw.dma("act", V(mixT_s.t[12:16, :, rows].rearrange("h p c -> p h c"), mixT_s.part(("ret", lc)).res), ytb[:])
        fw.phase_reset()

    def phase_ssd(l, need_ctx):
        OW, OB, ODT, OA, ODD = 0, 4608, 6144, 6176, 6208
        prm = fw.carve("prm", [128, 6224], F32)
        fw.dma("sp", prm[:], ssdp_d[l, :, 0:6224])
        tri = fw.carve("tri", [128, 5, 128], F32)
        fw.dma("sp", tri[:], tri_d[:])
        ssdn = fw.carve("ssdn", [128, 8], F32)
        fw.dma("sp", ssdn[:], ssdn_d[l])
        negA = fw.carve("negA", [128, 32], F32)
        fw.act(negA[:], prm[:, OA:OA + 32], AF.Exp)
        fw.ts(negA[:], negA[:], -1.0, ALU.mult)
        one_t = fw.carve("one_t", [128, 1], F32)
        fw.memset(one_t[:], 1.0)
        U = [fw.carve(f"U{i}", [128, 1536], F32) for i in range(3)]
        dtr = fw.carve("dtr", [128, 32], F32)
        la = fw.carve("la", [128, 32], F32)
        E = fw.carve("E", [128, 96], F32)
        praw = fw.carve("praw", [128, 96], F32)
        cumraw = Buf(fw, "cumraw", _APHandle(praw.t[:, 0:32]))
        cumraw.whole = praw.whole
        tots = fw.carve("tots", [128, 32], F32)
        v = [fw.carve(f"v{d}", [128, 1024], BF16) for d in range(2)]
        vte = [fw.carve(f"vte{d}", [128, 1024], BF16) for d in range(2)]
        BCbf = fw.carve("BCbf", [128, 512], BF16)
        BCT = fw.carve("BCT", [128, 4, 128], BF16)
        zt = fw.carve("zt", [128, 1024], F32)
        R1 = fw.carve("R1", [128, 16, 128], F32)
        seg = fw.carve("seg", [128, 16, 128], F32)
        Dm = fw.carve("Dm", [128, 16, 128], BF16)
        Sm = [fw.carve(f"Sm{d}", [128, 2, 128], F32) for d in range(2)]
        Wt = fw.carve("Wt", [128, 16, 128], BF16)
        S = [fw.carve(f"S{d}", [128, 1024], F32) for d in range(2)]
        Sx = [fw.carve(f"Sx{d}", [128, 1024], F32) for d in range(2)]
        Sbf = [fw.carve(f"Sbf{d}", [128, 1024], BF16) for d in range(2)]
        Pb = fw.carve("Pb", [128, 16], F32)
        yt = fw.carve("yt", [128, 1024], F32)
        y2 = fw.carve("y2", [128, 1024], F32)
        vtmp = Buf(fw, "vtmp", _APHandle(y2.t[:].rearrange("p (h n) -> p h n", n=64)))
        vtmp.whole = y2.whole
        gss = fw.carve("gss", [128, 2], F32)
        ybf = fw.carve("ybf", [128, 1024], BF16)
        ytb = fw.carve("ytb", [128, 8, 128], BF16)
        Aex = fw.carve("Aex", [128, 16], F32)
        h16 = lambda vv: V(vv.ap.unsqueeze(2).to_broadcast([128, 16, 64]), vv.res)
        as16 = lambda b_: V(b_.t[:].rearrange("p (h n) -> p h n", n=64), b_.whole)

        def prep(lc):
            is_ctx = lc >= NCH
            src_t, r0 = (xbc_cpad, (lc - NCH) * 128) if is_ctx else (xbc_pad, lc * 128)
            rr = [xbc_pad.part("hl").res, xbc_pad.part("hr").res]
            for k in range(3):
                fw.dma("sp", U[k][:], V(src_t.t[r0 + k:r0 + k + 128, :], rr))
            fw.dma("sp", dtr[:], V(proj_s.t[lc * 128:(lc + 1) * 128, 3584:3616], proj_s.part(("dtr", lc)).res))
            fw.tt(U[0][:], U[0][:], prm[:, OW:OW + 1536], ALU.mult, eng="pool")
            fw.tt(U[1][:], U[1][:], prm[:, OW + 1536:OW + 3072], ALU.mult)
            fw.tt(U[2][:], U[2][:], prm[:, OW + 3072:OW + 4608], ALU.mult, eng="pool")
            fw.tt(U[1][:], U[1][:], U[0][:], ALU.add)
            fw.tt(U[1][:], U[1][:], U[2][:], ALU.add)
            fw.tt(U[1][:], U[1][:], prm[:, OB:OB + 1536], ALU.add)
            fw.act(U[0][:], U[1][:], AF.Silu)
            fw.tt(dtr[:], dtr[:], prm[:, ODT:ODT + 32], ALU.add)
            fw.act(dtr[:], dtr[:], AF.Exp)
            fw.act(dtr[:], dtr[:], AF.Ln, bias=one_t[:])
            fw.tt(la[:], dtr[:], negA[:], ALU.mult)
            pe = pb[0]
            for i, (w, c0, c1) in enumerate(((0, 0, 16), (1, 16, 32), (2, 0, 16), (3, 16, 32), (4, 0, 32))):
                o0 = (0, 16, 32, 48, 64)[i]
                fw.mm(pe[:, o0:o0 + (c1 - c0)], tri[:, w, :], la[:, c0:c1], start=(i == 0), stop=(i == 4), skip_group_check=True)
            fw.copy(praw[:], pe[:, 0:96])
            fw.act(E[:], praw[:], AF.Exp)
            fw.tt(tots[:], tots[:], praw[:, 64:96], ALU.add)
            xs = V(U[0].t[:, 0:1024].rearrange("p (h n) -> p h n", n=64), U[0].whole)
            for d in range(2):
                fw.tt(vtmp[:], xs, h16(dtr[:, d * 16:(d + 1) * 16]), ALU.mult)
                fw.copy(as16(v[d]), vtmp[:], eng="pool")
                fw.tt(as16(vte[d]), vtmp[:], h16(E[:, 32 + d * 16:48 + d * 16]), ALU.mult)
            fw.copy(BCbf[:], U[0][:, 1024:1536], eng="pool")

        def chunk_state(d):
            banks = (pb[4], pb[5])
            for g in range(2):
                fw.mm(banks[g][:, :], BCbf[:, g * 128:(g + 1) * 128], vte[d][:, g * 512:(g + 1) * 512], True, True)
            return banks

        def mulA(dst, srcS, acol):
            fw.tt(as16(dst), as16(srcS), h16(acol), ALU.mult)

        for grp in ((16, 17), tuple(range(NCH))):
            for d in range(2):
                fw.memset(S[d][:], 0.0)
            fw.memset(Pb[:], 1.0)
            fw.memset(tots[:], 0.0)
            for lc in grp:
                prep(lc)
                if lc < NCH or need_ctx:
                    pr_ = pc_h.part(lc).res
                    fw.dma("sp", V(pc_h.t[lc, :, 0:1024], pr_), v[0][:])
                    fw.dma("sp", V(pc_h.t[lc, :, 1024:2048], pr_), v[1][:])
                    fw.dma("sp", V(pc_h.t[lc, :, 2048:3072], pr_), vte[0][:])
                    fw.dma("sp", V(pc_h.t[lc, :, 3072:3584], pr_), BCbf[:])
                    pf_ = pc_f.part(lc).res
                    fw.dma("sp", V(pc_f.t[lc, :, 0:1024], pf_), U[0][:, 0:1024])
                    fw.dma("sp", V(pc_f.t[lc, :, 1024:1120], pf_), praw[:])
                    fw.dma("sp", V(pc_f.t[lc, :, 1120:1152], pf_), la[:])
                for d in range(2):
                    banks = chunk_state(d)
                    csv = V(pb_t[:, 4:6, :], [pb[4].whole, pb[5].whole])
                    cs_sb = V(seg.t[:, d * 8:(d + 1) * 8, :].rearrange("p a (g c) -> p (a g) c", g=2)[:, 0:2, :] if False else seg.t[:, d * 8:(d + 1) * 8, :], seg.whole)
                    cs_flat = V(seg.t[:].rearrange("p h n -> p (h n)")[:, d * 1024:(d + 1) * 1024], seg.whole)
                    fw.copy(V(cs_flat.ap.rearrange("p (g c) -> p g c", g=2), seg.whole), csv)
                    fw.dma("sp", V(cst_s.t[lc, d, :, 0:1024], cst_s.part(("s", lc, d)).res), cs_flat)
                    if d == 0:
                        mulA(S[0], S[0], E[:, 64:80])
                        fw.tt(S[0][:], S[0][:], cs_flat, ALU.add)
                    else:
                        fw.tt(as16(y2), V(cs_flat.ap.rearrange("p (h n) -> p h n", n=64), seg.whole), h16(Pb[:, :]), ALU.mult)
                        fw.tt(S[1][:], S[1][:], y2[:], ALU.add)
                        fw.tt(Pb[:], Pb[:], E[:, 80:96], ALU.mult)
                fw.dma("sp", V(cst_s.t[lc, 0, :, 1536:1568], cst_s.part(("e", lc)).res), E[:, 64:96])
            if grp[0] == 16:
                for d in range(2):
                    fw.copy(Sx[d][:], S[d][:])
        if "ssd_sf" in dbg:
            fw.dma("sp", dbg["ssd_sf"][:, :], Sx[0][:])
            fw.dma("sp", dbg["ssd_sb"][:, :], Sx[1][:])
        for d in range(2):
            fw.dma("sp", st_stage[d][:, 0:1024], S[d][:])
            fw.dma("sp", st_stage[d][:, 1536:1552], tots[:, d * 16:(d + 1) * 16])
            fw.dma("sp", V(st_src[d].t[bass.ds(rank * 128, 128), :], st_src[d].whole), st_stage[d][:, :])
            fw.collective("AllReduce", ALU.add, GROUPS, st_src[d][:, :], st_dst[d][:, :])
        for d in range(2):
            order = range(4) if d == 0 else range(3, -1, -1)
            for sidx in order:
                fw.dma("sp", yt[:], st_dst[d][sidx * 128:(sidx + 1) * 128, 0:1024])
                fw.dma("sp", Aex[:], st_dst[d][sidx * 128:(sidx + 1) * 128, 1536:1552])
                fw.act(Aex[:], Aex[:], AF.Exp)
                mulA(y2, Sx[d], Aex[:, :])
                fw.tt(y2[:], y2[:], yt[:], ALU.add)
                fw.tt(y2[:], y2[:], Sx[d][:], ALU.subtract)
                fw.stt(Sx[d][:], y2[:], cmask[:, d * 4 + sidx:d * 4 + sidx + 1], Sx[d][:], ALU.mult, ALU.add)
        for lc in range(NCH - 1, -1, -1):
            fw.copy(Sbf[1][:], Sx[1][:])
            fw.dma("sp", V(sb_s.t[lc, :, 0:1024], sb_s.part(("s", lc)).res), Sbf[1][:])
            ld = yt if lc % 2 == 0 else y2
            fw.dma("sp", ld[:], V(cst_s.t[lc, 1, :, 0:1024], cst_s.part(("s", lc, 1)).res))
            fw.dma("sp", Aex[:], V(cst_s.t[lc, 0, :, 1552:1568], cst_s.part(("e", lc)).res))
            mulA(Sx[1], Sx[1], Aex[:, :])
            fw.tt(Sx[1][:], Sx[1][:], ld[:], ALU.add)
        if need_ctx:
            fw.dma("sp", yt[:], V(cst_s.t[17, 1, :, 0:1024], cst_s.part(("s", 17, 1)).res))
            fw.copy(Sbf[1][:], yt[:])
            fw.dma("sp", V(sb_s.t[16, :, 0:1024], sb_s.part(("s", 16)).res), Sbf[1][:])
            fw.memset(Sbf[0][:], 0.0)
            fw.dma("sp", V(sb_s.t[17, :, 0:1024], sb_s.part(("s", 17)).res), Sbf[0][:])
        if stop_after == "ssd_p2":
            fw.phase_reset(); return
        groups3 = [tuple(range(NCH))] + ([(16, 17)] if need_ctx else [])
        for grp in groups3:
            if grp[0] == 16:
                fw.memset(Sx[0][:], 0.0)
            for lc in grp:
                rows = slice(lc * 128, (lc + 1) * 128)
                pr_ = pc_h.part(lc).res
                pf_ = pc_f.part(lc).res
                fw.dma("sp", v[0][:], V(pc_h.t[lc, :, 0:1024], pr_))
                fw.dma("sp", v[1][:], V(pc_h.t[lc, :, 1024:2048], pr_))
                fw.dma("sp", vte[0][:], V(pc_h.t[lc, :, 2048:3072], pr_))
                fw.dma("sp", BCbf[:], V(pc_h.t[lc, :, 3072:3584], pr_))
                fw.dma("sp", U[0][:, 0:1024], V(pc_f.t[lc, :, 0:1024], pf_))
                fw.dma("sp", praw[:], V(pc_f.t[lc, :, 1024:1120], pf_))
                fw.dma("sp", la[:], V(pc_f.t[lc, :, 1120:1152], pf_))
                fw.act(E[:], praw[:], AF.Exp)
                fw.dma("sp", zt[:, 0:512], V(proj_s.t[rows, 3616:4128], proj_s.part(("z0", lc)).res))
                fw.dma("sp", zt[:, 512:1024], V(proj_s.t[rows, 4128:4640], proj_s.part(("z1", lc)).res))
                fw.dma("sp", Sbf[1][:], V(sb_s.t[lc, :, 0:1024], sb_s.part(("s", lc)).res))
                fw.copy(Sbf[0][:], Sx[0][:], eng="pool")
                tb_ = pbh[1]
                tbv = tb_.t[:, 0:512].rearrange("p (a c) -> p a c", a=4)
                for a in range(4):
                    fw.transpose(V(tbv[:, a, :], tb_.whole), BCbf[:, a * 128:(a + 1) * 128], ident_bf[:], last=(a == 3))
                fw.copy(BCT[:], V(tbv, tb_.whole))
                sc = pb[1]
                scv = sc.t[:, 256:512].rearrange("p (g i) -> p g i", g=2)
                for g in range(2):
                    fw.mm(V(scv[:, g, :], sc.whole), BCT[:, g, :], BCT[:, 2 + g, :], start=False if False else (g == 0), stop=(g == 1), skip_group_check=True)
                for d in range(2):
                    fw.tt(Sm[d][:], V(scv, sc.whole), V(tri.t[:, d:d + 1, :].to_broadcast([128, 2, 128]), tri.whole), ALU.mult)
                yb = (pb[6], pb[7])
                for d in range(2):
                    fw.tt(R1[:], V(la.t[:, d * 16:(d + 1) * 16].unsqueeze(2).to_broadcast([128, 16, 128]), la.whole),
                          V(tri.t[:, d:d + 1, :].to_broadcast([128, 16, 128]), tri.whole), ALU.mult, eng="pool")
                    for q in range(4):
                        bank = pb[2 + q % 2]
                        fw.mm(bank[:, :], tri[:, 4, :], V(R1.t[:, 4 * q:4 * q + 4, :].rearrange("p h n -> p (h n)"), R1.whole), True, True)
                        for hh in range(4):
                            h = 4 * q + hh
                            fw.ts(seg[:, h, :], bank[:, hh * 128:(hh + 1) * 128], cumraw[:, d * 16 + h:d * 16 + h + 1], ALU.subtract, 0.0, ALU.min)
                    fw.act(Dm[:], seg[:], AF.Exp)
                    for g in range(2):
                        fw.tt(Wt[:, g * 8:(g + 1) * 8, :], Dm[:, g * 8:(g + 1) * 8, :],
                              V(Sm[d].t[:, g:g + 1, :].to_broadcast([128, 8, 128]), Sm[d].whole), ALU.mult)
                    for h in range(16):
                        fw.mm(yb[h // 8][:, (h % 8) * 64:(h % 8 + 1) * 64], Wt[:, h, :], v[d][:, h * 64:(h + 1) * 64],
                              start=(d == 0 and h % 8 == 0), stop=(d == 1 and h % 8 == 7), last=(d == 1 and h % 8 == 7), skip_group_check=True)
                for d in range(2):
                    for g in range(2):
                        fw.mm(pb[2 + g][:, :], BCT[:, 2 + g, :], Sbf[d][:, g * 512:(g + 1) * 512], True, True)
                    ysv = V(pb_t[:, 2:4, :].rearrange("p a (h n) -> p (a h) n", n=64), [pb[2].whole, pb[3].whole])
                    fw.tt(as16(yt if d == 0 else y2), ysv, h16(E[:, d * 16:(d + 1) * 16]), ALU.mult)
                fw.tt(yt[:], yt[:], y2[:], ALU.add)
                yv = V(pb_t[:, 6:8, :].rearrange("p a c -> p (a c)") if False else pb_t[:, 6:8, :], [pb[6].whole, pb[7].whole])
                fw.tt(V(yt.t[:].rearrange("p (a c) -> p a c", a=2), yt.whole), V(yt.t[:].rearrange("p (a c) -> p a c", a=2), yt.whole), yv, ALU.add)
                xs = V(U[0].t[:, 0:1024].rearrange("p (h n) -> p h n", n=64), U[0].whole)
                fw.tt(as16(y2), xs, h16(prm[:, ODD:ODD + 16]), ALU.mult, eng="pool")
                fw.tt(yt[:], yt[:], y2[:], ALU.add)
                fw.act(zt[:], zt[:], AF.Silu)
                fw.tt(yt[:], yt[:], zt[:], ALU.mult)
                banks = chunk_state(0)
                mulA(Sx[0], Sx[0], E[:, 64:80])
                fw.tt(V(Sx[0].t[:].rearrange("p (g c) -> p g c", g=2), Sx[0].whole), V(Sx[0].t[:].rearrange("p (g c) -> p g c", g=2), Sx[0].whole),
                      V(pb_t[:, 4:6, :], [pb[4].whole, pb[5].whole]), ALU.add)
                fw.act(y2[:], yt[:], AF.Square)
                fw.reduce(gss[:], V(y2.t[:].rearrange("p (g c) -> p g c", g=2), y2.whole), ALU.add)
                fw.act(gss[:], gss[:], AF.Sqrt, bias=eps_t[:], scale=1.0 / 512)
                fw.recip(gss[:], gss[:])
                fw.tt(V(ybf.t[:].rearrange("p (g c) -> p g c", g=2), ybf.whole), V(yt.t[:].rearrange("p (g c) -> p g c", g=2), yt.whole),
                      V(gss.t[:].unsqueeze(2).to_broadcast([128, 2, 512]), gss.whole), ALU.mult)
                to = pbh[1]
                tov = to.t[:, 0:1024].rearrange("p (a c) -> p a c", a=8)
                for a in range(8):
                    fw.transpose(V(tov[:, a, :], to.whole), ybf[:, a * 128:(a + 1) * 128], ident_bf[:], last=(a == 7))
                fw.tt(ytb[:], V(tov, to.whole), V(ssdn.t[:].unsqueeze(2).to_broadcast([128, 8, 128]), ssdn.whole), ALU.mult)
                fw.dma("sp", V(mixT_s.t[4:12, :, rows].rearrange("h p c -> p h c"), mixT_s.part(("ssd", lc)).res), ytb[:])
        fw.phase_reset()

    def phase_out(l, need_ctx):
        wout = fw.carve("wout", [128, 16, D], BF16)
        wst = fw.carve("wost", [128, 4, D], F32)
        mx = [fw.carve(f"mx{i}", [128, 16, 128], BF16) for i in range(2)]
        tmp = fw.carve("otmp", [128, 512], F32)
        for q in range(4):
            fw.dma("sp", V(wst.t[:], wst.whole),
                   V(wout_d.t[l, q * 512:(q + 1) * 512, :].rearrange("(f p) c -> p f c", p=128), wout_d.whole))
            fw.copy(wout[:, q * 4:(q + 1) * 4, :], wst[:], eng="pool")
        allmix = [r for r in mixT_s.parts.values()]
        for lc in (range(NLC) if need_ctx else range(NCH)):
            is_ctx = lc >= NCH
            m = mx[lc % 2]
            fw.dma("sp", m[:], V(mixT_s.t[:, :, lc * 128:(lc + 1) * 128].rearrange("f p c -> p f c"), allmix))
            for hh in range(2):
                bank = pb[(lc % 2) * 2 + hh]
                for fc in range(16):
                    fw.mm(bank[:, :], m[:, fc, :], wout[:, fc, hh * 512:(hh + 1) * 512], start=(fc == 0), stop=(fc == 15))
                cs = slice(hh * 512, (hh + 1) * 512)
                fw.tt(tmp[:], bank[:, :], gate[:, 1 if is_ctx else 0, cs], ALU.mult)
                dst = ctx_sb[:, lc - NCH, cs] if is_ctx else x_sb[:, lc, cs]
                fw.tt(dst, dst, tmp[:], ALU.add)
        fw.phase_reset()

    for l in range(depth):
        need_ctx = l < depth - 1
        adaln(l)
        layer_params(l)
        phase_proj(l, need_ctx)
        if stop_after == "t_proj": break
        phase_attn(l, need_ctx)
        if stop_after == "t_attn": break
        phase_halo(l)
        phase_ret(l, need_ctx, "a")
        phase_ssd(l, need_ctx)
        phase_ret(l, need_ctx, "b")
        if stop_after == "t_ssd": break
        phase_out(l, need_ctx)
        if l == 0 and "x0" in dbg:
            fw.dma("sp", V(dbg["x0"].t.ap().rearrange("(c p) d -> p c d", p=128), dbg["x0"].whole), V(x_sb.t[:], x_sb.whole))
            fw.dma("sp", V(dbg["ctx0"].t.ap().rearrange("(c p) d -> p c d", p=128), dbg["ctx0"].whole), V(ctx_sb.t[:], ctx_sb.whole))

    if "qT" in dbg:
        fw.dma("sp", dbg["qT"][:], V(qT_s.t[:], [qT_s.part(c).res for c in range(NLC)]))
    if "kv0" in dbg:
        fw.dma("sp", dbg["kv0"][:], kv_dst[0][:, :])
    if "proj" in dbg:
        fw.dma("sp", dbg["proj"][:], V(proj_s.t[:], [r for r in proj_s.parts.values()]))
    if "mixT" in dbg:
        fw.dma("sp", dbg["mixT"][:], V(mixT_s.t[0:4], [r for r in mixT_s.parts.values()]))
    if "mixS" in dbg:
        fw.dma("sp", dbg["mixS"][:], V(mixT_s.t[4:12], [r for r in mixT_s.parts.values()]))
    if "mixR" in dbg:
        fw.dma("sp", dbg["mixR"][:], V(mixT_s.t[12:16], [r for r in mixT_s.parts.values()]))
    fw.dma("sp", V(out_d.t.ap().rearrange("(c p) d -> p c d", p=128), out_d.whole), V(x_sb.t[:], x_sb.whole))
    fw.wait_all("sp", [out_d[:]] + [V(b.t[:], b.whole) for b in dbg.values()])
    return nc, fw


def rope_tables():
    n_freq = 16
    inv_freq = (10000.0 ** (-np.arange(n_freq, dtype=np.float32) / n_freq)).astype(np.float32)
    pos = np.arange(8192)
    row = (pos // 64).astype(np.float32)
    col = (pos % 64).astype(np.float32)
    ang = np.concatenate([row[:, None] * inv_freq, col[:, None] * inv_freq], axis=-1).astype(np.float32)
    return np.cos(ang).astype(np.float32), np.sin(ang).astype(np.float32)


def const_tables():
    j = np.arange(128)[:, None]; i = np.arange(128)[None, :]
    tri = np.stack([(j <= i), (j >= i), (j > i), (j < i), np.ones((128, 128), bool)], axis=1).astype(np.float32)
    gf = np.array([1.0 - 2.0 ** -e for e in RET_EXP_F], np.float64)
    gb = np.array([1.0 - 2.0 ** -e for e in RET_EXP_B], np.float64)
    dif = (i - j).astype(np.float64)
    Dret = np.zeros((128, 4, 128), np.float64)
    for h in range(4):
        Dret[:, h, :] = np.where(dif > 0, gf[h] ** np.abs(dif), 0.0) + np.where(dif < 0, gb[h] ** np.abs(dif), 0.0) + np.where(dif == 0, 2.0, 0.0)
    Dret *= 0.125
    pos = np.arange(128, dtype=np.float64)[:, None]
    te_f = gf[None, :] ** (127 - pos) * 0.125
    te_b = gb[None, :] ** pos * 0.125
    qsc_f = gf[None, :] ** (pos + 1)
    qsc_b = gb[None, :] ** (128 - pos)
    a_f = np.broadcast_to(gf[None, :] ** 128, (128, 4)); a_b = np.broadcast_to(gb[None, :] ** 128, (128, 4))
    rett = np.concatenate([Dret.reshape(128, 512), te_f, te_b, qsc_f, qsc_b, a_f, a_b], axis=1).astype(np.float32)
    return np.ascontiguousarray(tri), np.ascontiguousarray(rett)


def make_inputs(inp):
    cos, sin = rope_tables()
    tri, rett = const_tables()
    ssdp = np.concatenate([inp["ssd_conv_w"].reshape(2, -1), inp["ssd_conv_b"], inp["ssd_dt_bias"].reshape(2, -1),
                           inp["ssd_a_log"].reshape(2, -1), inp["ssd_d"], inp["ret_norm"]], axis=1).astype(np.float32)
    ssdp = np.ascontiguousarray(np.broadcast_to(ssdp[:, None, :], (2, 128, ssdp.shape[1])))
    ssdn = np.ascontiguousarray(inp["ssd_norm"].reshape(2, 8, 128).transpose(0, 2, 1))
    rep = lambda a: np.ascontiguousarray(np.broadcast_to(a[:, None], (a.shape[0], 128) + a.shape[1:]))
    qkg = rep(np.stack([inp["attn_q_norm"], inp["attn_k_norm"]], axis=1))
    lamv = rep(np.stack([inp["lambda_q1"], inp["lambda_k1"], inp["lambda_q2"], inp["lambda_k2"]], axis=1))
    subln = np.ascontiguousarray(inp["attn_subln"][:, :, None])
    maps = []
    for core in range(8):
        b, t = core // 4, core % 4
        lo = t * TOK
        cc = np.stack([inp["c"][b].reshape(8, 128).T, inp["c_ctx"].reshape(8, 128).T], axis=-1)
        rp = np.stack([cos[lo:lo + TOK], sin[lo:lo + TOK]], axis=1)
        rp = rp.reshape(NCH, 128, 2, 32).transpose(1, 0, 2, 3)
        m = {
            "x": np.ascontiguousarray(inp["x"][b, lo:lo + TOK]),
            "ctx": np.ascontiguousarray(inp["ctx"][b]),
            "cc": np.ascontiguousarray(cc.astype(np.float32)),
            "w_ada": inp["w_ada"],
            "b_ada_f": np.ascontiguousarray(inp["b_ada"][:, :2 * D].reshape(2, 16, 128).transpose(0, 2, 1)),
            "b_ada": inp["b_ada"],
            "w_in": inp["w_in"], "w_out": inp["w_out"],
            "qkg": qkg, "lamv": lamv, "subln": subln,
            "rope": np.ascontiguousarray(rp),
            "tri": tri, "rett": rett, "ssdp": ssdp, "ssdn": ssdn,
            "cmask": np.ascontiguousarray(np.broadcast_to(np.array([float(s_ < t) for s_ in range(4)] + [float(s_ > t) for s_ in range(4)], np.float32)[None], (128, 8))),
        }
        maps.append(m)
    return maps


from concourse.bass_utils import run_bass_kernel_spmd


def kernel(**inputs):
    inp = {k: np.asarray(v) for k, v in inputs.items()}
    nc, _ = build(depth=2)
    maps = make_inputs(inp)
    res = run_bass_kernel_spmd(nc, maps, core_ids=list(range(8)))
    outs = [np.asarray(res.results[c]["out"]) for c in range(8)]
    return np.stack([np.concatenate(outs[0:4], 0), np.concatenate(outs[4:8], 0)]).astype(np.float32)
```

```python
import numpy as np
import concourse.bass as bass
import concourse.mybir as mybir

F32 = mybir.dt.float32
BF16 = mybir.dt.bfloat16
AF = mybir.ActivationFunctionType
ALU = mybir.AluOpType
AX = mybir.AxisListType


class Res:
    __slots__ = ("name", "w", "r")

    def __init__(self, name):
        self.name = name
        self.w = None
        self.r = {}


class V:
    __slots__ = ("ap", "res")

    def __init__(self, ap, res):
        self.ap = ap
        self.res = res if isinstance(res, (list, tuple)) else [res]


class Buf:
    def __init__(self, fw, name, t, nparts=1):
        self.fw = fw
        self.name = name
        self.t = t
        self.parts = {}
        self.whole = Res(name)

    def __getitem__(self, idx):
        return V(self.t[idx], self.whole)

    def part(self, key):
        if key not in self.parts:
            self.parts[key] = Res(f"{self.name}.{key}")
        return _PartView(self, self.parts[key])

    def ap(self):
        return self.t.ap()


class _PartView:
    def __init__(self, buf, res):
        self.buf = buf
        self.res = res

    def __getitem__(self, idx):
        return V(self.buf.t[idx], self.res)


class EngState:
    def __init__(self, name, eng, sem):
        self.name = name
        self.eng = eng
        self.sem = sem
        self.count = 0
        self.pending = False
        self.seen = {}
        self.seen_dma = {}


class FW:
    def __init__(self, nc, n_dma_sems=24, same_engine_sync=True):
        self.nc = nc
        self.same_engine_sync = same_engine_sync
        self.engs = {}
        for name, eng in (("pe", nc.tensor), ("dve", nc.vector), ("act", nc.scalar),
                          ("pool", nc.gpsimd), ("sp", nc.sync)):
            self.engs[name] = EngState(name, eng, nc.alloc_semaphore(f"s_{name}"))
        self.dma_sems = [nc.alloc_semaphore(f"s_dma{i}") for i in range(n_dma_sems)]
        self.dma_vals = [0] * n_dma_sems
        self.dma_next = 0
        self.n_inst = 0
        self.out_tokens = []
        self.cc_sem = None
        self.cc_val = 0

    def sbuf(self, name, shape, dtype):
        return Buf(self, name, self.nc.alloc_sbuf_tensor("sb_" + name, list(shape), dtype))

    def psum(self, name, shape, dtype=F32):
        return Buf(self, name, self.nc.alloc_psum_tensor("ps_" + name, list(shape), dtype))

    def dram(self, name, shape, dtype, kind="Internal", **kw):
        return Buf(self, name, self.nc.dram_tensor(name, list(shape), dtype, kind=kind, **kw))

    def _need(self, E, tok):
        if tok is None:
            return
        if tok[0] == "eng":
            _, e, c = tok
            if e == E.name:
                if not self.same_engine_sync or e == "pe":
                    return
            if E.seen.get(e, 0) >= c:
                return
            P = self.engs[e]
            assert c <= P.count, f"{E.name} waits on pending (never-incremented) {e} count {c} > {P.count}"
            E.eng.wait_ge(P.sem, c)
            E.seen[e] = c
        elif tok[0] == "cc":
            val = tok[1]
            if E.seen_dma.get("cc", 0) >= val:
                return
            E.eng.wait_ge(self.cc_sem, val)
            E.seen_dma["cc"] = val
        else:
            _, si, val = tok
            if E.seen_dma.get(si, 0) >= val:
                return
            E.eng.wait_ge(self.dma_sems[si], val)
            E.seen_dma[si] = val

    def _pre(self, E, reads, writes):
        for v in reads:
            for r in v.res:
                self._need(E, r.w)
        for v in writes:
            for r in v.res:
                self._need(E, r.w)
                for tok in r.r.values():
                    self._need(E, tok)

    def _post(self, tok, key, reads, writes):
        for v in reads:
            for r in v.res:
                r.r[key] = tok
        for v in writes:
            for r in v.res:
                r.w = tok
                r.r = {}

    def op(self, engname, fn, reads, writes, inc=True):
        E = self.engs[engname]
        self._pre(E, reads, writes)
        ins = fn(E.eng)
        self.n_inst += 1
        if inc:
            E.count += 1
            ins.then_inc(E.sem, 1)
            tok = ("eng", engname, E.count)
        else:
            tok = ("eng", engname, E.count + 1)
        self._post(tok, engname, reads, writes)
        return ins

    def dma(self, qname, out, in_, **kw):
        E = self.engs[qname]
        self._pre(E, [in_], [out])
        si = self.dma_next
        self.dma_next = (self.dma_next + 1) % len(self.dma_sems)
        if self.dma_vals[si] > 0:
            self._need(E, ("dma", si, self.dma_vals[si]))
        self.dma_vals[si] += 16
        ins = E.eng.dma_start(out=out.ap, in_=in_.ap, **kw)
        ins.then_inc(self.dma_sems[si], 16)
        self.n_inst += 1
        tok = ("dma", si, self.dma_vals[si])
        self._post(tok, f"dma{si}", [in_], [out])
        return tok

    def wait_all(self, engname, views):
        E = self.engs[engname]
        for v in views:
            for r in v.res:
                self._need(E, r.w)

    def mm(self, out, lhsT, rhs, start, stop, last=None, **kw):
        if last is None:
            last = stop
        return self.op("pe", lambda e: e.matmul(out.ap, lhsT.ap, rhs.ap, start=start, stop=stop, **kw),
                       [lhsT, rhs], [out], inc=last)

    def transpose(self, out, in_, ident, last=True):
        return self.op("pe", lambda e: e.transpose(out.ap, in_.ap, ident.ap), [in_, ident], [out], inc=last)

    def act(self, out, in_, func, bias=None, scale=1.0, accum_out=None, eng="act"):
        reads = [in_]
        kw = {}
        if bias is not None:
            if isinstance(bias, V):
                reads.append(bias)
                kw["bias"] = bias.ap
            else:
                kw["bias"] = bias
        if isinstance(scale, V):
            reads.append(scale)
            kw["scale"] = scale.ap
        else:
            kw["scale"] = scale
        writes = [out]
        if accum_out is not None:
            writes.append(accum_out)
            kw["accum_out"] = accum_out.ap
        return self.op(eng, lambda e: e.activation(out.ap, in_.ap, func, **kw), reads, writes)

    def tt(self, out, in0, in1, op, eng="dve"):
        return self.op(eng, lambda e: e.tensor_tensor(out.ap, in0.ap, in1.ap, op), [in0, in1], [out])

    def ts(self, out, in0, s1, op0, s2=None, op1=None, eng="dve", accum_out=None):
        reads = [in0]
        a1 = s1
        if isinstance(s1, V):
            reads.append(s1)
            a1 = s1.ap
        a2 = s2
        if isinstance(s2, V):
            reads.append(s2)
            a2 = s2.ap
        kw = {}
        writes = [out]
        if op1 is not None:
            kw["op1"] = op1
        if accum_out is not None:
            kw["accum_out"] = accum_out.ap
            writes.append(accum_out)
        return self.op(eng, lambda e: e.tensor_scalar(out.ap, in0.ap, a1, a2, op0, **kw), reads, writes)

    def stt(self, out, in0, scalar, in1, op0, op1, eng="dve"):
        reads = [in0, in1]
        a = scalar
        if isinstance(scalar, V):
            reads.append(scalar)
            a = scalar.ap
        return self.op(eng, lambda e: e.scalar_tensor_tensor(out.ap, in0.ap, a, in1.ap, op0, op1), reads, [out])

    def copy(self, out, in_, eng="dve"):
        if eng == "act":
            return self.op("act", lambda e: e.copy(out.ap, in_.ap), [in_], [out])
        return self.op(eng, lambda e: e.tensor_copy(out.ap, in_.ap), [in_], [out])

    def memset(self, out, val, eng="dve"):
        return self.op(eng, lambda e: e.memset(out.ap, val), [], [out])

    def reduce(self, out, in_, op, axis=AX.X, eng="dve"):
        return self.op(eng, lambda e: e.tensor_reduce(out.ap, in_.ap, axis, op), [in_], [out])

    def recip(self, out, in_):
        return self.op("dve", lambda e: e.reciprocal(out.ap, in_.ap), [in_], [out])

    def collective(self, kind, op, groups, in_, out):
        E = self.engs["pool"]
        self._pre(E, [in_], [out])
        if self.cc_sem is None:
            self.cc_sem = self.nc.alloc_semaphore("s_cc")
        self.cc_val += 1
        ins = E.eng.collective_compute(kind, op, replica_groups=groups, ins=[in_.ap], outs=[out.ap])
        ins.then_inc(self.cc_sem)
        self.n_inst += 1
        tok = ("cc", self.cc_val)
        self._post(tok, "cc", [in_], [out])
        return tok

    def make_arena(self, kbytes):
        self.arena_t = self.nc.alloc_sbuf_tensor("sb_arena", [128, kbytes * 256], F32)
        self.arena_words = kbytes * 256
        self.arena_off = 0
        self.arena_gen = 0

    def carve(self, name, shape, dtype):
        esz = 2 if dtype == BF16 else 4
        n = 1
        for s in shape[1:]:
            n *= s
        words = (n * esz + 3) // 4
        words = (words + 7) // 8 * 8
        assert self.arena_off + words <= self.arena_words, f"arena overflow for {name}: {self.arena_off}+{words}>{self.arena_words}"
        raw = self.arena_t[0:shape[0], self.arena_off:self.arena_off + words]
        self.arena_off += words
        ap = raw.bitcast(dtype) if dtype != F32 else raw
        ap = ap[:, 0:n]
        if len(shape) > 2:
            names = " ".join(f"d{i}" for i in range(1, len(shape)))
            kw = {f"d{i}": shape[i] for i in range(1, len(shape))}
            ap = ap.rearrange(f"p ({names}) -> p {names}", **kw)
        return Buf(self, f"{name}@{self.arena_gen}", _APHandle(ap))

    def barrier(self):
        for E in self.engs.values():
            for P in self.engs.values():
                if P is not E and P.count > 0:
                    self._need(E, ("eng", P.name, P.count))
            for si, val in enumerate(self.dma_vals):
                if val > 0:
                    self._need(E, ("dma", si, val))
            if self.cc_val > 0:
                self._need(E, ("cc", self.cc_val))

    def phase_reset(self):
        self.barrier()
        self.arena_off = 0
        self.arena_gen += 1


class _APHandle:
    def __init__(self, ap):
        self._ap = ap

    def __getitem__(self, idx):
        return self._ap[idx]

    def ap(self):
        return self._ap


import math
KCUT = 9

D = 1024
NCH = 16
TOK = 2048
NTOK = TOK + 256
NLC = 18
DIN = 6176
EPS = 1e-6
GROUPS = [[0, 1, 2, 3], [4, 5, 6, 7]]
SW = 1568
NP = 3 * 1536 + 1536 + 32 + 32 + 16 + 128
RET_EXP_F = (5.0, 6.0, 7.0, 8.0)
RET_EXP_B = (5.5, 6.5, 7.5, 8.5)
BLOCKS = [("aq", 0, 512), ("ak", 512, 512), ("av", 1024, 512), ("ag", 1536, 512),
          ("xbc0", 2048, 512), ("xbc1", 2560, 512), ("xbc2", 3072, 512), ("dtr", 3584, 32),
          ("z0", 3616, 512), ("z1", 4128, 512), ("rqk", 4640, 512), ("rv", 5152, 512), ("rg", 5664, 512)]


class Alt:
    def __init__(self, bufs, par):
        self.bufs, self.par = bufs, par

    @property
    def cur(self):
        return self.bufs[self.par[0] % len(self.bufs)]

    def __getitem__(self, idx):
        return self.cur[idx]

    @property
    def t(self):
        return self.cur.t

    @property
    def whole(self):
        return self.cur.whole


def lam_init_of(layer):
    return 0.8 - 0.6 * math.exp(-0.3 * layer)


def build(depth=2, debug=None, stop_after=None):
    debug = debug or {}
    nc = bass.Bass("TRN2", target_bir_lowering=False)
    fw = FW(nc, same_engine_sync=True)
    I = lambda n, s, d=F32: fw.dram(n, s, d, kind="ExternalInput")
    x_d = I("x", [TOK, D])
    ctx_d = I("ctx", [256, D])
    cc_d = I("cc", [128, 8, 2])
    wada_d = I("w_ada", [2, D, 3 * D])
    bada_f_d = I("b_ada_f", [2, 128, 16])
    bada_d = I("b_ada", [2, 3 * D])
    win_d = I("w_in", [2, D, DIN])
    wout_d = I("w_out", [2, 2 * D, D])
    qkg_d = I("qkg", [2, 128, 2, 64])
    lamv_d = I("lamv", [2, 128, 4, 64])
    subln_d = I("subln", [2, 128, 1])
    rope_d = I("rope", [128, NCH, 2, 32])
    tri_d = I("tri", [128, 5, 128])
    rett_d = I("rett", [128, 4 * 128 + 24])
    ssdp_d = I("ssdp", [2, 128, NP])
    ssdn_d = I("ssdn", [2, 128, 8])
    cmask_d = I("cmask", [128, 8])
    out_d = fw.dram("out", [TOK, D], F32, kind="ExternalOutput")
    dbg = {k: fw.dram("dbg_" + k, shape, dt_, kind="ExternalOutput") for k, (shape, dt_) in debug.items()}

    proj_s = fw.dram("proj_s", [NTOK, DIN], F32)
    qT_s = fw.dram("qT_s", [4, 128, NTOK], BF16)
    agT_s = fw.dram("agT_s", [4, 128, NTOK], BF16)
    mixT_s = fw.dram("mixT_s", [16, 128, NTOK], BF16)
    kc_s = fw.dram("kc_s", [4, 128, 256], BF16)
    vc_s = fw.dram("vc_s", [256, 512], BF16)
    xbc_pad = fw.dram("xbc_pad", [TOK + 2, 1536], F32)
    xbc_cpad = fw.dram("xbc_cpad", [258, 1536], F32)
    hx_stage = fw.dram("hx_stage", [2, 1536], F32)
    hx_src = fw.dram("hx_src", [8, 1536], F32)
    hx_dst = fw.dram("hx_dst", [8, 1536], F32)
    hxL = fw.dram("hxL", [9, 1536], F32)
    hxR = fw.dram("hxR", [8, 1536], F32)
    cst_s = fw.dram("cst_s", [NLC, 2, 128, SW], F32)
    sb_s = fw.dram("sb_s", [NLC, 128, 1536], BF16)
    pc_h = fw.dram("pc_h", [NLC, 128, 3584], BF16)
    pc_f = fw.dram("pc_f", [NLC, 128, 1152], F32)
    rc_f = fw.dram("rc_f", [NLC, 128, 512], F32)
    rc_h = fw.dram("rc_h", [NLC, 128, 1024], BF16)
    rctx_s = fw.dram("rctx_s", [2, 64, 512], F32)
    st_stage = [fw.dram(f"st_stage{d}", [128, SW], F32) for d in range(2)]
    st_src = [fw.dram(f"st_src{d}", [512, SW], F32) for d in range(2)]
    st_dst = [fw.dram(f"st_dst{d}", [512, SW], F32) for d in range(2)]
    kv_src = [fw.dram(f"kv_src{h}", [512, 4096], BF16) for h in range(4)]
    kv_dst = [fw.dram(f"kv_dst{h}", [512, 4096], BF16) for h in range(4)]
    kv_stage = [fw.dram(f"kv_stage{h}", [128, 4096], BF16) for h in range(4)]

    x_sb = fw.sbuf("x_sb", [128, NCH, D], F32)
    ctx_sb = fw.sbuf("ctx_sb", [128, 2, D], F32)
    gate = fw.sbuf("gate", [128, 2, D], F32)
    sc1 = fw.sbuf("sc1", [128, 2, 8], F32)
    sh = fw.sbuf("sh", [128, 2, 8], F32)
    ident = fw.sbuf("ident", [128, 128], F32)
    ident_bf = fw.sbuf("ident_bf", [128, 128], BF16)
    ones_bf = fw.sbuf("ones_bf", [128, 128], BF16)
    eps_t = fw.sbuf("eps_t", [128, 1], F32)
    cc = fw.sbuf("cc", [128, 8, 2], F32)
    rope = fw.sbuf("rope", [128, NCH, 2, 32], F32)
    qkg = fw.sbuf("qkg", [128, 2, 64], F32)
    lamv = fw.sbuf("lamv", [128, 4, 64], F32)
    neglam = fw.sbuf("neglam", [128, 1], F32)
    subln = fw.sbuf("subln", [128, 1], F32)
    small = fw.sbuf("small", [128, 64], F32)
    cmask = fw.sbuf("cmask", [128, 8], F32)
    fw.make_arena(119)
    pb_t = nc.alloc_psum_tensor("ps_banks", [128, 8, 512], F32)
    pb = [Buf(fw, f"pb{i}", _APHandle(pb_t[:, i, :])) for i in range(8)]
    pbh = [Buf(fw, f"pbh{i}", _APHandle(pb_t[:, i, :].bitcast(BF16))) for i in range(8)]
    for i in range(8):
        pbh[i].whole = pb[i].whole
    rank = nc.partition_id() % 4

    fw.memset(ident[:], 1.0, eng="pool")
    fw.op("pool", lambda e: e.affine_select(ident.t[:], ident.t[:], [[-1, 128]], ALU.is_equal, 0.0,
                                             base=0, channel_multiplier=1), [ident[:]], [ident[:]])
    fw.copy(ident_bf[:], ident[:])
    fw.memset(ones_bf[:], 1.0)
    fw.memset(eps_t[:], EPS)
    fw.dma("sp", V(x_sb.t[:], x_sb.whole), V(x_d.t.ap().rearrange("(c p) d -> p c d", p=128), x_d.whole))
    fw.dma("sp", V(ctx_sb.t[:], ctx_sb.whole), V(ctx_d.t.ap().rearrange("(c p) d -> p c d", p=128), ctx_d.whole))
    fw.dma("sp", cc[:], cc_d[:])
    fw.dma("sp", rope[:], rope_d[:])
    fw.dma("sp", cmask[:], cmask_d[:])
    fw.act(cc[:], cc[:], AF.Silu)
    zt = fw.carve("zt", [128, 4096], BF16)
    fw.memset(zt[:], 0.0)
    for h in range(4):
        fw.dma("sp", V(kv_src[h].t.ap().rearrange("(r p) c -> p r c", p=128), kv_src[h].whole),
               V(zt.t[:].unsqueeze(1).to_broadcast([128, 4, 4096]), zt.whole))
    zf = fw.carve("zf", [128, SW], F32)
    fw.memset(zf[:], 0.0)
    fw.dma("sp", xbc_cpad[0:1, :], zf[0:1, 0:1536])
    fw.dma("sp", xbc_cpad[257:258, :], zf[0:1, 0:1536])
    fw.dma("sp", hx_src[:, :], zf[0:8, 0:1536])
    fw.dma("sp", hxL[:, :], zf[0:9, 0:1536])
    fw.dma("sp", hxR[:, :], zf[0:8, 0:1536])
    for d in range(2):
        fw.dma("sp", V(st_src[d].t.ap().rearrange("(r p) c -> p r c", p=128), st_src[d].whole),
               V(zf.t[:].unsqueeze(1).to_broadcast([128, 4, SW]), zf.whole))
        fw.dma("sp", st_stage[d][:, :], zf[:, :])
    fw.phase_reset()

    def adaln(l):
        ccrep = fw.carve("ccrep", [128, 8, 2, 128], F32)
        badaf = fw.carve("badaf", [128, 16], F32)
        gbias = fw.carve("gbias", [128, D], F32)
        wada2 = [fw.carve(f"wada_sb{i}", [128, 8, 512], F32) for i in range(2)]
        fw.copy(ccrep[:], V(cc.t[:].unsqueeze(3).to_broadcast([128, 8, 2, 128]), cc.whole))
        fw.dma("sp", badaf[:], bada_f_d[l])
        fw.dma("sp", gbias[:], V(bada_d.t[l:l + 1, 2 * D:3 * D].partition_broadcast(128), bada_d.whole))
        ps_s = V(pb[2].t[:, 0:32].rearrange("p (a b) -> p a b", b=2), pb[2].whole)
        for piece in range(6):
            wada_sb = wada2[piece % 2]
            fw.dma("sp", V(wada_sb.t[:], wada_sb.whole),
                   V(wada_d.t[l, :, piece * 512:(piece + 1) * 512].rearrange("(k p) c -> p k c", p=128), wada_d.whole))
            if piece < 4:
                for j in range(4):
                    blk = piece * 4 + j
                    for k in range(8):
                        fw.mm(V(ps_s.ap[:, blk, :], ps_s.res), wada_sb[:, k, j * 128:(j + 1) * 128], cc[:, k, :],
                              start=(k == 0), stop=(k == 7))
            else:
                half = piece - 4
                for v in range(2):
                    for k in range(8):
                        fw.mm(pb[3][:, :], ccrep[:, k, v, :], wada_sb[:, k, :], start=(k == 0), stop=(k == 7))
                    fw.tt(gate[:, v, half * 512:(half + 1) * 512], pb[3][:, :], gbias[:, half * 512:(half + 1) * 512], ALU.add)
        for v in range(2):
            fw.tt(sh[:, v, :], V(ps_s.ap[:, 0:8, v], ps_s.res), badaf[:, 0:8], ALU.add)
            fw.tt(sc1[:, v, :], V(ps_s.ap[:, 8:16, v], ps_s.res), badaf[:, 8:16], ALU.add)
        fw.ts(sc1[:], sc1[:], 1.0, ALU.add)
        fw.phase_reset()

    def layer_params(l):
        fw.dma("sp", qkg[:], qkg_d[l])
        fw.dma("sp", lamv[:], lamv_d[l])
        fw.dma("sp", subln[:], subln_d[l])
        fw.tt(small[:, 0:64], lamv[:, 0, :], lamv[:, 1, :], ALU.mult)
        s1 = fw.sbuf(f"lam_s1_{l}", [128, 1], F32)
        s2 = fw.sbuf(f"lam_s2_{l}", [128, 1], F32)
        fw.reduce(s1[:], small[:, 0:64], ALU.add)
        fw.tt(small[:, 0:64], lamv[:, 2, :], lamv[:, 3, :], ALU.mult)
        fw.reduce(s2[:], small[:, 0:64], ALU.add)
        fw.act(s1[:], s1[:], AF.Exp)
        fw.act(s2[:], s2[:], AF.Exp)
        fw.tt(neglam[:], s2[:], s1[:], ALU.subtract)
        fw.ts(neglam[:], neglam[:], -lam_init_of(l), ALU.add)
        fw.ts(subln[:], subln[:], 1.0 - lam_init_of(l), ALU.mult)

    def phase_proj(l, need_ctx_q):
        hT = fw.carve("hT", [128, 8, NTOK], BF16)
        par = [0]
        alt = lambda n, sh, dt_: Alt([fw.carve(f"{n}_{i}", sh, dt_) for i in range(2)], par)
        xn = alt("xn", [128, D], F32)
        junk = alt("junk", [128, D], F32)
        ss = alt("ss", [128, 1], F32)
        rs = alt("rs", [128, 1], F32)
        wblk = [fw.carve(f"wblk{i}", [128, 8, 512], BF16) for i in range(2)]
        wst = fw.carve("wst", [128, 8, 512], F32)
        stage = [fw.carve(f"stage{i}", [128, 512], F32) for i in range(3)]
        sq = alt("sq", [128, 8, 64], F32)
        qn = alt("qn", [128, 8, 64], F32)
        t1 = alt("t1", [128, 8, 32], F32)
        t2 = alt("t2", [128, 8, 32], F32)
        ss8 = alt("ss8", [128, 8], F32)
        qbf = [fw.carve(f"qbf{i}", [128, 8, 64], BF16) for i in range(2)]
        tb = [fw.carve(f"tb{i}", [128, 4, 128], BF16) for i in range(2)]
        vbf = [fw.carve(f"vbf{i}", [128, 512], BF16) for i in range(2)]

        for lc in range(NLC):
            par[0] = lc
            src = x_sb[:, lc, :] if lc < NCH else ctx_sb[:, lc - NCH, :]
            v = 0 if lc < NCH else 1
            fw.act(junk[:], src, AF.Square, accum_out=ss[:])
            fw.act(rs[:], ss[:], AF.Sqrt, bias=eps_t[:], scale=1.0 / D)
            fw.recip(rs[:], rs[:])
            fw.ts(xn[:], src, rs[:], ALU.mult)
            pt = V(pb[lc % 2 * 2].t[:, :], [pb[lc % 2 * 2].whole, pb[lc % 2 * 2 + 1].whole])
            ptt = pb_t[:, lc % 2 * 2:lc % 2 * 2 + 2, :].rearrange("p a (k c) -> p (a k) c", c=128)
            for k in range(8):
                fw.transpose(V(ptt[:, k, :], pt.res), xn[:, k * 128:(k + 1) * 128], ident[:], last=(k == 7))
            for k in range(8):
                fw.ts(hT[:, k, lc * 128:(lc + 1) * 128], V(ptt[:, k, :], pt.res), sc1[:, v, k:k + 1], ALU.mult,
                      sh[:, v, k:k + 1], ALU.add)
        if "hT" in dbg:
            fw.dma("sp", V(dbg["hT"].t[:], dbg["hT"].whole), V(hT.t[:], hT.whole))

        it = 0
        for bi, (bname, col0, ncols) in enumerate(BLOCKS):
            wb = wblk[bi % 2]
            fw.dma("sp", V(wst.t[:, :, 0:ncols], wst.whole),
                   V(win_d.t[l, :, col0:col0 + ncols].rearrange("(k p) c -> p k c", p=128), win_d.whole))
            fw.copy(V(wb.t[:, :, 0:ncols], wb.whole), V(wst.t[:, :, 0:ncols], wst.whole), eng="pool")
            for lc in range(NLC):
                is_ctx = lc >= NCH
                if is_ctx and not need_ctx_q and bname in ("ag", "z0", "z1", "rg"):
                    continue
                bank = pb[4 + it % 2]
                it += 1
                par[0] = it
                for k in range(8):
                    fw.mm(bank[:, 0:ncols], hT[:, k, lc * 128:(lc + 1) * 128], wb[:, k, 0:ncols],
                          start=(k == 0), stop=(k == 7))
                rows = slice(lc * 128, (lc + 1) * 128)
                if bname in ("aq", "ak"):
                    if bname == "aq" and is_ctx and not need_ctx_q:
                        continue
                    gi = 0 if bname == "aq" else 1
                    psv = V(bank.t[:, :].rearrange("p (a b) -> p a b", b=64), bank.whole)
                    fw.act(sq[:], psv, AF.Square)
                    fw.reduce(ss8[:], sq[:], ALU.add)
                    fw.act(ss8[:], ss8[:], AF.Sqrt, bias=eps_t[:], scale=1.0 / 64)
                    fw.recip(ss8[:], ss8[:])
                    fw.tt(qn[:], psv, V(ss8.t[:].unsqueeze(2).to_broadcast([128, 8, 64]), ss8.whole), ALU.mult)
                    fw.tt(qn[:], qn[:], V(qkg.t[:, gi:gi + 1, :].to_broadcast([128, 8, 64]), qkg.whole), ALU.mult, eng="pool")
                    qo = qbf[it % 2]
                    if not is_ctx:
                        cosb = V(rope.t[:, lc, 0:1, :].to_broadcast([128, 8, 32]), rope.whole)
                        sinb = V(rope.t[:, lc, 1:2, :].to_broadcast([128, 8, 32]), rope.whole)
                        fw.tt(t1[:], qn[:, :, 0:32], cosb, ALU.mult)
                        fw.tt(t2[:], qn[:, :, 32:64], sinb, ALU.mult, eng="pool")
                        fw.tt(qo[:, :, 0:32], t1[:], t2[:], ALU.subtract)
                        fw.tt(t1[:], qn[:, :, 0:32], sinb, ALU.mult)
                        fw.tt(t2[:], qn[:, :, 32:64], cosb, ALU.mult, eng="pool")
                        fw.tt(qo[:, :, 32:64], t1[:], t2[:], ALU.add)
                    else:
                        fw.copy(qo[:], qn[:])
                    tbank = pbh[6 + lc % 2]
                    tbv = tbank.t[:, 0:512].rearrange("p (h c) -> p h c", c=128)
                    qof = qo.t[:].rearrange("p a b -> p (a b)")
                    for hd in range(4):
                        fw.transpose(V(tbv[:, hd, :], tbank.whole), V(qof[:, hd * 128:(hd + 1) * 128], qo.whole), ident_bf[:], last=(hd == 3))
                    tbs = tb[lc % 2]
                    fw.copy(tbs[:], V(tbv, tbank.whole), eng="act")
                    if bname == "aq":
                        fw.dma("act", V(qT_s.t[:, :, rows].rearrange("h p c -> p h c"), qT_s.part(lc).res), tbs[:])
                    elif is_ctx:
                        c0 = (lc - NCH) * 128
                        fw.dma("act", V(kc_s.t[:, :, c0:c0 + 128].rearrange("h p c -> p h c"), kc_s.part(lc).res), tbs[:])
                    else:
                        for hd in range(4):
                            fw.dma("act", V(kv_stage[hd].t[:, lc * 128:(lc + 1) * 128], kv_stage[hd].part(("k", lc)).res),
                                   tbs[:, hd, :])
                elif bname == "av":
                    vb = vbf[lc % 2]
                    fw.copy(vb[:], bank[:, :], eng="act")
                    if is_ctx:
                        c0 = (lc - NCH) * 128
                        fw.dma("act", V(vc_s.t[c0:c0 + 128, :], vc_s.part(lc).res), vb[:])
                    else:
                        for hd in range(4):
                            fw.dma("act", V(kv_stage[hd].t[:, 2048 + lc * 128:2048 + (lc + 1) * 128],
                                            kv_stage[hd].part(("v", lc)).res), vb[:, hd * 128:(hd + 1) * 128])
                elif bname == "ag":
                    st = stage[lc % 3]
                    fw.act(st[:], bank[:, :], AF.Silu)
                    tbank = pb[6 + lc % 2]
                    for hd in range(4):
                        fw.transpose(tbank[:, hd * 128:(hd + 1) * 128], st[:, hd * 128:(hd + 1) * 128], ident[:], last=(hd == 3))
                    tbs = tb[lc % 2]
                    fw.copy(V(tbs.t[:].rearrange("p h c -> p (h c)"), tbs.whole), tbank[:, :])
                    fw.dma("act", V(agT_s.t[:, :, rows].rearrange("h p c -> p h c"), agT_s.part(lc).res), tbs[:])
                else:
                    st = stage[lc % 3]
                    if lc % 2 == 0:
                        fw.copy(st[:, 0:ncols], bank[:, 0:ncols])
                    else:
                        fw.copy(st[:, 0:ncols], bank[:, 0:ncols], eng="act")
                    if bname.startswith("xbc"):
                        xc = (int(bname[3]) * 512)
                        if is_ctx:
                            r1 = 1 + (lc - NCH) * 128
                            fw.dma("sp", V(xbc_cpad.t[r1:r1 + 128, xc:xc + 512], xbc_cpad.part((bname, lc)).res), st[:, 0:ncols])
                        else:
                            r1 = 1 + lc * 128
                            fw.dma("sp", V(xbc_pad.t[r1:r1 + 128, xc:xc + 512], xbc_pad.part((bname, lc)).res), st[:, 0:ncols])
                    else:
                        fw.dma("sp", V(proj_s.t[rows, col0:col0 + ncols], proj_s.part((bname, lc)).res), st[:, 0:ncols])
            if bname == "xbc2":
                halo_issue(l)
            if bname == "av":
                for hd in range(4):
                    allres = [kv_stage[hd].part(("k", c)).res for c in range(NCH)] + [kv_stage[hd].part(("v", c)).res for c in range(NCH)]
                    fw.dma("sp", V(kv_src[hd].t[bass.ds(rank * 128, 128), :], kv_src[hd].whole), V(kv_stage[hd].t[:, :], allres))
                    fw.collective("AllReduce", ALU.add, GROUPS, kv_src[hd][:, :], kv_dst[hd][:, :])
        fw.phase_reset()

    def phase_attn(l, with_ctx_q):
        kTb = [fw.carve(f"kT{i}", [128, 8448], BF16) for i in range(2)]
        vvb = [fw.carve(f"vv{i}", [128, 66, 128], BF16) for i in range(2)]
        qhb = [fw.carve(f"qh{i}", [128, NTOK], BF16) for i in range(2)]
        aghb = [fw.carve(f"agh{i}", [128, NTOK], BF16) for i in range(2)]
        rec = fw.carve("rec", [128, 512], F32)
        om = [fw.carve(f"om{i}", [128, 512], F32) for i in range(2)]
        A = fw.carve("A", [128, 512], F32)
        sqb = fw.carve("sqb", [128, 512], BF16)
        rstd = fw.carve("rstd", [128, 512], F32)
        mixo = [fw.carve(f"mixo{i}", [128, 512], BF16) for i in range(2)]
        ones_f = fw.carve("ones_f", [128, 128], F32)
        fw.memset(ones_f[:], 1.0)
        qblocks = [(q0, 512, 0, 66) for q0 in range(0, TOK, 512)]
        if with_ctx_q:
            qblocks.append((TOK, 256, 64, 66))
        sbank = (pb[0], pb[1], pb[7])
        nq_all = NTOK if with_ctx_q else TOK

        def load_head(hd):
            kT, vv, qh, agh = kTb[hd % 2], vvb[hd % 2], qhb[hd % 2], aghb[hd % 2]
            fw.dma("sp", V(kT.t[:, 0:8192].rearrange("p (r c) -> p r c", r=4), kT.whole),
                   V(kv_dst[hd].t[:, 0:2048].rearrange("(r p) c -> p r c", p=128), kv_dst[hd].whole))
            fw.dma("sp", kT[:, 8192:8448], V(kc_s.t[hd], [kc_s.part(16).res, kc_s.part(17).res]))
            fw.dma("sp", V(vv.t[:, 0:64, :].rearrange("p (r c) e -> p r c e", r=4), vv.whole),
                   V(kv_dst[hd].t[:, 2048:4096].rearrange("(r p) (c e) -> p r c e", p=128, e=128), kv_dst[hd].whole))
            fw.dma("sp", vv[:, 64:66, :], V(vc_s.t[:, hd * 128:(hd + 1) * 128].rearrange("(c p) e -> p c e", p=128),
                                           [vc_s.part(16).res, vc_s.part(17).res]))
            qres = [qT_s.part(c).res for c in range(NLC if with_ctx_q else NCH)]
            fw.dma("sp", qh[:, 0:nq_all], V(qT_s.t[hd, :, 0:nq_all], qres))
            fw.dma("sp", agh[:, 0:NTOK], V(agT_s.t[hd], [agT_s.part(c).res for c in range(NLC)]))

        load_head(0)
        pT2 = [fw.carve(f"pTT{i}", [128, 2, 512], BF16) for i in range(3)]
        acc2 = [fw.carve(f"accT{i}", [128, 2, 512], F32) for i in range(2)]
        stage_banks = ((0, 1), (4, 5))
        for hd in range(4):
            if hd + 1 < 4:
                load_head(hd + 1)
            kT, vv, qh, agh = kTb[hd % 2], vvb[hd % 2], qhb[hd % 2], aghb[hd % 2]
            for qi, (q0, nq_, kc0, kc1) in enumerate(qblocks):
                kcs = list(range(kc0, kc1))
                n = len(kcs)

                def qk(i):
                    kc = kcs[i]
                    b0, b1 = stage_banks[i % 2]
                    for m, bk in ((0, b0), (1, b1)):
                        fw.mm(pb[bk][:, 0:nq_], kT[m * 64:(m + 1) * 64, kc * 128:(kc + 1) * 128], qh[m * 64:(m + 1) * 64, q0:q0 + nq_], True, True)

                qk(0)
                for i, kc in enumerate(kcs):
                    if i + 1 < n:
                        qk(i + 1)
                    b0, b1 = stage_banks[i % 2]
                    p = pT2[i % 3]
                    sc2 = V(pb_t[:, b0:b0 + 2, 0:nq_], [pb[b0].whole, pb[b1].whole])
                    fw.act(p[:, :, 0:nq_], sc2, AF.Exp, scale=0.125)
                    first, lastk = (i == 0), (i == n - 1)
                    for m in range(2):
                        fw.mm(pb[2 + m][:, 0:nq_], vv[:, kc, :], p[:, m, 0:nq_], start=first, stop=lastk)
                    eng_ = "dve" if i % 2 == 0 else "pool"
                    acc_ = acc2[i % 2]
                    if i < 2:
                        fw.copy(acc_[:, :, 0:nq_], p[:, :, 0:nq_], eng=eng_)
                    else:
                        fw.tt(acc_[:, :, 0:nq_], acc_[:, :, 0:nq_], p[:, :, 0:nq_], ALU.add, eng=eng_)
                if n > 1:
                    fw.tt(acc2[0][:, :, 0:nq_], acc2[0][:, :, 0:nq_], acc2[1][:, :, 0:nq_], ALU.add)
                for m in range(2):
                    fw.mm(pb[7][:, 0:nq_], ones_f[:], acc2[0][:, m, 0:nq_], True, True)
                    fw.recip(rec[:, 0:nq_], pb[7][:, 0:nq_])
                    fw.tt(om[m][:, 0:nq_], pb[2 + m][:, 0:nq_], rec[:, 0:nq_], ALU.mult)
                fw.stt(A[:, 0:nq_], om[1][:, 0:nq_], neglam[:], om[0][:, 0:nq_], ALU.mult, ALU.add)
                fw.act(sqb[:, 0:nq_], A[:, 0:nq_], AF.Square)
                fw.mm(pb[6][:, 0:nq_], ones_bf[:], sqb[:, 0:nq_], True, True)
                fw.act(rstd[:, 0:nq_], pb[6][:, 0:nq_], AF.Sqrt, bias=eps_t[:], scale=1.0 / 128)
                fw.recip(rstd[:, 0:nq_], rstd[:, 0:nq_])
                fw.stt(A[:, 0:nq_], A[:, 0:nq_], subln[:], rstd[:, 0:nq_], ALU.mult, ALU.mult)
                mo = mixo[qi % 2]
                fw.tt(mo[:, 0:nq_], A[:, 0:nq_], agh[:, q0:q0 + nq_], ALU.mult, eng="pool")
                fw.dma("act", V(mixT_s.t[hd, :, q0:q0 + nq_], mixT_s.part((hd, qi)).res), mo[:, 0:nq_])
        fw.phase_reset()

    def halo_issue(l):
        xr = lambda c: [xbc_pad.part((f"xbc{i}", c)).res for i in range(3)]
        fw.dma("sp", hx_stage[0:1, :], V(xbc_pad.t[1:2, :], xr(0)))
        fw.dma("sp", hx_stage[1:2, :], V(xbc_pad.t[TOK:TOK + 1, :], xr(NCH - 1)))
        fw.dma("sp", V(hx_src.t[bass.ds(rank * 2, 2), :], hx_src.whole), hx_stage[:, :])
        fw.collective("AllReduce", ALU.add, GROUPS, hx_src[:, :], hx_dst[:, :])

    def phase_halo(l):
        fw.dma("sp", hxL[1:9, :], hx_dst[:, :])
        fw.dma("sp", hxR[0:6, :], hx_dst[2:8, :])
        fw.dma("sp", V(xbc_pad.t[0:1, :], xbc_pad.part("hl").res), V(hxL.t[bass.ds(rank * 2, 1), :], hxL.whole))
        fw.dma("sp", V(xbc_pad.t[TOK + 1:TOK + 2, :], xbc_pad.part("hr").res), V(hxR.t[bass.ds(rank * 2, 1), :], hxR.whole))

    def rope_apply(out, x, lc, nh, t1, t2):
        cosb = V(rope.t[:, lc, 0:1, :].to_broadcast([128, nh, 32]), rope.whole)
        sinb = V(rope.t[:, lc, 1:2, :].to_broadcast([128, nh, 32]), rope.whole)
        fw.tt(t1[:, 0:nh, :], x[:, :, 0:32], cosb, ALU.mult)
        fw.tt(t2[:, 0:nh, :], x[:, :, 32:64], sinb, ALU.mult, eng="pool")
        fw.tt(out[:, :, 0:32], t1[:, 0:nh, :], t2[:, 0:nh, :], ALU.subtract)
        fw.tt(t1[:, 0:nh, :], x[:, :, 0:32], sinb, ALU.mult)
        fw.tt(t2[:, 0:nh, :], x[:, :, 32:64], cosb, ALU.mult, eng="pool")
        fw.tt(out[:, :, 32:64], t1[:, 0:nh, :], t2[:, 0:nh, :], ALU.add)

    def chain_combine(Sin, Sctx, d, col0, ncol, nparts, Aexp_of, tmp, slot):
        fw.copy(Sin[0:nparts, :], Sctx[0:nparts, :])
        order = range(4) if d == 0 else range(3, -1, -1)
        for sidx in order:
            fw.dma("sp", slot[0:nparts, 0:ncol], st_dst[d][sidx * 128:sidx * 128 + nparts, col0:col0 + ncol])
            Aexp_of(sidx, tmp)
            fw.tt(tmp[0:nparts, :], tmp[0:nparts, :], slot[0:nparts, 0:ncol], ALU.add)
            fw.tt(tmp[0:nparts, :], tmp[0:nparts, :], Sin[0:nparts, :], ALU.subtract)
            mcol = cmask[:, d * 4 + sidx:d * 4 + sidx + 1]
            fw.stt(Sin[0:nparts, :], tmp[0:nparts, :], V(mcol.ap[0:nparts], mcol.res), Sin[0:nparts, :], ALU.mult, ALU.add)

    def phase_ret(l, need_ctx, part):
        rett = fw.carve("rett", [128, 4 * 128 + 24], F32)
        rnorm = fw.carve("rnorm", [128, 128], F32)
        fw.dma("sp", rett[:], rett_d[:])
        fw.dma("sp", rnorm[:], ssdp_d[l, :, NP - 128:NP])
        Dret = V(rett.t[:, 0:512].rearrange("p (h i) -> p h i", i=128), rett.whole)
        tab = lambda k: V(rett.t[:, 512 + 4 * k:512 + 4 * k + 4], rett.whole)
        par = [0]
        alt = lambda n, sh, dt_: Alt([fw.carve(f"{n}_{i}", sh, dt_) for i in range(2)], par)
        qk = alt("qk", [128, 8, 64], F32)
        qkr = alt("qkr", [128, 8, 64], F32)
        t1 = alt("rt1", [128, 8, 32], F32)
        t2 = alt("rt2", [128, 8, 32], F32)
        rv = alt("rv", [128, 512], F32)
        rvbf = alt("rvbf", [128, 512], BF16)
        kte = [alt(f"kte{d}", [128, 4, 64], BF16) for d in range(2)]
        q3 = alt("q3", [128, 3, 4, 64], BF16)
        kbf = alt("kbf", [128, 4, 64], BF16)
        qT = alt("qT", [64, 3, 4, 128], BF16)
        kT = alt("kT", [64, 4, 128], BF16)
        Wt = alt("Wt", [128, 4, 128], BF16)
        R = [fw.carve(f"R{d}", [64, 512], F32) for d in range(2)]
        Rbf = [fw.carve(f"Rbf{d}", [64, 512], BF16) for d in range(2)]
        Rctx = [fw.carve(f"Rctx{d}", [64, 512], F32) for d in range(2)]
        Pb = fw.carve("Pb", [128, 4], F32)
        cs_sb = [alt(f"cs_sb{d}", [64, 512], F32) for d in range(2)]
        tmp = fw.carve("rtmp", [64, 512], F32)
        slot = fw.carve("rslot", [64, 512], F32)
        rg = alt("rg", [128, 512], F32)
        ysq = alt("ysq", [128, 4, 128], F32)
        yss = alt("yss", [128, 4], F32)
        yn = alt("yn", [128, 4, 128], F32)
        ybf = alt("ybf", [128, 512], BF16)
        ytb = alt("ytb", [128, 4, 128], BF16)
        bc4 = lambda v, n: V(v.ap.unsqueeze(2).to_broadcast([v.ap.shape[0], 4, n]), v.res)

        def prep(lc):
            par[0] = lc
            is_ctx = lc >= NCH
            rows = slice(lc * 128, (lc + 1) * 128)
            pr = lambda n: proj_s.part((n, lc)).res
            fw.dma("sp", V(qk.t[:].rearrange("p a b -> p (a b)"), qk.whole), V(proj_s.t[rows, 4640:5152], pr("rqk")))
            fw.dma("sp", rv[:], V(proj_s.t[rows, 5152:5664], pr("rv")))
            if is_ctx:
                src = qk
            else:
                rope_apply(qkr, qk, lc, 8, t1, t2)
                src = qkr
            fw.copy(rvbf[:], rv[:], eng="pool")
            for d in range(2):
                fw.tt(kte[d][:], src[:, 4:8, :], bc4(tab(d), 64), ALU.mult)
            return src

        def chunk_states(start_banks=(0, 1)):
            for d in range(2):
                bank = pb[start_banks[d]]
                for h in range(4):
                    fw.mm(bank[0:64, h * 128:(h + 1) * 128], kte[d][:, h, :], rvbf[:, h * 128:(h + 1) * 128],
                          start=(h == 0), stop=(h == 3), skip_group_check=True)
            return [pb[start_banks[0]], pb[start_banks[1]]]

        A128 = [tab(4), tab(5)]

        def fold(acc, csb, lc, store):
            fw.tt(V(acc[0].t[:].rearrange("p (h n) -> p h n", n=128), acc[0].whole),
                  V(acc[0].t[:].rearrange("p (h n) -> p h n", n=128), acc[0].whole),
                  V(A128[0].ap[0:64].unsqueeze(2).to_broadcast([64, 4, 128]), A128[0].res), ALU.mult)
            fw.tt(acc[0][:], acc[0][:], csb[0][0:64, :], ALU.add)
            fw.tt(V(tmp.t[:].rearrange("p (h n) -> p h n", n=128), tmp.whole),
                  V(csb[1].t[0:64, :].rearrange("p (h n) -> p h n", n=128), csb[1].whole),
                  V(Pb.t[0:64, :].unsqueeze(2).to_broadcast([64, 4, 128]), Pb.whole), ALU.mult)
            fw.tt(acc[1][:], acc[1][:], tmp[:], ALU.add)
            fw.tt(Pb[:], Pb[:], A128[1], ALU.mult)
            if store:
                for d in range(2):
                    fw.copy(cs_sb[d][:], csb[d][0:64, :])
                    fw.dma("act", V(cst_s.t[lc, d, 0:64, 1024:1536], cst_s.part(("r", lc, d)).res), cs_sb[d][:])

        for grp in (((16, 17), tuple(range(NCH))) if part == "a" else ()):
            for d in range(2):
                fw.memset(R[d][:], 0.0)
            fw.memset(Pb[:], 1.0)
            for lc in grp:
                src_ = prep(lc)
                if lc < NCH or need_ctx:
                    fw.dma("sp", V(rc_f.t[lc], rc_f.part(lc).res), V(src_.t[:].rearrange("p a b -> p (a b)"), src_.whole))
                    fw.dma("sp", V(rc_h.t[lc, :, 0:512], rc_h.part(lc).res), rvbf[:])
                    for d in range(2):
                        fw.dma("sp", V(rc_h.t[lc, :, 512 + d * 256:768 + d * 256], rc_h.part(lc).res),
                               V(kte[d].t[:].rearrange("p a b -> p (a b)"), kte[d].whole))
                if KCUT >= 2:
                    csb = chunk_states()
                if KCUT >= 3:
                    fold(R, csb, lc, KCUT >= 4)
            if grp[0] == 16:
                for d in range(2):
                    fw.copy(Rctx[d][:], R[d][:])
        if "ret_sf" in dbg:
            fw.dma("sp", dbg["ret_sf"][:, :], Rctx[0][:])
            fw.dma("sp", dbg["ret_sb"][:, :], Rctx[1][:])
        if stop_after == "ret_p1":
            fw.phase_reset(); return
        if part == "a":
            for d in range(2):
                fw.dma("sp", st_stage[d][0:64, 1024:1536], R[d][:])
                fw.dma("sp", V(rctx_s.t[d], rctx_s.whole), Rctx[d][:])
            fw.phase_reset()
            return
        for d in range(2):
            fw.dma("sp", Rctx[d][:], V(rctx_s.t[d], rctx_s.whole))
        A2048 = [[(1.0 - 2.0 ** -e) ** 2048 for e in RET_EXP_F], [(1.0 - 2.0 ** -e) ** 2048 for e in RET_EXP_B]]
        Rin = [fw.carve(f"Rin{d}", [64, 512], F32) for d in range(2)]
        for d in range(2):
            def aexp(sidx, t_, d=d):
                for h in range(4):
                    fw.ts(t_[0:64, h * 128:(h + 1) * 128], Rin[d][0:64, h * 128:(h + 1) * 128], float(A2048[d][h]), ALU.mult)
            chain_combine(Rin[d], Rctx[d], d, 1024, 512, 64, aexp, tmp, slot)
        snap = fw.carve("rsnap", [64, 512], BF16)
        csl = fw.carve("rcsl", [64, 512], F32)
        fw.copy(R[1][:], Rin[1][:])
        for lc in range(NCH - 1, -1, -1):
            fw.copy(snap[:], R[1][:])
            fw.dma("act", V(sb_s.t[lc, 0:64, 1024:1536], sb_s.part(("r", lc)).res), snap[:])
            fw.dma("sp", csl[:], V(cst_s.t[lc, 1, 0:64, 1024:1536], cst_s.part(("r", lc, 1)).res))
            fw.tt(V(R[1].t[:].rearrange("p (h n) -> p h n", n=128), R[1].whole),
                  V(R[1].t[:].rearrange("p (h n) -> p h n", n=128), R[1].whole),
                  V(A128[1].ap[0:64].unsqueeze(2).to_broadcast([64, 4, 128]), A128[1].res), ALU.mult)
            fw.tt(R[1][:], R[1][:], csl[:], ALU.add)
        if need_ctx:
            fw.dma("sp", csl[:], V(cst_s.t[17, 1, 0:64, 1024:1536], cst_s.part(("r", 17, 1)).res))
            fw.copy(snap[:], csl[:])
            fw.dma("sp", V(sb_s.t[16, 0:64, 1024:1536], sb_s.part(("r", 16)).res), snap[:])
            snap0 = fw.carve("rsnap0", [64, 512], BF16)
            fw.memset(snap0[:], 0.0)
            fw.dma("sp", V(sb_s.t[17, 0:64, 1024:1536], sb_s.part(("r", 17)).res), snap0[:])
        if stop_after == "ret_p2":
            fw.phase_reset(); return
        groups3 = [tuple(range(NCH))] + ([(16, 17)] if need_ctx else [])
        for grp in groups3:
            if grp[0] == 16:
                fw.memset(R[0][:], 0.0)
            else:
                fw.copy(R[0][:], Rin[0][:])
            for lc in grp:
                is_ctx = lc >= NCH
                rows = slice(lc * 128, (lc + 1) * 128)
                par[0] = lc
                src = qkr
                fw.dma("sp", V(qkr.t[:].rearrange("p a b -> p (a b)"), qkr.whole), V(rc_f.t[lc], rc_f.part(lc).res))
                fw.dma("sp", rvbf[:], V(rc_h.t[lc, :, 0:512], rc_h.part(lc).res))
                for d in range(2):
                    fw.dma("sp", V(kte[d].t[:].rearrange("p a b -> p (a b)"), kte[d].whole),
                           V(rc_h.t[lc, :, 512 + d * 256:768 + d * 256], rc_h.part(lc).res))
                fw.dma("sp", rg[:], V(proj_s.t[rows, 5664:6176], proj_s.part(("rg", lc)).res))
                fw.dma("sp", Rbf[1][:], V(sb_s.t[lc, 0:64, 1024:1536], sb_s.part(("r", lc)).res))
                fw.copy(Rbf[0][:], R[0][:])
                fw.copy(q3[:, 0, :, :], src[:, 0:4, :])
                fw.tt(q3[:, 1, :, :], src[:, 0:4, :], bc4(tab(2), 64), ALU.mult)
                fw.tt(q3[:, 2, :, :], src[:, 0:4, :], bc4(tab(3), 64), ALU.mult, eng="pool")
                fw.copy(kbf[:], src[:, 4:8, :])
                tqa = pbh[2]
                tqav = tqa.t[0:64, 0:1024].rearrange("p (k h c) -> p k h c", k=2, h=4)
                tqb = pbh[3]
                tqbv = tqb.t[0:64, 0:512].rearrange("p (h c) -> p h c", h=4)
                for k3 in range(2):
                    for h in range(4):
                        fw.transpose(V(tqav[:, k3, h, :], tqa.whole), q3[:, k3, h, :], ident_bf[:], last=(k3 == 1 and h == 3))
                for h in range(4):
                    fw.transpose(V(tqbv[:, h, :], tqb.whole), q3[:, 2, h, :], ident_bf[:], last=(h == 3))
                fw.copy(qT[:, 0:2, :, :], V(tqav, tqa.whole))
                fw.copy(qT[:, 2, :, :], V(tqbv, tqb.whole))
                tk = pbh[4]
                tkv = tk.t[0:64, 0:512].rearrange("p (h c) -> p h c", h=4)
                for h in range(4):
                    fw.transpose(V(tkv[:, h, :], tk.whole), kbf[:, h, :], ident_bf[:], last=(h == 3))
                fw.copy(kT[:], V(tkv, tk.whole))
                sc = pb[5]
                for h in range(4):
                    fw.mm(sc[:, h * 128:(h + 1) * 128], kT[:, h, :], qT[:, 0, h, :], start=(h == 0), stop=(h == 3), skip_group_check=True)
                fw.tt(Wt[:], V(sc.t[:, :].rearrange("p (h i) -> p h i", i=128), sc.whole), Dret, ALU.mult)
                csb = chunk_states((0, 1))
                yb = pb[6]
                for h in range(4):
                    o = yb[:, h * 128:(h + 1) * 128]
                    fw.mm(o, Wt[:, h, :], rvbf[:, h * 128:(h + 1) * 128], start=(h == 0), stop=False, last=False, skip_group_check=True)
                    fw.mm(o, qT[:, 1, h, :], Rbf[0][:, h * 128:(h + 1) * 128], start=False, stop=False, last=False, skip_group_check=True)
                    fw.mm(o, qT[:, 2, h, :], Rbf[1][:, h * 128:(h + 1) * 128], start=False, stop=(h == 3), last=(h == 3), skip_group_check=True)
                fw.tt(V(R[0].t[:].rearrange("p (h n) -> p h n", n=128), R[0].whole),
                      V(R[0].t[:].rearrange("p (h n) -> p h n", n=128), R[0].whole),
                      V(A128[0].ap[0:64].unsqueeze(2).to_broadcast([64, 4, 128]), A128[0].res), ALU.mult)
                fw.tt(R[0][:], R[0][:], csb[0][0:64, :], ALU.add)
                ybv = V(yb.t[:, :].rearrange("p (h n) -> p h n", n=128), yb.whole)
                fw.act(ysq[:], ybv, AF.Square)
                fw.reduce(yss[:], ysq[:], ALU.add)
                fw.act(yss[:], yss[:], AF.Sqrt, bias=eps_t[:], scale=1.0 / 128)
                fw.recip(yss[:], yss[:])
                fw.tt(yn[:], ybv, V(yss.t[:].unsqueeze(2).to_broadcast([128, 4, 128]), yss.whole), ALU.mult)
                fw.tt(yn[:], yn[:], V(rnorm.t[:].unsqueeze(1).to_broadcast([128, 4, 128]), rnorm.whole), ALU.mult, eng="pool")
                fw.act(rg[:], rg[:], AF.Silu)
                fw.tt(ybf[:], V(yn.t[:].rearrange("p h n -> p (h n)"), yn.whole), rg[:], ALU.mult)
                to = pbh[7]
                tov = to.t[:, 0:512].rearrange("p (h c) -> p h c", h=4)
                for h in range(4):
                    fw.transpose(V(tov[:, h, :], to.whole), ybf[:, h * 128:(h + 1) * 128], ident_bf[:], last=(h == 3))
                fw.copy(ytb[:], V(tov, to.whole))
                fw.dma("act", V(mixT_s.t[12:16, :, rows].rearrange("h p c -> p h c"), mixT_s.part(("ret", lc)).res), ytb[:])
        fw.phase_reset()

    def phase_ssd(l, need_ctx):
        OW, OB, ODT, OA, ODD = 0, 4608, 6144, 6176, 6208
        prm = fw.carve("prm", [128, 6224], F32)
        fw.dma("sp", prm[:], ssdp_d[l, :, 0:6224])
        tri = fw.carve("tri", [128, 5, 128], F32)
        fw.dma("sp", tri[:], tri_d[:])
        ssdn = fw.carve("ssdn", [128, 8], F32)
        fw.dma("sp", ssdn[:], ssdn_d[l])
        negA = fw.carve("negA", [128, 32], F32)
        fw.act(negA[:], prm[:, OA:OA + 32], AF.Exp)
        fw.ts(negA[:], negA[:], -1.0, ALU.mult)
        one_t = fw.carve("one_t", [128, 1], F32)
        fw.memset(one_t[:], 1.0)
        U = [fw.carve(f"U{i}", [128, 1536], F32) for i in range(3)]
        dtr = fw.carve("dtr", [128, 32], F32)
        la = fw.carve("la", [128, 32], F32)
        E = fw.carve("E", [128, 96], F32)
        praw = fw.carve("praw", [128, 96], F32)
        cumraw = Buf(fw, "cumraw", _APHandle(praw.t[:, 0:32]))
        cumraw.whole = praw.whole
        tots = fw.carve("tots", [128, 32], F32)
        v = [fw.carve(f"v{d}", [128, 1024], BF16) for d in range(2)]
        vte = [fw.carve(f"vte{d}", [128, 1024], BF16) for d in range(2)]
        BCbf = fw.carve("BCbf", [128, 512], BF16)
        BCT = fw.carve("BCT", [128, 4, 128], BF16)
        zt = fw.carve("zt", [128, 1024], F32)
        R1 = fw.carve("R1", [128, 16, 128], F32)
        seg = fw.carve("seg", [128, 16, 128], F32)
        Dm = fw.carve("Dm", [128, 16, 128], BF16)
        Sm = [fw.carve(f"Sm{d}", [128, 2, 128], F32) for d in range(2)]
        Wt = fw.carve("Wt", [128, 16, 128], BF16)
        S = [fw.carve(f"S{d}", [128, 1024], F32) for d in range(2)]
        Sx = [fw.carve(f"Sx{d}", [128, 1024], F32) for d in range(2)]
        Sbf = [fw.carve(f"Sbf{d}", [128, 1024], BF16) for d in range(2)]
        Pb = fw.carve("Pb", [128, 16], F32)
        yt = fw.carve("yt", [128, 1024], F32)
        y2 = fw.carve("y2", [128, 1024], F32)
        vtmp = Buf(fw, "vtmp", _APHandle(y2.t[:].rearrange("p (h n) -> p h n", n=64)))
        vtmp.whole = y2.whole
        gss = fw.carve("gss", [128, 2], F32)
        ybf = fw.carve("ybf", [128, 1024], BF16)
        ytb = fw.carve("ytb", [128, 8, 128], BF16)
        Aex = fw.carve("Aex", [128, 16], F32)
        h16 = lambda vv: V(vv.ap.unsqueeze(2).to_broadcast([128, 16, 64]), vv.res)
        as16 = lambda b_: V(b_.t[:].rearrange("p (h n) -> p h n", n=64), b_.whole)

        def prep(lc):
            is_ctx = lc >= NCH
            src_t, r0 = (xbc_cpad, (lc - NCH) * 128) if is_ctx else (xbc_pad, lc * 128)
            rr = [xbc_pad.part("hl").res, xbc_pad.part("hr").res]
            for k in range(3):
                fw.dma("sp", U[k][:], V(src_t.t[r0 + k:r0 + k + 128, :], rr))
            fw.dma("sp", dtr[:], V(proj_s.t[lc * 128:(lc + 1) * 128, 3584:3616], proj_s.part(("dtr", lc)).res))
            fw.tt(U[0][:], U[0][:], prm[:, OW:OW + 1536], ALU.mult, eng="pool")
            fw.tt(U[1][:], U[1][:], prm[:, OW + 1536:OW + 3072], ALU.mult)
            fw.tt(U[2][:], U[2][:], prm[:, OW + 3072:OW + 4608], ALU.mult, eng="pool")
            fw.tt(U[1][:], U[1][:], U[0][:], ALU.add)
            fw.tt(U[1][:], U[1][:], U[2][:], ALU.add)
            fw.tt(U[1][:], U[1][:], prm[:, OB:OB + 1536], ALU.add)
            fw.act(U[0][:], U[1][:], AF.Silu)
            fw.tt(dtr[:], dtr[:], prm[:, ODT:ODT + 32], ALU.add)
            fw.act(dtr[:], dtr[:], AF.Exp)
            fw.act(dtr[:], dtr[:], AF.Ln, bias=one_t[:])
            fw.tt(la[:], dtr[:], negA[:], ALU.mult)
            pe = pb[0]
            for i, (w, c0, c1) in enumerate(((0, 0, 16), (1, 16, 32), (2, 0, 16), (3, 16, 32), (4, 0, 32))):
                o0 = (0, 16, 32, 48, 64)[i]
                fw.mm(pe[:, o0:o0 + (c1 - c0)], tri[:, w, :], la[:, c0:c1], start=(i == 0), stop=(i == 4), skip_group_check=True)
            fw.copy(praw[:], pe[:, 0:96])
            fw.act(E[:], praw[:], AF.Exp)
            fw.tt(tots[:], tots[:], praw[:, 64:96], ALU.add)
            xs = V(U[0].t[:, 0:1024].rearrange("p (h n) -> p h n", n=64), U[0].whole)
            for d in range(2):
                fw.tt(vtmp[:], xs, h16(dtr[:, d * 16:(d + 1) * 16]), ALU.mult)
                fw.copy(as16(v[d]), vtmp[:], eng="pool")
                fw.tt(as16(vte[d]), vtmp[:], h16(E[:, 32 + d * 16:48 + d * 16]), ALU.mult)
            fw.copy(BCbf[:], U[0][:, 1024:1536], eng="pool")

        def chunk_state(d):
            banks = (pb[4], pb[5])
            for g in range(2):
                fw.mm(banks[g][:, :], BCbf[:, g * 128:(g + 1) * 128], vte[d][:, g * 512:(g + 1) * 512], True, True)
            return banks

        def mulA(dst, srcS, acol):
            fw.tt(as16(dst), as16(srcS), h16(acol), ALU.mult)

        for grp in ((16, 17), tuple(range(NCH))):
            for d in range(2):
                fw.memset(S[d][:], 0.0)
            fw.memset(Pb[:], 1.0)
            fw.memset(tots[:], 0.0)
            for lc in grp:
                prep(lc)
                if lc < NCH or need_ctx:
                    pr_ = pc_h.part(lc).res
                    fw.dma("sp", V(pc_h.t[lc, :, 0:1024], pr_), v[0][:])
                    fw.dma("sp", V(pc_h.t[lc, :, 1024:2048], pr_), v[1][:])
                    fw.dma("sp", V(pc_h.t[lc, :, 2048:3072], pr_), vte[0][:])
                    fw.dma("sp", V(pc_h.t[lc, :, 3072:3584], pr_), BCbf[:])
                    pf_ = pc_f.part(lc).res
                    fw.dma("sp", V(pc_f.t[lc, :, 0:1024], pf_), U[0][:, 0:1024])
                    fw.dma("sp", V(pc_f.t[lc, :, 1024:1120], pf_), praw[:])
                    fw.dma("sp", V(pc_f.t[lc, :, 1120:1152], pf_), la[:])
                for d in range(2):
                    banks = chunk_state(d)
                    csv = V(pb_t[:, 4:6, :], [pb[4].whole, pb[5].whole])
                    cs_sb = V(seg.t[:, d * 8:(d + 1) * 8, :].rearrange("p a (g c) -> p (a g) c", g=2)[:, 0:2, :] if False else seg.t[:, d * 8:(d + 1) * 8, :], seg.whole)
                    cs_flat = V(seg.t[:].rearrange("p h n -> p (h n)")[:, d * 1024:(d + 1) * 1024], seg.whole)
                    fw.copy(V(cs_flat.ap.rearrange("p (g c) -> p g c", g=2), seg.whole), csv)
                    fw.dma("sp", V(cst_s.t[lc, d, :, 0:1024], cst_s.part(("s", lc, d)).res), cs_flat)
                    if d == 0:
                        mulA(S[0], S[0], E[:, 64:80])
                        fw.tt(S[0][:], S[0][:], cs_flat, ALU.add)
                    else:
                        fw.tt(as16(y2), V(cs_flat.ap.rearrange("p (h n) -> p h n", n=64), seg.whole), h16(Pb[:, :]), ALU.mult)
                        fw.tt(S[1][:], S[1][:], y2[:], ALU.add)
                        fw.tt(Pb[:], Pb[:], E[:, 80:96], ALU.mult)
                fw.dma("sp", V(cst_s.t[lc, 0, :, 1536:1568], cst_s.part(("e", lc)).res), E[:, 64:96])
            if grp[0] == 16:
                for d in range(2):
                    fw.copy(Sx[d][:], S[d][:])
        if "ssd_sf" in dbg:
            fw.dma("sp", dbg["ssd_sf"][:, :], Sx[0][:])
            fw.dma("sp", dbg["ssd_sb"][:, :], Sx[1][:])
        for d in range(2):
            fw.dma("sp", st_stage[d][:, 0:1024], S[d][:])
            fw.dma("sp", st_stage[d][:, 1536:1552], tots[:, d * 16:(d + 1) * 16])
            fw.dma("sp", V(st_src[d].t[bass.ds(rank * 128, 128), :], st_src[d].whole), st_stage[d][:, :])
            fw.collective("AllReduce", ALU.add, GROUPS, st_src[d][:, :], st_dst[d][:, :])
        for d in range(2):
            order = range(4) if d == 0 else range(3, -1, -1)
            for sidx in order:
                fw.dma("sp", yt[:], st_dst[d][sidx * 128:(sidx + 1) * 128, 0:1024])
                fw.dma("sp", Aex[:], st_dst[d][sidx * 128:(sidx + 1) * 128, 1536:1552])
                fw.act(Aex[:], Aex[:], AF.Exp)
                mulA(y2, Sx[d], Aex[:, :])
                fw.tt(y2[:], y2[:], yt[:], ALU.add)
                fw.tt(y2[:], y2[:], Sx[d][:], ALU.subtract)
                fw.stt(Sx[d][:], y2[:], cmask[:, d * 4 + sidx:d * 4 + sidx + 1], Sx[d][:], ALU.mult, ALU.add)
        for lc in range(NCH - 1, -1, -1):
            fw.copy(Sbf[1][:], Sx[1][:])
            fw.dma("sp", V(sb_s.t[lc, :, 0:1024], sb_s.part(("s", lc)).res), Sbf[1][:])
            ld = yt if lc % 2 == 0 else y2
            fw.dma("sp", ld[:], V(cst_s.t[lc, 1, :, 0:1024], cst_s.part(("s", lc, 1)).res))
            fw.dma("sp", Aex[:], V(cst_s.t[lc, 0, :, 1552:1568], cst_s.part(("e", lc)).res))
            mulA(Sx[1], Sx[1], Aex[:, :])
            fw.tt(Sx[1][:], Sx[1][:], ld[:], ALU.add)
        if need_ctx:
            fw.dma("sp", yt[:], V(cst_s.t[17, 1, :, 0:1024], cst_s.part(("s", 17, 1)).res))
            fw.copy(Sbf[1][:], yt[:])
            fw.dma("sp", V(sb_s.t[16, :, 0:1024], sb_s.part(("s", 16)).res), Sbf[1][:])
            fw.memset(Sbf[0][:], 0.0)
            fw.dma("sp", V(sb_s.t[17, :, 0:1024], sb_s.part(("s", 17)).res), Sbf[0][:])
        if stop_after == "ssd_p2":
            fw.phase_reset(); return
        groups3 = [tuple(range(NCH))] + ([(16, 17)] if need_ctx else [])
        for grp in groups3:
            if grp[0] == 16:
                fw.memset(Sx[0][:], 0.0)
            for lc in grp:
                rows = slice(lc * 128, (lc + 1) * 128)
                pr_ = pc_h.part(lc).res
                pf_ = pc_f.part(lc).res
                fw.dma("sp", v[0][:], V(pc_h.t[lc, :, 0:1024], pr_))
                fw.dma("sp", v[1][:], V(pc_h.t[lc, :, 1024:2048], pr_))
                fw.dma("sp", vte[0][:], V(pc_h.t[lc, :, 2048:3072], pr_))
                fw.dma("sp", BCbf[:], V(pc_h.t[lc, :, 3072:3584], pr_))
                fw.dma("sp", U[0][:, 0:1024], V(pc_f.t[lc, :, 0:1024], pf_))
                fw.dma("sp", praw[:], V(pc_f.t[lc, :, 1024:1120], pf_))
                fw.dma("sp", la[:], V(pc_f.t[lc, :, 1120:1152], pf_))
                fw.act(E[:], praw[:], AF.Exp)
                fw.dma("sp", zt[:, 0:512], V(proj_s.t[rows, 3616:4128], proj_s.part(("z0", lc)).res))
                fw.dma("sp", zt[:, 512:1024], V(proj_s.t[rows, 4128:4640], proj_s.part(("z1", lc)).res))
                fw.dma("sp", Sbf[1][:], V(sb_s.t[lc, :, 0:1024], sb_s.part(("s", lc)).res))
                fw.copy(Sbf[0][:], Sx[0][:], eng="pool")
                tb_ = pbh[1]
                tbv = tb_.t[:, 0:512].rearrange("p (a c) -> p a c", a=4)
                for a in range(4):
                    fw.transpose(V(tbv[:, a, :], tb_.whole), BCbf[:, a * 128:(a + 1) * 128], ident_bf[:], last=(a == 3))
                fw.copy(BCT[:], V(tbv, tb_.whole))
                sc = pb[1]
                scv = sc.t[:, 256:512].rearrange("p (g i) -> p g i", g=2)
                for g in range(2):
                    fw.mm(V(scv[:, g, :], sc.whole), BCT[:, g, :], BCT[:, 2 + g, :], start=False if False else (g == 0), stop=(g == 1), skip_group_check=True)
                for d in range(2):
                    fw.tt(Sm[d][:], V(scv, sc.whole), V(tri.t[:, d:d + 1, :].to_broadcast([128, 2, 128]), tri.whole), ALU.mult)
                yb = (pb[6], pb[7])
                for d in range(2):
                    for half, eng_ in ((0, "pool"), (1, "dve")):
                        fw.tt(R1[:, half * 8:(half + 1) * 8, :],
                              V(la.t[:, d * 16 + half * 8:d * 16 + (half + 1) * 8].unsqueeze(2).to_broadcast([128, 8, 128]), la.whole),
                              V(tri.t[:, d:d + 1, :].to_broadcast([128, 8, 128]), tri.whole), ALU.mult, eng=eng_)
                    for q in range(4):
                        bank = pb[2 + q % 2]
                        fw.mm(bank[:, :], tri[:, 4, :], V(R1.t[:, 4 * q:4 * q + 4, :].rearrange("p h n -> p (h n)"), R1.whole), True, True)
                        for hh in range(4):
                            h = 4 * q + hh
                            fw.ts(seg[:, h, :], bank[:, hh * 128:(hh + 1) * 128], cumraw[:, d * 16 + h:d * 16 + h + 1], ALU.subtract, 0.0, ALU.min)
                    fw.act(Dm[:], seg[:], AF.Exp)
                    for g in range(2):
                        fw.tt(Wt[:, g * 8:(g + 1) * 8, :], Dm[:, g * 8:(g + 1) * 8, :],
                              V(Sm[d].t[:, g:g + 1, :].to_broadcast([128, 8, 128]), Sm[d].whole), ALU.mult)
                    for h in range(16):
                        fw.mm(yb[h // 8][:, (h % 8) * 64:(h % 8 + 1) * 64], Wt[:, h, :], v[d][:, h * 64:(h + 1) * 64],
                              start=(d == 0 and h % 8 == 0), stop=(d == 1 and h % 8 == 7), last=(d == 1 and h % 8 == 7), skip_group_check=True)
                for d in range(2):
                    for g in range(2):
                        fw.mm(pb[2 + g][:, :], BCT[:, 2 + g, :], Sbf[d][:, g * 512:(g + 1) * 512], True, True)
                    ysv = V(pb_t[:, 2:4, :].rearrange("p a (h n) -> p (a h) n", n=64), [pb[2].whole, pb[3].whole])
                    fw.tt(as16(yt if d == 0 else y2), ysv, h16(E[:, d * 16:(d + 1) * 16]), ALU.mult)
                fw.tt(yt[:], yt[:], y2[:], ALU.add)
                yv = V(pb_t[:, 6:8, :].rearrange("p a c -> p (a c)") if False else pb_t[:, 6:8, :], [pb[6].whole, pb[7].whole])
                fw.tt(V(yt.t[:].rearrange("p (a c) -> p a c", a=2), yt.whole), V(yt.t[:].rearrange("p (a c) -> p a c", a=2), yt.whole), yv, ALU.add)
                xs = V(U[0].t[:, 0:1024].rearrange("p (h n) -> p h n", n=64), U[0].whole)
                fw.tt(as16(y2), xs, h16(prm[:, ODD:ODD + 16]), ALU.mult, eng="pool")
                fw.tt(yt[:], yt[:], y2[:], ALU.add)
                fw.act(zt[:], zt[:], AF.Silu)
                fw.tt(yt[:], yt[:], zt[:], ALU.mult)
                banks = chunk_state(0)
                mulA(Sx[0], Sx[0], E[:, 64:80])
                fw.tt(V(Sx[0].t[:].rearrange("p (g c) -> p g c", g=2), Sx[0].whole), V(Sx[0].t[:].rearrange("p (g c) -> p g c", g=2), Sx[0].whole),
                      V(pb_t[:, 4:6, :], [pb[4].whole, pb[5].whole]), ALU.add)
                fw.act(y2[:], yt[:], AF.Square)
                fw.reduce(gss[:], V(y2.t[:].rearrange("p (g c) -> p g c", g=2), y2.whole), ALU.add)
                fw.act(gss[:], gss[:], AF.Sqrt, bias=eps_t[:], scale=1.0 / 512)
                fw.recip(gss[:], gss[:])
                fw.tt(V(ybf.t[:].rearrange("p (g c) -> p g c", g=2), ybf.whole), V(yt.t[:].rearrange("p (g c) -> p g c", g=2), yt.whole),
                      V(gss.t[:].unsqueeze(2).to_broadcast([128, 2, 512]), gss.whole), ALU.mult)
                to = pbh[1]
                tov = to.t[:, 0:1024].rearrange("p (a c) -> p a c", a=8)
                for a in range(8):
                    fw.transpose(V(tov[:, a, :], to.whole), ybf[:, a * 128:(a + 1) * 128], ident_bf[:], last=(a == 7))
                fw.tt(ytb[:], V(tov, to.whole), V(ssdn.t[:].unsqueeze(2).to_broadcast([128, 8, 128]), ssdn.whole), ALU.mult)
                fw.dma("sp", V(mixT_s.t[4:12, :, rows].rearrange("h p c -> p h c"), mixT_s.part(("ssd", lc)).res), ytb[:])
        fw.phase_reset()

    def phase_out(l, need_ctx):
        wout = fw.carve("wout", [128, 16, D], BF16)
        wst = fw.carve("wost", [128, 4, D], F32)
        mx = [fw.carve(f"mx{i}", [128, 16, 128], BF16) for i in range(2)]
        tmp = fw.carve("otmp", [128, 512], F32)
        for q in range(4):
            fw.dma("sp", V(wst.t[:], wst.whole),
                   V(wout_d.t[l, q * 512:(q + 1) * 512, :].rearrange("(f p) c -> p f c", p=128), wout_d.whole))
            fw.copy(wout[:, q * 4:(q + 1) * 4, :], wst[:], eng="pool")
        allmix = [r for r in mixT_s.parts.values()]
        for lc in (range(NLC) if need_ctx else range(NCH)):
            is_ctx = lc >= NCH
            m = mx[lc % 2]
            fw.dma("sp", m[:], V(mixT_s.t[:, :, lc * 128:(lc + 1) * 128].rearrange("f p c -> p f c"), allmix))
            for hh in range(2):
                bank = pb[(lc % 2) * 2 + hh]
                for fc in range(16):
                    fw.mm(bank[:, :], m[:, fc, :], wout[:, fc, hh * 512:(hh + 1) * 512], start=(fc == 0), stop=(fc == 15))
                cs = slice(hh * 512, (hh + 1) * 512)
                fw.tt(tmp[:], bank[:, :], gate[:, 1 if is_ctx else 0, cs], ALU.mult)
                dst = ctx_sb[:, lc - NCH, cs] if is_ctx else x_sb[:, lc, cs]
                fw.tt(dst, dst, tmp[:], ALU.add)
        fw.phase_reset()

    for l in range(depth):
        need_ctx = l < depth - 1
        adaln(l)
        layer_params(l)
        phase_proj(l, need_ctx)
        if stop_after == "t_proj": break
        phase_attn(l, need_ctx)
        if stop_after == "t_attn": break
        phase_halo(l)
        phase_ret(l, need_ctx, "a")
        phase_ssd(l, need_ctx)
        phase_ret(l, need_ctx, "b")
        if stop_after == "t_ssd": break
        phase_out(l, need_ctx)
        if l == 0 and "x0" in dbg:
            fw.dma("sp", V(dbg["x0"].t.ap().rearrange("(c p) d -> p c d", p=128), dbg["x0"].whole), V(x_sb.t[:], x_sb.whole))
            fw.dma("sp", V(dbg["ctx0"].t.ap().rearrange("(c p) d -> p c d", p=128), dbg["ctx0"].whole), V(ctx_sb.t[:], ctx_sb.whole))

    if "qT" in dbg:
        fw.dma("sp", dbg["qT"][:], V(qT_s.t[:], [qT_s.part(c).res for c in range(NLC)]))
    if "kv0" in dbg:
        fw.dma("sp", dbg["kv0"][:], kv_dst[0][:, :])
    if "proj" in dbg:
        fw.dma("sp", dbg["proj"][:], V(proj_s.t[:], [r for r in proj_s.parts.values()]))
    if "mixT" in dbg:
        fw.dma("sp", dbg["mixT"][:], V(mixT_s.t[0:4], [r for r in mixT_s.parts.values()]))
    if "mixS" in dbg:
        fw.dma("sp", dbg["mixS"][:], V(mixT_s.t[4:12], [r for r in mixT_s.parts.values()]))
    if "mixR" in dbg:
        fw.dma("sp", dbg["mixR"][:], V(mixT_s.t[12:16], [r for r in mixT_s.parts.values()]))
    fw.dma("sp", V(out_d.t.ap().rearrange("(c p) d -> p c d", p=128), out_d.whole), V(x_sb.t[:], x_sb.whole))
    fw.wait_all("sp", [out_d[:]] + [V(b.t[:], b.whole) for b in dbg.values()])
    return nc, fw


def rope_tables():
    n_freq = 16
    inv_freq = (10000.0 ** (-np.arange(n_freq, dtype=np.float32) / n_freq)).astype(np.float32)
    pos = np.arange(8192)
    row = (pos // 64).astype(np.float32)
    col = (pos % 64).astype(np.float32)
    ang = np.concatenate([row[:, None] * inv_freq, col[:, None] * inv_freq], axis=-1).astype(np.float32)
    return np.cos(ang).astype(np.float32), np.sin(ang).astype(np.float32)


def const_tables():
    j = np.arange(128)[:, None]; i = np.arange(128)[None, :]
    tri = np.stack([(j <= i), (j >= i), (j > i), (j < i), np.ones((128, 128), bool)], axis=1).astype(np.float32)
    gf = np.array([1.0 - 2.0 ** -e for e in RET_EXP_F], np.float64)
    gb = np.array([1.0 - 2.0 ** -e for e in RET_EXP_B], np.float64)
    dif = (i - j).astype(np.float64)
    Dret = np.zeros((128, 4, 128), np.float64)
    for h in range(4):
        Dret[:, h, :] = np.where(dif > 0, gf[h] ** np.abs(dif), 0.0) + np.where(dif < 0, gb[h] ** np.abs(dif), 0.0) + np.where(dif == 0, 2.0, 0.0)
    Dret *= 0.125
    pos = np.arange(128, dtype=np.float64)[:, None]
    te_f = gf[None, :] ** (127 - pos) * 0.125
    te_b = gb[None, :] ** pos * 0.125
    qsc_f = gf[None, :] ** (pos + 1)
    qsc_b = gb[None, :] ** (128 - pos)
    a_f = np.broadcast_to(gf[None, :] ** 128, (128, 4)); a_b = np.broadcast_to(gb[None, :] ** 128, (128, 4))
    rett = np.concatenate([Dret.reshape(128, 512), te_f, te_b, qsc_f, qsc_b, a_f, a_b], axis=1).astype(np.float32)
    return np.ascontiguousarray(tri), np.ascontiguousarray(rett)


def make_inputs(inp):
    cos, sin = rope_tables()
    tri, rett = const_tables()
    ssdp = np.concatenate([inp["ssd_conv_w"].reshape(2, -1), inp["ssd_conv_b"], inp["ssd_dt_bias"].reshape(2, -1),
                           inp["ssd_a_log"].reshape(2, -1), inp["ssd_d"], inp["ret_norm"]], axis=1).astype(np.float32)
    ssdp = np.ascontiguousarray(np.broadcast_to(ssdp[:, None, :], (2, 128, ssdp.shape[1])))
    ssdn = np.ascontiguousarray(inp["ssd_norm"].reshape(2, 8, 128).transpose(0, 2, 1))
    rep = lambda a: np.ascontiguousarray(np.broadcast_to(a[:, None], (a.shape[0], 128) + a.shape[1:]))
    qkg = rep(np.stack([inp["attn_q_norm"], inp["attn_k_norm"]], axis=1))
    lamv = rep(np.stack([inp["lambda_q1"], inp["lambda_k1"], inp["lambda_q2"], inp["lambda_k2"]], axis=1))
    subln = np.ascontiguousarray(inp["attn_subln"][:, :, None])
    maps = []
    for core in range(8):
        b, t = core // 4, core % 4
        lo = t * TOK
        cc = np.stack([inp["c"][b].reshape(8, 128).T, inp["c_ctx"].reshape(8, 128).T], axis=-1)
        rp = np.stack([cos[lo:lo + TOK], sin[lo:lo + TOK]], axis=1)
        rp = rp.reshape(NCH, 128, 2, 32).transpose(1, 0, 2, 3)
        m = {
            "x": np.ascontiguousarray(inp["x"][b, lo:lo + TOK]),
            "ctx": np.ascontiguousarray(inp["ctx"][b]),
            "cc": np.ascontiguousarray(cc.astype(np.float32)),
            "w_ada": inp["w_ada"],
            "b_ada_f": np.ascontiguousarray(inp["b_ada"][:, :2 * D].reshape(2, 16, 128).transpose(0, 2, 1)),
            "b_ada": inp["b_ada"],
            "w_in": inp["w_in"], "w_out": inp["w_out"],
            "qkg": qkg, "lamv": lamv, "subln": subln,
            "rope": np.ascontiguousarray(rp),
            "tri": tri, "rett": rett, "ssdp": ssdp, "ssdn": ssdn,
            "cmask": np.ascontiguousarray(np.broadcast_to(np.array([float(s_ < t) for s_ in range(4)] + [float(s_ > t) for s_ in range(4)], np.float32)[None], (128, 8))),
        }
        maps.append(m)
    return maps


from concourse.bass_utils import run_bass_kernel_spmd


def kernel(**inputs):
    inp = {k: np.asarray(v) for k, v in inputs.items()}
    nc, _ = build(depth=2)
    maps = make_inputs(inp)
    res = run_bass_kernel_spmd(nc, maps, core_ids=list(range(8)))
    outs = [np.asarray(res.results[c]["out"]) for c in range(8)]
    return np.stack([np.concatenate(outs[0:4], 0), np.concatenate(outs[4:8], 0)]).astype(np.float32)
```

```python
import numpy as np
import concourse.bass as bass
import concourse.mybir as mybir

F32 = mybir.dt.float32
BF16 = mybir.dt.bfloat16
AF = mybir.ActivationFunctionType
ALU = mybir.AluOpType
AX = mybir.AxisListType


class Res:
    __slots__ = ("name", "w", "r")

    def __init__(self, name):
        self.name = name
        self.w = None
        self.r = {}


class V:
    __slots__ = ("ap", "res")

    def __init__(self, ap, res):
        self.ap = ap
        self.res = res if isinstance(res, (list, tuple)) else [res]


class Buf:
    def __init__(self, fw, name, t, nparts=1):
        self.fw = fw
        self.name = name
        self.t = t
        self.parts = {}
        self.whole = Res(name)

    def __getitem__(self, idx):
        return V(self.t[idx], self.whole)

    def part(self, key):
        if key not in self.parts:
            self.parts[key] = Res(f"{self.name}.{key}")
        return _PartView(self, self.parts[key])

    def ap(self):
        return self.t.ap()


class _PartView:
    def __init__(self, buf, res):
        self.buf = buf
        self.res = res

    def __getitem__(self, idx):
        return V(self.buf.t[idx], self.res)


class EngState:
    def __init__(self, name, eng, sem):
        self.name = name
        self.eng = eng
        self.sem = sem
        self.count = 0
        self.pending = False
        self.seen = {}
        self.seen_dma = {}


class FW:
    def __init__(self, nc, n_dma_sems=24, same_engine_sync=True):
        self.nc = nc
        self.same_engine_sync = same_engine_sync
        self.engs = {}
        for name, eng in (("pe", nc.tensor), ("dve", nc.vector), ("act", nc.scalar),
                          ("pool", nc.gpsimd), ("sp", nc.sync)):
            self.engs[name] = EngState(name, eng, nc.alloc_semaphore(f"s_{name}"))
        self.dma_sems = [nc.alloc_semaphore(f"s_dma{i}") for i in range(n_dma_sems)]
        self.dma_vals = [0] * n_dma_sems
        self.dma_next = 0
        self.n_inst = 0
        self.out_tokens = []
        self.cc_sem = None
        self.cc_val = 0

    def sbuf(self, name, shape, dtype):
        return Buf(self, name, self.nc.alloc_sbuf_tensor("sb_" + name, list(shape), dtype))

    def psum(self, name, shape, dtype=F32):
        return Buf(self, name, self.nc.alloc_psum_tensor("ps_" + name, list(shape), dtype))

    def dram(self, name, shape, dtype, kind="Internal", **kw):
        return Buf(self, name, self.nc.dram_tensor(name, list(shape), dtype, kind=kind, **kw))

    def _need(self, E, tok):
        if tok is None:
            return
        if tok[0] == "eng":
            _, e, c = tok
            if e == E.name:
                if not self.same_engine_sync or e == "pe":
                    return
            if E.seen.get(e, 0) >= c:
                return
            P = self.engs[e]
            assert c <= P.count, f"{E.name} waits on pending (never-incremented) {e} count {c} > {P.count}"
            E.eng.wait_ge(P.sem, c)
            E.seen[e] = c
        elif tok[0] == "cc":
            val = tok[1]
            if E.seen_dma.get("cc", 0) >= val:
                return
            E.eng.wait_ge(self.cc_sem, val)
            E.seen_dma["cc"] = val
        else:
            _, si, val = tok
            if E.seen_dma.get(si, 0) >= val:
                return
            E.eng.wait_ge(self.dma_sems[si], val)
            E.seen_dma[si] = val

    def _pre(self, E, reads, writes):
        for v in reads:
            for r in v.res:
                self._need(E, r.w)
        for v in writes:
            for r in v.res:
                self._need(E, r.w)
                for tok in r.r.values():
                    self._need(E, tok)

    def _post(self, tok, key, reads, writes):
        for v in reads:
            for r in v.res:
                r.r[key] = tok
        for v in writes:
            for r in v.res:
                r.w = tok
                r.r = {}

    def op(self, engname, fn, reads, writes, inc=True):
        E = self.engs[engname]
        self._pre(E, reads, writes)
        ins = fn(E.eng)
        self.n_inst += 1
        if inc:
            E.count += 1
            ins.then_inc(E.sem, 1)
            tok = ("eng", engname, E.count)
        else:
            tok = ("eng", engname, E.count + 1)
        self._post(tok, engname, reads, writes)
        return ins

    def dma(self, qname, out, in_, **kw):
        E = self.engs[qname]
        self._pre(E, [in_], [out])
        si = self.dma_next
        self.dma_next = (self.dma_next + 1) % len(self.dma_sems)
        if self.dma_vals[si] > 0:
            self._need(E, ("dma", si, self.dma_vals[si]))
        self.dma_vals[si] += 16
        ins = E.eng.dma_start(out=out.ap, in_=in_.ap, **kw)
        ins.then_inc(self.dma_sems[si], 16)
        self.n_inst += 1
        tok = ("dma", si, self.dma_vals[si])
        self._post(tok, f"dma{si}", [in_], [out])
        return tok

    def wait_all(self, engname, views):
        E = self.engs[engname]
        for v in views:
            for r in v.res:
                self._need(E, r.w)

    def mm(self, out, lhsT, rhs, start, stop, last=None, **kw):
        if last is None:
            last = stop
        return self.op("pe", lambda e: e.matmul(out.ap, lhsT.ap, rhs.ap, start=start, stop=stop, **kw),
                       [lhsT, rhs], [out], inc=last)

    def transpose(self, out, in_, ident, last=True):
        return self.op("pe", lambda e: e.transpose(out.ap, in_.ap, ident.ap), [in_, ident], [out], inc=last)

    def act(self, out, in_, func, bias=None, scale=1.0, accum_out=None, eng="act"):
        reads = [in_]
        kw = {}
        if bias is not None:
            if isinstance(bias, V):
                reads.append(bias)
                kw["bias"] = bias.ap
            else:
                kw["bias"] = bias
        if isinstance(scale, V):
            reads.append(scale)
            kw["scale"] = scale.ap
        else:
            kw["scale"] = scale
        writes = [out]
        if accum_out is not None:
            writes.append(accum_out)
            kw["accum_out"] = accum_out.ap
        return self.op(eng, lambda e: e.activation(out.ap, in_.ap, func, **kw), reads, writes)

    def tt(self, out, in0, in1, op, eng="dve"):
        return self.op(eng, lambda e: e.tensor_tensor(out.ap, in0.ap, in1.ap, op), [in0, in1], [out])

    def ts(self, out, in0, s1, op0, s2=None, op1=None, eng="dve", accum_out=None):
        reads = [in0]
        a1 = s1
        if isinstance(s1, V):
            reads.append(s1)
            a1 = s1.ap
        a2 = s2
        if isinstance(s2, V):
            reads.append(s2)
            a2 = s2.ap
        kw = {}
        writes = [out]
        if op1 is not None:
            kw["op1"] = op1
        if accum_out is not None:
            kw["accum_out"] = accum_out.ap
            writes.append(accum_out)
        return self.op(eng, lambda e: e.tensor_scalar(out.ap, in0.ap, a1, a2, op0, **kw), reads, writes)

    def stt(self, out, in0, scalar, in1, op0, op1, eng="dve"):
        reads = [in0, in1]
        a = scalar
        if isinstance(scalar, V):
            reads.append(scalar)
            a = scalar.ap
        return self.op(eng, lambda e: e.scalar_tensor_tensor(out.ap, in0.ap, a, in1.ap, op0, op1), reads, [out])

    def copy(self, out, in_, eng="dve"):
        if eng == "act":
            return self.op("act", lambda e: e.copy(out.ap, in_.ap), [in_], [out])
        return self.op(eng, lambda e: e.tensor_copy(out.ap, in_.ap), [in_], [out])

    def memset(self, out, val, eng="dve"):
        return self.op(eng, lambda e: e.memset(out.ap, val), [], [out])

    def reduce(self, out, in_, op, axis=AX.X, eng="dve"):
        return self.op(eng, lambda e: e.tensor_reduce(out.ap, in_.ap, axis, op), [in_], [out])

    def recip(self, out, in_):
        return self.op("dve", lambda e: e.reciprocal(out.ap, in_.ap), [in_], [out])

    def collective(self, kind, op, groups, in_, out):
        E = self.engs["pool"]
        self._pre(E, [in_], [out])
        if self.cc_sem is None:
            self.cc_sem = self.nc.alloc_semaphore("s_cc")
        self.cc_val += 1
        ins = E.eng.collective_compute(kind, op, replica_groups=groups, ins=[in_.ap], outs=[out.ap])
        ins.then_inc(self.cc_sem)
        self.n_inst += 1
        tok = ("cc", self.cc_val)
        self._post(tok, "cc", [in_], [out])
        return tok

    def make_arena(self, kbytes):
        self.arena_t = self.nc.alloc_sbuf_tensor("sb_arena", [128, kbytes * 256], F32)
        self.arena_words = kbytes * 256
        self.arena_off = 0
        self.arena_gen = 0

    def carve(self, name, shape, dtype):
        esz = 2 if dtype == BF16 else 4
        n = 1
        for s in shape[1:]:
            n *= s
        words = (n * esz + 3) // 4
        words = (words + 7) // 8 * 8
        assert self.arena_off + words <= self.arena_words, f"arena overflow for {name}: {self.arena_off}+{words}>{self.arena_words}"
        raw = self.arena_t[0:shape[0], self.arena_off:self.arena_off + words]
        self.arena_off += words
        ap = raw.bitcast(dtype) if dtype != F32 else raw
        ap = ap[:, 0:n]
        if len(shape) > 2:
            names = " ".join(f"d{i}" for i in range(1, len(shape)))
            kw = {f"d{i}": shape[i] for i in range(1, len(shape))}
            ap = ap.rearrange(f"p ({names}) -> p {names}", **kw)
        return Buf(self, f"{name}@{self.arena_gen}", _APHandle(ap))

    def barrier(self):
        for E in self.engs.values():
            for P in self.engs.values():
                if P is not E and P.count > 0:
                    self._need(E, ("eng", P.name, P.count))
            for si, val in enumerate(self.dma_vals):
                if val > 0:
                    self._need(E, ("dma", si, val))
            if self.cc_val > 0:
                self._need(E, ("cc", self.cc_val))

    def phase_reset(self):
        self.barrier()
        self.arena_off = 0
        self.arena_gen += 1


class _APHandle:
    def __init__(self, ap):
        self._ap = ap

    def __getitem__(self, idx):
        return self._ap[idx]

    def ap(self):
        return self._ap


import math
KCUT = 9

D = 1024
NCH = 16
TOK = 2048
NTOK = TOK + 256
NLC = 18
DIN = 6176
EPS = 1e-6
GROUPS = [[0, 1, 2, 3], [4, 5, 6, 7]]
SW = 1568
NP = 3 * 1536 + 1536 + 32 + 32 + 16 + 128
RET_EXP_F = (5.0, 6.0, 7.0, 8.0)
RET_EXP_B = (5.5, 6.5, 7.5, 8.5)
BLOCKS = [("aq", 0, 512), ("ak", 512, 512), ("av", 1024, 512), ("ag", 1536, 512),
          ("xbc0", 2048, 512), ("xbc1", 2560, 512), ("xbc2", 3072, 512), ("dtr", 3584, 32),
          ("z0", 3616, 512), ("z1", 4128, 512), ("rqk", 4640, 512), ("rv", 5152, 512), ("rg", 5664, 512)]


class Alt:
    def __init__(self, bufs, par):
        self.bufs, self.par = bufs, par

    @property
    def cur(self):
        return self.bufs[self.par[0] % len(self.bufs)]

    def __getitem__(self, idx):
        return self.cur[idx]

    @property
    def t(self):
        return self.cur.t

    @property
    def whole(self):
        return self.cur.whole


def lam_init_of(layer):
    return 0.8 - 0.6 * math.exp(-0.3 * layer)


def build(depth=2, debug=None, stop_after=None):
    debug = debug or {}
    nc = bass.Bass("TRN2", target_bir_lowering=False)
    fw = FW(nc, same_engine_sync=True)
    I = lambda n, s, d=F32: fw.dram(n, s, d, kind="ExternalInput")
    x_d = I("x", [TOK, D])
    ctx_d = I("ctx", [256, D])
    cc_d = I("cc", [128, 8, 2])
    wada_d = I("w_ada", [2, D, 3 * D])
    bada_f_d = I("b_ada_f", [2, 128, 16])
    bada_d = I("b_ada", [2, 3 * D])
    win_d = I("w_in", [2, D, DIN])
    wout_d = I("w_out", [2, 2 * D, D])
    qkg_d = I("qkg", [2, 128, 2, 64])
    lamv_d = I("lamv", [2, 128, 4, 64])
    subln_d = I("subln", [2, 128, 1])
    rope_d = I("rope", [128, NCH, 2, 32])
    tri_d = I("tri", [128, 5, 128])
    rett_d = I("rett", [128, 4 * 128 + 24])
    ssdp_d = I("ssdp", [2, 128, NP])
    ssdn_d = I("ssdn", [2, 128, 8])
    cmask_d = I("cmask", [128, 8])
    out_d = fw.dram("out", [TOK, D], F32, kind="ExternalOutput")
    dbg = {k: fw.dram("dbg_" + k, shape, dt_, kind="ExternalOutput") for k, (shape, dt_) in debug.items()}

    proj_s = fw.dram("proj_s", [NTOK, DIN], F32)
    qT_s = fw.dram("qT_s", [4, 128, NTOK], BF16)
    agT_s = fw.dram("agT_s", [4, 128, NTOK], BF16)
    mixT_s = fw.dram("mixT_s", [16, 128, NTOK], BF16)
    kc_s = fw.dram("kc_s", [4, 128, 256], BF16)
    vc_s = fw.dram("vc_s", [256, 512], BF16)
    xbc_pad = fw.dram("xbc_pad", [TOK + 2, 1536], F32)
    xbc_cpad = fw.dram("xbc_cpad", [258, 1536], F32)
    hx_stage = fw.dram("hx_stage", [2, 1536], F32)
    hx_src = fw.dram("hx_src", [8, 1536], F32)
    hx_dst = fw.dram("hx_dst", [8, 1536], F32)
    hxL = fw.dram("hxL", [9, 1536], F32)
    hxR = fw.dram("hxR", [8, 1536], F32)
    cst_s = fw.dram("cst_s", [NLC, 2, 128, SW], F32)
    sb_s = fw.dram("sb_s", [NLC, 128, 1536], BF16)
    pc_h = fw.dram("pc_h", [NLC, 128, 3584], BF16)
    pc_f = fw.dram("pc_f", [NLC, 128, 1152], F32)
    rc_f = fw.dram("rc_f", [NLC, 128, 512], F32)
    rc_h = fw.dram("rc_h", [NLC, 128, 1024], BF16)
    rctx_s = fw.dram("rctx_s", [2, 64, 512], F32)
    st_stage = [fw.dram(f"st_stage{d}", [128, SW], F32) for d in range(2)]
    st_src = [fw.dram(f"st_src{d}", [512, SW], F32) for d in range(2)]
    st_dst = [fw.dram(f"st_dst{d}", [512, SW], F32) for d in range(2)]
    kv_src = [fw.dram(f"kv_src{h}", [512, 4096], BF16) for h in range(4)]
    kv_dst = [fw.dram(f"kv_dst{h}", [512, 4096], BF16) for h in range(4)]
    kv_stage = [fw.dram(f"kv_stage{h}", [128, 4096], BF16) for h in range(4)]

    x_sb = fw.sbuf("x_sb", [128, NCH, D], F32)
    ctx_sb = fw.sbuf("ctx_sb", [128, 2, D], F32)
    gate = fw.sbuf("gate", [128, 2, D], F32)
    sc1 = fw.sbuf("sc1", [128, 2, 8], F32)
    sh = fw.sbuf("sh", [128, 2, 8], F32)
    ident = fw.sbuf("ident", [128, 128], F32)
    ident_bf = fw.sbuf("ident_bf", [128, 128], BF16)
    ones_bf = fw.sbuf("ones_bf", [128, 128], BF16)
    eps_t = fw.sbuf("eps_t", [128, 1], F32)
    cc = fw.sbuf("cc", [128, 8, 2], F32)
    rope = fw.sbuf("rope", [128, NCH, 2, 32], F32)
    qkg = fw.sbuf("qkg", [128, 2, 64], F32)
    lamv = fw.sbuf("lamv", [128, 4, 64], F32)
    neglam = fw.sbuf("neglam", [128, 1], F32)
    subln = fw.sbuf("subln", [128, 1], F32)
    small = fw.sbuf("small", [128, 64], F32)
    cmask = fw.sbuf("cmask", [128, 8], F32)
    fw.make_arena(119)
    pb_t = nc.alloc_psum_tensor("ps_banks", [128, 8, 512], F32)
    pb = [Buf(fw, f"pb{i}", _APHandle(pb_t[:, i, :])) for i in range(8)]
    pbh = [Buf(fw, f"pbh{i}", _APHandle(pb_t[:, i, :].bitcast(BF16))) for i in range(8)]
    for i in range(8):
        pbh[i].whole = pb[i].whole
    rank = nc.partition_id() % 4

    fw.memset(ident[:], 1.0, eng="pool")
    fw.op("pool", lambda e: e.affine_select(ident.t[:], ident.t[:], [[-1, 128]], ALU.is_equal, 0.0,
                                             base=0, channel_multiplier=1), [ident[:]], [ident[:]])
    fw.copy(ident_bf[:], ident[:])
    fw.memset(ones_bf[:], 1.0)
    fw.memset(eps_t[:], EPS)
    fw.dma("sp", V(x_sb.t[:], x_sb.whole), V(x_d.t.ap().rearrange("(c p) d -> p c d", p=128), x_d.whole))
    fw.dma("sp", V(ctx_sb.t[:], ctx_sb.whole), V(ctx_d.t.ap().rearrange("(c p) d -> p c d", p=128), ctx_d.whole))
    fw.dma("sp", cc[:], cc_d[:])
    fw.dma("sp", rope[:], rope_d[:])
    fw.dma("sp", cmask[:], cmask_d[:])
    fw.act(cc[:], cc[:], AF.Silu)
    zt = fw.carve("zt", [128, 4096], BF16)
    fw.memset(zt[:], 0.0)
    for h in range(4):
        fw.dma("sp", V(kv_src[h].t.ap().rearrange("(r p) c -> p r c", p=128), kv_src[h].whole),
               V(zt.t[:].unsqueeze(1).to_broadcast([128, 4, 4096]), zt.whole))
    zf = fw.carve("zf", [128, SW], F32)
    fw.memset(zf[:], 0.0)
    fw.dma("sp", xbc_cpad[0:1, :], zf[0:1, 0:1536])
    fw.dma("sp", xbc_cpad[257:258, :], zf[0:1, 0:1536])
    fw.dma("sp", hx_src[:, :], zf[0:8, 0:1536])
    fw.dma("sp", hxL[:, :], zf[0:9, 0:1536])
    fw.dma("sp", hxR[:, :], zf[0:8, 0:1536])
    for d in range(2):
        fw.dma("sp", V(st_src[d].t.ap().rearrange("(r p) c -> p r c", p=128), st_src[d].whole),
               V(zf.t[:].unsqueeze(1).to_broadcast([128, 4, SW]), zf.whole))
        fw.dma("sp", st_stage[d][:, :], zf[:, :])
    fw.phase_reset()

    def adaln(l):
        ccrep = fw.carve("ccrep", [128, 8, 2, 128], F32)
        badaf = fw.carve("badaf", [128, 16], F32)
        gbias = fw.carve("gbias", [128, D], F32)
        wada2 = [fw.carve(f"wada_sb{i}", [128, 8, 512], F32) for i in range(2)]
        fw.copy(ccrep[:], V(cc.t[:].unsqueeze(3).to_broadcast([128, 8, 2, 128]), cc.whole))
        fw.dma("sp", badaf[:], bada_f_d[l])
        fw.dma("sp", gbias[:], V(bada_d.t[l:l + 1, 2 * D:3 * D].partition_broadcast(128), bada_d.whole))
        ps_s = V(pb[2].t[:, 0:32].rearrange("p (a b) -> p a b", b=2), pb[2].whole)
        for piece in range(6):
            wada_sb = wada2[piece % 2]
            fw.dma("sp", V(wada_sb.t[:], wada_sb.whole),
                   V(wada_d.t[l, :, piece * 512:(piece + 1) * 512].rearrange("(k p) c -> p k c", p=128), wada_d.whole))
            if piece < 4:
                for j in range(4):
                    blk = piece * 4 + j
                    for k in range(8):
                        fw.mm(V(ps_s.ap[:, blk, :], ps_s.res), wada_sb[:, k, j * 128:(j + 1) * 128], cc[:, k, :],
                              start=(k == 0), stop=(k == 7))
            else:
                half = piece - 4
                for v in range(2):
                    for k in range(8):
                        fw.mm(pb[3][:, :], ccrep[:, k, v, :], wada_sb[:, k, :], start=(k == 0), stop=(k == 7))
                    fw.tt(gate[:, v, half * 512:(half + 1) * 512], pb[3][:, :], gbias[:, half * 512:(half + 1) * 512], ALU.add)
        for v in range(2):
            fw.tt(sh[:, v, :], V(ps_s.ap[:, 0:8, v], ps_s.res), badaf[:, 0:8], ALU.add)
            fw.tt(sc1[:, v, :], V(ps_s.ap[:, 8:16, v], ps_s.res), badaf[:, 8:16], ALU.add)
        fw.ts(sc1[:], sc1[:], 1.0, ALU.add)
        fw.phase_reset()

    def layer_params(l):
        fw.dma("sp", qkg[:], qkg_d[l])
        fw.dma("sp", lamv[:], lamv_d[l])
        fw.dma("sp", subln[:], subln_d[l])
        fw.tt(small[:, 0:64], lamv[:, 0, :], lamv[:, 1, :], ALU.mult)
        s1 = fw.sbuf(f"lam_s1_{l}", [128, 1], F32)
        s2 = fw.sbuf(f"lam_s2_{l}", [128, 1], F32)
        fw.reduce(s1[:], small[:, 0:64], ALU.add)
        fw.tt(small[:, 0:64], lamv[:, 2, :], lamv[:, 3, :], ALU.mult)
        fw.reduce(s2[:], small[:, 0:64], ALU.add)
        fw.act(s1[:], s1[:], AF.Exp)
        fw.act(s2[:], s2[:], AF.Exp)
        fw.tt(neglam[:], s2[:], s1[:], ALU.subtract)
        fw.ts(neglam[:], neglam[:], -lam_init_of(l), ALU.add)
        fw.ts(subln[:], subln[:], 1.0 - lam_init_of(l), ALU.mult)

    def phase_proj(l, need_ctx_q):
        hT = fw.carve("hT", [128, 8, NTOK], BF16)
        par = [0]
        alt = lambda n, sh, dt_: Alt([fw.carve(f"{n}_{i}", sh, dt_) for i in range(2)], par)
        xn = alt("xn", [128, D], F32)
        junk = alt("junk", [128, D], F32)
        ss = alt("ss", [128, 1], F32)
        rs = alt("rs", [128, 1], F32)
        wblk = [fw.carve(f"wblk{i}", [128, 8, 512], BF16) for i in range(2)]
        wst = fw.carve("wst", [128, 8, 512], F32)
        stage = [fw.carve(f"stage{i}", [128, 512], F32) for i in range(3)]
        sq = alt("sq", [128, 8, 64], F32)
        qn = alt("qn", [128, 8, 64], F32)
        t1 = alt("t1", [128, 8, 32], F32)
        t2 = alt("t2", [128, 8, 32], F32)
        ss8 = alt("ss8", [128, 8], F32)
        qbf = [fw.carve(f"qbf{i}", [128, 8, 64], BF16) for i in range(2)]
        tb = [fw.carve(f"tb{i}", [128, 4, 128], BF16) for i in range(2)]
        vbf = [fw.carve(f"vbf{i}", [128, 512], BF16) for i in range(2)]

        for lc in range(NLC):
            par[0] = lc
            src = x_sb[:, lc, :] if lc < NCH else ctx_sb[:, lc - NCH, :]
            v = 0 if lc < NCH else 1
            fw.act(junk[:], src, AF.Square, accum_out=ss[:])
            fw.act(rs[:], ss[:], AF.Sqrt, bias=eps_t[:], scale=1.0 / D)
            fw.recip(rs[:], rs[:])
            fw.ts(xn[:], src, rs[:], ALU.mult)
            pt = V(pb[lc % 2 * 2].t[:, :], [pb[lc % 2 * 2].whole, pb[lc % 2 * 2 + 1].whole])
            ptt = pb_t[:, lc % 2 * 2:lc % 2 * 2 + 2, :].rearrange("p a (k c) -> p (a k) c", c=128)
            for k in range(8):
                fw.transpose(V(ptt[:, k, :], pt.res), xn[:, k * 128:(k + 1) * 128], ident[:], last=(k == 7))
            for k in range(8):
                fw.ts(hT[:, k, lc * 128:(lc + 1) * 128], V(ptt[:, k, :], pt.res), sc1[:, v, k:k + 1], ALU.mult,
                      sh[:, v, k:k + 1], ALU.add)
        if "hT" in dbg:
            fw.dma("sp", V(dbg["hT"].t[:], dbg["hT"].whole), V(hT.t[:], hT.whole))

        it = 0
        for bi, (bname, col0, ncols) in enumerate(BLOCKS):
            wb = wblk[bi % 2]
            fw.dma("sp", V(wst.t[:, :, 0:ncols], wst.whole),
                   V(win_d.t[l, :, col0:col0 + ncols].rearrange("(k p) c -> p k c", p=128), win_d.whole))
            fw.copy(V(wb.t[:, :, 0:ncols], wb.whole), V(wst.t[:, :, 0:ncols], wst.whole), eng="pool")
            for lc in range(NLC):
                is_ctx = lc >= NCH
                if is_ctx and not need_ctx_q and bname in ("ag", "z0", "z1", "rg"):
                    continue
                bank = pb[4 + it % 2]
                it += 1
                par[0] = it
                for k in range(8):
                    fw.mm(bank[:, 0:ncols], hT[:, k, lc * 128:(lc + 1) * 128], wb[:, k, 0:ncols],
                          start=(k == 0), stop=(k == 7))
                rows = slice(lc * 128, (lc + 1) * 128)
                if bname in ("aq", "ak"):
                    if bname == "aq" and is_ctx and not need_ctx_q:
                        continue
                    gi = 0 if bname == "aq" else 1
                    psv = V(bank.t[:, :].rearrange("p (a b) -> p a b", b=64), bank.whole)
                    fw.act(sq[:], psv, AF.Square)
                    fw.reduce(ss8[:], sq[:], ALU.add)
                    fw.act(ss8[:], ss8[:], AF.Sqrt, bias=eps_t[:], scale=1.0 / 64)
                    fw.recip(ss8[:], ss8[:])
                    fw.tt(qn[:], psv, V(ss8.t[:].unsqueeze(2).to_broadcast([128, 8, 64]), ss8.whole), ALU.mult)
                    fw.tt(qn[:], qn[:], V(qkg.t[:, gi:gi + 1, :].to_broadcast([128, 8, 64]), qkg.whole), ALU.mult, eng="pool")
                    qo = qbf[it % 2]
                    if not is_ctx:
                        cosb = V(rope.t[:, lc, 0:1, :].to_broadcast([128, 8, 32]), rope.whole)
                        sinb = V(rope.t[:, lc, 1:2, :].to_broadcast([128, 8, 32]), rope.whole)
                        fw.tt(t1[:], qn[:, :, 0:32], cosb, ALU.mult)
                        fw.tt(t2[:], qn[:, :, 32:64], sinb, ALU.mult, eng="pool")
                        fw.tt(qo[:, :, 0:32], t1[:], t2[:], ALU.subtract)
                        fw.tt(t1[:], qn[:, :, 0:32], sinb, ALU.mult)
                        fw.tt(t2[:], qn[:, :, 32:64], cosb, ALU.mult, eng="pool")
                        fw.tt(qo[:, :, 32:64], t1[:], t2[:], ALU.add)
                    else:
                        fw.copy(qo[:], qn[:])
                    tbank = pbh[6 + lc % 2]
                    tbv = tbank.t[:, 0:512].rearrange("p (h c) -> p h c", c=128)
                    qof = qo.t[:].rearrange("p a b -> p (a b)")
                    for hd in range(4):
                        fw.transpose(V(tbv[:, hd, :], tbank.whole), V(qof[:, hd * 128:(hd + 1) * 128], qo.whole), ident_bf[:], last=(hd == 3))
                    tbs = tb[lc % 2]
                    fw.copy(tbs[:], V(tbv, tbank.whole), eng="act")
                    if bname == "aq":
                        fw.dma("act", V(qT_s.t[:, :, rows].rearrange("h p c -> p h c"), qT_s.part(lc).res), tbs[:])
                    elif is_ctx:
                        c0 = (lc - NCH) * 128
                        fw.dma("act", V(kc_s.t[:, :, c0:c0 + 128].rearrange("h p c -> p h c"), kc_s.part(lc).res), tbs[:])
                    else:
                        for hd in range(4):
                            fw.dma("act", V(kv_stage[hd].t[:, lc * 128:(lc + 1) * 128], kv_stage[hd].part(("k", lc)).res),
                                   tbs[:, hd, :])
                elif bname == "av":
                    vb = vbf[lc % 2]
                    fw.copy(vb[:], bank[:, :], eng="act")
                    if is_ctx:
                        c0 = (lc - NCH) * 128
                        fw.dma("act", V(vc_s.t[c0:c0 + 128, :], vc_s.part(lc).res), vb[:])
                    else:
                        for hd in range(4):
                            fw.dma("act", V(kv_stage[hd].t[:, 2048 + lc * 128:2048 + (lc + 1) * 128],
                                            kv_stage[hd].part(("v", lc)).res), vb[:, hd * 128:(hd + 1) * 128])
                elif bname == "ag":
                    st = stage[lc % 3]
                    fw.act(st[:], bank[:, :], AF.Silu)
                    tbank = pb[6 + lc % 2]
                    for hd in range(4):
                        fw.transpose(tbank[:, hd * 128:(hd + 1) * 128], st[:, hd * 128:(hd + 1) * 128], ident[:], last=(hd == 3))
                    tbs = tb[lc % 2]
                    fw.copy(V(tbs.t[:].rearrange("p h c -> p (h c)"), tbs.whole), tbank[:, :])
                    fw.dma("act", V(agT_s.t[:, :, rows].rearrange("h p c -> p h c"), agT_s.part(lc).res), tbs[:])
                else:
                    st = stage[lc % 3]
                    if lc % 2 == 0:
                        fw.copy(st[:, 0:ncols], bank[:, 0:ncols])
                    else:
                        fw.copy(st[:, 0:ncols], bank[:, 0:ncols], eng="act")
                    if bname.startswith("xbc"):
                        xc = (int(bname[3]) * 512)
                        if is_ctx:
                            r1 = 1 + (lc - NCH) * 128
                            fw.dma("sp", V(xbc_cpad.t[r1:r1 + 128, xc:xc + 512], xbc_cpad.part((bname, lc)).res), st[:, 0:ncols])
                        else:
                            r1 = 1 + lc * 128
                            fw.dma("sp", V(xbc_pad.t[r1:r1 + 128, xc:xc + 512], xbc_pad.part((bname, lc)).res), st[:, 0:ncols])
                    else:
                        fw.dma("sp", V(proj_s.t[rows, col0:col0 + ncols], proj_s.part((bname, lc)).res), st[:, 0:ncols])
            if bname == "xbc2":
                halo_issue(l)
            if bname == "av":
                for hd in range(4):
                    allres = [kv_stage[hd].part(("k", c)).res for c in range(NCH)] + [kv_stage[hd].part(("v", c)).res for c in range(NCH)]
                    fw.dma("sp", V(kv_src[hd].t[bass.ds(rank * 128, 128), :], kv_src[hd].whole), V(kv_stage[hd].t[:, :], allres))
                    fw.collective("AllReduce", ALU.add, GROUPS, kv_src[hd][:, :], kv_dst[hd][:, :])
        fw.phase_reset()

    def phase_attn(l, with_ctx_q):
        kTb = [fw.carve(f"kT{i}", [128, 8448], BF16) for i in range(2)]
        vvb = [fw.carve(f"vv{i}", [128, 66, 128], BF16) for i in range(2)]
        qhb = [fw.carve(f"qh{i}", [128, NTOK], BF16) for i in range(2)]
        aghb = [fw.carve(f"agh{i}", [128, NTOK], BF16) for i in range(2)]
        rec = fw.carve("rec", [128, 512], F32)
        om = [fw.carve(f"om{i}", [128, 512], F32) for i in range(2)]
        A = fw.carve("A", [128, 512], F32)
        sqb = fw.carve("sqb", [128, 512], BF16)
        rstd = fw.carve("rstd", [128, 512], F32)
        mixo = [fw.carve(f"mixo{i}", [128, 512], BF16) for i in range(2)]
        ones_f = fw.carve("ones_f", [128, 128], F32)
        fw.memset(ones_f[:], 1.0)
        qblocks = [(q0, 512, 0, 66) for q0 in range(0, TOK, 512)]
        if with_ctx_q:
            qblocks.append((TOK, 256, 64, 66))
        sbank = (pb[0], pb[1], pb[7])
        nq_all = NTOK if with_ctx_q else TOK

        def load_head(hd):
            kT, vv, qh, agh = kTb[hd % 2], vvb[hd % 2], qhb[hd % 2], aghb[hd % 2]
            fw.dma("sp", V(kT.t[:, 0:8192].rearrange("p (r c) -> p r c", r=4), kT.whole),
                   V(kv_dst[hd].t[:, 0:2048].rearrange("(r p) c -> p r c", p=128), kv_dst[hd].whole))
            fw.dma("sp", kT[:, 8192:8448], V(kc_s.t[hd], [kc_s.part(16).res, kc_s.part(17).res]))
            fw.dma("sp", V(vv.t[:, 0:64, :].rearrange("p (r c) e -> p r c e", r=4), vv.whole),
                   V(kv_dst[hd].t[:, 2048:4096].rearrange("(r p) (c e) -> p r c e", p=128, e=128), kv_dst[hd].whole))
            fw.dma("sp", vv[:, 64:66, :], V(vc_s.t[:, hd * 128:(hd + 1) * 128].rearrange("(c p) e -> p c e", p=128),
                                           [vc_s.part(16).res, vc_s.part(17).res]))
            qres = [qT_s.part(c).res for c in range(NLC if with_ctx_q else NCH)]
            fw.dma("sp", qh[:, 0:nq_all], V(qT_s.t[hd, :, 0:nq_all], qres))
            fw.dma("sp", agh[:, 0:NTOK], V(agT_s.t[hd], [agT_s.part(c).res for c in range(NLC)]))

        load_head(0)
        pT2 = [fw.carve(f"pTT{i}", [128, 2, 512], BF16) for i in range(3)]
        acc2 = [fw.carve(f"accT{i}", [128, 2, 512], F32) for i in range(2)]
        stage_banks = ((0, 1), (4, 5))
        for hd in range(4):
            if hd + 1 < 4:
                load_head(hd + 1)
            kT, vv, qh, agh = kTb[hd % 2], vvb[hd % 2], qhb[hd % 2], aghb[hd % 2]
            for qi, (q0, nq_, kc0, kc1) in enumerate(qblocks):
                kcs = list(range(kc0, kc1))
                n = len(kcs)

                def qk(i):
                    kc = kcs[i]
                    b0, b1 = stage_banks[i % 2]
                    for m, bk in ((0, b0), (1, b1)):
                        fw.mm(pb[bk][:, 0:nq_], kT[m * 64:(m + 1) * 64, kc * 128:(kc + 1) * 128], qh[m * 64:(m + 1) * 64, q0:q0 + nq_], True, True)

                qk(0)
                for i, kc in enumerate(kcs):
                    if i + 1 < n:
                        qk(i + 1)
                    b0, b1 = stage_banks[i % 2]
                    p = pT2[i % 3]
                    sc2 = V(pb_t[:, b0:b0 + 2, 0:nq_], [pb[b0].whole, pb[b1].whole])
                    fw.act(p[:, :, 0:nq_], sc2, AF.Exp, scale=0.125)
                    first, lastk = (i == 0), (i == n - 1)
                    for m in range(2):
                        fw.mm(pb[2 + m][:, 0:nq_], vv[:, kc, :], p[:, m, 0:nq_], start=first, stop=lastk)
                    eng_ = "dve" if i % 2 == 0 else "pool"
                    acc_ = acc2[i % 2]
                    if i < 2:
                        fw.copy(acc_[:, :, 0:nq_], p[:, :, 0:nq_], eng=eng_)
                    else:
                        fw.tt(acc_[:, :, 0:nq_], acc_[:, :, 0:nq_], p[:, :, 0:nq_], ALU.add, eng=eng_)
                if n > 1:
                    fw.tt(acc2[0][:, :, 0:nq_], acc2[0][:, :, 0:nq_], acc2[1][:, :, 0:nq_], ALU.add)
                for m in range(2):
                    fw.mm(pb[7][:, 0:nq_], ones_f[:], acc2[0][:, m, 0:nq_], True, True)
                    fw.recip(rec[:, 0:nq_], pb[7][:, 0:nq_])
                    fw.tt(om[m][:, 0:nq_], pb[2 + m][:, 0:nq_], rec[:, 0:nq_], ALU.mult)
                fw.stt(A[:, 0:nq_], om[1][:, 0:nq_], neglam[:], om[0][:, 0:nq_], ALU.mult, ALU.add)
                fw.act(sqb[:, 0:nq_], A[:, 0:nq_], AF.Square)
                fw.mm(pb[6][:, 0:nq_], ones_bf[:], sqb[:, 0:nq_], True, True)
                fw.act(rstd[:, 0:nq_], pb[6][:, 0:nq_], AF.Sqrt, bias=eps_t[:], scale=1.0 / 128)
                fw.recip(rstd[:, 0:nq_], rstd[:, 0:nq_])
                fw.stt(A[:, 0:nq_], A[:, 0:nq_], subln[:], rstd[:, 0:nq_], ALU.mult, ALU.mult)
                mo = mixo[qi % 2]
                fw.tt(mo[:, 0:nq_], A[:, 0:nq_], agh[:, q0:q0 + nq_], ALU.mult, eng="pool")
                fw.dma("act", V(mixT_s.t[hd, :, q0:q0 + nq_], mixT_s.part((hd, qi)).res), mo[:, 0:nq_])
        fw.phase_reset()

    def halo_issue(l):
        xr = lambda c: [xbc_pad.part((f"xbc{i}", c)).res for i in range(3)]
        fw.dma("sp", hx_stage[0:1, :], V(xbc_pad.t[1:2, :], xr(0)))
        fw.dma("sp", hx_stage[1:2, :], V(xbc_pad.t[TOK:TOK + 1, :], xr(NCH - 1)))
        fw.dma("sp", V(hx_src.t[bass.ds(rank * 2, 2), :], hx_src.whole), hx_stage[:, :])
        fw.collective("AllReduce", ALU.add, GROUPS, hx_src[:, :], hx_dst[:, :])

    def phase_halo(l):
        fw.dma("sp", hxL[1:9, :], hx_dst[:, :])
        fw.dma("sp", hxR[0:6, :], hx_dst[2:8, :])
        fw.dma("sp", V(xbc_pad.t[0:1, :], xbc_pad.part("hl").res), V(hxL.t[bass.ds(rank * 2, 1), :], hxL.whole))
        fw.dma("sp", V(xbc_pad.t[TOK + 1:TOK + 2, :], xbc_pad.part("hr").res), V(hxR.t[bass.ds(rank * 2, 1), :], hxR.whole))

    def rope_apply(out, x, lc, nh, t1, t2):
        cosb = V(rope.t[:, lc, 0:1, :].to_broadcast([128, nh, 32]), rope.whole)
        sinb = V(rope.t[:, lc, 1:2, :].to_broadcast([128, nh, 32]), rope.whole)
        fw.tt(t1[:, 0:nh, :], x[:, :, 0:32], cosb, ALU.mult)
        fw.tt(t2[:, 0:nh, :], x[:, :, 32:64], sinb, ALU.mult, eng="pool")
        fw.tt(out[:, :, 0:32], t1[:, 0:nh, :], t2[:, 0:nh, :], ALU.subtract)
        fw.tt(t1[:, 0:nh, :], x[:, :, 0:32], sinb, ALU.mult)
        fw.tt(t2[:, 0:nh, :], x[:, :, 32:64], cosb, ALU.mult, eng="pool")
        fw.tt(out[:, :, 32:64], t1[:, 0:nh, :], t2[:, 0:nh, :], ALU.add)

    def chain_combine(Sin, Sctx, d, col0, ncol, nparts, Aexp_of, tmp, slot):
        fw.copy(Sin[0:nparts, :], Sctx[0:nparts, :])
        order = range(4) if d == 0 else range(3, -1, -1)
        for sidx in order:
            fw.dma("sp", slot[0:nparts, 0:ncol], st_dst[d][sidx * 128:sidx * 128 + nparts, col0:col0 + ncol])
            Aexp_of(sidx, tmp)
            fw.tt(tmp[0:nparts, :], tmp[0:nparts, :], slot[0:nparts, 0:ncol], ALU.add)
            fw.tt(tmp[0:nparts, :], tmp[0:nparts, :], Sin[0:nparts, :], ALU.subtract)
            mcol = cmask[:, d * 4 + sidx:d * 4 + sidx + 1]
            fw.stt(Sin[0:nparts, :], tmp[0:nparts, :], V(mcol.ap[0:nparts], mcol.res), Sin[0:nparts, :], ALU.mult, ALU.add)

    def phase_ret(l, need_ctx, part):
        rett = fw.carve("rett", [128, 4 * 128 + 24], F32)
        rnorm = fw.carve("rnorm", [128, 128], F32)
        fw.dma("sp", rett[:], rett_d[:])
        fw.dma("sp", rnorm[:], ssdp_d[l, :, NP - 128:NP])
        Dret = V(rett.t[:, 0:512].rearrange("p (h i) -> p h i", i=128), rett.whole)
        tab = lambda k: V(rett.t[:, 512 + 4 * k:512 + 4 * k + 4], rett.whole)
        par = [0]
        alt = lambda n, sh, dt_: Alt([fw.carve(f"{n}_{i}", sh, dt_) for i in range(2)], par)
        qk = alt("qk", [128, 8, 64], F32)
        qkr = alt("qkr", [128, 8, 64], F32)
        t1 = alt("rt1", [128, 8, 32], F32)
        t2 = alt("rt2", [128, 8, 32], F32)
        rv = alt("rv", [128, 512], F32)
        rvbf = alt("rvbf", [128, 512], BF16)
        kte = [alt(f"kte{d}", [128, 4, 64], BF16) for d in range(2)]
        q3 = alt("q3", [128, 3, 4, 64], BF16)
        kbf = alt("kbf", [128, 4, 64], BF16)
        qT = alt("qT", [64, 3, 4, 128], BF16)
        kT = alt("kT", [64, 4, 128], BF16)
        Wt = alt("Wt", [128, 4, 128], BF16)
        R = [fw.carve(f"R{d}", [64, 512], F32) for d in range(2)]
        Rbf = [fw.carve(f"Rbf{d}", [64, 512], BF16) for d in range(2)]
        Rctx = [fw.carve(f"Rctx{d}", [64, 512], F32) for d in range(2)]
        Pb = fw.carve("Pb", [128, 4], F32)
        cs_sb = [alt(f"cs_sb{d}", [64, 512], F32) for d in range(2)]
        tmp = fw.carve("rtmp", [64, 512], F32)
        slot = fw.carve("rslot", [64, 512], F32)
        rg = alt("rg", [128, 512], F32)
        ysq = alt("ysq", [128, 4, 128], F32)
        yss = alt("yss", [128, 4], F32)
        yn = alt("yn", [128, 4, 128], F32)
        ybf = alt("ybf", [128, 512], BF16)
        ytb = alt("ytb", [128, 4, 128], BF16)
        bc4 = lambda v, n: V(v.ap.unsqueeze(2).to_broadcast([v.ap.shape[0], 4, n]), v.res)

        def prep(lc):
            par[0] = lc
            is_ctx = lc >= NCH
            rows = slice(lc * 128, (lc + 1) * 128)
            pr = lambda n: proj_s.part((n, lc)).res
            fw.dma("sp", V(qk.t[:].rearrange("p a b -> p (a b)"), qk.whole), V(proj_s.t[rows, 4640:5152], pr("rqk")))
            fw.dma("sp", rv[:], V(proj_s.t[rows, 5152:5664], pr("rv")))
            if is_ctx:
                src = qk
            else:
                rope_apply(qkr, qk, lc, 8, t1, t2)
                src = qkr
            fw.copy(rvbf[:], rv[:], eng="pool")
            for d in range(2):
                fw.tt(kte[d][:], src[:, 4:8, :], bc4(tab(d), 64), ALU.mult)
            return src

        def chunk_states(start_banks=(0, 1)):
            for d in range(2):
                bank = pb[start_banks[d]]
                for h in range(4):
                    fw.mm(bank[0:64, h * 128:(h + 1) * 128], kte[d][:, h, :], rvbf[:, h * 128:(h + 1) * 128],
                          start=(h == 0), stop=(h == 3), skip_group_check=True)
            return [pb[start_banks[0]], pb[start_banks[1]]]

        A128 = [tab(4), tab(5)]

        def fold(acc, csb, lc, store):
            fw.tt(V(acc[0].t[:].rearrange("p (h n) -> p h n", n=128), acc[0].whole),
                  V(acc[0].t[:].rearrange("p (h n) -> p h n", n=128), acc[0].whole),
                  V(A128[0].ap[0:64].unsqueeze(2).to_broadcast([64, 4, 128]), A128[0].res), ALU.mult)
            fw.tt(acc[0][:], acc[0][:], csb[0][0:64, :], ALU.add)
            fw.tt(V(tmp.t[:].rearrange("p (h n) -> p h n", n=128), tmp.whole),
                  V(csb[1].t[0:64, :].rearrange("p (h n) -> p h n", n=128), csb[1].whole),
                  V(Pb.t[0:64, :].unsqueeze(2).to_broadcast([64, 4, 128]), Pb.whole), ALU.mult)
            fw.tt(acc[1][:], acc[1][:], tmp[:], ALU.add)
            fw.tt(Pb[:], Pb[:], A128[1], ALU.mult)
            if store:
                for d in range(2):
                    fw.copy(cs_sb[d][:], csb[d][0:64, :])
                    fw.dma("act", V(cst_s.t[lc, d, 0:64, 1024:1536], cst_s.part(("r", lc, d)).res), cs_sb[d][:])

        for grp in (((16, 17), tuple(range(NCH))) if part == "a" else ()):
            for d in range(2):
                fw.memset(R[d][:], 0.0)
            fw.memset(Pb[:], 1.0)
            for lc in grp:
                src_ = prep(lc)
                if lc < NCH or need_ctx:
                    fw.dma("sp", V(rc_f.t[lc], rc_f.part(lc).res), V(src_.t[:].rearrange("p a b -> p (a b)"), src_.whole))
                    fw.dma("sp", V(rc_h.t[lc, :, 0:512], rc_h.part(lc).res), rvbf[:])
                    for d in range(2):
                        fw.dma("sp", V(rc_h.t[lc, :, 512 + d * 256:768 + d * 256], rc_h.part(lc).res),
                               V(kte[d].t[:].rearrange("p a b -> p (a b)"), kte[d].whole))
                if KCUT >= 2:
                    csb = chunk_states()
                if KCUT >= 3:
                    fold(R, csb, lc, KCUT >= 4)
            if grp[0] == 16:
                for d in range(2):
                    fw.copy(Rctx[d][:], R[d][:])
        if "ret_sf" in dbg:
            fw.dma("sp", dbg["ret_sf"][:, :], Rctx[0][:])
            fw.dma("sp", dbg["ret_sb"][:, :], Rctx[1][:])
        if stop_after == "ret_p1":
            fw.phase_reset(); return
        if part == "a":
            for d in range(2):
                fw.dma("sp", st_stage[d][0:64, 1024:1536], R[d][:])
                fw.dma("sp", V(rctx_s.t[d], rctx_s.whole), Rctx[d][:])
            fw.phase_reset()
            return
        for d in range(2):
            fw.dma("sp", Rctx[d][:], V(rctx_s.t[d], rctx_s.whole))
        A2048 = [[(1.0 - 2.0 ** -e) ** 2048 for e in RET_EXP_F], [(1.0 - 2.0 ** -e) ** 2048 for e in RET_EXP_B]]
        Rin = [fw.carve(f"Rin{d}", [64, 512], F32) for d in range(2)]
        for d in range(2):
            def aexp(sidx, t_, d=d):
                for h in range(4):
                    fw.ts(t_[0:64, h * 128:(h + 1) * 128], Rin[d][0:64, h * 128:(h + 1) * 128], float(A2048[d][h]), ALU.mult)
            chain_combine(Rin[d], Rctx[d], d, 1024, 512, 64, aexp, tmp, slot)
        snap = fw.carve("rsnap", [64, 512], BF16)
        csl = fw.carve("rcsl", [64, 512], F32)
        fw.copy(R[1][:], Rin[1][:])
        for lc in range(NCH - 1, -1, -1):
            fw.copy(snap[:], R[1][:])
            fw.dma("act", V(sb_s.t[lc, 0:64, 1024:1536], sb_s.part(("r", lc)).res), snap[:])
            fw.dma("sp", csl[:], V(cst_s.t[lc, 1, 0:64, 1024:1536], cst_s.part(("r", lc, 1)).res))
            fw.tt(V(R[1].t[:].rearrange("p (h n) -> p h n", n=128), R[1].whole),
                  V(R[1].t[:].rearrange("p (h n) -> p h n", n=128), R[1].whole),
                  V(A128[1].ap[0:64].unsqueeze(2).to_broadcast([64, 4, 128]), A128[1].res), ALU.mult)
            fw.tt(R[1][:], R[1][:], csl[:], ALU.add)
        if need_ctx:
            fw.dma("sp", csl[:], V(cst_s.t[17, 1, 0:64, 1024:1536], cst_s.part(("r", 17, 1)).res))
            fw.copy(snap[:], csl[:])
            fw.dma("sp", V(sb_s.t[16, 0:64, 1024:1536], sb_s.part(("r", 16)).res), snap[:])
            snap0 = fw.carve("rsnap0", [64, 512], BF16)
            fw.memset(snap0[:], 0.0)
            fw.dma("sp", V(sb_s.t[17, 0:64, 1024:1536], sb_s.part(("r", 17)).res), snap0[:])
        if stop_after == "ret_p2":
            fw.phase_reset(); return
        groups3 = [tuple(range(NCH))] + ([(16, 17)] if need_ctx else [])
        for grp in groups3:
            if grp[0] == 16:
                fw.memset(R[0][:], 0.0)
            else:
                fw.copy(R[0][:], Rin[0][:])
            for lc in grp:
                is_ctx = lc >= NCH
                rows = slice(lc * 128, (lc + 1) * 128)
                par[0] = lc
                src = qkr
                fw.dma("sp", V(qkr.t[:].rearrange("p a b -> p (a b)"), qkr.whole), V(rc_f.t[lc], rc_f.part(lc).res))
                fw.dma("sp", rvbf[:], V(rc_h.t[lc, :, 0:512], rc_h.part(lc).res))
                for d in range(2):
                    fw.dma("sp", V(kte[d].t[:].rearrange("p a b -> p (a b)"), kte[d].whole),
                           V(rc_h.t[lc, :, 512 + d * 256:768 + d * 256], rc_h.part(lc).res))
                fw.dma("sp", rg[:], V(proj_s.t[rows, 5664:6176], proj_s.part(("rg", lc)).res))
                fw.dma("sp", Rbf[1][:], V(sb_s.t[lc, 0:64, 1024:1536], sb_s.part(("r", lc)).res))
                fw.copy(Rbf[0][:], R[0][:])
                fw.copy(q3[:, 0, :, :], src[:, 0:4, :])
                fw.tt(q3[:, 1, :, :], src[:, 0:4, :], bc4(tab(2), 64), ALU.mult)
                fw.tt(q3[:, 2, :, :], src[:, 0:4, :], bc4(tab(3), 64), ALU.mult, eng="pool")
                fw.copy(kbf[:], src[:, 4:8, :])
                tqa = pbh[2]
                tqav = tqa.t[0:64, 0:1024].rearrange("p (k h c) -> p k h c", k=2, h=4)
                tqb = pbh[3]
                tqbv = tqb.t[0:64, 0:512].rearrange("p (h c) -> p h c", h=4)
                for k3 in range(2):
                    for h in range(4):
                        fw.transpose(V(tqav[:, k3, h, :], tqa.whole), q3[:, k3, h, :], ident_bf[:], last=(k3 == 1 and h == 3))
                for h in range(4):
                    fw.transpose(V(tqbv[:, h, :], tqb.whole), q3[:, 2, h, :], ident_bf[:], last=(h == 3))
                fw.copy(qT[:, 0:2, :, :], V(tqav, tqa.whole))
                fw.copy(qT[:, 2, :, :], V(tqbv, tqb.whole))
                tk = pbh[4]
                tkv = tk.t[0:64, 0:512].rearrange("p (h c) -> p h c", h=4)
                for h in range(4):
                    fw.transpose(V(tkv[:, h, :], tk.whole), kbf[:, h, :], ident_bf[:], last=(h == 3))
                fw.copy(kT[:], V(tkv, tk.whole))
                sc = pb[5]
                for h in range(4):
                    fw.mm(sc[:, h * 128:(h + 1) * 128], kT[:, h, :], qT[:, 0, h, :], start=(h == 0), stop=(h == 3), skip_group_check=True)
                fw.tt(Wt[:], V(sc.t[:, :].rearrange("p (h i) -> p h i", i=128), sc.whole), Dret, ALU.mult)
                csb = chunk_states((0, 1))
                yb = pb[6]
                for h in range(4):
                    o = yb[:, h * 128:(h + 1) * 128]
                    fw.mm(o, Wt[:, h, :], rvbf[:, h * 128:(h + 1) * 128], start=(h == 0), stop=False, last=False, skip_group_check=True)
                    fw.mm(o, qT[:, 1, h, :], Rbf[0][:, h * 128:(h + 1) * 128], start=False, stop=False, last=False, skip_group_check=True)
                    fw.mm(o, qT[:, 2, h, :], Rbf[1][:, h * 128:(h + 1) * 128], start=False, stop=(h == 3), last=(h == 3), skip_group_check=True)
                fw.tt(V(R[0].t[:].rearrange("p (h n) -> p h n", n=128), R[0].whole),
                      V(R[0].t[:].rearrange("p (h n) -> p h n", n=128), R[0].whole),
                      V(A128[0].ap[0:64].unsqueeze(2).to_broadcast([64, 4, 128]), A128[0].res), ALU.mult)
                fw.tt(R[0][:], R[0][:], csb[0][0:64, :], ALU.add)
                ybv = V(yb.t[:, :].rearrange("p (h n) -> p h n", n=128), yb.whole)
                fw.act(ysq[:], ybv, AF.Square)
                fw.reduce(yss[:], ysq[:], ALU.add)
                fw.act(yss[:], yss[:], AF.Sqrt, bias=eps_t[:], scale=1.0 / 128)
                fw.recip(yss[:], yss[:])
                fw.tt(yn[:], ybv, V(yss.t[:].unsqueeze(2).to_broadcast([128, 4, 128]), yss.whole), ALU.mult)
                fw.tt(yn[:], yn[:], V(rnorm.t[:].unsqueeze(1).to_broadcast([128, 4, 128]), rnorm.whole), ALU.mult, eng="pool")
                fw.act(rg[:], rg[:], AF.Silu)
                fw.tt(ybf[:], V(yn.t[:].rearrange("p h n -> p (h n)"), yn.whole), rg[:], ALU.mult)
                to = pbh[7]
                tov = to.t[:, 0:512].rearrange("p (h c) -> p h c", h=4)
                for h in range(4):
                    fw.transpose(V(tov[:, h, :], to.whole), ybf[:, h * 128:(h + 1) * 128], ident_bf[:], last=(h == 3))
                fw.copy(ytb[:], V(tov, to.whole))
                fw.dma("act", V(mixT_s.t[12:16, :, rows].rearrange("h p c -> p h c"), mixT_s.part(("ret", lc)).res), ytb[:])
        fw.phase_reset()

    def phase_ssd(l, need_ctx):
        OW, OB, ODT, OA, ODD = 0, 4608, 6144, 6176, 6208
        prm = fw.carve("prm", [128, 6224], F32)
        fw.dma("sp", prm[:], ssdp_d[l, :, 0:6224])
        tri = fw.carve("tri", [128, 5, 128], F32)
        fw.dma("sp", tri[:], tri_d[:])
        ssdn = fw.carve("ssdn", [128, 8], F32)
        fw.dma("sp", ssdn[:], ssdn_d[l])
        negA = fw.carve("negA", [128, 32], F32)
        fw.act(negA[:], prm[:, OA:OA + 32], AF.Exp)
        fw.ts(negA[:], negA[:], -1.0, ALU.mult)
        one_t = fw.carve("one_t", [128, 1], F32)
        fw.memset(one_t[:], 1.0)
        U = [fw.carve(f"U{i}", [128, 1536], F32) for i in range(3)]
        dtr = fw.carve("dtr", [128, 32], F32)
        la = fw.carve("la", [128, 32], F32)
        E = fw.carve("E", [128, 96], F32)
        praw = fw.carve("praw", [128, 96], F32)
        cumraw = Buf(fw, "cumraw", _APHandle(praw.t[:, 0:32]))
        cumraw.whole = praw.whole
        tots = fw.carve("tots", [128, 32], F32)
        v = [fw.carve(f"v{d}", [128, 1024], BF16) for d in range(2)]
        vte = [fw.carve(f"vte{d}", [128, 1024], BF16) for d in range(2)]
        BCbf = fw.carve("BCbf", [128, 512], BF16)
        BCT = fw.carve("BCT", [128, 4, 128], BF16)
        zt = fw.carve("zt", [128, 1024], F32)
        R1 = fw.carve("R1", [128, 16, 128], F32)
        seg = fw.carve("seg", [128, 16, 128], F32)
        Dm = fw.carve("Dm", [128, 16, 128], BF16)
        Sm = [fw.carve(f"Sm{d}", [128, 2, 128], F32) for d in range(2)]
        Wt = fw.carve("Wt", [128, 16, 128], BF16)
        S = [fw.carve(f"S{d}", [128, 1024], F32) for d in range(2)]
        Sx = [fw.carve(f"Sx{d}", [128, 1024], F32) for d in range(2)]
        Sbf = [fw.carve(f"Sbf{d}", [128, 1024], BF16) for d in range(2)]
        Pb = fw.carve("Pb", [128, 16], F32)
        yt = fw.carve("yt", [128, 1024], F32)
        y2 = fw.carve("y2", [128, 1024], F32)
        vtmp = Buf(fw, "vtmp", _APHandle(y2.t[:].rearrange("p (h n) -> p h n", n=64)))
        vtmp.whole = y2.whole
        gss = fw.carve("gss", [128, 2], F32)
        ybf = fw.carve("ybf", [128, 1024], BF16)
        ytb = fw.carve("ytb", [128, 8, 128], BF16)
        Aex = fw.carve("Aex", [128, 16], F32)
        h16 = lambda vv: V(vv.ap.unsqueeze(2).to_broadcast([128, 16, 64]), vv.res)
        as16 = lambda b_: V(b_.t[:].rearrange("p (h n) -> p h n", n=64), b_.whole)

        def prep(lc):
            is_ctx = lc >= NCH
            src_t, r0 = (xbc_cpad, (lc - NCH) * 128) if is_ctx else (xbc_pad, lc * 128)
            rr = [xbc_pad.part("hl").res, xbc_pad.part("hr").res]
            for k in range(3):
                fw.dma("sp", U[k][:], V(src_t.t[r0 + k:r0 + k + 128, :], rr))
            fw.dma("sp", dtr[:], V(proj_s.t[lc * 128:(lc + 1) * 128, 3584:3616], proj_s.part(("dtr", lc)).res))
            fw.tt(U[0][:], U[0][:], prm[:, OW:OW + 1536], ALU.mult, eng="pool")
            fw.tt(U[1][:], U[1][:], prm[:, OW + 1536:OW + 3072], ALU.mult)
            fw.tt(U[2][:], U[2][:], prm[:, OW + 3072:OW + 4608], ALU.mult, eng="pool")
            fw.tt(U[1][:], U[1][:], U[0][:], ALU.add)
            fw.tt(U[1][:], U[1][:], U[2][:], ALU.add)
            fw.tt(U[1][:], U[1][:], prm[:, OB:OB + 1536], ALU.add)
            fw.act(U[0][:], U[1][:], AF.Silu)
            fw.tt(dtr[:], dtr[:], prm[:, ODT:ODT + 32], ALU.add)
            fw.act(dtr[:], dtr[:], AF.Exp)
            fw.act(dtr[:], dtr[:], AF.Ln, bias=one_t[:])
            fw.tt(la[:], dtr[:], negA[:], ALU.mult)
            pe = pb[0]
            for i, (w, c0, c1) in enumerate(((0, 0, 16), (1, 16, 32), (2, 0, 16), (3, 16, 32), (4, 0, 32))):
                o0 = (0, 16, 32, 48, 64)[i]
                fw.mm(pe[:, o0:o0 + (c1 - c0)], tri[:, w, :], la[:, c0:c1], start=(i == 0), stop=(i == 4), skip_group_check=True)
            fw.copy(praw[:], pe[:, 0:96])
            fw.act(E[:], praw[:], AF.Exp)
            fw.tt(tots[:], tots[:], praw[:, 64:96], ALU.add)
            xs = V(U[0].t[:, 0:1024].rearrange("p (h n) -> p h n", n=64), U[0].whole)
            for d in range(2):
                fw.tt(vtmp[:], xs, h16(dtr[:, d * 16:(d + 1) * 16]), ALU.mult)
                fw.copy(as16(v[d]), vtmp[:], eng="pool")
                fw.tt(as16(vte[d]), vtmp[:], h16(E[:, 32 + d * 16:48 + d * 16]), ALU.mult)
            fw.copy(BCbf[:], U[0][:, 1024:1536], eng="pool")

        def chunk_state(d):
            banks = (pb[4], pb[5])
            for g in range(2):
                fw.mm(banks[g][:, :], BCbf[:, g * 128:(g + 1) * 128], vte[d][:, g * 512:(g + 1) * 512], True, True)
            return banks

        def mulA(dst, srcS, acol):
            fw.tt(as16(dst), as16(srcS), h16(acol), ALU.mult)

        for grp in ((16, 17), tuple(range(NCH))):
            for d in range(2):
                fw.memset(S[d][:], 0.0)
            fw.memset(Pb[:], 1.0)
            fw.memset(tots[:], 0.0)
            for lc in grp:
                prep(lc)
                if lc < NCH or need_ctx:
                    pr_ = pc_h.part(lc).res
                    fw.dma("sp", V(pc_h.t[lc, :, 0:1024], pr_), v[0][:])
                    fw.dma("sp", V(pc_h.t[lc, :, 1024:2048], pr_), v[1][:])
                    fw.dma("sp", V(pc_h.t[lc, :, 2048:3072], pr_), vte[0][:])
                    fw.dma("sp", V(pc_h.t[lc, :, 3072:3584], pr_), BCbf[:])
                    pf_ = pc_f.part(lc).res
                    fw.dma("sp", V(pc_f.t[lc, :, 0:1024], pf_), U[0][:, 0:1024])
                    fw.dma("sp", V(pc_f.t[lc, :, 1024:1120], pf_), praw[:])
                    fw.dma("sp", V(pc_f.t[lc, :, 1120:1152], pf_), la[:])
                for d in range(2):
                    banks = chunk_state(d)
                    csv = V(pb_t[:, 4:6, :], [pb[4].whole, pb[5].whole])
                    cs_sb = V(seg.t[:, d * 8:(d + 1) * 8, :].rearrange("p a (g c) -> p (a g) c", g=2)[:, 0:2, :] if False else seg.t[:, d * 8:(d + 1) * 8, :], seg.whole)
                    cs_flat = V(seg.t[:].rearrange("p h n -> p (h n)")[:, d * 1024:(d + 1) * 1024], seg.whole)
                    fw.copy(V(cs_flat.ap.rearrange("p (g c) -> p g c", g=2), seg.whole), csv)
                    fw.dma("sp", V(cst_s.t[lc, d, :, 0:1024], cst_s.part(("s", lc, d)).res), cs_flat)
                    if d == 0:
                        mulA(S[0], S[0], E[:, 64:80])
                        fw.tt(S[0][:], S[0][:], cs_flat, ALU.add)
                    else:
                        fw.tt(as16(y2), V(cs_flat.ap.rearrange("p (h n) -> p h n", n=64), seg.whole), h16(Pb[:, :]), ALU.mult)
                        fw.tt(S[1][:], S[1][:], y2[:], ALU.add)
                        fw.tt(Pb[:], Pb[:], E[:, 80:96], ALU.mult)
                fw.dma("sp", V(cst_s.t[lc, 0, :, 1536:1568], cst_s.part(("e", lc)).res), E[:, 64:96])
            if grp[0] == 16:
                for d in range(2):
                    fw.copy(Sx[d][:], S[d][:])
        if "ssd_sf" in dbg:
            fw.dma("sp", dbg["ssd_sf"][:, :], Sx[0][:])
            fw.dma("sp", dbg["ssd_sb"][:, :], Sx[1][:])
        for d in range(2):
            fw.dma("sp", st_stage[d][:, 0:1024], S[d][:])
            fw.dma("sp", st_stage[d][:, 1536:1552], tots[:, d * 16:(d + 1) * 16])
            fw.dma("sp", V(st_src[d].t[bass.ds(rank * 128, 128), :], st_src[d].whole), st_stage[d][:, :])
            fw.collective("AllReduce", ALU.add, GROUPS, st_src[d][:, :], st_dst[d][:, :])
        for d in range(2):
            order = range(4) if d == 0 else range(3, -1, -1)
            for sidx in order:
                fw.dma("sp", yt[:], st_dst[d][sidx * 128:(sidx + 1) * 128, 0:1024])
                fw.dma("sp", Aex[:], st_dst[d][sidx * 128:(sidx + 1) * 128, 1536:1552])
                fw.act(Aex[:], Aex[:], AF.Exp)
                mulA(y2, Sx[d], Aex[:, :])
                fw.tt(y2[:], y2[:], yt[:], ALU.add)
                fw.tt(y2[:], y2[:], Sx[d][:], ALU.subtract)
                fw.stt(Sx[d][:], y2[:], cmask[:, d * 4 + sidx:d * 4 + sidx + 1], Sx[d][:], ALU.mult, ALU.add)
        for lc in range(NCH - 1, -1, -1):
            fw.copy(Sbf[1][:], Sx[1][:])
            fw.dma("sp", V(sb_s.t[lc, :, 0:1024], sb_s.part(("s", lc)).res), Sbf[1][:])
            ld = yt if lc % 2 == 0 else y2
            fw.dma("sp", ld[:], V(cst_s.t[lc, 1, :, 0:1024], cst_s.part(("s", lc, 1)).res))
            fw.dma("sp", Aex[:], V(cst_s.t[lc, 0, :, 1552:1568], cst_s.part(("e", lc)).res))
            mulA(Sx[1], Sx[1], Aex[:, :])
            fw.tt(Sx[1][:], Sx[1][:], ld[:], ALU.add)
        if need_ctx:
            fw.dma("sp", yt[:], V(cst_s.t[17, 1, :, 0:1024], cst_s.part(("s", 17, 1)).res))
            fw.copy(Sbf[1][:], yt[:])
            fw.dma("sp", V(sb_s.t[16, :, 0:1024], sb_s.part(("s", 16)).res), Sbf[1][:])
            fw.memset(Sbf[0][:], 0.0)
            fw.dma("sp", V(sb_s.t[17, :, 0:1024], sb_s.part(("s", 17)).res), Sbf[0][:])
        if stop_after == "ssd_p2":
            fw.phase_reset(); return
        groups3 = [tuple(range(NCH))] + ([(16, 17)] if need_ctx else [])
        for grp in groups3:
            if grp[0] == 16:
                fw.memset(Sx[0][:], 0.0)
            for lc in grp:
                rows = slice(lc * 128, (lc + 1) * 128)
                pr_ = pc_h.part(lc).res
                pf_ = pc_f.part(lc).res
                fw.dma("sp", v[0][:], V(pc_h.t[lc, :, 0:1024], pr_))
                fw.dma("sp", v[1][:], V(pc_h.t[lc, :, 1024:2048], pr_))
                fw.dma("sp", vte[0][:], V(pc_h.t[lc, :, 2048:3072], pr_))
                fw.dma("sp", BCbf[:], V(pc_h.t[lc, :, 3072:3584], pr_))
                fw.dma("sp", U[0][:, 0:1024], V(pc_f.t[lc, :, 0:1024], pf_))
                fw.dma("sp", praw[:], V(pc_f.t[lc, :, 1024:1120], pf_))
                fw.dma("sp", la[:], V(pc_f.t[lc, :, 1120:1152], pf_))
                fw.act(E[:], praw[:], AF.Exp)
                fw.dma("sp", zt[:, 0:512], V(proj_s.t[rows, 3616:4128], proj_s.part(("z0", lc)).res))
                fw.dma("sp", zt[:, 512:1024], V(proj_s.t[rows, 4128:4640], proj_s.part(("z1", lc)).res))
                fw.dma("sp", Sbf[1][:], V(sb_s.t[lc, :, 0:1024], sb_s.part(("s", lc)).res))
                fw.copy(Sbf[0][:], Sx[0][:], eng="pool")
                tb_ = pbh[1]
                tbv = tb_.t[:, 0:512].rearrange("p (a c) -> p a c", a=4)
                for a in range(4):
                    fw.transpose(V(tbv[:, a, :], tb_.whole), BCbf[:, a * 128:(a + 1) * 128], ident_bf[:], last=(a == 3))
                fw.copy(BCT[:], V(tbv, tb_.whole))
                sc = pb[1]
                scv = sc.t[:, 256:512].rearrange("p (g i) -> p g i", g=2)
                for g in range(2):
                    fw.mm(V(scv[:, g, :], sc.whole), BCT[:, g, :], BCT[:, 2 + g, :], start=False if False else (g == 0), stop=(g == 1), skip_group_check=True)
                for d in range(2):
                    fw.tt(Sm[d][:], V(scv, sc.whole), V(tri.t[:, d:d + 1, :].to_broadcast([128, 2, 128]), tri.whole), ALU.mult)
                yb = (pb[6], pb[7])
                for d in range(2):
                    for half, eng_ in ((0, "pool"), (1, "dve")):
                        fw.tt(R1[:, half * 8:(half + 1) * 8, :],
                              V(la.t[:, d * 16 + half * 8:d * 16 + (half + 1) * 8].unsqueeze(2).to_broadcast([128, 8, 128]), la.whole),
                              V(tri.t[:, d:d + 1, :].to_broadcast([128, 8, 128]), tri.whole), ALU.mult, eng=eng_)
                    for q in range(4):
                        bank = pb[2 + q % 2]
                        fw.mm(bank[:, :], tri[:, 4, :], V(R1.t[:, 4 * q:4 * q + 4, :].rearrange("p h n -> p (h n)"), R1.whole), True, True)
                        for hh in range(4):
                            h = 4 * q + hh
                            fw.ts(seg[:, h, :], bank[:, hh * 128:(hh + 1) * 128], cumraw[:, d * 16 + h:d * 16 + h + 1], ALU.subtract, 0.0, ALU.min)
                    fw.act(Dm[:], seg[:], AF.Exp)
                    for g in range(2):
                        fw.tt(Wt[:, g * 8:(g + 1) * 8, :], Dm[:, g * 8:(g + 1) * 8, :],
                              V(Sm[d].t[:, g:g + 1, :].to_broadcast([128, 8, 128]), Sm[d].whole), ALU.mult)
                    for h in range(16):
                        fw.mm(yb[h // 8][:, (h % 8) * 64:(h % 8 + 1) * 64], Wt[:, h, :], v[d][:, h * 64:(h + 1) * 64],
                              start=(d == 0 and h % 8 == 0), stop=(d == 1 and h % 8 == 7), last=(d == 1 and h % 8 == 7), skip_group_check=True)
                for d in range(2):
                    for g in range(2):
                        fw.mm(pb[2 + g][:, :], BCT[:, 2 + g, :], Sbf[d][:, g * 512:(g + 1) * 512], True, True)
                    ysv = V(pb_t[:, 2:4, :].rearrange("p a (h n) -> p (a h) n", n=64), [pb[2].whole, pb[3].whole])
                    fw.tt(as16(yt if d == 0 else y2), ysv, h16(E[:, d * 16:(d + 1) * 16]), ALU.mult)
                fw.tt(yt[:], yt[:], y2[:], ALU.add)
                yv = V(pb_t[:, 6:8, :].rearrange("p a c -> p (a c)") if False else pb_t[:, 6:8, :], [pb[6].whole, pb[7].whole])
                fw.tt(V(yt.t[:].rearrange("p (a c) -> p a c", a=2), yt.whole), V(yt.t[:].rearrange("p (a c) -> p a c", a=2), yt.whole), yv, ALU.add)
                xs = V(U[0].t[:, 0:1024].rearrange("p (h n) -> p h n", n=64), U[0].whole)
                fw.tt(as16(y2), xs, h16(prm[:, ODD:ODD + 16]), ALU.mult, eng="pool")
                fw.tt(yt[:], yt[:], y2[:], ALU.add)
                fw.act(zt[:], zt[:], AF.Silu)
                fw.tt(yt[:], yt[:], zt[:], ALU.mult)
                banks = chunk_state(0)
                mulA(Sx[0], Sx[0], E[:, 64:80])
                fw.tt(V(Sx[0].t[:].rearrange("p (g c) -> p g c", g=2), Sx[0].whole), V(Sx[0].t[:].rearrange("p (g c) -> p g c", g=2), Sx[0].whole),
                      V(pb_t[:, 4:6, :], [pb[4].whole, pb[5].whole]), ALU.add)
                fw.act(y2[:], yt[:], AF.Square)
                fw.reduce(gss[:], V(y2.t[:].rearrange("p (g c) -> p g c", g=2), y2.whole), ALU.add)
                fw.act(gss[:], gss[:], AF.Sqrt, bias=eps_t[:], scale=1.0 / 512)
                fw.recip(gss[:], gss[:])
                fw.tt(V(ybf.t[:].rearrange("p (g c) -> p g c", g=2), ybf.whole), V(yt.t[:].rearrange("p (g c) -> p g c", g=2), yt.whole),
                      V(gss.t[:].unsqueeze(2).to_broadcast([128, 2, 512]), gss.whole), ALU.mult)
                to = pbh[1]
                tov = to.t[:, 0:1024].rearrange("p (a c) -> p a c", a=8)
                for a in range(8):
                    fw.transpose(V(tov[:, a, :], to.whole), ybf[:, a * 128:(a + 1) * 128], ident_bf[:], last=(a == 7))
                fw.tt(ytb[:], V(tov, to.whole), V(ssdn.t[:].unsqueeze(2).to_broadcast([128, 8, 128]), ssdn.whole), ALU.mult)
                fw.dma("sp", V(mixT_s.t[4:12, :, rows].rearrange("h p c -> p h c"), mixT_s.part(("ssd", lc)).res), ytb[:])
        fw.phase_reset()

    def phase_out(l, need_ctx):
        wout = fw.carve("wout", [128, 16, D], BF16)
        wst2 = [fw.carve(f"wost{i}", [128, 4, D], F32) for i in range(2)]
        mx = [fw.carve(f"mx{i}", [128, 16, 128], BF16) for i in range(2)]
        tmp = fw.carve("otmp", [128, 512], F32)
        for q in range(4):
            wst = wst2[q % 2]
            fw.dma("sp", V(wst.t[:], wst.whole),
                   V(wout_d.t[l, q * 512:(q + 1) * 512, :].rearrange("(f p) c -> p f c", p=128), wout_d.whole))
            fw.copy(wout[:, q * 4:(q + 1) * 4, :], wst[:], eng="pool")
        allmix = [r for r in mixT_s.parts.values()]
        for lc in (range(NLC) if need_ctx else range(NCH)):
            is_ctx = lc >= NCH
            m = mx[lc % 2]
            fw.dma("sp", m[:], V(mixT_s.t[:, :, lc * 128:(lc + 1) * 128].rearrange("f p c -> p f c"), allmix))
            for hh in range(2):
                bank = pb[(lc % 2) * 2 + hh]
                for fc in range(16):
                    fw.mm(bank[:, :], m[:, fc, :], wout[:, fc, hh * 512:(hh + 1) * 512], start=(fc == 0), stop=(fc == 15))
                cs = slice(hh * 512, (hh + 1) * 512)
                fw.tt(tmp[:], bank[:, :], gate[:, 1 if is_ctx else 0, cs], ALU.mult)
                dst = ctx_sb[:, lc - NCH, cs] if is_ctx else x_sb[:, lc, cs]
                fw.tt(dst, dst, tmp[:], ALU.add)
        fw.phase_reset()

    for l in range(depth):
        need_ctx = l < depth - 1
        adaln(l)
        layer_params(l)
        phase_proj(l, need_ctx)
        if stop_after == "t_proj": break
        phase_attn(l, need_ctx)
        if stop_after == "t_attn": break
        phase_halo(l)
        phase_ret(l, need_ctx, "a")
        phase_ssd(l, need_ctx)
        phase_ret(l, need_ctx, "b")
        if stop_after == "t_ssd": break
        phase_out(l, need_ctx)
        if l == 0 and "x0" in dbg:
            fw.dma("sp", V(dbg["x0"].t.ap().rearrange("(c p) d -> p c d", p=128), dbg["x0"].whole), V(x_sb.t[:], x_sb.whole))
            fw.dma("sp", V(dbg["ctx0"].t.ap().rearrange("(c p) d -> p c d", p=128), dbg["ctx0"].whole), V(ctx_sb.t[:], ctx_sb.whole))

    if "qT" in dbg:
        fw.dma("sp", dbg["qT"][:], V(qT_s.t[:], [qT_s.part(c).res for c in range(NLC)]))
    if "kv0" in dbg:
        fw.dma("sp", dbg["kv0"][:], kv_dst[0][:, :])
    if "proj" in dbg:
        fw.dma("sp", dbg["proj"][:], V(proj_s.t[:], [r for r in proj_s.parts.values()]))
    if "mixT" in dbg:
        fw.dma("sp", dbg["mixT"][:], V(mixT_s.t[0:4], [r for r in mixT_s.parts.values()]))
    if "mixS" in dbg:
        fw.dma("sp", dbg["mixS"][:], V(mixT_s.t[4:12], [r for r in mixT_s.parts.values()]))
    if "mixR" in dbg:
        fw.dma("sp", dbg["mixR"][:], V(mixT_s.t[12:16], [r for r in mixT_s.parts.values()]))
    fw.dma("sp", V(out_d.t.ap().rearrange("(c p) d -> p c d", p=128), out_d.whole), V(x_sb.t[:], x_sb.whole))
    fw.wait_all("sp", [out_d[:]] + [V(b.t[:], b.whole) for b in dbg.values()])
    return nc, fw


def rope_tables():
    n_freq = 16
    inv_freq = (10000.0 ** (-np.arange(n_freq, dtype=np.float32) / n_freq)).astype(np.float32)
    pos = np.arange(8192)
    row = (pos // 64).astype(np.float32)
    col = (pos % 64).astype(np.float32)
    ang = np.concatenate([row[:, None] * inv_freq, col[:, None] * inv_freq], axis=-1).astype(np.float32)
    return np.cos(ang).astype(np.float32), np.sin(ang).astype(np.float32)


def const_tables():
    j = np.arange(128)[:, None]; i = np.arange(128)[None, :]
    tri = np.stack([(j <= i), (j >= i), (j > i), (j < i), np.ones((128, 128), bool)], axis=1).astype(np.float32)
    gf = np.array([1.0 - 2.0 ** -e for e in RET_EXP_F], np.float64)
    gb = np.array([1.0 - 2.0 ** -e for e in RET_EXP_B], np.float64)
    dif = (i - j).astype(np.float64)
    Dret = np.zeros((128, 4, 128), np.float64)
    for h in range(4):
        Dret[:, h, :] = np.where(dif > 0, gf[h] ** np.abs(dif), 0.0) + np.where(dif < 0, gb[h] ** np.abs(dif), 0.0) + np.where(dif == 0, 2.0, 0.0)
    Dret *= 0.125
    pos = np.arange(128, dtype=np.float64)[:, None]
    te_f = gf[None, :] ** (127 - pos) * 0.125
    te_b = gb[None, :] ** pos * 0.125
    qsc_f = gf[None, :] ** (pos + 1)
    qsc_b = gb[None, :] ** (128 - pos)
    a_f = np.broadcast_to(gf[None, :] ** 128, (128, 4)); a_b = np.broadcast_to(gb[None, :] ** 128, (128, 4))
    rett = np.concatenate([Dret.reshape(128, 512), te_f, te_b, qsc_f, qsc_b, a_f, a_b], axis=1).astype(np.float32)
    return np.ascontiguousarray(tri), np.ascontiguousarray(rett)


def make_inputs(inp):
    cos, sin = rope_tables()
    tri, rett = const_tables()
    ssdp = np.concatenate([inp["ssd_conv_w"].reshape(2, -1), inp["ssd_conv_b"], inp["ssd_dt_bias"].reshape(2, -1),
                           inp["ssd_a_log"].reshape(2, -1), inp["ssd_d"], inp["ret_norm"]], axis=1).astype(np.float32)
    ssdp = np.ascontiguousarray(np.broadcast_to(ssdp[:, None, :], (2, 128, ssdp.shape[1])))
    ssdn = np.ascontiguousarray(inp["ssd_norm"].reshape(2, 8, 128).transpose(0, 2, 1))
    rep = lambda a: np.ascontiguousarray(np.broadcast_to(a[:, None], (a.shape[0], 128) + a.shape[1:]))
    qkg = rep(np.stack([inp["attn_q_norm"], inp["attn_k_norm"]], axis=1))
    lamv = rep(np.stack([inp["lambda_q1"], inp["lambda_k1"], inp["lambda_q2"], inp["lambda_k2"]], axis=1))
    subln = np.ascontiguousarray(inp["attn_subln"][:, :, None])
    maps = []
    for core in range(8):
        b, t = core // 4, core % 4
        lo = t * TOK
        cc = np.stack([inp["c"][b].reshape(8, 128).T, inp["c_ctx"].reshape(8, 128).T], axis=-1)
        rp = np.stack([cos[lo:lo + TOK], sin[lo:lo + TOK]], axis=1)
        rp = rp.reshape(NCH, 128, 2, 32).transpose(1, 0, 2, 3)
        m = {
            "x": np.ascontiguousarray(inp["x"][b, lo:lo + TOK]),
            "ctx": np.ascontiguousarray(inp["ctx"][b]),
            "cc": np.ascontiguousarray(cc.astype(np.float32)),
            "w_ada": inp["w_ada"],
            "b_ada_f": np.ascontiguousarray(inp["b_ada"][:, :2 * D].reshape(2, 16, 128).transpose(0, 2, 1)),
            "b_ada": inp["b_ada"],
            "w_in": inp["w_in"], "w_out": inp["w_out"],
            "qkg": qkg, "lamv": lamv, "subln": subln,
            "rope": np.ascontiguousarray(rp),
            "tri": tri, "rett": rett, "ssdp": ssdp, "ssdn": ssdn,
            "cmask": np.ascontiguousarray(np.broadcast_to(np.array([float(s_ < t) for s_ in range(4)] + [float(s_ > t) for s_ in range(4)], np.float32)[None], (128, 8))),
        }
        maps.append(m)
    return maps


from concourse.bass_utils import run_bass_kernel_spmd


def kernel(**inputs):
    inp = {k: np.asarray(v) for k, v in inputs.items()}
    nc, _ = build(depth=2)
    maps = make_inputs(inp)
    res = run_bass_kernel_spmd(nc, maps, core_ids=list(range(8)))
    outs = [np.asarray(res.results[c]["out"]) for c in range(8)]
    return np.stack([np.concatenate(outs[0:4], 0), np.concatenate(outs[4:8], 0)]).astype(np.float32)
```

```python
import numpy as np
import concourse.bass as bass
import concourse.mybir as mybir

F32 = mybir.dt.float32
BF16 = mybir.dt.bfloat16
AF = mybir.ActivationFunctionType
ALU = mybir.AluOpType
AX = mybir.AxisListType


class Res:
    __slots__ = ("name", "w", "r")

    def __init__(self, name):
        self.name = name
        self.w = None
        self.r = {}


class V:
    __slots__ = ("ap", "res")

    def __init__(self, ap, res):
        self.ap = ap
        self.res = res if isinstance(res, (list, tuple)) else [res]


class Buf:
    def __init__(self, fw, name, t, nparts=1):
        self.fw = fw
        self.name = name
        self.t = t
        self.parts = {}
        self.whole = Res(name)

    def __getitem__(self, idx):
        return V(self.t[idx], self.whole)

    def part(self, key):
        if key not in self.parts:
            self.parts[key] = Res(f"{self.name}.{key}")
        return _PartView(self, self.parts[key])

    def ap(self):
        return self.t.ap()


class _PartView:
    def __init__(self, buf, res):
        self.buf = buf
        self.res = res

    def __getitem__(self, idx):
        return V(self.buf.t[idx], self.res)


class EngState:
    def __init__(self, name, eng, sem):
        self.name = name
        self.eng = eng
        self.sem = sem
        self.count = 0
        self.pending = False
        self.seen = {}
        self.seen_dma = {}


class FW:
    def __init__(self, nc, n_dma_sems=24, same_engine_sync=True):
        self.nc = nc
        self.same_engine_sync = same_engine_sync
        self.engs = {}
        for name, eng in (("pe", nc.tensor), ("dve", nc.vector), ("act", nc.scalar),
                          ("pool", nc.gpsimd), ("sp", nc.sync)):
            self.engs[name] = EngState(name, eng, nc.alloc_semaphore(f"s_{name}"))
        self.dma_sems = [nc.alloc_semaphore(f"s_dma{i}") for i in range(n_dma_sems)]
        self.dma_vals = [0] * n_dma_sems
        self.dma_next = 0
        self.n_inst = 0
        self.out_tokens = []
        self.cc_sem = None
        self.cc_val = 0

    def sbuf(self, name, shape, dtype):
        return Buf(self, name, self.nc.alloc_sbuf_tensor("sb_" + name, list(shape), dtype))

    def psum(self, name, shape, dtype=F32):
        return Buf(self, name, self.nc.alloc_psum_tensor("ps_" + name, list(shape), dtype))

    def dram(self, name, shape, dtype, kind="Internal", **kw):
        return Buf(self, name, self.nc.dram_tensor(name, list(shape), dtype, kind=kind, **kw))

    def _need(self, E, tok):
        if tok is None:
            return
        if tok[0] == "eng":
            _, e, c = tok
            if e == E.name:
                if not self.same_engine_sync or e == "pe":
                    return
            if E.seen.get(e, 0) >= c:
                return
            P = self.engs[e]
            assert c <= P.count, f"{E.name} waits on pending (never-incremented) {e} count {c} > {P.count}"
            E.eng.wait_ge(P.sem, c)
            E.seen[e] = c
        elif tok[0] == "cc":
            val = tok[1]
            if E.seen_dma.get("cc", 0) >= val:
                return
            E.eng.wait_ge(self.cc_sem, val)
            E.seen_dma["cc"] = val
        else:
            _, si, val = tok
            if E.seen_dma.get(si, 0) >= val:
                return
            E.eng.wait_ge(self.dma_sems[si], val)
            E.seen_dma[si] = val

    def _pre(self, E, reads, writes):
        for v in reads:
            for r in v.res:
                self._need(E, r.w)
        for v in writes:
            for r in v.res:
                self._need(E, r.w)
                for tok in r.r.values():
                    self._need(E, tok)

    def _post(self, tok, key, reads, writes):
        for v in reads:
            for r in v.res:
                r.r[key] = tok
        for v in writes:
            for r in v.res:
                r.w = tok
                r.r = {}

    def op(self, engname, fn, reads, writes, inc=True):
        E = self.engs[engname]
        self._pre(E, reads, writes)
        ins = fn(E.eng)
        self.n_inst += 1
        if inc:
            E.count += 1
            ins.then_inc(E.sem, 1)
            tok = ("eng", engname, E.count)
        else:
            tok = ("eng", engname, E.count + 1)
        self._post(tok, engname, reads, writes)
        return ins

    def dma(self, qname, out, in_, **kw):
        E = self.engs[qname]
        self._pre(E, [in_], [out])
        si = self.dma_next
        self.dma_next = (self.dma_next + 1) % len(self.dma_sems)
        if self.dma_vals[si] > 0:
            self._need(E, ("dma", si, self.dma_vals[si]))
        self.dma_vals[si] += 16
        ins = E.eng.dma_start(out=out.ap, in_=in_.ap, **kw)
        ins.then_inc(self.dma_sems[si], 16)
        self.n_inst += 1
        tok = ("dma", si, self.dma_vals[si])
        self._post(tok, f"dma{si}", [in_], [out])
        return tok

    def wait_all(self, engname, views):
        E = self.engs[engname]
        for v in views:
            for r in v.res:
                self._need(E, r.w)

    def mm(self, out, lhsT, rhs, start, stop, last=None, **kw):
        if last is None:
            last = stop
        return self.op("pe", lambda e: e.matmul(out.ap, lhsT.ap, rhs.ap, start=start, stop=stop, **kw),
                       [lhsT, rhs], [out], inc=last)

    def transpose(self, out, in_, ident, last=True):
        return self.op("pe", lambda e: e.transpose(out.ap, in_.ap, ident.ap), [in_, ident], [out], inc=last)

    def act(self, out, in_, func, bias=None, scale=1.0, accum_out=None, eng="act"):
        reads = [in_]
        kw = {}
        if bias is not None:
            if isinstance(bias, V):
                reads.append(bias)
                kw["bias"] = bias.ap
            else:
                kw["bias"] = bias
        if isinstance(scale, V):
            reads.append(scale)
            kw["scale"] = scale.ap
        else:
            kw["scale"] = scale
        writes = [out]
        if accum_out is not None:
            writes.append(accum_out)
            kw["accum_out"] = accum_out.ap
        return self.op(eng, lambda e: e.activation(out.ap, in_.ap, func, **kw), reads, writes)

    def tt(self, out, in0, in1, op, eng="dve"):
        return self.op(eng, lambda e: e.tensor_tensor(out.ap, in0.ap, in1.ap, op), [in0, in1], [out])

    def ts(self, out, in0, s1, op0, s2=None, op1=None, eng="dve", accum_out=None):
        reads = [in0]
        a1 = s1
        if isinstance(s1, V):
            reads.append(s1)
            a1 = s1.ap
        a2 = s2
        if isinstance(s2, V):
            reads.append(s2)
            a2 = s2.ap
        kw = {}
        writes = [out]
        if op1 is not None:
            kw["op1"] = op1
        if accum_out is not None:
            kw["accum_out"] = accum_out.ap
            writes.append(accum_out)
        return self.op(eng, lambda e: e.tensor_scalar(out.ap, in0.ap, a1, a2, op0, **kw), reads, writes)

    def stt(self, out, in0, scalar, in1, op0, op1, eng="dve"):
        reads = [in0, in1]
        a = scalar
        if isinstance(scalar, V):
            reads.append(scalar)
            a = scalar.ap
        return self.op(eng, lambda e: e.scalar_tensor_tensor(out.ap, in0.ap, a, in1.ap, op0, op1), reads, [out])

    def copy(self, out, in_, eng="dve"):
        if eng == "act":
            return self.op("act", lambda e: e.copy(out.ap, in_.ap), [in_], [out])
        return self.op(eng, lambda e: e.tensor_copy(out.ap, in_.ap), [in_], [out])

    def memset(self, out, val, eng="dve"):
        return self.op(eng, lambda e: e.memset(out.ap, val), [], [out])

    def reduce(self, out, in_, op, axis=AX.X, eng="dve"):
        return self.op(eng, lambda e: e.tensor_reduce(out.ap, in_.ap, axis, op), [in_], [out])

    def recip(self, out, in_):
        return self.op("dve", lambda e: e.reciprocal(out.ap, in_.ap), [in_], [out])

    def collective(self, kind, op, groups, in_, out):
        E = self.engs["pool"]
        self._pre(E, [in_], [out])
        if self.cc_sem is None:
            self.cc_sem = self.nc.alloc_semaphore("s_cc")
        self.cc_val += 1
        ins = E.eng.collective_compute(kind, op, replica_groups=groups, ins=[in_.ap], outs=[out.ap])
        ins.then_inc(self.cc_sem)
        self.n_inst += 1
        tok = ("cc", self.cc_val)
        self._post(tok, "cc", [in_], [out])
        return tok

    def make_arena(self, kbytes):
        self.arena_t = self.nc.alloc_sbuf_tensor("sb_arena", [128, kbytes * 256], F32)
        self.arena_words = kbytes * 256
        self.arena_off = 0
        self.arena_gen = 0

    def carve(self, name, shape, dtype):
        esz = 2 if dtype == BF16 else 4
        n = 1
        for s in shape[1:]:
            n *= s
        words = (n * esz + 3) // 4
        words = (words + 7) // 8 * 8
        assert self.arena_off + words <= self.arena_words, f"arena overflow for {name}: {self.arena_off}+{words}>{self.arena_words}"
        raw = self.arena_t[0:shape[0], self.arena_off:self.arena_off + words]
        self.arena_off += words
        ap = raw.bitcast(dtype) if dtype != F32 else raw
        ap = ap[:, 0:n]
        if len(shape) > 2:
            names = " ".join(f"d{i}" for i in range(1, len(shape)))
            kw = {f"d{i}": shape[i] for i in range(1, len(shape))}
            ap = ap.rearrange(f"p ({names}) -> p {names}", **kw)
        return Buf(self, f"{name}@{self.arena_gen}", _APHandle(ap))

    def barrier(self):
        for E in self.engs.values():
            for P in self.engs.values():
                if P is not E and P.count > 0:
                    self._need(E, ("eng", P.name, P.count))
            for si, val in enumerate(self.dma_vals):
                if val > 0:
                    self._need(E, ("dma", si, val))
            if self.cc_val > 0:
                self._need(E, ("cc", self.cc_val))

    def phase_reset(self):
        self.barrier()
        self.arena_off = 0
        self.arena_gen += 1


class _APHandle:
    def __init__(self, ap):
        self._ap = ap

    def __getitem__(self, idx):
        return self._ap[idx]

    def ap(self):
        return self._ap


import math
KCUT = 9

D = 1024
NCH = 16
TOK = 2048
NTOK = TOK + 256
NLC = 18
DIN = 6176
EPS = 1e-6
GROUPS = [[0, 1, 2, 3], [4, 5, 6, 7]]
SW = 1568
NP = 3 * 1536 + 1536 + 32 + 32 + 16 + 128
RET_EXP_F = (5.0, 6.0, 7.0, 8.0)
RET_EXP_B = (5.5, 6.5, 7.5, 8.5)
BLOCKS = [("aq", 0, 512), ("ak", 512, 512), ("av", 1024, 512), ("ag", 1536, 512),
          ("xbc0", 2048, 512), ("xbc1", 2560, 512), ("xbc2", 3072, 512), ("dtr", 3584, 32),
          ("z0", 3616, 512), ("z1", 4128, 512), ("rqk", 4640, 512), ("rv", 5152, 512), ("rg", 5664, 512)]


class Alt:
    def __init__(self, bufs, par):
        self.bufs, self.par = bufs, par

    @property
    def cur(self):
        return self.bufs[self.par[0] % len(self.bufs)]

    def __getitem__(self, idx):
        return self.cur[idx]

    @property
    def t(self):
        return self.cur.t

    @property
    def whole(self):
        return self.cur.whole


def lam_init_of(layer):
    return 0.8 - 0.6 * math.exp(-0.3 * layer)


def build(depth=2, debug=None, stop_after=None):
    debug = debug or {}
    nc = bass.Bass("TRN2", target_bir_lowering=False)
    fw = FW(nc, same_engine_sync=True)
    I = lambda n, s, d=F32: fw.dram(n, s, d, kind="ExternalInput")
    x_d = I("x", [TOK, D])
    ctx_d = I("ctx", [256, D])
    cc_d = I("cc", [128, 8, 2])
    wada_d = I("w_ada", [2, D, 3 * D])
    bada_f_d = I("b_ada_f", [2, 128, 16])
    bada_d = I("b_ada", [2, 3 * D])
    win_d = I("w_in", [2, D, DIN])
    wout_d = I("w_out", [2, 2 * D, D])
    qkg_d = I("qkg", [2, 128, 2, 64])
    lamv_d = I("lamv", [2, 128, 4, 64])
    subln_d = I("subln", [2, 128, 1])
    rope_d = I("rope", [128, NCH, 2, 32])
    tri_d = I("tri", [128, 5, 128])
    rett_d = I("rett", [128, 4 * 128 + 24])
    ssdp_d = I("ssdp", [2, 128, NP])
    ssdn_d = I("ssdn", [2, 128, 8])
    cmask_d = I("cmask", [128, 8])
    out_d = fw.dram("out", [TOK, D], F32, kind="ExternalOutput")
    dbg = {k: fw.dram("dbg_" + k, shape, dt_, kind="ExternalOutput") for k, (shape, dt_) in debug.items()}

    proj_s = fw.dram("proj_s", [NTOK, DIN], F32)
    qT_s = fw.dram("qT_s", [4, 128, NTOK], BF16)
    agT_s = fw.dram("agT_s", [4, 128, NTOK], BF16)
    mixT_s = fw.dram("mixT_s", [16, 128, NTOK], BF16)
    kc_s = fw.dram("kc_s", [4, 128, 256], BF16)
    vc_s = fw.dram("vc_s", [256, 512], BF16)
    xbc_pad = fw.dram("xbc_pad", [TOK + 2, 1536], F32)
    xbc_cpad = fw.dram("xbc_cpad", [258, 1536], F32)
    hx_stage = fw.dram("hx_stage", [2, 1536], F32)
    hx_src = fw.dram("hx_src", [8, 1536], F32)
    hx_dst = fw.dram("hx_dst", [8, 1536], F32)
    hxL = fw.dram("hxL", [9, 1536], F32)
    hxR = fw.dram("hxR", [8, 1536], F32)
    cst_s = fw.dram("cst_s", [NLC, 2, 128, SW], F32)
    sb_s = fw.dram("sb_s", [NLC, 128, 1536], BF16)
    pc_h = fw.dram("pc_h", [NLC, 128, 3584], BF16)
    pc_f = fw.dram("pc_f", [NLC, 128, 1152], F32)
    rc_f = fw.dram("rc_f", [NLC, 128, 512], F32)
    rc_h = fw.dram("rc_h", [NLC, 128, 1024], BF16)
    rctx_s = fw.dram("rctx_s", [2, 64, 512], F32)
    st_stage = [fw.dram(f"st_stage{d}", [128, SW], F32) for d in range(2)]
    st_src = [fw.dram(f"st_src{d}", [512, SW], F32) for d in range(2)]
    st_dst = [fw.dram(f"st_dst{d}", [512, SW], F32) for d in range(2)]
    kv_src = [fw.dram(f"kv_src{h}", [512, 4096], BF16) for h in range(4)]
    kv_dst = [fw.dram(f"kv_dst{h}", [512, 4096], BF16) for h in range(4)]
    kv_stage = [fw.dram(f"kv_stage{h}", [128, 4096], BF16) for h in range(4)]

    x_sb = fw.sbuf("x_sb", [128, NCH, D], F32)
    ctx_sb = fw.sbuf("ctx_sb", [128, 2, D], F32)
    gate = fw.sbuf("gate", [128, 2, D], F32)
    sc1 = fw.sbuf("sc1", [128, 2, 8], F32)
    sh = fw.sbuf("sh", [128, 2, 8], F32)
    ident = fw.sbuf("ident", [128, 128], F32)
    ident_bf = fw.sbuf("ident_bf", [128, 128], BF16)
    ones_bf = fw.sbuf("ones_bf", [128, 128], BF16)
    eps_t = fw.sbuf("eps_t", [128, 1], F32)
    cc = fw.sbuf("cc", [128, 8, 2], F32)
    rope = fw.sbuf("rope", [128, NCH, 2, 32], F32)
    qkg = fw.sbuf("qkg", [128, 2, 64], F32)
    lamv = fw.sbuf("lamv", [128, 4, 64], F32)
    neglam = fw.sbuf("neglam", [128, 1], F32)
    subln = fw.sbuf("subln", [128, 1], F32)
    small = fw.sbuf("small", [128, 64], F32)
    cmask = fw.sbuf("cmask", [128, 8], F32)
    fw.make_arena(119)
    pb_t = nc.alloc_psum_tensor("ps_banks", [128, 8, 512], F32)
    pb = [Buf(fw, f"pb{i}", _APHandle(pb_t[:, i, :])) for i in range(8)]
    pbh = [Buf(fw, f"pbh{i}", _APHandle(pb_t[:, i, :].bitcast(BF16))) for i in range(8)]
    for i in range(8):
        pbh[i].whole = pb[i].whole
    rank = nc.partition_id() % 4

    fw.memset(ident[:], 1.0, eng="pool")
    fw.op("pool", lambda e: e.affine_select(ident.t[:], ident.t[:], [[-1, 128]], ALU.is_equal, 0.0,
                                             base=0, channel_multiplier=1), [ident[:]], [ident[:]])
    fw.copy(ident_bf[:], ident[:])
    fw.memset(ones_bf[:], 1.0)
    fw.memset(eps_t[:], EPS)
    fw.dma("sp", V(x_sb.t[:], x_sb.whole), V(x_d.t.ap().rearrange("(c p) d -> p c d", p=128), x_d.whole))
    fw.dma("sp", V(ctx_sb.t[:], ctx_sb.whole), V(ctx_d.t.ap().rearrange("(c p) d -> p c d", p=128), ctx_d.whole))
    fw.dma("sp", cc[:], cc_d[:])
    fw.dma("sp", rope[:], rope_d[:])
    fw.dma("sp", cmask[:], cmask_d[:])
    fw.act(cc[:], cc[:], AF.Silu)
    zt = fw.carve("zt", [128, 4096], BF16)
    fw.memset(zt[:], 0.0)
    for h in range(4):
        fw.dma("sp", V(kv_src[h].t.ap().rearrange("(r p) c -> p r c", p=128), kv_src[h].whole),
               V(zt.t[:].unsqueeze(1).to_broadcast([128, 4, 4096]), zt.whole))
    zf = fw.carve("zf", [128, SW], F32)
    fw.memset(zf[:], 0.0)
    fw.dma("sp", xbc_cpad[0:1, :], zf[0:1, 0:1536])
    fw.dma("sp", xbc_cpad[257:258, :], zf[0:1, 0:1536])
    fw.dma("sp", hx_src[:, :], zf[0:8, 0:1536])
    fw.dma("sp", hxL[:, :], zf[0:9, 0:1536])
    fw.dma("sp", hxR[:, :], zf[0:8, 0:1536])
    for d in range(2):
        fw.dma("sp", V(st_src[d].t.ap().rearrange("(r p) c -> p r c", p=128), st_src[d].whole),
               V(zf.t[:].unsqueeze(1).to_broadcast([128, 4, SW]), zf.whole))
        fw.dma("sp", st_stage[d][:, :], zf[:, :])
    fw.phase_reset()

    def adaln(l):
        ccrep = fw.carve("ccrep", [128, 8, 2, 128], F32)
        badaf = fw.carve("badaf", [128, 16], F32)
        gbias = fw.carve("gbias", [128, D], F32)
        wada2 = [fw.carve(f"wada_sb{i}", [128, 8, 512], F32) for i in range(2)]
        fw.copy(ccrep[:], V(cc.t[:].unsqueeze(3).to_broadcast([128, 8, 2, 128]), cc.whole))
        fw.dma("sp", badaf[:], bada_f_d[l])
        fw.dma("sp", gbias[:], V(bada_d.t[l:l + 1, 2 * D:3 * D].partition_broadcast(128), bada_d.whole))
        ps_s = V(pb[2].t[:, 0:32].rearrange("p (a b) -> p a b", b=2), pb[2].whole)
        for piece in range(6):
            wada_sb = wada2[piece % 2]
            fw.dma("sp", V(wada_sb.t[:], wada_sb.whole),
                   V(wada_d.t[l, :, piece * 512:(piece + 1) * 512].rearrange("(k p) c -> p k c", p=128), wada_d.whole))
            if piece < 4:
                for j in range(4):
                    blk = piece * 4 + j
                    for k in range(8):
                        fw.mm(V(ps_s.ap[:, blk, :], ps_s.res), wada_sb[:, k, j * 128:(j + 1) * 128], cc[:, k, :],
                              start=(k == 0), stop=(k == 7))
            else:
                half = piece - 4
                for v in range(2):
                    for k in range(8):
                        fw.mm(pb[3][:, :], ccrep[:, k, v, :], wada_sb[:, k, :], start=(k == 0), stop=(k == 7))
                    fw.tt(gate[:, v, half * 512:(half + 1) * 512], pb[3][:, :], gbias[:, half * 512:(half + 1) * 512], ALU.add)
        for v in range(2):
            fw.tt(sh[:, v, :], V(ps_s.ap[:, 0:8, v], ps_s.res), badaf[:, 0:8], ALU.add)
            fw.tt(sc1[:, v, :], V(ps_s.ap[:, 8:16, v], ps_s.res), badaf[:, 8:16], ALU.add)
        fw.ts(sc1[:], sc1[:], 1.0, ALU.add)
        fw.phase_reset()

    def layer_params(l):
        fw.dma("sp", qkg[:], qkg_d[l])
        fw.dma("sp", lamv[:], lamv_d[l])
        fw.dma("sp", subln[:], subln_d[l])
        fw.tt(small[:, 0:64], lamv[:, 0, :], lamv[:, 1, :], ALU.mult)
        s1 = fw.sbuf(f"lam_s1_{l}", [128, 1], F32)
        s2 = fw.sbuf(f"lam_s2_{l}", [128, 1], F32)
        fw.reduce(s1[:], small[:, 0:64], ALU.add)
        fw.tt(small[:, 0:64], lamv[:, 2, :], lamv[:, 3, :], ALU.mult)
        fw.reduce(s2[:], small[:, 0:64], ALU.add)
        fw.act(s1[:], s1[:], AF.Exp)
        fw.act(s2[:], s2[:], AF.Exp)
        fw.tt(neglam[:], s2[:], s1[:], ALU.subtract)
        fw.ts(neglam[:], neglam[:], -lam_init_of(l), ALU.add)
        fw.ts(subln[:], subln[:], 1.0 - lam_init_of(l), ALU.mult)

    def phase_proj(l, need_ctx_q):
        hT = fw.carve("hT", [128, 8, NTOK], BF16)
        par = [0]
        alt = lambda n, sh, dt_: Alt([fw.carve(f"{n}_{i}", sh, dt_) for i in range(2)], par)
        xn = alt("xn", [128, D], F32)
        junk = alt("junk", [128, D], F32)
        ss = alt("ss", [128, 1], F32)
        rs = alt("rs", [128, 1], F32)
        wblk = [fw.carve(f"wblk{i}", [128, 8, 512], BF16) for i in range(2)]
        wst = fw.carve("wst", [128, 8, 512], F32)
        NS = 6
        stage = [fw.carve(f"stage{i}", [128, 512], F32) for i in range(NS)]
        sq = alt("sq", [128, 8, 64], F32)
        qn = alt("qn", [128, 8, 64], F32)
        t1 = alt("t1", [128, 8, 32], F32)
        t2 = alt("t2", [128, 8, 32], F32)
        ss8 = alt("ss8", [128, 8], F32)
        qbf = [fw.carve(f"qbf{i}", [128, 8, 64], BF16) for i in range(2)]
        tb = [fw.carve(f"tb{i}", [128, 4, 128], BF16) for i in range(2)]
        vbf = [fw.carve(f"vbf{i}", [128, 512], BF16) for i in range(2)]

        for lc in range(NLC):
            par[0] = lc
            src = x_sb[:, lc, :] if lc < NCH else ctx_sb[:, lc - NCH, :]
            v = 0 if lc < NCH else 1
            fw.act(junk[:], src, AF.Square, accum_out=ss[:])
            fw.act(rs[:], ss[:], AF.Sqrt, bias=eps_t[:], scale=1.0 / D)
            fw.recip(rs[:], rs[:])
            fw.ts(xn[:], src, rs[:], ALU.mult)
            pt = V(pb[lc % 2 * 2].t[:, :], [pb[lc % 2 * 2].whole, pb[lc % 2 * 2 + 1].whole])
            ptt = pb_t[:, lc % 2 * 2:lc % 2 * 2 + 2, :].rearrange("p a (k c) -> p (a k) c", c=128)
            for k in range(8):
                fw.transpose(V(ptt[:, k, :], pt.res), xn[:, k * 128:(k + 1) * 128], ident[:], last=(k == 7))
            for k in range(8):
                fw.ts(hT[:, k, lc * 128:(lc + 1) * 128], V(ptt[:, k, :], pt.res), sc1[:, v, k:k + 1], ALU.mult,
                      sh[:, v, k:k + 1], ALU.add)
        if "hT" in dbg:
            fw.dma("sp", V(dbg["hT"].t[:], dbg["hT"].whole), V(hT.t[:], hT.whole))

        it = 0
        for bi, (bname, col0, ncols) in enumerate(BLOCKS):
            wb = wblk[bi % 2]
            fw.dma("sp", V(wst.t[:, :, 0:ncols], wst.whole),
                   V(win_d.t[l, :, col0:col0 + ncols].rearrange("(k p) c -> p k c", p=128), win_d.whole))
            fw.copy(V(wb.t[:, :, 0:ncols], wb.whole), V(wst.t[:, :, 0:ncols], wst.whole), eng="pool")
            for lc in range(NLC):
                is_ctx = lc >= NCH
                if is_ctx and not need_ctx_q and bname in ("ag", "z0", "z1", "rg"):
                    continue
                bank = pb[(4, 5, 0, 1, 2, 3)[it % 6]]
                it += 1
                par[0] = it
                for k in range(8):
                    fw.mm(bank[:, 0:ncols], hT[:, k, lc * 128:(lc + 1) * 128], wb[:, k, 0:ncols],
                          start=(k == 0), stop=(k == 7))
                rows = slice(lc * 128, (lc + 1) * 128)
                if bname in ("aq", "ak"):
                    if bname == "aq" and is_ctx and not need_ctx_q:
                        continue
                    gi = 0 if bname == "aq" else 1
                    psv = V(bank.t[:, :].rearrange("p (a b) -> p a b", b=64), bank.whole)
                    fw.act(sq[:], psv, AF.Square)
                    fw.reduce(ss8[:], sq[:], ALU.add)
                    fw.act(ss8[:], ss8[:], AF.Sqrt, bias=eps_t[:], scale=1.0 / 64)
                    fw.recip(ss8[:], ss8[:])
                    fw.tt(qn[:], psv, V(ss8.t[:].unsqueeze(2).to_broadcast([128, 8, 64]), ss8.whole), ALU.mult)
                    fw.tt(qn[:], qn[:], V(qkg.t[:, gi:gi + 1, :].to_broadcast([128, 8, 64]), qkg.whole), ALU.mult, eng="pool")
                    qo = qbf[it % 2]
                    if not is_ctx:
                        cosb = V(rope.t[:, lc, 0:1, :].to_broadcast([128, 8, 32]), rope.whole)
                        sinb = V(rope.t[:, lc, 1:2, :].to_broadcast([128, 8, 32]), rope.whole)
                        fw.tt(t1[:], qn[:, :, 0:32], cosb, ALU.mult)
                        fw.tt(t2[:], qn[:, :, 32:64], sinb, ALU.mult, eng="pool")
                        fw.tt(qo[:, :, 0:32], t1[:], t2[:], ALU.subtract)
                        fw.tt(t1[:], qn[:, :, 0:32], sinb, ALU.mult)
                        fw.tt(t2[:], qn[:, :, 32:64], cosb, ALU.mult, eng="pool")
                        fw.tt(qo[:, :, 32:64], t1[:], t2[:], ALU.add)
                    else:
                        fw.copy(qo[:], qn[:])
                    tbank = pbh[6 + lc % 2]
                    tbv = tbank.t[:, 0:512].rearrange("p (h c) -> p h c", c=128)
                    qof = qo.t[:].rearrange("p a b -> p (a b)")
                    for hd in range(4):
                        fw.transpose(V(tbv[:, hd, :], tbank.whole), V(qof[:, hd * 128:(hd + 1) * 128], qo.whole), ident_bf[:], last=(hd == 3))
                    tbs = tb[lc % 2]
                    fw.copy(tbs[:], V(tbv, tbank.whole), eng="act")
                    if bname == "aq":
                        fw.dma("act", V(qT_s.t[:, :, rows].rearrange("h p c -> p h c"), qT_s.part(lc).res), tbs[:])
                    elif is_ctx:
                        c0 = (lc - NCH) * 128
                        fw.dma("act", V(kc_s.t[:, :, c0:c0 + 128].rearrange("h p c -> p h c"), kc_s.part(lc).res), tbs[:])
                    else:
                        for hd in range(4):
                            fw.dma("act", V(kv_stage[hd].t[:, lc * 128:(lc + 1) * 128], kv_stage[hd].part(("k", lc)).res),
                                   tbs[:, hd, :])
                elif bname == "av":
                    vb = vbf[lc % 2]
                    fw.copy(vb[:], bank[:, :], eng="act")
                    if is_ctx:
                        c0 = (lc - NCH) * 128
                        fw.dma("act", V(vc_s.t[c0:c0 + 128, :], vc_s.part(lc).res), vb[:])
                    else:
                        for hd in range(4):
                            fw.dma("act", V(kv_stage[hd].t[:, 2048 + lc * 128:2048 + (lc + 1) * 128],
                                            kv_stage[hd].part(("v", lc)).res), vb[:, hd * 128:(hd + 1) * 128])
                elif bname == "ag":
                    st = stage[it % NS]
                    fw.act(st[:], bank[:, :], AF.Silu)
                    tbank = pb[6 + lc % 2]
                    for hd in range(4):
                        fw.transpose(tbank[:, hd * 128:(hd + 1) * 128], st[:, hd * 128:(hd + 1) * 128], ident[:], last=(hd == 3))
                    tbs = tb[lc % 2]
                    fw.copy(V(tbs.t[:].rearrange("p h c -> p (h c)"), tbs.whole), tbank[:, :])
                    fw.dma("act", V(agT_s.t[:, :, rows].rearrange("h p c -> p h c"), agT_s.part(lc).res), tbs[:])
                else:
                    st = stage[it % NS]
                    if lc % 2 == 0:
                        fw.copy(st[:, 0:ncols], bank[:, 0:ncols])
                    else:
                        fw.copy(st[:, 0:ncols], bank[:, 0:ncols], eng="act")
                    if bname.startswith("xbc"):
                        xc = (int(bname[3]) * 512)
                        if is_ctx:
                            r1 = 1 + (lc - NCH) * 128
                            fw.dma("sp", V(xbc_cpad.t[r1:r1 + 128, xc:xc + 512], xbc_cpad.part((bname, lc)).res), st[:, 0:ncols])
                        else:
                            r1 = 1 + lc * 128
                            fw.dma("sp", V(xbc_pad.t[r1:r1 + 128, xc:xc + 512], xbc_pad.part((bname, lc)).res), st[:, 0:ncols])
                    else:
                        fw.dma("sp", V(proj_s.t[rows, col0:col0 + ncols], proj_s.part((bname, lc)).res), st[:, 0:ncols])
            if bname == "xbc2":
                halo_issue(l)
            if bname == "av":
                for hd in range(4):
                    allres = [kv_stage[hd].part(("k", c)).res for c in range(NCH)] + [kv_stage[hd].part(("v", c)).res for c in range(NCH)]
                    fw.dma("sp", V(kv_src[hd].t[bass.ds(rank * 128, 128), :], kv_src[hd].whole), V(kv_stage[hd].t[:, :], allres))
                    fw.collective("AllReduce", ALU.add, GROUPS, kv_src[hd][:, :], kv_dst[hd][:, :])
        fw.phase_reset()

    def phase_attn(l, with_ctx_q):
        kTb = [fw.carve(f"kT{i}", [128, 8448], BF16) for i in range(2)]
        vvb = [fw.carve(f"vv{i}", [128, 66, 128], BF16) for i in range(2)]
        qhb = [fw.carve(f"qh{i}", [128, NTOK], BF16) for i in range(2)]
        aghb = [fw.carve(f"agh{i}", [128, NTOK], BF16) for i in range(2)]
        rec = fw.carve("rec", [128, 512], F32)
        om = [fw.carve(f"om{i}", [128, 512], F32) for i in range(2)]
        A = fw.carve("A", [128, 512], F32)
        sqb = fw.carve("sqb", [128, 512], BF16)
        rstd = fw.carve("rstd", [128, 512], F32)
        mixo = [fw.carve(f"mixo{i}", [128, 512], BF16) for i in range(2)]
        ones_f = fw.carve("ones_f", [128, 128], F32)
        fw.memset(ones_f[:], 1.0)
        qblocks = [(q0, 512, 0, 66) for q0 in range(0, TOK, 512)]
        if with_ctx_q:
            qblocks.append((TOK, 256, 64, 66))
        sbank = (pb[0], pb[1], pb[7])
        nq_all = NTOK if with_ctx_q else TOK

        def load_head(hd):
            kT, vv, qh, agh = kTb[hd % 2], vvb[hd % 2], qhb[hd % 2], aghb[hd % 2]
            fw.dma("sp", V(kT.t[:, 0:8192].rearrange("p (r c) -> p r c", r=4), kT.whole),
                   V(kv_dst[hd].t[:, 0:2048].rearrange("(r p) c -> p r c", p=128), kv_dst[hd].whole))
            fw.dma("sp", kT[:, 8192:8448], V(kc_s.t[hd], [kc_s.part(16).res, kc_s.part(17).res]))
            fw.dma("sp", V(vv.t[:, 0:64, :].rearrange("p (r c) e -> p r c e", r=4), vv.whole),
                   V(kv_dst[hd].t[:, 2048:4096].rearrange("(r p) (c e) -> p r c e", p=128, e=128), kv_dst[hd].whole))
            fw.dma("sp", vv[:, 64:66, :], V(vc_s.t[:, hd * 128:(hd + 1) * 128].rearrange("(c p) e -> p c e", p=128),
                                           [vc_s.part(16).res, vc_s.part(17).res]))
            qres = [qT_s.part(c).res for c in range(NLC if with_ctx_q else NCH)]
            fw.dma("sp", qh[:, 0:nq_all], V(qT_s.t[hd, :, 0:nq_all], qres))
            fw.dma("sp", agh[:, 0:NTOK], V(agT_s.t[hd], [agT_s.part(c).res for c in range(NLC)]))

        load_head(0)
        pT2 = [fw.carve(f"pTT{i}", [128, 2, 512], BF16) for i in range(3)]
        acc2 = [fw.carve(f"accT{i}", [128, 2, 512], F32) for i in range(2)]
        stage_banks = ((0, 1), (4, 5))
        for hd in range(4):
            if hd + 1 < 4:
                load_head(hd + 1)
            kT, vv, qh, agh = kTb[hd % 2], vvb[hd % 2], qhb[hd % 2], aghb[hd % 2]
            for qi, (q0, nq_, kc0, kc1) in enumerate(qblocks):
                kcs = list(range(kc0, kc1))
                n = len(kcs)

                def qk(i):
                    kc = kcs[i]
                    b0, b1 = stage_banks[i % 2]
                    for m, bk in ((0, b0), (1, b1)):
                        fw.mm(pb[bk][:, 0:nq_], kT[m * 64:(m + 1) * 64, kc * 128:(kc + 1) * 128], qh[m * 64:(m + 1) * 64, q0:q0 + nq_], True, True)

                qk(0)
                for i, kc in enumerate(kcs):
                    if i + 1 < n:
                        qk(i + 1)
                    b0, b1 = stage_banks[i % 2]
                    p = pT2[i % 3]
                    sc2 = V(pb_t[:, b0:b0 + 2, 0:nq_], [pb[b0].whole, pb[b1].whole])
                    fw.act(p[:, :, 0:nq_], sc2, AF.Exp, scale=0.125)
                    first, lastk = (i == 0), (i == n - 1)
                    for m in range(2):
                        fw.mm(pb[2 + m][:, 0:nq_], vv[:, kc, :], p[:, m, 0:nq_], start=first, stop=lastk)
                    eng_ = "dve" if i % 2 == 0 else "pool"
                    acc_ = acc2[i % 2]
                    if i < 2:
                        fw.copy(acc_[:, :, 0:nq_], p[:, :, 0:nq_], eng=eng_)
                    else:
                        fw.tt(acc_[:, :, 0:nq_], acc_[:, :, 0:nq_], p[:, :, 0:nq_], ALU.add, eng=eng_)
                if n > 1:
                    fw.tt(acc2[0][:, :, 0:nq_], acc2[0][:, :, 0:nq_], acc2[1][:, :, 0:nq_], ALU.add)
                for m in range(2):
                    fw.mm(pb[7][:, 0:nq_], ones_f[:], acc2[0][:, m, 0:nq_], True, True)
                    fw.recip(rec[:, 0:nq_], pb[7][:, 0:nq_])
                    fw.tt(om[m][:, 0:nq_], pb[2 + m][:, 0:nq_], rec[:, 0:nq_], ALU.mult)
                fw.stt(A[:, 0:nq_], om[1][:, 0:nq_], neglam[:], om[0][:, 0:nq_], ALU.mult, ALU.add)
                fw.act(sqb[:, 0:nq_], A[:, 0:nq_], AF.Square)
                fw.mm(pb[6][:, 0:nq_], ones_bf[:], sqb[:, 0:nq_], True, True)
                fw.act(rstd[:, 0:nq_], pb[6][:, 0:nq_], AF.Sqrt, bias=eps_t[:], scale=1.0 / 128)
                fw.recip(rstd[:, 0:nq_], rstd[:, 0:nq_])
                fw.stt(A[:, 0:nq_], A[:, 0:nq_], subln[:], rstd[:, 0:nq_], ALU.mult, ALU.mult)
                mo = mixo[qi % 2]
                fw.tt(mo[:, 0:nq_], A[:, 0:nq_], agh[:, q0:q0 + nq_], ALU.mult, eng="pool")
                fw.dma("act", V(mixT_s.t[hd, :, q0:q0 + nq_], mixT_s.part((hd, qi)).res), mo[:, 0:nq_])
        fw.phase_reset()

    def halo_issue(l):
        xr = lambda c: [xbc_pad.part((f"xbc{i}", c)).res for i in range(3)]
        fw.dma("sp", hx_stage[0:1, :], V(xbc_pad.t[1:2, :], xr(0)))
        fw.dma("sp", hx_stage[1:2, :], V(xbc_pad.t[TOK:TOK + 1, :], xr(NCH - 1)))
        fw.dma("sp", V(hx_src.t[bass.ds(rank * 2, 2), :], hx_src.whole), hx_stage[:, :])
        fw.collective("AllReduce", ALU.add, GROUPS, hx_src[:, :], hx_dst[:, :])

    def phase_halo(l):
        fw.dma("sp", hxL[1:9, :], hx_dst[:, :])
        fw.dma("sp", hxR[0:6, :], hx_dst[2:8, :])
        fw.dma("sp", V(xbc_pad.t[0:1, :], xbc_pad.part("hl").res), V(hxL.t[bass.ds(rank * 2, 1), :], hxL.whole))
        fw.dma("sp", V(xbc_pad.t[TOK + 1:TOK + 2, :], xbc_pad.part("hr").res), V(hxR.t[bass.ds(rank * 2, 1), :], hxR.whole))

    def rope_apply(out, x, lc, nh, t1, t2):
        cosb = V(rope.t[:, lc, 0:1, :].to_broadcast([128, nh, 32]), rope.whole)
        sinb = V(rope.t[:, lc, 1:2, :].to_broadcast([128, nh, 32]), rope.whole)
        fw.tt(t1[:, 0:nh, :], x[:, :, 0:32], cosb, ALU.mult)
        fw.tt(t2[:, 0:nh, :], x[:, :, 32:64], sinb, ALU.mult, eng="pool")
        fw.tt(out[:, :, 0:32], t1[:, 0:nh, :], t2[:, 0:nh, :], ALU.subtract)
        fw.tt(t1[:, 0:nh, :], x[:, :, 0:32], sinb, ALU.mult)
        fw.tt(t2[:, 0:nh, :], x[:, :, 32:64], cosb, ALU.mult, eng="pool")
        fw.tt(out[:, :, 32:64], t1[:, 0:nh, :], t2[:, 0:nh, :], ALU.add)

    def chain_combine(Sin, Sctx, d, col0, ncol, nparts, Aexp_of, tmp, slot):
        fw.copy(Sin[0:nparts, :], Sctx[0:nparts, :])
        order = range(4) if d == 0 else range(3, -1, -1)
        for sidx in order:
            fw.dma("sp", slot[0:nparts, 0:ncol], st_dst[d][sidx * 128:sidx * 128 + nparts, col0:col0 + ncol])
            Aexp_of(sidx, tmp)
            fw.tt(tmp[0:nparts, :], tmp[0:nparts, :], slot[0:nparts, 0:ncol], ALU.add)
            fw.tt(tmp[0:nparts, :], tmp[0:nparts, :], Sin[0:nparts, :], ALU.subtract)
            mcol = cmask[:, d * 4 + sidx:d * 4 + sidx + 1]
            fw.stt(Sin[0:nparts, :], tmp[0:nparts, :], V(mcol.ap[0:nparts], mcol.res), Sin[0:nparts, :], ALU.mult, ALU.add)

    def phase_ret(l, need_ctx, part):
        rett = fw.carve("rett", [128, 4 * 128 + 24], F32)
        rnorm = fw.carve("rnorm", [128, 128], F32)
        fw.dma("sp", rett[:], rett_d[:])
        fw.dma("sp", rnorm[:], ssdp_d[l, :, NP - 128:NP])
        Dret = V(rett.t[:, 0:512].rearrange("p (h i) -> p h i", i=128), rett.whole)
        tab = lambda k: V(rett.t[:, 512 + 4 * k:512 + 4 * k + 4], rett.whole)
        par = [0]
        alt = lambda n, sh, dt_: Alt([fw.carve(f"{n}_{i}", sh, dt_) for i in range(2)], par)
        qk = alt("qk", [128, 8, 64], F32)
        qkr = alt("qkr", [128, 8, 64], F32)
        t1 = alt("rt1", [128, 8, 32], F32)
        t2 = alt("rt2", [128, 8, 32], F32)
        rv = alt("rv", [128, 512], F32)
        rvbf = alt("rvbf", [128, 512], BF16)
        kte = [alt(f"kte{d}", [128, 4, 64], BF16) for d in range(2)]
        q3 = alt("q3", [128, 3, 4, 64], BF16)
        kbf = alt("kbf", [128, 4, 64], BF16)
        qT = alt("qT", [64, 3, 4, 128], BF16)
        kT = alt("kT", [64, 4, 128], BF16)
        Wt = alt("Wt", [128, 4, 128], BF16)
        R = [fw.carve(f"R{d}", [64, 512], F32) for d in range(2)]
        Rbf = [fw.carve(f"Rbf{d}", [64, 512], BF16) for d in range(2)]
        Rctx = [fw.carve(f"Rctx{d}", [64, 512], F32) for d in range(2)]
        Pb = fw.carve("Pb", [128, 4], F32)
        cs_sb = [alt(f"cs_sb{d}", [64, 512], F32) for d in range(2)]
        tmp = fw.carve("rtmp", [64, 512], F32)
        slot = fw.carve("rslot", [64, 512], F32)
        rg = alt("rg", [128, 512], F32)
        ysq = alt("ysq", [128, 4, 128], F32)
        yss = alt("yss", [128, 4], F32)
        yn = alt("yn", [128, 4, 128], F32)
        ybf = alt("ybf", [128, 512], BF16)
        ytb = alt("ytb", [128, 4, 128], BF16)
        bc4 = lambda v, n: V(v.ap.unsqueeze(2).to_broadcast([v.ap.shape[0], 4, n]), v.res)

        def prep(lc):
            par[0] = lc
            is_ctx = lc >= NCH
            rows = slice(lc * 128, (lc + 1) * 128)
            pr = lambda n: proj_s.part((n, lc)).res
            fw.dma("sp", V(qk.t[:].rearrange("p a b -> p (a b)"), qk.whole), V(proj_s.t[rows, 4640:5152], pr("rqk")))
            fw.dma("sp", rv[:], V(proj_s.t[rows, 5152:5664], pr("rv")))
            if is_ctx:
                src = qk
            else:
                rope_apply(qkr, qk, lc, 8, t1, t2)
                src = qkr
            fw.copy(rvbf[:], rv[:], eng="pool")
            for d in range(2):
                fw.tt(kte[d][:], src[:, 4:8, :], bc4(tab(d), 64), ALU.mult)
            return src

        def chunk_states(start_banks=(0, 1)):
            for d in range(2):
                bank = pb[start_banks[d]]
                for h in range(4):
                    fw.mm(bank[0:64, h * 128:(h + 1) * 128], kte[d][:, h, :], rvbf[:, h * 128:(h + 1) * 128],
                          start=(h == 0), stop=(h == 3), skip_group_check=True)
            return [pb[start_banks[0]], pb[start_banks[1]]]

        A128 = [tab(4), tab(5)]

        def fold(acc, csb, lc, store):
            fw.tt(V(acc[0].t[:].rearrange("p (h n) -> p h n", n=128), acc[0].whole),
                  V(acc[0].t[:].rearrange("p (h n) -> p h n", n=128), acc[0].whole),
                  V(A128[0].ap[0:64].unsqueeze(2).to_broadcast([64, 4, 128]), A128[0].res), ALU.mult)
            fw.tt(acc[0][:], acc[0][:], csb[0][0:64, :], ALU.add)
            fw.tt(V(tmp.t[:].rearrange("p (h n) -> p h n", n=128), tmp.whole),
                  V(csb[1].t[0:64, :].rearrange("p (h n) -> p h n", n=128), csb[1].whole),
                  V(Pb.t[0:64, :].unsqueeze(2).to_broadcast([64, 4, 128]), Pb.whole), ALU.mult)
            fw.tt(acc[1][:], acc[1][:], tmp[:], ALU.add)
            fw.tt(Pb[:], Pb[:], A128[1], ALU.mult)
            if store:
                for d in range(2):
                    fw.copy(cs_sb[d][:], csb[d][0:64, :])
                    fw.dma("act", V(cst_s.t[lc, d, 0:64, 1024:1536], cst_s.part(("r", lc, d)).res), cs_sb[d][:])

        for grp in (((16, 17), tuple(range(NCH))) if part == "a" else ()):
            for d in range(2):
                fw.memset(R[d][:], 0.0)
            fw.memset(Pb[:], 1.0)
            for lc in grp:
                src_ = prep(lc)
                if lc < NCH or need_ctx:
                    fw.dma("sp", V(rc_f.t[lc], rc_f.part(lc).res), V(src_.t[:].rearrange("p a b -> p (a b)"), src_.whole))
                    fw.dma("sp", V(rc_h.t[lc, :, 0:512], rc_h.part(lc).res), rvbf[:])
                    for d in range(2):
                        fw.dma("sp", V(rc_h.t[lc, :, 512 + d * 256:768 + d * 256], rc_h.part(lc).res),
                               V(kte[d].t[:].rearrange("p a b -> p (a b)"), kte[d].whole))
                if KCUT >= 2:
                    csb = chunk_states()
                if KCUT >= 3:
                    fold(R, csb, lc, KCUT >= 4)
            if grp[0] == 16:
                for d in range(2):
                    fw.copy(Rctx[d][:], R[d][:])
        if "ret_sf" in dbg:
            fw.dma("sp", dbg["ret_sf"][:, :], Rctx[0][:])
            fw.dma("sp", dbg["ret_sb"][:, :], Rctx[1][:])
        if stop_after == "ret_p1":
            fw.phase_reset(); return
        if part == "a":
            for d in range(2):
                fw.dma("sp", st_stage[d][0:64, 1024:1536], R[d][:])
                fw.dma("sp", V(rctx_s.t[d], rctx_s.whole), Rctx[d][:])
            fw.phase_reset()
            return
        for d in range(2):
            fw.dma("sp", Rctx[d][:], V(rctx_s.t[d], rctx_s.whole))
        A2048 = [[(1.0 - 2.0 ** -e) ** 2048 for e in RET_EXP_F], [(1.0 - 2.0 ** -e) ** 2048 for e in RET_EXP_B]]
        Rin = [fw.carve(f"Rin{d}", [64, 512], F32) for d in range(2)]
        for d in range(2):
            def aexp(sidx, t_, d=d):
                for h in range(4):
                    fw.ts(t_[0:64, h * 128:(h + 1) * 128], Rin[d][0:64, h * 128:(h + 1) * 128], float(A2048[d][h]), ALU.mult)
            chain_combine(Rin[d], Rctx[d], d, 1024, 512, 64, aexp, tmp, slot)
        snap = fw.carve("rsnap", [64, 512], BF16)
        csl = fw.carve("rcsl", [64, 512], F32)
        fw.copy(R[1][:], Rin[1][:])
        for lc in range(NCH - 1, -1, -1):
            fw.copy(snap[:], R[1][:])
            fw.dma("act", V(sb_s.t[lc, 0:64, 1024:1536], sb_s.part(("r", lc)).res), snap[:])
            fw.dma("sp", csl[:], V(cst_s.t[lc, 1, 0:64, 1024:1536], cst_s.part(("r", lc, 1)).res))
            fw.tt(V(R[1].t[:].rearrange("p (h n) -> p h n", n=128), R[1].whole),
                  V(R[1].t[:].rearrange("p (h n) -> p h n", n=128), R[1].whole),
                  V(A128[1].ap[0:64].unsqueeze(2).to_broadcast([64, 4, 128]), A128[1].res), ALU.mult)
            fw.tt(R[1][:], R[1][:], csl[:], ALU.add)
        if need_ctx:
            fw.dma("sp", csl[:], V(cst_s.t[17, 1, 0:64, 1024:1536], cst_s.part(("r", 17, 1)).res))
            fw.copy(snap[:], csl[:])
            fw.dma("sp", V(sb_s.t[16, 0:64, 1024:1536], sb_s.part(("r", 16)).res), snap[:])
            snap0 = fw.carve("rsnap0", [64, 512], BF16)
            fw.memset(snap0[:], 0.0)
            fw.dma("sp", V(sb_s.t[17, 0:64, 1024:1536], sb_s.part(("r", 17)).res), snap0[:])
        if stop_after == "ret_p2":
            fw.phase_reset(); return
        groups3 = [tuple(range(NCH))] + ([(16, 17)] if need_ctx else [])
        for grp in groups3:
            if grp[0] == 16:
                fw.memset(R[0][:], 0.0)
            else:
                fw.copy(R[0][:], Rin[0][:])
            for lc in grp:
                is_ctx = lc >= NCH
                rows = slice(lc * 128, (lc + 1) * 128)
                par[0] = lc
                src = qkr
                fw.dma("sp", V(qkr.t[:].rearrange("p a b -> p (a b)"), qkr.whole), V(rc_f.t[lc], rc_f.part(lc).res))
                fw.dma("sp", rvbf[:], V(rc_h.t[lc, :, 0:512], rc_h.part(lc).res))
                for d in range(2):
                    fw.dma("sp", V(kte[d].t[:].rearrange("p a b -> p (a b)"), kte[d].whole),
                           V(rc_h.t[lc, :, 512 + d * 256:768 + d * 256], rc_h.part(lc).res))
                fw.dma("sp", rg[:], V(proj_s.t[rows, 5664:6176], proj_s.part(("rg", lc)).res))
                fw.dma("sp", Rbf[1][:], V(sb_s.t[lc, 0:64, 1024:1536], sb_s.part(("r", lc)).res))
                fw.copy(Rbf[0][:], R[0][:])
                fw.copy(q3[:, 0, :, :], src[:, 0:4, :])
                fw.tt(q3[:, 1, :, :], src[:, 0:4, :], bc4(tab(2), 64), ALU.mult)
                fw.tt(q3[:, 2, :, :], src[:, 0:4, :], bc4(tab(3), 64), ALU.mult, eng="pool")
                fw.copy(kbf[:], src[:, 4:8, :])
                tqa = pbh[2]
                tqav = tqa.t[0:64, 0:1024].rearrange("p (k h c) -> p k h c", k=2, h=4)
                tqb = pbh[3]
                tqbv = tqb.t[0:64, 0:512].rearrange("p (h c) -> p h c", h=4)
                for k3 in range(2):
                    for h in range(4):
                        fw.transpose(V(tqav[:, k3, h, :], tqa.whole), q3[:, k3, h, :], ident_bf[:], last=(k3 == 1 and h == 3))
                for h in range(4):
                    fw.transpose(V(tqbv[:, h, :], tqb.whole), q3[:, 2, h, :], ident_bf[:], last=(h == 3))
                fw.copy(qT[:, 0:2, :, :], V(tqav, tqa.whole))
                fw.copy(qT[:, 2, :, :], V(tqbv, tqb.whole))
                tk = pbh[4]
                tkv = tk.t[0:64, 0:512].rearrange("p (h c) -> p h c", h=4)
                for h in range(4):
                    fw.transpose(V(tkv[:, h, :], tk.whole), kbf[:, h, :], ident_bf[:], last=(h == 3))
                fw.copy(kT[:], V(tkv, tk.whole))
                sc = pb[5]
                for h in range(4):
                    fw.mm(sc[:, h * 128:(h + 1) * 128], kT[:, h, :], qT[:, 0, h, :], start=(h == 0), stop=(h == 3), skip_group_check=True)
                fw.tt(Wt[:], V(sc.t[:, :].rearrange("p (h i) -> p h i", i=128), sc.whole), Dret, ALU.mult)
                csb = chunk_states((0, 1))
                yb = pb[6]
                for h in range(4):
                    o = yb[:, h * 128:(h + 1) * 128]
                    fw.mm(o, Wt[:, h, :], rvbf[:, h * 128:(h + 1) * 128], start=(h == 0), stop=False, last=False, skip_group_check=True)
                    fw.mm(o, qT[:, 1, h, :], Rbf[0][:, h * 128:(h + 1) * 128], start=False, stop=False, last=False, skip_group_check=True)
                    fw.mm(o, qT[:, 2, h, :], Rbf[1][:, h * 128:(h + 1) * 128], start=False, stop=(h == 3), last=(h == 3), skip_group_check=True)
                fw.tt(V(R[0].t[:].rearrange("p (h n) -> p h n", n=128), R[0].whole),
                      V(R[0].t[:].rearrange("p (h n) -> p h n", n=128), R[0].whole),
                      V(A128[0].ap[0:64].unsqueeze(2).to_broadcast([64, 4, 128]), A128[0].res), ALU.mult)
                fw.tt(R[0][:], R[0][:], csb[0][0:64, :], ALU.add)
                ybv = V(yb.t[:, :].rearrange("p (h n) -> p h n", n=128), yb.whole)
                fw.act(ysq[:], ybv, AF.Square)
                fw.reduce(yss[:], ysq[:], ALU.add)
                fw.act(yss[:], yss[:], AF.Sqrt, bias=eps_t[:], scale=1.0 / 128)
                fw.recip(yss[:], yss[:])
                fw.tt(yn[:], ybv, V(yss.t[:].unsqueeze(2).to_broadcast([128, 4, 128]), yss.whole), ALU.mult)
                fw.tt(yn[:], yn[:], V(rnorm.t[:].unsqueeze(1).to_broadcast([128, 4, 128]), rnorm.whole), ALU.mult, eng="pool")
                fw.act(rg[:], rg[:], AF.Silu)
                fw.tt(ybf[:], V(yn.t[:].rearrange("p h n -> p (h n)"), yn.whole), rg[:], ALU.mult)
                to = pbh[7]
                tov = to.t[:, 0:512].rearrange("p (h c) -> p h c", h=4)
                for h in range(4):
                    fw.transpose(V(tov[:, h, :], to.whole), ybf[:, h * 128:(h + 1) * 128], ident_bf[:], last=(h == 3))
                fw.copy(ytb[:], V(tov, to.whole))
                fw.dma("act", V(mixT_s.t[12:16, :, rows].rearrange("h p c -> p h c"), mixT_s.part(("ret", lc)).res), ytb[:])
        fw.phase_reset()

    def phase_ssd(l, need_ctx):
        OW, OB, ODT, OA, ODD = 0, 4608, 6144, 6176, 6208
        prm = fw.carve("prm", [128, 6224], F32)
        fw.dma("sp", prm[:], ssdp_d[l, :, 0:6224])
        tri = fw.carve("tri", [128, 5, 128], F32)
        fw.dma("sp", tri[:], tri_d[:])
        ssdn = fw.carve("ssdn", [128, 8], F32)
        fw.dma("sp", ssdn[:], ssdn_d[l])
        negA = fw.carve("negA", [128, 32], F32)
        fw.act(negA[:], prm[:, OA:OA + 32], AF.Exp)
        fw.ts(negA[:], negA[:], -1.0, ALU.mult)
        one_t = fw.carve("one_t", [128, 1], F32)
        fw.memset(one_t[:], 1.0)
        U = [fw.carve(f"U{i}", [128, 1536], F32) for i in range(3)]
        dtr = fw.carve("dtr", [128, 32], F32)
        la = fw.carve("la", [128, 32], F32)
        E = fw.carve("E", [128, 96], F32)
        praw = fw.carve("praw", [128, 96], F32)
        cumraw = Buf(fw, "cumraw", _APHandle(praw.t[:, 0:32]))
        cumraw.whole = praw.whole
        tots = fw.carve("tots", [128, 32], F32)
        v = [fw.carve(f"v{d}", [128, 1024], BF16) for d in range(2)]
        vte = [fw.carve(f"vte{d}", [128, 1024], BF16) for d in range(2)]
        BCbf = fw.carve("BCbf", [128, 512], BF16)
        BCT = fw.carve("BCT", [128, 4, 128], BF16)
        zt = fw.carve("zt", [128, 1024], F32)
        R1 = fw.carve("R1", [128, 16, 128], F32)
        seg = fw.carve("seg", [128, 16, 128], F32)
        Dm = fw.carve("Dm", [128, 16, 128], BF16)
        Sm = [fw.carve(f"Sm{d}", [128, 2, 128], F32) for d in range(2)]
        Wt = fw.carve("Wt", [128, 16, 128], BF16)
        S = [fw.carve(f"S{d}", [128, 1024], F32) for d in range(2)]
        Sx = [fw.carve(f"Sx{d}", [128, 1024], F32) for d in range(2)]
        Sbf = [fw.carve(f"Sbf{d}", [128, 1024], BF16) for d in range(2)]
        Pb = fw.carve("Pb", [128, 16], F32)
        yt = fw.carve("yt", [128, 1024], F32)
        y2 = fw.carve("y2", [128, 1024], F32)
        vtmp = Buf(fw, "vtmp", _APHandle(y2.t[:].rearrange("p (h n) -> p h n", n=64)))
        vtmp.whole = y2.whole
        gss = fw.carve("gss", [128, 2], F32)
        ybf = fw.carve("ybf", [128, 1024], BF16)
        ytb = fw.carve("ytb", [128, 8, 128], BF16)
        Aex = fw.carve("Aex", [128, 16], F32)
        h16 = lambda vv: V(vv.ap.unsqueeze(2).to_broadcast([128, 16, 64]), vv.res)
        as16 = lambda b_: V(b_.t[:].rearrange("p (h n) -> p h n", n=64), b_.whole)

        def prep(lc):
            is_ctx = lc >= NCH
            src_t, r0 = (xbc_cpad, (lc - NCH) * 128) if is_ctx else (xbc_pad, lc * 128)
            rr = [xbc_pad.part("hl").res, xbc_pad.part("hr").res]
            for k in range(3):
                fw.dma("sp", U[k][:], V(src_t.t[r0 + k:r0 + k + 128, :], rr))
            fw.dma("sp", dtr[:], V(proj_s.t[lc * 128:(lc + 1) * 128, 3584:3616], proj_s.part(("dtr", lc)).res))
            fw.tt(U[0][:], U[0][:], prm[:, OW:OW + 1536], ALU.mult, eng="pool")
            fw.tt(U[1][:], U[1][:], prm[:, OW + 1536:OW + 3072], ALU.mult)
            fw.tt(U[2][:], U[2][:], prm[:, OW + 3072:OW + 4608], ALU.mult, eng="pool")
            fw.tt(U[1][:], U[1][:], U[0][:], ALU.add)
            fw.tt(U[1][:], U[1][:], U[2][:], ALU.add)
            fw.tt(U[1][:], U[1][:], prm[:, OB:OB + 1536], ALU.add)
            fw.act(U[0][:], U[1][:], AF.Silu)
            fw.tt(dtr[:], dtr[:], prm[:, ODT:ODT + 32], ALU.add)
            fw.act(dtr[:], dtr[:], AF.Exp)
            fw.act(dtr[:], dtr[:], AF.Ln, bias=one_t[:])
            fw.tt(la[:], dtr[:], negA[:], ALU.mult)
            pe = pb[0]
            for i, (w, c0, c1) in enumerate(((0, 0, 16), (1, 16, 32), (2, 0, 16), (3, 16, 32), (4, 0, 32))):
                o0 = (0, 16, 32, 48, 64)[i]
                fw.mm(pe[:, o0:o0 + (c1 - c0)], tri[:, w, :], la[:, c0:c1], start=(i == 0), stop=(i == 4), skip_group_check=True)
            fw.copy(praw[:], pe[:, 0:96])
            fw.act(E[:], praw[:], AF.Exp)
            fw.tt(tots[:], tots[:], praw[:, 64:96], ALU.add)
            xs = V(U[0].t[:, 0:1024].rearrange("p (h n) -> p h n", n=64), U[0].whole)
            for d in range(2):
                fw.tt(vtmp[:], xs, h16(dtr[:, d * 16:(d + 1) * 16]), ALU.mult)
                fw.copy(as16(v[d]), vtmp[:], eng="pool")
                fw.tt(as16(vte[d]), vtmp[:], h16(E[:, 32 + d * 16:48 + d * 16]), ALU.mult)
            fw.copy(BCbf[:], U[0][:, 1024:1536], eng="pool")

        def chunk_state(d):
            banks = (pb[4], pb[5])
            for g in range(2):
                fw.mm(banks[g][:, :], BCbf[:, g * 128:(g + 1) * 128], vte[d][:, g * 512:(g + 1) * 512], True, True)
            return banks

        def mulA(dst, srcS, acol):
            fw.tt(as16(dst), as16(srcS), h16(acol), ALU.mult)

        for grp in ((16, 17), tuple(range(NCH))):
            for d in range(2):
                fw.memset(S[d][:], 0.0)
            fw.memset(Pb[:], 1.0)
            fw.memset(tots[:], 0.0)
            for lc in grp:
                prep(lc)
                if lc < NCH or need_ctx:
                    pr_ = pc_h.part(lc).res
                    fw.dma("sp", V(pc_h.t[lc, :, 0:1024], pr_), v[0][:])
                    fw.dma("sp", V(pc_h.t[lc, :, 1024:2048], pr_), v[1][:])
                    fw.dma("sp", V(pc_h.t[lc, :, 2048:3072], pr_), vte[0][:])
                    fw.dma("sp", V(pc_h.t[lc, :, 3072:3584], pr_), BCbf[:])
                    pf_ = pc_f.part(lc).res
                    fw.dma("sp", V(pc_f.t[lc, :, 0:1024], pf_), U[0][:, 0:1024])
                    fw.dma("sp", V(pc_f.t[lc, :, 1024:1120], pf_), praw[:])
                    fw.dma("sp", V(pc_f.t[lc, :, 1120:1152], pf_), la[:])
                for d in range(2):
                    banks = chunk_state(d)
                    csv = V(pb_t[:, 4:6, :], [pb[4].whole, pb[5].whole])
                    cs_sb = V(seg.t[:, d * 8:(d + 1) * 8, :].rearrange("p a (g c) -> p (a g) c", g=2)[:, 0:2, :] if False else seg.t[:, d * 8:(d + 1) * 8, :], seg.whole)
                    cs_flat = V(seg.t[:].rearrange("p h n -> p (h n)")[:, d * 1024:(d + 1) * 1024], seg.whole)
                    fw.copy(V(cs_flat.ap.rearrange("p (g c) -> p g c", g=2), seg.whole), csv)
                    fw.dma("sp", V(cst_s.t[lc, d, :, 0:1024], cst_s.part(("s", lc, d)).res), cs_flat)
                    if d == 0:
                        mulA(S[0], S[0], E[:, 64:80])
                        fw.tt(S[0][:], S[0][:], cs_flat, ALU.add)
                    else:
                        fw.tt(as16(y2), V(cs_flat.ap.rearrange("p (h n) -> p h n", n=64), seg.whole), h16(Pb[:, :]), ALU.mult)
                        fw.tt(S[1][:], S[1][:], y2[:], ALU.add)
                        fw.tt(Pb[:], Pb[:], E[:, 80:96], ALU.mult)
                fw.dma("sp", V(cst_s.t[lc, 0, :, 1536:1568], cst_s.part(("e", lc)).res), E[:, 64:96])
            if grp[0] == 16:
                for d in range(2):
                    fw.copy(Sx[d][:], S[d][:])
        if "ssd_sf" in dbg:
            fw.dma("sp", dbg["ssd_sf"][:, :], Sx[0][:])
            fw.dma("sp", dbg["ssd_sb"][:, :], Sx[1][:])
        for d in range(2):
            fw.dma("sp", st_stage[d][:, 0:1024], S[d][:])
            fw.dma("sp", st_stage[d][:, 1536:1552], tots[:, d * 16:(d + 1) * 16])
            fw.dma("sp", V(st_src[d].t[bass.ds(rank * 128, 128), :], st_src[d].whole), st_stage[d][:, :])
            fw.collective("AllReduce", ALU.add, GROUPS, st_src[d][:, :], st_dst[d][:, :])
        for d in range(2):
            order = range(4) if d == 0 else range(3, -1, -1)
            for sidx in order:
                fw.dma("sp", yt[:], st_dst[d][sidx * 128:(sidx + 1) * 128, 0:1024])
                fw.dma("sp", Aex[:], st_dst[d][sidx * 128:(sidx + 1) * 128, 1536:1552])
                fw.act(Aex[:], Aex[:], AF.Exp)
                mulA(y2, Sx[d], Aex[:, :])
                fw.tt(y2[:], y2[:], yt[:], ALU.add)
                fw.tt(y2[:], y2[:], Sx[d][:], ALU.subtract)
                fw.stt(Sx[d][:], y2[:], cmask[:, d * 4 + sidx:d * 4 + sidx + 1], Sx[d][:], ALU.mult, ALU.add)
        for lc in range(NCH - 1, -1, -1):
            fw.copy(Sbf[1][:], Sx[1][:])
            fw.dma("sp", V(sb_s.t[lc, :, 0:1024], sb_s.part(("s", lc)).res), Sbf[1][:])
            ld = yt if lc % 2 == 0 else y2
            fw.dma("sp", ld[:], V(cst_s.t[lc, 1, :, 0:1024], cst_s.part(("s", lc, 1)).res))
            fw.dma("sp", Aex[:], V(cst_s.t[lc, 0, :, 1552:1568], cst_s.part(("e", lc)).res))
            mulA(Sx[1], Sx[1], Aex[:, :])
            fw.tt(Sx[1][:], Sx[1][:], ld[:], ALU.add)
        if need_ctx:
            fw.dma("sp", yt[:], V(cst_s.t[17, 1, :, 0:1024], cst_s.part(("s", 17, 1)).res))
            fw.copy(Sbf[1][:], yt[:])
            fw.dma("sp", V(sb_s.t[16, :, 0:1024], sb_s.part(("s", 16)).res), Sbf[1][:])
            fw.memset(Sbf[0][:], 0.0)
            fw.dma("sp", V(sb_s.t[17, :, 0:1024], sb_s.part(("s", 17)).res), Sbf[0][:])
        if stop_after == "ssd_p2":
            fw.phase_reset(); return
        groups3 = [tuple(range(NCH))] + ([(16, 17)] if need_ctx else [])
        for grp in groups3:
            if grp[0] == 16:
                fw.memset(Sx[0][:], 0.0)
            for lc in grp:
                rows = slice(lc * 128, (lc + 1) * 128)
                pr_ = pc_h.part(lc).res
                pf_ = pc_f.part(lc).res
                fw.dma("sp", v[0][:], V(pc_h.t[lc, :, 0:1024], pr_))
                fw.dma("sp", v[1][:], V(pc_h.t[lc, :, 1024:2048], pr_))
                fw.dma("sp", vte[0][:], V(pc_h.t[lc, :, 2048:3072], pr_))
                fw.dma("sp", BCbf[:], V(pc_h.t[lc, :, 3072:3584], pr_))
                fw.dma("sp", U[0][:, 0:1024], V(pc_f.t[lc, :, 0:1024], pf_))
                fw.dma("sp", praw[:], V(pc_f.t[lc, :, 1024:1120], pf_))
                fw.dma("sp", la[:], V(pc_f.t[lc, :, 1120:1152], pf_))
                fw.act(E[:], praw[:], AF.Exp)
                fw.dma("sp", zt[:, 0:512], V(proj_s.t[rows, 3616:4128], proj_s.part(("z0", lc)).res))
                fw.dma("sp", zt[:, 512:1024], V(proj_s.t[rows, 4128:4640], proj_s.part(("z1", lc)).res))
                fw.dma("sp", Sbf[1][:], V(sb_s.t[lc, :, 0:1024], sb_s.part(("s", lc)).res))
                fw.copy(Sbf[0][:], Sx[0][:], eng="pool")
                tb_ = pbh[1]
                tbv = tb_.t[:, 0:512].rearrange("p (a c) -> p a c", a=4)
                for a in range(4):
                    fw.transpose(V(tbv[:, a, :], tb_.whole), BCbf[:, a * 128:(a + 1) * 128], ident_bf[:], last=(a == 3))
                fw.copy(BCT[:], V(tbv, tb_.whole))
                sc = pb[1]
                scv = sc.t[:, 256:512].rearrange("p (g i) -> p g i", g=2)
                for g in range(2):
                    fw.mm(V(scv[:, g, :], sc.whole), BCT[:, g, :], BCT[:, 2 + g, :], start=False if False else (g == 0), stop=(g == 1), skip_group_check=True)
                for d in range(2):
                    fw.tt(Sm[d][:], V(scv, sc.whole), V(tri.t[:, d:d + 1, :].to_broadcast([128, 2, 128]), tri.whole), ALU.mult)
                yb = (pb[6], pb[7])
                for d in range(2):
                    for half, eng_ in ((0, "pool"), (1, "dve")):
                        fw.tt(R1[:, half * 8:(half + 1) * 8, :],
                              V(la.t[:, d * 16 + half * 8:d * 16 + (half + 1) * 8].unsqueeze(2).to_broadcast([128, 8, 128]), la.whole),
                              V(tri.t[:, d:d + 1, :].to_broadcast([128, 8, 128]), tri.whole), ALU.mult, eng=eng_)
                    for q in range(4):
                        bank = pb[2 + q % 2]
                        fw.mm(bank[:, :], tri[:, 4, :], V(R1.t[:, 4 * q:4 * q + 4, :].rearrange("p h n -> p (h n)"), R1.whole), True, True)
                        for hh in range(4):
                            h = 4 * q + hh
                            fw.ts(seg[:, h, :], bank[:, hh * 128:(hh + 1) * 128], cumraw[:, d * 16 + h:d * 16 + h + 1], ALU.subtract, 0.0, ALU.min)
                    fw.act(Dm[:], seg[:], AF.Exp)
                    for g in range(2):
                        fw.tt(Wt[:, g * 8:(g + 1) * 8, :], Dm[:, g * 8:(g + 1) * 8, :],
                              V(Sm[d].t[:, g:g + 1, :].to_broadcast([128, 8, 128]), Sm[d].whole), ALU.mult)
                    for h in range(16):
                        fw.mm(yb[h // 8][:, (h % 8) * 64:(h % 8 + 1) * 64], Wt[:, h, :], v[d][:, h * 64:(h + 1) * 64],
                              start=(d == 0 and h % 8 == 0), stop=(d == 1 and h % 8 == 7), last=(d == 1 and h % 8 == 7), skip_group_check=True)
                for d in range(2):
                    for g in range(2):
                        fw.mm(pb[2 + g][:, :], BCT[:, 2 + g, :], Sbf[d][:, g * 512:(g + 1) * 512], True, True)
                    ysv = V(pb_t[:, 2:4, :].rearrange("p a (h n) -> p (a h) n", n=64), [pb[2].whole, pb[3].whole])
                    fw.tt(as16(yt if d == 0 else y2), ysv, h16(E[:, d * 16:(d + 1) * 16]), ALU.mult)
                fw.tt(yt[:], yt[:], y2[:], ALU.add)
                yv = V(pb_t[:, 6:8, :].rearrange("p a c -> p (a c)") if False else pb_t[:, 6:8, :], [pb[6].whole, pb[7].whole])
                fw.tt(V(yt.t[:].rearrange("p (a c) -> p a c", a=2), yt.whole), V(yt.t[:].rearrange("p (a c) -> p a c", a=2), yt.whole), yv, ALU.add)
                xs = V(U[0].t[:, 0:1024].rearrange("p (h n) -> p h n", n=64), U[0].whole)
                fw.tt(as16(y2), xs, h16(prm[:, ODD:ODD + 16]), ALU.mult, eng="pool")
                fw.tt(yt[:], yt[:], y2[:], ALU.add)
                fw.act(zt[:], zt[:], AF.Silu)
                fw.tt(yt[:], yt[:], zt[:], ALU.mult)
                banks = chunk_state(0)
                mulA(Sx[0], Sx[0], E[:, 64:80])
                fw.tt(V(Sx[0].t[:].rearrange("p (g c) -> p g c", g=2), Sx[0].whole), V(Sx[0].t[:].rearrange("p (g c) -> p g c", g=2), Sx[0].whole),
                      V(pb_t[:, 4:6, :], [pb[4].whole, pb[5].whole]), ALU.add)
                fw.act(y2[:], yt[:], AF.Square)
                fw.reduce(gss[:], V(y2.t[:].rearrange("p (g c) -> p g c", g=2), y2.whole), ALU.add)
                fw.act(gss[:], gss[:], AF.Sqrt, bias=eps_t[:], scale=1.0 / 512)
                fw.recip(gss[:], gss[:])
                fw.tt(V(ybf.t[:].rearrange("p (g c) -> p g c", g=2), ybf.whole), V(yt.t[:].rearrange("p (g c) -> p g c", g=2), yt.whole),
                      V(gss.t[:].unsqueeze(2).to_broadcast([128, 2, 512]), gss.whole), ALU.mult)
                to = pbh[1]
                tov = to.t[:, 0:1024].rearrange("p (a c) -> p a c", a=8)
                for a in range(8):
                    fw.transpose(V(tov[:, a, :], to.whole), ybf[:, a * 128:(a + 1) * 128], ident_bf[:], last=(a == 7))
                fw.tt(ytb[:], V(tov, to.whole), V(ssdn.t[:].unsqueeze(2).to_broadcast([128, 8, 128]), ssdn.whole), ALU.mult)
                fw.dma("sp", V(mixT_s.t[4:12, :, rows].rearrange("h p c -> p h c"), mixT_s.part(("ssd", lc)).res), ytb[:])
        fw.phase_reset()

    def phase_out(l, need_ctx):
        wout = fw.carve("wout", [128, 16, D], BF16)
        wst = fw.carve("wost", [128, 4, D], F32)
        mx = [fw.carve(f"mx{i}", [128, 16, 128], BF16) for i in range(2)]
        tmp = fw.carve("otmp", [128, 512], F32)
        for q in range(4):
            fw.dma("sp", V(wst.t[:], wst.whole),
                   V(wout_d.t[l, q * 512:(q + 1) * 512, :].rearrange("(f p) c -> p f c", p=128), wout_d.whole))
            fw.copy(wout[:, q * 4:(q + 1) * 4, :], wst[:], eng="pool")
        allmix = [r for r in mixT_s.parts.values()]
        for lc in (range(NLC) if need_ctx else range(NCH)):
            is_ctx = lc >= NCH
            m = mx[lc % 2]
            fw.dma("sp", m[:], V(mixT_s.t[:, :, lc * 128:(lc + 1) * 128].rearrange("f p c -> p f c"), allmix))
            for hh in range(2):
                bank = pb[(lc % 2) * 2 + hh]
                for fc in range(16):
                    fw.mm(bank[:, :], m[:, fc, :], wout[:, fc, hh * 512:(hh + 1) * 512], start=(fc == 0), stop=(fc == 15))
                cs = slice(hh * 512, (hh + 1) * 512)
                fw.tt(tmp[:], bank[:, :], gate[:, 1 if is_ctx else 0, cs], ALU.mult)
                dst = ctx_sb[:, lc - NCH, cs] if is_ctx else x_sb[:, lc, cs]
                fw.tt(dst, dst, tmp[:], ALU.add)
        fw.phase_reset()

    for l in range(depth):
        need_ctx = l < depth - 1
        adaln(l)
        layer_params(l)
        phase_proj(l, need_ctx)
        if stop_after == "t_proj": break
        phase_attn(l, need_ctx)
        if stop_after == "t_attn": break
        phase_halo(l)
        phase_ret(l, need_ctx, "a")
        phase_ssd(l, need_ctx)
        phase_ret(l, need_ctx, "b")
        if stop_after == "t_ssd": break
        phase_out(l, need_ctx)
        if l == 0 and "x0" in dbg:
            fw.dma("sp", V(dbg["x0"].t.ap().rearrange("(c p) d -> p c d", p=128), dbg["x0"].whole), V(x_sb.t[:], x_sb.whole))
            fw.dma("sp", V(dbg["ctx0"].t.ap().rearrange("(c p) d -> p c d", p=128), dbg["ctx0"].whole), V(ctx_sb.t[:], ctx_sb.whole))

    if "qT" in dbg:
        fw.dma("sp", dbg["qT"][:], V(qT_s.t[:], [qT_s.part(c).res for c in range(NLC)]))
    if "kv0" in dbg:
        fw.dma("sp", dbg["kv0"][:], kv_dst[0][:, :])
    if "proj" in dbg:
        fw.dma("sp", dbg["proj"][:], V(proj_s.t[:], [r for r in proj_s.parts.values()]))
    if "mixT" in dbg:
        fw.dma("sp", dbg["mixT"][:], V(mixT_s.t[0:4], [r for r in mixT_s.parts.values()]))
    if "mixS" in dbg:
        fw.dma("sp", dbg["mixS"][:], V(mixT_s.t[4:12], [r for r in mixT_s.parts.values()]))
    if "mixR" in dbg:
        fw.dma("sp", dbg["mixR"][:], V(mixT_s.t[12:16], [r for r in mixT_s.parts.values()]))
    fw.dma("sp", V(out_d.t.ap().rearrange("(c p) d -> p c d", p=128), out_d.whole), V(x_sb.t[:], x_sb.whole))
    fw.wait_all("sp", [out_d[:]] + [V(b.t[:], b.whole) for b in dbg.values()])
    return nc, fw


def rope_tables():
    n_freq = 16
    inv_freq = (10000.0 ** (-np.arange(n_freq, dtype=np.float32) / n_freq)).astype(np.float32)
    pos = np.arange(8192)
    row = (pos // 64).astype(np.float32)
    col = (pos % 64).astype(np.float32)
    ang = np.concatenate([row[:, None] * inv_freq, col[:, None] * inv_freq], axis=-1).astype(np.float32)
    return np.cos(ang).astype(np.float32), np.sin(ang).astype(np.float32)


def const_tables():
    j = np.arange(128)[:, None]; i = np.arange(128)[None, :]
    tri = np.stack([(j <= i), (j >= i), (j > i), (j < i), np.ones((128, 128), bool)], axis=1).astype(np.float32)
    gf = np.array([1.0 - 2.0 ** -e for e in RET_EXP_F], np.float64)
    gb = np.array([1.0 - 2.0 ** -e for e in RET_EXP_B], np.float64)
    dif = (i - j).astype(np.float64)
    Dret = np.zeros((128, 4, 128), np.float64)
    for h in range(4):
        Dret[:, h, :] = np.where(dif > 0, gf[h] ** np.abs(dif), 0.0) + np.where(dif < 0, gb[h] ** np.abs(dif), 0.0) + np.where(dif == 0, 2.0, 0.0)
    Dret *= 0.125
    pos = np.arange(128, dtype=np.float64)[:, None]
    te_f = gf[None, :] ** (127 - pos) * 0.125
    te_b = gb[None, :] ** pos * 0.125
    qsc_f = gf[None, :] ** (pos + 1)
    qsc_b = gb[None, :] ** (128 - pos)
    a_f = np.broadcast_to(gf[None, :] ** 128, (128, 4)); a_b = np.broadcast_to(gb[None, :] ** 128, (128, 4))
    rett = np.concatenate([Dret.reshape(128, 512), te_f, te_b, qsc_f, qsc_b, a_f, a_b], axis=1).astype(np.float32)
    return np.ascontiguousarray(tri), np.ascontiguousarray(rett)


def make_inputs(inp):
    cos, sin = rope_tables()
    tri, rett = const_tables()
    ssdp = np.concatenate([inp["ssd_conv_w"].reshape(2, -1), inp["ssd_conv_b"], inp["ssd_dt_bias"].reshape(2, -1),
                           inp["ssd_a_log"].reshape(2, -1), inp["ssd_d"], inp["ret_norm"]], axis=1).astype(np.float32)
    ssdp = np.ascontiguousarray(np.broadcast_to(ssdp[:, None, :], (2, 128, ssdp.shape[1])))
    ssdn = np.ascontiguousarray(inp["ssd_norm"].reshape(2, 8, 128).transpose(0, 2, 1))
    rep = lambda a: np.ascontiguousarray(np.broadcast_to(a[:, None], (a.shape[0], 128) + a.shape[1:]))
    qkg = rep(np.stack([inp["attn_q_norm"], inp["attn_k_norm"]], axis=1))
    lamv = rep(np.stack([inp["lambda_q1"], inp["lambda_k1"], inp["lambda_q2"], inp["lambda_k2"]], axis=1))
    subln = np.ascontiguousarray(inp["attn_subln"][:, :, None])
    maps = []
    for core in range(8):
        b, t = core // 4, core % 4
        lo = t * TOK
        cc = np.stack([inp["c"][b].reshape(8, 128).T, inp["c_ctx"].reshape(8, 128).T], axis=-1)
        rp = np.stack([cos[lo:lo + TOK], sin[lo:lo + TOK]], axis=1)
        rp = rp.reshape(NCH, 128, 2, 32).transpose(1, 0, 2, 3)
        m = {
            "x": np.ascontiguousarray(inp["x"][b, lo:lo + TOK]),
            "ctx": np.ascontiguousarray(inp["ctx"][b]),
            "cc": np.ascontiguousarray(cc.astype(np.float32)),
            "w_ada": inp["w_ada"],
            "b_ada_f": np.ascontiguousarray(inp["b_ada"][:, :2 * D].reshape(2, 16, 128).transpose(0, 2, 1)),
            "b_ada": inp["b_ada"],
            "w_in": inp["w_in"], "w_out": inp["w_out"],
            "qkg": qkg, "lamv": lamv, "subln": subln,
            "rope": np.ascontiguousarray(rp),
            "tri": tri, "rett": rett, "ssdp": ssdp, "ssdn": ssdn,
            "cmask": np.ascontiguousarray(np.broadcast_to(np.array([float(s_ < t) for s_ in range(4)] + [float(s_ > t) for s_ in range(4)], np.float32)[None], (128, 8))),
        }
        maps.append(m)
    return maps


from concourse.bass_utils import run_bass_kernel_spmd


def kernel(**inputs):
    inp = {k: np.asarray(v) for k, v in inputs.items()}
    nc, _ = build(depth=2)
    maps = make_inputs(inp)
    res = run_bass_kernel_spmd(nc, maps, core_ids=list(range(8)))
    outs = [np.asarray(res.results[c]["out"]) for c in range(8)]
    return np.stack([np.concatenate(outs[0:4], 0), np.concatenate(outs[4:8], 0)]).astype(np.float32)
```

```python
import numpy as np
import concourse.bass as bass
import concourse.mybir as mybir

F32 = mybir.dt.float32
BF16 = mybir.dt.bfloat16
AF = mybir.ActivationFunctionType
ALU = mybir.AluOpType
AX = mybir.AxisListType


class Res:
    __slots__ = ("name", "w", "r")

    def __init__(self, name):
        self.name = name
        self.w = None
        self.r = {}


class V:
    __slots__ = ("ap", "res")

    def __init__(self, ap, res):
        self.ap = ap
        self.res = res if isinstance(res, (list, tuple)) else [res]


class Buf:
    def __init__(self, fw, name, t, nparts=1):
        self.fw = fw
        self.name = name
        self.t = t
        self.parts = {}
        self.whole = Res(name)

    def __getitem__(self, idx):
        return V(self.t[idx], self.whole)

    def part(self, key):
        if key not in self.parts:
            self.parts[key] = Res(f"{self.name}.{key}")
        return _PartView(self, self.parts[key])

    def ap(self):
        return self.t.ap()


class _PartView:
    def __init__(self, buf, res):
        self.buf = buf
        self.res = res

    def __getitem__(self, idx):
        return V(self.buf.t[idx], self.res)


class EngState:
    def __init__(self, name, eng, sem):
        self.name = name
        self.eng = eng
        self.sem = sem
        self.count = 0
        self.pending = False
        self.seen = {}
        self.seen_dma = {}


class FW:
    def __init__(self, nc, n_dma_sems=24, same_engine_sync=True):
        self.nc = nc
        self.same_engine_sync = same_engine_sync
        self.engs = {}
        for name, eng in (("pe", nc.tensor), ("dve", nc.vector), ("act", nc.scalar),
                          ("pool", nc.gpsimd), ("sp", nc.sync)):
            self.engs[name] = EngState(name, eng, nc.alloc_semaphore(f"s_{name}"))
        self.dma_sems = [nc.alloc_semaphore(f"s_dma{i}") for i in range(n_dma_sems)]
        self.dma_vals = [0] * n_dma_sems
        self.dma_next = 0
        self.n_inst = 0
        self.out_tokens = []
        self.cc_sem = None
        self.cc_val = 0

    def sbuf(self, name, shape, dtype):
        return Buf(self, name, self.nc.alloc_sbuf_tensor("sb_" + name, list(shape), dtype))

    def psum(self, name, shape, dtype=F32):
        return Buf(self, name, self.nc.alloc_psum_tensor("ps_" + name, list(shape), dtype))

    def dram(self, name, shape, dtype, kind="Internal", **kw):
        return Buf(self, name, self.nc.dram_tensor(name, list(shape), dtype, kind=kind, **kw))

    def _need(self, E, tok):
        if tok is None:
            return
        if tok[0] == "eng":
            _, e, c = tok
            if e == E.name:
                if not self.same_engine_sync or e == "pe":
                    return
            if E.seen.get(e, 0) >= c:
                return
            P = self.engs[e]
            assert c <= P.count, f"{E.name} waits on pending (never-incremented) {e} count {c} > {P.count}"
            E.eng.wait_ge(P.sem, c)
            E.seen[e] = c
        elif tok[0] == "cc":
            val = tok[1]
            if E.seen_dma.get("cc", 0) >= val:
                return
            E.eng.wait_ge(self.cc_sem, val)
            E.seen_dma["cc"] = val
        else:
            _, si, val = tok
            if E.seen_dma.get(si, 0) >= val:
                return
            E.eng.wait_ge(self.dma_sems[si], val)
            E.seen_dma[si] = val

    def _pre(self, E, reads, writes):
        for v in reads:
            for r in v.res:
                self._need(E, r.w)
        for v in writes:
            for r in v.res:
                self._need(E, r.w)
                for tok in r.r.values():
                    self._need(E, tok)

    def _post(self, tok, key, reads, writes):
        for v in reads:
            for r in v.res:
                r.r[key] = tok
        for v in writes:
            for r in v.res:
                r.w = tok
                r.r = {}

    def op(self, engname, fn, reads, writes, inc=True):
        E = self.engs[engname]
        self._pre(E, reads, writes)
        ins = fn(E.eng)
        self.n_inst += 1
        if inc:
            E.count += 1
            ins.then_inc(E.sem, 1)
            tok = ("eng", engname, E.count)
        else:
            tok = ("eng", engname, E.count + 1)
        self._post(tok, engname, reads, writes)
        return ins

    def dma(self, qname, out, in_, **kw):
        E = self.engs[qname]
        self._pre(E, [in_], [out])
        si = self.dma_next
        self.dma_next = (self.dma_next + 1) % len(self.dma_sems)
        if self.dma_vals[si] > 0:
            self._need(E, ("dma", si, self.dma_vals[si]))
        self.dma_vals[si] += 16
        ins = E.eng.dma_start(out=out.ap, in_=in_.ap, **kw)
        ins.then_inc(self.dma_sems[si], 16)
        self.n_inst += 1
        tok = ("dma", si, self.dma_vals[si])
        self._post(tok, f"dma{si}", [in_], [out])
        return tok

    def wait_all(self, engname, views):
        E = self.engs[engname]
        for v in views:
            for r in v.res:
                self._need(E, r.w)

    def mm(self, out, lhsT, rhs, start, stop, last=None, **kw):
        if last is None:
            last = stop
        return self.op("pe", lambda e: e.matmul(out.ap, lhsT.ap, rhs.ap, start=start, stop=stop, **kw),
                       [lhsT, rhs], [out], inc=last)

    def transpose(self, out, in_, ident, last=True):
        return self.op("pe", lambda e: e.transpose(out.ap, in_.ap, ident.ap), [in_, ident], [out], inc=last)

    def act(self, out, in_, func, bias=None, scale=1.0, accum_out=None, eng="act"):
        reads = [in_]
        kw = {}
        if bias is not None:
            if isinstance(bias, V):
                reads.append(bias)
                kw["bias"] = bias.ap
            else:
                kw["bias"] = bias
        if isinstance(scale, V):
            reads.append(scale)
            kw["scale"] = scale.ap
        else:
            kw["scale"] = scale
        writes = [out]
        if accum_out is not None:
            writes.append(accum_out)
            kw["accum_out"] = accum_out.ap
        return self.op(eng, lambda e: e.activation(out.ap, in_.ap, func, **kw), reads, writes)

    def tt(self, out, in0, in1, op, eng="dve"):
        return self.op(eng, lambda e: e.tensor_tensor(out.ap, in0.ap, in1.ap, op), [in0, in1], [out])

    def ts(self, out, in0, s1, op0, s2=None, op1=None, eng="dve", accum_out=None):
        reads = [in0]
        a1 = s1
        if isinstance(s1, V):
            reads.append(s1)
            a1 = s1.ap
        a2 = s2
        if isinstance(s2, V):
            reads.append(s2)
            a2 = s2.ap
        kw = {}
        writes = [out]
        if op1 is not None:
            kw["op1"] = op1
        if accum_out is not None:
            kw["accum_out"] = accum_out.ap
            writes.append(accum_out)
        return self.op(eng, lambda e: e.tensor_scalar(out.ap, in0.ap, a1, a2, op0, **kw), reads, writes)

    def stt(self, out, in0, scalar, in1, op0, op1, eng="dve"):
        reads = [in0, in1]
        a = scalar
        if isinstance(scalar, V):
            reads.append(scalar)
            a = scalar.ap
        return self.op(eng, lambda e: e.scalar_tensor_tensor(out.ap, in0.ap, a, in1.ap, op0, op1), reads, [out])

    def copy(self, out, in_, eng="dve"):
        if eng == "act":
            return self.op("act", lambda e: e.copy(out.ap, in_.ap), [in_], [out])
        return self.op(eng, lambda e: e.tensor_copy(out.ap, in_.ap), [in_], [out])

    def memset(self, out, val, eng="dve"):
        return self.op(eng, lambda e: e.memset(out.ap, val), [], [out])

    def reduce(self, out, in_, op, axis=AX.X, eng="dve"):
        return self.op(eng, lambda e: e.tensor_reduce(out.ap, in_.ap, axis, op), [in_], [out])

    def recip(self, out, in_):
        return self.op("dve", lambda e: e.reciprocal(out.ap, in_.ap), [in_], [out])

    def collective(self, kind, op, groups, in_, out):
        E = self.engs["pool"]
        self._pre(E, [in_], [out])
        if self.cc_sem is None:
            self.cc_sem = self.nc.alloc_semaphore("s_cc")
        self.cc_val += 1
        ins = E.eng.collective_compute(kind, op, replica_groups=groups, ins=[in_.ap], outs=[out.ap])
        ins.then_inc(self.cc_sem)
        self.n_inst += 1
        tok = ("cc", self.cc_val)
        self._post(tok, "cc", [in_], [out])
        return tok

    def make_arena(self, kbytes):
        self.arena_t = self.nc.alloc_sbuf_tensor("sb_arena", [128, kbytes * 256], F32)
        self.arena_words = kbytes * 256
        self.arena_off = 0
        self.arena_gen = 0

    def carve(self, name, shape, dtype):
        esz = 2 if dtype == BF16 else 4
        n = 1
        for s in shape[1:]:
            n *= s
        words = (n * esz + 3) // 4
        words = (words + 7) // 8 * 8
        assert self.arena_off + words <= self.arena_words, f"arena overflow for {name}: {self.arena_off}+{words}>{self.arena_words}"
        raw = self.arena_t[0:shape[0], self.arena_off:self.arena_off + words]
        self.arena_off += words
        ap = raw.bitcast(dtype) if dtype != F32 else raw
        ap = ap[:, 0:n]
        if len(shape) > 2:
            names = " ".join(f"d{i}" for i in range(1, len(shape)))
            kw = {f"d{i}": shape[i] for i in range(1, len(shape))}
            ap = ap.rearrange(f"p ({names}) -> p {names}", **kw)
        return Buf(self, f"{name}@{self.arena_gen}", _APHandle(ap))

    def barrier(self):
        for E in self.engs.values():
            for P in self.engs.values():
                if P is not E and P.count > 0:
                    self._need(E, ("eng", P.name, P.count))
            for si, val in enumerate(self.dma_vals):
                if val > 0:
                    self._need(E, ("dma", si, val))
            if self.cc_val > 0:
                self._need(E, ("cc", self.cc_val))

    def phase_reset(self):
        self.barrier()
        self.arena_off = 0
        self.arena_gen += 1


class _APHandle:
    def __init__(self, ap):
        self._ap = ap

    def __getitem__(self, idx):
        return self._ap[idx]

    def ap(self):
        return self._ap


import math
KCUT = 9

D = 1024
NCH = 16
TOK = 2048
NTOK = TOK + 256
NLC = 18
DIN = 6176
EPS = 1e-6
GROUPS = [[0, 1, 2, 3], [4, 5, 6, 7]]
SW = 1568
NP = 3 * 1536 + 1536 + 32 + 32 + 16 + 128
RET_EXP_F = (5.0, 6.0, 7.0, 8.0)
RET_EXP_B = (5.5, 6.5, 7.5, 8.5)
BLOCKS = [("aq", 0, 512), ("ak", 512, 512), ("av", 1024, 512), ("ag", 1536, 512),
          ("xbc0", 2048, 512), ("xbc1", 2560, 512), ("xbc2", 3072, 512), ("dtr", 3584, 32),
          ("z0", 3616, 512), ("z1", 4128, 512), ("rqk", 4640, 512), ("rv", 5152, 512), ("rg", 5664, 512)]


class Alt:
    def __init__(self, bufs, par):
        self.bufs, self.par = bufs, par

    @property
    def cur(self):
        return self.bufs[self.par[0] % len(self.bufs)]

    def __getitem__(self, idx):
        return self.cur[idx]

    @property
    def t(self):
        return self.cur.t

    @property
    def whole(self):
        return self.cur.whole


def lam_init_of(layer):
    return 0.8 - 0.6 * math.exp(-0.3 * layer)


def build(depth=2, debug=None, stop_after=None):
    debug = debug or {}
    nc = bass.Bass("TRN2", target_bir_lowering=False)
    fw = FW(nc, same_engine_sync=True)
    I = lambda n, s, d=F32: fw.dram(n, s, d, kind="ExternalInput")
    x_d = I("x", [TOK, D])
    ctx_d = I("ctx", [256, D])
    cc_d = I("cc", [128, 8, 2])
    wada_d = I("w_ada", [2, D, 3 * D])
    bada_f_d = I("b_ada_f", [2, 128, 16])
    bada_d = I("b_ada", [2, 3 * D])
    win_d = I("w_in", [2, D, DIN])
    wout_d = I("w_out", [2, 2 * D, D])
    qkg_d = I("qkg", [2, 128, 2, 64])
    lamv_d = I("lamv", [2, 128, 4, 64])
    subln_d = I("subln", [2, 128, 1])
    rope_d = I("rope", [128, NCH, 2, 32])
    tri_d = I("tri", [128, 5, 128])
    rett_d = I("rett", [128, 4 * 128 + 24])
    ssdp_d = I("ssdp", [2, 128, NP])
    ssdn_d = I("ssdn", [2, 128, 8])
    cmask_d = I("cmask", [128, 8])
    out_d = fw.dram("out", [TOK, D], F32, kind="ExternalOutput")
    dbg = {k: fw.dram("dbg_" + k, shape, dt_, kind="ExternalOutput") for k, (shape, dt_) in debug.items()}

    proj_s = fw.dram("proj_s", [NTOK, DIN], F32)
    qT_s = fw.dram("qT_s", [4, 128, NTOK], BF16)
    agT_s = fw.dram("agT_s", [4, 128, NTOK], BF16)
    mixT_s = fw.dram("mixT_s", [16, 128, NTOK], BF16)
    kc_s = fw.dram("kc_s", [4, 128, 256], BF16)
    vc_s = fw.dram("vc_s", [256, 512], BF16)
    xbc_pad = fw.dram("xbc_pad", [TOK + 2, 1536], F32)
    xbc_cpad = fw.dram("xbc_cpad", [258, 1536], F32)
    hx_stage = fw.dram("hx_stage", [2, 1536], F32)
    hx_src = fw.dram("hx_src", [8, 1536], F32)
    hx_dst = fw.dram("hx_dst", [8, 1536], F32)
    hxL = fw.dram("hxL", [9, 1536], F32)
    hxR = fw.dram("hxR", [8, 1536], F32)
    cst_s = fw.dram("cst_s", [NLC, 2, 128, SW], F32)
    sb_s = fw.dram("sb_s", [NLC, 128, 1536], BF16)
    pc_h = fw.dram("pc_h", [NLC, 128, 3584], BF16)
    pc_f = fw.dram("pc_f", [NLC, 128, 1152], F32)
    rc_f = fw.dram("rc_f", [NLC, 128, 512], F32)
    rc_h = fw.dram("rc_h", [NLC, 128, 1024], BF16)
    rctx_s = fw.dram("rctx_s", [2, 64, 512], F32)
    st_stage = [fw.dram(f"st_stage{d}", [128, SW], F32) for d in range(2)]
    st_src = [fw.dram(f"st_src{d}", [512, SW], F32) for d in range(2)]
    st_dst = [fw.dram(f"st_dst{d}", [512, SW], F32) for d in range(2)]
    kv_src = [fw.dram(f"kv_src{h}", [512, 4096], BF16) for h in range(4)]
    kv_dst = [fw.dram(f"kv_dst{h}", [512, 4096], BF16) for h in range(4)]
    kv_stage = [fw.dram(f"kv_stage{h}", [128, 4096], BF16) for h in range(4)]

    x_sb = fw.sbuf("x_sb", [128, NCH, D], F32)
    ctx_sb = fw.sbuf("ctx_sb", [128, 2, D], F32)
    gate = fw.sbuf("gate", [128, 2, D], F32)
    sc1 = fw.sbuf("sc1", [128, 2, 8], F32)
    sh = fw.sbuf("sh", [128, 2, 8], F32)
    ident = fw.sbuf("ident", [128, 128], F32)
    ident_bf = fw.sbuf("ident_bf", [128, 128], BF16)
    ones_bf = fw.sbuf("ones_bf", [128, 128], BF16)
    eps_t = fw.sbuf("eps_t", [128, 1], F32)
    cc = fw.sbuf("cc", [128, 8, 2], F32)
    rope = fw.sbuf("rope", [128, NCH, 2, 32], F32)
    qkg = fw.sbuf("qkg", [128, 2, 64], F32)
    lamv = fw.sbuf("lamv", [128, 4, 64], F32)
    neglam = fw.sbuf("neglam", [128, 1], F32)
    subln = fw.sbuf("subln", [128, 1], F32)
    small = fw.sbuf("small", [128, 64], F32)
    cmask = fw.sbuf("cmask", [128, 8], F32)
    fw.make_arena(119)
    pb_t = nc.alloc_psum_tensor("ps_banks", [128, 8, 512], F32)
    pb = [Buf(fw, f"pb{i}", _APHandle(pb_t[:, i, :])) for i in range(8)]
    pbh = [Buf(fw, f"pbh{i}", _APHandle(pb_t[:, i, :].bitcast(BF16))) for i in range(8)]
    for i in range(8):
        pbh[i].whole = pb[i].whole
    rank = nc.partition_id() % 4

    fw.memset(ident[:], 1.0, eng="pool")
    fw.op("pool", lambda e: e.affine_select(ident.t[:], ident.t[:], [[-1, 128]], ALU.is_equal, 0.0,
                                             base=0, channel_multiplier=1), [ident[:]], [ident[:]])
    fw.copy(ident_bf[:], ident[:])
    fw.memset(ones_bf[:], 1.0)
    fw.memset(eps_t[:], EPS)
    fw.dma("sp", V(x_sb.t[:], x_sb.whole), V(x_d.t.ap().rearrange("(c p) d -> p c d", p=128), x_d.whole))
    fw.dma("sp", V(ctx_sb.t[:], ctx_sb.whole), V(ctx_d.t.ap().rearrange("(c p) d -> p c d", p=128), ctx_d.whole))
    fw.dma("sp", cc[:], cc_d[:])
    fw.dma("sp", rope[:], rope_d[:])
    fw.dma("sp", cmask[:], cmask_d[:])
    fw.act(cc[:], cc[:], AF.Silu)
    zt = fw.carve("zt", [128, 4096], BF16)
    fw.memset(zt[:], 0.0)
    for h in range(4):
        fw.dma("sp", V(kv_src[h].t.ap().rearrange("(r p) c -> p r c", p=128), kv_src[h].whole),
               V(zt.t[:].unsqueeze(1).to_broadcast([128, 4, 4096]), zt.whole))
    zf = fw.carve("zf", [128, SW], F32)
    fw.memset(zf[:], 0.0)
    fw.dma("sp", xbc_cpad[0:1, :], zf[0:1, 0:1536])
    fw.dma("sp", xbc_cpad[257:258, :], zf[0:1, 0:1536])
    fw.dma("sp", hx_src[:, :], zf[0:8, 0:1536])
    fw.dma("sp", hxL[:, :], zf[0:9, 0:1536])
    fw.dma("sp", hxR[:, :], zf[0:8, 0:1536])
    for d in range(2):
        fw.dma("sp", V(st_src[d].t.ap().rearrange("(r p) c -> p r c", p=128), st_src[d].whole),
               V(zf.t[:].unsqueeze(1).to_broadcast([128, 4, SW]), zf.whole))
        fw.dma("sp", st_stage[d][:, :], zf[:, :])
    fw.phase_reset()

    def adaln(l):
        ccrep = fw.carve("ccrep", [128, 8, 2, 128], F32)
        badaf = fw.carve("badaf", [128, 16], F32)
        gbias = fw.carve("gbias", [128, D], F32)
        wada2 = [fw.carve(f"wada_sb{i}", [128, 8, 512], F32) for i in range(2)]
        fw.copy(ccrep[:], V(cc.t[:].unsqueeze(3).to_broadcast([128, 8, 2, 128]), cc.whole))
        fw.dma("sp", badaf[:], bada_f_d[l])
        fw.dma("sp", gbias[:], V(bada_d.t[l:l + 1, 2 * D:3 * D].partition_broadcast(128), bada_d.whole))
        ps_s = V(pb[2].t[:, 0:32].rearrange("p (a b) -> p a b", b=2), pb[2].whole)
        for piece in range(6):
            wada_sb = wada2[piece % 2]
            fw.dma("sp", V(wada_sb.t[:], wada_sb.whole),
                   V(wada_d.t[l, :, piece * 512:(piece + 1) * 512].rearrange("(k p) c -> p k c", p=128), wada_d.whole))
            if piece < 4:
                for j in range(4):
                    blk = piece * 4 + j
                    for k in range(8):
                        fw.mm(V(ps_s.ap[:, blk, :], ps_s.res), wada_sb[:, k, j * 128:(j + 1) * 128], cc[:, k, :],
                              start=(k == 0), stop=(k == 7))
            else:
                half = piece - 4
                for v in range(2):
                    for k in range(8):
                        fw.mm(pb[3][:, :], ccrep[:, k, v, :], wada_sb[:, k, :], start=(k == 0), stop=(k == 7))
                    fw.tt(gate[:, v, half * 512:(half + 1) * 512], pb[3][:, :], gbias[:, half * 512:(half + 1) * 512], ALU.add)
        for v in range(2):
            fw.tt(sh[:, v, :], V(ps_s.ap[:, 0:8, v], ps_s.res), badaf[:, 0:8], ALU.add)
            fw.tt(sc1[:, v, :], V(ps_s.ap[:, 8:16, v], ps_s.res), badaf[:, 8:16], ALU.add)
        fw.ts(sc1[:], sc1[:], 1.0, ALU.add)
        fw.phase_reset()

    def layer_params(l):
        fw.dma("sp", qkg[:], qkg_d[l])
        fw.dma("sp", lamv[:], lamv_d[l])
        fw.dma("sp", subln[:], subln_d[l])
        fw.tt(small[:, 0:64], lamv[:, 0, :], lamv[:, 1, :], ALU.mult)
        s1 = fw.sbuf(f"lam_s1_{l}", [128, 1], F32)
        s2 = fw.sbuf(f"lam_s2_{l}", [128, 1], F32)
        fw.reduce(s1[:], small[:, 0:64], ALU.add)
        fw.tt(small[:, 0:64], lamv[:, 2, :], lamv[:, 3, :], ALU.mult)
        fw.reduce(s2[:], small[:, 0:64], ALU.add)
        fw.act(s1[:], s1[:], AF.Exp)
        fw.act(s2[:], s2[:], AF.Exp)
        fw.tt(neglam[:], s2[:], s1[:], ALU.subtract)
        fw.ts(neglam[:], neglam[:], -lam_init_of(l), ALU.add)
        fw.ts(subln[:], subln[:], 1.0 - lam_init_of(l), ALU.mult)

    def phase_proj(l, need_ctx_q):
        hT = fw.carve("hT", [128, 8, NTOK], BF16)
        par = [0]
        alt = lambda n, sh, dt_: Alt([fw.carve(f"{n}_{i}", sh, dt_) for i in range(2)], par)
        xn = alt("xn", [128, D], F32)
        junk = alt("junk", [128, D], F32)
        ss = alt("ss", [128, 1], F32)
        rs = alt("rs", [128, 1], F32)
        wblk = [fw.carve(f"wblk{i}", [128, 8, 512], BF16) for i in range(2)]
        wst = fw.carve("wst", [128, 8, 512], F32)
        NS = 6
        stage = [fw.carve(f"stage{i}", [128, 512], F32) for i in range(NS)]
        sq = alt("sq", [128, 8, 64], F32)
        qn = alt("qn", [128, 8, 64], F32)
        t1 = alt("t1", [128, 8, 32], F32)
        t2 = alt("t2", [128, 8, 32], F32)
        ss8 = alt("ss8", [128, 8], F32)
        qbf = [fw.carve(f"qbf{i}", [128, 8, 64], BF16) for i in range(2)]
        tb = [fw.carve(f"tb{i}", [128, 4, 128], BF16) for i in range(2)]
        vbf = [fw.carve(f"vbf{i}", [128, 512], BF16) for i in range(2)]

        for lc in range(NLC):
            par[0] = lc
            src = x_sb[:, lc, :] if lc < NCH else ctx_sb[:, lc - NCH, :]
            v = 0 if lc < NCH else 1
            fw.act(junk[:], src, AF.Square, accum_out=ss[:])
            fw.act(rs[:], ss[:], AF.Sqrt, bias=eps_t[:], scale=1.0 / D)
            fw.recip(rs[:], rs[:])
            fw.ts(xn[:], src, rs[:], ALU.mult)
            pt = V(pb[lc % 2 * 2].t[:, :], [pb[lc % 2 * 2].whole, pb[lc % 2 * 2 + 1].whole])
            ptt = pb_t[:, lc % 2 * 2:lc % 2 * 2 + 2, :].rearrange("p a (k c) -> p (a k) c", c=128)
            for k in range(8):
                fw.transpose(V(ptt[:, k, :], pt.res), xn[:, k * 128:(k + 1) * 128], ident[:], last=(k == 7))
            for k in range(8):
                fw.ts(hT[:, k, lc * 128:(lc + 1) * 128], V(ptt[:, k, :], pt.res), sc1[:, v, k:k + 1], ALU.mult,
                      sh[:, v, k:k + 1], ALU.add)
        if "hT" in dbg:
            fw.dma("sp", V(dbg["hT"].t[:], dbg["hT"].whole), V(hT.t[:], hT.whole))

        it = 0
        for bi, (bname, col0, ncols) in enumerate(BLOCKS):
            wb = wblk[bi % 2]
            fw.dma("sp", V(wst.t[:, :, 0:ncols], wst.whole),
                   V(win_d.t[l, :, col0:col0 + ncols].rearrange("(k p) c -> p k c", p=128), win_d.whole))
            fw.copy(V(wb.t[:, :, 0:ncols], wb.whole), V(wst.t[:, :, 0:ncols], wst.whole), eng="pool")
            for lc in range(NLC):
                is_ctx = lc >= NCH
                if is_ctx and not need_ctx_q and bname in ("ag", "z0", "z1", "rg"):
                    continue
                bank = pb[(4, 5, 0, 1, 2, 3)[it % 6]]
                it += 1
                par[0] = it
                for k in range(8):
                    fw.mm(bank[:, 0:ncols], hT[:, k, lc * 128:(lc + 1) * 128], wb[:, k, 0:ncols],
                          start=(k == 0), stop=(k == 7))
                rows = slice(lc * 128, (lc + 1) * 128)
                if bname in ("aq", "ak"):
                    if bname == "aq" and is_ctx and not need_ctx_q:
                        continue
                    gi = 0 if bname == "aq" else 1
                    psv = V(bank.t[:, :].rearrange("p (a b) -> p a b", b=64), bank.whole)
                    fw.act(sq[:], psv, AF.Square)
                    fw.reduce(ss8[:], sq[:], ALU.add)
                    fw.act(ss8[:], ss8[:], AF.Sqrt, bias=eps_t[:], scale=1.0 / 64)
                    fw.recip(ss8[:], ss8[:])
                    fw.tt(qn[:], psv, V(ss8.t[:].unsqueeze(2).to_broadcast([128, 8, 64]), ss8.whole), ALU.mult)
                    fw.tt(qn[:], qn[:], V(qkg.t[:, gi:gi + 1, :].to_broadcast([128, 8, 64]), qkg.whole), ALU.mult, eng="pool")
                    qo = qbf[it % 2]
                    if not is_ctx:
                        cosb = V(rope.t[:, lc, 0:1, :].to_broadcast([128, 8, 32]), rope.whole)
                        sinb = V(rope.t[:, lc, 1:2, :].to_broadcast([128, 8, 32]), rope.whole)
                        fw.tt(t1[:], qn[:, :, 0:32], cosb, ALU.mult)
                        fw.tt(t2[:], qn[:, :, 32:64], sinb, ALU.mult, eng="pool")
                        fw.tt(qo[:, :, 0:32], t1[:], t2[:], ALU.subtract)
                        fw.tt(t1[:], qn[:, :, 0:32], sinb, ALU.mult)
                        fw.tt(t2[:], qn[:, :, 32:64], cosb, ALU.mult, eng="pool")
                        fw.tt(qo[:, :, 32:64], t1[:], t2[:], ALU.add)
                    else:
                        fw.copy(qo[:], qn[:])
                    tbank = pbh[6 + lc % 2]
                    tbv = tbank.t[:, 0:512].rearrange("p (h c) -> p h c", c=128)
                    qof = qo.t[:].rearrange("p a b -> p (a b)")
                    for hd in range(4):
                        fw.transpose(V(tbv[:, hd, :], tbank.whole), V(qof[:, hd * 128:(hd + 1) * 128], qo.whole), ident_bf[:], last=(hd == 3))
                    tbs = tb[lc % 2]
                    fw.copy(tbs[:], V(tbv, tbank.whole), eng="act")
                    if bname == "aq":
                        fw.dma("act", V(qT_s.t[:, :, rows].rearrange("h p c -> p h c"), qT_s.part(lc).res), tbs[:])
                    elif is_ctx:
                        c0 = (lc - NCH) * 128
                        fw.dma("act", V(kc_s.t[:, :, c0:c0 + 128].rearrange("h p c -> p h c"), kc_s.part(lc).res), tbs[:])
                    else:
                        for hd in range(4):
                            fw.dma("act", V(kv_stage[hd].t[:, lc * 128:(lc + 1) * 128], kv_stage[hd].part(("k", lc)).res),
                                   tbs[:, hd, :])
                elif bname == "av":
                    vb = vbf[lc % 2]
                    fw.copy(vb[:], bank[:, :], eng="act")
                    if is_ctx:
                        c0 = (lc - NCH) * 128
                        fw.dma("act", V(vc_s.t[c0:c0 + 128, :], vc_s.part(lc).res), vb[:])
                    else:
                        for hd in range(4):
                            fw.dma("act", V(kv_stage[hd].t[:, 2048 + lc * 128:2048 + (lc + 1) * 128],
                                            kv_stage[hd].part(("v", lc)).res), vb[:, hd * 128:(hd + 1) * 128])
                elif bname == "ag":
                    st = stage[it % NS]
                    fw.act(st[:], bank[:, :], AF.Silu)
                    tbank = pb[6 + lc % 2]
                    for hd in range(4):
                        fw.transpose(tbank[:, hd * 128:(hd + 1) * 128], st[:, hd * 128:(hd + 1) * 128], ident[:], last=(hd == 3))
                    tbs = tb[lc % 2]
                    fw.copy(V(tbs.t[:].rearrange("p h c -> p (h c)"), tbs.whole), tbank[:, :])
                    fw.dma("act", V(agT_s.t[:, :, rows].rearrange("h p c -> p h c"), agT_s.part(lc).res), tbs[:])
                else:
                    st = stage[it % NS]
                    if lc % 2 == 0:
                        fw.copy(st[:, 0:ncols], bank[:, 0:ncols])
                    else:
                        fw.copy(st[:, 0:ncols], bank[:, 0:ncols], eng="act")
                    if bname.startswith("xbc"):
                        xc = (int(bname[3]) * 512)
                        if is_ctx:
                            r1 = 1 + (lc - NCH) * 128
                            fw.dma("sp", V(xbc_cpad.t[r1:r1 + 128, xc:xc + 512], xbc_cpad.part((bname, lc)).res), st[:, 0:ncols])
                        else:
                            r1 = 1 + lc * 128
                            fw.dma("sp", V(xbc_pad.t[r1:r1 + 128, xc:xc + 512], xbc_pad.part((bname, lc)).res), st[:, 0:ncols])
                    else:
                        fw.dma("sp", V(proj_s.t[rows, col0:col0 + ncols], proj_s.part((bname, lc)).res), st[:, 0:ncols])
            if bname == "xbc2":
                halo_issue(l)
            if bname == "av":
                for hd in range(4):
                    allres = [kv_stage[hd].part(("k", c)).res for c in range(NCH)] + [kv_stage[hd].part(("v", c)).res for c in range(NCH)]
                    fw.dma("sp", V(kv_src[hd].t[bass.ds(rank * 128, 128), :], kv_src[hd].whole), V(kv_stage[hd].t[:, :], allres))
                    fw.collective("AllReduce", ALU.add, GROUPS, kv_src[hd][:, :], kv_dst[hd][:, :])
        fw.phase_reset()

    def phase_attn(l, with_ctx_q):
        kTb = [fw.carve(f"kT{i}", [128, 8448], BF16) for i in range(2)]
        vvb = [fw.carve(f"vv{i}", [128, 66, 128], BF16) for i in range(2)]
        qhb = [fw.carve(f"qh{i}", [128, NTOK], BF16) for i in range(2)]
        aghb = [fw.carve(f"agh{i}", [128, NTOK], BF16) for i in range(2)]
        rec = fw.carve("rec", [128, 512], F32)
        om = [fw.carve(f"om{i}", [128, 512], F32) for i in range(2)]
        A = fw.carve("A", [128, 512], F32)
        sqb = fw.carve("sqb", [128, 512], BF16)
        rstd = fw.carve("rstd", [128, 512], F32)
        mixo = [fw.carve(f"mixo{i}", [128, 512], BF16) for i in range(2)]
        ones_f = fw.carve("ones_f", [128, 128], F32)
        fw.memset(ones_f[:], 1.0)
        qblocks = [(q0, 512, 0, 66) for q0 in range(0, TOK, 512)]
        if with_ctx_q:
            qblocks.append((TOK, 256, 64, 66))
        sbank = (pb[0], pb[1], pb[7])
        nq_all = NTOK if with_ctx_q else TOK

        def load_head(hd):
            kT, vv, qh, agh = kTb[hd % 2], vvb[hd % 2], qhb[hd % 2], aghb[hd % 2]
            fw.dma("sp", V(kT.t[:, 0:8192].rearrange("p (r c) -> p r c", r=4), kT.whole),
                   V(kv_dst[hd].t[:, 0:2048].rearrange("(r p) c -> p r c", p=128), kv_dst[hd].whole))
            fw.dma("sp", kT[:, 8192:8448], V(kc_s.t[hd], [kc_s.part(16).res, kc_s.part(17).res]))
            fw.dma("sp", V(vv.t[:, 0:64, :].rearrange("p (r c) e -> p r c e", r=4), vv.whole),
                   V(kv_dst[hd].t[:, 2048:4096].rearrange("(r p) (c e) -> p r c e", p=128, e=128), kv_dst[hd].whole))
            fw.dma("sp", vv[:, 64:66, :], V(vc_s.t[:, hd * 128:(hd + 1) * 128].rearrange("(c p) e -> p c e", p=128),
                                           [vc_s.part(16).res, vc_s.part(17).res]))
            qres = [qT_s.part(c).res for c in range(NLC if with_ctx_q else NCH)]
            fw.dma("sp", qh[:, 0:nq_all], V(qT_s.t[hd, :, 0:nq_all], qres))
            fw.dma("sp", agh[:, 0:NTOK], V(agT_s.t[hd], [agT_s.part(c).res for c in range(NLC)]))

        load_head(0)
        pT2 = [fw.carve(f"pTT{i}", [128, 2, 512], BF16) for i in range(3)]
        acc2 = [fw.carve(f"accT{i}", [128, 2, 512], F32) for i in range(2)]
        stage_banks = ((0, 1), (4, 5))
        for hd in range(4):
            if hd + 1 < 4:
                load_head(hd + 1)
            kT, vv, qh, agh = kTb[hd % 2], vvb[hd % 2], qhb[hd % 2], aghb[hd % 2]
            for qi, (q0, nq_, kc0, kc1) in enumerate(qblocks):
                kcs = list(range(kc0, kc1))
                n = len(kcs)

                def qk(i):
                    kc = kcs[i]
                    b0, b1 = stage_banks[i % 2]
                    for m, bk in ((0, b0), (1, b1)):
                        fw.mm(pb[bk][:, 0:nq_], kT[m * 64:(m + 1) * 64, kc * 128:(kc + 1) * 128], qh[m * 64:(m + 1) * 64, q0:q0 + nq_], True, True)

                qk(0)
                for i, kc in enumerate(kcs):
                    if i + 1 < n:
                        qk(i + 1)
                    b0, b1 = stage_banks[i % 2]
                    p = pT2[i % 3]
                    sc2 = V(pb_t[:, b0:b0 + 2, 0:nq_], [pb[b0].whole, pb[b1].whole])
                    fw.act(p[:, :, 0:nq_], sc2, AF.Exp, scale=0.125)
                    first, lastk = (i == 0), (i == n - 1)
                    for m in range(2):
                        fw.mm(pb[2 + m][:, 0:nq_], vv[:, kc, :], p[:, m, 0:nq_], start=first, stop=lastk)
                    eng_ = "dve" if i % 2 == 0 else "pool"
                    acc_ = acc2[i % 2]
                    if i < 2:
                        fw.copy(acc_[:, :, 0:nq_], p[:, :, 0:nq_], eng=eng_)
                    else:
                        fw.tt(acc_[:, :, 0:nq_], acc_[:, :, 0:nq_], p[:, :, 0:nq_], ALU.add, eng=eng_)
                if n > 1:
                    fw.tt(acc2[0][:, :, 0:nq_], acc2[0][:, :, 0:nq_], acc2[1][:, :, 0:nq_], ALU.add)
                for m in range(2):
                    nb = pb[7] if m == 0 else pb[6]
                    rc = rec if m == 0 else rstd
                    fw.mm(nb[:, 0:nq_], ones_f[:], acc2[0][:, m, 0:nq_], True, True)
                    fw.recip(rc[:, 0:nq_], nb[:, 0:nq_])
                    fw.tt(om[m][:, 0:nq_], pb[2 + m][:, 0:nq_], rc[:, 0:nq_], ALU.mult)
                fw.stt(A[:, 0:nq_], om[1][:, 0:nq_], neglam[:], om[0][:, 0:nq_], ALU.mult, ALU.add)
                fw.act(sqb[:, 0:nq_], A[:, 0:nq_], AF.Square)
                fw.mm(pb[6][:, 0:nq_], ones_bf[:], sqb[:, 0:nq_], True, True)
                fw.act(rstd[:, 0:nq_], pb[6][:, 0:nq_], AF.Sqrt, bias=eps_t[:], scale=1.0 / 128)
                fw.recip(rstd[:, 0:nq_], rstd[:, 0:nq_])
                fw.stt(A[:, 0:nq_], A[:, 0:nq_], subln[:], rstd[:, 0:nq_], ALU.mult, ALU.mult)
                mo = mixo[qi % 2]
                fw.tt(mo[:, 0:nq_], A[:, 0:nq_], agh[:, q0:q0 + nq_], ALU.mult, eng="pool")
                fw.dma("act", V(mixT_s.t[hd, :, q0:q0 + nq_], mixT_s.part((hd, qi)).res), mo[:, 0:nq_])
        fw.phase_reset()

    def halo_issue(l):
        xr = lambda c: [xbc_pad.part((f"xbc{i}", c)).res for i in range(3)]
        fw.dma("sp", hx_stage[0:1, :], V(xbc_pad.t[1:2, :], xr(0)))
        fw.dma("sp", hx_stage[1:2, :], V(xbc_pad.t[TOK:TOK + 1, :], xr(NCH - 1)))
        fw.dma("sp", V(hx_src.t[bass.ds(rank * 2, 2), :], hx_src.whole), hx_stage[:, :])
        fw.collective("AllReduce", ALU.add, GROUPS, hx_src[:, :], hx_dst[:, :])

    def phase_halo(l):
        fw.dma("sp", hxL[1:9, :], hx_dst[:, :])
        fw.dma("sp", hxR[0:6, :], hx_dst[2:8, :])
        fw.dma("sp", V(xbc_pad.t[0:1, :], xbc_pad.part("hl").res), V(hxL.t[bass.ds(rank * 2, 1), :], hxL.whole))
        fw.dma("sp", V(xbc_pad.t[TOK + 1:TOK + 2, :], xbc_pad.part("hr").res), V(hxR.t[bass.ds(rank * 2, 1), :], hxR.whole))

    def rope_apply(out, x, lc, nh, t1, t2):
        cosb = V(rope.t[:, lc, 0:1, :].to_broadcast([128, nh, 32]), rope.whole)
        sinb = V(rope.t[:, lc, 1:2, :].to_broadcast([128, nh, 32]), rope.whole)
        fw.tt(t1[:, 0:nh, :], x[:, :, 0:32], cosb, ALU.mult)
        fw.tt(t2[:, 0:nh, :], x[:, :, 32:64], sinb, ALU.mult, eng="pool")
        fw.tt(out[:, :, 0:32], t1[:, 0:nh, :], t2[:, 0:nh, :], ALU.subtract)
        fw.tt(t1[:, 0:nh, :], x[:, :, 0:32], sinb, ALU.mult)
        fw.tt(t2[:, 0:nh, :], x[:, :, 32:64], cosb, ALU.mult, eng="pool")
        fw.tt(out[:, :, 32:64], t1[:, 0:nh, :], t2[:, 0:nh, :], ALU.add)

    def chain_combine(Sin, Sctx, d, col0, ncol, nparts, Aexp_of, tmp, slot):
        fw.copy(Sin[0:nparts, :], Sctx[0:nparts, :])
        order = range(4) if d == 0 else range(3, -1, -1)
        for sidx in order:
            fw.dma("sp", slot[0:nparts, 0:ncol], st_dst[d][sidx * 128:sidx * 128 + nparts, col0:col0 + ncol])
            Aexp_of(sidx, tmp)
            fw.tt(tmp[0:nparts, :], tmp[0:nparts, :], slot[0:nparts, 0:ncol], ALU.add)
            fw.tt(tmp[0:nparts, :], tmp[0:nparts, :], Sin[0:nparts, :], ALU.subtract)
            mcol = cmask[:, d * 4 + sidx:d * 4 + sidx + 1]
            fw.stt(Sin[0:nparts, :], tmp[0:nparts, :], V(mcol.ap[0:nparts], mcol.res), Sin[0:nparts, :], ALU.mult, ALU.add)

    def phase_ret(l, need_ctx, part):
        rett = fw.carve("rett", [128, 4 * 128 + 24], F32)
        rnorm = fw.carve("rnorm", [128, 128], F32)
        fw.dma("sp", rett[:], rett_d[:])
        fw.dma("sp", rnorm[:], ssdp_d[l, :, NP - 128:NP])
        Dret = V(rett.t[:, 0:512].rearrange("p (h i) -> p h i", i=128), rett.whole)
        tab = lambda k: V(rett.t[:, 512 + 4 * k:512 + 4 * k + 4], rett.whole)
        par = [0]
        alt = lambda n, sh, dt_: Alt([fw.carve(f"{n}_{i}", sh, dt_) for i in range(2)], par)
        qk = alt("qk", [128, 8, 64], F32)
        qkr = alt("qkr", [128, 8, 64], F32)
        t1 = alt("rt1", [128, 8, 32], F32)
        t2 = alt("rt2", [128, 8, 32], F32)
        rv = alt("rv", [128, 512], F32)
        rvbf = alt("rvbf", [128, 512], BF16)
        kte = [alt(f"kte{d}", [128, 4, 64], BF16) for d in range(2)]
        q3 = alt("q3", [128, 3, 4, 64], BF16)
        kbf = alt("kbf", [128, 4, 64], BF16)
        qT = alt("qT", [64, 3, 4, 128], BF16)
        kT = alt("kT", [64, 4, 128], BF16)
        Wt = alt("Wt", [128, 4, 128], BF16)
        R = [fw.carve(f"R{d}", [64, 512], F32) for d in range(2)]
        Rbf = [fw.carve(f"Rbf{d}", [64, 512], BF16) for d in range(2)]
        Rctx = [fw.carve(f"Rctx{d}", [64, 512], F32) for d in range(2)]
        Pb = fw.carve("Pb", [128, 4], F32)
        cs_sb = [alt(f"cs_sb{d}", [64, 512], F32) for d in range(2)]
        tmp = fw.carve("rtmp", [64, 512], F32)
        slot = fw.carve("rslot", [64, 512], F32)
        rg = alt("rg", [128, 512], F32)
        ysq = alt("ysq", [128, 4, 128], F32)
        yss = alt("yss", [128, 4], F32)
        yn = alt("yn", [128, 4, 128], F32)
        ybf = alt("ybf", [128, 512], BF16)
        ytb = alt("ytb", [128, 4, 128], BF16)
        bc4 = lambda v, n: V(v.ap.unsqueeze(2).to_broadcast([v.ap.shape[0], 4, n]), v.res)

        def prep(lc):
            par[0] = lc
            is_ctx = lc >= NCH
            rows = slice(lc * 128, (lc + 1) * 128)
            pr = lambda n: proj_s.part((n, lc)).res
            fw.dma("sp", V(qk.t[:].rearrange("p a b -> p (a b)"), qk.whole), V(proj_s.t[rows, 4640:5152], pr("rqk")))
            fw.dma("sp", rv[:], V(proj_s.t[rows, 5152:5664], pr("rv")))
            if is_ctx:
                src = qk
            else:
                rope_apply(qkr, qk, lc, 8, t1, t2)
                src = qkr
            fw.copy(rvbf[:], rv[:], eng="pool")
            for d in range(2):
                fw.tt(kte[d][:], src[:, 4:8, :], bc4(tab(d), 64), ALU.mult)
            return src

        def chunk_states(start_banks=(0, 1)):
            for d in range(2):
                bank = pb[start_banks[d]]
                for h in range(4):
                    fw.mm(bank[0:64, h * 128:(h + 1) * 128], kte[d][:, h, :], rvbf[:, h * 128:(h + 1) * 128],
                          start=(h == 0), stop=(h == 3), skip_group_check=True)
            return [pb[start_banks[0]], pb[start_banks[1]]]

        A128 = [tab(4), tab(5)]

        def fold(acc, csb, lc, store):
            fw.tt(V(acc[0].t[:].rearrange("p (h n) -> p h n", n=128), acc[0].whole),
                  V(acc[0].t[:].rearrange("p (h n) -> p h n", n=128), acc[0].whole),
                  V(A128[0].ap[0:64].unsqueeze(2).to_broadcast([64, 4, 128]), A128[0].res), ALU.mult)
            fw.tt(acc[0][:], acc[0][:], csb[0][0:64, :], ALU.add)
            fw.tt(V(tmp.t[:].rearrange("p (h n) -> p h n", n=128), tmp.whole),
                  V(csb[1].t[0:64, :].rearrange("p (h n) -> p h n", n=128), csb[1].whole),
                  V(Pb.t[0:64, :].unsqueeze(2).to_broadcast([64, 4, 128]), Pb.whole), ALU.mult)
            fw.tt(acc[1][:], acc[1][:], tmp[:], ALU.add)
            fw.tt(Pb[:], Pb[:], A128[1], ALU.mult)
            if store:
                for d in range(2):
                    fw.copy(cs_sb[d][:], csb[d][0:64, :])
                    fw.dma("act", V(cst_s.t[lc, d, 0:64, 1024:1536], cst_s.part(("r", lc, d)).res), cs_sb[d][:])

        for grp in (((16, 17), tuple(range(NCH))) if part == "a" else ()):
            for d in range(2):
                fw.memset(R[d][:], 0.0)
            fw.memset(Pb[:], 1.0)
            for lc in grp:
                src_ = prep(lc)
                if lc < NCH or need_ctx:
                    fw.dma("sp", V(rc_f.t[lc], rc_f.part(lc).res), V(src_.t[:].rearrange("p a b -> p (a b)"), src_.whole))
                    fw.dma("sp", V(rc_h.t[lc, :, 0:512], rc_h.part(lc).res), rvbf[:])
                    for d in range(2):
                        fw.dma("sp", V(rc_h.t[lc, :, 512 + d * 256:768 + d * 256], rc_h.part(lc).res),
                               V(kte[d].t[:].rearrange("p a b -> p (a b)"), kte[d].whole))
                if KCUT >= 2:
                    csb = chunk_states()
                if KCUT >= 3:
                    fold(R, csb, lc, KCUT >= 4)
            if grp[0] == 16:
                for d in range(2):
                    fw.copy(Rctx[d][:], R[d][:])
        if "ret_sf" in dbg:
            fw.dma("sp", dbg["ret_sf"][:, :], Rctx[0][:])
            fw.dma("sp", dbg["ret_sb"][:, :], Rctx[1][:])
        if stop_after == "ret_p1":
            fw.phase_reset(); return
        if part == "a":
            for d in range(2):
                fw.dma("sp", st_stage[d][0:64, 1024:1536], R[d][:])
                fw.dma("sp", V(rctx_s.t[d], rctx_s.whole), Rctx[d][:])
            fw.phase_reset()
            return
        for d in range(2):
            fw.dma("sp", Rctx[d][:], V(rctx_s.t[d], rctx_s.whole))
        A2048 = [[(1.0 - 2.0 ** -e) ** 2048 for e in RET_EXP_F], [(1.0 - 2.0 ** -e) ** 2048 for e in RET_EXP_B]]
        Rin = [fw.carve(f"Rin{d}", [64, 512], F32) for d in range(2)]
        for d in range(2):
            def aexp(sidx, t_, d=d):
                for h in range(4):
                    fw.ts(t_[0:64, h * 128:(h + 1) * 128], Rin[d][0:64, h * 128:(h + 1) * 128], float(A2048[d][h]), ALU.mult)
            chain_combine(Rin[d], Rctx[d], d, 1024, 512, 64, aexp, tmp, slot)
        snap = fw.carve("rsnap", [64, 512], BF16)
        csl = fw.carve("rcsl", [64, 512], F32)
        fw.copy(R[1][:], Rin[1][:])
        for lc in range(NCH - 1, -1, -1):
            fw.copy(snap[:], R[1][:])
            fw.dma("act", V(sb_s.t[lc, 0:64, 1024:1536], sb_s.part(("r", lc)).res), snap[:])
            fw.dma("sp", csl[:], V(cst_s.t[lc, 1, 0:64, 1024:1536], cst_s.part(("r", lc, 1)).res))
            fw.tt(V(R[1].t[:].rearrange("p (h n) -> p h n", n=128), R[1].whole),
                  V(R[1].t[:].rearrange("p (h n) -> p h n", n=128), R[1].whole),
                  V(A128[1].ap[0:64].unsqueeze(2).to_broadcast([64, 4, 128]), A128[1].res), ALU.mult)
            fw.tt(R[1][:], R[1][:], csl[:], ALU.add)
        if need_ctx:
            fw.dma("sp", csl[:], V(cst_s.t[17, 1, 0:64, 1024:1536], cst_s.part(("r", 17, 1)).res))
            fw.copy(snap[:], csl[:])
            fw.dma("sp", V(sb_s.t[16, 0:64, 1024:1536], sb_s.part(("r", 16)).res), snap[:])
            snap0 = fw.carve("rsnap0", [64, 512], BF16)
            fw.memset(snap0[:], 0.0)
            fw.dma("sp", V(sb_s.t[17, 0:64, 1024:1536], sb_s.part(("r", 17)).res), snap0[:])
        if stop_after == "ret_p2":
            fw.phase_reset(); return
        groups3 = [tuple(range(NCH))] + ([(16, 17)] if need_ctx else [])
        for grp in groups3:
            if grp[0] == 16:
                fw.memset(R[0][:], 0.0)
            else:
                fw.copy(R[0][:], Rin[0][:])
            for lc in grp:
                is_ctx = lc >= NCH
                rows = slice(lc * 128, (lc + 1) * 128)
                par[0] = lc
                src = qkr
                fw.dma("sp", V(qkr.t[:].rearrange("p a b -> p (a b)"), qkr.whole), V(rc_f.t[lc], rc_f.part(lc).res))
                fw.dma("sp", rvbf[:], V(rc_h.t[lc, :, 0:512], rc_h.part(lc).res))
                for d in range(2):
                    fw.dma("sp", V(kte[d].t[:].rearrange("p a b -> p (a b)"), kte[d].whole),
                           V(rc_h.t[lc, :, 512 + d * 256:768 + d * 256], rc_h.part(lc).res))
                fw.dma("sp", rg[:], V(proj_s.t[rows, 5664:6176], proj_s.part(("rg", lc)).res))
                fw.dma("sp", Rbf[1][:], V(sb_s.t[lc, 0:64, 1024:1536], sb_s.part(("r", lc)).res))
                fw.copy(Rbf[0][:], R[0][:])
                fw.copy(q3[:, 0, :, :], src[:, 0:4, :])
                fw.tt(q3[:, 1, :, :], src[:, 0:4, :], bc4(tab(2), 64), ALU.mult)
                fw.tt(q3[:, 2, :, :], src[:, 0:4, :], bc4(tab(3), 64), ALU.mult, eng="pool")
                fw.copy(kbf[:], src[:, 4:8, :])
                tqa = pbh[2]
                tqav = tqa.t[0:64, 0:1024].rearrange("p (k h c) -> p k h c", k=2, h=4)
                tqb = pbh[3]
                tqbv = tqb.t[0:64, 0:512].rearrange("p (h c) -> p h c", h=4)
                for k3 in range(2):
                    for h in range(4):
                        fw.transpose(V(tqav[:, k3, h, :], tqa.whole), q3[:, k3, h, :], ident_bf[:], last=(k3 == 1 and h == 3))
                for h in range(4):
                    fw.transpose(V(tqbv[:, h, :], tqb.whole), q3[:, 2, h, :], ident_bf[:], last=(h == 3))
                fw.copy(qT[:, 0:2, :, :], V(tqav, tqa.whole))
                fw.copy(qT[:, 2, :, :], V(tqbv, tqb.whole))
                tk = pbh[4]
                tkv = tk.t[0:64, 0:512].rearrange("p (h c) -> p h c", h=4)
                for h in range(4):
                    fw.transpose(V(tkv[:, h, :], tk.whole), kbf[:, h, :], ident_bf[:], last=(h == 3))
                fw.copy(kT[:], V(tkv, tk.whole))
                sc = pb[5]
                for h in range(4):
                    fw.mm(sc[:, h * 128:(h + 1) * 128], kT[:, h, :], qT[:, 0, h, :], start=(h == 0), stop=(h == 3), skip_group_check=True)
                fw.tt(Wt[:], V(sc.t[:, :].rearrange("p (h i) -> p h i", i=128), sc.whole), Dret, ALU.mult)
                csb = chunk_states((0, 1))
                yb = pb[6]
                for h in range(4):
                    o = yb[:, h * 128:(h + 1) * 128]
                    fw.mm(o, Wt[:, h, :], rvbf[:, h * 128:(h + 1) * 128], start=(h == 0), stop=False, last=False, skip_group_check=True)
                    fw.mm(o, qT[:, 1, h, :], Rbf[0][:, h * 128:(h + 1) * 128], start=False, stop=False, last=False, skip_group_check=True)
                    fw.mm(o, qT[:, 2, h, :], Rbf[1][:, h * 128:(h + 1) * 128], start=False, stop=(h == 3), last=(h == 3), skip_group_check=True)
                fw.tt(V(R[0].t[:].rearrange("p (h n) -> p h n", n=128), R[0].whole),
                      V(R[0].t[:].rearrange("p (h n) -> p h n", n=128), R[0].whole),
                      V(A128[0].ap[0:64].unsqueeze(2).to_broadcast([64, 4, 128]), A128[0].res), ALU.mult)
                fw.tt(R[0][:], R[0][:], csb[0][0:64, :], ALU.add)
                ybv = V(yb.t[:, :].rearrange("p (h n) -> p h n", n=128), yb.whole)
                fw.act(ysq[:], ybv, AF.Square)
                fw.reduce(yss[:], ysq[:], ALU.add)
                fw.act(yss[:], yss[:], AF.Sqrt, bias=eps_t[:], scale=1.0 / 128)
                fw.recip(yss[:], yss[:])
                fw.tt(yn[:], ybv, V(yss.t[:].unsqueeze(2).to_broadcast([128, 4, 128]), yss.whole), ALU.mult)
                fw.tt(yn[:], yn[:], V(rnorm.t[:].unsqueeze(1).to_broadcast([128, 4, 128]), rnorm.whole), ALU.mult, eng="pool")
                fw.act(rg[:], rg[:], AF.Silu)
                fw.tt(ybf[:], V(yn.t[:].rearrange("p h n -> p (h n)"), yn.whole), rg[:], ALU.mult)
                to = pbh[7]
                tov = to.t[:, 0:512].rearrange("p (h c) -> p h c", h=4)
                for h in range(4):
                    fw.transpose(V(tov[:, h, :], to.whole), ybf[:, h * 128:(h + 1) * 128], ident_bf[:], last=(h == 3))
                fw.copy(ytb[:], V(tov, to.whole))
                fw.dma("act", V(mixT_s.t[12:16, :, rows].rearrange("h p c -> p h c"), mixT_s.part(("ret", lc)).res), ytb[:])
        fw.phase_reset()

    def phase_ssd(l, need_ctx):
        OW, OB, ODT, OA, ODD = 0, 4608, 6144, 6176, 6208
        prm = fw.carve("prm", [128, 6224], F32)
        fw.dma("sp", prm[:], ssdp_d[l, :, 0:6224])
        tri = fw.carve("tri", [128, 5, 128], F32)
        fw.dma("sp", tri[:], tri_d[:])
        ssdn = fw.carve("ssdn", [128, 8], F32)
        fw.dma("sp", ssdn[:], ssdn_d[l])
        negA = fw.carve("negA", [128, 32], F32)
        fw.act(negA[:], prm[:, OA:OA + 32], AF.Exp)
        fw.ts(negA[:], negA[:], -1.0, ALU.mult)
        one_t = fw.carve("one_t", [128, 1], F32)
        fw.memset(one_t[:], 1.0)
        U = [fw.carve(f"U{i}", [128, 1536], F32) for i in range(3)]
        dtr = fw.carve("dtr", [128, 32], F32)
        la = fw.carve("la", [128, 32], F32)
        E = fw.carve("E", [128, 96], F32)
        praw = fw.carve("praw", [128, 96], F32)
        cumraw = Buf(fw, "cumraw", _APHandle(praw.t[:, 0:32]))
        cumraw.whole = praw.whole
        tots = fw.carve("tots", [128, 32], F32)
        v = [fw.carve(f"v{d}", [128, 1024], BF16) for d in range(2)]
        vte = [fw.carve(f"vte{d}", [128, 1024], BF16) for d in range(2)]
        BCbf = fw.carve("BCbf", [128, 512], BF16)
        BCT = fw.carve("BCT", [128, 4, 128], BF16)
        zt = fw.carve("zt", [128, 1024], F32)
        R1 = fw.carve("R1", [128, 16, 128], F32)
        seg = fw.carve("seg", [128, 16, 128], F32)
        Dm = fw.carve("Dm", [128, 16, 128], BF16)
        Sm = [fw.carve(f"Sm{d}", [128, 2, 128], F32) for d in range(2)]
        Wt = fw.carve("Wt", [128, 16, 128], BF16)
        S = [fw.carve(f"S{d}", [128, 1024], F32) for d in range(2)]
        Sx = [fw.carve(f"Sx{d}", [128, 1024], F32) for d in range(2)]
        Sbf = [fw.carve(f"Sbf{d}", [128, 1024], BF16) for d in range(2)]
        Pb = fw.carve("Pb", [128, 16], F32)
        yt = fw.carve("yt", [128, 1024], F32)
        y2 = fw.carve("y2", [128, 1024], F32)
        vtmp = Buf(fw, "vtmp", _APHandle(y2.t[:].rearrange("p (h n) -> p h n", n=64)))
        vtmp.whole = y2.whole
        gss = fw.carve("gss", [128, 2], F32)
        ybf = fw.carve("ybf", [128, 1024], BF16)
        ytb = fw.carve("ytb", [128, 8, 128], BF16)
        Aex = fw.carve("Aex", [128, 16], F32)
        h16 = lambda vv: V(vv.ap.unsqueeze(2).to_broadcast([128, 16, 64]), vv.res)
        as16 = lambda b_: V(b_.t[:].rearrange("p (h n) -> p h n", n=64), b_.whole)

        def prep(lc):
            is_ctx = lc >= NCH
            src_t, r0 = (xbc_cpad, (lc - NCH) * 128) if is_ctx else (xbc_pad, lc * 128)
            rr = [xbc_pad.part("hl").res, xbc_pad.part("hr").res]
            for k in range(3):
                fw.dma("sp", U[k][:], V(src_t.t[r0 + k:r0 + k + 128, :], rr))
            fw.dma("sp", dtr[:], V(proj_s.t[lc * 128:(lc + 1) * 128, 3584:3616], proj_s.part(("dtr", lc)).res))
            fw.tt(U[0][:], U[0][:], prm[:, OW:OW + 1536], ALU.mult, eng="pool")
            fw.tt(U[1][:], U[1][:], prm[:, OW + 1536:OW + 3072], ALU.mult)
            fw.tt(U[2][:], U[2][:], prm[:, OW + 3072:OW + 4608], ALU.mult, eng="pool")
            fw.tt(U[1][:], U[1][:], U[0][:], ALU.add)
            fw.tt(U[1][:], U[1][:], U[2][:], ALU.add)
            fw.tt(U[1][:], U[1][:], prm[:, OB:OB + 1536], ALU.add)
            fw.act(U[0][:], U[1][:], AF.Silu)
            fw.tt(dtr[:], dtr[:], prm[:, ODT:ODT + 32], ALU.add)
            fw.act(dtr[:], dtr[:], AF.Exp)
            fw.act(dtr[:], dtr[:], AF.Ln, bias=one_t[:])
            fw.tt(la[:], dtr[:], negA[:], ALU.mult)
            pe = pb[0]
            for i, (w, c0, c1) in enumerate(((0, 0, 16), (1, 16, 32), (2, 0, 16), (3, 16, 32), (4, 0, 32))):
                o0 = (0, 16, 32, 48, 64)[i]
                fw.mm(pe[:, o0:o0 + (c1 - c0)], tri[:, w, :], la[:, c0:c1], start=(i == 0), stop=(i == 4), skip_group_check=True)
            fw.copy(praw[:], pe[:, 0:96])
            fw.act(E[:], praw[:], AF.Exp)
            fw.tt(tots[:], tots[:], praw[:, 64:96], ALU.add)
            xs = V(U[0].t[:, 0:1024].rearrange("p (h n) -> p h n", n=64), U[0].whole)
            for d in range(2):
                fw.tt(vtmp[:], xs, h16(dtr[:, d * 16:(d + 1) * 16]), ALU.mult)
                fw.copy(as16(v[d]), vtmp[:], eng="pool")
                fw.tt(as16(vte[d]), vtmp[:], h16(E[:, 32 + d * 16:48 + d * 16]), ALU.mult)
            fw.copy(BCbf[:], U[0][:, 1024:1536], eng="pool")

        def chunk_state(d):
            banks = (pb[4], pb[5])
            for g in range(2):
                fw.mm(banks[g][:, :], BCbf[:, g * 128:(g + 1) * 128], vte[d][:, g * 512:(g + 1) * 512], True, True)
            return banks

        def mulA(dst, srcS, acol):
            fw.tt(as16(dst), as16(srcS), h16(acol), ALU.mult)

        for grp in ((16, 17), tuple(range(NCH))):
            for d in range(2):
                fw.memset(S[d][:], 0.0)
            fw.memset(Pb[:], 1.0)
            fw.memset(tots[:], 0.0)
            for lc in grp:
                prep(lc)
                if lc < NCH or need_ctx:
                    pr_ = pc_h.part(lc).res
                    fw.dma("sp", V(pc_h.t[lc, :, 0:1024], pr_), v[0][:])
                    fw.dma("sp", V(pc_h.t[lc, :, 1024:2048], pr_), v[1][:])
                    fw.dma("sp", V(pc_h.t[lc, :, 2048:3072], pr_), vte[0][:])
                    fw.dma("sp", V(pc_h.t[lc, :, 3072:3584], pr_), BCbf[:])
                    pf_ = pc_f.part(lc).res
                    fw.dma("sp", V(pc_f.t[lc, :, 0:1024], pf_), U[0][:, 0:1024])
                    fw.dma("sp", V(pc_f.t[lc, :, 1024:1120], pf_), praw[:])
                    fw.dma("sp", V(pc_f.t[lc, :, 1120:1152], pf_), la[:])
                for d in range(2):
                    banks = chunk_state(d)
                    csv = V(pb_t[:, 4:6, :], [pb[4].whole, pb[5].whole])
                    cs_sb = V(seg.t[:, d * 8:(d + 1) * 8, :].rearrange("p a (g c) -> p (a g) c", g=2)[:, 0:2, :] if False else seg.t[:, d * 8:(d + 1) * 8, :], seg.whole)
                    cs_flat = V(seg.t[:].rearrange("p h n -> p (h n)")[:, d * 1024:(d + 1) * 1024], seg.whole)
                    fw.copy(V(cs_flat.ap.rearrange("p (g c) -> p g c", g=2), seg.whole), csv)
                    fw.dma("sp", V(cst_s.t[lc, d, :, 0:1024], cst_s.part(("s", lc, d)).res), cs_flat)
                    if d == 0:
                        mulA(S[0], S[0], E[:, 64:80])
                        fw.tt(S[0][:], S[0][:], cs_flat, ALU.add)
                    else:
                        fw.tt(as16(y2), V(cs_flat.ap.rearrange("p (h n) -> p h n", n=64), seg.whole), h16(Pb[:, :]), ALU.mult)
                        fw.tt(S[1][:], S[1][:], y2[:], ALU.add)
                        fw.tt(Pb[:], Pb[:], E[:, 80:96], ALU.mult)
                fw.dma("sp", V(cst_s.t[lc, 0, :, 1536:1568], cst_s.part(("e", lc)).res), E[:, 64:96])
            if grp[0] == 16:
                for d in range(2):
                    fw.copy(Sx[d][:], S[d][:])
        if "ssd_sf" in dbg:
            fw.dma("sp", dbg["ssd_sf"][:, :], Sx[0][:])
            fw.dma("sp", dbg["ssd_sb"][:, :], Sx[1][:])
        for d in range(2):
            fw.dma("sp", st_stage[d][:, 0:1024], S[d][:])
            fw.dma("sp", st_stage[d][:, 1536:1552], tots[:, d * 16:(d + 1) * 16])
            fw.dma("sp", V(st_src[d].t[bass.ds(rank * 128, 128), :], st_src[d].whole), st_stage[d][:, :])
            fw.collective("AllReduce", ALU.add, GROUPS, st_src[d][:, :], st_dst[d][:, :])
        for d in range(2):
            order = range(4) if d == 0 else range(3, -1, -1)
            for sidx in order:
                fw.dma("sp", yt[:], st_dst[d][sidx * 128:(sidx + 1) * 128, 0:1024])
                fw.dma("sp", Aex[:], st_dst[d][sidx * 128:(sidx + 1) * 128, 1536:1552])
                fw.act(Aex[:], Aex[:], AF.Exp)
                mulA(y2, Sx[d], Aex[:, :])
                fw.tt(y2[:], y2[:], yt[:], ALU.add)
                fw.tt(y2[:], y2[:], Sx[d][:], ALU.subtract)
                fw.stt(Sx[d][:], y2[:], cmask[:, d * 4 + sidx:d * 4 + sidx + 1], Sx[d][:], ALU.mult, ALU.add)
        for lc in range(NCH - 1, -1, -1):
            fw.copy(Sbf[1][:], Sx[1][:])
            fw.dma("sp", V(sb_s.t[lc, :, 0:1024], sb_s.part(("s", lc)).res), Sbf[1][:])
            ld = yt if lc % 2 == 0 else y2
            fw.dma("sp", ld[:], V(cst_s.t[lc, 1, :, 0:1024], cst_s.part(("s", lc, 1)).res))
            fw.dma("sp", Aex[:], V(cst_s.t[lc, 0, :, 1552:1568], cst_s.part(("e", lc)).res))
            mulA(Sx[1], Sx[1], Aex[:, :])
            fw.tt(Sx[1][:], Sx[1][:], ld[:], ALU.add)
        if need_ctx:
            fw.dma("sp", yt[:], V(cst_s.t[17, 1, :, 0:1024], cst_s.part(("s", 17, 1)).res))
            fw.copy(Sbf[1][:], yt[:])
            fw.dma("sp", V(sb_s.t[16, :, 0:1024], sb_s.part(("s", 16)).res), Sbf[1][:])
            fw.memset(Sbf[0][:], 0.0)
            fw.dma("sp", V(sb_s.t[17, :, 0:1024], sb_s.part(("s", 17)).res), Sbf[0][:])
        if stop_after == "ssd_p2":
            fw.phase_reset(); return
        groups3 = [tuple(range(NCH))] + ([(16, 17)] if need_ctx else [])
        for grp in groups3:
            if grp[0] == 16:
                fw.memset(Sx[0][:], 0.0)
            for lc in grp:
                rows = slice(lc * 128, (lc + 1) * 128)
                pr_ = pc_h.part(lc).res
                pf_ = pc_f.part(lc).res
                fw.dma("sp", v[0][:], V(pc_h.t[lc, :, 0:1024], pr_))
                fw.dma("sp", v[1][:], V(pc_h.t[lc, :, 1024:2048], pr_))
                fw.dma("sp", vte[0][:], V(pc_h.t[lc, :, 2048:3072], pr_))
                fw.dma("sp", BCbf[:], V(pc_h.t[lc, :, 3072:3584], pr_))
                fw.dma("sp", U[0][:, 0:1024], V(pc_f.t[lc, :, 0:1024], pf_))
                fw.dma("sp", praw[:], V(pc_f.t[lc, :, 1024:1120], pf_))
                fw.dma("sp", la[:], V(pc_f.t[lc, :, 1120:1152], pf_))
                fw.act(E[:], praw[:], AF.Exp)
                fw.dma("sp", zt[:, 0:512], V(proj_s.t[rows, 3616:4128], proj_s.part(("z0", lc)).res))
                fw.dma("sp", zt[:, 512:1024], V(proj_s.t[rows, 4128:4640], proj_s.part(("z1", lc)).res))
                fw.dma("sp", Sbf[1][:], V(sb_s.t[lc, :, 0:1024], sb_s.part(("s", lc)).res))
                fw.copy(Sbf[0][:], Sx[0][:], eng="pool")
                tb_ = pbh[1]
                tbv = tb_.t[:, 0:512].rearrange("p (a c) -> p a c", a=4)
                for a in range(4):
                    fw.transpose(V(tbv[:, a, :], tb_.whole), BCbf[:, a * 128:(a + 1) * 128], ident_bf[:], last=(a == 3))
                fw.copy(BCT[:], V(tbv, tb_.whole))
                sc = pb[1]
                scv = sc.t[:, 256:512].rearrange("p (g i) -> p g i", g=2)
                for g in range(2):
                    fw.mm(V(scv[:, g, :], sc.whole), BCT[:, g, :], BCT[:, 2 + g, :], start=False if False else (g == 0), stop=(g == 1), skip_group_check=True)
                for d in range(2):
                    fw.tt(Sm[d][:], V(scv, sc.whole), V(tri.t[:, d:d + 1, :].to_broadcast([128, 2, 128]), tri.whole), ALU.mult)
                yb = (pb[6], pb[7])
                for d in range(2):
                    for half, eng_ in ((0, "pool"), (1, "dve")):
                        fw.tt(R1[:, half * 8:(half + 1) * 8, :],
                              V(la.t[:, d * 16 + half * 8:d * 16 + (half + 1) * 8].unsqueeze(2).to_broadcast([128, 8, 128]), la.whole),
                              V(tri.t[:, d:d + 1, :].to_broadcast([128, 8, 128]), tri.whole), ALU.mult, eng=eng_)
                    for q in range(4):
                        bank = pb[2 + q % 2]
                        fw.mm(bank[:, :], tri[:, 4, :], V(R1.t[:, 4 * q:4 * q + 4, :].rearrange("p h n -> p (h n)"), R1.whole), True, True)
                        for hh in range(4):
                            h = 4 * q + hh
                            fw.ts(seg[:, h, :], bank[:, hh * 128:(hh + 1) * 128], cumraw[:, d * 16 + h:d * 16 + h + 1], ALU.subtract, 0.0, ALU.min)
                    fw.act(Dm[:], seg[:], AF.Exp)
                    for g in range(2):
                        fw.tt(Wt[:, g * 8:(g + 1) * 8, :], Dm[:, g * 8:(g + 1) * 8, :],
                              V(Sm[d].t[:, g:g + 1, :].to_broadcast([128, 8, 128]), Sm[d].whole), ALU.mult)
                    for h in range(16):
                        fw.mm(yb[h // 8][:, (h % 8) * 64:(h % 8 + 1) * 64], Wt[:, h, :], v[d][:, h * 64:(h + 1) * 64],
                              start=(d == 0 and h % 8 == 0), stop=(d == 1 and h % 8 == 7), last=(d == 1 and h % 8 == 7), skip_group_check=True)
                for d in range(2):
                    for g in range(2):
                        fw.mm(pb[2 + g][:, :], BCT[:, 2 + g, :], Sbf[d][:, g * 512:(g + 1) * 512], True, True)
                    ysv = V(pb_t[:, 2:4, :].rearrange("p a (h n) -> p (a h) n", n=64), [pb[2].whole, pb[3].whole])
                    fw.tt(as16(yt if d == 0 else y2), ysv, h16(E[:, d * 16:(d + 1) * 16]), ALU.mult)
                fw.tt(yt[:], yt[:], y2[:], ALU.add)
                yv = V(pb_t[:, 6:8, :].rearrange("p a c -> p (a c)") if False else pb_t[:, 6:8, :], [pb[6].whole, pb[7].whole])
                fw.tt(V(yt.t[:].rearrange("p (a c) -> p a c", a=2), yt.whole), V(yt.t[:].rearrange("p (a c) -> p a c", a=2), yt.whole), yv, ALU.add)
                xs = V(U[0].t[:, 0:1024].rearrange("p (h n) -> p h n", n=64), U[0].whole)
                fw.tt(as16(y2), xs, h16(prm[:, ODD:ODD + 16]), ALU.mult, eng="pool")
                fw.tt(yt[:], yt[:], y2[:], ALU.add)
                fw.act(zt[:], zt[:], AF.Silu)
                fw.tt(yt[:], yt[:], zt[:], ALU.mult)
                banks = chunk_state(0)
                mulA(Sx[0], Sx[0], E[:, 64:80])
                fw.tt(V(Sx[0].t[:].rearrange("p (g c) -> p g c", g=2), Sx[0].whole), V(Sx[0].t[:].rearrange("p (g c) -> p g c", g=2), Sx[0].whole),
                      V(pb_t[:, 4:6, :], [pb[4].whole, pb[5].whole]), ALU.add)
                fw.act(y2[:], yt[:], AF.Square)
                fw.reduce(gss[:], V(y2.t[:].rearrange("p (g c) -> p g c", g=2), y2.whole), ALU.add)
                fw.act(gss[:], gss[:], AF.Sqrt, bias=eps_t[:], scale=1.0 / 512)
                fw.recip(gss[:], gss[:])
                fw.tt(V(ybf.t[:].rearrange("p (g c) -> p g c", g=2), ybf.whole), V(yt.t[:].rearrange("p (g c) -> p g c", g=2), yt.whole),
                      V(gss.t[:].unsqueeze(2).to_broadcast([128, 2, 512]), gss.whole), ALU.mult)
                to = pbh[1]
                tov = to.t[:, 0:1024].rearrange("p (a c) -> p a c", a=8)
                for a in range(8):
                    fw.transpose(V(tov[:, a, :], to.whole), ybf[:, a * 128:(a + 1) * 128], ident_bf[:], last=(a == 7))
                fw.tt(ytb[:], V(tov, to.whole), V(ssdn.t[:].unsqueeze(2).to_broadcast([128, 8, 128]), ssdn.whole), ALU.mult)
                fw.dma("sp", V(mixT_s.t[4:12, :, rows].rearrange("h p c -> p h c"), mixT_s.part(("ssd", lc)).res), ytb[:])
        fw.phase_reset()

    def phase_out(l, need_ctx):
        wout = fw.carve("wout", [128, 16, D], BF16)
        wst = fw.carve("wost", [128, 4, D], F32)
        mx = [fw.carve(f"mx{i}", [128, 16, 128], BF16) for i in range(2)]
        tmp = fw.carve("otmp", [128, 512], F32)
        for q in range(4):
            fw.dma("sp", V(wst.t[:], wst.whole),
                   V(wout_d.t[l, q * 512:(q + 1) * 512, :].rearrange("(f p) c -> p f c", p=128), wout_d.whole))
            fw.copy(wout[:, q * 4:(q + 1) * 4, :], wst[:], eng="pool")
        allmix = [r for r in mixT_s.parts.values()]
        for lc in (range(NLC) if need_ctx else range(NCH)):
            is_ctx = lc >= NCH
            m = mx[lc % 2]
            fw.dma("sp", m[:], V(mixT_s.t[:, :, lc * 128:(lc + 1) * 128].rearrange("f p c -> p f c"), allmix))
            for hh in range(2):
                bank = pb[(lc % 2) * 2 + hh]
                for fc in range(16):
                    fw.mm(bank[:, :], m[:, fc, :], wout[:, fc, hh * 512:(hh + 1) * 512], start=(fc == 0), stop=(fc == 15))
                cs = slice(hh * 512, (hh + 1) * 512)
                fw.tt(tmp[:], bank[:, :], gate[:, 1 if is_ctx else 0, cs], ALU.mult)
                dst = ctx_sb[:, lc - NCH, cs] if is_ctx else x_sb[:, lc, cs]
                fw.tt(dst, dst, tmp[:], ALU.add)
        fw.phase_reset()

    for l in range(depth):
        need_ctx = l < depth - 1
        adaln(l)
        layer_params(l)
        phase_proj(l, need_ctx)
        if stop_after == "t_proj": break
        phase_attn(l, need_ctx)
        if stop_after == "t_attn": break
        phase_halo(l)
        phase_ret(l, need_ctx, "a")
        phase_ssd(l, need_ctx)
        phase_ret(l, need_ctx, "b")
        if stop_after == "t_ssd": break
        phase_out(l, need_ctx)
        if l == 0 and "x0" in dbg:
            fw.dma("sp", V(dbg["x0"].t.ap().rearrange("(c p) d -> p c d", p=128), dbg["x0"].whole), V(x_sb.t[:], x_sb.whole))
            fw.dma("sp", V(dbg["ctx0"].t.ap().rearrange("(c p) d -> p c d", p=128), dbg["ctx0"].whole), V(ctx_sb.t[:], ctx_sb.whole))

    if "qT" in dbg:
        fw.dma("sp", dbg["qT"][:], V(qT_s.t[:], [qT_s.part(c).res for c in range(NLC)]))
    if "kv0" in dbg:
        fw.dma("sp", dbg["kv0"][:], kv_dst[0][:, :])
    if "proj" in dbg:
        fw.dma("sp", dbg["proj"][:], V(proj_s.t[:], [r for r in proj_s.parts.values()]))
    if "mixT" in dbg:
        fw.dma("sp", dbg["mixT"][:], V(mixT_s.t[0:4], [r for r in mixT_s.parts.values()]))
    if "mixS" in dbg:
        fw.dma("sp", dbg["mixS"][:], V(mixT_s.t[4:12], [r for r in mixT_s.parts.values()]))
    if "mixR" in dbg:
        fw.dma("sp", dbg["mixR"][:], V(mixT_s.t[12:16], [r for r in mixT_s.parts.values()]))
    fw.dma("sp", V(out_d.t.ap().rearrange("(c p) d -> p c d", p=128), out_d.whole), V(x_sb.t[:], x_sb.whole))
    fw.wait_all("sp", [out_d[:]] + [V(b.t[:], b.whole) for b in dbg.values()])
    return nc, fw


def rope_tables():
    n_freq = 16
    inv_freq = (10000.0 ** (-np.arange(n_freq, dtype=np.float32) / n_freq)).astype(np.float32)
    pos = np.arange(8192)
    row = (pos // 64).astype(np.float32)
    col = (pos % 64).astype(np.float32)
    ang = np.concatenate([row[:, None] * inv_freq, col[:, None] * inv_freq], axis=-1).astype(np.float32)
    return np.cos(ang).astype(np.float32), np.sin(ang).astype(np.float32)


def const_tables():
    j = np.arange(128)[:, None]; i = np.arange(128)[None, :]
    tri = np.stack([(j <= i), (j >= i), (j > i), (j < i), np.ones((128, 128), bool)], axis=1).astype(np.float32)
    gf = np.array([1.0 - 2.0 ** -e for e in RET_EXP_F], np.float64)
    gb = np.array([1.0 - 2.0 ** -e for e in RET_EXP_B], np.float64)
    dif = (i - j).astype(np.float64)
    Dret = np.zeros((128, 4, 128), np.float64)
    for h in range(4):
        Dret[:, h, :] = np.where(dif > 0, gf[h] ** np.abs(dif), 0.0) + np.where(dif < 0, gb[h] ** np.abs(dif), 0.0) + np.where(dif == 0, 2.0, 0.0)
    Dret *= 0.125
    pos = np.arange(128, dtype=np.float64)[:, None]
    te_f = gf[None, :] ** (127 - pos) * 0.125
    te_b = gb[None, :] ** pos * 0.125
    qsc_f = gf[None, :] ** (pos + 1)
    qsc_b = gb[None, :] ** (128 - pos)
    a_f = np.broadcast_to(gf[None, :] ** 128, (128, 4)); a_b = np.broadcast_to(gb[None, :] ** 128, (128, 4))
    rett = np.concatenate([Dret.reshape(128, 512), te_f, te_b, qsc_f, qsc_b, a_f, a_b], axis=1).astype(np.float32)
    return np.ascontiguousarray(tri), np.ascontiguousarray(rett)


def make_inputs(inp):
    cos, sin = rope_tables()
    tri, rett = const_tables()
    ssdp = np.concatenate([inp["ssd_conv_w"].reshape(2, -1), inp["ssd_conv_b"], inp["ssd_dt_bias"].reshape(2, -1),
                           inp["ssd_a_log"].reshape(2, -1), inp["ssd_d"], inp["ret_norm"]], axis=1).astype(np.float32)
    ssdp = np.ascontiguousarray(np.broadcast_to(ssdp[:, None, :], (2, 128, ssdp.shape[1])))
    ssdn = np.ascontiguousarray(inp["ssd_norm"].reshape(2, 8, 128).transpose(0, 2, 1))
    rep = lambda a: np.ascontiguousarray(np.broadcast_to(a[:, None], (a.shape[0], 128) + a.shape[1:]))
    qkg = rep(np.stack([inp["attn_q_norm"], inp["attn_k_norm"]], axis=1))
    lamv = rep(np.stack([inp["lambda_q1"], inp["lambda_k1"], inp["lambda_q2"], inp["lambda_k2"]], axis=1))
    subln = np.ascontiguousarray(inp["attn_subln"][:, :, None])
    maps = []
    for core in range(8):
        b, t = core // 4, core % 4
        lo = t * TOK
        cc = np.stack([inp["c"][b].reshape(8, 128).T, inp["c_ctx"].reshape(8, 128).T], axis=-1)
        rp = np.stack([cos[lo:lo + TOK], sin[lo:lo + TOK]], axis=1)
        rp = rp.reshape(NCH, 128, 2, 32).transpose(1, 0, 2, 3)
        m = {
            "x": np.ascontiguousarray(inp["x"][b, lo:lo + TOK]),
            "ctx": np.ascontiguousarray(inp["ctx"][b]),
            "cc": np.ascontiguousarray(cc.astype(np.float32)),
            "w_ada": inp["w_ada"],
            "b_ada_f": np.ascontiguousarray(inp["b_ada"][:, :2 * D].reshape(2, 16, 128).transpose(0, 2, 1)),
            "b_ada": inp["b_ada"],
            "w_in": inp["w_in"], "w_out": inp["w_out"],
            "qkg": qkg, "lamv": lamv, "subln": subln,
            "rope": np.ascontiguousarray(rp),
            "tri": tri, "rett": rett, "ssdp": ssdp, "ssdn": ssdn,
            "cmask": np.ascontiguousarray(np.broadcast_to(np.array([float(s_ < t) for s_ in range(4)] + [float(s_ > t) for s_ in range(4)], np.float32)[None], (128, 8))),
        }
        maps.append(m)
    return maps


from concourse.bass_utils import run_bass_kernel_spmd


def kernel(**inputs):
    inp = {k: np.asarray(v) for k, v in inputs.items()}
    nc, _ = build(depth=2)
    maps = make_inputs(inp)
    res = run_bass_kernel_spmd(nc, maps, core_ids=list(range(8)))
    outs = [np.asarray(res.results[c]["out"]) for c in range(8)]
    return np.stack([np.concatenate(outs[0:4], 0), np.concatenate(outs[4:8], 0)]).astype(np.float32)
```
